# Optimizing a Trainium2 kernel written in Bass

```python
import math
import jax, jax.numpy as jnp
from jax import lax
import numpy as np

D_MODEL = 1024
BATCH = 8
SEQ = 4096
DEPTH = 1

ATTN_HEADS = 8
ATTN_HEAD_DIM = 64
ATTN_WIDTH = ATTN_HEADS * ATTN_HEAD_DIM
IDX_HEADS = 8
IDX_DIM = 64
TOPK_MAX = 256
Q_BLOCK = 128
REL_BUCKETS = 32
REL_MAX_DISTANCE = 128
RWKV_HEADS = 8
RWKV_HEAD_DIM = 64
RWKV_WIDTH = RWKV_HEADS * RWKV_HEAD_DIM
DECAY_LORA = 64
AAA_LORA = 64
GATE_LORA = 160
FFN_HIDDEN = ((8 * D_MODEL + 3 * 256 - 1) // (3 * 256)) * 256

RMS_EPS = 1e-6
GN_EPS = 64e-5

ATTN_SPLITS = (ATTN_WIDTH, ATTN_WIDTH, ATTN_WIDTH, IDX_HEADS * IDX_DIM, IDX_DIM, IDX_HEADS)
RWKV_SPLITS = (RWKV_WIDTH, RWKV_WIDTH, RWKV_WIDTH, DECAY_LORA, AAA_LORA, GATE_LORA)
ATTN_IN = sum(ATTN_SPLITS)
RWKV_IN = sum(RWKV_SPLITS)
GATE_IN = 2 * D_MODEL
IN_WIDTH = ATTN_IN + RWKV_IN + GATE_IN

kernel_name = 'hybrid_dsa_rwkv7_gated_block'


def _split(z, sizes):
    return jnp.split(z, [int(o) for o in np.cumsum(sizes)[:-1]], axis=-1)


def _rms_norm(x, gain):
    xf = x.astype(jnp.float32)
    y = xf * lax.rsqrt(jnp.mean(xf * xf, axis=-1, keepdims=True) + RMS_EPS)
    return (y * gain.astype(jnp.float32)).astype(x.dtype)


def _t5_bucket(dist):
    max_exact = REL_BUCKETS // 2
    d = jnp.maximum(dist, 0)
    log_ratio = jnp.log(jnp.maximum(d, 1).astype(jnp.float32) / max_exact) / math.log(REL_MAX_DISTANCE / max_exact)
    large = jnp.minimum(max_exact + (log_ratio * (REL_BUCKETS - max_exact)).astype(jnp.int32), REL_BUCKETS - 1)
    return jnp.where(d < max_exact, d, large)


def _sparse_attention(q, k, v, q_idx, k_idx, w_idx, rel_bias, topk):
    B, S, H, Dh = q.shape
    f32 = jnp.float32
    n_blocks = S // Q_BLOCK
    key_pos = jnp.arange(S, dtype=jnp.int32)
    batch_ix = jnp.arange(B)[:, None, None]
    k_idx32 = k_idx.astype(f32)
    rel_bias32 = rel_bias.astype(f32)
    scale = Dh ** -0.5

    def block(i):
        start = i * Q_BLOCK
        qb = lax.dynamic_slice_in_dim(q, start, Q_BLOCK, axis=1)
        qib = lax.dynamic_slice_in_dim(q_idx, start, Q_BLOCK, axis=1).astype(f32)
        wib = lax.dynamic_slice_in_dim(w_idx, start, Q_BLOCK, axis=1).astype(f32)
        q_pos = start + jnp.arange(Q_BLOCK, dtype=jnp.int32)
        head_scores = jax.nn.relu(jnp.einsum('bqhd,bsd->bqhs', qib, k_idx32) * IDX_DIM ** -0.5)
        scores = jnp.einsum('bqhs,bqh->bqs', head_scores, wib) * IDX_HEADS ** -0.5
        causal = key_pos[None, :] <= q_pos[:, None]
        scores = jnp.where(causal[None], scores, -jnp.inf)
        _, sel = lax.top_k(scores, topk)
        valid = sel <= q_pos[None, :, None]
        k_sel = k[batch_ix, sel]
        v_sel = v[batch_ix, sel]
        logits = jnp.einsum('bqhd,bqkhd->bhqk', qb, k_sel).astype(f32) * scale
        bias = rel_bias32[_t5_bucket(q_pos[None, :, None] - sel)]
        logits = logits + jnp.transpose(bias, (0, 3, 1, 2))
        logits = jnp.where(valid[:, None], logits, -jnp.inf)
        probs = jax.nn.softmax(logits, axis=-1).astype(v.dtype)
        return jnp.einsum('bhqk,bqkhd->bqhd', probs, v_sel)

    out = lax.map(block, jnp.arange(n_blocks, dtype=jnp.int32))
    return jnp.moveaxis(out, 0, 1).reshape(B, S, H * Dh)


def _rwkv7_time_mix(z, mu, w0, w2, a0, a2, g2, k_k, k_a, r_k, ln_w, ln_b):
    B, S, _ = z.shape
    H, N = RWKV_HEADS, RWKV_HEAD_DIM
    f32 = jnp.float32
    z = z.astype(f32)
    z_prev = jnp.pad(z, ((0, 0), (1, 0), (0, 0)))[:, :-1]
    z = z + mu * (z_prev - z)
    r, k, v, w_lo, a_lo, g_lo = _split(z, RWKV_SPLITS)
    w = -jax.nn.softplus(-(w0 + jnp.tanh(w_lo) @ w2)) - 0.5
    a = jax.nn.sigmoid(a0 + a_lo @ a2)
    g = jax.nn.sigmoid(g_lo) @ g2
    kk = (k * k_k).reshape(B, S, H, N)
    kk = kk / jnp.maximum(jnp.sqrt(jnp.sum(kk * kk, axis=-1, keepdims=True)), 1e-12)
    k = k * (1.0 + (a - 1.0) * k_a)

    def heads(t):
        return t.reshape(B, S, H, N)

    r_h, k_h, v_h, a_h = heads(r), heads(k), heads(v), heads(a)
    decay = jnp.exp(-jnp.exp(heads(w)))
    b_h = kk * a_h

    def step(state, inp):
        r_t, d_t, k_t, v_t, kk_t, b_t = inp
        s_a = jnp.einsum('bhvk,bhk->bhv', state, -kk_t)
        state = state * d_t[:, :, None, :] + s_a[..., None] * b_t[:, :, None, :] + v_t[..., None] * k_t[:, :, None, :]
        return state, jnp.einsum('bhvk,bhk->bhv', state, r_t)

    xs = tuple(jnp.moveaxis(t, 1, 0) for t in (r_h, decay, k_h, v_h, kk, b_h))
    _, y = lax.scan(step, jnp.zeros((B, H, N, N), f32), xs)
    y = jnp.moveaxis(y, 0, 1)
    mean = jnp.mean(y, axis=-1, keepdims=True)
    var = jnp.mean(jnp.square(y - mean), axis=-1, keepdims=True)
    y = ((y - mean) * lax.rsqrt(var + GN_EPS)).reshape(B, S, RWKV_WIDTH) * ln_w + ln_b
    bonus = jnp.sum(r_h * k_h * r_k, axis=-1, keepdims=True) * v_h
    y = y + bonus.reshape(B, S, RWKV_WIDTH)
    return y * g


def setup_inputs(seed: int = 0) -> dict:
    key = jax.random.key(seed)
    ks = jax.random.split(key, 24)
    L = DEPTH

    def nrm(k, shape, scale):
        return jax.random.normal(k, shape, jnp.float32) * scale

    return {
        'x': nrm(ks[0], (BATCH, SEQ, D_MODEL), 1.0),
        'mix_norm': 1.0 + nrm(ks[1], (L, D_MODEL), 0.02),
        'w_in': nrm(ks[2], (L, D_MODEL, IN_WIDTH), D_MODEL ** -0.5),
        'attn_q_norm': 1.0 + nrm(ks[3], (L, ATTN_HEAD_DIM), 0.02),
        'attn_k_norm': 1.0 + nrm(ks[4], (L, ATTN_HEAD_DIM), 0.02),
        'rel_bias': nrm(ks[5], (REL_BUCKETS, ATTN_HEADS), 0.5),
        'rwkv_mu': jax.random.uniform(ks[6], (L, RWKV_IN), jnp.float32, 0.0, 1.0),
        'rwkv_w0': nrm(ks[7], (L, RWKV_WIDTH), 0.5),
        'rwkv_w2': nrm(ks[8], (L, DECAY_LORA, RWKV_WIDTH), 0.1),
        'rwkv_a0': nrm(ks[9], (L, RWKV_WIDTH), 0.1),
        'rwkv_a2': nrm(ks[10], (L, AAA_LORA, RWKV_WIDTH), 0.1),
        'rwkv_g2': nrm(ks[11], (L, GATE_LORA, RWKV_WIDTH), GATE_LORA ** -0.5),
        'rwkv_k_k': 0.85 + nrm(ks[12], (L, RWKV_WIDTH), 0.02),
        'rwkv_k_a': 1.0 + nrm(ks[13], (L, RWKV_WIDTH), 0.02),
        'rwkv_r_k': nrm(ks[14], (L, RWKV_HEADS, RWKV_HEAD_DIM), 0.1),
        'rwkv_ln_w': 1.0 + nrm(ks[15], (L, RWKV_WIDTH), 0.02),
        'rwkv_ln_b': nrm(ks[16], (L, RWKV_WIDTH), 0.02),
        'w_branch_attn': nrm(ks[17], (L, ATTN_WIDTH, D_MODEL), ATTN_WIDTH ** -0.5),
        'w_branch_rwkv': nrm(ks[18], (L, RWKV_WIDTH, D_MODEL), RWKV_WIDTH ** -0.5),
        'w_out': nrm(ks[19], (L, D_MODEL, D_MODEL), D_MODEL ** -0.5),
        'ffn_norm': 1.0 + nrm(ks[20], (L, D_MODEL), 0.02),
        'w_gate_up': nrm(ks[21], (L, D_MODEL, 2 * FFN_HIDDEN), D_MODEL ** -0.5),
        'w_down': nrm(ks[22], (L, FFN_HIDDEN, D_MODEL), FFN_HIDDEN ** -0.5),
    }


def reference(x, mix_norm, w_in, attn_q_norm, attn_k_norm, rel_bias, rwkv_mu, rwkv_w0, rwkv_w2,
              rwkv_a0, rwkv_a2, rwkv_g2, rwkv_k_k, rwkv_k_a, rwkv_r_k, rwkv_ln_w, rwkv_ln_b,
              w_branch_attn, w_branch_rwkv, w_out, ffn_norm, w_gate_up, w_down):
    B, S, _ = x.shape
    topk = min(TOPK_MAX, S // 4)
    h = x
    for l in range(DEPTH):
        xn = _rms_norm(h, mix_norm[l])
        proj = xn @ w_in[l]
        attn_cols, rwkv_cols, gate_cols = _split(proj, (ATTN_IN, RWKV_IN, GATE_IN))
        q, k, v, q_idx, k_idx, w_idx = _split(attn_cols, ATTN_SPLITS)
        q = _rms_norm(q.reshape(B, S, ATTN_HEADS, ATTN_HEAD_DIM), attn_q_norm[l])
        k = _rms_norm(k.reshape(B, S, ATTN_HEADS, ATTN_HEAD_DIM), attn_k_norm[l])
        v = v.reshape(B, S, ATTN_HEADS, ATTN_HEAD_DIM)
        y_attn = _sparse_attention(q, k, v, q_idx.reshape(B, S, IDX_HEADS, IDX_DIM), k_idx, w_idx,
                                   rel_bias, topk)
        y_rwkv = _rwkv7_time_mix(rwkv_cols, rwkv_mu[l], rwkv_w0[l], rwkv_w2[l], rwkv_a0[l], rwkv_a2[l],
                                 rwkv_g2[l], rwkv_k_k[l], rwkv_k_a[l], rwkv_r_k[l], rwkv_ln_w[l],
                                 rwkv_ln_b[l]).astype(x.dtype)
        gate_attn, gate_rwkv = _split(gate_cols, (D_MODEL, D_MODEL))
        merged = (jax.nn.sigmoid(gate_attn) * (y_attn @ w_branch_attn[l])
                  + jax.nn.sigmoid(gate_rwkv) * (y_rwkv @ w_branch_rwkv[l]))
        h = h + merged @ w_out[l]
        hn = _rms_norm(h, ffn_norm[l])
        gate, up = _split(hn @ w_gate_up[l], (FFN_HIDDEN, FFN_HIDDEN))
        h = h + (jax.nn.silu(gate) * up) @ w_down[l]
    return h
```

```python
import math
from contextlib import ExitStack

import numpy as np
import concourse.bass as bass
import concourse.mybir as mybir
from concourse.bass_utils import run_bass_kernel_spmd

F32 = mybir.dt.float32
BF16 = mybir.dt.bfloat16
AF = mybir.ActivationFunctionType
ALU = mybir.AluOpType
AX = mybir.AxisListType

S = 4096
D = 1024
NT = S // 128
ATTN_IN = 2120
RWKV_IN = 1824
FFN_H = 2816
RMS_EPS = 1e-6
GN_EPS = 64e-5
TOPK = 256
NEG = -1.0e30

ENGS = ("pe", "act", "dve", "pool", "sp")


class Buf:
    __slots__ = ("name", "w", "r")

    def __init__(self, name=""):
        self.name = name
        self.w = None
        self.r = []


class Op:
    __slots__ = ("eng", "thunk", "deps", "is_dma", "sem", "val", "need_inc", "pos", "prev_on_sem")


class Prog:
    def __init__(self, nc, n_dma_sems=40):
        self.nc = nc
        self.streams = {e: [] for e in ENGS}
        self.n_dma_sems = n_dma_sems
        self.dma_count = 0
        self.all_ops = []
        self.open_dmas = []

    def _hazards(self, op, reads, writes):
        deps = []
        for b in reads:
            if b.w is not None:
                deps.append(b.w)
        for b in writes:
            if b.w is not None:
                deps.append(b.w)
            deps.extend(b.r)
        for b in reads:
            b.r.append(op)
        for b in writes:
            b.w = op
            b.r = []
        return deps

    def op(self, eng, thunk, reads=(), writes=(), extra_deps=()):
        o = Op()
        o.eng = eng
        o.thunk = thunk
        o.is_dma = False
        o.need_inc = False
        o.sem = None
        o.val = None
        o.prev_on_sem = None
        deps = self._hazards(o, reads, writes) + list(extra_deps)
        seen = set()
        o.deps = []
        for d in deps:
            if d is o or id(d) in seen:
                continue
            if eng == "pe" and d.eng == "pe" and not d.is_dma:
                continue
            seen.add(id(d))
            o.deps.append(d)
        self.streams[eng].append(o)
        self.all_ops.append(o)
        return o

    def dma(self, eng, thunk, reads=(), writes=(), extra_deps=()):
        o = self.op(eng, thunk, reads, writes, extra_deps)
        o.is_dma = True
        o.pos = self.dma_count
        self.dma_count += 1
        self.open_dmas.append(o)
        return o

    def barrier(self):
        lasts = []
        for e in ENGS:
            for o in reversed(self.streams[e]):
                if not o.is_dma:
                    lasts.append(o)
                    break
        deps = lasts + self.open_dmas
        self.open_dmas = []
        for e in ENGS:
            self.op(e, lambda eng: eng.nop(), extra_deps=deps)

    def emit(self, final_wait_ops=()):
        nc = self.nc
        for o in self.all_ops:
            for d in o.deps:
                d.need_inc = True
        eng_sems = {e: nc.alloc_semaphore("s_" + e) for e in ENGS}
        dma_sems = [nc.alloc_semaphore("s_dma%d" % i) for i in range(self.n_dma_sems)]
        dma_sem_val = [0] * self.n_dma_sems
        dma_prev = [None] * self.n_dma_sems
        cnt = {e: 0 for e in ENGS}
        for o in self.all_ops:
            if o.is_dma:
                k = o.pos % self.n_dma_sems
                dma_sem_val[k] += 16
                o.sem = ("dma", k)
                o.val = dma_sem_val[k]
                o.prev_on_sem = dma_prev[k]
                dma_prev[k] = o
            elif o.need_inc:
                cnt[o.eng] += 1
                o.sem = ("eng", o.eng)
                o.val = cnt[o.eng]

        def semh(key):
            return eng_sems[key[1]] if key[0] == "eng" else dma_sems[key[1]]

        engines = {"pe": "tensor", "act": "scalar", "dve": "vector", "pool": "gpsimd", "sp": "sync"}
        with nc.Block() as block:
            for e in ENGS:
                stream = self.streams[e]
                final = list(final_wait_ops) if e == "sp" else []

                def body(engine, stream=stream, final=final):
                    known = {}
                    for o in stream:
                        waits = {}
                        deps = list(o.deps)
                        if o.is_dma and o.prev_on_sem is not None:
                            deps.append(o.prev_on_sem)
                        for d in deps:
                            if known.get(d.sem, 0) >= d.val:
                                continue
                            if waits.get(d.sem, 0) < d.val:
                                waits[d.sem] = d.val
                        for key, val in waits.items():
                            engine.wait_ge(semh(key), val)
                            known[key] = val
                        ins = o.thunk(engine)
                        if o.is_dma:
                            ins.then_inc(semh(o.sem), 16)
                        elif o.need_inc:
                            ins.then_inc(semh(o.sem), 1)
                    for o in final:
                        engine.wait_ge(semh(o.sem), o.val)

                getattr(block, engines[e])(body)


class Ring:
    def __init__(self, items):
        self.items = items
        self.i = 0

    def next(self):
        it = self.items[self.i % len(self.items)]
        self.i += 1
        return it


def load_weight_bf16(P, nc, dst, dst_buf, w_ap, c0, c1, kchunks, eng="pool"):
    wv = w_ap.rearrange("(kc p) n -> p kc n", p=128)
    for kc in range(kchunks):
        for a in range(c0, c1, 2048):
            b = min(c1, a + 2048)
            P.dma(eng, lambda e, kc=kc, a=a, b=b: e.dma_start(out=dst[:, kc, a - c0:b - c0], in_=wv[:, kc, a:b]),
                  writes=[dst_buf])


def load_col_vec(P, nc, dst, dst_buf, v_ap, n):
    src = v_ap.rearrange("o (c p) -> p (o c)", p=128)
    P.dma("sp", lambda e: e.dma_start(out=dst, in_=src, allow_slow_non_contiguous=True), writes=[dst_buf])


class Consts:
    pass


def make_consts(P, nc, es):
    C = Consts()
    C.ident_bf = es.enter_context(nc.sbuf_tensor("ident_bf", [128, 128], BF16))
    C.ident_f = es.enter_context(nc.sbuf_tensor("ident_f", [128, 128], F32))
    C.eps = es.enter_context(nc.sbuf_tensor("eps_c", [128, 2], F32))
    C.b = Buf("consts")

    def mk(e):
        e.memset(C.ident_f[:], 0.0)
        e.affine_select(out=C.ident_f[:], in_=C.ident_f[:], pattern=[[-1, 128]], compare_op=ALU.not_equal,
                        fill=1.0, base=0, channel_multiplier=1)
        e.memset(C.eps[:, 0:1], RMS_EPS)
        return e.memset(C.eps[:, 1:2], GN_EPS)

    P.op("pool", mk, writes=[C.b])
    P.op("pool", lambda e: e.tensor_copy(out=C.ident_bf[:], in_=C.ident_f[:]), reads=[C.b], writes=[C.b])
    return C


class Normer:
    def __init__(self, P, nc, es, C, gain_ap, name, nslots=2):
        self.P, self.nc, self.C = P, nc, C
        self.gcol = es.enter_context(nc.sbuf_tensor(name + "_g", [128, 8], F32))
        self.gb = Buf(name + "_g")
        load_col_vec(P, nc, self.gcol[:, :], self.gb, gain_ap, 8)
        self.stat = es.enter_context(nc.sbuf_tensor(name + "_st", [128, nslots, 4], F32))
        self.junk = es.enter_context(nc.sbuf_tensor(name + "_junk", [128, 1024], BF16))
        self.xs = es.enter_context(nc.sbuf_tensor(name + "_xs", [128, nslots, 1024], BF16))
        self.tp = es.enter_context(nc.psum_tensor(name + "_tp", [128, nslots, 8, 128], BF16))
        self.ring = Ring([(i, Buf(), Buf(), Buf()) for i in range(nslots)])
        self.junkb = Buf()

    def run(self, xt_ap, xt_buf, dst, dst_buf, col0):
        P, C = self.P, self.C
        i, sb, xb, pb = self.ring.next()
        st = self.stat
        P.op("act", lambda e: e.activation(out=self.junk[:], in_=xt_ap, func=AF.Square, accum_out=st[:, i, 0:1]),
             reads=[xt_buf], writes=[self.junkb, sb])
        P.op("act", lambda e: e.activation(out=st[:, i, 1:2], in_=st[:, i, 0:1], func=AF.Sqrt, scale=1.0 / D,
                                           bias=C.eps[:, 0:1]), reads=[sb, C.b], writes=[sb])
        P.op("dve", lambda e: e.reciprocal(out=st[:, i, 2:3], in_=st[:, i, 1:2]), reads=[sb], writes=[sb])
        P.op("act", lambda e: e.activation(out=self.xs[:, i, :], in_=xt_ap, func=AF.Copy, scale=st[:, i, 2:3]),
             reads=[xt_buf, sb], writes=[xb])

        def tr(e):
            ins = None
            for kc in range(8):
                ins = e.transpose(out=self.tp[:, i, kc, :], in_=self.xs[:, i, kc * 128:(kc + 1) * 128],
                                  identity=C.ident_bf[:])
            return ins

        P.op("pe", tr, reads=[xb, C.b], writes=[pb])
        P.op("dve", lambda e: e.tensor_tensor(out=dst[:, :, col0:col0 + 128], in0=self.tp[:, i, :, :],
                                              in1=self.gcol[:, :].unsqueeze(2).broadcast_to([128, 8, 128]),
                                              op=ALU.mult), reads=[pb, self.gb], writes=[dst_buf])


def phase_ffn(P, nc, C, h_dram, out_dram, ffn_norm, w_gate_up, w_down, final_ops):
    with ExitStack() as es:
        wgu = es.enter_context(nc.sbuf_tensor("wgu", [128, 8, 2 * FFN_H], BF16))
        wd = es.enter_context(nc.sbuf_tensor("wd", [128, 22, D], BF16))
        wgu_b, wd_b = Buf("wgu"), Buf("wd")
        load_weight_bf16(P, nc, wgu, wgu_b, w_gate_up, 0, 2 * FFN_H, 8)
        load_weight_bf16(P, nc, wd, wd_b, w_down, 0, D, 22)
        nm = Normer(P, nc, es, C, ffn_norm, "fn")
        hbuf = es.enter_context(nc.sbuf_tensor("hbuf", [128, 2, D], F32))
        hr = Ring([(i, Buf()) for i in range(2)])
        hnT = es.enter_context(nc.sbuf_tensor("hnT", [128, 2, 8, 512], BF16))
        hnb = [Buf(), Buf()]
        actT = es.enter_context(nc.sbuf_tensor("actT", [128, 22, 512], BF16))
        actb = [Buf() for _ in range(22)]
        sg = es.enter_context(nc.sbuf_tensor("sg", [128, 2, 512], F32))
        sgr = Ring([(i, Buf()) for i in range(2)])
        ost = es.enter_context(nc.sbuf_tensor("ost", [128, 2, D], F32))
        ostr = Ring([(i, Buf()) for i in range(2)])
        pg = es.enter_context(nc.psum_tensor("pg", [128, 2, 512], F32))
        pu = es.enter_context(nc.psum_tensor("pu", [128, 2, 512], F32))
        po = es.enter_context(nc.psum_tensor("po", [128, 2, 512], F32))
        pgr = Ring([(i, Buf()) for i in range(2)])
        pur = Ring([(i, Buf()) for i in range(2)])
        por = Ring([(i, Buf()) for i in range(2)])
        hview = h_dram.rearrange("(t p) d -> t p d", p=128)
        oview = out_dram.rearrange("(t p) d -> t p d", p=128)
        for c in range(S // 512):
            cb = c % 2
            for j in range(4):
                t = c * 4 + j
                hi, hb_ = hr.next()
                P.dma("sp", lambda e, t=t, hi=hi: e.dma_start(out=hbuf[:, hi, :], in_=hview[t]), writes=[hb_])
                nm.run(hbuf[:, hi, :], hb_, hnT[:, cb], hnb[cb], j * 128)
            for m in range(22):
                gi, gbuf = pgr.next()
                ui, ubuf = pur.next()

                def mm(e, m=m, gi=gi, ui=ui, cb=cb):
                    for kc in range(8):
                        e.matmul(pg[:, gi, :], lhsT=wgu[:, kc, m * 128:(m + 1) * 128], rhs=hnT[:, cb, kc, :],
                                 start=(kc == 0), stop=(kc == 7))
                    ins = None
                    for kc in range(8):
                        ins = e.matmul(pu[:, ui, :], lhsT=wgu[:, kc, FFN_H + m * 128:FFN_H + (m + 1) * 128],
                                       rhs=hnT[:, cb, kc, :], start=(kc == 0), stop=(kc == 7))
                    return ins

                P.op("pe", mm, reads=[wgu_b, hnb[cb]], writes=[gbuf, ubuf])
                si, sbuf_ = sgr.next()
                P.op("act", lambda e, gi=gi, si=si: e.activation(out=sg[:, si, :], in_=pg[:, gi, :], func=AF.Silu),
                     reads=[gbuf], writes=[sbuf_])
                P.op("dve", lambda e, ui=ui, si=si, m=m: e.tensor_tensor(out=actT[:, m, :], in0=pu[:, ui, :],
                                                                         in1=sg[:, si, :], op=ALU.mult),
                     reads=[ubuf, sbuf_], writes=[actb[m]])
            for j in range(4):
                t = c * 4 + j
                oi, obuf = ostr.next()
                P.dma("sp", lambda e, t=t, oi=oi: e.dma_start(out=ost[:, oi, :], in_=hview[t]), writes=[obuf])
                for nh in range(2):
                    pi, pbuf = por.next()

                    def mmd(e, j=j, nh=nh, pi=pi):
                        ins = None
                        for m in range(22):
                            ins = e.matmul(po[:, pi, :], lhsT=actT[:, m, j * 128:(j + 1) * 128],
                                           rhs=wd[:, m, nh * 512:(nh + 1) * 512], start=(m == 0), stop=(m == 21))
                        return ins

                    P.op("pe", mmd, reads=actb + [wd_b], writes=[pbuf])
                    P.op("dve", lambda e, nh=nh, pi=pi, oi=oi: e.tensor_tensor(
                        out=ost[:, oi, nh * 512:(nh + 1) * 512], in0=po[:, pi, :],
                        in1=ost[:, oi, nh * 512:(nh + 1) * 512], op=ALU.add),
                        reads=[pbuf], writes=[obuf])
                final_ops.append(P.dma("sp", lambda e, t=t, oi=oi: e.dma_start(out=oview[t], in_=ost[:, oi, :]),
                                       reads=[obuf]))
    P.barrier()


def phase_merge(P, nc, C, x_dram, ya_dram, yr_dram, h_dram, mix_norm, w_in, w_ba, w_br, w_out):
    with ExitStack() as es:
        wg = es.enter_context(nc.sbuf_tensor("wg", [128, 8, 2 * D], BF16))
        wba = es.enter_context(nc.sbuf_tensor("wba", [128, 4, D], BF16))
        wbr = es.enter_context(nc.sbuf_tensor("wbr", [128, 4, D], BF16))
        wo = es.enter_context(nc.sbuf_tensor("wo", [128, 8, D], BF16))
        wg_b, wba_b, wbr_b, wo_b = Buf(), Buf(), Buf(), Buf()
        load_weight_bf16(P, nc, wg, wg_b, w_in, ATTN_IN + RWKV_IN, ATTN_IN + RWKV_IN + 2 * D, 8)
        load_weight_bf16(P, nc, wba, wba_b, w_ba, 0, D, 4)
        load_weight_bf16(P, nc, wbr, wbr_b, w_br, 0, D, 4)
        load_weight_bf16(P, nc, wo, wo_b, w_out, 0, D, 8)
        nm = Normer(P, nc, es, C, mix_norm, "mn")
        xbuf = es.enter_context(nc.sbuf_tensor("xbuf", [128, 2, 4, D], F32))
        xb = [[Buf() for _ in range(4)] for _ in range(2)]
        xnT = es.enter_context(nc.sbuf_tensor("xnT", [128, 2, 8, 512], BF16))
        xnb = [Buf(), Buf()]
        yaT = es.enter_context(nc.sbuf_tensor("yaT", [128, 2, 4, 512], BF16))
        yrT = es.enter_context(nc.sbuf_tensor("yrT", [128, 2, 4, 512], BF16))
        yab, yrb = [Buf(), Buf()], [Buf(), Buf()]
        mT = es.enter_context(nc.sbuf_tensor("mT", [128, 8, 512], BF16))
        mb = [Buf() for _ in range(8)]
        sg = es.enter_context(nc.sbuf_tensor("sgm", [128, 2, 2, 512], F32))
        sgr = Ring([(i, Buf()) for i in range(2)])
        tt = es.enter_context(nc.sbuf_tensor("ttm", [128, 2, 2, 512], F32))
        ttr = Ring([(i, Buf()) for i in range(2)])
        hst = es.enter_context(nc.sbuf_tensor("hst", [128, 2, D], F32))
        hstr = Ring([(i, Buf()) for i in range(2)])
        pga = es.enter_context(nc.psum_tensor("pga", [128, 2, 512], F32))
        pbr = es.enter_context(nc.psum_tensor("pbr", [128, 2, 512], F32))
        po = es.enter_context(nc.psum_tensor("pom", [128, 2, 512], F32))
        pgb, pbb = Buf(), Buf()
        por = Ring([(i, Buf()) for i in range(2)])
        xview = x_dram.rearrange("(t p) d -> t p d", p=128)
        hview = h_dram.rearrange("(t p) d -> t p d", p=128)
        yav = ya_dram.rearrange("(kc p) s -> p kc s", p=128)
        yrv = yr_dram.rearrange("(kc p) s -> p kc s", p=128)
        for c in range(S // 512):
            cb = c % 2
            P.dma("sp", lambda e, c=c, cb=cb: e.dma_start(out=yaT[:, cb], in_=yav[:, :, c * 512:(c + 1) * 512]),
                  writes=[yab[cb]])
            P.dma("sp", lambda e, c=c, cb=cb: e.dma_start(out=yrT[:, cb], in_=yrv[:, :, c * 512:(c + 1) * 512]),
                  writes=[yrb[cb]])
            for j in range(4):
                t = c * 4 + j
                P.dma("sp", lambda e, t=t, cb=cb, j=j: e.dma_start(out=xbuf[:, cb, j, :], in_=xview[t]),
                      writes=[xb[cb][j]])
                nm.run(xbuf[:, cb, j, :], xb[cb][j], xnT[:, cb], xnb[cb], j * 128)
            for m in range(8):
                def mmg(e, m=m, cb=cb):
                    ins = None
                    for g in range(2):
                        for kc in range(8):
                            ins = e.matmul(pga[:, g, :], lhsT=wg[:, kc, g * D + m * 128:g * D + (m + 1) * 128],
                                           rhs=xnT[:, cb, kc, :], start=(kc == 0), stop=(kc == 7))
                    return ins

                P.op("pe", mmg, reads=[wg_b, xnb[cb]], writes=[pgb])

                def mmb(e, m=m, cb=cb):
                    ins = None
                    for kc in range(4):
                        ins = e.matmul(pbr[:, 0, :], lhsT=wba[:, kc, m * 128:(m + 1) * 128], rhs=yaT[:, cb, kc, :],
                                       start=(kc == 0), stop=(kc == 3))
                    for kc in range(4):
                        ins = e.matmul(pbr[:, 1, :], lhsT=wbr[:, kc, m * 128:(m + 1) * 128], rhs=yrT[:, cb, kc, :],
                                       start=(kc == 0), stop=(kc == 3))
                    return ins

                P.op("pe", mmb, reads=[wba_b, wbr_b, yab[cb], yrb[cb]], writes=[pbb])
                si, sbuf_ = sgr.next()
                P.op("act", lambda e, si=si: e.activation(out=sg[:, si], in_=pga[:, :, :], func=AF.Sigmoid),
                     reads=[pgb], writes=[sbuf_])
                ti, tbuf = ttr.next()
                P.op("dve", lambda e, si=si, ti=ti: e.tensor_tensor(out=tt[:, ti], in0=pbr[:, :, :], in1=sg[:, si],
                                                                    op=ALU.mult),
                     reads=[pbb, sbuf_], writes=[tbuf])
                P.op("pool", lambda e, ti=ti, m=m: e.tensor_tensor(out=mT[:, m, :], in0=tt[:, ti, 0, :],
                                                                   in1=tt[:, ti, 1, :], op=ALU.add),
                     reads=[tbuf], writes=[mb[m]])
            for j in range(4):
                t = c * 4 + j
                hi, hbuf_ = hstr.next()
                for nh in range(2):
                    pi, pbuf = por.next()

                    def mmo(e, j=j, nh=nh, pi=pi):
                        ins = None
                        for m in range(8):
                            ins = e.matmul(po[:, pi, :], lhsT=mT[:, m, j * 128:(j + 1) * 128],
                                           rhs=wo[:, m, nh * 512:(nh + 1) * 512], start=(m == 0), stop=(m == 7))
                        return ins

                    P.op("pe", mmo, reads=mb + [wo_b], writes=[pbuf])
                    P.op("dve", lambda e, j=j, nh=nh, pi=pi, hi=hi, cb=cb: e.tensor_tensor(
                        out=hst[:, hi, nh * 512:(nh + 1) * 512], in0=po[:, pi, :],
                        in1=xbuf[:, cb, j, nh * 512:(nh + 1) * 512], op=ALU.add),
                        reads=[pbuf, xb[cb][j]], writes=[hbuf_])
                P.dma("sp", lambda e, t=t, hi=hi: e.dma_start(out=hview[t], in_=hst[:, hi, :]), reads=[hbuf_])
    P.barrier()


TL = 128
C0 = math.exp(-0.5)


def col8(P, nc, es, name, v_ap):
    t = es.enter_context(nc.sbuf_tensor(name, [64, 8], F32))
    b = Buf(name)
    P.dma("sp", lambda e: e.dma_start(out=t[:, :], in_=v_ap.rearrange("o (h k) -> k (o h)", k=64),
                                      allow_slow_non_contiguous=True), writes=[b])
    return t, b


def phase_rwkv(P, nc, C, A, yr_dram):
    x_dram = A["x"]
    with ExitStack() as es:
        sb = lambda name, shape, dt=F32: es.enter_context(nc.sbuf_tensor(name, shape, dt))
        wr = sb("wr", [128, 8, RWKV_IN], BF16)
        wmu = sb("wmu", [128, 8, RWKV_IN], BF16)
        mub = sb("mub", [128, RWKV_IN], F32)
        wr_b, wmu_b, mub_b = Buf(), Buf(), Buf()
        load_weight_bf16(P, nc, wr, wr_b, A["w_in"], ATTN_IN, ATTN_IN + RWKV_IN, 8)
        P.dma("sp", lambda e: e.dma_start(out=mub[:], in_=A["rwkv_mu"].partition_broadcast(128)), writes=[mub_b])
        for kc in range(8):
            P.op("pool", lambda e, kc=kc: e.tensor_tensor(out=wmu[:, kc, :], in0=wr[:, kc, :], in1=mub[:],
                                                          op=ALU.mult), reads=[wr_b, mub_b], writes=[wmu_b])
        w2 = sb("w2", [64, 512], BF16)
        a2 = sb("a2", [64, 512], BF16)
        g2 = sb("g2", [64, 3, 512], BF16)
        lw_b = Buf()
        P.dma("pool", lambda e: e.dma_start(out=w2[:], in_=A["rwkv_w2"]), writes=[lw_b])
        P.dma("pool", lambda e: e.dma_start(out=a2[:], in_=A["rwkv_a2"]), writes=[lw_b])
        P.dma("pool", lambda e: e.dma_start(out=g2[:, 0, :], in_=A["rwkv_g2"][0:64, :]), writes=[lw_b])
        P.dma("pool", lambda e: e.dma_start(out=g2[:, 1, :], in_=A["rwkv_g2"][64:128, :]), writes=[lw_b])
        P.dma("pool", lambda e: e.dma_start(out=g2[0:32, 2, :], in_=A["rwkv_g2"][128:160, :]), writes=[lw_b])
        cw0, b_w0 = col8(P, nc, es, "cw0", A["rwkv_w0"])
        ca0, b_a0 = col8(P, nc, es, "ca0", A["rwkv_a0"])
        ckk, b_kk = col8(P, nc, es, "ckk", A["rwkv_k_k"])
        cka, b_ka = col8(P, nc, es, "cka", A["rwkv_k_a"])
        crk, b_rk = col8(P, nc, es, "crk", A["rwkv_r_k"])
        clw, b_lw = col8(P, nc, es, "clw", A["rwkv_ln_w"])
        clb, b_lb = col8(P, nc, es, "clb", A["rwkv_ln_b"])
        msk = sb("rmsk", [64, 3, 64], F32)
        ones_bf = sb("ones_bf", [64, 64], BF16)
        rmask = sb("rmask", [64, 8, TL // 64, 64], F32)
        mk_b = Buf()

        def mkmasks(e):
            e.memset(msk[:], 1.0)
            e.memset(ones_bf[:], 1.0)
            e.memset(rmask[:], 1.0)
            e.memset(rmask[:, :, :, 0:1], 0.0)
            e.affine_select(out=msk[:, 0, :], in_=msk[:, 0, :], pattern=[[1, 64]], compare_op=ALU.is_ge,
                            fill=0.0, base=-1, channel_multiplier=-1)
            e.affine_select(out=msk[:, 1, :], in_=msk[:, 1, :], pattern=[[1, 64]], compare_op=ALU.is_ge,
                            fill=0.0, base=0, channel_multiplier=-1)
            return e.affine_select(out=msk[:, 2, :], in_=msk[:, 2, :], pattern=[[-1, 64]], compare_op=ALU.is_ge,
                                   fill=0.0, base=-1, channel_multiplier=1)

        P.op("pool", mkmasks, writes=[mk_b])
        mb3 = lambda i: msk[:, i, :].unsqueeze(1).broadcast_to([64, 8, 64])
        idb = C.ident_bf[0:64, 0:64]
        idf3 = C.ident_f[0:64, 0:64].unsqueeze(1).broadcast_to([64, 8, 64])
        bc = lambda col: col[:, :].unsqueeze(2).broadcast_to([64, 8, TL])

        nm = Normer(P, nc, es, C, A["mix_norm"], "rn", nslots=1)
        xbuf = sb("rxbuf", [128, 2, D], F32)
        xr = Ring([(i, Buf()) for i in range(2)])
        xnx = sb("xnx", [128, 2, 8, TL + 1], BF16)
        xnb = [Buf(), Buf()]
        dxn = sb("dxn", [128, 8, TL], BF16)
        dxb = Buf()
        P.op("pool", lambda e: e.memset(xnx[:, 1, :, TL:TL + 1], 0.0), writes=[xnb[1]])
        F = {}
        FB = {}
        for nme, dt in [("r", F32), ("k", F32), ("v", F32), ("sgd", F32), ("a", F32), ("g", BF16), ("kk", F32),
                        ("t1", F32), ("t2", F32), ("cum", F32), ("e1", F32), ("e2", F32),
                        ("rT", BF16), ("aT", BF16), ("bT", BF16), ("kT", BF16), ("bH", BF16), ("kH", BF16),
                        ("vb", BF16), ("sqb", BF16), ("bon", F32), ("ynT", F32)]:
            alias = {"e1": "t1", "e2": "a", "bon": "sgd", "ynT": "kk"}
            if nme in alias:
                F[nme] = F[alias[nme]]
                FB[nme] = FB[alias[nme]]
                continue
            F[nme] = sb("f_" + nme, [64, 8, TL], dt)
            FB[nme] = Buf(nme)
        lora = sb("lora", [64, 5, TL], BF16)
        lora_b = Buf()
        gC = sb("gC", [64, 8, TL // 64], F32)
        gC_b = Buf()
        Sf = sb("Sf", [64, 8, 64], F32)
        Sb = sb("Sb", [64, 8, 64], BF16)
        St = sb("St", [64, 8, 64], F32)
        S_b, St_b = Buf(), Buf()
        P.op("dve", lambda e: e.memset(Sf[:], 0.0), writes=[S_b])
        P.op("dve", lambda e: e.memset(Sb[:], 0.0), reads=[S_b], writes=[S_b])
        ost = sb("rost", [64, 8, TL], BF16)
        ost_b = Buf()
        def ring2(name, dt=BF16, shape=(64, 8, 64)):
            t = sb(name, [shape[0], 2] + list(shape[1:]), dt)
            return t, Ring([(i, Buf()) for i in range(2)])
        Atok, Atok_r = ring2("Atok")
        BHtok, BHtok_r = ring2("BHtok")
        KHtok, KHtok_r = ring2("KHtok")
        Vtok, Vtok_r = ring2("Vtok")
        Mrb, Mrb_r = ring2("Mrb")
        Mrk, Mrk_r = ring2("Mrk")
        Lak, Lak_r = ring2("Lak")
        Nn, Nn_r = ring2("Nn")
        Mm, Mm_r = ring2("Mm")
        Qq, Qq_r = ring2("Qq")
        WT, WT_r = ring2("WT")
        Xx, Xx_r = ring2("Xx")
        Uu, Uu_r = ring2("Uu")
        Ys, Ys_r = ring2("Ys", F32)
        Yq, Yq_r = ring2("Yq", F32)
        Yn, Yn_r = ring2("Yn", BF16)
        gst, gst_r = ring2("gst", F32, (64, 8, 6))
        pb = es.enter_context(nc.psum_tensor("rpb", [128, 7, 512], F32))
        pr = Ring([(i, Buf()) for i in range(7)])
        pv = lambda i: pb[0:64, i, :].rearrange("p (h t) -> p h t", h=8)
        pvb = lambda i: pb[0:64, i, :].bitcast(BF16)[:, 0:512].rearrange("p (h t) -> p h t", h=8)

        xview = x_dram.rearrange("(t p) d -> t p d", p=128)

        def mm8(bank, parts, rd):
            i, bbuf = bank
            ops = [[(lf(h), rf(h)) for (lf, rf) in parts] for h in range(8)]

            def th(e):
                ins = None
                for h in range(8):
                    for pi, (l, r) in enumerate(ops[h]):
                        ins = e.matmul(pb[0:64, i, h * 64:(h + 1) * 64], lhsT=l, rhs=r,
                                       start=(pi == 0), stop=(pi == len(ops[h]) - 1))
                return ins

            P.op("pe", th, reads=rd, writes=[bbuf])

        def tr8(bank, src_fn, rd):
            i, bbuf = bank
            srcs = [src_fn(h) for h in range(8)]

            def th(e):
                ins = None
                v = pvb(i)
                for h in range(8):
                    ins = e.transpose(out=v[:, h, :], in_=srcs[h], identity=idb)
                return ins

            P.op("pe", th, reads=rd + [C.b], writes=[bbuf])

        for c5 in range(S // TL):
            cb = c5 % 2
            pc = 1 - cb
            P.op("pool", lambda e, cb=cb, pc=pc: e.tensor_copy(out=xnx[:, cb, :, 0:1], in_=xnx[:, pc, :, TL:TL + 1]),
                 reads=[xnb[pc]], writes=[xnb[cb]])
            for j in range(TL // 128):
                xi, xb_ = xr.next()
                P.dma("sp", lambda e, t=c5 * (TL // 128) + j, xi=xi: e.dma_start(out=xbuf[:, xi, :], in_=xview[t]), writes=[xb_])
                nm.run(xbuf[:, xi, :], xb_, xnx[:, cb], xnb[cb], 1 + j * 128)
            P.op("pool", lambda e, cb=cb: e.tensor_tensor(out=dxn[:], in0=xnx[:, cb, :, 0:TL], in1=xnx[:, cb, :, 1:TL + 1],
                                                          op=ALU.subtract), reads=[xnb[cb]], writes=[dxb])

            def proj(c0, m, bank, cb=cb):
                i, bbuf = bank

                def th(e):
                    ins = None
                    for kc in range(8):
                        e.matmul(pb[0:m, i, 0:TL], lhsT=wr[:, kc, c0:c0 + m], rhs=xnx[:, cb, kc, 1:TL + 1],
                                 start=(kc == 0), stop=False)
                    for kc in range(8):
                        ins = e.matmul(pb[0:m, i, 0:TL], lhsT=wmu[:, kc, c0:c0 + m], rhs=dxn[:, kc, :],
                                       start=False, stop=(kc == 7))
                    return ins

                P.op("pe", th, reads=[wr_b, wmu_b, xnb[cb], dxb], writes=[bbuf])

            for qi, qn in enumerate(("r", "k", "v")):
                for h in range(8):
                    bank = pr.next()
                    proj(qi * 512 + h * 64, 64, bank)
                    P.op("act", lambda e, qn=qn, h=h, i=bank[0]: e.activation(out=F[qn][:, h, :], in_=pb[0:64, i, 0:TL],
                                                                              func=AF.Copy),
                         reads=[bank[1]], writes=[FB[qn]])
            for li, (c0, m, fn) in enumerate([(1536, 64, AF.Tanh), (1600, 64, AF.Copy), (1664, 64, AF.Sigmoid),
                                              (1728, 64, AF.Sigmoid), (1792, 32, AF.Sigmoid)]):
                bank = pr.next()
                proj(c0, m, bank)
                P.op("act", lambda e, li=li, m=m, fn=fn, i=bank[0]: e.activation(out=lora[0:m, li, :], in_=pb[0:m, i, 0:TL],
                                                                                 func=fn),
                     reads=[bank[1]], writes=[lora_b])
            for h in range(8):
                bank = pr.next()
                P.op("pe", lambda e, h=h, i=bank[0]: e.matmul(pb[0:64, i, 0:TL], lhsT=w2[:, h * 64:(h + 1) * 64],
                                                              rhs=lora[:, 0, :], start=True, stop=True),
                     reads=[lw_b, lora_b], writes=[bank[1]])
                P.op("act", lambda e, h=h, i=bank[0]: e.activation(out=F["sgd"][:, h, :], in_=pb[0:64, i, 0:TL],
                                                                   func=AF.Sigmoid, bias=cw0[:, h:h + 1]),
                     reads=[bank[1], b_w0], writes=[FB["sgd"]])
                bank = pr.next()
                P.op("pe", lambda e, h=h, i=bank[0]: e.matmul(pb[0:64, i, 0:TL], lhsT=a2[:, h * 64:(h + 1) * 64],
                                                              rhs=lora[:, 1, :], start=True, stop=True),
                     reads=[lw_b, lora_b], writes=[bank[1]])
                P.op("act", lambda e, h=h, i=bank[0]: e.activation(out=F["a"][:, h, :], in_=pb[0:64, i, 0:TL],
                                                                   func=AF.Sigmoid, bias=ca0[:, h:h + 1]),
                     reads=[bank[1], b_a0], writes=[FB["a"]])
                bank = pr.next()

                def gmm(e, h=h, i=bank[0]):
                    e.matmul(pb[0:64, i, 0:TL], lhsT=g2[:, 0, h * 64:(h + 1) * 64], rhs=lora[:, 2, :], start=True, stop=False)
                    e.matmul(pb[0:64, i, 0:TL], lhsT=g2[:, 1, h * 64:(h + 1) * 64], rhs=lora[:, 3, :], start=False, stop=False)
                    return e.matmul(pb[0:64, i, 0:TL], lhsT=g2[0:32, 2, h * 64:(h + 1) * 64], rhs=lora[0:32, 4, :],
                                    start=False, stop=True)

                P.op("pe", gmm, reads=[lw_b, lora_b], writes=[bank[1]])
                P.op("act", lambda e, h=h, i=bank[0]: e.activation(out=F["g"][:, h, :], in_=pb[0:64, i, 0:TL], func=AF.Copy),
                     reads=[bank[1]], writes=[FB["g"]])
            P.op("dve", lambda e: e.tensor_tensor(out=F["kk"][:], in0=F["k"][:], in1=bc(ckk), op=ALU.mult),
                 reads=[FB["k"], b_kk], writes=[FB["kk"]])
            P.op("pool", lambda e: e.tensor_tensor(out=F["sqb"][:], in0=F["kk"][:], in1=F["kk"][:], op=ALU.mult),
                 reads=[FB["kk"]], writes=[FB["sqb"]])
            for h in range(8):
                bank = pr.next()
                P.op("pe", lambda e, h=h, i=bank[0]: e.matmul(pb[0:64, i, 0:TL], lhsT=ones_bf[:], rhs=F["sqb"][:, h, :],
                                                              start=True, stop=True),
                     reads=[mk_b, FB["sqb"]], writes=[bank[1]])
                P.op("act", lambda e, h=h, i=bank[0]: e.activation(out=F["t1"][:, h, :], in_=pb[0:64, i, 0:TL], func=AF.Sqrt),
                     reads=[bank[1]], writes=[FB["t1"]])
            P.op("dve", lambda e: e.tensor_scalar(out=F["t1"][:], in0=F["t1"][:], scalar1=1e-12, scalar2=None,
                                                  op0=ALU.max), reads=[FB["t1"]], writes=[FB["t1"]])
            P.op("dve", lambda e: e.reciprocal(out=F["t1"][:], in_=F["t1"][:]), reads=[FB["t1"]], writes=[FB["t1"]])
            P.op("dve", lambda e: e.tensor_tensor(out=F["kk"][:], in0=F["kk"][:], in1=F["t1"][:], op=ALU.mult),
                 reads=[FB["kk"], FB["t1"]], writes=[FB["kk"]])
            P.op("dve", lambda e: e.scalar_tensor_tensor(out=F["t2"][:], in0=F["a"][:], scalar=-1.0, in1=bc(cka),
                                                         op0=ALU.add, op1=ALU.mult),
                 reads=[FB["a"], b_ka], writes=[FB["t2"]])
            P.op("dve", lambda e: e.scalar_tensor_tensor(out=F["k"][:], in0=F["t2"][:], scalar=1.0, in1=F["k"][:],
                                                         op0=ALU.add, op1=ALU.mult),
                 reads=[FB["t2"], FB["k"]], writes=[FB["k"]])
            P.op("pool", lambda e: e.tensor_tensor(out=F["t2"][:], in0=F["kk"][:], in1=F["a"][:], op=ALU.mult),
                 reads=[FB["kk"], FB["a"]], writes=[FB["t2"]])
            P.op("dve", lambda e: e.tensor_tensor_scan(out=F["cum"][:].rearrange("p h t -> p (h t)"),
                                                       data0=rmask[:].rearrange("p h c t -> p (h c t)"),
                                                       data1=F["sgd"][:].rearrange("p h t -> p (h t)"),
                                                       initial=0.0, op0=ALU.mult, op1=ALU.add),
                 reads=[FB["sgd"], mk_b], writes=[FB["cum"]])
            cum4 = F["cum"][:].rearrange("p h (c t) -> p h c t", t=64)
            P.op("act", lambda e: e.activation(out=F["e1"][:], in_=F["cum"][:], func=AF.Exp, scale=-C0),
                 reads=[FB["cum"]], writes=[FB["e1"]])
            P.op("dve", lambda e: e.tensor_tensor(out=F["rT"][:], in0=F["r"][:], in1=F["e1"][:], op=ALU.mult),
                 reads=[FB["r"], FB["e1"]], writes=[FB["rT"]])
            P.op("act", lambda e: e.activation(out=gC[:], in_=cum4[:, :, :, 63], func=AF.Exp, scale=-C0),
                 reads=[FB["cum"]], writes=[gC_b])
            P.op("pool", lambda e: e.tensor_tensor(out=F["e2"][:], in0=F["cum"][:], in1=F["sgd"][:], op=ALU.subtract),
                 reads=[FB["cum"], FB["sgd"]], writes=[FB["e2"]])
            P.op("act", lambda e: e.activation(out=F["e2"][:], in_=F["e2"][:], func=AF.Exp, scale=-C0),
                 reads=[FB["e2"]], writes=[FB["e2"]])
            P.op("dve", lambda e: e.scalar_tensor_tensor(out=F["aT"][:], in0=F["kk"][:], scalar=-1.0, in1=F["e2"][:],
                                                         op0=ALU.mult, op1=ALU.mult),
                 reads=[FB["kk"], FB["e2"]], writes=[FB["aT"]])
            P.op("act", lambda e: e.activation(out=F["e1"][:], in_=F["cum"][:], func=AF.Exp, scale=C0),
                 reads=[FB["cum"], FB["rT"]], writes=[FB["e1"]])
            P.op("dve", lambda e: e.tensor_tensor(out=F["bT"][:], in0=F["t2"][:], in1=F["e1"][:], op=ALU.mult),
                 reads=[FB["t2"], FB["e1"]], writes=[FB["bT"]])
            P.op("pool", lambda e: e.tensor_tensor(out=F["kT"][:], in0=F["k"][:], in1=F["e1"][:], op=ALU.mult),
                 reads=[FB["k"], FB["e1"]], writes=[FB["kT"]])
            P.op("dve", lambda e: e.tensor_tensor(out=F["e2"][:].rearrange("p h (c t) -> p h c t", t=64),
                                                  in0=cum4[:, :, :, 63:64].broadcast_to([64, 8, TL // 64, 64]), in1=cum4,
                                                  op=ALU.subtract),
                 reads=[FB["cum"], FB["aT"]], writes=[FB["e2"]])
            P.op("act", lambda e: e.activation(out=F["e2"][:], in_=F["e2"][:], func=AF.Exp, scale=-C0),
                 reads=[FB["e2"]], writes=[FB["e2"]])
            P.op("dve", lambda e: e.tensor_tensor(out=F["bH"][:], in0=F["t2"][:], in1=F["e2"][:], op=ALU.mult),
                 reads=[FB["t2"], FB["e2"]], writes=[FB["bH"]])
            P.op("pool", lambda e: e.tensor_tensor(out=F["kH"][:], in0=F["k"][:], in1=F["e2"][:], op=ALU.mult),
                 reads=[FB["k"], FB["e2"]], writes=[FB["kH"]])
            P.op("act", lambda e: e.activation(out=F["vb"][:], in_=F["v"][:], func=AF.Copy),
                 reads=[FB["v"]], writes=[FB["vb"]])
            P.op("dve", lambda e: e.tensor_tensor(out=F["t1"][:], in0=F["r"][:], in1=F["k"][:], op=ALU.mult),
                 reads=[FB["r"], FB["k"]], writes=[FB["t1"]])
            P.op("pool", lambda e: e.tensor_tensor(out=F["sqb"][:], in0=F["t1"][:], in1=bc(crk), op=ALU.mult),
                 reads=[FB["t1"], b_rk], writes=[FB["sqb"]])
            for h in range(8):
                bank = pr.next()
                P.op("pe", lambda e, h=h, i=bank[0]: e.matmul(pb[0:64, i, 0:TL], lhsT=ones_bf[:], rhs=F["sqb"][:, h, :],
                                                              start=True, stop=True),
                     reads=[mk_b, FB["sqb"]], writes=[bank[1]])
                P.op("dve", lambda e, h=h, i=bank[0]: e.tensor_tensor(out=F["bon"][:, h, :], in0=pb[0:64, i, 0:TL],
                                                                      in1=F["v"][:, h, :], op=ALU.mult),
                     reads=[bank[1], FB["v"]], writes=[FB["bon"]])

            for c in range(TL // 64):
                cs = slice(c * 64, (c + 1) * 64)
                toks = []
                for (dst, dr, src) in ((Atok, Atok_r, "aT"), (BHtok, BHtok_r, "bH"), (KHtok, KHtok_r, "kH"),
                                       (Vtok, Vtok_r, "vb")):
                    bank = pr.next()
                    tr8(bank, lambda h, src=src: F[src][:, h, cs], [FB[src]])
                    di, db = dr.next()
                    P.op("act", lambda e, dst=dst, di=di, i=bank[0]: e.activation(out=dst[:, di], in_=pvb(i), func=AF.Copy),
                         reads=[bank[1]], writes=[db])
                    toks.append((di, db))
                (ai, a_b), (bhi, bh_b), (khi, kh_b), (vi, v_b) = toks
                def gmat(lname, rname, mi, dst, dr, add_ident=False):
                    bank = pr.next()
                    mm8(bank, [(lambda h: F[lname][:, h, cs], lambda h: F[rname][:, h, cs])], [FB[lname], FB[rname]])
                    di, db = dr.next()
                    P.op("dve", lambda e, i=bank[0]: e.tensor_tensor(out=dst[:, di], in0=pv(i), in1=mb3(mi), op=ALU.mult),
                         reads=[bank[1], mk_b], writes=[db])
                    return di, db

                m0i, m0_b = gmat("bT", "aT", 0, Mm, Mm_r)
                rbi, rb_b = gmat("bT", "rT", 1, Mrb, Mrb_r)
                lki, lk_b = gmat("kT", "aT", 0, Lak, Lak_r)
                rki, rk_b = gmat("kT", "rT", 1, Mrk, Mrk_r)
                n0i, n0_b = gmat("aT", "bT", 2, Nn, Nn_r)
                qi_, q_b = Qq_r.next()
                P.op("pool", lambda e, qi_=qi_, m0i=m0i: e.tensor_tensor(out=Qq[:, qi_], in0=Mm[:, m0i], in1=idf3, op=ALU.add),
                     reads=[m0_b, C.b], writes=[q_b])
                ni, n_b, mi_, m_b = n0i, n0_b, m0i, m0_b
                for lvl in range(1, 6):
                    bank = pr.next()
                    mm8(bank, [(lambda h, mi_=mi_: Mm[:, mi_, h, :], lambda h, ni=ni: Nn[:, ni, h, :])], [m_b, n_b])
                    if lvl < 5:
                        bank2 = pr.next()
                        mm8(bank2, [(lambda h, ni=ni: Nn[:, ni, h, :], lambda h, mi_=mi_: Mm[:, mi_, h, :])], [m_b, n_b])
                    nni, nn_b = Nn_r.next()
                    P.op("act", lambda e, nni=nni, i=bank[0]: e.activation(out=Nn[:, nni], in_=pv(i), func=AF.Copy),
                         reads=[bank[1]], writes=[nn_b])
                    if lvl < 5:
                        nmi, nm_b = Mm_r.next()
                        P.op("dve", lambda e, nmi=nmi, i=bank2[0]: e.tensor_copy(out=Mm[:, nmi], in_=pv(i)),
                             reads=[bank2[1]], writes=[nm_b])
                        mi_, m_b = nmi, nm_b
                    ni, n_b = nni, nn_b
                    bank3 = pr.next()
                    mm8(bank3, [(lambda h: idb, lambda h, qi_=qi_: Qq[:, qi_, h, :]),
                                (lambda h, ni=ni: Nn[:, ni, h, :], lambda h, qi_=qi_: Qq[:, qi_, h, :])], [q_b, n_b, C.b])
                    nqi, nq_b = Qq_r.next()
                    P.op("dve", lambda e, nqi=nqi, i=bank3[0]: e.tensor_copy(out=Qq[:, nqi], in_=pv(i)),
                         reads=[bank3[1]], writes=[nq_b])
                    qi_, q_b = nqi, nq_b
                bank = pr.next()
                mm8(bank, [(lambda h: Atok[:, ai, h, :], lambda h: Qq[:, qi_, h, :])], [a_b, q_b])
                wi, w_b = WT_r.next()
                P.op("act", lambda e, wi=wi, i=bank[0]: e.activation(out=WT[:, wi], in_=pv(i), func=AF.Copy),
                     reads=[bank[1]], writes=[w_b])
                bank = pr.next()
                mm8(bank, [(lambda h: Lak[:, lki, h, :], lambda h: Vtok[:, vi, h, :])], [lk_b, v_b])
                xi_, x_b = Xx_r.next()
                P.op("dve", lambda e, xi_=xi_, i=bank[0]: e.tensor_copy(out=Xx[:, xi_], in_=pv(i)),
                     reads=[bank[1]], writes=[x_b])
                bank = pr.next()
                mm8(bank, [(lambda h: WT[:, wi, h, :], lambda h: Sb[:, h, :]),
                           (lambda h: Qq[:, qi_, h, :], lambda h: Xx[:, xi_, h, :])], [w_b, S_b, q_b, x_b])
                ui, u_b = Uu_r.next()
                P.op("act", lambda e, ui=ui, i=bank[0]: e.activation(out=Uu[:, ui], in_=pv(i), func=AF.Copy),
                     reads=[bank[1]], writes=[u_b])
                banky = pr.next()
                mm8(banky, [(lambda h: F["rT"][:, h, cs], lambda h: Sb[:, h, :]),
                            (lambda h: Mrb[:, rbi, h, :], lambda h: Uu[:, ui, h, :]),
                            (lambda h: Mrk[:, rki, h, :], lambda h: Vtok[:, vi, h, :])],
                    [FB["rT"], S_b, rb_b, u_b, rk_b, v_b])
                banks = pr.next()
                mm8(banks, [(lambda h: BHtok[:, bhi, h, :], lambda h: Uu[:, ui, h, :]),
                            (lambda h: KHtok[:, khi, h, :], lambda h: Vtok[:, vi, h, :])], [bh_b, u_b, kh_b, v_b])
                P.op("dve", lambda e, c=c: e.tensor_tensor(out=St[:], in0=Sf[:],
                                                           in1=gC[:, :, c:c + 1].broadcast_to([64, 8, 64]), op=ALU.mult),
                     reads=[S_b, gC_b], writes=[St_b])
                P.op("dve", lambda e, i=banks[0]: e.tensor_tensor(out=Sf[:], in0=pv(i), in1=St[:], op=ALU.add),
                     reads=[banks[1], St_b], writes=[S_b])
                P.op("act", lambda e: e.activation(out=Sb[:], in_=Sf[:], func=AF.Copy), reads=[S_b], writes=[S_b])
                yi, y_b = Ys_r.next()
                P.op("act", lambda e, yi=yi, i=banky[0]: e.activation(out=Ys[:, yi], in_=pv(i), func=AF.Copy),
                     reads=[banky[1]], writes=[y_b])
                qi2, yq_b = Yq_r.next()
                P.op("pool", lambda e, yi=yi, qi2=qi2: e.tensor_tensor(out=Yq[:, qi2], in0=Ys[:, yi], in1=Ys[:, yi], op=ALU.mult),
                     reads=[y_b], writes=[yq_b])
                gi, g_b = gst_r.next()
                P.op("dve", lambda e, yi=yi, gi=gi: e.tensor_reduce(out=gst[:, gi, :, 0], in_=Ys[:, yi], axis=AX.X, op=ALU.add),
                     reads=[y_b], writes=[g_b])
                P.op("dve", lambda e, qi2=qi2, gi=gi: e.tensor_reduce(out=gst[:, gi, :, 1], in_=Yq[:, qi2], axis=AX.X, op=ALU.add),
                     reads=[yq_b, g_b], writes=[g_b])
                P.op("dve", lambda e, gi=gi: e.tensor_scalar(out=gst[:, gi, :, 2], in0=gst[:, gi, :, 0], scalar1=1.0 / 64,
                                                             scalar2=None, op0=ALU.mult), reads=[g_b], writes=[g_b])
                P.op("dve", lambda e, gi=gi: e.tensor_tensor(out=gst[:, gi, :, 3], in0=gst[:, gi, :, 2], in1=gst[:, gi, :, 2],
                                                             op=ALU.mult), reads=[g_b], writes=[g_b])
                P.op("dve", lambda e, gi=gi: e.scalar_tensor_tensor(out=gst[:, gi, :, 4], in0=gst[:, gi, :, 1], scalar=1.0 / 64,
                                                                    in1=gst[:, gi, :, 3], op0=ALU.mult, op1=ALU.subtract),
                     reads=[g_b], writes=[g_b])
                P.op("act", lambda e, gi=gi: e.activation(out=gst[:, gi, :, 5], in_=gst[:, gi, :, 4], func=AF.Sqrt,
                                                          bias=C.eps[0:64, 1:2]), reads=[g_b, C.b], writes=[g_b])
                P.op("dve", lambda e, gi=gi: e.reciprocal(out=gst[:, gi, :, 5], in_=gst[:, gi, :, 5]), reads=[g_b], writes=[g_b])
                P.op("dve", lambda e, yi=yi, gi=gi: e.tensor_tensor(out=Ys[:, yi], in0=Ys[:, yi],
                                                                    in1=gst[:, gi, :, 2:3].broadcast_to([64, 8, 64]),
                                                                    op=ALU.subtract), reads=[y_b, g_b], writes=[y_b])
                ni2, yn_b = Yn_r.next()
                P.op("dve", lambda e, yi=yi, gi=gi, ni2=ni2: e.tensor_tensor(out=Yn[:, ni2], in0=Ys[:, yi],
                                                                             in1=gst[:, gi, :, 5:6].broadcast_to([64, 8, 64]),
                                                                             op=ALU.mult), reads=[y_b, g_b], writes=[yn_b])
                bank = pr.next()
                tr8(bank, lambda h, ni2=ni2: Yn[:, ni2, h, :], [yn_b])
                P.op("act", lambda e, i=bank[0], cs=cs: e.activation(out=F["ynT"][:, :, cs], in_=pvb(i), func=AF.Copy),
                     reads=[bank[1]], writes=[FB["ynT"]])
            P.op("dve", lambda e: e.tensor_tensor(out=F["ynT"][:], in0=F["ynT"][:], in1=bc(clw), op=ALU.mult),
                 reads=[FB["ynT"], b_lw], writes=[FB["ynT"]])
            P.op("pool", lambda e: e.tensor_tensor(out=F["ynT"][:], in0=F["ynT"][:], in1=bc(clb), op=ALU.add),
                 reads=[FB["ynT"], b_lb], writes=[FB["ynT"]])
            P.op("dve", lambda e: e.tensor_tensor(out=F["ynT"][:], in0=F["ynT"][:], in1=F["bon"][:], op=ALU.add),
                 reads=[FB["ynT"], FB["bon"]], writes=[FB["ynT"]])
            P.op("dve", lambda e: e.tensor_tensor(out=ost[:], in0=F["ynT"][:], in1=F["g"][:], op=ALU.mult),
                 reads=[FB["ynT"], FB["g"]], writes=[ost_b])
            P.dma("sp", lambda e, c5=c5: e.dma_start(
                out=yr_dram.rearrange("(h v) s -> v h s", v=64)[:, :, c5 * TL:(c5 + 1) * TL], in_=ost[:]),
                reads=[ost_b])
    P.barrier()


NITER = 16


def phase_attn(P, nc, C, A, ya_dram):
    x_dram = A["x"]
    with ExitStack() as es:
        sb = lambda name, shape, dt=F32: es.enter_context(nc.sbuf_tensor(name, shape, dt))
        wa = sb("wa", [128, 8, ATTN_IN + 64], BF16)
        wa_b = Buf()
        load_weight_bf16(P, nc, wa, wa_b, A["w_in"], 0, ATTN_IN, 8)
        wv_ = A["w_in"].rearrange("(kc p) n -> p kc n", p=128)
        for kc in range(8):
            P.dma("pool", lambda e, kc=kc: e.dma_start(out=wa[:, kc, ATTN_IN:ATTN_IN + 64], in_=wv_[:, kc, 2048:2112]),
                  writes=[wa_b])
        kT = sb("kT", [128, 4, S], BF16)
        kiT = sb("kiT", [128, S], BF16)
        vaug = sb("vaug", [128, NT, 8, 65], BF16)
        kT_b, kiT_b, va_b = Buf(), Buf(), Buf()
        P.op("pool", lambda e: e.memset(vaug[:, :, :, 64:65], 1.0), writes=[va_b])
        gqk = sb("gqk", [128, 2], F32)
        gqk_b = Buf()
        for half in range(2):
            P.dma("sp", lambda e, half=half: e.dma_start(out=gqk[half * 64:(half + 1) * 64, 0:1],
                                                         in_=A["attn_q_norm"].rearrange("o d -> d o"),
                                                         allow_slow_non_contiguous=True), writes=[gqk_b])
            P.dma("sp", lambda e, half=half: e.dma_start(out=gqk[half * 64:(half + 1) * 64, 1:2],
                                                         in_=A["attn_k_norm"].rearrange("o d -> d o"),
                                                         allow_slow_non_contiguous=True), writes=[gqk_b])
        P.op("dve", lambda e: e.tensor_scalar(out=gqk[:, 0:1], in0=gqk[:, 0:1], scalar1=0.125, scalar2=None, op0=ALU.mult),
             reads=[gqk_b], writes=[gqk_b])
        btf = sb("btf", [128, 8, 2, 128], F32)
        bt = sb("bt", [128, 8, 2, 128], BF16)
        b31 = sb("b31_sb", [128, 8], F32)
        bt_b = Buf()
        P.dma("sp", lambda e: e.dma_start(out=btf[:], in_=A["bias_tiles"].rearrange("h c s t -> s h c t")), writes=[bt_b])
        P.dma("sp", lambda e: e.dma_start(out=b31[:], in_=A["b31"].partition_broadcast(128)), writes=[bt_b])
        P.op("dve", lambda e: e.tensor_tensor(out=bt[:].rearrange("p h c t -> p h (c t)"),
                                              in0=btf[:].rearrange("p h c t -> p h (c t)"),
                                              in1=b31[:, :].unsqueeze(2).broadcast_to([128, 8, 256]), op=ALU.subtract),
             reads=[bt_b], writes=[bt_b])
        cmask = sb("cmask", [128, 128], F32)
        onesblk = sb("onesblk", [128, 128], BF16)
        cm_b = Buf()

        def mkc(e):
            e.memset(cmask[:], 0.0)
            e.affine_select(out=cmask[:], in_=cmask[:], pattern=[[-1, 128]], compare_op=ALU.is_ge, fill=NEG, base=0,
                            channel_multiplier=1)
            e.memset(onesblk[:], 0.0)
            e.memset(onesblk[0:64, 0:64], 1.0)
            return e.memset(onesblk[64:128, 64:128], 1.0)

        P.op("pool", mkc, writes=[cm_b])
        nm = Normer(P, nc, es, C, A["mix_norm"], "an", nslots=1)
        xbuf = sb("axbuf", [128, 2, D], F32)
        xr = Ring([(i, Buf()) for i in range(2)])
        xn = sb("axn", [128, 8, 128], BF16)
        xn_b = Buf()
        q_i = sb("q_i", [128, 4, 128], BF16)
        qi_i = sb("qi_i", [128, 4, 128], BF16)
        wi_i = sb("wi_i", [128, 8], F32)
        q_b, qi_b, wi_b = Buf(), Buf(), Buf()
        sq = sb("asq", [128, 512], BF16)
        rn = sb("arn", [128, 512], F32)
        sq_b, rn_b = Buf(), Buf()
        sc = sb("sc", [128, S], F32)
        sc_b = Buf()
        rbuf = sb("rbuf", [128, 2, 512], F32)
        rr = Ring([(i, Buf()) for i in range(2)])
        junk = sb("ajunk", [128, S], BF16)
        junk_b = Buf()
        bs = sb("bs", [128, 8], F32)
        bs_b = Buf()
        maskT = sb("maskT", [128, NT, 128], BF16)
        mT_b = Buf()
        Et = sb("Et", [128, 2, 4, 128], BF16)
        er = Ring([(i, Buf()) for i in range(2)])
        Pt = sb("Pt", [128, 2, 4, 128], BF16)
        ptr = Ring([(i, Buf()) for i in range(2)])
        rden = sb("rden", [128, 2], F32)
        rdr = Ring([(i, Buf()) for i in range(2)])
        ytile = sb("ytile", [128, 512], BF16)
        yt_b = Buf()
        yst = sb("yst", [128, 4, 128], BF16)
        ys_b = Buf()
        pg = es.enter_context(nc.psum_tensor("apg", [128, 5, 512], F32))
        gr = Ring([(i, Buf()) for i in range(5)])
        po = es.enter_context(nc.psum_tensor("apo", [128, 2, 512], F32))
        orr = Ring([(i, Buf()) for i in range(2)])
        pgb = lambda i: pg[:, i, :].bitcast(BF16)
        xview = x_dram.rearrange("(t p) d -> t p d", p=128)
        yav = ya_dram.rearrange("(m p) s -> p m s", p=128)

        for i in range(NT):
            ts_ = slice(i * 128, (i + 1) * 128)
            nkb = i + 1
            W = nkb * 128
            xi, xb_ = xr.next()
            P.dma("sp", lambda e, i=i, xi=xi: e.dma_start(out=xbuf[:, xi, :], in_=xview[i]), writes=[xb_])
            nm.run(xbuf[:, xi, :], xb_, xn, xn_b, 0)

            def proj4(c0, bank):
                bi, bb = bank

                def th(e):
                    ins = None
                    for m in range(4):
                        for kc in range(8):
                            ins = e.matmul(pg[:, bi, m * 128:(m + 1) * 128], lhsT=wa[:, kc, c0 + m * 128:c0 + (m + 1) * 128],
                                           rhs=xn[:, kc, :], start=(kc == 0), stop=(kc == 7))
                    return ins

                P.op("pe", th, reads=[wa_b, xn_b], writes=[bb])

            for which, c0 in ((0, 0), (1, 512)):
                bank = gr.next()
                proj4(c0, bank)
                P.op("act", lambda e, bi=bank[0]: e.activation(out=sq[:], in_=pg[:, bi, :], func=AF.Square),
                     reads=[bank[1]], writes=[sq_b])
                bank2 = gr.next()
                P.op("pe", lambda e, bi=bank2[0]: e.matmul(pg[:, bi, :], lhsT=onesblk[:], rhs=sq[:], start=True, stop=True),
                     reads=[cm_b, sq_b], writes=[bank2[1]])
                P.op("act", lambda e, bi=bank2[0]: e.activation(out=rn[:], in_=pg[:, bi, :], func=AF.Sqrt, scale=1.0 / 64,
                                                                bias=C.eps[:, 0:1]), reads=[bank2[1], C.b], writes=[rn_b])
                P.op("dve", lambda e: e.reciprocal(out=rn[:], in_=rn[:]), reads=[rn_b], writes=[rn_b])
                if which == 0:
                    P.op("dve", lambda e, bi=bank[0]: e.scalar_tensor_tensor(
                        out=q_i[:].rearrange("p m t -> p (m t)"), in0=pg[:, bi, :], scalar=gqk[:, 0:1], in1=rn[:],
                        op0=ALU.mult, op1=ALU.mult), reads=[bank[1], rn_b, gqk_b], writes=[q_b])
                else:
                    P.op("dve", lambda e, bi=bank[0], ts_=ts_: e.scalar_tensor_tensor(
                        out=kT[:, :, ts_], in0=pg[:, bi, :].rearrange("p (m t) -> p m t", m=4), scalar=gqk[:, 1:2],
                        in1=rn[:].rearrange("p (m t) -> p m t", m=4), op0=ALU.mult, op1=ALU.mult),
                        reads=[bank[1], rn_b, gqk_b], writes=[kT_b])
            bank = gr.next()
            proj4(1536, bank)
            P.op("act", lambda e, bi=bank[0]: e.activation(out=qi_i[:].rearrange("p m t -> p (m t)"), in_=pg[:, bi, :],
                                                           func=AF.Copy), reads=[bank[1]], writes=[qi_b])
            bank = gr.next()

            def kiw(e, bi=bank[0]):
                for kc in range(8):
                    e.matmul(pg[0:64, bi, 0:128], lhsT=wa[:, kc, 2048:2112], rhs=xn[:, kc, :], start=(kc == 0), stop=(kc == 7))
                for kc in range(8):
                    e.matmul(pg[64:128, bi, 0:128], lhsT=wa[:, kc, ATTN_IN:ATTN_IN + 64], rhs=xn[:, kc, :], start=(kc == 0),
                             stop=(kc == 7))
                ins = None
                for kc in range(8):
                    ins = e.matmul(pg[:, bi, 128:136], lhsT=xn[:, kc, :], rhs=wa[:, kc, 2112:2120], start=(kc == 0), stop=(kc == 7))
                return ins

            P.op("pe", kiw, reads=[wa_b, xn_b], writes=[bank[1]])
            P.op("act", lambda e, bi=bank[0], ts_=ts_: e.activation(out=kiT[:, ts_], in_=pg[:, bi, 0:128], func=AF.Copy),
                 reads=[bank[1]], writes=[kiT_b])
            P.op("dve", lambda e, bi=bank[0]: e.tensor_copy(out=wi_i[:], in_=pg[:, bi, 128:136]), reads=[bank[1]], writes=[wi_b])
            bank = gr.next()

            def vmm(e, bi=bank[0]):
                ins = None
                for kc in range(8):
                    ins = e.matmul(pg[:, bi, :], lhsT=xn[:, kc, :], rhs=wa[:, kc, 1024:1536], start=(kc == 0), stop=(kc == 7))
                return ins

            P.op("pe", vmm, reads=[wa_b, xn_b], writes=[bank[1]])
            P.op("act", lambda e, bi=bank[0], i=i: e.activation(out=vaug[:, i, :, 0:64],
                                                                in_=pg[:, bi, :].rearrange("p (h d) -> p h d", h=8), func=AF.Copy),
                 reads=[bank[1]], writes=[va_b])

            for gk in range((nkb + 3) // 4):
                w_ = min(512, W - gk * 512)
                for h in range(8):
                    hb = (h % 2) * 64
                    bank = gr.next()
                    P.op("pe", lambda e, bi=bank[0], h=h, hb=hb, gk=gk, w_=w_: e.matmul(
                        pg[:, bi, 0:w_], lhsT=qi_i[hb:hb + 64, h // 2, :], rhs=kiT[hb:hb + 64, gk * 512:gk * 512 + w_],
                        start=True, stop=True), reads=[qi_b, kiT_b], writes=[bank[1]])
                    ri, rb_ = rr.next()
                    P.op("act", lambda e, bi=bank[0], ri=ri, w_=w_: e.activation(out=rbuf[:, ri, 0:w_], in_=pg[:, bi, 0:w_],
                                                                                 func=AF.Relu), reads=[bank[1]], writes=[rb_])
                    if h == 0:
                        P.op("dve", lambda e, ri=ri, gk=gk, w_=w_: e.tensor_scalar(
                            out=sc[:, gk * 512:gk * 512 + w_], in0=rbuf[:, ri, 0:w_], scalar1=wi_i[:, 0:1], scalar2=None,
                            op0=ALU.mult), reads=[rb_, wi_b], writes=[sc_b])
                    else:
                        P.op("dve", lambda e, ri=ri, gk=gk, w_=w_, h=h: e.scalar_tensor_tensor(
                            out=sc[:, gk * 512:gk * 512 + w_], in0=rbuf[:, ri, 0:w_], scalar=wi_i[:, h:h + 1],
                            in1=sc[:, gk * 512:gk * 512 + w_], op0=ALU.mult, op1=ALU.add), reads=[rb_, wi_b, sc_b], writes=[sc_b])
            P.op("dve", lambda e, ts_=ts_: e.tensor_tensor(out=sc[:, ts_], in0=sc[:, ts_], in1=cmask[:], op=ALU.add),
                 reads=[sc_b, cm_b], writes=[sc_b])
            if i < 2:
                P.op("dve", lambda e: e.memset(bs[:, 0:1], -1.0e29), writes=[bs_b])
            else:
                P.op("dve", lambda e, i=i: e.tensor_reduce(out=bs[:, 0:1], in_=sc[:, 0:i * 128], axis=AX.X, op=ALU.min),
                     reads=[sc_b], writes=[bs_b])
                P.op("dve", lambda e, W=W: e.tensor_reduce(out=bs[:, 6:7], in_=sc[:, 0:W], axis=AX.X, op=ALU.max),
                     reads=[sc_b, bs_b], writes=[bs_b])
                P.op("dve", lambda e: e.tensor_tensor(out=bs[:, 1:2], in0=bs[:, 6:7], in1=bs[:, 0:1], op=ALU.subtract),
                     reads=[bs_b], writes=[bs_b])
                for it in range(NITER):
                    P.op("dve", lambda e, it=it: e.tensor_scalar(out=bs[:, 2:3], in0=bs[:, 1:2], scalar1=0.5 ** (it + 1),
                                                                 scalar2=None, op0=ALU.mult), reads=[bs_b], writes=[bs_b])
                    P.op("dve", lambda e: e.tensor_tensor(out=bs[:, 3:4], in0=bs[:, 0:1], in1=bs[:, 2:3], op=ALU.add),
                         reads=[bs_b], writes=[bs_b])
                    P.op("dve", lambda e, W=W: e.tensor_scalar(out=junk[:, 0:W], in0=sc[:, 0:W], scalar1=bs[:, 3:4], scalar2=None,
                                                               op0=ALU.is_ge, op1=ALU.add, accum_out=bs[:, 4:5]),
                         reads=[sc_b, bs_b], writes=[junk_b, bs_b])
                    P.op("dve", lambda e: e.scalar_tensor_tensor(out=bs[:, 5:6], in0=bs[:, 4:5], scalar=TOPK - 0.5, in1=bs[:, 2:3],
                                                                 op0=ALU.is_ge, op1=ALU.mult), reads=[bs_b], writes=[bs_b])
                    P.op("dve", lambda e: e.tensor_tensor(out=bs[:, 0:1], in0=bs[:, 0:1], in1=bs[:, 5:6], op=ALU.add),
                         reads=[bs_b], writes=[bs_b])
            P.op("dve", lambda e, W=W: e.tensor_scalar(out=junk[:, 0:W], in0=sc[:, 0:W], scalar1=bs[:, 0:1], scalar2=None,
                                                       op0=ALU.is_ge), reads=[sc_b, bs_b], writes=[junk_b])
            for j0 in range(0, nkb, 8):
                nb = min(8, nkb - j0)
                bank = gr.next()

                def trm(e, bi=bank[0], j0=j0, nb=nb):
                    ins = None
                    for jj in range(nb):
                        ins = e.transpose(out=pgb(bi)[:, jj * 128:(jj + 1) * 128], in_=junk[:, (j0 + jj) * 128:(j0 + jj + 1) * 128],
                                          identity=C.ident_bf[:])
                    return ins

                P.op("pe", trm, reads=[junk_b, C.b], writes=[bank[1]])
                P.op("act", lambda e, bi=bank[0], j0=j0, nb=nb: e.activation(
                    out=maskT[:, j0:j0 + nb, :].rearrange("p j t -> p (j t)"), in_=pgb(bi)[:, 0:nb * 128], func=AF.Copy),
                    reads=[bank[1]], writes=[mT_b])
            for h in range(8):
                hb = (h % 2) * 64
                m = h // 2
                ob = orr.next()
                for j0 in range(0, nkb, 4):
                    nb = min(4, nkb - j0)
                    bank = gr.next()

                    def qk(e, bi=bank[0], j0=j0, nb=nb, h=h, hb=hb, m=m, i=i):
                        ins = None
                        for jj in range(nb):
                            j = j0 + jj
                            near = j >= i - 1
                            ins = e.matmul(pg[:, bi, jj * 128:(jj + 1) * 128], lhsT=kT[hb:hb + 64, m, j * 128:(j + 1) * 128],
                                           rhs=q_i[hb:hb + 64, m, :], start=True, stop=not near)
                            if near:
                                ins = e.matmul(pg[:, bi, jj * 128:(jj + 1) * 128], lhsT=C.ident_bf[:],
                                               rhs=bt[:, h, 0 if j == i else 1, :], start=False, stop=True)
                        return ins

                    P.op("pe", qk, reads=[kT_b, q_b, bt_b, C.b], writes=[bank[1]])
                    ei, eb = er.next()
                    P.op("act", lambda e, bi=bank[0], ei=ei, nb=nb: e.activation(
                        out=Et[:, ei, 0:nb, :].rearrange("p j t -> p (j t)"), in_=pg[:, bi, 0:nb * 128], func=AF.Exp),
                        reads=[bank[1]], writes=[eb])
                    pi, pb_ = ptr.next()
                    P.op("dve", lambda e, ei=ei, pi=pi, j0=j0, nb=nb: e.tensor_tensor(
                        out=Pt[:, pi, 0:nb, :], in0=Et[:, ei, 0:nb, :], in1=maskT[:, j0:j0 + nb, :], op=ALU.mult),
                        reads=[eb, mT_b], writes=[pb_])

                    def pv(e, oi=ob[0], pi=pi, j0=j0, nb=nb, h=h, i=i):
                        ins = None
                        for jj in range(nb):
                            j = j0 + jj
                            ins = e.matmul(po[:, oi, 0:65], lhsT=Pt[:, pi, jj, :], rhs=vaug[:, j, h, :], start=(j == 0),
                                           stop=(j == i))
                        return ins

                    P.op("pe", pv, reads=[pb_, va_b], writes=[ob[1]])
                di, db = rdr.next()
                P.op("dve", lambda e, oi=ob[0], di=di: e.reciprocal(out=rden[:, di:di + 1], in_=po[:, oi, 64:65]),
                     reads=[ob[1]], writes=[db])
                P.op("act", lambda e, oi=ob[0], di=di, h=h: e.activation(out=ytile[:, h * 64:(h + 1) * 64], in_=po[:, oi, 0:64],
                                                                         func=AF.Copy, scale=rden[:, di:di + 1]),
                     reads=[ob[1], db], writes=[yt_b])
            bank = gr.next()

            def try_(e, bi=bank[0]):
                ins = None
                for m in range(4):
                    ins = e.transpose(out=pgb(bi)[:, m * 128:(m + 1) * 128], in_=ytile[:, m * 128:(m + 1) * 128],
                                      identity=C.ident_bf[:])
                return ins

            P.op("pe", try_, reads=[yt_b, C.b], writes=[bank[1]])
            P.op("act", lambda e, bi=bank[0]: e.activation(out=yst[:].rearrange("p m t -> p (m t)"), in_=pgb(bi)[:, 0:512],
                                                           func=AF.Copy), reads=[bank[1]], writes=[ys_b])
            P.dma("sp", lambda e, ts_=ts_: e.dma_start(out=yav[:, :, ts_], in_=yst[:]), reads=[ys_b])
    P.barrier()


WEIGHT_SPECS = [
    ("mix_norm", [1, D]), ("w_in", [D, 5992]), ("attn_q_norm", [1, 64]), ("attn_k_norm", [1, 64]),
    ("bias_tiles", [8, 2, 128, 128]), ("b31", [1, 8]), ("rwkv_mu", [1, RWKV_IN]), ("rwkv_w0", [1, 512]), ("rwkv_w2", [64, 512]),
    ("rwkv_a0", [1, 512]), ("rwkv_a2", [64, 512]), ("rwkv_g2", [160, 512]), ("rwkv_k_k", [1, 512]),
    ("rwkv_k_a", [1, 512]), ("rwkv_r_k", [1, 512]), ("rwkv_ln_w", [1, 512]), ("rwkv_ln_b", [1, 512]),
    ("w_branch_attn", [512, D]), ("w_branch_rwkv", [512, D]), ("w_out", [D, D]), ("ffn_norm", [1, D]),
    ("w_gate_up", [D, 2 * FFN_H]), ("w_down", [FFN_H, D]),
]


def build_program(phases=("attn", "rwkv", "merge", "ffn"), debug=False):
    nc = bass.Bass("TRN2", target_bir_lowering=False)
    A = {}
    A["x"] = nc.dram_tensor("x", [S, D], F32, kind="ExternalInput").ap()
    for name, shp in WEIGHT_SPECS:
        A[name] = nc.dram_tensor(name, shp, F32, kind="ExternalInput").ap()
    out = nc.dram_tensor("out", [S, D], F32, kind="ExternalOutput").ap()
    def kind(prod, cons):
        if not debug:
            return "Internal"
        if prod in phases and cons not in phases:
            return "ExternalOutput"
        if prod not in phases and cons in phases:
            return "ExternalInput"
        return "Internal"

    ya = nc.dram_tensor("ya_scr", [512, S], BF16, kind=kind("attn", "merge")).ap()
    yr = nc.dram_tensor("yr_scr", [512, S], BF16, kind=kind("rwkv", "merge")).ap()
    hs = nc.dram_tensor("h_scr", [S, D], F32, kind=kind("merge", "ffn")).ap()
    P = Prog(nc)
    final_ops = []
    with ExitStack() as es:
        C = make_consts(P, nc, es)
        P.barrier()
        if "attn" in phases:
            phase_attn(P, nc, C, A, ya)
        if "rwkv" in phases:
            phase_rwkv(P, nc, C, A, yr)
        if "merge" in phases:
            phase_merge(P, nc, C, A["x"], ya, yr, hs, A["mix_norm"], A["w_in"], A["w_branch_attn"],
                        A["w_branch_rwkv"], A["w_out"])
        if "ffn" in phases:
            phase_ffn(P, nc, C, hs, out, A["ffn_norm"], A["w_gate_up"], A["w_down"], final_ops)
        P.emit(final_wait_ops=final_ops)
    return nc


def t5_bucket_np(d):
    d = np.maximum(d, 0)
    max_exact = 16
    log_ratio = np.log(np.maximum(d, 1).astype(np.float32) / max_exact) / math.log(128 / max_exact)
    large = np.minimum(max_exact + (log_ratio * 16).astype(np.int32), 31)
    return np.where(d < max_exact, d, large)


def host_layout(inputs):
    w = {}
    for name, shp in WEIGHT_SPECS:
        if name in ("bias_tiles", "b31"):
            continue
        w[name] = np.ascontiguousarray(np.asarray(inputs[name], dtype=np.float32).reshape(shp))
    s_idx = np.arange(128)[:, None]
    t_idx = np.arange(128)[None, :]
    rb = np.asarray(inputs["rel_bias"], dtype=np.float32)
    tiles = np.empty((8, 2, 128, 128), np.float32)
    for cls in range(2):
        bk = t5_bucket_np(t_idx - s_idx + 128 * cls)
        tiles[:, cls] = np.transpose(rb[bk], (2, 0, 1))
    w["bias_tiles"] = tiles
    w["b31"] = np.ascontiguousarray(rb[31:32, :])
    return w


_NC_CACHE = {}


def kernel(**inputs):
    x = np.asarray(inputs["x"], dtype=np.float32)
    w = host_layout(inputs)
    if "nc" not in _NC_CACHE:
        _NC_CACHE["nc"] = build_program()
    nc = _NC_CACHE["nc"]
    in_maps = []
    for b in range(8):
        m = dict(w)
        m["x"] = np.ascontiguousarray(x[b])
        in_maps.append(m)
    res = run_bass_kernel_spmd(nc, in_maps, core_ids=list(range(8)))
    return np.stack([np.asarray(r["out"], dtype=np.float32) for r in res.results], axis=0)
```

```python
import math
from contextlib import ExitStack

import numpy as np
import concourse.bass as bass
import concourse.mybir as mybir
from concourse.bass_utils import run_bass_kernel_spmd

F32 = mybir.dt.float32
BF16 = mybir.dt.bfloat16
AF = mybir.ActivationFunctionType
ALU = mybir.AluOpType
AX = mybir.AxisListType

S = 4096
D = 1024
NT = S // 128
ATTN_IN = 2120
RWKV_IN = 1824
FFN_H = 2816
RMS_EPS = 1e-6
GN_EPS = 64e-5
TOPK = 256
NEG = -1.0e30

ENGS = ("pe", "act", "dve", "pool", "sp")


class Buf:
    __slots__ = ("name", "w", "r")

    def __init__(self, name=""):
        self.name = name
        self.w = None
        self.r = []


class Op:
    __slots__ = ("eng", "thunk", "deps", "is_dma", "sem", "val", "need_inc", "pos", "prev_on_sem")


class Prog:
    def __init__(self, nc, n_dma_sems=40):
        self.nc = nc
        self.streams = {e: [] for e in ENGS}
        self.n_dma_sems = n_dma_sems
        self.dma_count = 0
        self.all_ops = []
        self.open_dmas = []

    def _hazards(self, op, reads, writes):
        deps = []
        for b in reads:
            if b.w is not None:
                deps.append(b.w)
        for b in writes:
            if b.w is not None:
                deps.append(b.w)
            deps.extend(b.r)
        for b in reads:
            b.r.append(op)
        for b in writes:
            b.w = op
            b.r = []
        return deps

    def op(self, eng, thunk, reads=(), writes=(), extra_deps=()):
        o = Op()
        o.eng = eng
        o.thunk = thunk
        o.is_dma = False
        o.need_inc = False
        o.sem = None
        o.val = None
        o.prev_on_sem = None
        deps = self._hazards(o, reads, writes) + list(extra_deps)
        seen = set()
        o.deps = []
        for d in deps:
            if d is o or id(d) in seen:
                continue
            if eng == "pe" and d.eng == "pe" and not d.is_dma:
                continue
            seen.add(id(d))
            o.deps.append(d)
        self.streams[eng].append(o)
        self.all_ops.append(o)
        return o

    def dma(self, eng, thunk, reads=(), writes=(), extra_deps=()):
        o = self.op(eng, thunk, reads, writes, extra_deps)
        o.is_dma = True
        o.pos = self.dma_count
        self.dma_count += 1
        self.open_dmas.append(o)
        return o

    def barrier(self):
        lasts = []
        for e in ENGS:
            for o in reversed(self.streams[e]):
                if not o.is_dma:
                    lasts.append(o)
                    break
        deps = lasts + self.open_dmas
        self.open_dmas = []
        for e in ENGS:
            self.op(e, lambda eng: eng.nop(), extra_deps=deps)

    def emit(self, final_wait_ops=()):
        nc = self.nc
        for o in self.all_ops:
            for d in o.deps:
                d.need_inc = True
        eng_sems = {e: nc.alloc_semaphore("s_" + e) for e in ENGS}
        dma_sems = [nc.alloc_semaphore("s_dma%d" % i) for i in range(self.n_dma_sems)]
        dma_sem_val = [0] * self.n_dma_sems
        dma_prev = [None] * self.n_dma_sems
        cnt = {e: 0 for e in ENGS}
        for o in self.all_ops:
            if o.is_dma:
                k = o.pos % self.n_dma_sems
                dma_sem_val[k] += 16
                o.sem = ("dma", k)
                o.val = dma_sem_val[k]
                o.prev_on_sem = dma_prev[k]
                dma_prev[k] = o
            elif o.need_inc:
                cnt[o.eng] += 1
                o.sem = ("eng", o.eng)
                o.val = cnt[o.eng]

        def semh(key):
            return eng_sems[key[1]] if key[0] == "eng" else dma_sems[key[1]]

        engines = {"pe": "tensor", "act": "scalar", "dve": "vector", "pool": "gpsimd", "sp": "sync"}
        with nc.Block() as block:
            for e in ENGS:
                stream = self.streams[e]
                final = list(final_wait_ops) if e == "sp" else []

                def body(engine, stream=stream, final=final):
                    known = {}
                    for o in stream:
                        waits = {}
                        deps = list(o.deps)
                        if o.is_dma and o.prev_on_sem is not None:
                            deps.append(o.prev_on_sem)
                        for d in deps:
                            if known.get(d.sem, 0) >= d.val:
                                continue
                            if waits.get(d.sem, 0) < d.val:
                                waits[d.sem] = d.val
                        for key, val in waits.items():
                            engine.wait_ge(semh(key), val)
                            known[key] = val
                        ins = o.thunk(engine)
                        if o.is_dma:
                            ins.then_inc(semh(o.sem), 16)
                        elif o.need_inc:
                            ins.then_inc(semh(o.sem), 1)
                    for o in final:
                        engine.wait_ge(semh(o.sem), o.val)

                getattr(block, engines[e])(body)


class Ring:
    def __init__(self, items):
        self.items = items
        self.i = 0

    def next(self):
        it = self.items[self.i % len(self.items)]
        self.i += 1
        return it


def load_weight_bf16(P, nc, dst, dst_buf, w_ap, c0, c1, kchunks, eng="pool"):
    wv = w_ap.rearrange("(kc p) n -> p kc n", p=128)
    for kc in range(kchunks):
        for a in range(c0, c1, 2048):
            b = min(c1, a + 2048)
            P.dma(eng, lambda e, kc=kc, a=a, b=b: e.dma_start(out=dst[:, kc, a - c0:b - c0], in_=wv[:, kc, a:b]),
                  writes=[dst_buf])


def load_col_vec(P, nc, dst, dst_buf, v_ap, n):
    src = v_ap.rearrange("o (c p) -> p (o c)", p=128)
    P.dma("sp", lambda e: e.dma_start(out=dst, in_=src, allow_slow_non_contiguous=True), writes=[dst_buf])


class Consts:
    pass


def make_consts(P, nc, es):
    C = Consts()
    C.ident_bf = es.enter_context(nc.sbuf_tensor("ident_bf", [128, 128], BF16))
    C.ident_f = es.enter_context(nc.sbuf_tensor("ident_f", [128, 128], F32))
    C.eps = es.enter_context(nc.sbuf_tensor("eps_c", [128, 2], F32))
    C.b = Buf("consts")

    def mk(e):
        e.memset(C.ident_f[:], 0.0)
        e.affine_select(out=C.ident_f[:], in_=C.ident_f[:], pattern=[[-1, 128]], compare_op=ALU.not_equal,
                        fill=1.0, base=0, channel_multiplier=1)
        e.memset(C.eps[:, 0:1], RMS_EPS)
        return e.memset(C.eps[:, 1:2], GN_EPS)

    P.op("pool", mk, writes=[C.b])
    P.op("pool", lambda e: e.tensor_copy(out=C.ident_bf[:], in_=C.ident_f[:]), reads=[C.b], writes=[C.b])
    return C


class Normer:
    def __init__(self, P, nc, es, C, gain_ap, name, nslots=2):
        self.P, self.nc, self.C = P, nc, C
        self.gcol = es.enter_context(nc.sbuf_tensor(name + "_g", [128, 8], F32))
        self.gb = Buf(name + "_g")
        load_col_vec(P, nc, self.gcol[:, :], self.gb, gain_ap, 8)
        self.stat = es.enter_context(nc.sbuf_tensor(name + "_st", [128, nslots, 4], F32))
        self.junk = es.enter_context(nc.sbuf_tensor(name + "_junk", [128, 1024], BF16))
        self.xs = es.enter_context(nc.sbuf_tensor(name + "_xs", [128, nslots, 1024], BF16))
        self.tp = es.enter_context(nc.psum_tensor(name + "_tp", [128, nslots, 8, 128], BF16))
        self.ring = Ring([(i, Buf(), Buf(), Buf()) for i in range(nslots)])
        self.junkb = Buf()

    def run(self, xt_ap, xt_buf, dst, dst_buf, col0):
        P, C = self.P, self.C
        i, sb, xb, pb = self.ring.next()
        st = self.stat
        P.op("act", lambda e: e.activation(out=self.junk[:], in_=xt_ap, func=AF.Square, accum_out=st[:, i, 0:1]),
             reads=[xt_buf], writes=[self.junkb, sb])
        P.op("act", lambda e: e.activation(out=st[:, i, 1:2], in_=st[:, i, 0:1], func=AF.Sqrt, scale=1.0 / D,
                                           bias=C.eps[:, 0:1]), reads=[sb, C.b], writes=[sb])
        P.op("dve", lambda e: e.reciprocal(out=st[:, i, 2:3], in_=st[:, i, 1:2]), reads=[sb], writes=[sb])
        P.op("act", lambda e: e.activation(out=self.xs[:, i, :], in_=xt_ap, func=AF.Copy, scale=st[:, i, 2:3]),
             reads=[xt_buf, sb], writes=[xb])

        def tr(e):
            ins = None
            for kc in range(8):
                ins = e.transpose(out=self.tp[:, i, kc, :], in_=self.xs[:, i, kc * 128:(kc + 1) * 128],
                                  identity=C.ident_bf[:])
            return ins

        P.op("pe", tr, reads=[xb, C.b], writes=[pb])
        P.op("dve", lambda e: e.tensor_tensor(out=dst[:, :, col0:col0 + 128], in0=self.tp[:, i, :, :],
                                              in1=self.gcol[:, :].unsqueeze(2).broadcast_to([128, 8, 128]),
                                              op=ALU.mult), reads=[pb, self.gb], writes=[dst_buf])


def phase_ffn(P, nc, C, h_dram, out_dram, ffn_norm, w_gate_up, w_down, final_ops):
    with ExitStack() as es:
        wgu = es.enter_context(nc.sbuf_tensor("wgu", [128, 8, 2 * FFN_H], BF16))
        wd = es.enter_context(nc.sbuf_tensor("wd", [128, 22, D], BF16))
        wgu_b, wd_b = Buf("wgu"), Buf("wd")
        load_weight_bf16(P, nc, wgu, wgu_b, w_gate_up, 0, 2 * FFN_H, 8)
        load_weight_bf16(P, nc, wd, wd_b, w_down, 0, D, 22)
        nm = Normer(P, nc, es, C, ffn_norm, "fn")
        hbuf = es.enter_context(nc.sbuf_tensor("hbuf", [128, 2, D], F32))
        hr = Ring([(i, Buf()) for i in range(2)])
        hnT = es.enter_context(nc.sbuf_tensor("hnT", [128, 2, 8, 512], BF16))
        hnb = [Buf(), Buf()]
        actT = es.enter_context(nc.sbuf_tensor("actT", [128, 22, 512], BF16))
        actb = [Buf() for _ in range(22)]
        sg = es.enter_context(nc.sbuf_tensor("sg", [128, 2, 512], F32))
        sgr = Ring([(i, Buf()) for i in range(2)])
        ost = es.enter_context(nc.sbuf_tensor("ost", [128, 2, D], F32))
        ostr = Ring([(i, Buf()) for i in range(2)])
        pg = es.enter_context(nc.psum_tensor("pg", [128, 2, 512], F32))
        pu = es.enter_context(nc.psum_tensor("pu", [128, 2, 512], F32))
        po = es.enter_context(nc.psum_tensor("po", [128, 2, 512], F32))
        pgr = Ring([(i, Buf()) for i in range(2)])
        pur = Ring([(i, Buf()) for i in range(2)])
        por = Ring([(i, Buf()) for i in range(2)])
        hview = h_dram.rearrange("(t p) d -> t p d", p=128)
        oview = out_dram.rearrange("(t p) d -> t p d", p=128)
        for c in range(S // 512):
            cb = c % 2
            for j in range(4):
                t = c * 4 + j
                hi, hb_ = hr.next()
                P.dma("sp", lambda e, t=t, hi=hi: e.dma_start(out=hbuf[:, hi, :], in_=hview[t]), writes=[hb_])
                nm.run(hbuf[:, hi, :], hb_, hnT[:, cb], hnb[cb], j * 128)
            for m in range(22):
                gi, gbuf = pgr.next()
                ui, ubuf = pur.next()

                def mm(e, m=m, gi=gi, ui=ui, cb=cb):
                    for kc in range(8):
                        e.matmul(pg[:, gi, :], lhsT=wgu[:, kc, m * 128:(m + 1) * 128], rhs=hnT[:, cb, kc, :],
                                 start=(kc == 0), stop=(kc == 7))
                    ins = None
                    for kc in range(8):
                        ins = e.matmul(pu[:, ui, :], lhsT=wgu[:, kc, FFN_H + m * 128:FFN_H + (m + 1) * 128],
                                       rhs=hnT[:, cb, kc, :], start=(kc == 0), stop=(kc == 7))
                    return ins

                P.op("pe", mm, reads=[wgu_b, hnb[cb]], writes=[gbuf, ubuf])
                si, sbuf_ = sgr.next()
                P.op("act", lambda e, gi=gi, si=si: e.activation(out=sg[:, si, :], in_=pg[:, gi, :], func=AF.Silu),
                     reads=[gbuf], writes=[sbuf_])
                P.op("dve", lambda e, ui=ui, si=si, m=m: e.tensor_tensor(out=actT[:, m, :], in0=pu[:, ui, :],
                                                                         in1=sg[:, si, :], op=ALU.mult),
                     reads=[ubuf, sbuf_], writes=[actb[m]])
            for j in range(4):
                t = c * 4 + j
                oi, obuf = ostr.next()
                P.dma("sp", lambda e, t=t, oi=oi: e.dma_start(out=ost[:, oi, :], in_=hview[t]), writes=[obuf])
                for nh in range(2):
                    pi, pbuf = por.next()

                    def mmd(e, j=j, nh=nh, pi=pi):
                        ins = None
                        for m in range(22):
                            ins = e.matmul(po[:, pi, :], lhsT=actT[:, m, j * 128:(j + 1) * 128],
                                           rhs=wd[:, m, nh * 512:(nh + 1) * 512], start=(m == 0), stop=(m == 21))
                        return ins

                    P.op("pe", mmd, reads=actb + [wd_b], writes=[pbuf])
                    P.op("dve", lambda e, nh=nh, pi=pi, oi=oi: e.tensor_tensor(
                        out=ost[:, oi, nh * 512:(nh + 1) * 512], in0=po[:, pi, :],
                        in1=ost[:, oi, nh * 512:(nh + 1) * 512], op=ALU.add),
                        reads=[pbuf], writes=[obuf])
                final_ops.append(P.dma("sp", lambda e, t=t, oi=oi: e.dma_start(out=oview[t], in_=ost[:, oi, :]),
                                       reads=[obuf]))
    P.barrier()


def phase_merge(P, nc, C, x_dram, ya_dram, yr_dram, h_dram, mix_norm, w_in, w_ba, w_br, w_out):
    with ExitStack() as es:
        wg = es.enter_context(nc.sbuf_tensor("wg", [128, 8, 2 * D], BF16))
        wba = es.enter_context(nc.sbuf_tensor("wba", [128, 4, D], BF16))
        wbr = es.enter_context(nc.sbuf_tensor("wbr", [128, 4, D], BF16))
        wo = es.enter_context(nc.sbuf_tensor("wo", [128, 8, D], BF16))
        wg_b, wba_b, wbr_b, wo_b = Buf(), Buf(), Buf(), Buf()
        load_weight_bf16(P, nc, wg, wg_b, w_in, ATTN_IN + RWKV_IN, ATTN_IN + RWKV_IN + 2 * D, 8)
        load_weight_bf16(P, nc, wba, wba_b, w_ba, 0, D, 4)
        load_weight_bf16(P, nc, wbr, wbr_b, w_br, 0, D, 4)
        load_weight_bf16(P, nc, wo, wo_b, w_out, 0, D, 8)
        nm = Normer(P, nc, es, C, mix_norm, "mn")
        xbuf = es.enter_context(nc.sbuf_tensor("xbuf", [128, 2, 4, D], F32))
        xb = [[Buf() for _ in range(4)] for _ in range(2)]
        xnT = es.enter_context(nc.sbuf_tensor("xnT", [128, 2, 8, 512], BF16))
        xnb = [Buf(), Buf()]
        yaT = es.enter_context(nc.sbuf_tensor("yaT", [128, 2, 4, 512], BF16))
        yrT = es.enter_context(nc.sbuf_tensor("yrT", [128, 2, 4, 512], BF16))
        yab, yrb = [Buf(), Buf()], [Buf(), Buf()]
        mT = es.enter_context(nc.sbuf_tensor("mT", [128, 8, 512], BF16))
        mb = [Buf() for _ in range(8)]
        sg = es.enter_context(nc.sbuf_tensor("sgm", [128, 2, 2, 512], F32))
        sgr = Ring([(i, Buf()) for i in range(2)])
        tt = es.enter_context(nc.sbuf_tensor("ttm", [128, 2, 2, 512], F32))
        ttr = Ring([(i, Buf()) for i in range(2)])
        hst = es.enter_context(nc.sbuf_tensor("hst", [128, 2, D], F32))
        hstr = Ring([(i, Buf()) for i in range(2)])
        pga = es.enter_context(nc.psum_tensor("pga", [128, 2, 512], F32))
        pbr = es.enter_context(nc.psum_tensor("pbr", [128, 2, 512], F32))
        po = es.enter_context(nc.psum_tensor("pom", [128, 2, 512], F32))
        pgb, pbb = Buf(), Buf()
        por = Ring([(i, Buf()) for i in range(2)])
        xview = x_dram.rearrange("(t p) d -> t p d", p=128)
        hview = h_dram.rearrange("(t p) d -> t p d", p=128)
        yav = ya_dram.rearrange("(kc p) s -> p kc s", p=128)
        yrv = yr_dram.rearrange("(kc p) s -> p kc s", p=128)
        for c in range(S // 512):
            cb = c % 2
            P.dma("sp", lambda e, c=c, cb=cb: e.dma_start(out=yaT[:, cb], in_=yav[:, :, c * 512:(c + 1) * 512]),
                  writes=[yab[cb]])
            P.dma("sp", lambda e, c=c, cb=cb: e.dma_start(out=yrT[:, cb], in_=yrv[:, :, c * 512:(c + 1) * 512]),
                  writes=[yrb[cb]])
            for j in range(4):
                t = c * 4 + j
                P.dma("sp", lambda e, t=t, cb=cb, j=j: e.dma_start(out=xbuf[:, cb, j, :], in_=xview[t]),
                      writes=[xb[cb][j]])
                nm.run(xbuf[:, cb, j, :], xb[cb][j], xnT[:, cb], xnb[cb], j * 128)
            for m in range(8):
                def mmg(e, m=m, cb=cb):
                    ins = None
                    for g in range(2):
                        for kc in range(8):
                            ins = e.matmul(pga[:, g, :], lhsT=wg[:, kc, g * D + m * 128:g * D + (m + 1) * 128],
                                           rhs=xnT[:, cb, kc, :], start=(kc == 0), stop=(kc == 7))
                    return ins

                P.op("pe", mmg, reads=[wg_b, xnb[cb]], writes=[pgb])

                def mmb(e, m=m, cb=cb):
                    ins = None
                    for kc in range(4):
                        ins = e.matmul(pbr[:, 0, :], lhsT=wba[:, kc, m * 128:(m + 1) * 128], rhs=yaT[:, cb, kc, :],
                                       start=(kc == 0), stop=(kc == 3))
                    for kc in range(4):
                        ins = e.matmul(pbr[:, 1, :], lhsT=wbr[:, kc, m * 128:(m + 1) * 128], rhs=yrT[:, cb, kc, :],
                                       start=(kc == 0), stop=(kc == 3))
                    return ins

                P.op("pe", mmb, reads=[wba_b, wbr_b, yab[cb], yrb[cb]], writes=[pbb])
                si, sbuf_ = sgr.next()
                P.op("act", lambda e, si=si: e.activation(out=sg[:, si], in_=pga[:, :, :], func=AF.Sigmoid),
                     reads=[pgb], writes=[sbuf_])
                ti, tbuf = ttr.next()
                P.op("dve", lambda e, si=si, ti=ti: e.tensor_tensor(out=tt[:, ti], in0=pbr[:, :, :], in1=sg[:, si],
                                                                    op=ALU.mult),
                     reads=[pbb, sbuf_], writes=[tbuf])
                P.op("pool", lambda e, ti=ti, m=m: e.tensor_tensor(out=mT[:, m, :], in0=tt[:, ti, 0, :],
                                                                   in1=tt[:, ti, 1, :], op=ALU.add),
                     reads=[tbuf], writes=[mb[m]])
            for j in range(4):
                t = c * 4 + j
                hi, hbuf_ = hstr.next()
                for nh in range(2):
                    pi, pbuf = por.next()

                    def mmo(e, j=j, nh=nh, pi=pi):
                        ins = None
                        for m in range(8):
                            ins = e.matmul(po[:, pi, :], lhsT=mT[:, m, j * 128:(j + 1) * 128],
                                           rhs=wo[:, m, nh * 512:(nh + 1) * 512], start=(m == 0), stop=(m == 7))
                        return ins

                    P.op("pe", mmo, reads=mb + [wo_b], writes=[pbuf])
                    P.op("dve", lambda e, j=j, nh=nh, pi=pi, hi=hi, cb=cb: e.tensor_tensor(
                        out=hst[:, hi, nh * 512:(nh + 1) * 512], in0=po[:, pi, :],
                        in1=xbuf[:, cb, j, nh * 512:(nh + 1) * 512], op=ALU.add),
                        reads=[pbuf, xb[cb][j]], writes=[hbuf_])
                P.dma("sp", lambda e, t=t, hi=hi: e.dma_start(out=hview[t], in_=hst[:, hi, :]), reads=[hbuf_])
    P.barrier()


TL = 128
C0 = math.exp(-0.5)


def col8(P, nc, es, name, v_ap):
    t = es.enter_context(nc.sbuf_tensor(name, [64, 8], F32))
    b = Buf(name)
    P.dma("sp", lambda e: e.dma_start(out=t[:, :], in_=v_ap.rearrange("o (h k) -> k (o h)", k=64),
                                      allow_slow_non_contiguous=True), writes=[b])
    return t, b


def phase_rwkv(P, nc, C, A, yr_dram):
    x_dram = A["x"]
    with ExitStack() as es:
        sb = lambda name, shape, dt=F32: es.enter_context(nc.sbuf_tensor(name, shape, dt))
        wr = sb("wr", [128, 8, RWKV_IN], BF16)
        wmu = sb("wmu", [128, 8, RWKV_IN], BF16)
        mub = sb("mub", [128, RWKV_IN], F32)
        wr_b, wmu_b, mub_b = Buf(), Buf(), Buf()
        load_weight_bf16(P, nc, wr, wr_b, A["w_in"], ATTN_IN, ATTN_IN + RWKV_IN, 8)
        P.dma("sp", lambda e: e.dma_start(out=mub[:], in_=A["rwkv_mu"].partition_broadcast(128)), writes=[mub_b])
        for kc in range(8):
            P.op("pool", lambda e, kc=kc: e.tensor_tensor(out=wmu[:, kc, :], in0=wr[:, kc, :], in1=mub[:],
                                                          op=ALU.mult), reads=[wr_b, mub_b], writes=[wmu_b])
        w2 = sb("w2", [64, 512], BF16)
        a2 = sb("a2", [64, 512], BF16)
        g2 = sb("g2", [64, 3, 512], BF16)
        lw_b = Buf()
        P.dma("pool", lambda e: e.dma_start(out=w2[:], in_=A["rwkv_w2"]), writes=[lw_b])
        P.dma("pool", lambda e: e.dma_start(out=a2[:], in_=A["rwkv_a2"]), writes=[lw_b])
        P.dma("pool", lambda e: e.dma_start(out=g2[:, 0, :], in_=A["rwkv_g2"][0:64, :]), writes=[lw_b])
        P.dma("pool", lambda e: e.dma_start(out=g2[:, 1, :], in_=A["rwkv_g2"][64:128, :]), writes=[lw_b])
        P.dma("pool", lambda e: e.dma_start(out=g2[0:32, 2, :], in_=A["rwkv_g2"][128:160, :]), writes=[lw_b])
        cw0, b_w0 = col8(P, nc, es, "cw0", A["rwkv_w0"])
        ca0, b_a0 = col8(P, nc, es, "ca0", A["rwkv_a0"])
        ckk, b_kk = col8(P, nc, es, "ckk", A["rwkv_k_k"])
        cka, b_ka = col8(P, nc, es, "cka", A["rwkv_k_a"])
        crk, b_rk = col8(P, nc, es, "crk", A["rwkv_r_k"])
        clw, b_lw = col8(P, nc, es, "clw", A["rwkv_ln_w"])
        clb, b_lb = col8(P, nc, es, "clb", A["rwkv_ln_b"])
        msk = sb("rmsk", [64, 3, 64], F32)
        ones_bf = sb("ones_bf", [64, 64], BF16)
        rmask = sb("rmask", [64, 8, TL // 64, 64], F32)
        mk_b = Buf()

        def mkmasks(e):
            e.memset(msk[:], 1.0)
            e.memset(ones_bf[:], 1.0)
            e.memset(rmask[:], 1.0)
            e.memset(rmask[:, :, :, 0:1], 0.0)
            e.affine_select(out=msk[:, 0, :], in_=msk[:, 0, :], pattern=[[1, 64]], compare_op=ALU.is_ge,
                            fill=0.0, base=-1, channel_multiplier=-1)
            e.affine_select(out=msk[:, 1, :], in_=msk[:, 1, :], pattern=[[1, 64]], compare_op=ALU.is_ge,
                            fill=0.0, base=0, channel_multiplier=-1)
            return e.affine_select(out=msk[:, 2, :], in_=msk[:, 2, :], pattern=[[-1, 64]], compare_op=ALU.is_ge,
                                   fill=0.0, base=-1, channel_multiplier=1)

        P.op("pool", mkmasks, writes=[mk_b])
        mb3 = lambda i: msk[:, i, :].unsqueeze(1).broadcast_to([64, 8, 64])
        idb = C.ident_bf[0:64, 0:64]
        idf3 = C.ident_f[0:64, 0:64].unsqueeze(1).broadcast_to([64, 8, 64])
        bc = lambda col: col[:, :].unsqueeze(2).broadcast_to([64, 8, TL])

        nm = Normer(P, nc, es, C, A["mix_norm"], "rn", nslots=1)
        xbuf = sb("rxbuf", [128, 2, D], F32)
        xr = Ring([(i, Buf()) for i in range(2)])
        xnx = sb("xnx", [128, 2, 8, TL + 1], BF16)
        xnb = [Buf(), Buf()]
        dxn = sb("dxn", [128, 8, TL], BF16)
        dxb = Buf()
        P.op("pool", lambda e: e.memset(xnx[:, 1, :, TL:TL + 1], 0.0), writes=[xnb[1]])
        F = {}
        FB = {}
        for nme, dt in [("r", F32), ("k", F32), ("v", F32), ("sgd", F32), ("a", F32), ("g", BF16), ("kk", F32),
                        ("t1", F32), ("t2", F32), ("cum", F32), ("e1", F32), ("e2", F32),
                        ("rT", BF16), ("aT", BF16), ("bT", BF16), ("kT", BF16), ("bH", BF16), ("kH", BF16),
                        ("vb", BF16), ("sqb", BF16), ("bon", F32), ("ynT", F32)]:
            alias = {"e1": "t1", "e2": "a", "bon": "sgd", "ynT": "kk"}
            if nme in alias:
                F[nme] = F[alias[nme]]
                FB[nme] = FB[alias[nme]]
                continue
            F[nme] = sb("f_" + nme, [64, 8, TL], dt)
            FB[nme] = Buf(nme)
        lora = sb("lora", [64, 5, TL], BF16)
        lora_b = Buf()
        gC = sb("gC", [64, 8, TL // 64], F32)
        gC_b = Buf()
        Sf = sb("Sf", [64, 8, 64], F32)
        Sb = sb("Sb", [64, 8, 64], BF16)
        St = sb("St", [64, 8, 64], F32)
        S_b, St_b = Buf(), Buf()
        P.op("dve", lambda e: e.memset(Sf[:], 0.0), writes=[S_b])
        P.op("dve", lambda e: e.memset(Sb[:], 0.0), reads=[S_b], writes=[S_b])
        ost = sb("rost", [64, 8, TL], BF16)
        ost_b = Buf()
        def ring2(name, dt=BF16, shape=(64, 8, 64)):
            t = sb(name, [shape[0], 2] + list(shape[1:]), dt)
            return t, Ring([(i, Buf()) for i in range(2)])
        Atok, Atok_r = ring2("Atok")
        BHtok, BHtok_r = ring2("BHtok")
        KHtok, KHtok_r = ring2("KHtok")
        Vtok, Vtok_r = ring2("Vtok")
        Mrb, Mrb_r = ring2("Mrb")
        Mrk, Mrk_r = ring2("Mrk")
        Lak, Lak_r = ring2("Lak")
        Nn, Nn_r = ring2("Nn")
        Mm, Mm_r = ring2("Mm")
        Qq, Qq_r = ring2("Qq")
        WT, WT_r = ring2("WT")
        Xx, Xx_r = ring2("Xx")
        Uu, Uu_r = ring2("Uu")
        Ys, Ys_r = ring2("Ys", F32)
        Yq, Yq_r = ring2("Yq", F32)
        Yn, Yn_r = ring2("Yn", BF16)
        gst, gst_r = ring2("gst", F32, (64, 8, 6))
        pb = es.enter_context(nc.psum_tensor("rpb", [128, 7, 512], F32))
        pr = Ring([(i, Buf()) for i in range(7)])
        pv = lambda i: pb[0:64, i, :].rearrange("p (h t) -> p h t", h=8)
        pvb = lambda i: pb[0:64, i, :].bitcast(BF16)[:, 0:512].rearrange("p (h t) -> p h t", h=8)

        xview = x_dram.rearrange("(t p) d -> t p d", p=128)

        def mm8(bank, parts, rd):
            i, bbuf = bank
            ops = [[(lf(h), rf(h)) for (lf, rf) in parts] for h in range(8)]

            def th(e):
                ins = None
                for h in range(8):
                    for pi, (l, r) in enumerate(ops[h]):
                        ins = e.matmul(pb[0:64, i, h * 64:(h + 1) * 64], lhsT=l, rhs=r,
                                       start=(pi == 0), stop=(pi == len(ops[h]) - 1))
                return ins

            P.op("pe", th, reads=rd, writes=[bbuf])

        def tr8(bank, src_fn, rd):
            i, bbuf = bank
            srcs = [src_fn(h) for h in range(8)]

            def th(e):
                ins = None
                v = pvb(i)
                for h in range(8):
                    ins = e.transpose(out=v[:, h, :], in_=srcs[h], identity=idb)
                return ins

            P.op("pe", th, reads=rd + [C.b], writes=[bbuf])

        for c5 in range(S // TL):
            cb = c5 % 2
            pc = 1 - cb
            P.op("pool", lambda e, cb=cb, pc=pc: e.tensor_copy(out=xnx[:, cb, :, 0:1], in_=xnx[:, pc, :, TL:TL + 1]),
                 reads=[xnb[pc]], writes=[xnb[cb]])
            for j in range(TL // 128):
                xi, xb_ = xr.next()
                P.dma("sp", lambda e, t=c5 * (TL // 128) + j, xi=xi: e.dma_start(out=xbuf[:, xi, :], in_=xview[t]), writes=[xb_])
                nm.run(xbuf[:, xi, :], xb_, xnx[:, cb], xnb[cb], 1 + j * 128)
            P.op("pool", lambda e, cb=cb: e.tensor_tensor(out=dxn[:], in0=xnx[:, cb, :, 0:TL], in1=xnx[:, cb, :, 1:TL + 1],
                                                          op=ALU.subtract), reads=[xnb[cb]], writes=[dxb])

            def proj(c0, m, bank, cb=cb):
                i, bbuf = bank

                def th(e):
                    ins = None
                    for kc in range(8):
                        e.matmul(pb[0:m, i, 0:TL], lhsT=wr[:, kc, c0:c0 + m], rhs=xnx[:, cb, kc, 1:TL + 1],
                                 start=(kc == 0), stop=False)
                    for kc in range(8):
                        ins = e.matmul(pb[0:m, i, 0:TL], lhsT=wmu[:, kc, c0:c0 + m], rhs=dxn[:, kc, :],
                                       start=False, stop=(kc == 7))
                    return ins

                P.op("pe", th, reads=[wr_b, wmu_b, xnb[cb], dxb], writes=[bbuf])

            for qi, qn in enumerate(("r", "k", "v")):
                for h in range(8):
                    bank = pr.next()
                    proj(qi * 512 + h * 64, 64, bank)
                    P.op("act", lambda e, qn=qn, h=h, i=bank[0]: e.activation(out=F[qn][:, h, :], in_=pb[0:64, i, 0:TL],
                                                                              func=AF.Copy),
                         reads=[bank[1]], writes=[FB[qn]])
            for li, (c0, m, fn) in enumerate([(1536, 64, AF.Tanh), (1600, 64, AF.Copy), (1664, 64, AF.Sigmoid),
                                              (1728, 64, AF.Sigmoid), (1792, 32, AF.Sigmoid)]):
                bank = pr.next()
                proj(c0, m, bank)
                P.op("act", lambda e, li=li, m=m, fn=fn, i=bank[0]: e.activation(out=lora[0:m, li, :], in_=pb[0:m, i, 0:TL],
                                                                                 func=fn),
                     reads=[bank[1]], writes=[lora_b])
            for h in range(8):
                bank = pr.next()
                P.op("pe", lambda e, h=h, i=bank[0]: e.matmul(pb[0:64, i, 0:TL], lhsT=w2[:, h * 64:(h + 1) * 64],
                                                              rhs=lora[:, 0, :], start=True, stop=True),
                     reads=[lw_b, lora_b], writes=[bank[1]])
                P.op("act", lambda e, h=h, i=bank[0]: e.activation(out=F["sgd"][:, h, :], in_=pb[0:64, i, 0:TL],
                                                                   func=AF.Sigmoid, bias=cw0[:, h:h + 1]),
                     reads=[bank[1], b_w0], writes=[FB["sgd"]])
                bank = pr.next()
                P.op("pe", lambda e, h=h, i=bank[0]: e.matmul(pb[0:64, i, 0:TL], lhsT=a2[:, h * 64:(h + 1) * 64],
                                                              rhs=lora[:, 1, :], start=True, stop=True),
                     reads=[lw_b, lora_b], writes=[bank[1]])
                P.op("act", lambda e, h=h, i=bank[0]: e.activation(out=F["a"][:, h, :], in_=pb[0:64, i, 0:TL],
                                                                   func=AF.Sigmoid, bias=ca0[:, h:h + 1]),
                     reads=[bank[1], b_a0], writes=[FB["a"]])
                bank = pr.next()

                def gmm(e, h=h, i=bank[0]):
                    e.matmul(pb[0:64, i, 0:TL], lhsT=g2[:, 0, h * 64:(h + 1) * 64], rhs=lora[:, 2, :], start=True, stop=False)
                    e.matmul(pb[0:64, i, 0:TL], lhsT=g2[:, 1, h * 64:(h + 1) * 64], rhs=lora[:, 3, :], start=False, stop=False)
                    return e.matmul(pb[0:64, i, 0:TL], lhsT=g2[0:32, 2, h * 64:(h + 1) * 64], rhs=lora[0:32, 4, :],
                                    start=False, stop=True)

                P.op("pe", gmm, reads=[lw_b, lora_b], writes=[bank[1]])
                P.op("act", lambda e, h=h, i=bank[0]: e.activation(out=F["g"][:, h, :], in_=pb[0:64, i, 0:TL], func=AF.Copy),
                     reads=[bank[1]], writes=[FB["g"]])
            P.op("dve", lambda e: e.tensor_tensor(out=F["kk"][:], in0=F["k"][:], in1=bc(ckk), op=ALU.mult),
                 reads=[FB["k"], b_kk], writes=[FB["kk"]])
            P.op("pool", lambda e: e.tensor_tensor(out=F["sqb"][:], in0=F["kk"][:], in1=F["kk"][:], op=ALU.mult),
                 reads=[FB["kk"]], writes=[FB["sqb"]])
            for h in range(8):
                bank = pr.next()
                P.op("pe", lambda e, h=h, i=bank[0]: e.matmul(pb[0:64, i, 0:TL], lhsT=ones_bf[:], rhs=F["sqb"][:, h, :],
                                                              start=True, stop=True),
                     reads=[mk_b, FB["sqb"]], writes=[bank[1]])
                P.op("act", lambda e, h=h, i=bank[0]: e.activation(out=F["t1"][:, h, :], in_=pb[0:64, i, 0:TL], func=AF.Sqrt),
                     reads=[bank[1]], writes=[FB["t1"]])
            P.op("dve", lambda e: e.tensor_scalar(out=F["t1"][:], in0=F["t1"][:], scalar1=1e-12, scalar2=None,
                                                  op0=ALU.max), reads=[FB["t1"]], writes=[FB["t1"]])
            P.op("dve", lambda e: e.reciprocal(out=F["t1"][:], in_=F["t1"][:]), reads=[FB["t1"]], writes=[FB["t1"]])
            P.op("dve", lambda e: e.tensor_tensor(out=F["kk"][:], in0=F["kk"][:], in1=F["t1"][:], op=ALU.mult),
                 reads=[FB["kk"], FB["t1"]], writes=[FB["kk"]])
            P.op("dve", lambda e: e.scalar_tensor_tensor(out=F["t2"][:], in0=F["a"][:], scalar=-1.0, in1=bc(cka),
                                                         op0=ALU.add, op1=ALU.mult),
                 reads=[FB["a"], b_ka], writes=[FB["t2"]])
            P.op("dve", lambda e: e.scalar_tensor_tensor(out=F["k"][:], in0=F["t2"][:], scalar=1.0, in1=F["k"][:],
                                                         op0=ALU.add, op1=ALU.mult),
                 reads=[FB["t2"], FB["k"]], writes=[FB["k"]])
            P.op("pool", lambda e: e.tensor_tensor(out=F["t2"][:], in0=F["kk"][:], in1=F["a"][:], op=ALU.mult),
                 reads=[FB["kk"], FB["a"]], writes=[FB["t2"]])
            P.op("dve", lambda e: e.tensor_tensor_scan(out=F["cum"][:].rearrange("p h t -> p (h t)"),
                                                       data0=rmask[:].rearrange("p h c t -> p (h c t)"),
                                                       data1=F["sgd"][:].rearrange("p h t -> p (h t)"),
                                                       initial=0.0, op0=ALU.mult, op1=ALU.add),
                 reads=[FB["sgd"], mk_b], writes=[FB["cum"]])
            cum4 = F["cum"][:].rearrange("p h (c t) -> p h c t", t=64)
            P.op("act", lambda e: e.activation(out=F["e1"][:], in_=F["cum"][:], func=AF.Exp, scale=-C0),
                 reads=[FB["cum"]], writes=[FB["e1"]])
            P.op("dve", lambda e: e.tensor_tensor(out=F["rT"][:], in0=F["r"][:], in1=F["e1"][:], op=ALU.mult),
                 reads=[FB["r"], FB["e1"]], writes=[FB["rT"]])
            P.op("act", lambda e: e.activation(out=gC[:], in_=cum4[:, :, :, 63], func=AF.Exp, scale=-C0),
                 reads=[FB["cum"]], writes=[gC_b])
            P.op("pool", lambda e: e.tensor_tensor(out=F["e2"][:], in0=F["cum"][:], in1=F["sgd"][:], op=ALU.subtract),
                 reads=[FB["cum"], FB["sgd"]], writes=[FB["e2"]])
            P.op("act", lambda e: e.activation(out=F["e2"][:], in_=F["e2"][:], func=AF.Exp, scale=-C0),
                 reads=[FB["e2"]], writes=[FB["e2"]])
            P.op("dve", lambda e: e.scalar_tensor_tensor(out=F["aT"][:], in0=F["kk"][:], scalar=-1.0, in1=F["e2"][:],
                                                         op0=ALU.mult, op1=ALU.mult),
                 reads=[FB["kk"], FB["e2"]], writes=[FB["aT"]])
            P.op("act", lambda e: e.activation(out=F["e1"][:], in_=F["cum"][:], func=AF.Exp, scale=C0),
                 reads=[FB["cum"], FB["rT"]], writes=[FB["e1"]])
            P.op("dve", lambda e: e.tensor_tensor(out=F["bT"][:], in0=F["t2"][:], in1=F["e1"][:], op=ALU.mult),
                 reads=[FB["t2"], FB["e1"]], writes=[FB["bT"]])
            P.op("pool", lambda e: e.tensor_tensor(out=F["kT"][:], in0=F["k"][:], in1=F["e1"][:], op=ALU.mult),
                 reads=[FB["k"], FB["e1"]], writes=[FB["kT"]])
            P.op("dve", lambda e: e.tensor_tensor(out=F["e2"][:].rearrange("p h (c t) -> p h c t", t=64),
                                                  in0=cum4[:, :, :, 63:64].broadcast_to([64, 8, TL // 64, 64]), in1=cum4,
                                                  op=ALU.subtract),
                 reads=[FB["cum"], FB["aT"]], writes=[FB["e2"]])
            P.op("act", lambda e: e.activation(out=F["e2"][:], in_=F["e2"][:], func=AF.Exp, scale=-C0),
                 reads=[FB["e2"]], writes=[FB["e2"]])
            P.op("dve", lambda e: e.tensor_tensor(out=F["bH"][:], in0=F["t2"][:], in1=F["e2"][:], op=ALU.mult),
                 reads=[FB["t2"], FB["e2"]], writes=[FB["bH"]])
            P.op("pool", lambda e: e.tensor_tensor(out=F["kH"][:], in0=F["k"][:], in1=F["e2"][:], op=ALU.mult),
                 reads=[FB["k"], FB["e2"]], writes=[FB["kH"]])
            P.op("act", lambda e: e.activation(out=F["vb"][:], in_=F["v"][:], func=AF.Copy),
                 reads=[FB["v"]], writes=[FB["vb"]])
            P.op("dve", lambda e: e.tensor_tensor(out=F["t1"][:], in0=F["r"][:], in1=F["k"][:], op=ALU.mult),
                 reads=[FB["r"], FB["k"]], writes=[FB["t1"]])
            P.op("pool", lambda e: e.tensor_tensor(out=F["sqb"][:], in0=F["t1"][:], in1=bc(crk), op=ALU.mult),
                 reads=[FB["t1"], b_rk], writes=[FB["sqb"]])
            for h in range(8):
                bank = pr.next()
                P.op("pe", lambda e, h=h, i=bank[0]: e.matmul(pb[0:64, i, 0:TL], lhsT=ones_bf[:], rhs=F["sqb"][:, h, :],
                                                              start=True, stop=True),
                     reads=[mk_b, FB["sqb"]], writes=[bank[1]])
                P.op("dve", lambda e, h=h, i=bank[0]: e.tensor_tensor(out=F["bon"][:, h, :], in0=pb[0:64, i, 0:TL],
                                                                      in1=F["v"][:, h, :], op=ALU.mult),
                     reads=[bank[1], FB["v"]], writes=[FB["bon"]])

            for c in range(TL // 64):
                cs = slice(c * 64, (c + 1) * 64)
                toks = []
                for (dst, dr, src) in ((Atok, Atok_r, "aT"), (BHtok, BHtok_r, "bH"), (KHtok, KHtok_r, "kH"),
                                       (Vtok, Vtok_r, "vb")):
                    bank = pr.next()
                    tr8(bank, lambda h, src=src: F[src][:, h, cs], [FB[src]])
                    di, db = dr.next()
                    P.op("act", lambda e, dst=dst, di=di, i=bank[0]: e.activation(out=dst[:, di], in_=pvb(i), func=AF.Copy),
                         reads=[bank[1]], writes=[db])
                    toks.append((di, db))
                (ai, a_b), (bhi, bh_b), (khi, kh_b), (vi, v_b) = toks
                def gmat(lname, rname, mi, dst, dr, add_ident=False):
                    bank = pr.next()
                    mm8(bank, [(lambda h: F[lname][:, h, cs], lambda h: F[rname][:, h, cs])], [FB[lname], FB[rname]])
                    di, db = dr.next()
                    P.op("dve", lambda e, i=bank[0]: e.tensor_tensor(out=dst[:, di], in0=pv(i), in1=mb3(mi), op=ALU.mult),
                         reads=[bank[1], mk_b], writes=[db])
                    return di, db

                m0i, m0_b = gmat("bT", "aT", 0, Mm, Mm_r)
                rbi, rb_b = gmat("bT", "rT", 1, Mrb, Mrb_r)
                lki, lk_b = gmat("kT", "aT", 0, Lak, Lak_r)
                rki, rk_b = gmat("kT", "rT", 1, Mrk, Mrk_r)
                n0i, n0_b = gmat("aT", "bT", 2, Nn, Nn_r)
                qi_, q_b = Qq_r.next()
                P.op("pool", lambda e, qi_=qi_, m0i=m0i: e.tensor_tensor(out=Qq[:, qi_], in0=Mm[:, m0i], in1=idf3, op=ALU.add),
                     reads=[m0_b, C.b], writes=[q_b])
                ni, n_b, mi_, m_b = n0i, n0_b, m0i, m0_b
                for lvl in range(1, 6):
                    bank = pr.next()
                    mm8(bank, [(lambda h, mi_=mi_: Mm[:, mi_, h, :], lambda h, ni=ni: Nn[:, ni, h, :])], [m_b, n_b])
                    if lvl < 5:
                        bank2 = pr.next()
                        mm8(bank2, [(lambda h, ni=ni: Nn[:, ni, h, :], lambda h, mi_=mi_: Mm[:, mi_, h, :])], [m_b, n_b])
                    nni, nn_b = Nn_r.next()
                    P.op("act", lambda e, nni=nni, i=bank[0]: e.activation(out=Nn[:, nni], in_=pv(i), func=AF.Copy),
                         reads=[bank[1]], writes=[nn_b])
                    if lvl < 5:
                        nmi, nm_b = Mm_r.next()
                        P.op("dve", lambda e, nmi=nmi, i=bank2[0]: e.tensor_copy(out=Mm[:, nmi], in_=pv(i)),
                             reads=[bank2[1]], writes=[nm_b])
                        mi_, m_b = nmi, nm_b
                    ni, n_b = nni, nn_b
                    bank3 = pr.next()
                    mm8(bank3, [(lambda h: idb, lambda h, qi_=qi_: Qq[:, qi_, h, :]),
                                (lambda h, ni=ni: Nn[:, ni, h, :], lambda h, qi_=qi_: Qq[:, qi_, h, :])], [q_b, n_b, C.b])
                    nqi, nq_b = Qq_r.next()
                    P.op("dve", lambda e, nqi=nqi, i=bank3[0]: e.tensor_copy(out=Qq[:, nqi], in_=pv(i)),
                         reads=[bank3[1]], writes=[nq_b])
                    qi_, q_b = nqi, nq_b
                bank = pr.next()
                mm8(bank, [(lambda h: Atok[:, ai, h, :], lambda h: Qq[:, qi_, h, :])], [a_b, q_b])
                wi, w_b = WT_r.next()
                P.op("act", lambda e, wi=wi, i=bank[0]: e.activation(out=WT[:, wi], in_=pv(i), func=AF.Copy),
                     reads=[bank[1]], writes=[w_b])
                bank = pr.next()
                mm8(bank, [(lambda h: Lak[:, lki, h, :], lambda h: Vtok[:, vi, h, :])], [lk_b, v_b])
                xi_, x_b = Xx_r.next()
                P.op("dve", lambda e, xi_=xi_, i=bank[0]: e.tensor_copy(out=Xx[:, xi_], in_=pv(i)),
                     reads=[bank[1]], writes=[x_b])
                bank = pr.next()
                mm8(bank, [(lambda h: WT[:, wi, h, :], lambda h: Sb[:, h, :]),
                           (lambda h: Qq[:, qi_, h, :], lambda h: Xx[:, xi_, h, :])], [w_b, S_b, q_b, x_b])
                ui, u_b = Uu_r.next()
                P.op("act", lambda e, ui=ui, i=bank[0]: e.activation(out=Uu[:, ui], in_=pv(i), func=AF.Copy),
                     reads=[bank[1]], writes=[u_b])
                banky = pr.next()
                mm8(banky, [(lambda h: F["rT"][:, h, cs], lambda h: Sb[:, h, :]),
                            (lambda h: Mrb[:, rbi, h, :], lambda h: Uu[:, ui, h, :]),
                            (lambda h: Mrk[:, rki, h, :], lambda h: Vtok[:, vi, h, :])],
                    [FB["rT"], S_b, rb_b, u_b, rk_b, v_b])
                banks = pr.next()
                mm8(banks, [(lambda h: BHtok[:, bhi, h, :], lambda h: Uu[:, ui, h, :]),
                            (lambda h: KHtok[:, khi, h, :], lambda h: Vtok[:, vi, h, :])], [bh_b, u_b, kh_b, v_b])
                P.op("dve", lambda e, c=c: e.tensor_tensor(out=St[:], in0=Sf[:],
                                                           in1=gC[:, :, c:c + 1].broadcast_to([64, 8, 64]), op=ALU.mult),
                     reads=[S_b, gC_b], writes=[St_b])
                P.op("dve", lambda e, i=banks[0]: e.tensor_tensor(out=Sf[:], in0=pv(i), in1=St[:], op=ALU.add),
                     reads=[banks[1], St_b], writes=[S_b])
                P.op("act", lambda e: e.activation(out=Sb[:], in_=Sf[:], func=AF.Copy), reads=[S_b], writes=[S_b])
                yi, y_b = Ys_r.next()
                P.op("act", lambda e, yi=yi, i=banky[0]: e.activation(out=Ys[:, yi], in_=pv(i), func=AF.Copy),
                     reads=[banky[1]], writes=[y_b])
                qi2, yq_b = Yq_r.next()
                P.op("pool", lambda e, yi=yi, qi2=qi2: e.tensor_tensor(out=Yq[:, qi2], in0=Ys[:, yi], in1=Ys[:, yi], op=ALU.mult),
                     reads=[y_b], writes=[yq_b])
                gi, g_b = gst_r.next()
                P.op("dve", lambda e, yi=yi, gi=gi: e.tensor_reduce(out=gst[:, gi, :, 0], in_=Ys[:, yi], axis=AX.X, op=ALU.add),
                     reads=[y_b], writes=[g_b])
                P.op("dve", lambda e, qi2=qi2, gi=gi: e.tensor_reduce(out=gst[:, gi, :, 1], in_=Yq[:, qi2], axis=AX.X, op=ALU.add),
                     reads=[yq_b, g_b], writes=[g_b])
                P.op("dve", lambda e, gi=gi: e.tensor_scalar(out=gst[:, gi, :, 2], in0=gst[:, gi, :, 0], scalar1=1.0 / 64,
                                                             scalar2=None, op0=ALU.mult), reads=[g_b], writes=[g_b])
                P.op("dve", lambda e, gi=gi: e.tensor_tensor(out=gst[:, gi, :, 3], in0=gst[:, gi, :, 2], in1=gst[:, gi, :, 2],
                                                             op=ALU.mult), reads=[g_b], writes=[g_b])
                P.op("dve", lambda e, gi=gi: e.scalar_tensor_tensor(out=gst[:, gi, :, 4], in0=gst[:, gi, :, 1], scalar=1.0 / 64,
                                                                    in1=gst[:, gi, :, 3], op0=ALU.mult, op1=ALU.subtract),
                     reads=[g_b], writes=[g_b])
                P.op("act", lambda e, gi=gi: e.activation(out=gst[:, gi, :, 5], in_=gst[:, gi, :, 4], func=AF.Sqrt,
                                                          bias=C.eps[0:64, 1:2]), reads=[g_b, C.b], writes=[g_b])
                P.op("dve", lambda e, gi=gi: e.reciprocal(out=gst[:, gi, :, 5], in_=gst[:, gi, :, 5]), reads=[g_b], writes=[g_b])
                P.op("dve", lambda e, yi=yi, gi=gi: e.tensor_tensor(out=Ys[:, yi], in0=Ys[:, yi],
                                                                    in1=gst[:, gi, :, 2:3].broadcast_to([64, 8, 64]),
                                                                    op=ALU.subtract), reads=[y_b, g_b], writes=[y_b])
                ni2, yn_b = Yn_r.next()
                P.op("dve", lambda e, yi=yi, gi=gi, ni2=ni2: e.tensor_tensor(out=Yn[:, ni2], in0=Ys[:, yi],
                                                                             in1=gst[:, gi, :, 5:6].broadcast_to([64, 8, 64]),
                                                                             op=ALU.mult), reads=[y_b, g_b], writes=[yn_b])
                bank = pr.next()
                tr8(bank, lambda h, ni2=ni2: Yn[:, ni2, h, :], [yn_b])
                P.op("act", lambda e, i=bank[0], cs=cs: e.activation(out=F["ynT"][:, :, cs], in_=pvb(i), func=AF.Copy),
                     reads=[bank[1]], writes=[FB["ynT"]])
            P.op("dve", lambda e: e.tensor_tensor(out=F["ynT"][:], in0=F["ynT"][:], in1=bc(clw), op=ALU.mult),
                 reads=[FB["ynT"], b_lw], writes=[FB["ynT"]])
            P.op("pool", lambda e: e.tensor_tensor(out=F["ynT"][:], in0=F["ynT"][:], in1=bc(clb), op=ALU.add),
                 reads=[FB["ynT"], b_lb], writes=[FB["ynT"]])
            P.op("dve", lambda e: e.tensor_tensor(out=F["ynT"][:], in0=F["ynT"][:], in1=F["bon"][:], op=ALU.add),
                 reads=[FB["ynT"], FB["bon"]], writes=[FB["ynT"]])
            P.op("dve", lambda e: e.tensor_tensor(out=ost[:], in0=F["ynT"][:], in1=F["g"][:], op=ALU.mult),
                 reads=[FB["ynT"], FB["g"]], writes=[ost_b])
            P.dma("sp", lambda e, c5=c5: e.dma_start(
                out=yr_dram.rearrange("(h v) s -> v h s", v=64)[:, :, c5 * TL:(c5 + 1) * TL], in_=ost[:]),
                reads=[ost_b])
    P.barrier()


NITER = 16


def phase_attn(P, nc, C, A, ya_dram):
    x_dram = A["x"]
    with ExitStack() as es:
        sb = lambda name, shape, dt=F32: es.enter_context(nc.sbuf_tensor(name, shape, dt))
        wa = sb("wa", [128, 8, ATTN_IN + 64], BF16)
        wa_b = Buf()
        load_weight_bf16(P, nc, wa, wa_b, A["w_in"], 0, ATTN_IN, 8)
        wv_ = A["w_in"].rearrange("(kc p) n -> p kc n", p=128)
        for kc in range(8):
            P.dma("pool", lambda e, kc=kc: e.dma_start(out=wa[:, kc, ATTN_IN:ATTN_IN + 64], in_=wv_[:, kc, 2048:2112]),
                  writes=[wa_b])
        kT = sb("kT", [128, 4, S], BF16)
        kiT = sb("kiT", [128, S], BF16)
        vaug = sb("vaug", [128, NT, 8, 65], BF16)
        kT_bs = [Buf() for _ in range(NT)]
        kiT_bs = [Buf() for _ in range(NT)]
        va_bs = [Buf() for _ in range(NT)]
        P.op("pool", lambda e: e.memset(vaug[:, :, :, 64:65], 1.0), writes=va_bs)
        gqk = sb("gqk", [128, 2], F32)
        gqk_b = Buf()
        for half in range(2):
            P.dma("sp", lambda e, half=half: e.dma_start(out=gqk[half * 64:(half + 1) * 64, 0:1],
                                                         in_=A["attn_q_norm"].rearrange("o d -> d o"),
                                                         allow_slow_non_contiguous=True), writes=[gqk_b])
            P.dma("sp", lambda e, half=half: e.dma_start(out=gqk[half * 64:(half + 1) * 64, 1:2],
                                                         in_=A["attn_k_norm"].rearrange("o d -> d o"),
                                                         allow_slow_non_contiguous=True), writes=[gqk_b])
        P.op("dve", lambda e: e.tensor_scalar(out=gqk[:, 0:1], in0=gqk[:, 0:1], scalar1=0.125, scalar2=None, op0=ALU.mult),
             reads=[gqk_b], writes=[gqk_b])
        btf = sb("btf", [128, 8, 2, 128], F32)
        bt = sb("bt", [128, 8, 2, 128], BF16)
        b31 = sb("b31_sb", [128, 8], F32)
        bt_b = Buf()
        P.dma("sp", lambda e: e.dma_start(out=btf[:], in_=A["bias_tiles"].rearrange("h c s t -> s h c t")), writes=[bt_b])
        P.dma("sp", lambda e: e.dma_start(out=b31[:], in_=A["b31"].partition_broadcast(128)), writes=[bt_b])
        P.op("dve", lambda e: e.tensor_tensor(out=bt[:].rearrange("p h c t -> p h (c t)"),
                                              in0=btf[:].rearrange("p h c t -> p h (c t)"),
                                              in1=b31[:, :].unsqueeze(2).broadcast_to([128, 8, 256]), op=ALU.subtract),
             reads=[bt_b], writes=[bt_b])
        cmask = sb("cmask", [128, 128], F32)
        onesblk = sb("onesblk", [128, 128], BF16)
        cm_b = Buf()
        pw2 = sb("pw2", [128, NITER], F32)
        halfs = sb("halfs", [128, NITER], F32)
        hf_b = Buf()

        def mkc(e):
            e.memset(cmask[:], 0.0)
            e.affine_select(out=cmask[:], in_=cmask[:], pattern=[[-1, 128]], compare_op=ALU.is_ge, fill=NEG, base=0,
                            channel_multiplier=1)
            for j in range(NITER):
                e.memset(pw2[:, j:j + 1], 0.5 ** (j + 1))
            e.memset(onesblk[:], 0.0)
            e.memset(onesblk[0:64, 0:64], 1.0)
            return e.memset(onesblk[64:128, 64:128], 1.0)

        P.op("pool", mkc, writes=[cm_b])
        nm = Normer(P, nc, es, C, A["mix_norm"], "an", nslots=1)
        xbuf = sb("axbuf", [128, 2, D], F32)
        xr = Ring([(i, Buf()) for i in range(2)])
        xn = sb("axn", [128, 8, 128], BF16)
        xn_b = Buf()
        q_i = sb("q_i", [128, 2, 4, 128], BF16)
        qi_i = sb("qi_i", [128, 4, 128], BF16)
        wi_i = sb("wi_i", [128, 8], F32)
        q_bs, qi_b, wi_b = [Buf(), Buf()], Buf(), Buf()
        sq = sb("asq", [128, 512], BF16)
        rn = sb("arn", [128, 512], F32)
        sq_b, rn_b = Buf(), Buf()
        sc = sb("sc", [128, S], F32)
        sc_b = Buf()
        rbuf = sb("rbuf", [128, 2, 512], F32)
        rr = Ring([(i, Buf()) for i in range(2)])
        junk = sb("ajunk", [128, S], BF16)
        junk_b = Buf()
        bs = sb("bs", [128, 8], F32)
        bs_b = Buf()
        maskT = sb("maskT", [128, 2, NT, 128], BF16)
        mT_bs = [Buf(), Buf()]
        Et = sb("Et", [128, 3, 4, 128], BF16)
        er = Ring([(i, Buf()) for i in range(3)])
        Pt = sb("Pt", [128, 4, 4, 128], BF16)
        ptr = Ring([(i, Buf()) for i in range(4)])
        rden = sb("rden", [128, 2], F32)
        rdr = Ring([(i, Buf()) for i in range(2)])
        ytile = sb("ytile", [128, 512], BF16)
        yt_b = Buf()
        yst = sb("yst", [128, 4, 128], BF16)
        ys_b = Buf()
        pg = es.enter_context(nc.psum_tensor("apg", [128, 5, 512], F32))
        gr = Ring([(i, Buf()) for i in range(3)])
        lgr = Ring([(i, Buf()) for i in range(3, 5)])
        po = es.enter_context(nc.psum_tensor("apo", [128, 2, 512], F32))
        orr = Ring([(i, Buf()) for i in range(2)])
        pgb = lambda i: pg[:, i, :].bitcast(BF16)
        xview = x_dram.rearrange("(t p) d -> t p d", p=128)
        yav = ya_dram.rearrange("(m p) s -> p m s", p=128)

        def front(i):
            ts_ = slice(i * 128, (i + 1) * 128)
            nkb = i + 1
            W = nkb * 128
            q_b = q_bs[i % 2]
            kT_b, kiT_b, va_b = kT_bs[i], kiT_bs[i], va_bs[i]
            xi, xb_ = xr.next()
            P.dma("sp", lambda e, i=i, xi=xi: e.dma_start(out=xbuf[:, xi, :], in_=xview[i]), writes=[xb_])
            nm.run(xbuf[:, xi, :], xb_, xn, xn_b, 0)
            yield

            def proj4(c0, bank):
                bi, bb = bank

                def th(e):
                    ins = None
                    for m in range(4):
                        for kc in range(8):
                            ins = e.matmul(pg[:, bi, m * 128:(m + 1) * 128], lhsT=wa[:, kc, c0 + m * 128:c0 + (m + 1) * 128],
                                           rhs=xn[:, kc, :], start=(kc == 0), stop=(kc == 7))
                    return ins

                P.op("pe", th, reads=[wa_b, xn_b], writes=[bb])

            for which, c0 in ((0, 0), (1, 512)):
                bank = gr.next()
                proj4(c0, bank)
                P.op("act", lambda e, bi=bank[0]: e.activation(out=sq[:], in_=pg[:, bi, :], func=AF.Square),
                     reads=[bank[1]], writes=[sq_b])
                bank2 = gr.next()
                P.op("pe", lambda e, bi=bank2[0]: e.matmul(pg[:, bi, :], lhsT=onesblk[:], rhs=sq[:], start=True, stop=True),
                     reads=[cm_b, sq_b], writes=[bank2[1]])
                P.op("act", lambda e, bi=bank2[0]: e.activation(out=rn[:], in_=pg[:, bi, :], func=AF.Sqrt, scale=1.0 / 64,
                                                                bias=C.eps[:, 0:1]), reads=[bank2[1], C.b], writes=[rn_b])
                P.op("dve", lambda e: e.reciprocal(out=rn[:], in_=rn[:]), reads=[rn_b], writes=[rn_b])
                if which == 0:
                    P.op("dve", lambda e, bi=bank[0]: e.scalar_tensor_tensor(
                        out=q_i[:, i % 2].rearrange("p m t -> p (m t)"), in0=pg[:, bi, :], scalar=gqk[:, 0:1], in1=rn[:],
                        op0=ALU.mult, op1=ALU.mult), reads=[bank[1], rn_b, gqk_b], writes=[q_b])
                else:
                    P.op("dve", lambda e, bi=bank[0], ts_=ts_: e.scalar_tensor_tensor(
                        out=kT[:, :, ts_], in0=pg[:, bi, :].rearrange("p (m t) -> p m t", m=4), scalar=gqk[:, 1:2],
                        in1=rn[:].rearrange("p (m t) -> p m t", m=4), op0=ALU.mult, op1=ALU.mult),
                        reads=[bank[1], rn_b, gqk_b], writes=[kT_b])
                yield
            bank = gr.next()
            proj4(1536, bank)
            P.op("act", lambda e, bi=bank[0]: e.activation(out=qi_i[:].rearrange("p m t -> p (m t)"), in_=pg[:, bi, :],
                                                           func=AF.Copy), reads=[bank[1]], writes=[qi_b])
            yield
            bank = gr.next()

            def kiw(e, bi=bank[0]):
                for kc in range(8):
                    e.matmul(pg[0:64, bi, 0:128], lhsT=wa[:, kc, 2048:2112], rhs=xn[:, kc, :], start=(kc == 0), stop=(kc == 7))
                for kc in range(8):
                    e.matmul(pg[64:128, bi, 0:128], lhsT=wa[:, kc, ATTN_IN:ATTN_IN + 64], rhs=xn[:, kc, :], start=(kc == 0),
                             stop=(kc == 7))
                ins = None
                for kc in range(8):
                    ins = e.matmul(pg[:, bi, 128:136], lhsT=xn[:, kc, :], rhs=wa[:, kc, 2112:2120], start=(kc == 0), stop=(kc == 7))
                return ins

            P.op("pe", kiw, reads=[wa_b, xn_b], writes=[bank[1]])
            P.op("act", lambda e, bi=bank[0], ts_=ts_: e.activation(out=kiT[:, ts_], in_=pg[:, bi, 0:128], func=AF.Copy),
                 reads=[bank[1]], writes=[kiT_b])
            P.op("dve", lambda e, bi=bank[0]: e.tensor_copy(out=wi_i[:], in_=pg[:, bi, 128:136]), reads=[bank[1]], writes=[wi_b])
            yield
            bank = gr.next()

            def vmm(e, bi=bank[0]):
                ins = None
                for kc in range(8):
                    ins = e.matmul(pg[:, bi, :], lhsT=xn[:, kc, :], rhs=wa[:, kc, 1024:1536], start=(kc == 0), stop=(kc == 7))
                return ins

            P.op("pe", vmm, reads=[wa_b, xn_b], writes=[bank[1]])
            P.op("act", lambda e, bi=bank[0], i=i: e.activation(out=vaug[:, i, :, 0:64],
                                                                in_=pg[:, bi, :].rearrange("p (h d) -> p h d", h=8), func=AF.Copy),
                 reads=[bank[1]], writes=[va_b])
            yield

            for gk in range((nkb + 3) // 4):
                w_ = min(512, W - gk * 512)
                for h in range(8):
                    hb = (h % 2) * 64
                    bank = gr.next()
                    P.op("pe", lambda e, bi=bank[0], h=h, hb=hb, gk=gk, w_=w_: e.matmul(
                        pg[:, bi, 0:w_], lhsT=qi_i[hb:hb + 64, h // 2, :], rhs=kiT[hb:hb + 64, gk * 512:gk * 512 + w_],
                        start=True, stop=True), reads=[qi_b] + kiT_bs[gk * 4:gk * 4 + (w_ // 128)], writes=[bank[1]])
                    ri, rb_ = rr.next()
                    P.op("act", lambda e, bi=bank[0], ri=ri, w_=w_: e.activation(out=rbuf[:, ri, 0:w_], in_=pg[:, bi, 0:w_],
                                                                                 func=AF.Relu), reads=[bank[1]], writes=[rb_])
                    if h == 0:
                        P.op("dve", lambda e, ri=ri, gk=gk, w_=w_: e.tensor_scalar(
                            out=sc[:, gk * 512:gk * 512 + w_], in0=rbuf[:, ri, 0:w_], scalar1=wi_i[:, 0:1], scalar2=None,
                            op0=ALU.mult), reads=[rb_, wi_b], writes=[sc_b])
                    else:
                        P.op("dve", lambda e, ri=ri, gk=gk, w_=w_, h=h: e.scalar_tensor_tensor(
                            out=sc[:, gk * 512:gk * 512 + w_], in0=rbuf[:, ri, 0:w_], scalar=wi_i[:, h:h + 1],
                            in1=sc[:, gk * 512:gk * 512 + w_], op0=ALU.mult, op1=ALU.add), reads=[rb_, wi_b, sc_b], writes=[sc_b])
                    yield
            P.op("dve", lambda e, ts_=ts_: e.tensor_tensor(out=sc[:, ts_], in0=sc[:, ts_], in1=cmask[:], op=ALU.add),
                 reads=[sc_b, cm_b], writes=[sc_b])
            if i < 2:
                P.op("dve", lambda e: e.memset(bs[:, 0:1], -1.0e29), writes=[bs_b])
            else:
                P.op("dve", lambda e, i=i: e.tensor_reduce(out=bs[:, 0:1], in_=sc[:, 0:i * 128], axis=AX.X, op=ALU.min),
                     reads=[sc_b], writes=[bs_b])
                P.op("dve", lambda e, W=W: e.tensor_reduce(out=bs[:, 6:7], in_=sc[:, 0:W], axis=AX.X, op=ALU.max),
                     reads=[sc_b, bs_b], writes=[bs_b])
                P.op("dve", lambda e: e.tensor_tensor(out=bs[:, 1:2], in0=bs[:, 6:7], in1=bs[:, 0:1], op=ALU.subtract),
                     reads=[bs_b], writes=[bs_b])
                P.op("dve", lambda e: e.tensor_tensor(out=halfs[:], in0=bs[:, 1:2].broadcast_to([128, NITER]), in1=pw2[:],
                                                      op=ALU.mult), reads=[bs_b, cm_b], writes=[hf_b])
                P.op("dve", lambda e: e.tensor_tensor(out=bs[:, 3:4], in0=bs[:, 0:1], in1=halfs[:, 0:1], op=ALU.add),
                     reads=[bs_b, hf_b], writes=[bs_b])
                for it in range(NITER):
                    P.op("dve", lambda e: e.tensor_scalar(out=junk[:, 0:W], in0=sc[:, 0:W], scalar1=bs[:, 3:4], scalar2=None,
                                                          op0=ALU.is_ge, op1=ALU.add, accum_out=bs[:, 4:5]),
                         reads=[sc_b, bs_b], writes=[junk_b, bs_b])
                    P.op("dve", lambda e, it=it: e.scalar_tensor_tensor(out=bs[:, 5:6], in0=bs[:, 4:5], scalar=TOPK - 0.5,
                                                                        in1=halfs[:, it:it + 1], op0=ALU.is_ge, op1=ALU.mult),
                         reads=[bs_b, hf_b], writes=[bs_b])
                    P.op("dve", lambda e: e.tensor_tensor(out=bs[:, 0:1], in0=bs[:, 0:1], in1=bs[:, 5:6], op=ALU.add),
                         reads=[bs_b], writes=[bs_b])
                    if it + 1 < NITER:
                        P.op("dve", lambda e, it=it: e.tensor_tensor(out=bs[:, 3:4], in0=bs[:, 0:1], in1=halfs[:, it + 1:it + 2],
                                                                     op=ALU.add), reads=[bs_b, hf_b], writes=[bs_b])
                    yield
            P.op("dve", lambda e, W=W: e.tensor_scalar(out=junk[:, 0:W], in0=sc[:, 0:W], scalar1=bs[:, 0:1], scalar2=None,
                                                       op0=ALU.is_ge), reads=[sc_b, bs_b], writes=[junk_b])
            for j0 in range(0, nkb, 8):
                nb = min(8, nkb - j0)
                bank = gr.next()

                def trm(e, bi=bank[0], j0=j0, nb=nb):
                    ins = None
                    for jj in range(nb):
                        ins = e.transpose(out=pgb(bi)[:, jj * 128:(jj + 1) * 128], in_=junk[:, (j0 + jj) * 128:(j0 + jj + 1) * 128],
                                          identity=C.ident_bf[:])
                    return ins

                P.op("pe", trm, reads=[junk_b, C.b], writes=[bank[1]])
                P.op("act", lambda e, bi=bank[0], j0=j0, nb=nb: e.activation(
                    out=maskT[:, i % 2, j0:j0 + nb, :].rearrange("p j t -> p (j t)"), in_=pgb(bi)[:, 0:nb * 128], func=AF.Copy),
                    reads=[bank[1]], writes=[mT_bs[i % 2]])
                yield
            yield

        def back(i):
            ts_ = slice(i * 128, (i + 1) * 128)
            nkb = i + 1
            q_b = q_bs[i % 2]
            mT_b = mT_bs[i % 2]
            items = [(h, j0, min(4, nkb - j0)) for h in range(8) for j0 in range(0, nkb, 4)]
            DEPTH = 2
            st = {}
            obs = {}
            for k in range(len(items) + DEPTH):
                if k < len(items):
                    h, j0, nb = items[k]
                    hb = (h % 2) * 64
                    m = h // 2
                    if j0 == 0:
                        obs[h] = orr.next()
                    bank = lgr.next()

                    def qk(e, bi=bank[0], j0=j0, nb=nb, h=h, hb=hb, m=m):
                        ins = None
                        for jj in range(nb):
                            j = j0 + jj
                            near = j >= i - 1
                            ins = e.matmul(pg[:, bi, jj * 128:(jj + 1) * 128], lhsT=kT[hb:hb + 64, m, j * 128:(j + 1) * 128],
                                           rhs=q_i[hb:hb + 64, i % 2, m, :], start=True, stop=not near)
                            if near:
                                ins = e.matmul(pg[:, bi, jj * 128:(jj + 1) * 128], lhsT=C.ident_bf[:],
                                               rhs=bt[:, h, 0 if j == i else 1, :], start=False, stop=True)
                        return ins

                    P.op("pe", qk, reads=kT_bs[j0:j0 + nb] + [q_b, bt_b, C.b], writes=[bank[1]])
                    ei, eb = er.next()
                    P.op("act", lambda e, bi=bank[0], ei=ei, nb=nb: e.activation(
                        out=Et[:, ei, 0:nb, :].rearrange("p j t -> p (j t)"), in_=pg[:, bi, 0:nb * 128], func=AF.Exp),
                        reads=[bank[1]], writes=[eb])
                    pi, pb_ = ptr.next()
                    P.op("pool", lambda e, ei=ei, pi=pi, j0=j0, nb=nb: e.tensor_tensor(
                        out=Pt[:, pi, 0:nb, :], in0=Et[:, ei, 0:nb, :], in1=maskT[:, i % 2, j0:j0 + nb, :], op=ALU.mult),
                        reads=[eb, mT_b], writes=[pb_])
                    st[k] = (pi, pb_)
                kk = k - DEPTH
                if kk >= 0:
                    h, j0, nb = items[kk]
                    pi, pb_ = st.pop(kk)
                    ob = obs[h]

                    def pv(e, oi=ob[0], pi=pi, j0=j0, nb=nb, h=h):
                        ins = None
                        for jj in range(nb):
                            j = j0 + jj
                            ins = e.matmul(po[:, oi, 0:65], lhsT=Pt[:, pi, jj, :], rhs=vaug[:, j, h, :], start=(j == 0),
                                           stop=(j == i))
                        return ins

                    P.op("pe", pv, reads=[pb_] + va_bs[j0:j0 + nb], writes=[ob[1]])
                    if j0 + nb == nkb:
                        di, db = rdr.next()
                        P.op("dve", lambda e, oi=ob[0], di=di: e.reciprocal(out=rden[:, di:di + 1], in_=po[:, oi, 64:65]),
                             reads=[ob[1]], writes=[db])
                        P.op("act", lambda e, oi=ob[0], di=di, h=h: e.activation(out=ytile[:, h * 64:(h + 1) * 64],
                                                                                 in_=po[:, oi, 0:64], func=AF.Copy,
                                                                                 scale=rden[:, di:di + 1]),
                             reads=[ob[1], db], writes=[yt_b])
                yield
            bank = lgr.next()

            def try_(e, bi=bank[0]):
                ins = None
                for m in range(4):
                    ins = e.transpose(out=pgb(bi)[:, m * 128:(m + 1) * 128], in_=ytile[:, m * 128:(m + 1) * 128],
                                      identity=C.ident_bf[:])
                return ins

            P.op("pe", try_, reads=[yt_b, C.b], writes=[bank[1]])
            P.op("act", lambda e, bi=bank[0]: e.activation(out=yst[:].rearrange("p m t -> p (m t)"), in_=pgb(bi)[:, 0:512],
                                                           func=AF.Copy), reads=[bank[1]], writes=[ys_b])
            P.dma("sp", lambda e: e.dma_start(out=yav[:, :, ts_], in_=yst[:]), reads=[ys_b])
            yield

        def run2(f, b):
            fa, ba = f is not None, b is not None
            while fa or ba:
                if fa:
                    try:
                        next(f)
                    except StopIteration:
                        fa = False
                if ba:
                    try:
                        next(b)
                    except StopIteration:
                        ba = False

        for i in range(NT + 1):
            run2(front(i) if i < NT else None, back(i - 1) if i >= 1 else None)
    P.barrier()


WEIGHT_SPECS = [
    ("mix_norm", [1, D]), ("w_in", [D, 5992]), ("attn_q_norm", [1, 64]), ("attn_k_norm", [1, 64]),
    ("bias_tiles", [8, 2, 128, 128]), ("b31", [1, 8]), ("rwkv_mu", [1, RWKV_IN]), ("rwkv_w0", [1, 512]), ("rwkv_w2", [64, 512]),
    ("rwkv_a0", [1, 512]), ("rwkv_a2", [64, 512]), ("rwkv_g2", [160, 512]), ("rwkv_k_k", [1, 512]),
    ("rwkv_k_a", [1, 512]), ("rwkv_r_k", [1, 512]), ("rwkv_ln_w", [1, 512]), ("rwkv_ln_b", [1, 512]),
    ("w_branch_attn", [512, D]), ("w_branch_rwkv", [512, D]), ("w_out", [D, D]), ("ffn_norm", [1, D]),
    ("w_gate_up", [D, 2 * FFN_H]), ("w_down", [FFN_H, D]),
]


def build_program(phases=("attn", "rwkv", "merge", "ffn"), debug=False):
    nc = bass.Bass("TRN2", target_bir_lowering=False)
    A = {}
    A["x"] = nc.dram_tensor("x", [S, D], F32, kind="ExternalInput").ap()
    for name, shp in WEIGHT_SPECS:
        A[name] = nc.dram_tensor(name, shp, F32, kind="ExternalInput").ap()
    out = nc.dram_tensor("out", [S, D], F32, kind="ExternalOutput").ap()
    def kind(prod, cons):
        if not debug:
            return "Internal"
        if prod in phases and cons not in phases:
            return "ExternalOutput"
        if prod not in phases and cons in phases:
            return "ExternalInput"
        return "Internal"

    ya = nc.dram_tensor("ya_scr", [512, S], BF16, kind=kind("attn", "merge")).ap()
    yr = nc.dram_tensor("yr_scr", [512, S], BF16, kind=kind("rwkv", "merge")).ap()
    hs = nc.dram_tensor("h_scr", [S, D], F32, kind=kind("merge", "ffn")).ap()
    P = Prog(nc)
    final_ops = []
    with ExitStack() as es:
        C = make_consts(P, nc, es)
        P.barrier()
        if "attn" in phases:
            phase_attn(P, nc, C, A, ya)
        if "rwkv" in phases:
            phase_rwkv(P, nc, C, A, yr)
        if "merge" in phases:
            phase_merge(P, nc, C, A["x"], ya, yr, hs, A["mix_norm"], A["w_in"], A["w_branch_attn"],
                        A["w_branch_rwkv"], A["w_out"])
        if "ffn" in phases:
            phase_ffn(P, nc, C, hs, out, A["ffn_norm"], A["w_gate_up"], A["w_down"], final_ops)
        P.emit(final_wait_ops=final_ops)
    return nc


def t5_bucket_np(d):
    d = np.maximum(d, 0)
    max_exact = 16
    log_ratio = np.log(np.maximum(d, 1).astype(np.float32) / max_exact) / math.log(128 / max_exact)
    large = np.minimum(max_exact + (log_ratio * 16).astype(np.int32), 31)
    return np.where(d < max_exact, d, large)


def host_layout(inputs):
    w = {}
    for name, shp in WEIGHT_SPECS:
        if name in ("bias_tiles", "b31"):
            continue
        w[name] = np.ascontiguousarray(np.asarray(inputs[name], dtype=np.float32).reshape(shp))
    s_idx = np.arange(128)[:, None]
    t_idx = np.arange(128)[None, :]
    rb = np.asarray(inputs["rel_bias"], dtype=np.float32)
    tiles = np.empty((8, 2, 128, 128), np.float32)
    for cls in range(2):
        bk = t5_bucket_np(t_idx - s_idx + 128 * cls)
        tiles[:, cls] = np.transpose(rb[bk], (2, 0, 1))
    w["bias_tiles"] = tiles
    w["b31"] = np.ascontiguousarray(rb[31:32, :])
    return w


_NC_CACHE = {}


def kernel(**inputs):
    x = np.asarray(inputs["x"], dtype=np.float32)
    w = host_layout(inputs)
    if "nc" not in _NC_CACHE:
        _NC_CACHE["nc"] = build_program()
    nc = _NC_CACHE["nc"]
    in_maps = []
    for b in range(8):
        m = dict(w)
        m["x"] = np.ascontiguousarray(x[b])
        in_maps.append(m)
    res = run_bass_kernel_spmd(nc, in_maps, core_ids=list(range(8)))
    return np.stack([np.asarray(r["out"], dtype=np.float32) for r in res.results], axis=0)
```

```python
import math
from contextlib import ExitStack

import numpy as np
import concourse.bass as bass
import concourse.mybir as mybir
from concourse.bass_utils import run_bass_kernel_spmd

F32 = mybir.dt.float32
BF16 = mybir.dt.bfloat16
AF = mybir.ActivationFunctionType
ALU = mybir.AluOpType
AX = mybir.AxisListType

S = 4096
D = 1024
NT = S // 128
ATTN_IN = 2120
RWKV_IN = 1824
FFN_H = 2816
RMS_EPS = 1e-6
GN_EPS = 64e-5
TOPK = 256
NEG = -1.0e30

ENGS = ("pe", "act", "dve", "pool", "sp")
SBUF_DEBUG = False


class Buf:
    __slots__ = ("name", "w", "r")

    def __init__(self, name=""):
        self.name = name
        self.w = None
        self.r = []


class Op:
    __slots__ = ("eng", "thunk", "deps", "is_dma", "sem", "val", "need_inc", "pos", "prev_on_sem")


class Prog:
    def __init__(self, nc, n_dma_sems=90):
        self.nc = nc
        self.streams = {e: [] for e in ENGS}
        self.n_dma_sems = n_dma_sems
        self.dma_count = 0
        self.all_ops = []
        self.open_dmas = []

    def _hazards(self, op, reads, writes):
        deps = []
        for b in reads:
            if b.w is not None:
                deps.append(b.w)
        for b in writes:
            if b.w is not None:
                deps.append(b.w)
            deps.extend(b.r)
        for b in reads:
            b.r.append(op)
        for b in writes:
            b.w = op
            b.r = []
        return deps

    def op(self, eng, thunk, reads=(), writes=(), extra_deps=()):
        o = Op()
        o.eng = eng
        o.thunk = thunk
        o.is_dma = False
        o.need_inc = False
        o.sem = None
        o.val = None
        o.prev_on_sem = None
        deps = self._hazards(o, reads, writes) + list(extra_deps)
        seen = set()
        o.deps = []
        for d in deps:
            if d is o or id(d) in seen:
                continue
            if eng == "pe" and d.eng == "pe" and not d.is_dma:
                continue
            seen.add(id(d))
            o.deps.append(d)
        self.streams[eng].append(o)
        self.all_ops.append(o)
        return o

    def dma(self, eng, thunk, reads=(), writes=(), extra_deps=()):
        o = self.op(eng, thunk, reads, writes, extra_deps)
        o.is_dma = True
        o.pos = self.dma_count
        self.dma_count += 1
        self.open_dmas.append(o)
        return o

    def barrier(self):
        lasts = []
        for e in ENGS:
            for o in reversed(self.streams[e]):
                if not o.is_dma:
                    lasts.append(o)
                    break
        deps = lasts + self.open_dmas
        self.open_dmas = []
        for e in ENGS:
            self.op(e, lambda eng: eng.nop(), extra_deps=deps)

    def emit(self, final_wait_ops=()):
        nc = self.nc
        for o in self.all_ops:
            for d in o.deps:
                d.need_inc = True
        eng_sems = {e: nc.alloc_semaphore("s_" + e) for e in ENGS}
        dma_sems = [nc.alloc_semaphore("s_dma%d" % i) for i in range(self.n_dma_sems)]
        dma_sem_val = [0] * self.n_dma_sems
        dma_prev = [None] * self.n_dma_sems
        cnt = {e: 0 for e in ENGS}
        for o in self.all_ops:
            if o.is_dma:
                k = o.pos % self.n_dma_sems
                dma_sem_val[k] += 16
                o.sem = ("dma", k)
                o.val = dma_sem_val[k]
                o.prev_on_sem = dma_prev[k]
                dma_prev[k] = o
            elif o.need_inc:
                cnt[o.eng] += 1
                o.sem = ("eng", o.eng)
                o.val = cnt[o.eng]

        def semh(key):
            return eng_sems[key[1]] if key[0] == "eng" else dma_sems[key[1]]

        engines = {"pe": "tensor", "act": "scalar", "dve": "vector", "pool": "gpsimd", "sp": "sync"}
        with nc.Block() as block:
            for e in ENGS:
                stream = self.streams[e]
                final = list(final_wait_ops) if e == "sp" else []

                def body(engine, stream=stream, final=final):
                    known = {}
                    for o in stream:
                        waits = {}
                        deps = list(o.deps)
                        if o.is_dma and o.prev_on_sem is not None:
                            deps.append(o.prev_on_sem)
                        for d in deps:
                            if known.get(d.sem, 0) >= d.val:
                                continue
                            if waits.get(d.sem, 0) < d.val:
                                waits[d.sem] = d.val
                        for key, val in waits.items():
                            engine.wait_ge(semh(key), val)
                            known[key] = val
                        ins = o.thunk(engine)
                        if o.is_dma:
                            ins.then_inc(semh(o.sem), 16)
                        elif o.need_inc:
                            ins.then_inc(semh(o.sem), 1)
                    for o in final:
                        engine.wait_ge(semh(o.sem), o.val)

                getattr(block, engines[e])(body)


class Ring:
    def __init__(self, items):
        self.items = items
        self.i = 0

    def next(self):
        it = self.items[self.i % len(self.items)]
        self.i += 1
        return it


def load_weight_bf16(P, nc, dst, dst_buf, w_ap, c0, c1, kchunks, eng="pool"):
    wv = w_ap.rearrange("(kc p) n -> p kc n", p=128)
    for kc in range(kchunks):
        for a in range(c0, c1, 2048):
            b = min(c1, a + 2048)
            P.dma(eng, lambda e, kc=kc, a=a, b=b: e.dma_start(out=dst[:, kc, a - c0:b - c0], in_=wv[:, kc, a:b]),
                  writes=[dst_buf])


def load_col_vec(P, nc, dst, dst_buf, v_ap, n):
    src = v_ap.rearrange("o (c p) -> p (o c)", p=128)
    P.dma("sp", lambda e: e.dma_start(out=dst, in_=src, allow_slow_non_contiguous=True), writes=[dst_buf])


class Consts:
    pass


def make_consts(P, nc, es):
    C = Consts()
    C.ident_bf = es.enter_context(nc.sbuf_tensor("ident_bf", [128, 128], BF16))
    C.ident_f = es.enter_context(nc.sbuf_tensor("ident_f", [128, 128], F32))
    C.eps = es.enter_context(nc.sbuf_tensor("eps_c", [128, 2], F32))
    C.b = Buf("consts")

    def mk(e):
        e.memset(C.ident_f[:], 0.0)
        e.affine_select(out=C.ident_f[:], in_=C.ident_f[:], pattern=[[-1, 128]], compare_op=ALU.not_equal,
                        fill=1.0, base=0, channel_multiplier=1)
        e.memset(C.eps[:, 0:1], RMS_EPS)
        return e.memset(C.eps[:, 1:2], GN_EPS)

    P.op("pool", mk, writes=[C.b])
    P.op("pool", lambda e: e.tensor_copy(out=C.ident_bf[:], in_=C.ident_f[:]), reads=[C.b], writes=[C.b])
    return C


class Normer:
    def __init__(self, P, nc, es, C, gain_ap, name, nslots=2):
        self.P, self.nc, self.C = P, nc, C
        self.gcol = es.enter_context(nc.sbuf_tensor(name + "_g", [128, 8], F32))
        self.gb = Buf(name + "_g")
        load_col_vec(P, nc, self.gcol[:, :], self.gb, gain_ap, 8)
        self.stat = es.enter_context(nc.sbuf_tensor(name + "_st", [128, nslots, 4], F32))
        self.junk = es.enter_context(nc.sbuf_tensor(name + "_junk", [128, 1024], BF16))
        self.xs = es.enter_context(nc.sbuf_tensor(name + "_xs", [128, nslots, 1024], BF16))
        self.tp = es.enter_context(nc.psum_tensor(name + "_tp", [128, nslots, 8, 128], BF16))
        self.ring = Ring([(i, Buf(), Buf(), Buf()) for i in range(nslots)])
        self.junkb = Buf()

    def run(self, xt_ap, xt_buf, dst, dst_buf, col0):
        P, C = self.P, self.C
        i, sb, xb, pb = self.ring.next()
        st = self.stat
        P.op("act", lambda e: e.activation(out=self.junk[:], in_=xt_ap, func=AF.Square, accum_out=st[:, i, 0:1]),
             reads=[xt_buf], writes=[self.junkb, sb])
        P.op("act", lambda e: e.activation(out=st[:, i, 1:2], in_=st[:, i, 0:1], func=AF.Sqrt, scale=1.0 / D,
                                           bias=C.eps[:, 0:1]), reads=[sb, C.b], writes=[sb])
        P.op("dve", lambda e: e.reciprocal(out=st[:, i, 2:3], in_=st[:, i, 1:2]), reads=[sb], writes=[sb])
        P.op("act", lambda e: e.activation(out=self.xs[:, i, :], in_=xt_ap, func=AF.Copy, scale=st[:, i, 2:3]),
             reads=[xt_buf, sb], writes=[xb])

        def tr(e):
            ins = None
            for kc in range(8):
                ins = e.transpose(out=self.tp[:, i, kc, :], in_=self.xs[:, i, kc * 128:(kc + 1) * 128],
                                  identity=C.ident_bf[:])
            return ins

        P.op("pe", tr, reads=[xb, C.b], writes=[pb])
        P.op("dve", lambda e: e.tensor_tensor(out=dst[:, :, col0:col0 + 128], in0=self.tp[:, i, :, :],
                                              in1=self.gcol[:, :].unsqueeze(2).broadcast_to([128, 8, 128]),
                                              op=ALU.mult), reads=[pb, self.gb], writes=[dst_buf])


def phase_ffn(P, nc, C, h_dram, out_dram, ffn_norm, w_gate_up, w_down, final_ops):
    with ExitStack() as es:
        wgu = es.enter_context(nc.sbuf_tensor("wgu", [128, 8, 2 * FFN_H], BF16))
        wd = es.enter_context(nc.sbuf_tensor("wd", [128, 22, D], BF16))
        wgu_b, wd_b = Buf("wgu"), Buf("wd")
        load_weight_bf16(P, nc, wgu, wgu_b, w_gate_up, 0, 2 * FFN_H, 8)
        load_weight_bf16(P, nc, wd, wd_b, w_down, 0, D, 22)
        nm = Normer(P, nc, es, C, ffn_norm, "fn")
        hbuf = es.enter_context(nc.sbuf_tensor("hbuf", [128, 2, D], F32))
        hr = Ring([(i, Buf()) for i in range(2)])
        hnT = es.enter_context(nc.sbuf_tensor("hnT", [128, 2, 8, 512], BF16))
        hnb = [Buf(), Buf()]
        actT = es.enter_context(nc.sbuf_tensor("actT", [128, 22, 512], BF16))
        actb = [Buf() for _ in range(22)]
        sg = es.enter_context(nc.sbuf_tensor("sg", [128, 2, 512], F32))
        sgr = Ring([(i, Buf()) for i in range(2)])
        ost = es.enter_context(nc.sbuf_tensor("ost", [128, 2, D], F32))
        ostr = Ring([(i, Buf()) for i in range(2)])
        pg = es.enter_context(nc.psum_tensor("pg", [128, 2, 512], F32))
        pu = es.enter_context(nc.psum_tensor("pu", [128, 2, 512], F32))
        po = es.enter_context(nc.psum_tensor("po", [128, 2, 512], F32))
        pgr = Ring([(i, Buf()) for i in range(2)])
        pur = Ring([(i, Buf()) for i in range(2)])
        por = Ring([(i, Buf()) for i in range(2)])
        hview = h_dram.rearrange("(t p) d -> t p d", p=128)
        oview = out_dram.rearrange("(t p) d -> t p d", p=128)
        for c in range(S // 512):
            cb = c % 2
            for j in range(4):
                t = c * 4 + j
                hi, hb_ = hr.next()
                P.dma("sp", lambda e, t=t, hi=hi: e.dma_start(out=hbuf[:, hi, :], in_=hview[t]), writes=[hb_])
                nm.run(hbuf[:, hi, :], hb_, hnT[:, cb], hnb[cb], j * 128)
            for m in range(22):
                gi, gbuf = pgr.next()
                ui, ubuf = pur.next()

                def mm(e, m=m, gi=gi, ui=ui, cb=cb):
                    for kc in range(8):
                        e.matmul(pg[:, gi, :], lhsT=wgu[:, kc, m * 128:(m + 1) * 128], rhs=hnT[:, cb, kc, :],
                                 start=(kc == 0), stop=(kc == 7))
                    ins = None
                    for kc in range(8):
                        ins = e.matmul(pu[:, ui, :], lhsT=wgu[:, kc, FFN_H + m * 128:FFN_H + (m + 1) * 128],
                                       rhs=hnT[:, cb, kc, :], start=(kc == 0), stop=(kc == 7))
                    return ins

                P.op("pe", mm, reads=[wgu_b, hnb[cb]], writes=[gbuf, ubuf])
                si, sbuf_ = sgr.next()
                P.op("act", lambda e, gi=gi, si=si: e.activation(out=sg[:, si, :], in_=pg[:, gi, :], func=AF.Silu),
                     reads=[gbuf], writes=[sbuf_])
                P.op("dve", lambda e, ui=ui, si=si, m=m: e.tensor_tensor(out=actT[:, m, :], in0=pu[:, ui, :],
                                                                         in1=sg[:, si, :], op=ALU.mult),
                     reads=[ubuf, sbuf_], writes=[actb[m]])
            for j in range(4):
                t = c * 4 + j
                oi, obuf = ostr.next()
                P.dma("sp", lambda e, t=t, oi=oi: e.dma_start(out=ost[:, oi, :], in_=hview[t]), writes=[obuf])
                for nh in range(2):
                    pi, pbuf = por.next()

                    def mmd(e, j=j, nh=nh, pi=pi):
                        ins = None
                        for m in range(22):
                            ins = e.matmul(po[:, pi, :], lhsT=actT[:, m, j * 128:(j + 1) * 128],
                                           rhs=wd[:, m, nh * 512:(nh + 1) * 512], start=(m == 0), stop=(m == 21))
                        return ins

                    P.op("pe", mmd, reads=actb + [wd_b], writes=[pbuf])
                    P.op("dve", lambda e, nh=nh, pi=pi, oi=oi: e.tensor_tensor(
                        out=ost[:, oi, nh * 512:(nh + 1) * 512], in0=po[:, pi, :],
                        in1=ost[:, oi, nh * 512:(nh + 1) * 512], op=ALU.add),
                        reads=[pbuf], writes=[obuf])
                final_ops.append(P.dma("sp", lambda e, t=t, oi=oi: e.dma_start(out=oview[t], in_=ost[:, oi, :]),
                                       reads=[obuf]))
    P.barrier()


def phase_merge(P, nc, C, x_dram, ya_dram, yr_dram, h_dram, mix_norm, w_in, w_ba, w_br, w_out):
    with ExitStack() as es:
        wg = es.enter_context(nc.sbuf_tensor("wg", [128, 8, 2 * D], BF16))
        wba = es.enter_context(nc.sbuf_tensor("wba", [128, 4, D], BF16))
        wbr = es.enter_context(nc.sbuf_tensor("wbr", [128, 4, D], BF16))
        wo = es.enter_context(nc.sbuf_tensor("wo", [128, 8, D], BF16))
        wg_b, wba_b, wbr_b, wo_b = Buf(), Buf(), Buf(), Buf()
        load_weight_bf16(P, nc, wg, wg_b, w_in, ATTN_IN + RWKV_IN, ATTN_IN + RWKV_IN + 2 * D, 8)
        load_weight_bf16(P, nc, wba, wba_b, w_ba, 0, D, 4)
        load_weight_bf16(P, nc, wbr, wbr_b, w_br, 0, D, 4)
        load_weight_bf16(P, nc, wo, wo_b, w_out, 0, D, 8)
        nm = Normer(P, nc, es, C, mix_norm, "mn")
        xbuf = es.enter_context(nc.sbuf_tensor("xbuf", [128, 2, 4, D], F32))
        xb = [[Buf() for _ in range(4)] for _ in range(2)]
        xnT = es.enter_context(nc.sbuf_tensor("xnT", [128, 2, 8, 512], BF16))
        xnb = [Buf(), Buf()]
        yaT = es.enter_context(nc.sbuf_tensor("yaT", [128, 2, 4, 512], BF16))
        yrT = es.enter_context(nc.sbuf_tensor("yrT", [128, 2, 4, 512], BF16))
        yab, yrb = [Buf(), Buf()], [Buf(), Buf()]
        mT = es.enter_context(nc.sbuf_tensor("mT", [128, 8, 512], BF16))
        mb = [Buf() for _ in range(8)]
        sg = es.enter_context(nc.sbuf_tensor("sgm", [128, 2, 2, 512], F32))
        sgr = Ring([(i, Buf()) for i in range(2)])
        tt = es.enter_context(nc.sbuf_tensor("ttm", [128, 2, 2, 512], F32))
        ttr = Ring([(i, Buf()) for i in range(2)])
        hst = es.enter_context(nc.sbuf_tensor("hst", [128, 2, D], F32))
        hstr = Ring([(i, Buf()) for i in range(2)])
        pga = es.enter_context(nc.psum_tensor("pga", [128, 2, 512], F32))
        pbr = es.enter_context(nc.psum_tensor("pbr", [128, 2, 512], F32))
        po = es.enter_context(nc.psum_tensor("pom", [128, 2, 512], F32))
        pgb, pbb = Buf(), Buf()
        por = Ring([(i, Buf()) for i in range(2)])
        xview = x_dram.rearrange("(t p) d -> t p d", p=128)
        hview = h_dram.rearrange("(t p) d -> t p d", p=128)
        yav = ya_dram.rearrange("(kc p) s -> p kc s", p=128)
        yrv = yr_dram.rearrange("(kc p) s -> p kc s", p=128)
        for c in range(S // 512):
            cb = c % 2
            P.dma("sp", lambda e, c=c, cb=cb: e.dma_start(out=yaT[:, cb], in_=yav[:, :, c * 512:(c + 1) * 512]),
                  writes=[yab[cb]])
            P.dma("sp", lambda e, c=c, cb=cb: e.dma_start(out=yrT[:, cb], in_=yrv[:, :, c * 512:(c + 1) * 512]),
                  writes=[yrb[cb]])
            for j in range(4):
                t = c * 4 + j
                P.dma("sp", lambda e, t=t, cb=cb, j=j: e.dma_start(out=xbuf[:, cb, j, :], in_=xview[t]),
                      writes=[xb[cb][j]])
                nm.run(xbuf[:, cb, j, :], xb[cb][j], xnT[:, cb], xnb[cb], j * 128)
            for m in range(8):
                def mmg(e, m=m, cb=cb):
                    ins = None
                    for g in range(2):
                        for kc in range(8):
                            ins = e.matmul(pga[:, g, :], lhsT=wg[:, kc, g * D + m * 128:g * D + (m + 1) * 128],
                                           rhs=xnT[:, cb, kc, :], start=(kc == 0), stop=(kc == 7))
                    return ins

                P.op("pe", mmg, reads=[wg_b, xnb[cb]], writes=[pgb])

                def mmb(e, m=m, cb=cb):
                    ins = None
                    for kc in range(4):
                        ins = e.matmul(pbr[:, 0, :], lhsT=wba[:, kc, m * 128:(m + 1) * 128], rhs=yaT[:, cb, kc, :],
                                       start=(kc == 0), stop=(kc == 3))
                    for kc in range(4):
                        ins = e.matmul(pbr[:, 1, :], lhsT=wbr[:, kc, m * 128:(m + 1) * 128], rhs=yrT[:, cb, kc, :],
                                       start=(kc == 0), stop=(kc == 3))
                    return ins

                P.op("pe", mmb, reads=[wba_b, wbr_b, yab[cb], yrb[cb]], writes=[pbb])
                si, sbuf_ = sgr.next()
                P.op("act", lambda e, si=si: e.activation(out=sg[:, si], in_=pga[:, :, :], func=AF.Sigmoid),
                     reads=[pgb], writes=[sbuf_])
                ti, tbuf = ttr.next()
                P.op("dve", lambda e, si=si, ti=ti: e.tensor_tensor(out=tt[:, ti], in0=pbr[:, :, :], in1=sg[:, si],
                                                                    op=ALU.mult),
                     reads=[pbb, sbuf_], writes=[tbuf])
                P.op("pool", lambda e, ti=ti, m=m: e.tensor_tensor(out=mT[:, m, :], in0=tt[:, ti, 0, :],
                                                                   in1=tt[:, ti, 1, :], op=ALU.add),
                     reads=[tbuf], writes=[mb[m]])
            for j in range(4):
                t = c * 4 + j
                hi, hbuf_ = hstr.next()
                for nh in range(2):
                    pi, pbuf = por.next()

                    def mmo(e, j=j, nh=nh, pi=pi):
                        ins = None
                        for m in range(8):
                            ins = e.matmul(po[:, pi, :], lhsT=mT[:, m, j * 128:(j + 1) * 128],
                                           rhs=wo[:, m, nh * 512:(nh + 1) * 512], start=(m == 0), stop=(m == 7))
                        return ins

                    P.op("pe", mmo, reads=mb + [wo_b], writes=[pbuf])
                    P.op("dve", lambda e, j=j, nh=nh, pi=pi, hi=hi, cb=cb: e.tensor_tensor(
                        out=hst[:, hi, nh * 512:(nh + 1) * 512], in0=po[:, pi, :],
                        in1=xbuf[:, cb, j, nh * 512:(nh + 1) * 512], op=ALU.add),
                        reads=[pbuf, xb[cb][j]], writes=[hbuf_])
                P.dma("sp", lambda e, t=t, hi=hi: e.dma_start(out=hview[t], in_=hst[:, hi, :]), reads=[hbuf_])
    P.barrier()


TL = 128
C0 = math.exp(-0.5)


def col8(P, nc, es, name, v_ap):
    t = es.enter_context(nc.sbuf_tensor(name, [64, 8], F32))
    b = Buf(name)
    P.dma("sp", lambda e: e.dma_start(out=t[:, :], in_=v_ap.rearrange("o (h k) -> k (o h)", k=64),
                                      allow_slow_non_contiguous=True), writes=[b])
    return t, b


def phase_rwkv(P, nc, C, A, yr_dram):
    x_dram = A["x"]
    with ExitStack() as es:
        sb = lambda name, shape, dt=F32: es.enter_context(nc.sbuf_tensor(name, shape, dt))
        wr = sb("wr", [128, 8, RWKV_IN], BF16)
        wmu = sb("wmu", [128, 8, RWKV_IN], BF16)
        wr_b, wmu_b, mub_b = Buf(), Buf(), Buf()
        load_weight_bf16(P, nc, wr, wr_b, A["w_in"], ATTN_IN, ATTN_IN + RWKV_IN, 8)
        with nc.sbuf_tensor("mub", [128, RWKV_IN], F32) as mub:
            P.dma("sp", lambda e: e.dma_start(out=mub[:], in_=A["rwkv_mu"].partition_broadcast(128)), writes=[mub_b])
            for kc in range(8):
                P.op("pool", lambda e, kc=kc: e.tensor_tensor(out=wmu[:, kc, :], in0=wr[:, kc, :], in1=mub[:],
                                                              op=ALU.mult), reads=[wr_b, mub_b], writes=[wmu_b])
        P.barrier()
        w2 = sb("w2", [64, 512], BF16)
        a2 = sb("a2", [64, 512], BF16)
        g2 = sb("g2", [64, 3, 512], BF16)
        lw_b = Buf()
        P.dma("pool", lambda e: e.dma_start(out=w2[:], in_=A["rwkv_w2"]), writes=[lw_b])
        P.dma("pool", lambda e: e.dma_start(out=a2[:], in_=A["rwkv_a2"]), writes=[lw_b])
        P.dma("pool", lambda e: e.dma_start(out=g2[:, 0, :], in_=A["rwkv_g2"][0:64, :]), writes=[lw_b])
        P.dma("pool", lambda e: e.dma_start(out=g2[:, 1, :], in_=A["rwkv_g2"][64:128, :]), writes=[lw_b])
        P.dma("pool", lambda e: e.dma_start(out=g2[0:32, 2, :], in_=A["rwkv_g2"][128:160, :]), writes=[lw_b])
        cw0, b_w0 = col8(P, nc, es, "cw0", A["rwkv_w0"])
        ca0, b_a0 = col8(P, nc, es, "ca0", A["rwkv_a0"])
        ckk, b_kk = col8(P, nc, es, "ckk", A["rwkv_k_k"])
        cka, b_ka = col8(P, nc, es, "cka", A["rwkv_k_a"])
        crk, b_rk = col8(P, nc, es, "crk", A["rwkv_r_k"])
        clw, b_lw = col8(P, nc, es, "clw", A["rwkv_ln_w"])
        clb, b_lb = col8(P, nc, es, "clb", A["rwkv_ln_b"])
        msk = sb("rmsk", [64, 3, 64], F32)
        ones_bf = sb("ones_bf", [64, 64], BF16)
        rmask = sb("rmask", [64, 8, TL // 64, 64], F32)
        mk_b = Buf()

        def mkmasks(e):
            e.memset(msk[:], 1.0)
            e.memset(ones_bf[:], 1.0)
            e.memset(rmask[:], 1.0)
            e.memset(rmask[:, :, :, 0:1], 0.0)
            e.affine_select(out=msk[:, 0, :], in_=msk[:, 0, :], pattern=[[1, 64]], compare_op=ALU.is_ge,
                            fill=0.0, base=-1, channel_multiplier=-1)
            e.affine_select(out=msk[:, 1, :], in_=msk[:, 1, :], pattern=[[1, 64]], compare_op=ALU.is_ge,
                            fill=0.0, base=0, channel_multiplier=-1)
            return e.affine_select(out=msk[:, 2, :], in_=msk[:, 2, :], pattern=[[-1, 64]], compare_op=ALU.is_ge,
                                   fill=0.0, base=-1, channel_multiplier=1)

        P.op("pool", mkmasks, writes=[mk_b])
        mb3 = lambda i: msk[:, i, :].unsqueeze(1).broadcast_to([64, 8, 64])
        idb = C.ident_bf[0:64, 0:64]
        idf3 = C.ident_f[0:64, 0:64].unsqueeze(1).broadcast_to([64, 8, 64])
        bc = lambda col: col[:, :].unsqueeze(2).broadcast_to([64, 8, TL])

        nm = Normer(P, nc, es, C, A["mix_norm"], "rn", nslots=1)
        xbuf = sb("rxbuf", [128, 1, D], F32)
        xr = Ring([(i, Buf()) for i in range(1)])
        xnx = sb("xnx", [128, 2, 8, TL + 1], BF16)
        xnb = [Buf(), Buf()]
        dxn = sb("dxn", [128, 8, TL], BF16)
        dxb = Buf()
        P.op("pool", lambda e: e.memset(xnx[:, 1, :, TL:TL + 1], 0.0), writes=[xnb[1]])
        F = {}
        FB = {}
        for nme, dt in [("r", F32), ("k", F32), ("sgd", F32), ("a", F32), ("kk", F32),
                        ("t1", F32), ("t2", F32), ("cum", F32), ("e1", F32), ("e2", F32), ("sqb", BF16)]:
            alias = {"e1": "t1", "e2": "a"}
            if nme in alias:
                F[nme] = F[alias[nme]]
                FB[nme] = FB[alias[nme]]
                continue
            F[nme] = sb("f_" + nme, [64, 8, TL], dt)
            FB[nme] = Buf(nme)
        X2 = {}
        X2B = {}
        for nme in ("rT", "aT", "bT", "kT", "bH", "kH", "vb", "g", "bon"):
            X2[nme] = sb("x_" + nme, [64, 2, 8, TL], BF16)
            X2B[nme] = [Buf(nme + "0"), Buf(nme + "1")]
        lora = sb("lora", [64, 5, TL], BF16)
        ztok = sb("ztok", [128, 2, 512], F32)
        ztr = Ring([(i, Buf()) for i in range(2)])
        lora_b = Buf()
        gC = sb("gC", [64, 2, 8, TL // 64], F32)
        gC_bs = [Buf(), Buf()]
        Sf = sb("Sf", [64, 8, 64], F32)
        Sb = sb("Sb", [64, 8, 64], BF16)
        St = sb("St", [64, 8, 64], F32)
        S_b, St_b = Buf(), Buf()
        P.op("dve", lambda e: e.memset(Sf[:], 0.0), writes=[S_b])
        P.op("dve", lambda e: e.memset(Sb[:], 0.0), reads=[S_b], writes=[S_b])
        ost = sb("rost", [64, 8, TL], BF16)
        ost_b = Buf()
        def pair(name, dt=BF16, n=2):
            t = sb(name, [64, n, 8, 64], dt)
            return t, [Buf() for _ in range(n)]
        Atok, Atok_b = pair("Atok")
        BHtok, BHtok_b = pair("BHtok")
        KHtok, KHtok_b = pair("KHtok")
        Vtok, Vtok_b = pair("Vtok")
        Mrb, Mrb_b = pair("Mrb")
        Mrk, Mrk_b = pair("Mrk")
        Lak, Lak_b = pair("Lak")
        Nn, Nn_b = pair("Nn", BF16, 4)
        Mm, Mm_b = pair("Mm", BF16, 4)
        Qq, Qq_b = pair("Qq", BF16, 4)
        WT, WT_b = pair("WT")
        Xx, Xx_b = pair("Xx")
        Uu, Uu_b = pair("Uu")
        Ys, Ys_b = pair("Ys", F32, 1)
        Yn, Yn_b = pair("Yn", BF16, 2)
        gst = sb("gst", [64, 2, 8, 6], F32)
        gst_b = [Buf(), Buf()]
        pb = es.enter_context(nc.psum_tensor("rpb", [128, 7, 512], F32))
        pr = Ring([(i, Buf()) for i in range(7)])
        pv = lambda i: pb[0:64, i, :].rearrange("p (h t) -> p h t", h=8)
        pvb = lambda i: pb[0:64, i, :].bitcast(BF16)[:, 0:512].rearrange("p (h t) -> p h t", h=8)

        xview = x_dram.rearrange("(t p) d -> t p d", p=128)

        def mm8(bank, parts, rd):
            i, bbuf = bank
            ops = [[(lf(h), rf(h)) for (lf, rf) in parts] for h in range(8)]

            def th(e):
                ins = None
                for h in range(8):
                    for pi, (l, r) in enumerate(ops[h]):
                        ins = e.matmul(pb[0:64, i, h * 64:(h + 1) * 64], lhsT=l, rhs=r,
                                       start=(pi == 0), stop=(pi == len(ops[h]) - 1))
                return ins

            P.op("pe", th, reads=rd, writes=[bbuf])

        def tr8(bank, src_fn, rd):
            i, bbuf = bank
            srcs = [src_fn(h) for h in range(8)]

            def th(e):
                ins = None
                v = pvb(i)
                for h in range(8):
                    ins = e.transpose(out=v[:, h, :], in_=srcs[h], identity=idb)
                return ins

            P.op("pe", th, reads=rd + [C.b], writes=[bbuf])

        NB = S // TL

        def prep(n):
            par = n % 2
            cb = n % 2
            pc = 1 - cb
            X = {k: X2[k][:, par] for k in X2}
            XB = {k: X2B[k][par] for k in X2}
            P.op("pool", lambda e: e.tensor_copy(out=xnx[:, cb, :, 0:1], in_=xnx[:, pc, :, TL:TL + 1]),
                 reads=[xnb[pc]], writes=[xnb[cb]])
            for j in range(TL // 128):
                xi, xb_ = xr.next()
                P.dma("sp", lambda e, t=n * (TL // 128) + j, xi=xi: e.dma_start(out=xbuf[:, xi, :], in_=xview[t]), writes=[xb_])
                nm.run(xbuf[:, xi, :], xb_, xnx[:, cb], xnb[cb], 1 + j * 128)
            P.op("pool", lambda e: e.tensor_tensor(out=dxn[:], in0=xnx[:, cb, :, 0:TL], in1=xnx[:, cb, :, 1:TL + 1],
                                                   op=ALU.subtract), reads=[xnb[cb]], writes=[dxb])
            yield

            def projtok(c0, ncol):
                bank = pr.next()
                i = bank[0]

                def th(e):
                    ins = None
                    for kc in range(8):
                        e.matmul(pb[:, i, 0:ncol], lhsT=xnx[:, cb, kc, 1:TL + 1], rhs=wr[:, kc, c0:c0 + ncol],
                                 start=(kc == 0), stop=False)
                    for kc in range(8):
                        ins = e.matmul(pb[:, i, 0:ncol], lhsT=dxn[:, kc, :], rhs=wmu[:, kc, c0:c0 + ncol],
                                       start=False, stop=(kc == 7))
                    return ins

                P.op("pe", th, reads=[wr_b, wmu_b, xnb[cb], dxb], writes=[bank[1]])
                zi, zb = ztr.next()
                P.op("act", lambda e: e.activation(out=ztok[:, zi, 0:ncol], in_=pb[:, i, 0:ncol], func=AF.Copy),
                     reads=[bank[1]], writes=[zb])
                return zi, zb

            def trz(zi, zb, cols, m):
                bank = pr.next()
                i = bank[0]

                def th(e):
                    ins = None
                    for q, c in enumerate(cols):
                        ins = e.transpose(out=pb[0:m, i, q * TL:(q + 1) * TL], in_=ztok[:, zi, c:c + m], identity=C.ident_f[:])
                    return ins

                P.op("pe", th, reads=[zb, C.b], writes=[bank[1]])
                return bank

            for qi, qn in enumerate(("r", "k", "v")):
                zi, zb = projtok(qi * 512, 512)
                yield
                for h0 in (0, 4):
                    bank = trz(zi, zb, [(h0 + q) * 64 for q in range(4)], 64)
                    dst, dstb = (X["vb"], XB["vb"]) if qn == "v" else (F[qn], FB[qn])
                    P.op("act", lambda e, dst=dst, h0=h0, i=bank[0]: e.activation(
                        out=dst[:, h0:h0 + 4, :], in_=pb[0:64, i, 0:4 * TL].rearrange("p (q t) -> p q t", q=4), func=AF.Copy),
                        reads=[bank[1]], writes=[dstb])
                    yield
            zi, zb = projtok(1536, 288)
            yield
            for li, (c0, m, fn) in enumerate([(0, 64, AF.Tanh), (64, 64, AF.Copy), (128, 64, AF.Sigmoid),
                                              (192, 64, AF.Sigmoid), (256, 32, AF.Sigmoid)]):
                bank = trz(zi, zb, [c0], m)
                P.op("act", lambda e, li=li, m=m, fn=fn, i=bank[0]: e.activation(out=lora[0:m, li, :], in_=pb[0:m, i, 0:TL],
                                                                                 func=fn),
                     reads=[bank[1]], writes=[lora_b])
                yield
            for h in range(8):
                bank = pr.next()
                P.op("pe", lambda e, h=h, i=bank[0]: e.matmul(pb[0:64, i, 0:TL], lhsT=w2[:, h * 64:(h + 1) * 64],
                                                              rhs=lora[:, 0, :], start=True, stop=True),
                     reads=[lw_b, lora_b], writes=[bank[1]])
                P.op("act", lambda e, h=h, i=bank[0]: e.activation(out=F["sgd"][:, h, :], in_=pb[0:64, i, 0:TL],
                                                                   func=AF.Sigmoid, bias=cw0[:, h:h + 1]),
                     reads=[bank[1], b_w0], writes=[FB["sgd"]])
                bank = pr.next()
                P.op("pe", lambda e, h=h, i=bank[0]: e.matmul(pb[0:64, i, 0:TL], lhsT=a2[:, h * 64:(h + 1) * 64],
                                                              rhs=lora[:, 1, :], start=True, stop=True),
                     reads=[lw_b, lora_b], writes=[bank[1]])
                P.op("act", lambda e, h=h, i=bank[0]: e.activation(out=F["a"][:, h, :], in_=pb[0:64, i, 0:TL],
                                                                   func=AF.Sigmoid, bias=ca0[:, h:h + 1]),
                     reads=[bank[1], b_a0], writes=[FB["a"]])
                bank = pr.next()

                def gmm(e, h=h, i=bank[0]):
                    e.matmul(pb[0:64, i, 0:TL], lhsT=g2[:, 0, h * 64:(h + 1) * 64], rhs=lora[:, 2, :], start=True, stop=False)
                    e.matmul(pb[0:64, i, 0:TL], lhsT=g2[:, 1, h * 64:(h + 1) * 64], rhs=lora[:, 3, :], start=False, stop=False)
                    return e.matmul(pb[0:64, i, 0:TL], lhsT=g2[0:32, 2, h * 64:(h + 1) * 64], rhs=lora[0:32, 4, :],
                                    start=False, stop=True)

                P.op("pe", gmm, reads=[lw_b, lora_b], writes=[bank[1]])
                P.op("act", lambda e, h=h, i=bank[0]: e.activation(out=X["g"][:, h, :], in_=pb[0:64, i, 0:TL], func=AF.Copy),
                     reads=[bank[1]], writes=[XB["g"]])
                yield
            P.op("dve", lambda e: e.tensor_tensor(out=F["kk"][:], in0=F["k"][:], in1=bc(ckk), op=ALU.mult),
                 reads=[FB["k"], b_kk], writes=[FB["kk"]])
            P.op("pool", lambda e: e.tensor_tensor(out=F["sqb"][:], in0=F["kk"][:], in1=F["kk"][:], op=ALU.mult),
                 reads=[FB["kk"]], writes=[FB["sqb"]])
            yield
            for h in range(8):
                bank = pr.next()
                P.op("pe", lambda e, h=h, i=bank[0]: e.matmul(pb[0:64, i, 0:TL], lhsT=ones_bf[:], rhs=F["sqb"][:, h, :],
                                                              start=True, stop=True),
                     reads=[mk_b, FB["sqb"]], writes=[bank[1]])
                P.op("act", lambda e, h=h, i=bank[0]: e.activation(out=F["t1"][:, h, :], in_=pb[0:64, i, 0:TL], func=AF.Sqrt),
                     reads=[bank[1]], writes=[FB["t1"]])
                if h % 2 == 1:
                    yield
            P.op("dve", lambda e: e.tensor_scalar(out=F["t1"][:], in0=F["t1"][:], scalar1=1e-12, scalar2=None,
                                                  op0=ALU.max), reads=[FB["t1"]], writes=[FB["t1"]])
            P.op("dve", lambda e: e.reciprocal(out=F["t1"][:], in_=F["t1"][:]), reads=[FB["t1"]], writes=[FB["t1"]])
            yield
            P.op("dve", lambda e: e.tensor_tensor(out=F["kk"][:], in0=F["kk"][:], in1=F["t1"][:], op=ALU.mult),
                 reads=[FB["kk"], FB["t1"]], writes=[FB["kk"]])
            P.op("dve", lambda e: e.scalar_tensor_tensor(out=F["t2"][:], in0=F["a"][:], scalar=-1.0, in1=bc(cka),
                                                         op0=ALU.add, op1=ALU.mult),
                 reads=[FB["a"], b_ka], writes=[FB["t2"]])
            yield
            P.op("dve", lambda e: e.scalar_tensor_tensor(out=F["k"][:], in0=F["t2"][:], scalar=1.0, in1=F["k"][:],
                                                         op0=ALU.add, op1=ALU.mult),
                 reads=[FB["t2"], FB["k"]], writes=[FB["k"]])
            P.op("pool", lambda e: e.tensor_tensor(out=F["t2"][:], in0=F["kk"][:], in1=F["a"][:], op=ALU.mult),
                 reads=[FB["kk"], FB["a"]], writes=[FB["t2"]])
            yield
            P.op("dve", lambda e: e.tensor_tensor_scan(out=F["cum"][:].rearrange("p h t -> p (h t)"),
                                                       data0=rmask[:].rearrange("p h c t -> p (h c t)"),
                                                       data1=F["sgd"][:].rearrange("p h t -> p (h t)"),
                                                       initial=0.0, op0=ALU.mult, op1=ALU.add),
                 reads=[FB["sgd"], mk_b], writes=[FB["cum"]])
            cum4 = F["cum"][:].rearrange("p h (c t) -> p h c t", t=64)
            yield
            P.op("act", lambda e: e.activation(out=F["e1"][:], in_=F["cum"][:], func=AF.Exp, scale=-C0),
                 reads=[FB["cum"]], writes=[FB["e1"]])
            P.op("dve", lambda e: e.tensor_tensor(out=X["rT"][:], in0=F["r"][:], in1=F["e1"][:], op=ALU.mult),
                 reads=[FB["r"], FB["e1"]], writes=[XB["rT"]])
            P.op("act", lambda e: e.activation(out=gC[:, par], in_=cum4[:, :, :, 63], func=AF.Exp, scale=-C0),
                 reads=[FB["cum"]], writes=[gC_bs[par]])
            yield
            P.op("pool", lambda e: e.tensor_tensor(out=F["e2"][:], in0=F["cum"][:], in1=F["sgd"][:], op=ALU.subtract),
                 reads=[FB["cum"], FB["sgd"]], writes=[FB["e2"]])
            P.op("act", lambda e: e.activation(out=F["e2"][:], in_=F["e2"][:], func=AF.Exp, scale=-C0),
                 reads=[FB["e2"]], writes=[FB["e2"]])
            P.op("dve", lambda e: e.scalar_tensor_tensor(out=X["aT"][:], in0=F["kk"][:], scalar=-1.0, in1=F["e2"][:],
                                                         op0=ALU.mult, op1=ALU.mult),
                 reads=[FB["kk"], FB["e2"]], writes=[XB["aT"]])
            yield
            P.op("act", lambda e: e.activation(out=F["e1"][:], in_=F["cum"][:], func=AF.Exp, scale=C0),
                 reads=[FB["cum"]], writes=[FB["e1"]])
            P.op("dve", lambda e: e.tensor_tensor(out=X["bT"][:], in0=F["t2"][:], in1=F["e1"][:], op=ALU.mult),
                 reads=[FB["t2"], FB["e1"]], writes=[XB["bT"]])
            P.op("pool", lambda e: e.tensor_tensor(out=X["kT"][:], in0=F["k"][:], in1=F["e1"][:], op=ALU.mult),
                 reads=[FB["k"], FB["e1"]], writes=[XB["kT"]])
            yield
            P.op("dve", lambda e: e.tensor_tensor(out=F["e2"][:].rearrange("p h (c t) -> p h c t", t=64),
                                                  in0=cum4[:, :, :, 63:64].broadcast_to([64, 8, TL // 64, 64]), in1=cum4,
                                                  op=ALU.subtract),
                 reads=[FB["cum"]], writes=[FB["e2"]])
            P.op("act", lambda e: e.activation(out=F["e2"][:], in_=F["e2"][:], func=AF.Exp, scale=-C0),
                 reads=[FB["e2"]], writes=[FB["e2"]])
            yield
            P.op("dve", lambda e: e.tensor_tensor(out=X["bH"][:], in0=F["t2"][:], in1=F["e2"][:], op=ALU.mult),
                 reads=[FB["t2"], FB["e2"]], writes=[XB["bH"]])
            P.op("pool", lambda e: e.tensor_tensor(out=X["kH"][:], in0=F["k"][:], in1=F["e2"][:], op=ALU.mult),
                 reads=[FB["k"], FB["e2"]], writes=[XB["kH"]])
            yield
            P.op("dve", lambda e: e.tensor_tensor(out=F["t1"][:], in0=F["r"][:], in1=F["k"][:], op=ALU.mult),
                 reads=[FB["r"], FB["k"]], writes=[FB["t1"]])
            P.op("pool", lambda e: e.tensor_tensor(out=F["sqb"][:], in0=F["t1"][:], in1=bc(crk), op=ALU.mult),
                 reads=[FB["t1"], b_rk], writes=[FB["sqb"]])
            yield
            for h in range(8):
                bank = pr.next()
                P.op("pe", lambda e, h=h, i=bank[0]: e.matmul(pb[0:64, i, 0:TL], lhsT=ones_bf[:], rhs=F["sqb"][:, h, :],
                                                              start=True, stop=True),
                     reads=[mk_b, FB["sqb"]], writes=[bank[1]])
                P.op("dve", lambda e, h=h, i=bank[0]: e.tensor_tensor(out=X["bon"][:, h, :], in0=pb[0:64, i, 0:TL],
                                                                      in1=X["vb"][:, h, :], op=ALU.mult),
                     reads=[bank[1], XB["vb"]], writes=[XB["bon"]])
                if h % 2 == 1:
                    yield

        def chunk_pre(n, c, out):
            par = n % 2
            X = {k: X2[k][:, par] for k in X2}
            XB = {k: X2B[k][par] for k in X2}
            cs = slice(c * 64, (c + 1) * 64)
            for (dst, dbs, src) in ((Atok, Atok_b, "aT"), (BHtok, BHtok_b, "bH"), (KHtok, KHtok_b, "kH"), (Vtok, Vtok_b, "vb")):
                bank = pr.next()
                tr8(bank, lambda h, src=src: X[src][:, h, cs], [XB[src]])
                P.op("act", lambda e, dst=dst, i=bank[0]: e.activation(out=dst[:, c], in_=pvb(i), func=AF.Copy),
                     reads=[bank[1]], writes=[dbs[c]])
                yield

            def gmat(lname, rname, mi, dst, dbs, slot):
                bank = pr.next()
                mm8(bank, [(lambda h: X[lname][:, h, cs], lambda h: X[rname][:, h, cs])], [XB[lname], XB[rname]])
                P.op("dve", lambda e, i=bank[0]: e.tensor_tensor(out=dst[:, slot], in0=pv(i), in1=mb3(mi), op=ALU.mult),
                     reads=[bank[1], mk_b], writes=[dbs[slot]])

            base = 2 * c
            gmat("bT", "aT", 0, Mm, Mm_b, base)
            yield
            gmat("bT", "rT", 1, Mrb, Mrb_b, c)
            yield
            gmat("kT", "aT", 0, Lak, Lak_b, c)
            yield
            gmat("kT", "rT", 1, Mrk, Mrk_b, c)
            yield
            gmat("aT", "bT", 2, Nn, Nn_b, base)
            yield
            P.op("pool", lambda e: e.tensor_tensor(out=Qq[:, base], in0=Mm[:, base], in1=idf3, op=ALU.add),
                 reads=[Mm_b[base], C.b], writes=[Qq_b[base]])
            ni = mi_ = qi_ = base
            for lvl in range(1, 6):
                nn_ = base + (1 - (ni - base))
                nm_ = base + (1 - (mi_ - base))
                nq_ = base + (1 - (qi_ - base))
                bank = pr.next()
                mm8(bank, [(lambda h: Mm[:, mi_, h, :], lambda h: Nn[:, ni, h, :])], [Mm_b[mi_], Nn_b[ni]])
                if lvl < 5:
                    bank2 = pr.next()
                    mm8(bank2, [(lambda h: Nn[:, ni, h, :], lambda h: Mm[:, mi_, h, :])], [Mm_b[mi_], Nn_b[ni]])
                P.op("act", lambda e, nn_=nn_, i=bank[0]: e.activation(out=Nn[:, nn_], in_=pv(i), func=AF.Copy),
                     reads=[bank[1]], writes=[Nn_b[nn_]])
                if lvl < 5:
                    P.op("dve", lambda e, nm_=nm_, i=bank2[0]: e.tensor_copy(out=Mm[:, nm_], in_=pv(i)),
                         reads=[bank2[1]], writes=[Mm_b[nm_]])
                    mi_ = nm_
                ni = nn_
                yield
                bank3 = pr.next()
                mm8(bank3, [(lambda h: Nn[:, ni, h, :], lambda h: Qq[:, qi_, h, :])], [Qq_b[qi_], Nn_b[ni]])
                P.op("dve", lambda e, nq_=nq_, qo=qi_, i=bank3[0]: e.tensor_tensor(out=Qq[:, nq_], in0=pv(i), in1=Qq[:, qo],
                                                                                   op=ALU.add),
                     reads=[bank3[1], Qq_b[qi_]], writes=[Qq_b[nq_]])
                qi_ = nq_
                yield
            bank = pr.next()
            mm8(bank, [(lambda h: Atok[:, c, h, :], lambda h: Qq[:, qi_, h, :])], [Atok_b[c], Qq_b[qi_]])
            P.op("act", lambda e, i=bank[0]: e.activation(out=WT[:, c], in_=pv(i), func=AF.Copy),
                 reads=[bank[1]], writes=[WT_b[c]])
            bank = pr.next()
            mm8(bank, [(lambda h: Lak[:, c, h, :], lambda h: Vtok[:, c, h, :])], [Lak_b[c], Vtok_b[c]])
            P.op("dve", lambda e, i=bank[0]: e.tensor_copy(out=Xx[:, c], in_=pv(i)), reads=[bank[1]], writes=[Xx_b[c]])
            out["q"] = qi_
            yield

        def chain(n, c, qi_):
            par = n % 2
            X = {k: X2[k][:, par] for k in X2}
            XB = {k: X2B[k][par] for k in X2}
            cs = slice(c * 64, (c + 1) * 64)
            bank = pr.next()
            mm8(bank, [(lambda h: WT[:, c, h, :], lambda h: Sb[:, h, :]),
                       (lambda h: Qq[:, qi_, h, :], lambda h: Xx[:, c, h, :])], [WT_b[c], S_b, Qq_b[qi_], Xx_b[c]])
            P.op("act", lambda e, i=bank[0]: e.activation(out=Uu[:, c], in_=pv(i), func=AF.Copy),
                 reads=[bank[1]], writes=[Uu_b[c]])
            banky = pr.next()
            mm8(banky, [(lambda h: X["rT"][:, h, cs], lambda h: Sb[:, h, :]),
                        (lambda h: Mrb[:, c, h, :], lambda h: Uu[:, c, h, :]),
                        (lambda h: Mrk[:, c, h, :], lambda h: Vtok[:, c, h, :])],
                [XB["rT"], S_b, Mrb_b[c], Uu_b[c], Mrk_b[c], Vtok_b[c]])
            banks = pr.next()
            mm8(banks, [(lambda h: BHtok[:, c, h, :], lambda h: Uu[:, c, h, :]),
                        (lambda h: KHtok[:, c, h, :], lambda h: Vtok[:, c, h, :])], [BHtok_b[c], Uu_b[c], KHtok_b[c], Vtok_b[c]])
            P.op("dve", lambda e: e.tensor_tensor(out=St[:], in0=Sf[:],
                                                  in1=gC[:, par, :, c:c + 1].broadcast_to([64, 8, 64]), op=ALU.mult),
                 reads=[S_b, gC_bs[par]], writes=[St_b])
            P.op("dve", lambda e, i=banks[0]: e.tensor_tensor(out=Sf[:], in0=pv(i), in1=St[:], op=ALU.add),
                 reads=[banks[1], St_b], writes=[S_b])
            P.op("act", lambda e: e.activation(out=Sb[:], in_=Sf[:], func=AF.Copy), reads=[S_b], writes=[S_b])
            yield
            y_b, yq_b, g_b = Ys_b[0], St_b, gst_b[c]
            P.op("act", lambda e, i=banky[0]: e.activation(out=Ys[:, 0], in_=pv(i), func=AF.Copy),
                 reads=[banky[1]], writes=[y_b])
            P.op("pool", lambda e: e.tensor_tensor(out=St[:], in0=Ys[:, 0], in1=Ys[:, 0], op=ALU.mult),
                 reads=[y_b], writes=[yq_b])
            P.op("dve", lambda e: e.tensor_reduce(out=gst[:, c, :, 0], in_=Ys[:, 0], axis=AX.X, op=ALU.add),
                 reads=[y_b], writes=[g_b])
            P.op("dve", lambda e: e.tensor_reduce(out=gst[:, c, :, 1], in_=St[:], axis=AX.X, op=ALU.add),
                 reads=[yq_b, g_b], writes=[g_b])
            yield
            P.op("dve", lambda e: e.tensor_scalar(out=gst[:, c, :, 2], in0=gst[:, c, :, 0], scalar1=1.0 / 64,
                                                  scalar2=None, op0=ALU.mult), reads=[g_b], writes=[g_b])
            P.op("dve", lambda e: e.tensor_tensor(out=gst[:, c, :, 3], in0=gst[:, c, :, 2], in1=gst[:, c, :, 2],
                                                  op=ALU.mult), reads=[g_b], writes=[g_b])
            P.op("dve", lambda e: e.scalar_tensor_tensor(out=gst[:, c, :, 4], in0=gst[:, c, :, 1], scalar=1.0 / 64,
                                                         in1=gst[:, c, :, 3], op0=ALU.mult, op1=ALU.subtract),
                 reads=[g_b], writes=[g_b])
            yield
            P.op("act", lambda e: e.activation(out=gst[:, c, :, 5], in_=gst[:, c, :, 4], func=AF.Sqrt,
                                               bias=C.eps[0:64, 1:2]), reads=[g_b, C.b], writes=[g_b])
            P.op("dve", lambda e: e.reciprocal(out=gst[:, c, :, 5], in_=gst[:, c, :, 5]), reads=[g_b], writes=[g_b])
            P.op("dve", lambda e: e.tensor_tensor(out=Ys[:, 0], in0=Ys[:, 0],
                                                  in1=gst[:, c, :, 2:3].broadcast_to([64, 8, 64]),
                                                  op=ALU.subtract), reads=[y_b, g_b], writes=[y_b])
            yield
            P.op("dve", lambda e: e.tensor_tensor(out=Yn[:, c], in0=Ys[:, 0],
                                                  in1=gst[:, c, :, 5:6].broadcast_to([64, 8, 64]),
                                                  op=ALU.mult), reads=[y_b, g_b], writes=[Yn_b[c]])
            bank = pr.next()
            tr8(bank, lambda h: Yn[:, c, h, :], [Yn_b[c]])
            bc64 = lambda col: col[:, :].unsqueeze(2).broadcast_to([64, 8, 64])
            P.op("dve", lambda e, i=bank[0]: e.tensor_tensor(out=Ys[:, 0], in0=pvb(i), in1=bc64(clw), op=ALU.mult),
                 reads=[bank[1], b_lw, Yn_b[c]], writes=[y_b])
            P.op("pool", lambda e: e.tensor_tensor(out=Ys[:, 0], in0=Ys[:, 0], in1=bc64(clb), op=ALU.add),
                 reads=[y_b, b_lb], writes=[y_b])
            yield
            P.op("dve", lambda e: e.tensor_tensor(out=Ys[:, 0], in0=Ys[:, 0], in1=X["bon"][:, :, cs], op=ALU.add),
                 reads=[y_b, XB["bon"]], writes=[y_b])
            P.op("dve", lambda e: e.tensor_tensor(out=ost[:, :, cs], in0=Ys[:, 0], in1=X["g"][:, :, cs], op=ALU.mult),
                 reads=[y_b, XB["g"]], writes=[ost_b])
            yield

        def scan(n):
            par = n % 2
            X = {k: X2[k][:, par] for k in X2}
            XB = {k: X2B[k][par] for k in X2}
            outs = [{} for _ in range(TL // 64)]
            gens = [chunk_pre(n, c, outs[c]) for c in range(TL // 64)]
            alive = [True] * len(gens)
            while any(alive):
                for gi_, g_ in enumerate(gens):
                    if alive[gi_]:
                        try:
                            next(g_)
                        except StopIteration:
                            alive[gi_] = False
                yield
            for c in range(TL // 64):
                for _ in chain(n, c, outs[c]["q"]):
                    yield
            P.dma("sp", lambda e: e.dma_start(
                out=yr_dram.rearrange("(h v) s -> v h s", v=64)[:, :, n * TL:(n + 1) * TL], in_=ost[:]),
                reads=[ost_b])
            yield

        def run2(f, b):
            fa, ba = f is not None, b is not None
            while fa or ba:
                if fa:
                    try:
                        next(f)
                    except StopIteration:
                        fa = False
                if ba:
                    try:
                        next(b)
                    except StopIteration:
                        ba = False

        for n in range(NB + 1):
            run2(prep(n) if n < NB else None, scan(n - 1) if n >= 1 else None)
    P.barrier()


NITER = 16


def phase_attn(P, nc, C, A, ya_dram):
    x_dram = A["x"]
    with ExitStack() as es:
        sb = lambda name, shape, dt=F32: es.enter_context(nc.sbuf_tensor(name, shape, dt))
        wa = sb("wa", [128, 8, ATTN_IN + 64], BF16)
        wa_b = Buf()
        load_weight_bf16(P, nc, wa, wa_b, A["w_in"], 0, ATTN_IN, 8)
        wv_ = A["w_in"].rearrange("(kc p) n -> p kc n", p=128)
        for kc in range(8):
            P.dma("pool", lambda e, kc=kc: e.dma_start(out=wa[:, kc, ATTN_IN:ATTN_IN + 64], in_=wv_[:, kc, 2048:2112]),
                  writes=[wa_b])
        kT = sb("kT", [128, 4, S], BF16)
        kiT = sb("kiT", [128, S], BF16)
        vaug = sb("vaug", [128, NT, 8, 65], BF16)
        kT_bs = [Buf() for _ in range(NT)]
        kiT_bs = [Buf() for _ in range(NT)]
        va_bs = [Buf() for _ in range(NT)]
        P.op("pool", lambda e: e.memset(vaug[:, :, :, 64:65], 1.0), writes=va_bs)
        gqk = sb("gqk", [128, 2], F32)
        gqk_b = Buf()
        for half in range(2):
            P.dma("sp", lambda e, half=half: e.dma_start(out=gqk[half * 64:(half + 1) * 64, 0:1],
                                                         in_=A["attn_q_norm"].rearrange("o d -> d o"),
                                                         allow_slow_non_contiguous=True), writes=[gqk_b])
            P.dma("sp", lambda e, half=half: e.dma_start(out=gqk[half * 64:(half + 1) * 64, 1:2],
                                                         in_=A["attn_k_norm"].rearrange("o d -> d o"),
                                                         allow_slow_non_contiguous=True), writes=[gqk_b])
        P.op("dve", lambda e: e.tensor_scalar(out=gqk[:, 0:1], in0=gqk[:, 0:1], scalar1=0.125, scalar2=None, op0=ALU.mult),
             reads=[gqk_b], writes=[gqk_b])
        btf = sb("btf", [128, 8, 2, 128], F32)
        bt = sb("bt", [128, 8, 2, 128], BF16)
        b31 = sb("b31_sb", [128, 8], F32)
        bt_b = Buf()
        P.dma("sp", lambda e: e.dma_start(out=btf[:], in_=A["bias_tiles"].rearrange("h c s t -> s h c t")), writes=[bt_b])
        P.dma("sp", lambda e: e.dma_start(out=b31[:], in_=A["b31"].partition_broadcast(128)), writes=[bt_b])
        P.op("dve", lambda e: e.tensor_tensor(out=bt[:].rearrange("p h c t -> p h (c t)"),
                                              in0=btf[:].rearrange("p h c t -> p h (c t)"),
                                              in1=b31[:, :].unsqueeze(2).broadcast_to([128, 8, 256]), op=ALU.subtract),
             reads=[bt_b], writes=[bt_b])
        cmask = sb("cmask", [128, 128], F32)
        onesblk = sb("onesblk", [128, 128], BF16)
        cm_b = Buf()
        pw2 = sb("pw2", [128, NITER], F32)
        halfs = sb("halfs", [128, NITER], F32)
        hf_b = Buf()

        def mkc(e):
            e.memset(cmask[:], 0.0)
            e.affine_select(out=cmask[:], in_=cmask[:], pattern=[[-1, 128]], compare_op=ALU.is_ge, fill=NEG, base=0,
                            channel_multiplier=1)
            for j in range(NITER):
                e.memset(pw2[:, j:j + 1], 0.5 ** (j + 1))
            e.memset(onesblk[:], 0.0)
            e.memset(onesblk[0:64, 0:64], 1.0)
            return e.memset(onesblk[64:128, 64:128], 1.0)

        P.op("pool", mkc, writes=[cm_b])
        nm = Normer(P, nc, es, C, A["mix_norm"], "an", nslots=1)
        xbuf = sb("axbuf", [128, 2, D], F32)
        xr = Ring([(i, Buf()) for i in range(2)])
        xn = sb("axn", [128, 8, 128], BF16)
        xn_b = Buf()
        q_i = sb("q_i", [128, 2, 4, 128], BF16)
        qi_i = sb("qi_i", [128, 4, 128], BF16)
        wi_i = sb("wi_i", [128, 8], F32)
        q_bs, qi_b, wi_b = [Buf(), Buf()], Buf(), Buf()
        sq = sb("asq", [128, 512], BF16)
        rn = sb("arn", [128, 512], F32)
        sq_b, rn_b = Buf(), Buf()
        sc = sb("sc", [128, S], F32)
        sc_b = Buf()
        rbuf = sb("rbuf", [128, 2, 512], F32)
        rr = Ring([(i, Buf()) for i in range(2)])
        junk = sb("ajunk", [128, S], BF16)
        junk_b = Buf()
        bs = sb("bs", [128, 8], F32)
        bs_b = Buf()
        maskT = sb("maskT", [128, 2, NT, 128], BF16)
        mT_bs = [Buf(), Buf()]
        Et = sb("Et", [128, 3, 4, 128], BF16)
        er = Ring([(i, Buf()) for i in range(3)])
        Pt = sb("Pt", [128, 4, 4, 128], BF16)
        ptr = Ring([(i, Buf()) for i in range(4)])
        rden = sb("rden", [128, 2], F32)
        rdr = Ring([(i, Buf()) for i in range(2)])
        ytile = sb("ytile", [128, 512], BF16)
        yt_b = Buf()
        yst = sb("yst", [128, 4, 128], BF16)
        ys_b = Buf()
        pg = es.enter_context(nc.psum_tensor("apg", [128, 5, 512], F32))
        gr = Ring([(i, Buf()) for i in range(3)])
        lgr = Ring([(i, Buf()) for i in range(3, 5)])
        po = es.enter_context(nc.psum_tensor("apo", [128, 2, 512], F32))
        orr = Ring([(i, Buf()) for i in range(2)])
        pgb = lambda i: pg[:, i, :].bitcast(BF16)
        if SBUF_DEBUG:
            print("attn sbuf remaining", nc.sbuf_bytes_remaining)
        xview = x_dram.rearrange("(t p) d -> t p d", p=128)
        yav = ya_dram.rearrange("(m p) s -> p m s", p=128)

        def front(i):
            ts_ = slice(i * 128, (i + 1) * 128)
            nkb = i + 1
            W = nkb * 128
            q_b = q_bs[i % 2]
            kT_b, kiT_b, va_b = kT_bs[i], kiT_bs[i], va_bs[i]
            xi, xb_ = xr.next()
            P.dma("sp", lambda e, i=i, xi=xi: e.dma_start(out=xbuf[:, xi, :], in_=xview[i]), writes=[xb_])
            nm.run(xbuf[:, xi, :], xb_, xn, xn_b, 0)
            yield

            def proj4(c0, bank):
                bi, bb = bank

                def th(e):
                    ins = None
                    for m in range(4):
                        for kc in range(8):
                            ins = e.matmul(pg[:, bi, m * 128:(m + 1) * 128], lhsT=wa[:, kc, c0 + m * 128:c0 + (m + 1) * 128],
                                           rhs=xn[:, kc, :], start=(kc == 0), stop=(kc == 7))
                    return ins

                P.op("pe", th, reads=[wa_b, xn_b], writes=[bb])

            for which, c0 in ((0, 0), (1, 512)):
                bank = gr.next()
                proj4(c0, bank)
                P.op("act", lambda e, bi=bank[0]: e.activation(out=sq[:], in_=pg[:, bi, :], func=AF.Square),
                     reads=[bank[1]], writes=[sq_b])
                bank2 = gr.next()
                P.op("pe", lambda e, bi=bank2[0]: e.matmul(pg[:, bi, :], lhsT=onesblk[:], rhs=sq[:], start=True, stop=True),
                     reads=[cm_b, sq_b], writes=[bank2[1]])
                P.op("act", lambda e, bi=bank2[0]: e.activation(out=rn[:], in_=pg[:, bi, :], func=AF.Sqrt, scale=1.0 / 64,
                                                                bias=C.eps[:, 0:1]), reads=[bank2[1], C.b], writes=[rn_b])
                P.op("dve", lambda e: e.reciprocal(out=rn[:], in_=rn[:]), reads=[rn_b], writes=[rn_b])
                if which == 0:
                    P.op("dve", lambda e, bi=bank[0]: e.scalar_tensor_tensor(
                        out=q_i[:, i % 2].rearrange("p m t -> p (m t)"), in0=pg[:, bi, :], scalar=gqk[:, 0:1], in1=rn[:],
                        op0=ALU.mult, op1=ALU.mult), reads=[bank[1], rn_b, gqk_b], writes=[q_b])
                else:
                    P.op("dve", lambda e, bi=bank[0], ts_=ts_: e.scalar_tensor_tensor(
                        out=kT[:, :, ts_], in0=pg[:, bi, :].rearrange("p (m t) -> p m t", m=4), scalar=gqk[:, 1:2],
                        in1=rn[:].rearrange("p (m t) -> p m t", m=4), op0=ALU.mult, op1=ALU.mult),
                        reads=[bank[1], rn_b, gqk_b], writes=[kT_b])
                yield
            bank = gr.next()
            proj4(1536, bank)
            P.op("act", lambda e, bi=bank[0]: e.activation(out=qi_i[:].rearrange("p m t -> p (m t)"), in_=pg[:, bi, :],
                                                           func=AF.Copy), reads=[bank[1]], writes=[qi_b])
            yield
            bank = gr.next()

            def kiw(e, bi=bank[0]):
                for kc in range(8):
                    e.matmul(pg[0:64, bi, 0:128], lhsT=wa[:, kc, 2048:2112], rhs=xn[:, kc, :], start=(kc == 0), stop=(kc == 7))
                for kc in range(8):
                    e.matmul(pg[64:128, bi, 0:128], lhsT=wa[:, kc, ATTN_IN:ATTN_IN + 64], rhs=xn[:, kc, :], start=(kc == 0),
                             stop=(kc == 7))
                ins = None
                for kc in range(8):
                    ins = e.matmul(pg[:, bi, 128:136], lhsT=xn[:, kc, :], rhs=wa[:, kc, 2112:2120], start=(kc == 0), stop=(kc == 7))
                return ins

            P.op("pe", kiw, reads=[wa_b, xn_b], writes=[bank[1]])
            P.op("act", lambda e, bi=bank[0], ts_=ts_: e.activation(out=kiT[:, ts_], in_=pg[:, bi, 0:128], func=AF.Copy),
                 reads=[bank[1]], writes=[kiT_b])
            P.op("dve", lambda e, bi=bank[0]: e.tensor_copy(out=wi_i[:], in_=pg[:, bi, 128:136]), reads=[bank[1]], writes=[wi_b])
            yield
            bank = gr.next()

            def vmm(e, bi=bank[0]):
                ins = None
                for kc in range(8):
                    ins = e.matmul(pg[:, bi, :], lhsT=xn[:, kc, :], rhs=wa[:, kc, 1024:1536], start=(kc == 0), stop=(kc == 7))
                return ins

            P.op("pe", vmm, reads=[wa_b, xn_b], writes=[bank[1]])
            P.op("act", lambda e, bi=bank[0], i=i: e.activation(out=vaug[:, i, :, 0:64],
                                                                in_=pg[:, bi, :].rearrange("p (h d) -> p h d", h=8), func=AF.Copy),
                 reads=[bank[1]], writes=[va_b])
            yield

            for gk in range((nkb + 3) // 4):
                w_ = min(512, W - gk * 512)
                for h in range(8):
                    hb = (h % 2) * 64
                    bank = gr.next()
                    P.op("pe", lambda e, bi=bank[0], h=h, hb=hb, gk=gk, w_=w_: e.matmul(
                        pg[:, bi, 0:w_], lhsT=qi_i[hb:hb + 64, h // 2, :], rhs=kiT[hb:hb + 64, gk * 512:gk * 512 + w_],
                        start=True, stop=True), reads=[qi_b] + kiT_bs[gk * 4:gk * 4 + (w_ // 128)], writes=[bank[1]])
                    ri, rb_ = rr.next()
                    P.op("act", lambda e, bi=bank[0], ri=ri, w_=w_: e.activation(out=rbuf[:, ri, 0:w_], in_=pg[:, bi, 0:w_],
                                                                                 func=AF.Relu), reads=[bank[1]], writes=[rb_])
                    if h == 0:
                        P.op("dve", lambda e, ri=ri, gk=gk, w_=w_: e.tensor_scalar(
                            out=sc[:, gk * 512:gk * 512 + w_], in0=rbuf[:, ri, 0:w_], scalar1=wi_i[:, 0:1], scalar2=None,
                            op0=ALU.mult), reads=[rb_, wi_b], writes=[sc_b])
                    else:
                        P.op("dve", lambda e, ri=ri, gk=gk, w_=w_, h=h: e.scalar_tensor_tensor(
                            out=sc[:, gk * 512:gk * 512 + w_], in0=rbuf[:, ri, 0:w_], scalar=wi_i[:, h:h + 1],
                            in1=sc[:, gk * 512:gk * 512 + w_], op0=ALU.mult, op1=ALU.add), reads=[rb_, wi_b, sc_b], writes=[sc_b])
                    yield
            P.op("dve", lambda e, ts_=ts_: e.tensor_tensor(out=sc[:, ts_], in0=sc[:, ts_], in1=cmask[:], op=ALU.add),
                 reads=[sc_b, cm_b], writes=[sc_b])
            if i < 2:
                P.op("dve", lambda e: e.memset(bs[:, 0:1], -1.0e29), writes=[bs_b])
            else:
                P.op("dve", lambda e, i=i: e.tensor_reduce(out=bs[:, 0:1], in_=sc[:, 0:i * 128], axis=AX.X, op=ALU.min),
                     reads=[sc_b], writes=[bs_b])
                P.op("dve", lambda e, W=W: e.tensor_reduce(out=bs[:, 6:7], in_=sc[:, 0:W], axis=AX.X, op=ALU.max),
                     reads=[sc_b, bs_b], writes=[bs_b])
                P.op("dve", lambda e: e.tensor_tensor(out=bs[:, 1:2], in0=bs[:, 6:7], in1=bs[:, 0:1], op=ALU.subtract),
                     reads=[bs_b], writes=[bs_b])
                P.op("dve", lambda e: e.tensor_tensor(out=halfs[:], in0=bs[:, 1:2].broadcast_to([128, NITER]), in1=pw2[:],
                                                      op=ALU.mult), reads=[bs_b, cm_b], writes=[hf_b])
                P.op("dve", lambda e: e.tensor_tensor(out=bs[:, 3:4], in0=bs[:, 0:1], in1=halfs[:, 0:1], op=ALU.add),
                     reads=[bs_b, hf_b], writes=[bs_b])
                for it in range(NITER):
                    P.op("dve", lambda e: e.tensor_scalar(out=junk[:, 0:W], in0=sc[:, 0:W], scalar1=bs[:, 3:4], scalar2=None,
                                                          op0=ALU.is_ge, op1=ALU.add, accum_out=bs[:, 4:5]),
                         reads=[sc_b, bs_b], writes=[junk_b, bs_b])
                    P.op("dve", lambda e, it=it: e.scalar_tensor_tensor(out=bs[:, 5:6], in0=bs[:, 4:5], scalar=TOPK - 0.5,
                                                                        in1=halfs[:, it:it + 1], op0=ALU.is_ge, op1=ALU.mult),
                         reads=[bs_b, hf_b], writes=[bs_b])
                    P.op("dve", lambda e: e.tensor_tensor(out=bs[:, 0:1], in0=bs[:, 0:1], in1=bs[:, 5:6], op=ALU.add),
                         reads=[bs_b], writes=[bs_b])
                    if it + 1 < NITER:
                        P.op("dve", lambda e, it=it: e.tensor_tensor(out=bs[:, 3:4], in0=bs[:, 0:1], in1=halfs[:, it + 1:it + 2],
                                                                     op=ALU.add), reads=[bs_b, hf_b], writes=[bs_b])
                    yield
            P.op("dve", lambda e, W=W: e.tensor_scalar(out=junk[:, 0:W], in0=sc[:, 0:W], scalar1=bs[:, 0:1], scalar2=None,
                                                       op0=ALU.is_ge), reads=[sc_b, bs_b], writes=[junk_b])
            for j0 in range(0, nkb, 8):
                nb = min(8, nkb - j0)
                bank = gr.next()

                def trm(e, bi=bank[0], j0=j0, nb=nb):
                    ins = None
                    for jj in range(nb):
                        ins = e.transpose(out=pgb(bi)[:, jj * 128:(jj + 1) * 128], in_=junk[:, (j0 + jj) * 128:(j0 + jj + 1) * 128],
                                          identity=C.ident_bf[:])
                    return ins

                P.op("pe", trm, reads=[junk_b, C.b], writes=[bank[1]])
                P.op("act", lambda e, bi=bank[0], j0=j0, nb=nb: e.activation(
                    out=maskT[:, i % 2, j0:j0 + nb, :].rearrange("p j t -> p (j t)"), in_=pgb(bi)[:, 0:nb * 128], func=AF.Copy),
                    reads=[bank[1]], writes=[mT_bs[i % 2]])
                yield
            yield

        def back(i):
            ts_ = slice(i * 128, (i + 1) * 128)
            nkb = i + 1
            q_b = q_bs[i % 2]
            mT_b = mT_bs[i % 2]
            items = [(h, j0, min(4, nkb - j0)) for h in range(8) for j0 in range(0, nkb, 4)]
            DEPTH = 2
            st = {}
            obs = {}
            for k in range(len(items) + DEPTH):
                if k < len(items):
                    h, j0, nb = items[k]
                    hb = (h % 2) * 64
                    m = h // 2
                    if j0 == 0:
                        obs[h] = orr.next()
                    bank = lgr.next()

                    def qk(e, bi=bank[0], j0=j0, nb=nb, h=h, hb=hb, m=m):
                        ins = None
                        for jj in range(nb):
                            j = j0 + jj
                            near = j >= i - 1
                            ins = e.matmul(pg[:, bi, jj * 128:(jj + 1) * 128], lhsT=kT[hb:hb + 64, m, j * 128:(j + 1) * 128],
                                           rhs=q_i[hb:hb + 64, i % 2, m, :], start=True, stop=not near)
                            if near:
                                ins = e.matmul(pg[:, bi, jj * 128:(jj + 1) * 128], lhsT=C.ident_bf[:],
                                               rhs=bt[:, h, 0 if j == i else 1, :], start=False, stop=True)
                        return ins

                    P.op("pe", qk, reads=kT_bs[j0:j0 + nb] + [q_b, bt_b, C.b], writes=[bank[1]])
                    ei, eb = er.next()
                    P.op("act", lambda e, bi=bank[0], ei=ei, nb=nb: e.activation(
                        out=Et[:, ei, 0:nb, :].rearrange("p j t -> p (j t)"), in_=pg[:, bi, 0:nb * 128], func=AF.Exp),
                        reads=[bank[1]], writes=[eb])
                    pi, pb_ = ptr.next()
                    P.op("pool", lambda e, ei=ei, pi=pi, j0=j0, nb=nb: e.tensor_tensor(
                        out=Pt[:, pi, 0:nb, :], in0=Et[:, ei, 0:nb, :], in1=maskT[:, i % 2, j0:j0 + nb, :], op=ALU.mult),
                        reads=[eb, mT_b], writes=[pb_])
                    st[k] = (pi, pb_)
                kk = k - DEPTH
                if kk >= 0:
                    h, j0, nb = items[kk]
                    pi, pb_ = st.pop(kk)
                    ob = obs[h]

                    def pv(e, oi=ob[0], pi=pi, j0=j0, nb=nb, h=h):
                        ins = None
                        for jj in range(nb):
                            j = j0 + jj
                            ins = e.matmul(po[:, oi, 0:65], lhsT=Pt[:, pi, jj, :], rhs=vaug[:, j, h, :], start=(j == 0),
                                           stop=(j == i))
                        return ins

                    P.op("pe", pv, reads=[pb_] + va_bs[j0:j0 + nb], writes=[ob[1]])
                    if j0 + nb == nkb:
                        di, db = rdr.next()
                        P.op("dve", lambda e, oi=ob[0], di=di: e.reciprocal(out=rden[:, di:di + 1], in_=po[:, oi, 64:65]),
                             reads=[ob[1]], writes=[db])
                        P.op("act", lambda e, oi=ob[0], di=di, h=h: e.activation(out=ytile[:, h * 64:(h + 1) * 64],
                                                                                 in_=po[:, oi, 0:64], func=AF.Copy,
                                                                                 scale=rden[:, di:di + 1]),
                             reads=[ob[1], db], writes=[yt_b])
                yield
            bank = lgr.next()

            def try_(e, bi=bank[0]):
                ins = None
                for m in range(4):
                    ins = e.transpose(out=pgb(bi)[:, m * 128:(m + 1) * 128], in_=ytile[:, m * 128:(m + 1) * 128],
                                      identity=C.ident_bf[:])
                return ins

            P.op("pe", try_, reads=[yt_b, C.b], writes=[bank[1]])
            P.op("act", lambda e, bi=bank[0]: e.activation(out=yst[:].rearrange("p m t -> p (m t)"), in_=pgb(bi)[:, 0:512],
                                                           func=AF.Copy), reads=[bank[1]], writes=[ys_b])
            P.dma("sp", lambda e: e.dma_start(out=yav[:, :, ts_], in_=yst[:]), reads=[ys_b])
            yield

        def run2(f, b):
            fa, ba = f is not None, b is not None
            while fa or ba:
                if fa:
                    try:
                        next(f)
                    except StopIteration:
                        fa = False
                if ba:
                    try:
                        next(b)
                    except StopIteration:
                        ba = False

        for i in range(NT + 1):
            run2(front(i) if i < NT else None, back(i - 1) if i >= 1 else None)
    P.barrier()


WEIGHT_SPECS = [
    ("mix_norm", [1, D]), ("w_in", [D, 5992]), ("attn_q_norm", [1, 64]), ("attn_k_norm", [1, 64]),
    ("bias_tiles", [8, 2, 128, 128]), ("b31", [1, 8]), ("rwkv_mu", [1, RWKV_IN]), ("rwkv_w0", [1, 512]), ("rwkv_w2", [64, 512]),
    ("rwkv_a0", [1, 512]), ("rwkv_a2", [64, 512]), ("rwkv_g2", [160, 512]), ("rwkv_k_k", [1, 512]),
    ("rwkv_k_a", [1, 512]), ("rwkv_r_k", [1, 512]), ("rwkv_ln_w", [1, 512]), ("rwkv_ln_b", [1, 512]),
    ("w_branch_attn", [512, D]), ("w_branch_rwkv", [512, D]), ("w_out", [D, D]), ("ffn_norm", [1, D]),
    ("w_gate_up", [D, 2 * FFN_H]), ("w_down", [FFN_H, D]),
]


def build_program(phases=("attn", "rwkv", "merge", "ffn"), debug=False):
    nc = bass.Bass("TRN2", target_bir_lowering=False)
    A = {}
    A["x"] = nc.dram_tensor("x", [S, D], F32, kind="ExternalInput").ap()
    for name, shp in WEIGHT_SPECS:
        A[name] = nc.dram_tensor(name, shp, F32, kind="ExternalInput").ap()
    out = nc.dram_tensor("out", [S, D], F32, kind="ExternalOutput").ap()
    def kind(prod, cons):
        if not debug:
            return "Internal"
        if prod in phases and cons not in phases:
            return "ExternalOutput"
        if prod not in phases and cons in phases:
            return "ExternalInput"
        return "Internal"

    ya = nc.dram_tensor("ya_scr", [512, S], BF16, kind=kind("attn", "merge")).ap()
    yr = nc.dram_tensor("yr_scr", [512, S], BF16, kind=kind("rwkv", "merge")).ap()
    hs = nc.dram_tensor("h_scr", [S, D], F32, kind=kind("merge", "ffn")).ap()
    P = Prog(nc)
    final_ops = []
    with ExitStack() as es:
        C = make_consts(P, nc, es)
        P.barrier()
        if "attn" in phases:
            phase_attn(P, nc, C, A, ya)
        if "rwkv" in phases:
            phase_rwkv(P, nc, C, A, yr)
        if "merge" in phases:
            phase_merge(P, nc, C, A["x"], ya, yr, hs, A["mix_norm"], A["w_in"], A["w_branch_attn"],
                        A["w_branch_rwkv"], A["w_out"])
        if "ffn" in phases:
            phase_ffn(P, nc, C, hs, out, A["ffn_norm"], A["w_gate_up"], A["w_down"], final_ops)
        P.emit(final_wait_ops=final_ops)
    return nc


def t5_bucket_np(d):
    d = np.maximum(d, 0)
    max_exact = 16
    log_ratio = np.log(np.maximum(d, 1).astype(np.float32) / max_exact) / math.log(128 / max_exact)
    large = np.minimum(max_exact + (log_ratio * 16).astype(np.int32), 31)
    return np.where(d < max_exact, d, large)


def host_layout(inputs):
    w = {}
    for name, shp in WEIGHT_SPECS:
        if name in ("bias_tiles", "b31"):
            continue
        w[name] = np.ascontiguousarray(np.asarray(inputs[name], dtype=np.float32).reshape(shp))
    s_idx = np.arange(128)[:, None]
    t_idx = np.arange(128)[None, :]
    rb = np.asarray(inputs["rel_bias"], dtype=np.float32)
    tiles = np.empty((8, 2, 128, 128), np.float32)
    for cls in range(2):
        bk = t5_bucket_np(t_idx - s_idx + 128 * cls)
        tiles[:, cls] = np.transpose(rb[bk], (2, 0, 1))
    w["bias_tiles"] = tiles
    w["b31"] = np.ascontiguousarray(rb[31:32, :])
    return w


_NC_CACHE = {}


def kernel(**inputs):
    x = np.asarray(inputs["x"], dtype=np.float32)
    w = host_layout(inputs)
    if "nc" not in _NC_CACHE:
        _NC_CACHE["nc"] = build_program()
    nc = _NC_CACHE["nc"]
    in_maps = []
    for b in range(8):
        m = dict(w)
        m["x"] = np.ascontiguousarray(x[b])
        in_maps.append(m)
    res = run_bass_kernel_spmd(nc, in_maps, core_ids=list(range(8)))
    return np.stack([np.asarray(r["out"], dtype=np.float32) for r in res.results], axis=0)
```

```python
import math
from contextlib import ExitStack

import numpy as np
import concourse.bass as bass
import concourse.mybir as mybir
from concourse.bass_utils import run_bass_kernel_spmd

F32 = mybir.dt.float32
BF16 = mybir.dt.bfloat16
AF = mybir.ActivationFunctionType
ALU = mybir.AluOpType
AX = mybir.AxisListType

S = 4096
D = 1024
NT = S // 128
ATTN_IN = 2120
RWKV_IN = 1824
FFN_H = 2816
RMS_EPS = 1e-6
GN_EPS = 64e-5
TOPK = 256
NEG = -1.0e30

ENGS = ("pe", "act", "dve", "pool", "sp")
SBUF_DEBUG = False


class Buf:
    __slots__ = ("name", "w", "r")

    def __init__(self, name=""):
        self.name = name
        self.w = None
        self.r = []


class Op:
    __slots__ = ("eng", "thunk", "deps", "is_dma", "sem", "val", "need_inc", "pos", "prev_on_sem")


class Prog:
    def __init__(self, nc, n_dma_sems=(56, 28)):
        self.nc = nc
        self.streams = {e: [] for e in ENGS}
        self.n_dma_sems = n_dma_sems
        self.dma_count = 0
        self.all_ops = []
        self.open_dmas = []

    def _hazards(self, op, reads, writes):
        deps = []
        for b in reads:
            if b.w is not None:
                deps.append(b.w)
        for b in writes:
            if b.w is not None:
                deps.append(b.w)
            deps.extend(b.r)
        for b in reads:
            b.r.append(op)
        for b in writes:
            b.w = op
            b.r = []
        return deps

    def op(self, eng, thunk, reads=(), writes=(), extra_deps=()):
        o = Op()
        o.eng = eng
        o.thunk = thunk
        o.is_dma = False
        o.need_inc = False
        o.sem = None
        o.val = None
        o.prev_on_sem = None
        deps = self._hazards(o, reads, writes) + list(extra_deps)
        seen = set()
        o.deps = []
        for d in deps:
            if d is o or id(d) in seen:
                continue
            if eng == "pe" and d.eng == "pe" and not d.is_dma:
                continue
            seen.add(id(d))
            o.deps.append(d)
        self.streams[eng].append(o)
        self.all_ops.append(o)
        return o

    def dma(self, eng, thunk, reads=(), writes=(), extra_deps=()):
        o = self.op(eng, thunk, reads, writes, extra_deps)
        o.is_dma = True
        o.pos = self.dma_count
        self.dma_count += 1
        self.open_dmas.append(o)
        return o

    def barrier(self):
        lasts = []
        for e in ENGS:
            for o in reversed(self.streams[e]):
                if not o.is_dma:
                    lasts.append(o)
                    break
        deps = lasts + self.open_dmas
        self.open_dmas = []
        for e in ENGS:
            self.op(e, lambda eng: eng.nop(), extra_deps=deps)

    def emit(self, final_wait_ops=()):
        nc = self.nc
        for o in self.all_ops:
            for d in o.deps:
                d.need_inc = True
        eng_sems = {e: nc.alloc_semaphore("s_" + e) for e in ENGS}
        ring_n = {"sp": self.n_dma_sems[0], "pool": self.n_dma_sems[1], "act": 2, "dve": 2, "pe": 2}
        dma_sems = {}
        dma_sem_val = {}
        dma_prev = {}
        qpos = {e: 0 for e in ENGS}
        for e in ENGS:
            if any(o.is_dma for o in self.streams[e]):
                for i in range(ring_n[e]):
                    dma_sems[(e, i)] = nc.alloc_semaphore("s_dma_%s%d" % (e, i))
                    dma_sem_val[(e, i)] = 0
                    dma_prev[(e, i)] = None
        cnt = {e: 0 for e in ENGS}
        for o in self.all_ops:
            if o.is_dma:
                kq = (o.eng, qpos[o.eng] % ring_n[o.eng])
                qpos[o.eng] += 1
                dma_sem_val[kq] += 16
                o.sem = ("dma", kq)
                o.val = dma_sem_val[kq]
                o.prev_on_sem = dma_prev[kq]
                dma_prev[kq] = o
            elif o.need_inc:
                cnt[o.eng] += 1
                o.sem = ("eng", o.eng)
                o.val = cnt[o.eng]

        def semh(key):
            return eng_sems[key[1]] if key[0] == "eng" else dma_sems[key[1]]

        engines = {"pe": "tensor", "act": "scalar", "dve": "vector", "pool": "gpsimd", "sp": "sync"}
        with nc.Block() as block:
            for e in ENGS:
                stream = self.streams[e]
                final = list(final_wait_ops) if e == "sp" else []

                def body(engine, stream=stream, final=final):
                    known = {}
                    for o in stream:
                        waits = {}
                        deps = list(o.deps)
                        if o.is_dma and o.prev_on_sem is not None:
                            deps.append(o.prev_on_sem)
                        for d in deps:
                            if known.get(d.sem, 0) >= d.val:
                                continue
                            if waits.get(d.sem, 0) < d.val:
                                waits[d.sem] = d.val
                        for key, val in waits.items():
                            engine.wait_ge(semh(key), val)
                            known[key] = val
                        ins = o.thunk(engine)
                        if o.is_dma:
                            ins.then_inc(semh(o.sem), 16)
                        elif o.need_inc:
                            ins.then_inc(semh(o.sem), 1)
                    for o in final:
                        engine.wait_ge(semh(o.sem), o.val)

                getattr(block, engines[e])(body)


class Ring:
    def __init__(self, items):
        self.items = items
        self.i = 0

    def next(self):
        it = self.items[self.i % len(self.items)]
        self.i += 1
        return it


def load_weight_bf16(P, nc, dst, dst_buf, w_ap, c0, c1, kchunks, eng="pool"):
    wv = w_ap.rearrange("(kc p) n -> p kc n", p=128)
    for kc in range(kchunks):
        for a in range(c0, c1, 2048):
            b = min(c1, a + 2048)
            P.dma(eng, lambda e, kc=kc, a=a, b=b: e.dma_start(out=dst[:, kc, a - c0:b - c0], in_=wv[:, kc, a:b]),
                  writes=[dst_buf])


def load_col_vec(P, nc, dst, dst_buf, v_ap, n):
    src = v_ap.rearrange("o (c p) -> p (o c)", p=128)
    P.dma("sp", lambda e: e.dma_start(out=dst, in_=src, allow_slow_non_contiguous=True), writes=[dst_buf])


class Consts:
    pass


def make_consts(P, nc, es):
    C = Consts()
    C.ident_bf = es.enter_context(nc.sbuf_tensor("ident_bf", [128, 128], BF16))
    C.ident_f = es.enter_context(nc.sbuf_tensor("ident_f", [128, 128], F32))
    C.eps = es.enter_context(nc.sbuf_tensor("eps_c", [128, 2], F32))
    C.b = Buf("consts")

    def mk(e):
        e.memset(C.ident_f[:], 0.0)
        e.affine_select(out=C.ident_f[:], in_=C.ident_f[:], pattern=[[-1, 128]], compare_op=ALU.not_equal,
                        fill=1.0, base=0, channel_multiplier=1)
        e.memset(C.eps[:, 0:1], RMS_EPS)
        return e.memset(C.eps[:, 1:2], GN_EPS)

    P.op("pool", mk, writes=[C.b])
    P.op("pool", lambda e: e.tensor_copy(out=C.ident_bf[:], in_=C.ident_f[:]), reads=[C.b], writes=[C.b])
    return C


class Normer:
    def __init__(self, P, nc, es, C, gain_ap, name, nslots=2):
        self.P, self.nc, self.C = P, nc, C
        self.gcol = es.enter_context(nc.sbuf_tensor(name + "_g", [128, 8], F32))
        self.gb = Buf(name + "_g")
        load_col_vec(P, nc, self.gcol[:, :], self.gb, gain_ap, 8)
        self.stat = es.enter_context(nc.sbuf_tensor(name + "_st", [128, nslots, 4], F32))
        self.junk = es.enter_context(nc.sbuf_tensor(name + "_junk", [128, 1024], BF16))
        self.xs = es.enter_context(nc.sbuf_tensor(name + "_xs", [128, nslots, 1024], BF16))
        self.tp = es.enter_context(nc.psum_tensor(name + "_tp", [128, nslots, 8, 128], BF16))
        self.ring = Ring([(i, Buf(), Buf(), Buf()) for i in range(nslots)])
        self.junkb = Buf()

    def run(self, xt_ap, xt_buf, dst, dst_buf, col0):
        P, C = self.P, self.C
        i, sb, xb, pb = self.ring.next()
        st = self.stat
        P.op("act", lambda e: e.activation(out=self.junk[:], in_=xt_ap, func=AF.Square, accum_out=st[:, i, 0:1]),
             reads=[xt_buf], writes=[self.junkb, sb])
        P.op("act", lambda e: e.activation(out=st[:, i, 1:2], in_=st[:, i, 0:1], func=AF.Sqrt, scale=1.0 / D,
                                           bias=C.eps[:, 0:1]), reads=[sb, C.b], writes=[sb])
        P.op("dve", lambda e: e.reciprocal(out=st[:, i, 2:3], in_=st[:, i, 1:2]), reads=[sb], writes=[sb])
        P.op("act", lambda e: e.activation(out=self.xs[:, i, :], in_=xt_ap, func=AF.Copy, scale=st[:, i, 2:3]),
             reads=[xt_buf, sb], writes=[xb])

        def tr(e):
            ins = None
            for kc in range(8):
                ins = e.transpose(out=self.tp[:, i, kc, :], in_=self.xs[:, i, kc * 128:(kc + 1) * 128],
                                  identity=C.ident_bf[:])
            return ins

        P.op("pe", tr, reads=[xb, C.b], writes=[pb])
        P.op("dve", lambda e: e.tensor_tensor(out=dst[:, :, col0:col0 + 128], in0=self.tp[:, i, :, :],
                                              in1=self.gcol[:, :].unsqueeze(2).broadcast_to([128, 8, 128]),
                                              op=ALU.mult), reads=[pb, self.gb], writes=[dst_buf])


def phase_ffn(P, nc, C, h_dram, out_dram, ffn_norm, w_gate_up, w_down, final_ops):
    with ExitStack() as es:
        wgu = es.enter_context(nc.sbuf_tensor("wgu", [128, 8, 2 * FFN_H], BF16))
        wd = es.enter_context(nc.sbuf_tensor("wd", [128, 22, D], BF16))
        wgu_b, wd_b = Buf("wgu"), Buf("wd")
        load_weight_bf16(P, nc, wgu, wgu_b, w_gate_up, 0, 2 * FFN_H, 8)
        load_weight_bf16(P, nc, wd, wd_b, w_down, 0, D, 22)
        nm = Normer(P, nc, es, C, ffn_norm, "fn")
        hbuf = es.enter_context(nc.sbuf_tensor("hbuf", [128, 2, D], F32))
        hr = Ring([(i, Buf()) for i in range(2)])
        hnT = es.enter_context(nc.sbuf_tensor("hnT", [128, 2, 8, 512], BF16))
        hnb = [Buf(), Buf()]
        actT = es.enter_context(nc.sbuf_tensor("actT", [128, 22, 512], BF16))
        actb = [Buf() for _ in range(22)]
        sg = es.enter_context(nc.sbuf_tensor("sg", [128, 2, 512], F32))
        sgr = Ring([(i, Buf()) for i in range(2)])
        ost = es.enter_context(nc.sbuf_tensor("ost", [128, 2, D], F32))
        ostr = Ring([(i, Buf()) for i in range(2)])
        pg = es.enter_context(nc.psum_tensor("pg", [128, 2, 512], F32))
        pu = es.enter_context(nc.psum_tensor("pu", [128, 2, 512], F32))
        po = es.enter_context(nc.psum_tensor("po", [128, 2, 512], F32))
        pgr = Ring([(i, Buf()) for i in range(2)])
        pur = Ring([(i, Buf()) for i in range(2)])
        por = Ring([(i, Buf()) for i in range(2)])
        hview = h_dram.rearrange("(t p) d -> t p d", p=128)
        oview = out_dram.rearrange("(t p) d -> t p d", p=128)
        for c in range(S // 512):
            cb = c % 2
            for j in range(4):
                t = c * 4 + j
                hi, hb_ = hr.next()
                P.dma("sp", lambda e, t=t, hi=hi: e.dma_start(out=hbuf[:, hi, :], in_=hview[t]), writes=[hb_])
                nm.run(hbuf[:, hi, :], hb_, hnT[:, cb], hnb[cb], j * 128)
            for m in range(22):
                gi, gbuf = pgr.next()
                ui, ubuf = pur.next()

                def mm(e, m=m, gi=gi, ui=ui, cb=cb):
                    for kc in range(8):
                        e.matmul(pg[:, gi, :], lhsT=wgu[:, kc, m * 128:(m + 1) * 128], rhs=hnT[:, cb, kc, :],
                                 start=(kc == 0), stop=(kc == 7))
                    ins = None
                    for kc in range(8):
                        ins = e.matmul(pu[:, ui, :], lhsT=wgu[:, kc, FFN_H + m * 128:FFN_H + (m + 1) * 128],
                                       rhs=hnT[:, cb, kc, :], start=(kc == 0), stop=(kc == 7))
                    return ins

                P.op("pe", mm, reads=[wgu_b, hnb[cb]], writes=[gbuf, ubuf])
                si, sbuf_ = sgr.next()
                P.op("act", lambda e, gi=gi, si=si: e.activation(out=sg[:, si, :], in_=pg[:, gi, :], func=AF.Silu),
                     reads=[gbuf], writes=[sbuf_])
                P.op("dve", lambda e, ui=ui, si=si, m=m: e.tensor_tensor(out=actT[:, m, :], in0=pu[:, ui, :],
                                                                         in1=sg[:, si, :], op=ALU.mult),
                     reads=[ubuf, sbuf_], writes=[actb[m]])
            for j in range(4):
                t = c * 4 + j
                oi, obuf = ostr.next()
                P.dma("sp", lambda e, t=t, oi=oi: e.dma_start(out=ost[:, oi, :], in_=hview[t]), writes=[obuf])
                for nh in range(2):
                    pi, pbuf = por.next()

                    def mmd(e, j=j, nh=nh, pi=pi):
                        ins = None
                        for m in range(22):
                            ins = e.matmul(po[:, pi, :], lhsT=actT[:, m, j * 128:(j + 1) * 128],
                                           rhs=wd[:, m, nh * 512:(nh + 1) * 512], start=(m == 0), stop=(m == 21))
                        return ins

                    P.op("pe", mmd, reads=actb + [wd_b], writes=[pbuf])
                    P.op("dve", lambda e, nh=nh, pi=pi, oi=oi: e.tensor_tensor(
                        out=ost[:, oi, nh * 512:(nh + 1) * 512], in0=po[:, pi, :],
                        in1=ost[:, oi, nh * 512:(nh + 1) * 512], op=ALU.add),
                        reads=[pbuf], writes=[obuf])
                final_ops.append(P.dma("sp", lambda e, t=t, oi=oi: e.dma_start(out=oview[t], in_=ost[:, oi, :]),
                                       reads=[obuf]))
    P.barrier()


def phase_merge(P, nc, C, x_dram, ya_dram, yr_dram, h_dram, mix_norm, w_in, w_ba, w_br, w_out):
    with ExitStack() as es:
        wg = es.enter_context(nc.sbuf_tensor("wg", [128, 8, 2 * D], BF16))
        wba = es.enter_context(nc.sbuf_tensor("wba", [128, 4, D], BF16))
        wbr = es.enter_context(nc.sbuf_tensor("wbr", [128, 4, D], BF16))
        wo = es.enter_context(nc.sbuf_tensor("wo", [128, 8, D], BF16))
        wg_b, wba_b, wbr_b, wo_b = Buf(), Buf(), Buf(), Buf()
        load_weight_bf16(P, nc, wg, wg_b, w_in, ATTN_IN + RWKV_IN, ATTN_IN + RWKV_IN + 2 * D, 8)
        load_weight_bf16(P, nc, wba, wba_b, w_ba, 0, D, 4)
        load_weight_bf16(P, nc, wbr, wbr_b, w_br, 0, D, 4)
        load_weight_bf16(P, nc, wo, wo_b, w_out, 0, D, 8)
        nm = Normer(P, nc, es, C, mix_norm, "mn")
        xbuf = es.enter_context(nc.sbuf_tensor("xbuf", [128, 2, 4, D], F32))
        xb = [[Buf() for _ in range(4)] for _ in range(2)]
        xnT = es.enter_context(nc.sbuf_tensor("xnT", [128, 2, 8, 512], BF16))
        xnb = [Buf(), Buf()]
        yaT = es.enter_context(nc.sbuf_tensor("yaT", [128, 2, 4, 512], BF16))
        yrT = es.enter_context(nc.sbuf_tensor("yrT", [128, 2, 4, 512], BF16))
        yab, yrb = [Buf(), Buf()], [Buf(), Buf()]
        mT = es.enter_context(nc.sbuf_tensor("mT", [128, 8, 512], BF16))
        mb = [Buf() for _ in range(8)]
        sg = es.enter_context(nc.sbuf_tensor("sgm", [128, 2, 2, 512], F32))
        sgr = Ring([(i, Buf()) for i in range(2)])
        tt = es.enter_context(nc.sbuf_tensor("ttm", [128, 2, 2, 512], F32))
        ttr = Ring([(i, Buf()) for i in range(2)])
        hst = es.enter_context(nc.sbuf_tensor("hst", [128, 2, D], F32))
        hstr = Ring([(i, Buf()) for i in range(2)])
        pga = es.enter_context(nc.psum_tensor("pga", [128, 2, 512], F32))
        pbr = es.enter_context(nc.psum_tensor("pbr", [128, 2, 512], F32))
        po = es.enter_context(nc.psum_tensor("pom", [128, 2, 512], F32))
        pgb, pbb = Buf(), Buf()
        por = Ring([(i, Buf()) for i in range(2)])
        xview = x_dram.rearrange("(t p) d -> t p d", p=128)
        hview = h_dram.rearrange("(t p) d -> t p d", p=128)
        yav = ya_dram.rearrange("(kc p) s -> p kc s", p=128)
        yrv = yr_dram.rearrange("(kc p) s -> p kc s", p=128)
        for c in range(S // 512):
            cb = c % 2
            P.dma("sp", lambda e, c=c, cb=cb: e.dma_start(out=yaT[:, cb], in_=yav[:, :, c * 512:(c + 1) * 512]),
                  writes=[yab[cb]])
            P.dma("sp", lambda e, c=c, cb=cb: e.dma_start(out=yrT[:, cb], in_=yrv[:, :, c * 512:(c + 1) * 512]),
                  writes=[yrb[cb]])
            for j in range(4):
                t = c * 4 + j
                P.dma("sp", lambda e, t=t, cb=cb, j=j: e.dma_start(out=xbuf[:, cb, j, :], in_=xview[t]),
                      writes=[xb[cb][j]])
                nm.run(xbuf[:, cb, j, :], xb[cb][j], xnT[:, cb], xnb[cb], j * 128)
            for m in range(8):
                def mmg(e, m=m, cb=cb):
                    ins = None
                    for g in range(2):
                        for kc in range(8):
                            ins = e.matmul(pga[:, g, :], lhsT=wg[:, kc, g * D + m * 128:g * D + (m + 1) * 128],
                                           rhs=xnT[:, cb, kc, :], start=(kc == 0), stop=(kc == 7))
                    return ins

                P.op("pe", mmg, reads=[wg_b, xnb[cb]], writes=[pgb])

                def mmb(e, m=m, cb=cb):
                    ins = None
                    for kc in range(4):
                        ins = e.matmul(pbr[:, 0, :], lhsT=wba[:, kc, m * 128:(m + 1) * 128], rhs=yaT[:, cb, kc, :],
                                       start=(kc == 0), stop=(kc == 3))
                    for kc in range(4):
                        ins = e.matmul(pbr[:, 1, :], lhsT=wbr[:, kc, m * 128:(m + 1) * 128], rhs=yrT[:, cb, kc, :],
                                       start=(kc == 0), stop=(kc == 3))
                    return ins

                P.op("pe", mmb, reads=[wba_b, wbr_b, yab[cb], yrb[cb]], writes=[pbb])
                si, sbuf_ = sgr.next()
                P.op("act", lambda e, si=si: e.activation(out=sg[:, si], in_=pga[:, :, :], func=AF.Sigmoid),
                     reads=[pgb], writes=[sbuf_])
                ti, tbuf = ttr.next()
                P.op("dve", lambda e, si=si, ti=ti: e.tensor_tensor(out=tt[:, ti], in0=pbr[:, :, :], in1=sg[:, si],
                                                                    op=ALU.mult),
                     reads=[pbb, sbuf_], writes=[tbuf])
                P.op("pool", lambda e, ti=ti, m=m: e.tensor_tensor(out=mT[:, m, :], in0=tt[:, ti, 0, :],
                                                                   in1=tt[:, ti, 1, :], op=ALU.add),
                     reads=[tbuf], writes=[mb[m]])
            for j in range(4):
                t = c * 4 + j
                hi, hbuf_ = hstr.next()
                for nh in range(2):
                    pi, pbuf = por.next()

                    def mmo(e, j=j, nh=nh, pi=pi):
                        ins = None
                        for m in range(8):
                            ins = e.matmul(po[:, pi, :], lhsT=mT[:, m, j * 128:(j + 1) * 128],
                                           rhs=wo[:, m, nh * 512:(nh + 1) * 512], start=(m == 0), stop=(m == 7))
                        return ins

                    P.op("pe", mmo, reads=mb + [wo_b], writes=[pbuf])
                    P.op("dve", lambda e, j=j, nh=nh, pi=pi, hi=hi, cb=cb: e.tensor_tensor(
                        out=hst[:, hi, nh * 512:(nh + 1) * 512], in0=po[:, pi, :],
                        in1=xbuf[:, cb, j, nh * 512:(nh + 1) * 512], op=ALU.add),
                        reads=[pbuf, xb[cb][j]], writes=[hbuf_])
                P.dma("sp", lambda e, t=t, hi=hi: e.dma_start(out=hview[t], in_=hst[:, hi, :]), reads=[hbuf_])
    P.barrier()


TL = 128
C0 = math.exp(-0.5)


def col8(P, nc, es, name, v_ap):
    t = es.enter_context(nc.sbuf_tensor(name, [64, 8], F32))
    b = Buf(name)
    P.dma("sp", lambda e: e.dma_start(out=t[:, :], in_=v_ap.rearrange("o (h k) -> k (o h)", k=64),
                                      allow_slow_non_contiguous=True), writes=[b])
    return t, b


def phase_rwkv(P, nc, C, A, yr_dram):
    x_dram = A["x"]
    with ExitStack() as es:
        sb = lambda name, shape, dt=F32: es.enter_context(nc.sbuf_tensor(name, shape, dt))
        wr = sb("wr", [128, 8, RWKV_IN], BF16)
        wmu = sb("wmu", [128, 8, RWKV_IN], BF16)
        wr_b, wmu_b, mub_b = Buf(), Buf(), Buf()
        load_weight_bf16(P, nc, wr, wr_b, A["w_in"], ATTN_IN, ATTN_IN + RWKV_IN, 8)
        with nc.sbuf_tensor("mub", [128, RWKV_IN], F32) as mub:
            P.dma("sp", lambda e: e.dma_start(out=mub[:], in_=A["rwkv_mu"].partition_broadcast(128)), writes=[mub_b])
            for kc in range(8):
                P.op("pool", lambda e, kc=kc: e.tensor_tensor(out=wmu[:, kc, :], in0=wr[:, kc, :], in1=mub[:],
                                                              op=ALU.mult), reads=[wr_b, mub_b], writes=[wmu_b])
        P.barrier()
        w2 = sb("w2", [64, 512], BF16)
        a2 = sb("a2", [64, 512], BF16)
        g2 = sb("g2", [64, 3, 512], BF16)
        lw_b = Buf()
        P.dma("pool", lambda e: e.dma_start(out=w2[:], in_=A["rwkv_w2"]), writes=[lw_b])
        P.dma("pool", lambda e: e.dma_start(out=a2[:], in_=A["rwkv_a2"]), writes=[lw_b])
        P.dma("pool", lambda e: e.dma_start(out=g2[:, 0, :], in_=A["rwkv_g2"][0:64, :]), writes=[lw_b])
        P.dma("pool", lambda e: e.dma_start(out=g2[:, 1, :], in_=A["rwkv_g2"][64:128, :]), writes=[lw_b])
        P.dma("pool", lambda e: e.dma_start(out=g2[0:32, 2, :], in_=A["rwkv_g2"][128:160, :]), writes=[lw_b])
        cw0, b_w0 = col8(P, nc, es, "cw0", A["rwkv_w0"])
        ca0, b_a0 = col8(P, nc, es, "ca0", A["rwkv_a0"])
        ckk, b_kk = col8(P, nc, es, "ckk", A["rwkv_k_k"])
        cka, b_ka = col8(P, nc, es, "cka", A["rwkv_k_a"])
        crk, b_rk = col8(P, nc, es, "crk", A["rwkv_r_k"])
        clw, b_lw = col8(P, nc, es, "clw", A["rwkv_ln_w"])
        clb, b_lb = col8(P, nc, es, "clb", A["rwkv_ln_b"])
        msk = sb("rmsk", [64, 3, 64], F32)
        ones_bf = sb("ones_bf", [64, 64], BF16)
        rmask = sb("rmask", [64, 8, TL // 64, 64], F32)
        mk_b = Buf()

        def mkmasks(e):
            e.memset(msk[:], 1.0)
            e.memset(ones_bf[:], 1.0)
            e.memset(rmask[:], 1.0)
            e.memset(rmask[:, :, :, 0:1], 0.0)
            e.affine_select(out=msk[:, 0, :], in_=msk[:, 0, :], pattern=[[1, 64]], compare_op=ALU.is_ge,
                            fill=0.0, base=-1, channel_multiplier=-1)
            e.affine_select(out=msk[:, 1, :], in_=msk[:, 1, :], pattern=[[1, 64]], compare_op=ALU.is_ge,
                            fill=0.0, base=0, channel_multiplier=-1)
            return e.affine_select(out=msk[:, 2, :], in_=msk[:, 2, :], pattern=[[-1, 64]], compare_op=ALU.is_ge,
                                   fill=0.0, base=-1, channel_multiplier=1)

        P.op("pool", mkmasks, writes=[mk_b])
        mb3 = lambda i: msk[:, i, :].unsqueeze(1).broadcast_to([64, 8, 64])
        idb = C.ident_bf[0:64, 0:64]
        idf3 = C.ident_f[0:64, 0:64].unsqueeze(1).broadcast_to([64, 8, 64])
        bc = lambda col: col[:, :].unsqueeze(2).broadcast_to([64, 8, TL])

        nm = Normer(P, nc, es, C, A["mix_norm"], "rn", nslots=1)
        xbuf = sb("rxbuf", [128, 1, D], F32)
        xr = Ring([(i, Buf()) for i in range(1)])
        xnx = sb("xnx", [128, 2, 8, TL + 1], BF16)
        xnb = [Buf(), Buf()]
        dxn = sb("dxn", [128, 8, TL], BF16)
        dxb = Buf()
        P.op("pool", lambda e: e.memset(xnx[:, 1, :, TL:TL + 1], 0.0), writes=[xnb[1]])
        F = {}
        FB = {}
        for nme, dt in [("r", F32), ("k", F32), ("sgd", F32), ("a", F32), ("kk", F32),
                        ("t1", F32), ("t2", F32), ("cum", F32), ("e1", F32), ("e2", F32), ("sqb", BF16)]:
            alias = {"e1": "t1", "e2": "a"}
            if nme in alias:
                F[nme] = F[alias[nme]]
                FB[nme] = FB[alias[nme]]
                continue
            F[nme] = sb("f_" + nme, [64, 8, TL], dt)
            FB[nme] = Buf(nme)
        X2 = {}
        X2B = {}
        for nme in ("rT", "aT", "bT", "kT", "bH", "kH", "vb", "g", "bon"):
            X2[nme] = sb("x_" + nme, [64, 2, 8, TL], BF16)
            X2B[nme] = [Buf(nme + "0"), Buf(nme + "1")]
        lora = sb("lora", [64, 5, TL], BF16)
        ztok = sb("ztok", [128, 2, 512], F32)
        ztr = Ring([(i, Buf()) for i in range(2)])
        lora_b = Buf()
        gC = sb("gC", [64, 2, 8, TL // 64], F32)
        gC_bs = [Buf(), Buf()]
        Sf = sb("Sf", [64, 8, 64], F32)
        Sb = sb("Sb", [64, 8, 64], BF16)
        St = sb("St", [64, 8, 64], F32)
        S_b, St_b = Buf(), Buf()
        P.op("dve", lambda e: e.memset(Sf[:], 0.0), writes=[S_b])
        P.op("dve", lambda e: e.memset(Sb[:], 0.0), reads=[S_b], writes=[S_b])
        ost = sb("rost", [64, 8, TL], BF16)
        ost_b = Buf()
        def pair(name, dt=BF16, n=2):
            t = sb(name, [64, n, 8, 64], dt)
            return t, [Buf() for _ in range(n)]
        Atok, Atok_b = pair("Atok")
        BHtok, BHtok_b = pair("BHtok")
        KHtok, KHtok_b = pair("KHtok")
        Vtok, Vtok_b = pair("Vtok")
        Mrb, Mrb_b = pair("Mrb")
        Mrk, Mrk_b = pair("Mrk")
        Lak, Lak_b = pair("Lak")
        Nn, Nn_b = pair("Nn", BF16, 4)
        Mm, Mm_b = pair("Mm", BF16, 4)
        Qq, Qq_b = pair("Qq", BF16, 4)
        WT, WT_b = pair("WT")
        Xx, Xx_b = pair("Xx")
        Uu, Uu_b = pair("Uu")
        Ys, Ys_b = pair("Ys", F32, 1)
        Yn, Yn_b = pair("Yn", BF16, 2)
        gst = sb("gst", [64, 2, 8, 6], F32)
        gst_b = [Buf(), Buf()]
        pb = es.enter_context(nc.psum_tensor("rpb", [128, 7, 512], F32))
        pr = Ring([(i, Buf()) for i in range(7)])
        pv = lambda i: pb[0:64, i, :].rearrange("p (h t) -> p h t", h=8)
        pvb = lambda i: pb[0:64, i, :].bitcast(BF16)[:, 0:512].rearrange("p (h t) -> p h t", h=8)

        xview = x_dram.rearrange("(t p) d -> t p d", p=128)

        def mm8(bank, parts, rd):
            i, bbuf = bank
            ops = [[(lf(h), rf(h)) for (lf, rf) in parts] for h in range(8)]

            def th(e):
                ins = None
                for h in range(8):
                    for pi, (l, r) in enumerate(ops[h]):
                        ins = e.matmul(pb[0:64, i, h * 64:(h + 1) * 64], lhsT=l, rhs=r,
                                       start=(pi == 0), stop=(pi == len(ops[h]) - 1))
                return ins

            P.op("pe", th, reads=rd, writes=[bbuf])

        def tr8(bank, src_fn, rd):
            i, bbuf = bank
            srcs = [src_fn(h) for h in range(8)]

            def th(e):
                ins = None
                v = pvb(i)
                for h in range(8):
                    ins = e.transpose(out=v[:, h, :], in_=srcs[h], identity=idb)
                return ins

            P.op("pe", th, reads=rd + [C.b], writes=[bbuf])

        NB = S // TL

        def prep(n):
            par = n % 2
            cb = n % 2
            pc = 1 - cb
            X = {k: X2[k][:, par] for k in X2}
            XB = {k: X2B[k][par] for k in X2}
            P.op("pool", lambda e: e.tensor_copy(out=xnx[:, cb, :, 0:1], in_=xnx[:, pc, :, TL:TL + 1]),
                 reads=[xnb[pc]], writes=[xnb[cb]])
            for j in range(TL // 128):
                xi, xb_ = xr.next()
                P.dma("sp", lambda e, t=n * (TL // 128) + j, xi=xi: e.dma_start(out=xbuf[:, xi, :], in_=xview[t]), writes=[xb_])
                nm.run(xbuf[:, xi, :], xb_, xnx[:, cb], xnb[cb], 1 + j * 128)
            P.op("pool", lambda e: e.tensor_tensor(out=dxn[:], in0=xnx[:, cb, :, 0:TL], in1=xnx[:, cb, :, 1:TL + 1],
                                                   op=ALU.subtract), reads=[xnb[cb]], writes=[dxb])
            yield

            def projtok(c0, ncol):
                bank = pr.next()
                i = bank[0]

                def th(e):
                    ins = None
                    for kc in range(8):
                        e.matmul(pb[:, i, 0:ncol], lhsT=xnx[:, cb, kc, 1:TL + 1], rhs=wr[:, kc, c0:c0 + ncol],
                                 start=(kc == 0), stop=False)
                    for kc in range(8):
                        ins = e.matmul(pb[:, i, 0:ncol], lhsT=dxn[:, kc, :], rhs=wmu[:, kc, c0:c0 + ncol],
                                       start=False, stop=(kc == 7))
                    return ins

                P.op("pe", th, reads=[wr_b, wmu_b, xnb[cb], dxb], writes=[bank[1]])
                zi, zb = ztr.next()
                P.op("act", lambda e: e.activation(out=ztok[:, zi, 0:ncol], in_=pb[:, i, 0:ncol], func=AF.Copy),
                     reads=[bank[1]], writes=[zb])
                return zi, zb

            def trz(zi, zb, cols, m):
                bank = pr.next()
                i = bank[0]

                def th(e):
                    ins = None
                    for q, c in enumerate(cols):
                        ins = e.transpose(out=pb[0:m, i, q * TL:(q + 1) * TL], in_=ztok[:, zi, c:c + m], identity=C.ident_f[:])
                    return ins

                P.op("pe", th, reads=[zb, C.b], writes=[bank[1]])
                return bank

            for qi, qn in enumerate(("r", "k", "v")):
                zi, zb = projtok(qi * 512, 512)
                yield
                for h0 in (0, 4):
                    bank = trz(zi, zb, [(h0 + q) * 64 for q in range(4)], 64)
                    dst, dstb = (X["vb"], XB["vb"]) if qn == "v" else (F[qn], FB[qn])
                    P.op("act", lambda e, dst=dst, h0=h0, i=bank[0]: e.activation(
                        out=dst[:, h0:h0 + 4, :], in_=pb[0:64, i, 0:4 * TL].rearrange("p (q t) -> p q t", q=4), func=AF.Copy),
                        reads=[bank[1]], writes=[dstb])
                    yield
            zi, zb = projtok(1536, 288)
            yield
            for li, (c0, m, fn) in enumerate([(0, 64, AF.Tanh), (64, 64, AF.Copy), (128, 64, AF.Sigmoid),
                                              (192, 64, AF.Sigmoid), (256, 32, AF.Sigmoid)]):
                bank = trz(zi, zb, [c0], m)
                P.op("act", lambda e, li=li, m=m, fn=fn, i=bank[0]: e.activation(out=lora[0:m, li, :], in_=pb[0:m, i, 0:TL],
                                                                                 func=fn),
                     reads=[bank[1]], writes=[lora_b])
                yield
            for h in range(8):
                bank = pr.next()
                P.op("pe", lambda e, h=h, i=bank[0]: e.matmul(pb[0:64, i, 0:TL], lhsT=w2[:, h * 64:(h + 1) * 64],
                                                              rhs=lora[:, 0, :], start=True, stop=True),
                     reads=[lw_b, lora_b], writes=[bank[1]])
                P.op("act", lambda e, h=h, i=bank[0]: e.activation(out=F["sgd"][:, h, :], in_=pb[0:64, i, 0:TL],
                                                                   func=AF.Sigmoid, bias=cw0[:, h:h + 1]),
                     reads=[bank[1], b_w0], writes=[FB["sgd"]])
                bank = pr.next()
                P.op("pe", lambda e, h=h, i=bank[0]: e.matmul(pb[0:64, i, 0:TL], lhsT=a2[:, h * 64:(h + 1) * 64],
                                                              rhs=lora[:, 1, :], start=True, stop=True),
                     reads=[lw_b, lora_b], writes=[bank[1]])
                P.op("act", lambda e, h=h, i=bank[0]: e.activation(out=F["a"][:, h, :], in_=pb[0:64, i, 0:TL],
                                                                   func=AF.Sigmoid, bias=ca0[:, h:h + 1]),
                     reads=[bank[1], b_a0], writes=[FB["a"]])
                bank = pr.next()

                def gmm(e, h=h, i=bank[0]):
                    e.matmul(pb[0:64, i, 0:TL], lhsT=g2[:, 0, h * 64:(h + 1) * 64], rhs=lora[:, 2, :], start=True, stop=False)
                    e.matmul(pb[0:64, i, 0:TL], lhsT=g2[:, 1, h * 64:(h + 1) * 64], rhs=lora[:, 3, :], start=False, stop=False)
                    return e.matmul(pb[0:64, i, 0:TL], lhsT=g2[0:32, 2, h * 64:(h + 1) * 64], rhs=lora[0:32, 4, :],
                                    start=False, stop=True)

                P.op("pe", gmm, reads=[lw_b, lora_b], writes=[bank[1]])
                P.op("act", lambda e, h=h, i=bank[0]: e.activation(out=X["g"][:, h, :], in_=pb[0:64, i, 0:TL], func=AF.Copy),
                     reads=[bank[1]], writes=[XB["g"]])
                yield
            P.op("dve", lambda e: e.tensor_tensor(out=F["kk"][:], in0=F["k"][:], in1=bc(ckk), op=ALU.mult),
                 reads=[FB["k"], b_kk], writes=[FB["kk"]])
            P.op("pool", lambda e: e.tensor_tensor(out=F["sqb"][:], in0=F["kk"][:], in1=F["kk"][:], op=ALU.mult),
                 reads=[FB["kk"]], writes=[FB["sqb"]])
            yield
            for h in range(8):
                bank = pr.next()
                P.op("pe", lambda e, h=h, i=bank[0]: e.matmul(pb[0:64, i, 0:TL], lhsT=ones_bf[:], rhs=F["sqb"][:, h, :],
                                                              start=True, stop=True),
                     reads=[mk_b, FB["sqb"]], writes=[bank[1]])
                P.op("act", lambda e, h=h, i=bank[0]: e.activation(out=F["t1"][:, h, :], in_=pb[0:64, i, 0:TL], func=AF.Sqrt),
                     reads=[bank[1]], writes=[FB["t1"]])
                if h % 2 == 1:
                    yield
            P.op("dve", lambda e: e.tensor_scalar(out=F["t1"][:], in0=F["t1"][:], scalar1=1e-12, scalar2=None,
                                                  op0=ALU.max), reads=[FB["t1"]], writes=[FB["t1"]])
            P.op("dve", lambda e: e.reciprocal(out=F["t1"][:], in_=F["t1"][:]), reads=[FB["t1"]], writes=[FB["t1"]])
            yield
            P.op("dve", lambda e: e.tensor_tensor(out=F["kk"][:], in0=F["kk"][:], in1=F["t1"][:], op=ALU.mult),
                 reads=[FB["kk"], FB["t1"]], writes=[FB["kk"]])
            P.op("dve", lambda e: e.scalar_tensor_tensor(out=F["t2"][:], in0=F["a"][:], scalar=-1.0, in1=bc(cka),
                                                         op0=ALU.add, op1=ALU.mult),
                 reads=[FB["a"], b_ka], writes=[FB["t2"]])
            yield
            P.op("dve", lambda e: e.scalar_tensor_tensor(out=F["k"][:], in0=F["t2"][:], scalar=1.0, in1=F["k"][:],
                                                         op0=ALU.add, op1=ALU.mult),
                 reads=[FB["t2"], FB["k"]], writes=[FB["k"]])
            P.op("pool", lambda e: e.tensor_tensor(out=F["t2"][:], in0=F["kk"][:], in1=F["a"][:], op=ALU.mult),
                 reads=[FB["kk"], FB["a"]], writes=[FB["t2"]])
            yield
            P.op("dve", lambda e: e.tensor_tensor_scan(out=F["cum"][:].rearrange("p h t -> p (h t)"),
                                                       data0=rmask[:].rearrange("p h c t -> p (h c t)"),
                                                       data1=F["sgd"][:].rearrange("p h t -> p (h t)"),
                                                       initial=0.0, op0=ALU.mult, op1=ALU.add),
                 reads=[FB["sgd"], mk_b], writes=[FB["cum"]])
            cum4 = F["cum"][:].rearrange("p h (c t) -> p h c t", t=64)
            yield
            P.op("act", lambda e: e.activation(out=F["e1"][:], in_=F["cum"][:], func=AF.Exp, scale=-C0),
                 reads=[FB["cum"]], writes=[FB["e1"]])
            P.op("dve", lambda e: e.tensor_tensor(out=X["rT"][:], in0=F["r"][:], in1=F["e1"][:], op=ALU.mult),
                 reads=[FB["r"], FB["e1"]], writes=[XB["rT"]])
            P.op("act", lambda e: e.activation(out=gC[:, par], in_=cum4[:, :, :, 63], func=AF.Exp, scale=-C0),
                 reads=[FB["cum"]], writes=[gC_bs[par]])
            yield
            P.op("pool", lambda e: e.tensor_tensor(out=F["e2"][:], in0=F["cum"][:], in1=F["sgd"][:], op=ALU.subtract),
                 reads=[FB["cum"], FB["sgd"]], writes=[FB["e2"]])
            P.op("act", lambda e: e.activation(out=F["e2"][:], in_=F["e2"][:], func=AF.Exp, scale=-C0),
                 reads=[FB["e2"]], writes=[FB["e2"]])
            P.op("dve", lambda e: e.scalar_tensor_tensor(out=X["aT"][:], in0=F["kk"][:], scalar=-1.0, in1=F["e2"][:],
                                                         op0=ALU.mult, op1=ALU.mult),
                 reads=[FB["kk"], FB["e2"]], writes=[XB["aT"]])
            yield
            P.op("act", lambda e: e.activation(out=F["e1"][:], in_=F["cum"][:], func=AF.Exp, scale=C0),
                 reads=[FB["cum"]], writes=[FB["e1"]])
            P.op("dve", lambda e: e.tensor_tensor(out=X["bT"][:], in0=F["t2"][:], in1=F["e1"][:], op=ALU.mult),
                 reads=[FB["t2"], FB["e1"]], writes=[XB["bT"]])
            P.op("pool", lambda e: e.tensor_tensor(out=X["kT"][:], in0=F["k"][:], in1=F["e1"][:], op=ALU.mult),
                 reads=[FB["k"], FB["e1"]], writes=[XB["kT"]])
            yield
            P.op("dve", lambda e: e.tensor_tensor(out=F["e2"][:].rearrange("p h (c t) -> p h c t", t=64),
                                                  in0=cum4[:, :, :, 63:64].broadcast_to([64, 8, TL // 64, 64]), in1=cum4,
                                                  op=ALU.subtract),
                 reads=[FB["cum"]], writes=[FB["e2"]])
            P.op("act", lambda e: e.activation(out=F["e2"][:], in_=F["e2"][:], func=AF.Exp, scale=-C0),
                 reads=[FB["e2"]], writes=[FB["e2"]])
            yield
            P.op("dve", lambda e: e.tensor_tensor(out=X["bH"][:], in0=F["t2"][:], in1=F["e2"][:], op=ALU.mult),
                 reads=[FB["t2"], FB["e2"]], writes=[XB["bH"]])
            P.op("pool", lambda e: e.tensor_tensor(out=X["kH"][:], in0=F["k"][:], in1=F["e2"][:], op=ALU.mult),
                 reads=[FB["k"], FB["e2"]], writes=[XB["kH"]])
            yield
            P.op("dve", lambda e: e.tensor_tensor(out=F["t1"][:], in0=F["r"][:], in1=F["k"][:], op=ALU.mult),
                 reads=[FB["r"], FB["k"]], writes=[FB["t1"]])
            P.op("pool", lambda e: e.tensor_tensor(out=F["sqb"][:], in0=F["t1"][:], in1=bc(crk), op=ALU.mult),
                 reads=[FB["t1"], b_rk], writes=[FB["sqb"]])
            yield
            for h in range(8):
                bank = pr.next()
                P.op("pe", lambda e, h=h, i=bank[0]: e.matmul(pb[0:64, i, 0:TL], lhsT=ones_bf[:], rhs=F["sqb"][:, h, :],
                                                              start=True, stop=True),
                     reads=[mk_b, FB["sqb"]], writes=[bank[1]])
                P.op("dve", lambda e, h=h, i=bank[0]: e.tensor_tensor(out=X["bon"][:, h, :], in0=pb[0:64, i, 0:TL],
                                                                      in1=X["vb"][:, h, :], op=ALU.mult),
                     reads=[bank[1], XB["vb"]], writes=[XB["bon"]])
                if h % 2 == 1:
                    yield

        def chunk_pre(n, c, out):
            par = n % 2
            X = {k: X2[k][:, par] for k in X2}
            XB = {k: X2B[k][par] for k in X2}
            cs = slice(c * 64, (c + 1) * 64)
            for (dst, dbs, src) in ((Atok, Atok_b, "aT"), (BHtok, BHtok_b, "bH"), (KHtok, KHtok_b, "kH"), (Vtok, Vtok_b, "vb")):
                bank = pr.next()
                tr8(bank, lambda h, src=src: X[src][:, h, cs], [XB[src]])
                P.op("act", lambda e, dst=dst, i=bank[0]: e.activation(out=dst[:, c], in_=pvb(i), func=AF.Copy),
                     reads=[bank[1]], writes=[dbs[c]])
                yield

            def gmat(lname, rname, mi, dst, dbs, slot):
                bank = pr.next()
                mm8(bank, [(lambda h: X[lname][:, h, cs], lambda h: X[rname][:, h, cs])], [XB[lname], XB[rname]])
                P.op("dve", lambda e, i=bank[0]: e.tensor_tensor(out=dst[:, slot], in0=pv(i), in1=mb3(mi), op=ALU.mult),
                     reads=[bank[1], mk_b], writes=[dbs[slot]])

            base = 2 * c
            gmat("bT", "aT", 0, Mm, Mm_b, base)
            yield
            gmat("bT", "rT", 1, Mrb, Mrb_b, c)
            yield
            gmat("kT", "aT", 0, Lak, Lak_b, c)
            yield
            gmat("kT", "rT", 1, Mrk, Mrk_b, c)
            yield
            gmat("aT", "bT", 2, Nn, Nn_b, base)
            yield
            P.op("pool", lambda e: e.tensor_tensor(out=Qq[:, base], in0=Mm[:, base], in1=idf3, op=ALU.add),
                 reads=[Mm_b[base], C.b], writes=[Qq_b[base]])
            ni = mi_ = qi_ = base
            for lvl in range(1, 6):
                nn_ = base + (1 - (ni - base))
                nm_ = base + (1 - (mi_ - base))
                nq_ = base + (1 - (qi_ - base))
                bank = pr.next()
                mm8(bank, [(lambda h: Mm[:, mi_, h, :], lambda h: Nn[:, ni, h, :])], [Mm_b[mi_], Nn_b[ni]])
                if lvl < 5:
                    bank2 = pr.next()
                    mm8(bank2, [(lambda h: Nn[:, ni, h, :], lambda h: Mm[:, mi_, h, :])], [Mm_b[mi_], Nn_b[ni]])
                P.op("act", lambda e, nn_=nn_, i=bank[0]: e.activation(out=Nn[:, nn_], in_=pv(i), func=AF.Copy),
                     reads=[bank[1]], writes=[Nn_b[nn_]])
                if lvl < 5:
                    P.op("dve", lambda e, nm_=nm_, i=bank2[0]: e.tensor_copy(out=Mm[:, nm_], in_=pv(i)),
                         reads=[bank2[1]], writes=[Mm_b[nm_]])
                    mi_ = nm_
                ni = nn_
                yield
                bank3 = pr.next()
                mm8(bank3, [(lambda h: Nn[:, ni, h, :], lambda h: Qq[:, qi_, h, :])], [Qq_b[qi_], Nn_b[ni]])
                P.op("dve", lambda e, nq_=nq_, qo=qi_, i=bank3[0]: e.tensor_tensor(out=Qq[:, nq_], in0=pv(i), in1=Qq[:, qo],
                                                                                   op=ALU.add),
                     reads=[bank3[1], Qq_b[qi_]], writes=[Qq_b[nq_]])
                qi_ = nq_
                yield
            bank = pr.next()
            mm8(bank, [(lambda h: Atok[:, c, h, :], lambda h: Qq[:, qi_, h, :])], [Atok_b[c], Qq_b[qi_]])
            P.op("act", lambda e, i=bank[0]: e.activation(out=WT[:, c], in_=pv(i), func=AF.Copy),
                 reads=[bank[1]], writes=[WT_b[c]])
            bank = pr.next()
            mm8(bank, [(lambda h: Lak[:, c, h, :], lambda h: Vtok[:, c, h, :])], [Lak_b[c], Vtok_b[c]])
            P.op("dve", lambda e, i=bank[0]: e.tensor_copy(out=Xx[:, c], in_=pv(i)), reads=[bank[1]], writes=[Xx_b[c]])
            out["q"] = qi_
            yield

        def chain(n, c, qi_):
            par = n % 2
            X = {k: X2[k][:, par] for k in X2}
            XB = {k: X2B[k][par] for k in X2}
            cs = slice(c * 64, (c + 1) * 64)
            bank = pr.next()
            mm8(bank, [(lambda h: WT[:, c, h, :], lambda h: Sb[:, h, :]),
                       (lambda h: Qq[:, qi_, h, :], lambda h: Xx[:, c, h, :])], [WT_b[c], S_b, Qq_b[qi_], Xx_b[c]])
            P.op("act", lambda e, i=bank[0]: e.activation(out=Uu[:, c], in_=pv(i), func=AF.Copy),
                 reads=[bank[1]], writes=[Uu_b[c]])
            banky = pr.next()
            mm8(banky, [(lambda h: X["rT"][:, h, cs], lambda h: Sb[:, h, :]),
                        (lambda h: Mrb[:, c, h, :], lambda h: Uu[:, c, h, :]),
                        (lambda h: Mrk[:, c, h, :], lambda h: Vtok[:, c, h, :])],
                [XB["rT"], S_b, Mrb_b[c], Uu_b[c], Mrk_b[c], Vtok_b[c]])
            banks = pr.next()
            mm8(banks, [(lambda h: BHtok[:, c, h, :], lambda h: Uu[:, c, h, :]),
                        (lambda h: KHtok[:, c, h, :], lambda h: Vtok[:, c, h, :])], [BHtok_b[c], Uu_b[c], KHtok_b[c], Vtok_b[c]])
            P.op("dve", lambda e: e.tensor_tensor(out=St[:], in0=Sf[:],
                                                  in1=gC[:, par, :, c:c + 1].broadcast_to([64, 8, 64]), op=ALU.mult),
                 reads=[S_b, gC_bs[par]], writes=[St_b])
            P.op("dve", lambda e, i=banks[0]: e.tensor_tensor(out=Sf[:], in0=pv(i), in1=St[:], op=ALU.add),
                 reads=[banks[1], St_b], writes=[S_b])
            P.op("act", lambda e: e.activation(out=Sb[:], in_=Sf[:], func=AF.Copy), reads=[S_b], writes=[S_b])
            yield
            y_b, yq_b, g_b = Ys_b[0], St_b, gst_b[c]
            P.op("act", lambda e, i=banky[0]: e.activation(out=Ys[:, 0], in_=pv(i), func=AF.Copy),
                 reads=[banky[1]], writes=[y_b])
            P.op("pool", lambda e: e.tensor_tensor(out=St[:], in0=Ys[:, 0], in1=Ys[:, 0], op=ALU.mult),
                 reads=[y_b], writes=[yq_b])
            P.op("dve", lambda e: e.tensor_reduce(out=gst[:, c, :, 0], in_=Ys[:, 0], axis=AX.X, op=ALU.add),
                 reads=[y_b], writes=[g_b])
            P.op("dve", lambda e: e.tensor_reduce(out=gst[:, c, :, 1], in_=St[:], axis=AX.X, op=ALU.add),
                 reads=[yq_b, g_b], writes=[g_b])
            yield
            P.op("dve", lambda e: e.tensor_scalar(out=gst[:, c, :, 2], in0=gst[:, c, :, 0], scalar1=1.0 / 64,
                                                  scalar2=None, op0=ALU.mult), reads=[g_b], writes=[g_b])
            P.op("dve", lambda e: e.tensor_tensor(out=gst[:, c, :, 3], in0=gst[:, c, :, 2], in1=gst[:, c, :, 2],
                                                  op=ALU.mult), reads=[g_b], writes=[g_b])
            P.op("dve", lambda e: e.scalar_tensor_tensor(out=gst[:, c, :, 4], in0=gst[:, c, :, 1], scalar=1.0 / 64,
                                                         in1=gst[:, c, :, 3], op0=ALU.mult, op1=ALU.subtract),
                 reads=[g_b], writes=[g_b])
            yield
            P.op("act", lambda e: e.activation(out=gst[:, c, :, 5], in_=gst[:, c, :, 4], func=AF.Sqrt,
                                               bias=C.eps[0:64, 1:2]), reads=[g_b, C.b], writes=[g_b])
            P.op("dve", lambda e: e.reciprocal(out=gst[:, c, :, 5], in_=gst[:, c, :, 5]), reads=[g_b], writes=[g_b])
            P.op("dve", lambda e: e.tensor_tensor(out=Ys[:, 0], in0=Ys[:, 0],
                                                  in1=gst[:, c, :, 2:3].broadcast_to([64, 8, 64]),
                                                  op=ALU.subtract), reads=[y_b, g_b], writes=[y_b])
            yield
            P.op("dve", lambda e: e.tensor_tensor(out=Yn[:, c], in0=Ys[:, 0],
                                                  in1=gst[:, c, :, 5:6].broadcast_to([64, 8, 64]),
                                                  op=ALU.mult), reads=[y_b, g_b], writes=[Yn_b[c]])
            bank = pr.next()
            tr8(bank, lambda h: Yn[:, c, h, :], [Yn_b[c]])
            bc64 = lambda col: col[:, :].unsqueeze(2).broadcast_to([64, 8, 64])
            P.op("dve", lambda e, i=bank[0]: e.tensor_tensor(out=Ys[:, 0], in0=pvb(i), in1=bc64(clw), op=ALU.mult),
                 reads=[bank[1], b_lw, Yn_b[c]], writes=[y_b])
            P.op("pool", lambda e: e.tensor_tensor(out=Ys[:, 0], in0=Ys[:, 0], in1=bc64(clb), op=ALU.add),
                 reads=[y_b, b_lb], writes=[y_b])
            yield
            P.op("dve", lambda e: e.tensor_tensor(out=Ys[:, 0], in0=Ys[:, 0], in1=X["bon"][:, :, cs], op=ALU.add),
                 reads=[y_b, XB["bon"]], writes=[y_b])
            P.op("dve", lambda e: e.tensor_tensor(out=ost[:, :, cs], in0=Ys[:, 0], in1=X["g"][:, :, cs], op=ALU.mult),
                 reads=[y_b, XB["g"]], writes=[ost_b])
            yield

        def scan(n):
            par = n % 2
            X = {k: X2[k][:, par] for k in X2}
            XB = {k: X2B[k][par] for k in X2}
            outs = [{} for _ in range(TL // 64)]
            gens = [chunk_pre(n, c, outs[c]) for c in range(TL // 64)]
            alive = [True] * len(gens)
            while any(alive):
                for gi_, g_ in enumerate(gens):
                    if alive[gi_]:
                        try:
                            next(g_)
                        except StopIteration:
                            alive[gi_] = False
                yield
            for c in range(TL // 64):
                for _ in chain(n, c, outs[c]["q"]):
                    yield
            P.dma("sp", lambda e: e.dma_start(
                out=yr_dram.rearrange("(h v) s -> v h s", v=64)[:, :, n * TL:(n + 1) * TL], in_=ost[:]),
                reads=[ost_b])
            yield

        def run2(f, b):
            fa, ba = f is not None, b is not None
            while fa or ba:
                if fa:
                    try:
                        next(f)
                    except StopIteration:
                        fa = False
                if ba:
                    try:
                        next(b)
                    except StopIteration:
                        ba = False

        for n in range(NB + 1):
            run2(prep(n) if n < NB else None, scan(n - 1) if n >= 1 else None)
    P.barrier()


NITER = 16


def phase_attn(P, nc, C, A, ya_dram):
    x_dram = A["x"]
    with ExitStack() as es:
        sb = lambda name, shape, dt=F32: es.enter_context(nc.sbuf_tensor(name, shape, dt))
        wa = sb("wa", [128, 8, ATTN_IN + 64], BF16)
        wa_b = Buf()
        load_weight_bf16(P, nc, wa, wa_b, A["w_in"], 0, ATTN_IN, 8)
        wv_ = A["w_in"].rearrange("(kc p) n -> p kc n", p=128)
        for kc in range(8):
            P.dma("pool", lambda e, kc=kc: e.dma_start(out=wa[:, kc, ATTN_IN:ATTN_IN + 64], in_=wv_[:, kc, 2048:2112]),
                  writes=[wa_b])
        kT = sb("kT", [128, 4, S], BF16)
        kiT = sb("kiT", [128, S], BF16)
        vaug = sb("vaug", [128, NT, 8, 65], BF16)
        kT_bs = [Buf() for _ in range(NT)]
        kiT_bs = [Buf() for _ in range(NT)]
        va_bs = [Buf() for _ in range(NT)]
        P.op("pool", lambda e: e.memset(vaug[:, :, :, 64:65], 1.0), writes=va_bs)
        gqk = sb("gqk", [128, 2], F32)
        gqk_b = Buf()
        for half in range(2):
            P.dma("sp", lambda e, half=half: e.dma_start(out=gqk[half * 64:(half + 1) * 64, 0:1],
                                                         in_=A["attn_q_norm"].rearrange("o d -> d o"),
                                                         allow_slow_non_contiguous=True), writes=[gqk_b])
            P.dma("sp", lambda e, half=half: e.dma_start(out=gqk[half * 64:(half + 1) * 64, 1:2],
                                                         in_=A["attn_k_norm"].rearrange("o d -> d o"),
                                                         allow_slow_non_contiguous=True), writes=[gqk_b])
        P.op("dve", lambda e: e.tensor_scalar(out=gqk[:, 0:1], in0=gqk[:, 0:1], scalar1=0.125, scalar2=None, op0=ALU.mult),
             reads=[gqk_b], writes=[gqk_b])
        btf = sb("btf", [128, 8, 2, 128], F32)
        bt = sb("bt", [128, 8, 2, 128], BF16)
        b31 = sb("b31_sb", [128, 8], F32)
        bt_b = Buf()
        P.dma("sp", lambda e: e.dma_start(out=btf[:], in_=A["bias_tiles"].rearrange("h c s t -> s h c t")), writes=[bt_b])
        P.dma("sp", lambda e: e.dma_start(out=b31[:], in_=A["b31"].partition_broadcast(128)), writes=[bt_b])
        P.op("dve", lambda e: e.tensor_tensor(out=bt[:].rearrange("p h c t -> p h (c t)"),
                                              in0=btf[:].rearrange("p h c t -> p h (c t)"),
                                              in1=b31[:, :].unsqueeze(2).broadcast_to([128, 8, 256]), op=ALU.subtract),
             reads=[bt_b], writes=[bt_b])
        cmask = sb("cmask", [128, 128], F32)
        onesblk = sb("onesblk", [128, 128], BF16)
        cm_b = Buf()
        pw2 = sb("pw2", [128, NITER], F32)
        halfs = sb("halfs", [128, NITER], F32)
        hf_b = Buf()

        def mkc(e):
            e.memset(cmask[:], 0.0)
            e.affine_select(out=cmask[:], in_=cmask[:], pattern=[[-1, 128]], compare_op=ALU.is_ge, fill=NEG, base=0,
                            channel_multiplier=1)
            for j in range(NITER):
                e.memset(pw2[:, j:j + 1], 0.5 ** (j + 1))
            e.memset(onesblk[:], 0.0)
            e.memset(onesblk[0:64, 0:64], 1.0)
            return e.memset(onesblk[64:128, 64:128], 1.0)

        P.op("pool", mkc, writes=[cm_b])
        nm = Normer(P, nc, es, C, A["mix_norm"], "an", nslots=1)
        xbuf = sb("axbuf", [128, 2, D], F32)
        xr = Ring([(i, Buf()) for i in range(2)])
        xn = sb("axn", [128, 8, 128], BF16)
        xn_b = Buf()
        q_i = sb("q_i", [128, 2, 4, 128], BF16)
        qi_i = sb("qi_i", [128, 4, 128], BF16)
        wi_i = sb("wi_i", [128, 8], F32)
        q_bs, qi_b, wi_b = [Buf(), Buf()], Buf(), Buf()
        sq = sb("asq", [128, 512], BF16)
        rn = sb("arn", [128, 512], F32)
        sq_b, rn_b = Buf(), Buf()
        sc = sb("sc", [128, S], F32)
        sc_b = Buf()
        rbuf = sb("rbuf", [128, 2, 512], F32)
        rr = Ring([(i, Buf()) for i in range(2)])
        junk = sb("ajunk", [128, S], BF16)
        junk_b = Buf()
        bs = sb("bs", [128, 8], F32)
        bs_b = Buf()
        maskT = sb("maskT", [128, 2, NT, 128], BF16)
        mT_bs = [Buf(), Buf()]
        Et = sb("Et", [128, 3, 4, 128], BF16)
        er = Ring([(i, Buf()) for i in range(3)])
        Pt = sb("Pt", [128, 4, 4, 128], BF16)
        ptr = Ring([(i, Buf()) for i in range(4)])
        rden = sb("rden", [128, 2], F32)
        rdr = Ring([(i, Buf()) for i in range(2)])
        ytile = sb("ytile", [128, 512], BF16)
        yt_b = Buf()
        yst = sb("yst", [128, 4, 128], BF16)
        ys_b = Buf()
        pg = es.enter_context(nc.psum_tensor("apg", [128, 5, 512], F32))
        gr = Ring([(i, Buf()) for i in range(3)])
        lgr = Ring([(i, Buf()) for i in range(3, 5)])
        po = es.enter_context(nc.psum_tensor("apo", [128, 2, 512], F32))
        orr = Ring([(i, Buf()) for i in range(2)])
        pgb = lambda i: pg[:, i, :].bitcast(BF16)
        if SBUF_DEBUG:
            print("attn sbuf remaining", nc.sbuf_bytes_remaining)
        xview = x_dram.rearrange("(t p) d -> t p d", p=128)
        yav = ya_dram.rearrange("(m p) s -> p m s", p=128)

        def front(i):
            ts_ = slice(i * 128, (i + 1) * 128)
            nkb = i + 1
            W = nkb * 128
            q_b = q_bs[i % 2]
            kT_b, kiT_b, va_b = kT_bs[i], kiT_bs[i], va_bs[i]
            xi, xb_ = xr.next()
            P.dma("sp", lambda e, i=i, xi=xi: e.dma_start(out=xbuf[:, xi, :], in_=xview[i]), writes=[xb_])
            nm.run(xbuf[:, xi, :], xb_, xn, xn_b, 0)
            yield

            def proj4(c0, bank):
                bi, bb = bank

                def th(e):
                    ins = None
                    for m in range(4):
                        for kc in range(8):
                            ins = e.matmul(pg[:, bi, m * 128:(m + 1) * 128], lhsT=wa[:, kc, c0 + m * 128:c0 + (m + 1) * 128],
                                           rhs=xn[:, kc, :], start=(kc == 0), stop=(kc == 7))
                    return ins

                P.op("pe", th, reads=[wa_b, xn_b], writes=[bb])

            for which, c0 in ((0, 0), (1, 512)):
                bank = gr.next()
                proj4(c0, bank)
                P.op("act", lambda e, bi=bank[0]: e.activation(out=sq[:], in_=pg[:, bi, :], func=AF.Square),
                     reads=[bank[1]], writes=[sq_b])
                bank2 = gr.next()
                P.op("pe", lambda e, bi=bank2[0]: e.matmul(pg[:, bi, :], lhsT=onesblk[:], rhs=sq[:], start=True, stop=True),
                     reads=[cm_b, sq_b], writes=[bank2[1]])
                P.op("act", lambda e, bi=bank2[0]: e.activation(out=rn[:], in_=pg[:, bi, :], func=AF.Sqrt, scale=1.0 / 64,
                                                                bias=C.eps[:, 0:1]), reads=[bank2[1], C.b], writes=[rn_b])
                P.op("dve", lambda e: e.reciprocal(out=rn[:], in_=rn[:]), reads=[rn_b], writes=[rn_b])
                if which == 0:
                    P.op("dve", lambda e, bi=bank[0]: e.scalar_tensor_tensor(
                        out=q_i[:, i % 2].rearrange("p m t -> p (m t)"), in0=pg[:, bi, :], scalar=gqk[:, 0:1], in1=rn[:],
                        op0=ALU.mult, op1=ALU.mult), reads=[bank[1], rn_b, gqk_b], writes=[q_b])
                else:
                    P.op("dve", lambda e, bi=bank[0], ts_=ts_: e.scalar_tensor_tensor(
                        out=kT[:, :, ts_], in0=pg[:, bi, :].rearrange("p (m t) -> p m t", m=4), scalar=gqk[:, 1:2],
                        in1=rn[:].rearrange("p (m t) -> p m t", m=4), op0=ALU.mult, op1=ALU.mult),
                        reads=[bank[1], rn_b, gqk_b], writes=[kT_b])
                yield
            bank = gr.next()
            proj4(1536, bank)
            P.op("act", lambda e, bi=bank[0]: e.activation(out=qi_i[:].rearrange("p m t -> p (m t)"), in_=pg[:, bi, :],
                                                           func=AF.Copy), reads=[bank[1]], writes=[qi_b])
            yield
            bank = gr.next()

            def kiw(e, bi=bank[0]):
                for kc in range(8):
                    e.matmul(pg[0:64, bi, 0:128], lhsT=wa[:, kc, 2048:2112], rhs=xn[:, kc, :], start=(kc == 0), stop=(kc == 7))
                for kc in range(8):
                    e.matmul(pg[64:128, bi, 0:128], lhsT=wa[:, kc, ATTN_IN:ATTN_IN + 64], rhs=xn[:, kc, :], start=(kc == 0),
                             stop=(kc == 7))
                ins = None
                for kc in range(8):
                    ins = e.matmul(pg[:, bi, 128:136], lhsT=xn[:, kc, :], rhs=wa[:, kc, 2112:2120], start=(kc == 0), stop=(kc == 7))
                return ins

            P.op("pe", kiw, reads=[wa_b, xn_b], writes=[bank[1]])
            P.op("act", lambda e, bi=bank[0], ts_=ts_: e.activation(out=kiT[:, ts_], in_=pg[:, bi, 0:128], func=AF.Copy),
                 reads=[bank[1]], writes=[kiT_b])
            P.op("dve", lambda e, bi=bank[0]: e.tensor_copy(out=wi_i[:], in_=pg[:, bi, 128:136]), reads=[bank[1]], writes=[wi_b])
            yield
            bank = gr.next()

            def vmm(e, bi=bank[0]):
                ins = None
                for kc in range(8):
                    ins = e.matmul(pg[:, bi, :], lhsT=xn[:, kc, :], rhs=wa[:, kc, 1024:1536], start=(kc == 0), stop=(kc == 7))
                return ins

            P.op("pe", vmm, reads=[wa_b, xn_b], writes=[bank[1]])
            P.op("act", lambda e, bi=bank[0], i=i: e.activation(out=vaug[:, i, :, 0:64],
                                                                in_=pg[:, bi, :].rearrange("p (h d) -> p h d", h=8), func=AF.Copy),
                 reads=[bank[1]], writes=[va_b])
            yield

            for gk in range((nkb + 3) // 4):
                w_ = min(512, W - gk * 512)
                for h in range(8):
                    hb = (h % 2) * 64
                    bank = gr.next()
                    P.op("pe", lambda e, bi=bank[0], h=h, hb=hb, gk=gk, w_=w_: e.matmul(
                        pg[:, bi, 0:w_], lhsT=qi_i[hb:hb + 64, h // 2, :], rhs=kiT[hb:hb + 64, gk * 512:gk * 512 + w_],
                        start=True, stop=True), reads=[qi_b] + kiT_bs[gk * 4:gk * 4 + (w_ // 128)], writes=[bank[1]])
                    ri, rb_ = rr.next()
                    P.op("act", lambda e, bi=bank[0], ri=ri, w_=w_: e.activation(out=rbuf[:, ri, 0:w_], in_=pg[:, bi, 0:w_],
                                                                                 func=AF.Relu), reads=[bank[1]], writes=[rb_])
                    if h == 0:
                        P.op("dve", lambda e, ri=ri, gk=gk, w_=w_: e.tensor_scalar(
                            out=sc[:, gk * 512:gk * 512 + w_], in0=rbuf[:, ri, 0:w_], scalar1=wi_i[:, 0:1], scalar2=None,
                            op0=ALU.mult), reads=[rb_, wi_b], writes=[sc_b])
                    else:
                        P.op("dve", lambda e, ri=ri, gk=gk, w_=w_, h=h: e.scalar_tensor_tensor(
                            out=sc[:, gk * 512:gk * 512 + w_], in0=rbuf[:, ri, 0:w_], scalar=wi_i[:, h:h + 1],
                            in1=sc[:, gk * 512:gk * 512 + w_], op0=ALU.mult, op1=ALU.add), reads=[rb_, wi_b, sc_b], writes=[sc_b])
                    yield
            P.op("dve", lambda e, ts_=ts_: e.tensor_tensor(out=sc[:, ts_], in0=sc[:, ts_], in1=cmask[:], op=ALU.add),
                 reads=[sc_b, cm_b], writes=[sc_b])
            if i < 2:
                P.op("dve", lambda e: e.memset(bs[:, 0:1], -1.0e29), writes=[bs_b])
            else:
                P.op("dve", lambda e, i=i: e.tensor_reduce(out=bs[:, 0:1], in_=sc[:, 0:i * 128], axis=AX.X, op=ALU.min),
                     reads=[sc_b], writes=[bs_b])
                P.op("dve", lambda e, W=W: e.tensor_reduce(out=bs[:, 6:7], in_=sc[:, 0:W], axis=AX.X, op=ALU.max),
                     reads=[sc_b, bs_b], writes=[bs_b])
                P.op("dve", lambda e: e.tensor_tensor(out=bs[:, 1:2], in0=bs[:, 6:7], in1=bs[:, 0:1], op=ALU.subtract),
                     reads=[bs_b], writes=[bs_b])
                P.op("dve", lambda e: e.tensor_tensor(out=halfs[:], in0=bs[:, 1:2].broadcast_to([128, NITER]), in1=pw2[:],
                                                      op=ALU.mult), reads=[bs_b, cm_b], writes=[hf_b])
                P.op("dve", lambda e: e.tensor_tensor(out=bs[:, 3:4], in0=bs[:, 0:1], in1=halfs[:, 0:1], op=ALU.add),
                     reads=[bs_b, hf_b], writes=[bs_b])
                for it in range(NITER):
                    P.op("dve", lambda e: e.tensor_scalar(out=junk[:, 0:W], in0=sc[:, 0:W], scalar1=bs[:, 3:4], scalar2=None,
                                                          op0=ALU.is_ge, op1=ALU.add, accum_out=bs[:, 4:5]),
                         reads=[sc_b, bs_b], writes=[junk_b, bs_b])
                    P.op("dve", lambda e, it=it: e.scalar_tensor_tensor(out=bs[:, 5:6], in0=bs[:, 4:5], scalar=TOPK - 0.5,
                                                                        in1=halfs[:, it:it + 1], op0=ALU.is_ge, op1=ALU.mult),
                         reads=[bs_b, hf_b], writes=[bs_b])
                    P.op("dve", lambda e: e.tensor_tensor(out=bs[:, 0:1], in0=bs[:, 0:1], in1=bs[:, 5:6], op=ALU.add),
                         reads=[bs_b], writes=[bs_b])
                    if it + 1 < NITER:
                        P.op("dve", lambda e, it=it: e.tensor_tensor(out=bs[:, 3:4], in0=bs[:, 0:1], in1=halfs[:, it + 1:it + 2],
                                                                     op=ALU.add), reads=[bs_b, hf_b], writes=[bs_b])
                    yield
            P.op("dve", lambda e, W=W: e.tensor_scalar(out=junk[:, 0:W], in0=sc[:, 0:W], scalar1=bs[:, 0:1], scalar2=None,
                                                       op0=ALU.is_ge), reads=[sc_b, bs_b], writes=[junk_b])
            for j0 in range(0, nkb, 8):
                nb = min(8, nkb - j0)
                bank = gr.next()

                def trm(e, bi=bank[0], j0=j0, nb=nb):
                    ins = None
                    for jj in range(nb):
                        ins = e.transpose(out=pgb(bi)[:, jj * 128:(jj + 1) * 128], in_=junk[:, (j0 + jj) * 128:(j0 + jj + 1) * 128],
                                          identity=C.ident_bf[:])
                    return ins

                P.op("pe", trm, reads=[junk_b, C.b], writes=[bank[1]])
                P.op("act", lambda e, bi=bank[0], j0=j0, nb=nb: e.activation(
                    out=maskT[:, i % 2, j0:j0 + nb, :].rearrange("p j t -> p (j t)"), in_=pgb(bi)[:, 0:nb * 128], func=AF.Copy),
                    reads=[bank[1]], writes=[mT_bs[i % 2]])
                yield
            yield

        def back(i):
            ts_ = slice(i * 128, (i + 1) * 128)
            nkb = i + 1
            q_b = q_bs[i % 2]
            mT_b = mT_bs[i % 2]
            items = [(h, j0, min(4, nkb - j0)) for h in range(8) for j0 in range(0, nkb, 4)]
            DEPTH = 2
            st = {}
            obs = {}
            for k in range(len(items) + DEPTH):
                if k < len(items):
                    h, j0, nb = items[k]
                    hb = (h % 2) * 64
                    m = h // 2
                    if j0 == 0:
                        obs[h] = orr.next()
                    bank = lgr.next()

                    def qk(e, bi=bank[0], j0=j0, nb=nb, h=h, hb=hb, m=m):
                        ins = None
                        for jj in range(nb):
                            j = j0 + jj
                            near = j >= i - 1
                            ins = e.matmul(pg[:, bi, jj * 128:(jj + 1) * 128], lhsT=kT[hb:hb + 64, m, j * 128:(j + 1) * 128],
                                           rhs=q_i[hb:hb + 64, i % 2, m, :], start=True, stop=not near)
                            if near:
                                ins = e.matmul(pg[:, bi, jj * 128:(jj + 1) * 128], lhsT=C.ident_bf[:],
                                               rhs=bt[:, h, 0 if j == i else 1, :], start=False, stop=True)
                        return ins

                    P.op("pe", qk, reads=kT_bs[j0:j0 + nb] + [q_b, bt_b, C.b], writes=[bank[1]])
                    ei, eb = er.next()
                    P.op("act", lambda e, bi=bank[0], ei=ei, nb=nb: e.activation(
                        out=Et[:, ei, 0:nb, :].rearrange("p j t -> p (j t)"), in_=pg[:, bi, 0:nb * 128], func=AF.Exp),
                        reads=[bank[1]], writes=[eb])
                    pi, pb_ = ptr.next()
                    P.op("pool", lambda e, ei=ei, pi=pi, j0=j0, nb=nb: e.tensor_tensor(
                        out=Pt[:, pi, 0:nb, :], in0=Et[:, ei, 0:nb, :], in1=maskT[:, i % 2, j0:j0 + nb, :], op=ALU.mult),
                        reads=[eb, mT_b], writes=[pb_])
                    st[k] = (pi, pb_)
                kk = k - DEPTH
                if kk >= 0:
                    h, j0, nb = items[kk]
                    pi, pb_ = st.pop(kk)
                    ob = obs[h]

                    def pv(e, oi=ob[0], pi=pi, j0=j0, nb=nb, h=h):
                        ins = None
                        for jj in range(nb):
                            j = j0 + jj
                            ins = e.matmul(po[:, oi, 0:65], lhsT=Pt[:, pi, jj, :], rhs=vaug[:, j, h, :], start=(j == 0),
                                           stop=(j == i))
                        return ins

                    P.op("pe", pv, reads=[pb_] + va_bs[j0:j0 + nb], writes=[ob[1]])
                    if j0 + nb == nkb:
                        di, db = rdr.next()
                        P.op("dve", lambda e, oi=ob[0], di=di: e.reciprocal(out=rden[:, di:di + 1], in_=po[:, oi, 64:65]),
                             reads=[ob[1]], writes=[db])
                        P.op("act", lambda e, oi=ob[0], di=di, h=h: e.activation(out=ytile[:, h * 64:(h + 1) * 64],
                                                                                 in_=po[:, oi, 0:64], func=AF.Copy,
                                                                                 scale=rden[:, di:di + 1]),
                             reads=[ob[1], db], writes=[yt_b])
                yield
            bank = lgr.next()

            def try_(e, bi=bank[0]):
                ins = None
                for m in range(4):
                    ins = e.transpose(out=pgb(bi)[:, m * 128:(m + 1) * 128], in_=ytile[:, m * 128:(m + 1) * 128],
                                      identity=C.ident_bf[:])
                return ins

            P.op("pe", try_, reads=[yt_b, C.b], writes=[bank[1]])
            P.op("act", lambda e, bi=bank[0]: e.activation(out=yst[:].rearrange("p m t -> p (m t)"), in_=pgb(bi)[:, 0:512],
                                                           func=AF.Copy), reads=[bank[1]], writes=[ys_b])
            P.dma("sp", lambda e: e.dma_start(out=yav[:, :, ts_], in_=yst[:]), reads=[ys_b])
            yield

        def run2(f, b):
            fa, ba = f is not None, b is not None
            while fa or ba:
                if fa:
                    try:
                        next(f)
                    except StopIteration:
                        fa = False
                if ba:
                    try:
                        next(b)
                    except StopIteration:
                        ba = False

        for i in range(NT + 1):
            run2(front(i) if i < NT else None, back(i - 1) if i >= 1 else None)
    P.barrier()


WEIGHT_SPECS = [
    ("mix_norm", [1, D]), ("w_in", [D, 5992]), ("attn_q_norm", [1, 64]), ("attn_k_norm", [1, 64]),
    ("bias_tiles", [8, 2, 128, 128]), ("b31", [1, 8]), ("rwkv_mu", [1, RWKV_IN]), ("rwkv_w0", [1, 512]), ("rwkv_w2", [64, 512]),
    ("rwkv_a0", [1, 512]), ("rwkv_a2", [64, 512]), ("rwkv_g2", [160, 512]), ("rwkv_k_k", [1, 512]),
    ("rwkv_k_a", [1, 512]), ("rwkv_r_k", [1, 512]), ("rwkv_ln_w", [1, 512]), ("rwkv_ln_b", [1, 512]),
    ("w_branch_attn", [512, D]), ("w_branch_rwkv", [512, D]), ("w_out", [D, D]), ("ffn_norm", [1, D]),
    ("w_gate_up", [D, 2 * FFN_H]), ("w_down", [FFN_H, D]),
]


def build_program(phases=("attn", "rwkv", "merge", "ffn"), debug=False):
    nc = bass.Bass("TRN2", target_bir_lowering=False)
    A = {}
    A["x"] = nc.dram_tensor("x", [S, D], F32, kind="ExternalInput").ap()
    for name, shp in WEIGHT_SPECS:
        A[name] = nc.dram_tensor(name, shp, F32, kind="ExternalInput").ap()
    out = nc.dram_tensor("out", [S, D], F32, kind="ExternalOutput").ap()
    def kind(prod, cons):
        if not debug:
            return "Internal"
        if prod in phases and cons not in phases:
            return "ExternalOutput"
        if prod not in phases and cons in phases:
            return "ExternalInput"
        return "Internal"

    ya = nc.dram_tensor("ya_scr", [512, S], BF16, kind=kind("attn", "merge")).ap()
    yr = nc.dram_tensor("yr_scr", [512, S], BF16, kind=kind("rwkv", "merge")).ap()
    hs = nc.dram_tensor("h_scr", [S, D], F32, kind=kind("merge", "ffn")).ap()
    P = Prog(nc)
    final_ops = []
    with ExitStack() as es:
        C = make_consts(P, nc, es)
        P.barrier()
        if "attn" in phases:
            phase_attn(P, nc, C, A, ya)
        if "rwkv" in phases:
            phase_rwkv(P, nc, C, A, yr)
        if "merge" in phases:
            phase_merge(P, nc, C, A["x"], ya, yr, hs, A["mix_norm"], A["w_in"], A["w_branch_attn"],
                        A["w_branch_rwkv"], A["w_out"])
        if "ffn" in phases:
            phase_ffn(P, nc, C, hs, out, A["ffn_norm"], A["w_gate_up"], A["w_down"], final_ops)
        P.emit(final_wait_ops=final_ops)
    return nc


def t5_bucket_np(d):
    d = np.maximum(d, 0)
    max_exact = 16
    log_ratio = np.log(np.maximum(d, 1).astype(np.float32) / max_exact) / math.log(128 / max_exact)
    large = np.minimum(max_exact + (log_ratio * 16).astype(np.int32), 31)
    return np.where(d < max_exact, d, large)


def host_layout(inputs):
    w = {}
    for name, shp in WEIGHT_SPECS:
        if name in ("bias_tiles", "b31"):
            continue
        w[name] = np.ascontiguousarray(np.asarray(inputs[name], dtype=np.float32).reshape(shp))
    s_idx = np.arange(128)[:, None]
    t_idx = np.arange(128)[None, :]
    rb = np.asarray(inputs["rel_bias"], dtype=np.float32)
    tiles = np.empty((8, 2, 128, 128), np.float32)
    for cls in range(2):
        bk = t5_bucket_np(t_idx - s_idx + 128 * cls)
        tiles[:, cls] = np.transpose(rb[bk], (2, 0, 1))
    w["bias_tiles"] = tiles
    w["b31"] = np.ascontiguousarray(rb[31:32, :])
    return w


_NC_CACHE = {}


def kernel(**inputs):
    x = np.asarray(inputs["x"], dtype=np.float32)
    w = host_layout(inputs)
    if "nc" not in _NC_CACHE:
        _NC_CACHE["nc"] = build_program()
    nc = _NC_CACHE["nc"]
    in_maps = []
    for b in range(8):
        m = dict(w)
        m["x"] = np.ascontiguousarray(x[b])
        in_maps.append(m)
    res = run_bass_kernel_spmd(nc, in_maps, core_ids=list(range(8)))
    return np.stack([np.asarray(r["out"], dtype=np.float32) for r in res.results], axis=0)
```

```python
import math
from contextlib import ExitStack

import numpy as np
import concourse.bass as bass
import concourse.mybir as mybir
from concourse.bass_utils import run_bass_kernel_spmd

F32 = mybir.dt.float32
BF16 = mybir.dt.bfloat16
AF = mybir.ActivationFunctionType
ALU = mybir.AluOpType
AX = mybir.AxisListType

S = 4096
D = 1024
NT = S // 128
ATTN_IN = 2120
RWKV_IN = 1824
FFN_H = 2816
RMS_EPS = 1e-6
GN_EPS = 64e-5
TOPK = 256
NEG = -1.0e30

ENGS = ("pe", "act", "dve", "pool", "sp")
SBUF_DEBUG = False


class Buf:
    __slots__ = ("name", "w", "r")

    def __init__(self, name=""):
        self.name = name
        self.w = None
        self.r = []


class Op:
    __slots__ = ("eng", "thunk", "deps", "is_dma", "sem", "val", "need_inc", "pos", "prev_on_sem")


class Prog:
    def __init__(self, nc, n_dma_sems=(56, 28)):
        self.nc = nc
        self.streams = {e: [] for e in ENGS}
        self.n_dma_sems = n_dma_sems
        self.dma_count = 0
        self.all_ops = []
        self.open_dmas = []

    def _hazards(self, op, reads, writes):
        deps = []
        for b in reads:
            if b.w is not None:
                deps.append(b.w)
        for b in writes:
            if b.w is not None:
                deps.append(b.w)
            deps.extend(b.r)
        for b in reads:
            b.r.append(op)
        for b in writes:
            b.w = op
            b.r = []
        return deps

    def op(self, eng, thunk, reads=(), writes=(), extra_deps=()):
        o = Op()
        o.eng = eng
        o.thunk = thunk
        o.is_dma = False
        o.need_inc = False
        o.sem = None
        o.val = None
        o.prev_on_sem = None
        deps = self._hazards(o, reads, writes) + list(extra_deps)
        seen = set()
        o.deps = []
        for d in deps:
            if d is o or id(d) in seen:
                continue
            if eng == "pe" and d.eng == "pe" and not d.is_dma:
                continue
            seen.add(id(d))
            o.deps.append(d)
        self.streams[eng].append(o)
        self.all_ops.append(o)
        return o

    def dma(self, eng, thunk, reads=(), writes=(), extra_deps=()):
        o = self.op(eng, thunk, reads, writes, extra_deps)
        o.is_dma = True
        o.pos = self.dma_count
        self.dma_count += 1
        self.open_dmas.append(o)
        return o

    def barrier(self):
        lasts = []
        for e in ENGS:
            for o in reversed(self.streams[e]):
                if not o.is_dma:
                    lasts.append(o)
                    break
        deps = lasts + self.open_dmas
        self.open_dmas = []
        for e in ENGS:
            self.op(e, lambda eng: eng.nop(), extra_deps=deps)

    def emit(self, final_wait_ops=()):
        nc = self.nc
        for o in self.all_ops:
            for d in o.deps:
                d.need_inc = True
        eng_sems = {e: nc.alloc_semaphore("s_" + e) for e in ENGS}
        ring_n = {"sp": self.n_dma_sems[0], "pool": self.n_dma_sems[1], "act": 2, "dve": 2, "pe": 2}
        dma_sems = {}
        dma_sem_val = {}
        dma_prev = {}
        qpos = {e: 0 for e in ENGS}
        for e in ENGS:
            if any(o.is_dma for o in self.streams[e]):
                for i in range(ring_n[e]):
                    dma_sems[(e, i)] = nc.alloc_semaphore("s_dma_%s%d" % (e, i))
                    dma_sem_val[(e, i)] = 0
                    dma_prev[(e, i)] = None
        cnt = {e: 0 for e in ENGS}
        for o in self.all_ops:
            if o.is_dma:
                kq = (o.eng, qpos[o.eng] % ring_n[o.eng])
                qpos[o.eng] += 1
                dma_sem_val[kq] += 16
                o.sem = ("dma", kq)
                o.val = dma_sem_val[kq]
                o.prev_on_sem = dma_prev[kq]
                dma_prev[kq] = o
            elif o.need_inc:
                cnt[o.eng] += 1
                o.sem = ("eng", o.eng)
                o.val = cnt[o.eng]

        def semh(key):
            return eng_sems[key[1]] if key[0] == "eng" else dma_sems[key[1]]

        engines = {"pe": "tensor", "act": "scalar", "dve": "vector", "pool": "gpsimd", "sp": "sync"}
        with nc.Block() as block:
            for e in ENGS:
                stream = self.streams[e]
                final = list(final_wait_ops) if e == "sp" else []

                def body(engine, stream=stream, final=final):
                    known = {}
                    for o in stream:
                        waits = {}
                        deps = list(o.deps)
                        if o.is_dma and o.prev_on_sem is not None:
                            deps.append(o.prev_on_sem)
                        for d in deps:
                            if known.get(d.sem, 0) >= d.val:
                                continue
                            if waits.get(d.sem, 0) < d.val:
                                waits[d.sem] = d.val
                        for key, val in waits.items():
                            engine.wait_ge(semh(key), val)
                            known[key] = val
                        ins = o.thunk(engine)
                        if o.is_dma:
                            ins.then_inc(semh(o.sem), 16)
                        elif o.need_inc:
                            ins.then_inc(semh(o.sem), 1)
                    for o in final:
                        engine.wait_ge(semh(o.sem), o.val)

                getattr(block, engines[e])(body)


class Ring:
    def __init__(self, items):
        self.items = items
        self.i = 0

    def next(self):
        it = self.items[self.i % len(self.items)]
        self.i += 1
        return it


def load_weight_bf16(P, nc, dst, dst_buf, w_ap, c0, c1, kchunks, eng="pool"):
    wv = w_ap.rearrange("(kc p) n -> p kc n", p=128)
    for kc in range(kchunks):
        for a in range(c0, c1, 2048):
            b = min(c1, a + 2048)
            P.dma(eng, lambda e, kc=kc, a=a, b=b: e.dma_start(out=dst[:, kc, a - c0:b - c0], in_=wv[:, kc, a:b]),
                  writes=[dst_buf])


def load_col_vec(P, nc, dst, dst_buf, v_ap, n):
    src = v_ap.rearrange("o (c p) -> p (o c)", p=128)
    P.dma("sp", lambda e: e.dma_start(out=dst, in_=src, allow_slow_non_contiguous=True), writes=[dst_buf])


class Consts:
    pass


def make_consts(P, nc, es):
    C = Consts()
    C.ident_bf = es.enter_context(nc.sbuf_tensor("ident_bf", [128, 128], BF16))
    C.ident_f = es.enter_context(nc.sbuf_tensor("ident_f", [128, 128], F32))
    C.eps = es.enter_context(nc.sbuf_tensor("eps_c", [128, 2], F32))
    C.b = Buf("consts")

    def mk(e):
        e.memset(C.ident_f[:], 0.0)
        e.affine_select(out=C.ident_f[:], in_=C.ident_f[:], pattern=[[-1, 128]], compare_op=ALU.not_equal,
                        fill=1.0, base=0, channel_multiplier=1)
        e.memset(C.eps[:, 0:1], RMS_EPS)
        return e.memset(C.eps[:, 1:2], GN_EPS)

    P.op("pool", mk, writes=[C.b])
    P.op("pool", lambda e: e.tensor_copy(out=C.ident_bf[:], in_=C.ident_f[:]), reads=[C.b], writes=[C.b])
    return C


class Normer:
    def __init__(self, P, nc, es, C, gain_ap, name, nslots=2):
        self.P, self.nc, self.C = P, nc, C
        self.gcol = es.enter_context(nc.sbuf_tensor(name + "_g", [128, 8], F32))
        self.gb = Buf(name + "_g")
        load_col_vec(P, nc, self.gcol[:, :], self.gb, gain_ap, 8)
        self.stat = es.enter_context(nc.sbuf_tensor(name + "_st", [128, nslots, 4], F32))
        self.junk = es.enter_context(nc.sbuf_tensor(name + "_junk", [128, 1024], BF16))
        self.xs = es.enter_context(nc.sbuf_tensor(name + "_xs", [128, nslots, 1024], BF16))
        self.tp = es.enter_context(nc.psum_tensor(name + "_tp", [128, nslots, 8, 128], BF16))
        self.ring = Ring([(i, Buf(), Buf(), Buf()) for i in range(nslots)])
        self.junkb = Buf()

    def run(self, xt_ap, xt_buf, dst, dst_buf, col0):
        P, C = self.P, self.C
        i, sb, xb, pb = self.ring.next()
        st = self.stat
        P.op("act", lambda e: e.activation(out=self.junk[:], in_=xt_ap, func=AF.Square, accum_out=st[:, i, 0:1]),
             reads=[xt_buf], writes=[self.junkb, sb])
        P.op("act", lambda e: e.activation(out=st[:, i, 1:2], in_=st[:, i, 0:1], func=AF.Sqrt, scale=1.0 / D,
                                           bias=C.eps[:, 0:1]), reads=[sb, C.b], writes=[sb])
        P.op("dve", lambda e: e.reciprocal(out=st[:, i, 2:3], in_=st[:, i, 1:2]), reads=[sb], writes=[sb])
        P.op("act", lambda e: e.activation(out=self.xs[:, i, :], in_=xt_ap, func=AF.Copy, scale=st[:, i, 2:3]),
             reads=[xt_buf, sb], writes=[xb])

        def tr(e):
            ins = None
            for kc in range(8):
                ins = e.transpose(out=self.tp[:, i, kc, :], in_=self.xs[:, i, kc * 128:(kc + 1) * 128],
                                  identity=C.ident_bf[:])
            return ins

        P.op("pe", tr, reads=[xb, C.b], writes=[pb])
        P.op("dve", lambda e: e.tensor_tensor(out=dst[:, :, col0:col0 + 128], in0=self.tp[:, i, :, :],
                                              in1=self.gcol[:, :].unsqueeze(2).broadcast_to([128, 8, 128]),
                                              op=ALU.mult), reads=[pb, self.gb], writes=[dst_buf])


def phase_ffn(P, nc, C, h_dram, out_dram, ffn_norm, w_gate_up, w_down, final_ops):
    with ExitStack() as es:
        wgu = es.enter_context(nc.sbuf_tensor("wgu", [128, 8, 2 * FFN_H], BF16))
        wd = es.enter_context(nc.sbuf_tensor("wd", [128, 22, D], BF16))
        wgu_b, wd_b = Buf("wgu"), Buf("wd")
        load_weight_bf16(P, nc, wgu, wgu_b, w_gate_up, 0, 2 * FFN_H, 8)
        load_weight_bf16(P, nc, wd, wd_b, w_down, 0, D, 22)
        nm = Normer(P, nc, es, C, ffn_norm, "fn")
        hbuf = es.enter_context(nc.sbuf_tensor("hbuf", [128, 2, D], F32))
        hr = Ring([(i, Buf()) for i in range(2)])
        hnT = es.enter_context(nc.sbuf_tensor("hnT", [128, 2, 8, 512], BF16))
        hnb = [Buf(), Buf()]
        actT = es.enter_context(nc.sbuf_tensor("actT", [128, 22, 512], BF16))
        actb = [Buf() for _ in range(22)]
        sg = es.enter_context(nc.sbuf_tensor("sg", [128, 2, 512], F32))
        sgr = Ring([(i, Buf()) for i in range(2)])
        ost = es.enter_context(nc.sbuf_tensor("ost", [128, 2, D], F32))
        ostr = Ring([(i, Buf()) for i in range(2)])
        pg = es.enter_context(nc.psum_tensor("pg", [128, 2, 512], F32))
        pu = es.enter_context(nc.psum_tensor("pu", [128, 2, 512], F32))
        po = es.enter_context(nc.psum_tensor("po", [128, 2, 512], F32))
        pgr = Ring([(i, Buf()) for i in range(2)])
        pur = Ring([(i, Buf()) for i in range(2)])
        por = Ring([(i, Buf()) for i in range(2)])
        hview = h_dram.rearrange("(t p) d -> t p d", p=128)
        oview = out_dram.rearrange("(t p) d -> t p d", p=128)
        def ffn_norm_chunk(c):
            cb = c % 2
            for j in range(4):
                t = c * 4 + j
                hi, hb_ = hr.next()
                P.dma("sp", lambda e, t=t, hi=hi: e.dma_start(out=hbuf[:, hi, :], in_=hview[t]), writes=[hb_])
                nm.run(hbuf[:, hi, :], hb_, hnT[:, cb], hnb[cb], j * 128)

        ffn_norm_chunk(0)
        for c in range(S // 512):
            cb = c % 2
            if c + 1 < S // 512:
                ffn_norm_chunk(c + 1)
            for m in range(22):
                gi, gbuf = pgr.next()
                ui, ubuf = pur.next()

                def mm(e, m=m, gi=gi, ui=ui, cb=cb):
                    for kc in range(8):
                        e.matmul(pg[:, gi, :], lhsT=wgu[:, kc, m * 128:(m + 1) * 128], rhs=hnT[:, cb, kc, :],
                                 start=(kc == 0), stop=(kc == 7))
                    ins = None
                    for kc in range(8):
                        ins = e.matmul(pu[:, ui, :], lhsT=wgu[:, kc, FFN_H + m * 128:FFN_H + (m + 1) * 128],
                                       rhs=hnT[:, cb, kc, :], start=(kc == 0), stop=(kc == 7))
                    return ins

                P.op("pe", mm, reads=[wgu_b, hnb[cb]], writes=[gbuf, ubuf])
                si, sbuf_ = sgr.next()
                P.op("act", lambda e, gi=gi, si=si: e.activation(out=sg[:, si, :], in_=pg[:, gi, :], func=AF.Silu),
                     reads=[gbuf], writes=[sbuf_])
                P.op("dve", lambda e, ui=ui, si=si, m=m: e.tensor_tensor(out=actT[:, m, :], in0=pu[:, ui, :],
                                                                         in1=sg[:, si, :], op=ALU.mult),
                     reads=[ubuf, sbuf_], writes=[actb[m]])
            for j in range(4):
                t = c * 4 + j
                oi, obuf = ostr.next()
                P.dma("sp", lambda e, t=t, oi=oi: e.dma_start(out=ost[:, oi, :], in_=hview[t]), writes=[obuf])
                for nh in range(2):
                    pi, pbuf = por.next()

                    def mmd(e, j=j, nh=nh, pi=pi):
                        ins = None
                        for m in range(22):
                            ins = e.matmul(po[:, pi, :], lhsT=actT[:, m, j * 128:(j + 1) * 128],
                                           rhs=wd[:, m, nh * 512:(nh + 1) * 512], start=(m == 0), stop=(m == 21))
                        return ins

                    P.op("pe", mmd, reads=actb + [wd_b], writes=[pbuf])
                    P.op("dve", lambda e, nh=nh, pi=pi, oi=oi: e.tensor_tensor(
                        out=ost[:, oi, nh * 512:(nh + 1) * 512], in0=po[:, pi, :],
                        in1=ost[:, oi, nh * 512:(nh + 1) * 512], op=ALU.add),
                        reads=[pbuf], writes=[obuf])
                final_ops.append(P.dma("sp", lambda e, t=t, oi=oi: e.dma_start(out=oview[t], in_=ost[:, oi, :]),
                                       reads=[obuf]))
    P.barrier()


def phase_merge(P, nc, C, x_dram, ya_dram, yr_dram, h_dram, mix_norm, w_in, w_ba, w_br, w_out):
    with ExitStack() as es:
        wg = es.enter_context(nc.sbuf_tensor("wg", [128, 8, 2 * D], BF16))
        wba = es.enter_context(nc.sbuf_tensor("wba", [128, 4, D], BF16))
        wbr = es.enter_context(nc.sbuf_tensor("wbr", [128, 4, D], BF16))
        wo = es.enter_context(nc.sbuf_tensor("wo", [128, 8, D], BF16))
        wg_b, wba_b, wbr_b, wo_b = Buf(), Buf(), Buf(), Buf()
        load_weight_bf16(P, nc, wg, wg_b, w_in, ATTN_IN + RWKV_IN, ATTN_IN + RWKV_IN + 2 * D, 8)
        load_weight_bf16(P, nc, wba, wba_b, w_ba, 0, D, 4)
        load_weight_bf16(P, nc, wbr, wbr_b, w_br, 0, D, 4)
        load_weight_bf16(P, nc, wo, wo_b, w_out, 0, D, 8)
        nm = Normer(P, nc, es, C, mix_norm, "mn")
        xbuf = es.enter_context(nc.sbuf_tensor("xbuf", [128, 2, 4, D], F32))
        xb = [[Buf() for _ in range(4)] for _ in range(2)]
        xnT = es.enter_context(nc.sbuf_tensor("xnT", [128, 2, 8, 512], BF16))
        xnb = [Buf(), Buf()]
        yaT = es.enter_context(nc.sbuf_tensor("yaT", [128, 2, 4, 512], BF16))
        yrT = es.enter_context(nc.sbuf_tensor("yrT", [128, 2, 4, 512], BF16))
        yab, yrb = [Buf(), Buf()], [Buf(), Buf()]
        mT = es.enter_context(nc.sbuf_tensor("mT", [128, 8, 512], BF16))
        mb = [Buf() for _ in range(8)]
        sg = es.enter_context(nc.sbuf_tensor("sgm", [128, 2, 2, 512], F32))
        sgr = Ring([(i, Buf()) for i in range(2)])
        tt = es.enter_context(nc.sbuf_tensor("ttm", [128, 2, 2, 512], F32))
        ttr = Ring([(i, Buf()) for i in range(2)])
        hst = es.enter_context(nc.sbuf_tensor("hst", [128, 2, D], F32))
        hstr = Ring([(i, Buf()) for i in range(2)])
        pga = es.enter_context(nc.psum_tensor("pga", [128, 2, 512], F32))
        pbr = es.enter_context(nc.psum_tensor("pbr", [128, 2, 512], F32))
        po = es.enter_context(nc.psum_tensor("pom", [128, 2, 512], F32))
        pgb, pbb = Buf(), Buf()
        por = Ring([(i, Buf()) for i in range(2)])
        xview = x_dram.rearrange("(t p) d -> t p d", p=128)
        hview = h_dram.rearrange("(t p) d -> t p d", p=128)
        yav = ya_dram.rearrange("(kc p) s -> p kc s", p=128)
        yrv = yr_dram.rearrange("(kc p) s -> p kc s", p=128)
        def merge_load_chunk(c):
            cb = c % 2
            P.dma("sp", lambda e: e.dma_start(out=yaT[:, cb], in_=yav[:, :, c * 512:(c + 1) * 512]), writes=[yab[cb]])
            P.dma("sp", lambda e: e.dma_start(out=yrT[:, cb], in_=yrv[:, :, c * 512:(c + 1) * 512]), writes=[yrb[cb]])
            for j in range(4):
                t = c * 4 + j
                P.dma("sp", lambda e, t=t, j=j: e.dma_start(out=xbuf[:, cb, j, :], in_=xview[t]), writes=[xb[cb][j]])
                nm.run(xbuf[:, cb, j, :], xb[cb][j], xnT[:, cb], xnb[cb], j * 128)

        merge_load_chunk(0)
        for c in range(S // 512):
            cb = c % 2
            if c + 1 < S // 512:
                merge_load_chunk(c + 1)
            for m in range(8):
                def mmg(e, m=m, cb=cb):
                    ins = None
                    for g in range(2):
                        for kc in range(8):
                            ins = e.matmul(pga[:, g, :], lhsT=wg[:, kc, g * D + m * 128:g * D + (m + 1) * 128],
                                           rhs=xnT[:, cb, kc, :], start=(kc == 0), stop=(kc == 7))
                    return ins

                P.op("pe", mmg, reads=[wg_b, xnb[cb]], writes=[pgb])

                def mmb(e, m=m, cb=cb):
                    ins = None
                    for kc in range(4):
                        ins = e.matmul(pbr[:, 0, :], lhsT=wba[:, kc, m * 128:(m + 1) * 128], rhs=yaT[:, cb, kc, :],
                                       start=(kc == 0), stop=(kc == 3))
                    for kc in range(4):
                        ins = e.matmul(pbr[:, 1, :], lhsT=wbr[:, kc, m * 128:(m + 1) * 128], rhs=yrT[:, cb, kc, :],
                                       start=(kc == 0), stop=(kc == 3))
                    return ins

                P.op("pe", mmb, reads=[wba_b, wbr_b, yab[cb], yrb[cb]], writes=[pbb])
                si, sbuf_ = sgr.next()
                P.op("act", lambda e, si=si: e.activation(out=sg[:, si], in_=pga[:, :, :], func=AF.Sigmoid),
                     reads=[pgb], writes=[sbuf_])
                ti, tbuf = ttr.next()
                P.op("dve", lambda e, si=si, ti=ti: e.tensor_tensor(out=tt[:, ti], in0=pbr[:, :, :], in1=sg[:, si],
                                                                    op=ALU.mult),
                     reads=[pbb, sbuf_], writes=[tbuf])
                P.op("pool", lambda e, ti=ti, m=m: e.tensor_tensor(out=mT[:, m, :], in0=tt[:, ti, 0, :],
                                                                   in1=tt[:, ti, 1, :], op=ALU.add),
                     reads=[tbuf], writes=[mb[m]])
            for j in range(4):
                t = c * 4 + j
                hi, hbuf_ = hstr.next()
                for nh in range(2):
                    pi, pbuf = por.next()

                    def mmo(e, j=j, nh=nh, pi=pi):
                        ins = None
                        for m in range(8):
                            ins = e.matmul(po[:, pi, :], lhsT=mT[:, m, j * 128:(j + 1) * 128],
                                           rhs=wo[:, m, nh * 512:(nh + 1) * 512], start=(m == 0), stop=(m == 7))
                        return ins

                    P.op("pe", mmo, reads=mb + [wo_b], writes=[pbuf])
                    P.op("dve", lambda e, j=j, nh=nh, pi=pi, hi=hi, cb=cb: e.tensor_tensor(
                        out=hst[:, hi, nh * 512:(nh + 1) * 512], in0=po[:, pi, :],
                        in1=xbuf[:, cb, j, nh * 512:(nh + 1) * 512], op=ALU.add),
                        reads=[pbuf, xb[cb][j]], writes=[hbuf_])
                P.dma("sp", lambda e, t=t, hi=hi: e.dma_start(out=hview[t], in_=hst[:, hi, :]), reads=[hbuf_])
    P.barrier()


TL = 128
C0 = math.exp(-0.5)


def col8(P, nc, es, name, v_ap):
    t = es.enter_context(nc.sbuf_tensor(name, [64, 8], F32))
    b = Buf(name)
    P.dma("sp", lambda e: e.dma_start(out=t[:, :], in_=v_ap.rearrange("o (h k) -> k (o h)", k=64),
                                      allow_slow_non_contiguous=True), writes=[b])
    return t, b


def phase_rwkv(P, nc, C, A, yr_dram):
    x_dram = A["x"]
    with ExitStack() as es:
        sb = lambda name, shape, dt=F32: es.enter_context(nc.sbuf_tensor(name, shape, dt))
        wr = sb("wr", [128, 8, RWKV_IN], BF16)
        wmu = sb("wmu", [128, 8, RWKV_IN], BF16)
        wr_b, wmu_b, mub_b = Buf(), Buf(), Buf()
        load_weight_bf16(P, nc, wr, wr_b, A["w_in"], ATTN_IN, ATTN_IN + RWKV_IN, 8)
        with nc.sbuf_tensor("mub", [128, RWKV_IN], F32) as mub:
            P.dma("sp", lambda e: e.dma_start(out=mub[:], in_=A["rwkv_mu"].partition_broadcast(128)), writes=[mub_b])
            for kc in range(8):
                P.op("pool", lambda e, kc=kc: e.tensor_tensor(out=wmu[:, kc, :], in0=wr[:, kc, :], in1=mub[:],
                                                              op=ALU.mult), reads=[wr_b, mub_b], writes=[wmu_b])
        P.barrier()
        w2 = sb("w2", [64, 512], BF16)
        a2 = sb("a2", [64, 512], BF16)
        g2 = sb("g2", [64, 3, 512], BF16)
        lw_b = Buf()
        P.dma("pool", lambda e: e.dma_start(out=w2[:], in_=A["rwkv_w2"]), writes=[lw_b])
        P.dma("pool", lambda e: e.dma_start(out=a2[:], in_=A["rwkv_a2"]), writes=[lw_b])
        P.dma("pool", lambda e: e.dma_start(out=g2[:, 0, :], in_=A["rwkv_g2"][0:64, :]), writes=[lw_b])
        P.dma("pool", lambda e: e.dma_start(out=g2[:, 1, :], in_=A["rwkv_g2"][64:128, :]), writes=[lw_b])
        P.dma("pool", lambda e: e.dma_start(out=g2[0:32, 2, :], in_=A["rwkv_g2"][128:160, :]), writes=[lw_b])
        cw0, b_w0 = col8(P, nc, es, "cw0", A["rwkv_w0"])
        ca0, b_a0 = col8(P, nc, es, "ca0", A["rwkv_a0"])
        ckk, b_kk = col8(P, nc, es, "ckk", A["rwkv_k_k"])
        cka, b_ka = col8(P, nc, es, "cka", A["rwkv_k_a"])
        crk, b_rk = col8(P, nc, es, "crk", A["rwkv_r_k"])
        clw, b_lw = col8(P, nc, es, "clw", A["rwkv_ln_w"])
        clb, b_lb = col8(P, nc, es, "clb", A["rwkv_ln_b"])
        msk = sb("rmsk", [64, 3, 64], F32)
        ones_bf = sb("ones_bf", [64, 64], BF16)
        rmask = sb("rmask", [64, 8, TL // 64, 64], F32)
        mk_b = Buf()

        def mkmasks(e):
            e.memset(msk[:], 1.0)
            e.memset(ones_bf[:], 1.0)
            e.memset(rmask[:], 1.0)
            e.memset(rmask[:, :, :, 0:1], 0.0)
            e.affine_select(out=msk[:, 0, :], in_=msk[:, 0, :], pattern=[[1, 64]], compare_op=ALU.is_ge,
                            fill=0.0, base=-1, channel_multiplier=-1)
            e.affine_select(out=msk[:, 1, :], in_=msk[:, 1, :], pattern=[[1, 64]], compare_op=ALU.is_ge,
                            fill=0.0, base=0, channel_multiplier=-1)
            return e.affine_select(out=msk[:, 2, :], in_=msk[:, 2, :], pattern=[[-1, 64]], compare_op=ALU.is_ge,
                                   fill=0.0, base=-1, channel_multiplier=1)

        P.op("pool", mkmasks, writes=[mk_b])
        mb3 = lambda i: msk[:, i, :].unsqueeze(1).broadcast_to([64, 8, 64])
        idb = C.ident_bf[0:64, 0:64]
        idf3 = C.ident_f[0:64, 0:64].unsqueeze(1).broadcast_to([64, 8, 64])
        bc = lambda col: col[:, :].unsqueeze(2).broadcast_to([64, 8, TL])

        nm = Normer(P, nc, es, C, A["mix_norm"], "rn", nslots=1)
        xbuf = sb("rxbuf", [128, 1, D], F32)
        xr = Ring([(i, Buf()) for i in range(1)])
        xnx = sb("xnx", [128, 2, 8, TL + 1], BF16)
        xnb = [Buf(), Buf()]
        dxn = sb("dxn", [128, 8, TL], BF16)
        dxb = Buf()
        P.op("pool", lambda e: e.memset(xnx[:, 1, :, TL:TL + 1], 0.0), writes=[xnb[1]])
        F = {}
        FB = {}
        for nme, dt in [("r", F32), ("k", F32), ("sgd", F32), ("a", F32), ("kk", F32),
                        ("t1", F32), ("t2", F32), ("cum", F32), ("e1", F32), ("e2", F32), ("sqb", BF16)]:
            alias = {"e1": "t1", "e2": "a"}
            if nme in alias:
                F[nme] = F[alias[nme]]
                FB[nme] = FB[alias[nme]]
                continue
            F[nme] = sb("f_" + nme, [64, 8, TL], dt)
            FB[nme] = Buf(nme)
        X2 = {}
        X2B = {}
        for nme in ("rT", "aT", "bT", "kT", "bH", "kH", "vb", "g", "bon"):
            X2[nme] = sb("x_" + nme, [64, 2, 8, TL], BF16)
            X2B[nme] = [Buf(nme + "0"), Buf(nme + "1")]
        lora = sb("lora", [64, 5, TL], BF16)
        ztok = sb("ztok", [128, 2, 512], F32)
        ztr = Ring([(i, Buf()) for i in range(2)])
        lora_b = Buf()
        gC = sb("gC", [64, 2, 8, TL // 64], F32)
        gC_bs = [Buf(), Buf()]
        Sf = sb("Sf", [64, 8, 64], F32)
        Sb = sb("Sb", [64, 8, 64], BF16)
        St = sb("St", [64, 8, 64], F32)
        S_b, St_b = Buf(), Buf()
        P.op("dve", lambda e: e.memset(Sf[:], 0.0), writes=[S_b])
        P.op("dve", lambda e: e.memset(Sb[:], 0.0), reads=[S_b], writes=[S_b])
        ost = sb("rost", [64, 8, TL], BF16)
        ost_b = Buf()
        def pair(name, dt=BF16, n=2):
            t = sb(name, [64, n, 8, 64], dt)
            return t, [Buf() for _ in range(n)]
        Atok, Atok_b = pair("Atok")
        BHtok, BHtok_b = pair("BHtok")
        KHtok, KHtok_b = pair("KHtok")
        Vtok, Vtok_b = pair("Vtok")
        Mrb, Mrb_b = pair("Mrb")
        Mrk, Mrk_b = pair("Mrk")
        Lak, Lak_b = pair("Lak")
        Nn, Nn_b = pair("Nn", BF16, 4)
        Mm, Mm_b = pair("Mm", BF16, 4)
        Qq, Qq_b = pair("Qq", BF16, 4)
        WT, WT_b = pair("WT")
        Xx, Xx_b = pair("Xx")
        Uu, Uu_b = pair("Uu")
        Ys, Ys_b = pair("Ys", F32, 1)
        Yn, Yn_b = pair("Yn", BF16, 2)
        gst = sb("gst", [64, 2, 8, 6], F32)
        gst_b = [Buf(), Buf()]
        pb = es.enter_context(nc.psum_tensor("rpb", [128, 7, 512], F32))
        pr = Ring([(i, Buf()) for i in range(7)])
        pv = lambda i: pb[0:64, i, :].rearrange("p (h t) -> p h t", h=8)
        pvb = lambda i: pb[0:64, i, :].bitcast(BF16)[:, 0:512].rearrange("p (h t) -> p h t", h=8)

        xview = x_dram.rearrange("(t p) d -> t p d", p=128)

        def mm8(bank, parts, rd):
            i, bbuf = bank
            ops = [[(lf(h), rf(h)) for (lf, rf) in parts] for h in range(8)]

            def th(e):
                ins = None
                for h in range(8):
                    for pi, (l, r) in enumerate(ops[h]):
                        ins = e.matmul(pb[0:64, i, h * 64:(h + 1) * 64], lhsT=l, rhs=r,
                                       start=(pi == 0), stop=(pi == len(ops[h]) - 1))
                return ins

            P.op("pe", th, reads=rd, writes=[bbuf])

        def tr8(bank, src_fn, rd):
            i, bbuf = bank
            srcs = [src_fn(h) for h in range(8)]

            def th(e):
                ins = None
                v = pvb(i)
                for h in range(8):
                    ins = e.transpose(out=v[:, h, :], in_=srcs[h], identity=idb)
                return ins

            P.op("pe", th, reads=rd + [C.b], writes=[bbuf])

        NB = S // TL

        def prep(n):
            par = n % 2
            cb = n % 2
            pc = 1 - cb
            X = {k: X2[k][:, par] for k in X2}
            XB = {k: X2B[k][par] for k in X2}
            P.op("pool", lambda e: e.tensor_copy(out=xnx[:, cb, :, 0:1], in_=xnx[:, pc, :, TL:TL + 1]),
                 reads=[xnb[pc]], writes=[xnb[cb]])
            for j in range(TL // 128):
                xi, xb_ = xr.next()
                P.dma("sp", lambda e, t=n * (TL // 128) + j, xi=xi: e.dma_start(out=xbuf[:, xi, :], in_=xview[t]), writes=[xb_])
                nm.run(xbuf[:, xi, :], xb_, xnx[:, cb], xnb[cb], 1 + j * 128)
            P.op("pool", lambda e: e.tensor_tensor(out=dxn[:], in0=xnx[:, cb, :, 0:TL], in1=xnx[:, cb, :, 1:TL + 1],
                                                   op=ALU.subtract), reads=[xnb[cb]], writes=[dxb])
            yield

            def projtok(c0, ncol):
                bank = pr.next()
                i = bank[0]

                def th(e):
                    ins = None
                    for kc in range(8):
                        e.matmul(pb[:, i, 0:ncol], lhsT=xnx[:, cb, kc, 1:TL + 1], rhs=wr[:, kc, c0:c0 + ncol],
                                 start=(kc == 0), stop=False)
                    for kc in range(8):
                        ins = e.matmul(pb[:, i, 0:ncol], lhsT=dxn[:, kc, :], rhs=wmu[:, kc, c0:c0 + ncol],
                                       start=False, stop=(kc == 7))
                    return ins

                P.op("pe", th, reads=[wr_b, wmu_b, xnb[cb], dxb], writes=[bank[1]])
                zi, zb = ztr.next()
                P.op("act", lambda e: e.activation(out=ztok[:, zi, 0:ncol], in_=pb[:, i, 0:ncol], func=AF.Copy),
                     reads=[bank[1]], writes=[zb])
                return zi, zb

            def trz(zi, zb, cols, m):
                bank = pr.next()
                i = bank[0]

                def th(e):
                    ins = None
                    for q, c in enumerate(cols):
                        ins = e.transpose(out=pb[0:m, i, q * TL:(q + 1) * TL], in_=ztok[:, zi, c:c + m], identity=C.ident_f[:])
                    return ins

                P.op("pe", th, reads=[zb, C.b], writes=[bank[1]])
                return bank

            for qi, qn in enumerate(("r", "k", "v")):
                zi, zb = projtok(qi * 512, 512)
                yield
                for h0 in (0, 4):
                    bank = trz(zi, zb, [(h0 + q) * 64 for q in range(4)], 64)
                    dst, dstb = (X["vb"], XB["vb"]) if qn == "v" else (F[qn], FB[qn])
                    P.op("act", lambda e, dst=dst, h0=h0, i=bank[0]: e.activation(
                        out=dst[:, h0:h0 + 4, :], in_=pb[0:64, i, 0:4 * TL].rearrange("p (q t) -> p q t", q=4), func=AF.Copy),
                        reads=[bank[1]], writes=[dstb])
                    yield
            zi, zb = projtok(1536, 288)
            yield
            for li, (c0, m, fn) in enumerate([(0, 64, AF.Tanh), (64, 64, AF.Copy), (128, 64, AF.Sigmoid),
                                              (192, 64, AF.Sigmoid), (256, 32, AF.Sigmoid)]):
                bank = trz(zi, zb, [c0], m)
                P.op("act", lambda e, li=li, m=m, fn=fn, i=bank[0]: e.activation(out=lora[0:m, li, :], in_=pb[0:m, i, 0:TL],
                                                                                 func=fn),
                     reads=[bank[1]], writes=[lora_b])
                yield
            for h in range(8):
                bank = pr.next()
                P.op("pe", lambda e, h=h, i=bank[0]: e.matmul(pb[0:64, i, 0:TL], lhsT=w2[:, h * 64:(h + 1) * 64],
                                                              rhs=lora[:, 0, :], start=True, stop=True),
                     reads=[lw_b, lora_b], writes=[bank[1]])
                P.op("act", lambda e, h=h, i=bank[0]: e.activation(out=F["sgd"][:, h, :], in_=pb[0:64, i, 0:TL],
                                                                   func=AF.Sigmoid, bias=cw0[:, h:h + 1]),
                     reads=[bank[1], b_w0], writes=[FB["sgd"]])
                bank = pr.next()
                P.op("pe", lambda e, h=h, i=bank[0]: e.matmul(pb[0:64, i, 0:TL], lhsT=a2[:, h * 64:(h + 1) * 64],
                                                              rhs=lora[:, 1, :], start=True, stop=True),
                     reads=[lw_b, lora_b], writes=[bank[1]])
                P.op("act", lambda e, h=h, i=bank[0]: e.activation(out=F["a"][:, h, :], in_=pb[0:64, i, 0:TL],
                                                                   func=AF.Sigmoid, bias=ca0[:, h:h + 1]),
                     reads=[bank[1], b_a0], writes=[FB["a"]])
                bank = pr.next()

                def gmm(e, h=h, i=bank[0]):
                    e.matmul(pb[0:64, i, 0:TL], lhsT=g2[:, 0, h * 64:(h + 1) * 64], rhs=lora[:, 2, :], start=True, stop=False)
                    e.matmul(pb[0:64, i, 0:TL], lhsT=g2[:, 1, h * 64:(h + 1) * 64], rhs=lora[:, 3, :], start=False, stop=False)
                    return e.matmul(pb[0:64, i, 0:TL], lhsT=g2[0:32, 2, h * 64:(h + 1) * 64], rhs=lora[0:32, 4, :],
                                    start=False, stop=True)

                P.op("pe", gmm, reads=[lw_b, lora_b], writes=[bank[1]])
                P.op("act", lambda e, h=h, i=bank[0]: e.activation(out=X["g"][:, h, :], in_=pb[0:64, i, 0:TL], func=AF.Copy),
                     reads=[bank[1]], writes=[XB["g"]])
                yield
            P.op("dve", lambda e: e.tensor_tensor(out=F["kk"][:], in0=F["k"][:], in1=bc(ckk), op=ALU.mult),
                 reads=[FB["k"], b_kk], writes=[FB["kk"]])
            P.op("pool", lambda e: e.tensor_tensor(out=F["sqb"][:], in0=F["kk"][:], in1=F["kk"][:], op=ALU.mult),
                 reads=[FB["kk"]], writes=[FB["sqb"]])
            yield
            for h in range(8):
                bank = pr.next()
                P.op("pe", lambda e, h=h, i=bank[0]: e.matmul(pb[0:64, i, 0:TL], lhsT=ones_bf[:], rhs=F["sqb"][:, h, :],
                                                              start=True, stop=True),
                     reads=[mk_b, FB["sqb"]], writes=[bank[1]])
                P.op("act", lambda e, h=h, i=bank[0]: e.activation(out=F["t1"][:, h, :], in_=pb[0:64, i, 0:TL], func=AF.Sqrt),
                     reads=[bank[1]], writes=[FB["t1"]])
                if h % 2 == 1:
                    yield
            P.op("dve", lambda e: e.tensor_scalar(out=F["t1"][:], in0=F["t1"][:], scalar1=1e-12, scalar2=None,
                                                  op0=ALU.max), reads=[FB["t1"]], writes=[FB["t1"]])
            P.op("dve", lambda e: e.reciprocal(out=F["t1"][:], in_=F["t1"][:]), reads=[FB["t1"]], writes=[FB["t1"]])
            yield
            P.op("dve", lambda e: e.tensor_tensor(out=F["kk"][:], in0=F["kk"][:], in1=F["t1"][:], op=ALU.mult),
                 reads=[FB["kk"], FB["t1"]], writes=[FB["kk"]])
            P.op("dve", lambda e: e.scalar_tensor_tensor(out=F["t2"][:], in0=F["a"][:], scalar=-1.0, in1=bc(cka),
                                                         op0=ALU.add, op1=ALU.mult),
                 reads=[FB["a"], b_ka], writes=[FB["t2"]])
            yield
            P.op("dve", lambda e: e.scalar_tensor_tensor(out=F["k"][:], in0=F["t2"][:], scalar=1.0, in1=F["k"][:],
                                                         op0=ALU.add, op1=ALU.mult),
                 reads=[FB["t2"], FB["k"]], writes=[FB["k"]])
            P.op("pool", lambda e: e.tensor_tensor(out=F["t2"][:], in0=F["kk"][:], in1=F["a"][:], op=ALU.mult),
                 reads=[FB["kk"], FB["a"]], writes=[FB["t2"]])
            yield
            P.op("dve", lambda e: e.tensor_tensor_scan(out=F["cum"][:].rearrange("p h t -> p (h t)"),
                                                       data0=rmask[:].rearrange("p h c t -> p (h c t)"),
                                                       data1=F["sgd"][:].rearrange("p h t -> p (h t)"),
                                                       initial=0.0, op0=ALU.mult, op1=ALU.add),
                 reads=[FB["sgd"], mk_b], writes=[FB["cum"]])
            cum4 = F["cum"][:].rearrange("p h (c t) -> p h c t", t=64)
            yield
            P.op("act", lambda e: e.activation(out=F["e1"][:], in_=F["cum"][:], func=AF.Exp, scale=-C0),
                 reads=[FB["cum"]], writes=[FB["e1"]])
            P.op("dve", lambda e: e.tensor_tensor(out=X["rT"][:], in0=F["r"][:], in1=F["e1"][:], op=ALU.mult),
                 reads=[FB["r"], FB["e1"]], writes=[XB["rT"]])
            P.op("act", lambda e: e.activation(out=gC[:, par], in_=cum4[:, :, :, 63], func=AF.Exp, scale=-C0),
                 reads=[FB["cum"]], writes=[gC_bs[par]])
            yield
            P.op("pool", lambda e: e.tensor_tensor(out=F["e2"][:], in0=F["cum"][:], in1=F["sgd"][:], op=ALU.subtract),
                 reads=[FB["cum"], FB["sgd"]], writes=[FB["e2"]])
            P.op("act", lambda e: e.activation(out=F["e2"][:], in_=F["e2"][:], func=AF.Exp, scale=-C0),
                 reads=[FB["e2"]], writes=[FB["e2"]])
            P.op("dve", lambda e: e.scalar_tensor_tensor(out=X["aT"][:], in0=F["kk"][:], scalar=-1.0, in1=F["e2"][:],
                                                         op0=ALU.mult, op1=ALU.mult),
                 reads=[FB["kk"], FB["e2"]], writes=[XB["aT"]])
            yield
            P.op("act", lambda e: e.activation(out=F["e1"][:], in_=F["cum"][:], func=AF.Exp, scale=C0),
                 reads=[FB["cum"]], writes=[FB["e1"]])
            P.op("dve", lambda e: e.tensor_tensor(out=X["bT"][:], in0=F["t2"][:], in1=F["e1"][:], op=ALU.mult),
                 reads=[FB["t2"], FB["e1"]], writes=[XB["bT"]])
            P.op("pool", lambda e: e.tensor_tensor(out=X["kT"][:], in0=F["k"][:], in1=F["e1"][:], op=ALU.mult),
                 reads=[FB["k"], FB["e1"]], writes=[XB["kT"]])
            yield
            P.op("dve", lambda e: e.tensor_tensor(out=F["e2"][:].rearrange("p h (c t) -> p h c t", t=64),
                                                  in0=cum4[:, :, :, 63:64].broadcast_to([64, 8, TL // 64, 64]), in1=cum4,
                                                  op=ALU.subtract),
                 reads=[FB["cum"]], writes=[FB["e2"]])
            P.op("act", lambda e: e.activation(out=F["e2"][:], in_=F["e2"][:], func=AF.Exp, scale=-C0),
                 reads=[FB["e2"]], writes=[FB["e2"]])
            yield
            P.op("dve", lambda e: e.tensor_tensor(out=X["bH"][:], in0=F["t2"][:], in1=F["e2"][:], op=ALU.mult),
                 reads=[FB["t2"], FB["e2"]], writes=[XB["bH"]])
            P.op("pool", lambda e: e.tensor_tensor(out=X["kH"][:], in0=F["k"][:], in1=F["e2"][:], op=ALU.mult),
                 reads=[FB["k"], FB["e2"]], writes=[XB["kH"]])
            yield
            P.op("dve", lambda e: e.tensor_tensor(out=F["t1"][:], in0=F["r"][:], in1=F["k"][:], op=ALU.mult),
                 reads=[FB["r"], FB["k"]], writes=[FB["t1"]])
            P.op("pool", lambda e: e.tensor_tensor(out=F["sqb"][:], in0=F["t1"][:], in1=bc(crk), op=ALU.mult),
                 reads=[FB["t1"], b_rk], writes=[FB["sqb"]])
            yield
            for h in range(8):
                bank = pr.next()
                P.op("pe", lambda e, h=h, i=bank[0]: e.matmul(pb[0:64, i, 0:TL], lhsT=ones_bf[:], rhs=F["sqb"][:, h, :],
                                                              start=True, stop=True),
                     reads=[mk_b, FB["sqb"]], writes=[bank[1]])
                P.op("dve", lambda e, h=h, i=bank[0]: e.tensor_tensor(out=X["bon"][:, h, :], in0=pb[0:64, i, 0:TL],
                                                                      in1=X["vb"][:, h, :], op=ALU.mult),
                     reads=[bank[1], XB["vb"]], writes=[XB["bon"]])
                if h % 2 == 1:
                    yield

        def chunk_pre(n, c, out):
            par = n % 2
            X = {k: X2[k][:, par] for k in X2}
            XB = {k: X2B[k][par] for k in X2}
            cs = slice(c * 64, (c + 1) * 64)
            for (dst, dbs, src) in ((Atok, Atok_b, "aT"), (BHtok, BHtok_b, "bH"), (KHtok, KHtok_b, "kH"), (Vtok, Vtok_b, "vb")):
                bank = pr.next()
                tr8(bank, lambda h, src=src: X[src][:, h, cs], [XB[src]])
                P.op("act", lambda e, dst=dst, i=bank[0]: e.activation(out=dst[:, c], in_=pvb(i), func=AF.Copy),
                     reads=[bank[1]], writes=[dbs[c]])
                yield

            def gmat(lname, rname, mi, dst, dbs, slot):
                bank = pr.next()
                mm8(bank, [(lambda h: X[lname][:, h, cs], lambda h: X[rname][:, h, cs])], [XB[lname], XB[rname]])
                P.op("dve", lambda e, i=bank[0]: e.tensor_tensor(out=dst[:, slot], in0=pv(i), in1=mb3(mi), op=ALU.mult),
                     reads=[bank[1], mk_b], writes=[dbs[slot]])

            base = 2 * c
            gmat("bT", "aT", 0, Mm, Mm_b, base)
            yield
            gmat("bT", "rT", 1, Mrb, Mrb_b, c)
            yield
            gmat("kT", "aT", 0, Lak, Lak_b, c)
            yield
            gmat("kT", "rT", 1, Mrk, Mrk_b, c)
            yield
            gmat("aT", "bT", 2, Nn, Nn_b, base)
            yield
            P.op("pool", lambda e: e.tensor_tensor(out=Qq[:, base], in0=Mm[:, base], in1=idf3, op=ALU.add),
                 reads=[Mm_b[base], C.b], writes=[Qq_b[base]])
            ni = mi_ = qi_ = base
            for lvl in range(1, 6):
                nn_ = base + (1 - (ni - base))
                nm_ = base + (1 - (mi_ - base))
                nq_ = base + (1 - (qi_ - base))
                bank = pr.next()
                mm8(bank, [(lambda h: Mm[:, mi_, h, :], lambda h: Nn[:, ni, h, :])], [Mm_b[mi_], Nn_b[ni]])
                if lvl < 5:
                    bank2 = pr.next()
                    mm8(bank2, [(lambda h: Nn[:, ni, h, :], lambda h: Mm[:, mi_, h, :])], [Mm_b[mi_], Nn_b[ni]])
                P.op("act", lambda e, nn_=nn_, i=bank[0]: e.activation(out=Nn[:, nn_], in_=pv(i), func=AF.Copy),
                     reads=[bank[1]], writes=[Nn_b[nn_]])
                if lvl < 5:
                    P.op("dve", lambda e, nm_=nm_, i=bank2[0]: e.tensor_copy(out=Mm[:, nm_], in_=pv(i)),
                         reads=[bank2[1]], writes=[Mm_b[nm_]])
                    mi_ = nm_
                ni = nn_
                yield
                bank3 = pr.next()
                mm8(bank3, [(lambda h: Nn[:, ni, h, :], lambda h: Qq[:, qi_, h, :])], [Qq_b[qi_], Nn_b[ni]])
                P.op("dve", lambda e, nq_=nq_, qo=qi_, i=bank3[0]: e.tensor_tensor(out=Qq[:, nq_], in0=pv(i), in1=Qq[:, qo],
                                                                                   op=ALU.add),
                     reads=[bank3[1], Qq_b[qi_]], writes=[Qq_b[nq_]])
                qi_ = nq_
                yield
            bank = pr.next()
            mm8(bank, [(lambda h: Atok[:, c, h, :], lambda h: Qq[:, qi_, h, :])], [Atok_b[c], Qq_b[qi_]])
            P.op("act", lambda e, i=bank[0]: e.activation(out=WT[:, c], in_=pv(i), func=AF.Copy),
                 reads=[bank[1]], writes=[WT_b[c]])
            bank = pr.next()
            mm8(bank, [(lambda h: Lak[:, c, h, :], lambda h: Vtok[:, c, h, :])], [Lak_b[c], Vtok_b[c]])
            P.op("dve", lambda e, i=bank[0]: e.tensor_copy(out=Xx[:, c], in_=pv(i)), reads=[bank[1]], writes=[Xx_b[c]])
            out["q"] = qi_
            yield

        def chain(n, c, qi_):
            par = n % 2
            X = {k: X2[k][:, par] for k in X2}
            XB = {k: X2B[k][par] for k in X2}
            cs = slice(c * 64, (c + 1) * 64)
            bank = pr.next()
            mm8(bank, [(lambda h: WT[:, c, h, :], lambda h: Sb[:, h, :]),
                       (lambda h: Qq[:, qi_, h, :], lambda h: Xx[:, c, h, :])], [WT_b[c], S_b, Qq_b[qi_], Xx_b[c]])
            P.op("act", lambda e, i=bank[0]: e.activation(out=Uu[:, c], in_=pv(i), func=AF.Copy),
                 reads=[bank[1]], writes=[Uu_b[c]])
            banky = pr.next()
            mm8(banky, [(lambda h: X["rT"][:, h, cs], lambda h: Sb[:, h, :]),
                        (lambda h: Mrb[:, c, h, :], lambda h: Uu[:, c, h, :]),
                        (lambda h: Mrk[:, c, h, :], lambda h: Vtok[:, c, h, :])],
                [XB["rT"], S_b, Mrb_b[c], Uu_b[c], Mrk_b[c], Vtok_b[c]])
            banks = pr.next()
            mm8(banks, [(lambda h: BHtok[:, c, h, :], lambda h: Uu[:, c, h, :]),
                        (lambda h: KHtok[:, c, h, :], lambda h: Vtok[:, c, h, :])], [BHtok_b[c], Uu_b[c], KHtok_b[c], Vtok_b[c]])
            P.op("dve", lambda e: e.tensor_tensor(out=St[:], in0=Sf[:],
                                                  in1=gC[:, par, :, c:c + 1].broadcast_to([64, 8, 64]), op=ALU.mult),
                 reads=[S_b, gC_bs[par]], writes=[St_b])
            P.op("dve", lambda e, i=banks[0]: e.tensor_tensor(out=Sf[:], in0=pv(i), in1=St[:], op=ALU.add),
                 reads=[banks[1], St_b], writes=[S_b])
            P.op("act", lambda e: e.activation(out=Sb[:], in_=Sf[:], func=AF.Copy), reads=[S_b], writes=[S_b])
            yield
            y_b, yq_b, g_b = Ys_b[0], St_b, gst_b[c]
            P.op("act", lambda e, i=banky[0]: e.activation(out=Ys[:, 0], in_=pv(i), func=AF.Copy),
                 reads=[banky[1]], writes=[y_b])
            P.op("pool", lambda e: e.tensor_tensor(out=St[:], in0=Ys[:, 0], in1=Ys[:, 0], op=ALU.mult),
                 reads=[y_b], writes=[yq_b])
            P.op("dve", lambda e: e.tensor_reduce(out=gst[:, c, :, 0], in_=Ys[:, 0], axis=AX.X, op=ALU.add),
                 reads=[y_b], writes=[g_b])
            P.op("dve", lambda e: e.tensor_reduce(out=gst[:, c, :, 1], in_=St[:], axis=AX.X, op=ALU.add),
                 reads=[yq_b, g_b], writes=[g_b])
            yield
            P.op("dve", lambda e: e.tensor_scalar(out=gst[:, c, :, 2], in0=gst[:, c, :, 0], scalar1=1.0 / 64,
                                                  scalar2=None, op0=ALU.mult), reads=[g_b], writes=[g_b])
            P.op("dve", lambda e: e.tensor_tensor(out=gst[:, c, :, 3], in0=gst[:, c, :, 2], in1=gst[:, c, :, 2],
                                                  op=ALU.mult), reads=[g_b], writes=[g_b])
            P.op("dve", lambda e: e.scalar_tensor_tensor(out=gst[:, c, :, 4], in0=gst[:, c, :, 1], scalar=1.0 / 64,
                                                         in1=gst[:, c, :, 3], op0=ALU.mult, op1=ALU.subtract),
                 reads=[g_b], writes=[g_b])
            yield
            P.op("act", lambda e: e.activation(out=gst[:, c, :, 5], in_=gst[:, c, :, 4], func=AF.Sqrt,
                                               bias=C.eps[0:64, 1:2]), reads=[g_b, C.b], writes=[g_b])
            P.op("dve", lambda e: e.reciprocal(out=gst[:, c, :, 5], in_=gst[:, c, :, 5]), reads=[g_b], writes=[g_b])
            P.op("dve", lambda e: e.tensor_tensor(out=Ys[:, 0], in0=Ys[:, 0],
                                                  in1=gst[:, c, :, 2:3].broadcast_to([64, 8, 64]),
                                                  op=ALU.subtract), reads=[y_b, g_b], writes=[y_b])
            yield
            P.op("dve", lambda e: e.tensor_tensor(out=Yn[:, c], in0=Ys[:, 0],
                                                  in1=gst[:, c, :, 5:6].broadcast_to([64, 8, 64]),
                                                  op=ALU.mult), reads=[y_b, g_b], writes=[Yn_b[c]])
            bank = pr.next()
            tr8(bank, lambda h: Yn[:, c, h, :], [Yn_b[c]])
            bc64 = lambda col: col[:, :].unsqueeze(2).broadcast_to([64, 8, 64])
            P.op("dve", lambda e, i=bank[0]: e.tensor_tensor(out=Ys[:, 0], in0=pvb(i), in1=bc64(clw), op=ALU.mult),
                 reads=[bank[1], b_lw, Yn_b[c]], writes=[y_b])
            P.op("pool", lambda e: e.tensor_tensor(out=Ys[:, 0], in0=Ys[:, 0], in1=bc64(clb), op=ALU.add),
                 reads=[y_b, b_lb], writes=[y_b])
            yield
            P.op("dve", lambda e: e.tensor_tensor(out=Ys[:, 0], in0=Ys[:, 0], in1=X["bon"][:, :, cs], op=ALU.add),
                 reads=[y_b, XB["bon"]], writes=[y_b])
            P.op("dve", lambda e: e.tensor_tensor(out=ost[:, :, cs], in0=Ys[:, 0], in1=X["g"][:, :, cs], op=ALU.mult),
                 reads=[y_b, XB["g"]], writes=[ost_b])
            yield

        def scan(n):
            par = n % 2
            X = {k: X2[k][:, par] for k in X2}
            XB = {k: X2B[k][par] for k in X2}
            outs = [{} for _ in range(TL // 64)]
            gens = [chunk_pre(n, c, outs[c]) for c in range(TL // 64)]
            alive = [True] * len(gens)
            while any(alive):
                for gi_, g_ in enumerate(gens):
                    if alive[gi_]:
                        try:
                            next(g_)
                        except StopIteration:
                            alive[gi_] = False
                yield
            for c in range(TL // 64):
                for _ in chain(n, c, outs[c]["q"]):
                    yield
            P.dma("sp", lambda e: e.dma_start(
                out=yr_dram.rearrange("(h v) s -> v h s", v=64)[:, :, n * TL:(n + 1) * TL], in_=ost[:]),
                reads=[ost_b])
            yield

        def run2(f, b):
            fa, ba = f is not None, b is not None
            while fa or ba:
                if fa:
                    try:
                        next(f)
                    except StopIteration:
                        fa = False
                if ba:
                    try:
                        next(b)
                    except StopIteration:
                        ba = False

        for n in range(NB + 1):
            run2(prep(n) if n < NB else None, scan(n - 1) if n >= 1 else None)
    P.barrier()


NITER = 16


def phase_attn(P, nc, C, A, ya_dram):
    x_dram = A["x"]
    with ExitStack() as es:
        sb = lambda name, shape, dt=F32: es.enter_context(nc.sbuf_tensor(name, shape, dt))
        wa = sb("wa", [128, 8, ATTN_IN + 64], BF16)
        wa_b = Buf()
        load_weight_bf16(P, nc, wa, wa_b, A["w_in"], 0, ATTN_IN, 8)
        wv_ = A["w_in"].rearrange("(kc p) n -> p kc n", p=128)
        for kc in range(8):
            P.dma("pool", lambda e, kc=kc: e.dma_start(out=wa[:, kc, ATTN_IN:ATTN_IN + 64], in_=wv_[:, kc, 2048:2112]),
                  writes=[wa_b])
        kT = sb("kT", [128, 4, S], BF16)
        kiT = sb("kiT", [128, S], BF16)
        vaug = sb("vaug", [128, NT, 8, 65], BF16)
        kT_bs = [Buf() for _ in range(NT)]
        kiT_bs = [Buf() for _ in range(NT)]
        va_bs = [Buf() for _ in range(NT)]
        P.op("pool", lambda e: e.memset(vaug[:, :, :, 64:65], 1.0), writes=va_bs)
        gqk = sb("gqk", [128, 2], F32)
        gqk_b = Buf()
        for half in range(2):
            P.dma("sp", lambda e, half=half: e.dma_start(out=gqk[half * 64:(half + 1) * 64, 0:1],
                                                         in_=A["attn_q_norm"].rearrange("o d -> d o"),
                                                         allow_slow_non_contiguous=True), writes=[gqk_b])
            P.dma("sp", lambda e, half=half: e.dma_start(out=gqk[half * 64:(half + 1) * 64, 1:2],
                                                         in_=A["attn_k_norm"].rearrange("o d -> d o"),
                                                         allow_slow_non_contiguous=True), writes=[gqk_b])
        P.op("dve", lambda e: e.tensor_scalar(out=gqk[:, 0:1], in0=gqk[:, 0:1], scalar1=0.125, scalar2=None, op0=ALU.mult),
             reads=[gqk_b], writes=[gqk_b])
        btf = sb("btf", [128, 8, 2, 128], F32)
        bt = sb("bt", [128, 8, 2, 128], BF16)
        b31 = sb("b31_sb", [128, 8], F32)
        bt_b = Buf()
        P.dma("sp", lambda e: e.dma_start(out=btf[:], in_=A["bias_tiles"].rearrange("h c s t -> s h c t")), writes=[bt_b])
        P.dma("sp", lambda e: e.dma_start(out=b31[:], in_=A["b31"].partition_broadcast(128)), writes=[bt_b])
        P.op("dve", lambda e: e.tensor_tensor(out=bt[:].rearrange("p h c t -> p h (c t)"),
                                              in0=btf[:].rearrange("p h c t -> p h (c t)"),
                                              in1=b31[:, :].unsqueeze(2).broadcast_to([128, 8, 256]), op=ALU.subtract),
             reads=[bt_b], writes=[bt_b])
        cmask = sb("cmask", [128, 128], F32)
        onesblk = sb("onesblk", [128, 128], BF16)
        cm_b = Buf()
        pw2 = sb("pw2", [128, 2 * NITER], F32)
        halfs = sb("halfs", [128, 2 * NITER], F32)
        hf_b = Buf()

        def mkc(e):
            e.memset(cmask[:], 0.0)
            e.affine_select(out=cmask[:], in_=cmask[:], pattern=[[-1, 128]], compare_op=ALU.is_ge, fill=NEG, base=0,
                            channel_multiplier=1)
            for j in range(NITER):
                e.memset(pw2[:, j:j + 1], 0.5 ** (j + 1))
                e.memset(pw2[:, NITER + j:NITER + j + 1], 0.5 ** (j + 2) if j < NITER - 1 else 0.5 ** NITER)
            e.memset(onesblk[:], 0.0)
            e.memset(onesblk[0:64, 0:64], 1.0)
            return e.memset(onesblk[64:128, 64:128], 1.0)

        P.op("pool", mkc, writes=[cm_b])
        nm = Normer(P, nc, es, C, A["mix_norm"], "an", nslots=1)
        xbuf = sb("axbuf", [128, 2, D], F32)
        xr = Ring([(i, Buf()) for i in range(2)])
        xn = sb("axn", [128, 8, 128], BF16)
        xn_b = Buf()
        q_i = sb("q_i", [128, 2, 4, 128], BF16)
        qi_i = sb("qi_i", [128, 4, 128], BF16)
        wi_i = sb("wi_i", [128, 8], F32)
        q_bs, qi_b, wi_b = [Buf(), Buf()], Buf(), Buf()
        sq = sb("asq", [128, 512], BF16)
        rn = sb("arn", [128, 512], F32)
        sq_b, rn_b = Buf(), Buf()
        sc = sb("sc", [128, S], F32)
        sc_b = Buf()
        rbuf = sb("rbuf", [128, 2, 512], F32)
        rr = Ring([(i, Buf()) for i in range(2)])
        junk = sb("ajunk", [128, S], BF16)
        junk_b = Buf()
        bs = sb("bs", [128, 8], F32)
        bs_b = Buf()
        maskT = sb("maskT", [128, 2, NT, 128], BF16)
        mT_bs = [Buf(), Buf()]
        Et = sb("Et", [128, 3, 4, 128], BF16)
        er = Ring([(i, Buf()) for i in range(3)])
        Pt = sb("Pt", [128, 4, 4, 128], BF16)
        ptr = Ring([(i, Buf()) for i in range(4)])
        rden = sb("rden", [128, 2], F32)
        rdr = Ring([(i, Buf()) for i in range(2)])
        ytile = sb("ytile", [128, 512], BF16)
        yt_b = Buf()
        yst = sb("yst", [128, 4, 128], BF16)
        ys_b = Buf()
        pg = es.enter_context(nc.psum_tensor("apg", [128, 5, 512], F32))
        gr = Ring([(i, Buf()) for i in range(3)])
        lgr = Ring([(i, Buf()) for i in range(3, 5)])
        po = es.enter_context(nc.psum_tensor("apo", [128, 2, 512], F32))
        orr = Ring([(i, Buf()) for i in range(2)])
        pgb = lambda i: pg[:, i, :].bitcast(BF16)
        if SBUF_DEBUG:
            print("attn sbuf remaining", nc.sbuf_bytes_remaining)
        xview = x_dram.rearrange("(t p) d -> t p d", p=128)
        yav = ya_dram.rearrange("(m p) s -> p m s", p=128)

        def front(i):
            ts_ = slice(i * 128, (i + 1) * 128)
            nkb = i + 1
            W = nkb * 128
            q_b = q_bs[i % 2]
            kT_b, kiT_b, va_b = kT_bs[i], kiT_bs[i], va_bs[i]
            xi, xb_ = xr.next()
            P.dma("sp", lambda e, i=i, xi=xi: e.dma_start(out=xbuf[:, xi, :], in_=xview[i]), writes=[xb_])
            nm.run(xbuf[:, xi, :], xb_, xn, xn_b, 0)
            yield

            def proj4(c0, bank):
                bi, bb = bank

                def th(e):
                    ins = None
                    for m in range(4):
                        for kc in range(8):
                            ins = e.matmul(pg[:, bi, m * 128:(m + 1) * 128], lhsT=wa[:, kc, c0 + m * 128:c0 + (m + 1) * 128],
                                           rhs=xn[:, kc, :], start=(kc == 0), stop=(kc == 7))
                    return ins

                P.op("pe", th, reads=[wa_b, xn_b], writes=[bb])

            for which, c0 in ((0, 0), (1, 512)):
                bank = gr.next()
                proj4(c0, bank)
                P.op("act", lambda e, bi=bank[0]: e.activation(out=sq[:], in_=pg[:, bi, :], func=AF.Square),
                     reads=[bank[1]], writes=[sq_b])
                bank2 = gr.next()
                P.op("pe", lambda e, bi=bank2[0]: e.matmul(pg[:, bi, :], lhsT=onesblk[:], rhs=sq[:], start=True, stop=True),
                     reads=[cm_b, sq_b], writes=[bank2[1]])
                P.op("act", lambda e, bi=bank2[0]: e.activation(out=rn[:], in_=pg[:, bi, :], func=AF.Sqrt, scale=1.0 / 64,
                                                                bias=C.eps[:, 0:1]), reads=[bank2[1], C.b], writes=[rn_b])
                P.op("dve", lambda e: e.reciprocal(out=rn[:], in_=rn[:]), reads=[rn_b], writes=[rn_b])
                if which == 0:
                    P.op("dve", lambda e, bi=bank[0]: e.scalar_tensor_tensor(
                        out=q_i[:, i % 2].rearrange("p m t -> p (m t)"), in0=pg[:, bi, :], scalar=gqk[:, 0:1], in1=rn[:],
                        op0=ALU.mult, op1=ALU.mult), reads=[bank[1], rn_b, gqk_b], writes=[q_b])
                else:
                    P.op("dve", lambda e, bi=bank[0], ts_=ts_: e.scalar_tensor_tensor(
                        out=kT[:, :, ts_], in0=pg[:, bi, :].rearrange("p (m t) -> p m t", m=4), scalar=gqk[:, 1:2],
                        in1=rn[:].rearrange("p (m t) -> p m t", m=4), op0=ALU.mult, op1=ALU.mult),
                        reads=[bank[1], rn_b, gqk_b], writes=[kT_b])
                yield
            bank = gr.next()
            proj4(1536, bank)
            P.op("act", lambda e, bi=bank[0]: e.activation(out=qi_i[:].rearrange("p m t -> p (m t)"), in_=pg[:, bi, :],
                                                           func=AF.Copy), reads=[bank[1]], writes=[qi_b])
            yield
            bank = gr.next()

            def kiw(e, bi=bank[0]):
                for kc in range(8):
                    e.matmul(pg[0:64, bi, 0:128], lhsT=wa[:, kc, 2048:2112], rhs=xn[:, kc, :], start=(kc == 0), stop=(kc == 7))
                for kc in range(8):
                    e.matmul(pg[64:128, bi, 0:128], lhsT=wa[:, kc, ATTN_IN:ATTN_IN + 64], rhs=xn[:, kc, :], start=(kc == 0),
                             stop=(kc == 7))
                ins = None
                for kc in range(8):
                    ins = e.matmul(pg[:, bi, 128:136], lhsT=xn[:, kc, :], rhs=wa[:, kc, 2112:2120], start=(kc == 0), stop=(kc == 7))
                return ins

            P.op("pe", kiw, reads=[wa_b, xn_b], writes=[bank[1]])
            P.op("act", lambda e, bi=bank[0], ts_=ts_: e.activation(out=kiT[:, ts_], in_=pg[:, bi, 0:128], func=AF.Copy),
                 reads=[bank[1]], writes=[kiT_b])
            P.op("dve", lambda e, bi=bank[0]: e.tensor_copy(out=wi_i[:], in_=pg[:, bi, 128:136]), reads=[bank[1]], writes=[wi_b])
            yield
            bank = gr.next()

            def vmm(e, bi=bank[0]):
                ins = None
                for kc in range(8):
                    ins = e.matmul(pg[:, bi, :], lhsT=xn[:, kc, :], rhs=wa[:, kc, 1024:1536], start=(kc == 0), stop=(kc == 7))
                return ins

            P.op("pe", vmm, reads=[wa_b, xn_b], writes=[bank[1]])
            P.op("act", lambda e, bi=bank[0], i=i: e.activation(out=vaug[:, i, :, 0:64],
                                                                in_=pg[:, bi, :].rearrange("p (h d) -> p h d", h=8), func=AF.Copy),
                 reads=[bank[1]], writes=[va_b])
            yield

            for gk in range((nkb + 3) // 4):
                w_ = min(512, W - gk * 512)
                for h in range(8):
                    hb = (h % 2) * 64
                    bank = gr.next()
                    P.op("pe", lambda e, bi=bank[0], h=h, hb=hb, gk=gk, w_=w_: e.matmul(
                        pg[:, bi, 0:w_], lhsT=qi_i[hb:hb + 64, h // 2, :], rhs=kiT[hb:hb + 64, gk * 512:gk * 512 + w_],
                        start=True, stop=True), reads=[qi_b] + kiT_bs[gk * 4:gk * 4 + (w_ // 128)], writes=[bank[1]])
                    ri, rb_ = rr.next()
                    P.op("act", lambda e, bi=bank[0], ri=ri, w_=w_: e.activation(out=rbuf[:, ri, 0:w_], in_=pg[:, bi, 0:w_],
                                                                                 func=AF.Relu), reads=[bank[1]], writes=[rb_])
                    if h == 0:
                        P.op("dve", lambda e, ri=ri, gk=gk, w_=w_: e.tensor_scalar(
                            out=sc[:, gk * 512:gk * 512 + w_], in0=rbuf[:, ri, 0:w_], scalar1=wi_i[:, 0:1], scalar2=None,
                            op0=ALU.mult), reads=[rb_, wi_b], writes=[sc_b])
                    else:
                        P.op("dve", lambda e, ri=ri, gk=gk, w_=w_, h=h: e.scalar_tensor_tensor(
                            out=sc[:, gk * 512:gk * 512 + w_], in0=rbuf[:, ri, 0:w_], scalar=wi_i[:, h:h + 1],
                            in1=sc[:, gk * 512:gk * 512 + w_], op0=ALU.mult, op1=ALU.add), reads=[rb_, wi_b, sc_b], writes=[sc_b])
                    yield
            P.op("dve", lambda e, ts_=ts_: e.tensor_tensor(out=sc[:, ts_], in0=sc[:, ts_], in1=cmask[:], op=ALU.add),
                 reads=[sc_b, cm_b], writes=[sc_b])
            if i < 2:
                P.op("dve", lambda e: e.memset(bs[:, 0:1], -1.0e29), writes=[bs_b])
            else:
                P.op("dve", lambda e, i=i: e.tensor_reduce(out=bs[:, 0:1], in_=sc[:, 0:i * 128], axis=AX.X, op=ALU.min),
                     reads=[sc_b], writes=[bs_b])
                P.op("dve", lambda e, W=W: e.tensor_reduce(out=bs[:, 6:7], in_=sc[:, 0:W], axis=AX.X, op=ALU.max),
                     reads=[sc_b, bs_b], writes=[bs_b])
                P.op("dve", lambda e: e.tensor_tensor(out=bs[:, 1:2], in0=bs[:, 6:7], in1=bs[:, 0:1], op=ALU.subtract),
                     reads=[bs_b], writes=[bs_b])
                P.op("dve", lambda e: e.tensor_tensor(out=halfs[:], in0=bs[:, 1:2].broadcast_to([128, 2 * NITER]), in1=pw2[:],
                                                      op=ALU.mult), reads=[bs_b, cm_b], writes=[hf_b])
                P.op("dve", lambda e: e.tensor_tensor(out=bs[:, 3:4], in0=bs[:, 0:1], in1=halfs[:, 0:1], op=ALU.add),
                     reads=[bs_b, hf_b], writes=[bs_b])
                for it in range(NITER):
                    P.op("dve", lambda e: e.tensor_scalar(out=junk[:, 0:W], in0=sc[:, 0:W], scalar1=bs[:, 3:4], scalar2=None,
                                                          op0=ALU.is_ge, op1=ALU.add, accum_out=bs[:, 4:5]),
                         reads=[sc_b, bs_b], writes=[junk_b, bs_b])
                    P.op("dve", lambda e, it=it: e.scalar_tensor_tensor(out=bs[:, 5:6], in0=bs[:, 4:5], scalar=TOPK - 0.5,
                                                                        in1=halfs[:, it:it + 1], op0=ALU.is_ge, op1=ALU.mult),
                         reads=[bs_b, hf_b], writes=[bs_b])
                    P.op("dve", lambda e, it=it: e.scalar_tensor_tensor(out=bs[:, 3:4], in0=bs[:, 5:6],
                                                                        scalar=halfs[:, NITER + it:NITER + it + 1],
                                                                        in1=bs[:, 3:4], op0=ALU.subtract, op1=ALU.add),
                         reads=[bs_b, hf_b], writes=[bs_b])
                    yield
                P.op("dve", lambda e: e.tensor_copy(out=bs[:, 0:1], in_=bs[:, 3:4]), reads=[bs_b], writes=[bs_b])
            P.op("dve", lambda e, W=W: e.tensor_scalar(out=junk[:, 0:W], in0=sc[:, 0:W], scalar1=bs[:, 0:1], scalar2=None,
                                                       op0=ALU.is_ge), reads=[sc_b, bs_b], writes=[junk_b])
            for j0 in range(0, nkb, 8):
                nb = min(8, nkb - j0)
                bank = gr.next()

                def trm(e, bi=bank[0], j0=j0, nb=nb):
                    ins = None
                    for jj in range(nb):
                        ins = e.transpose(out=pgb(bi)[:, jj * 128:(jj + 1) * 128], in_=junk[:, (j0 + jj) * 128:(j0 + jj + 1) * 128],
                                          identity=C.ident_bf[:])
                    return ins

                P.op("pe", trm, reads=[junk_b, C.b], writes=[bank[1]])
                P.op("act", lambda e, bi=bank[0], j0=j0, nb=nb: e.activation(
                    out=maskT[:, i % 2, j0:j0 + nb, :].rearrange("p j t -> p (j t)"), in_=pgb(bi)[:, 0:nb * 128], func=AF.Copy),
                    reads=[bank[1]], writes=[mT_bs[i % 2]])
                yield
            yield

        def back(i):
            ts_ = slice(i * 128, (i + 1) * 128)
            nkb = i + 1
            q_b = q_bs[i % 2]
            mT_b = mT_bs[i % 2]
            items = [(h, j0, min(4, nkb - j0)) for h in range(8) for j0 in range(0, nkb, 4)]
            DEPTH = 2
            st = {}
            obs = {}
            for k in range(len(items) + DEPTH):
                if k < len(items):
                    h, j0, nb = items[k]
                    hb = (h % 2) * 64
                    m = h // 2
                    if j0 == 0:
                        obs[h] = orr.next()
                    bank = lgr.next()

                    def qk(e, bi=bank[0], j0=j0, nb=nb, h=h, hb=hb, m=m):
                        ins = None
                        for jj in range(nb):
                            j = j0 + jj
                            near = j >= i - 1
                            ins = e.matmul(pg[:, bi, jj * 128:(jj + 1) * 128], lhsT=kT[hb:hb + 64, m, j * 128:(j + 1) * 128],
                                           rhs=q_i[hb:hb + 64, i % 2, m, :], start=True, stop=not near)
                            if near:
                                ins = e.matmul(pg[:, bi, jj * 128:(jj + 1) * 128], lhsT=C.ident_bf[:],
                                               rhs=bt[:, h, 0 if j == i else 1, :], start=False, stop=True)
                        return ins

                    P.op("pe", qk, reads=kT_bs[j0:j0 + nb] + [q_b, bt_b, C.b], writes=[bank[1]])
                    ei, eb = er.next()
                    P.op("act", lambda e, bi=bank[0], ei=ei, nb=nb: e.activation(
                        out=Et[:, ei, 0:nb, :].rearrange("p j t -> p (j t)"), in_=pg[:, bi, 0:nb * 128], func=AF.Exp),
                        reads=[bank[1]], writes=[eb])
                    pi, pb_ = ptr.next()
                    P.op("pool", lambda e, ei=ei, pi=pi, j0=j0, nb=nb: e.tensor_tensor(
                        out=Pt[:, pi, 0:nb, :], in0=Et[:, ei, 0:nb, :], in1=maskT[:, i % 2, j0:j0 + nb, :], op=ALU.mult),
                        reads=[eb, mT_b], writes=[pb_])
                    st[k] = (pi, pb_)
                kk = k - DEPTH
                if kk >= 0:
                    h, j0, nb = items[kk]
                    pi, pb_ = st.pop(kk)
                    ob = obs[h]

                    def pv(e, oi=ob[0], pi=pi, j0=j0, nb=nb, h=h):
                        ins = None
                        for jj in range(nb):
                            j = j0 + jj
                            ins = e.matmul(po[:, oi, 0:65], lhsT=Pt[:, pi, jj, :], rhs=vaug[:, j, h, :], start=(j == 0),
                                           stop=(j == i))
                        return ins

                    P.op("pe", pv, reads=[pb_] + va_bs[j0:j0 + nb], writes=[ob[1]])
                    if j0 + nb == nkb:
                        di, db = rdr.next()
                        P.op("dve", lambda e, oi=ob[0], di=di: e.reciprocal(out=rden[:, di:di + 1], in_=po[:, oi, 64:65]),
                             reads=[ob[1]], writes=[db])
                        P.op("act", lambda e, oi=ob[0], di=di, h=h: e.activation(out=ytile[:, h * 64:(h + 1) * 64],
                                                                                 in_=po[:, oi, 0:64], func=AF.Copy,
                                                                                 scale=rden[:, di:di + 1]),
                             reads=[ob[1], db], writes=[yt_b])
                yield
            bank = lgr.next()

            def try_(e, bi=bank[0]):
                ins = None
                for m in range(4):
                    ins = e.transpose(out=pgb(bi)[:, m * 128:(m + 1) * 128], in_=ytile[:, m * 128:(m + 1) * 128],
                                      identity=C.ident_bf[:])
                return ins

            P.op("pe", try_, reads=[yt_b, C.b], writes=[bank[1]])
            P.op("act", lambda e, bi=bank[0]: e.activation(out=yst[:].rearrange("p m t -> p (m t)"), in_=pgb(bi)[:, 0:512],
                                                           func=AF.Copy), reads=[bank[1]], writes=[ys_b])
            P.dma("sp", lambda e: e.dma_start(out=yav[:, :, ts_], in_=yst[:]), reads=[ys_b])
            yield

        def run2(f, b):
            fa, ba = f is not None, b is not None
            while fa or ba:
                if fa:
                    try:
                        next(f)
                    except StopIteration:
                        fa = False
                if ba:
                    try:
                        next(b)
                    except StopIteration:
                        ba = False

        for i in range(NT + 1):
            run2(front(i) if i < NT else None, back(i - 1) if i >= 1 else None)
    P.barrier()


WEIGHT_SPECS = [
    ("mix_norm", [1, D]), ("w_in", [D, 5992]), ("attn_q_norm", [1, 64]), ("attn_k_norm", [1, 64]),
    ("bias_tiles", [8, 2, 128, 128]), ("b31", [1, 8]), ("rwkv_mu", [1, RWKV_IN]), ("rwkv_w0", [1, 512]), ("rwkv_w2", [64, 512]),
    ("rwkv_a0", [1, 512]), ("rwkv_a2", [64, 512]), ("rwkv_g2", [160, 512]), ("rwkv_k_k", [1, 512]),
    ("rwkv_k_a", [1, 512]), ("rwkv_r_k", [1, 512]), ("rwkv_ln_w", [1, 512]), ("rwkv_ln_b", [1, 512]),
    ("w_branch_attn", [512, D]), ("w_branch_rwkv", [512, D]), ("w_out", [D, D]), ("ffn_norm", [1, D]),
    ("w_gate_up", [D, 2 * FFN_H]), ("w_down", [FFN_H, D]),
]


def build_program(phases=("attn", "rwkv", "merge", "ffn"), debug=False):
    nc = bass.Bass("TRN2", target_bir_lowering=False)
    A = {}
    A["x"] = nc.dram_tensor("x", [S, D], F32, kind="ExternalInput").ap()
    for name, shp in WEIGHT_SPECS:
        A[name] = nc.dram_tensor(name, shp, F32, kind="ExternalInput").ap()
    out = nc.dram_tensor("out", [S, D], F32, kind="ExternalOutput").ap()
    def kind(prod, cons):
        if not debug:
            return "Internal"
        if prod in phases and cons not in phases:
            return "ExternalOutput"
        if prod not in phases and cons in phases:
            return "ExternalInput"
        return "Internal"

    ya = nc.dram_tensor("ya_scr", [512, S], BF16, kind=kind("attn", "merge")).ap()
    yr = nc.dram_tensor("yr_scr", [512, S], BF16, kind=kind("rwkv", "merge")).ap()
    hs = nc.dram_tensor("h_scr", [S, D], F32, kind=kind("merge", "ffn")).ap()
    P = Prog(nc)
    final_ops = []
    with ExitStack() as es:
        C = make_consts(P, nc, es)
        P.barrier()
        if "attn" in phases:
            phase_attn(P, nc, C, A, ya)
        if "rwkv" in phases:
            phase_rwkv(P, nc, C, A, yr)
        if "merge" in phases:
            phase_merge(P, nc, C, A["x"], ya, yr, hs, A["mix_norm"], A["w_in"], A["w_branch_attn"],
                        A["w_branch_rwkv"], A["w_out"])
        if "ffn" in phases:
            phase_ffn(P, nc, C, hs, out, A["ffn_norm"], A["w_gate_up"], A["w_down"], final_ops)
        P.emit(final_wait_ops=final_ops)
    return nc


def t5_bucket_np(d):
    d = np.maximum(d, 0)
    max_exact = 16
    log_ratio = np.log(np.maximum(d, 1).astype(np.float32) / max_exact) / math.log(128 / max_exact)
    large = np.minimum(max_exact + (log_ratio * 16).astype(np.int32), 31)
    return np.where(d < max_exact, d, large)


def host_layout(inputs):
    w = {}
    for name, shp in WEIGHT_SPECS:
        if name in ("bias_tiles", "b31"):
            continue
        w[name] = np.ascontiguousarray(np.asarray(inputs[name], dtype=np.float32).reshape(shp))
    s_idx = np.arange(128)[:, None]
    t_idx = np.arange(128)[None, :]
    rb = np.asarray(inputs["rel_bias"], dtype=np.float32)
    tiles = np.empty((8, 2, 128, 128), np.float32)
    for cls in range(2):
        bk = t5_bucket_np(t_idx - s_idx + 128 * cls)
        tiles[:, cls] = np.transpose(rb[bk], (2, 0, 1))
    w["bias_tiles"] = tiles
    w["b31"] = np.ascontiguousarray(rb[31:32, :])
    return w


_NC_CACHE = {}


def kernel(**inputs):
    x = np.asarray(inputs["x"], dtype=np.float32)
    w = host_layout(inputs)
    if "nc" not in _NC_CACHE:
        _NC_CACHE["nc"] = build_program()
    nc = _NC_CACHE["nc"]
    in_maps = []
    for b in range(8):
        m = dict(w)
        m["x"] = np.ascontiguousarray(x[b])
        in_maps.append(m)
    res = run_bass_kernel_spmd(nc, in_maps, core_ids=list(range(8)))
    return np.stack([np.asarray(r["out"], dtype=np.float32) for r in res.results], axis=0)
```

```python
import math
from contextlib import ExitStack

import numpy as np
import concourse.bass as bass
import concourse.mybir as mybir
from concourse.bass_utils import run_bass_kernel_spmd

F32 = mybir.dt.float32
BF16 = mybir.dt.bfloat16
AF = mybir.ActivationFunctionType
ALU = mybir.AluOpType
AX = mybir.AxisListType

S = 4096
D = 1024
NT = S // 128
ATTN_IN = 2120
RWKV_IN = 1824
FFN_H = 2816
RMS_EPS = 1e-6
GN_EPS = 64e-5
TOPK = 256
NEG = -1.0e30

ENGS = ("pe", "act", "dve", "pool", "sp")
SBUF_DEBUG = False


class Buf:
    __slots__ = ("name", "w", "r")

    def __init__(self, name=""):
        self.name = name
        self.w = None
        self.r = []


class Op:
    __slots__ = ("eng", "thunk", "deps", "is_dma", "sem", "val", "need_inc", "pos", "prev_on_sem")


class Prog:
    def __init__(self, nc, n_dma_sems=(56, 28)):
        self.nc = nc
        self.streams = {e: [] for e in ENGS}
        self.n_dma_sems = n_dma_sems
        self.dma_count = 0
        self.all_ops = []
        self.open_dmas = []

    def _hazards(self, op, reads, writes):
        deps = []
        for b in reads:
            if b.w is not None:
                deps.append(b.w)
        for b in writes:
            if b.w is not None:
                deps.append(b.w)
            deps.extend(b.r)
        for b in reads:
            b.r.append(op)
        for b in writes:
            b.w = op
            b.r = []
        return deps

    def op(self, eng, thunk, reads=(), writes=(), extra_deps=()):
        o = Op()
        o.eng = eng
        o.thunk = thunk
        o.is_dma = False
        o.need_inc = False
        o.sem = None
        o.val = None
        o.prev_on_sem = None
        deps = self._hazards(o, reads, writes) + list(extra_deps)
        seen = set()
        o.deps = []
        for d in deps:
            if d is o or id(d) in seen:
                continue
            if eng == "pe" and d.eng == "pe" and not d.is_dma:
                continue
            seen.add(id(d))
            o.deps.append(d)
        self.streams[eng].append(o)
        self.all_ops.append(o)
        return o

    def dma(self, eng, thunk, reads=(), writes=(), extra_deps=()):
        o = self.op(eng, thunk, reads, writes, extra_deps)
        o.is_dma = True
        o.pos = self.dma_count
        self.dma_count += 1
        self.open_dmas.append(o)
        return o

    def barrier(self):
        lasts = []
        for e in ENGS:
            for o in reversed(self.streams[e]):
                if not o.is_dma:
                    lasts.append(o)
                    break
        deps = lasts + self.open_dmas
        self.open_dmas = []
        for e in ENGS:
            self.op(e, lambda eng: eng.nop(), extra_deps=deps)

    def emit(self, final_wait_ops=()):
        nc = self.nc
        for o in self.all_ops:
            for d in o.deps:
                d.need_inc = True
        eng_sems = {e: nc.alloc_semaphore("s_" + e) for e in ENGS}
        ring_n = {"sp": self.n_dma_sems[0], "pool": self.n_dma_sems[1], "act": 2, "dve": 2, "pe": 2}
        dma_sems = {}
        dma_sem_val = {}
        dma_prev = {}
        qpos = {e: 0 for e in ENGS}
        for e in ENGS:
            if any(o.is_dma for o in self.streams[e]):
                for i in range(ring_n[e]):
                    dma_sems[(e, i)] = nc.alloc_semaphore("s_dma_%s%d" % (e, i))
                    dma_sem_val[(e, i)] = 0
                    dma_prev[(e, i)] = None
        cnt = {e: 0 for e in ENGS}
        for o in self.all_ops:
            if o.is_dma:
                kq = (o.eng, qpos[o.eng] % ring_n[o.eng])
                qpos[o.eng] += 1
                dma_sem_val[kq] += 16
                o.sem = ("dma", kq)
                o.val = dma_sem_val[kq]
                o.prev_on_sem = dma_prev[kq]
                dma_prev[kq] = o
            elif o.need_inc:
                cnt[o.eng] += 1
                o.sem = ("eng", o.eng)
                o.val = cnt[o.eng]

        def semh(key):
            return eng_sems[key[1]] if key[0] == "eng" else dma_sems[key[1]]

        engines = {"pe": "tensor", "act": "scalar", "dve": "vector", "pool": "gpsimd", "sp": "sync"}
        with nc.Block() as block:
            for e in ENGS:
                stream = self.streams[e]
                final = list(final_wait_ops) if e == "sp" else []

                def body(engine, stream=stream, final=final):
                    known = {}
                    for o in stream:
                        waits = {}
                        deps = list(o.deps)
                        if o.is_dma and o.prev_on_sem is not None:
                            deps.append(o.prev_on_sem)
                        for d in deps:
                            if known.get(d.sem, 0) >= d.val:
                                continue
                            if waits.get(d.sem, 0) < d.val:
                                waits[d.sem] = d.val
                        for key, val in waits.items():
                            engine.wait_ge(semh(key), val)
                            known[key] = val
                        ins = o.thunk(engine)
                        if o.is_dma:
                            ins.then_inc(semh(o.sem), 16)
                        elif o.need_inc:
                            ins.then_inc(semh(o.sem), 1)
                    for o in final:
                        engine.wait_ge(semh(o.sem), o.val)

                getattr(block, engines[e])(body)


class Ring:
    def __init__(self, items):
        self.items = items
        self.i = 0

    def next(self):
        it = self.items[self.i % len(self.items)]
        self.i += 1
        return it


def load_weight_bf16(P, nc, dst, dst_buf, w_ap, c0, c1, kchunks, eng="pool"):
    wv = w_ap.rearrange("(kc p) n -> p kc n", p=128)
    for kc in range(kchunks):
        for a in range(c0, c1, 2048):
            b = min(c1, a + 2048)
            P.dma(eng, lambda e, kc=kc, a=a, b=b: e.dma_start(out=dst[:, kc, a - c0:b - c0], in_=wv[:, kc, a:b]),
                  writes=[dst_buf])


def load_col_vec(P, nc, dst, dst_buf, v_ap, n):
    src = v_ap.rearrange("o (c p) -> p (o c)", p=128)
    P.dma("sp", lambda e: e.dma_start(out=dst, in_=src, allow_slow_non_contiguous=True), writes=[dst_buf])


class Consts:
    pass


def make_consts(P, nc, es):
    C = Consts()
    C.ident_bf = es.enter_context(nc.sbuf_tensor("ident_bf", [128, 128], BF16))
    C.ident_f = es.enter_context(nc.sbuf_tensor("ident_f", [128, 128], F32))
    C.eps = es.enter_context(nc.sbuf_tensor("eps_c", [128, 2], F32))
    C.b = Buf("consts")

    def mk(e):
        e.memset(C.ident_f[:], 0.0)
        e.affine_select(out=C.ident_f[:], in_=C.ident_f[:], pattern=[[-1, 128]], compare_op=ALU.not_equal,
                        fill=1.0, base=0, channel_multiplier=1)
        e.memset(C.eps[:, 0:1], RMS_EPS)
        return e.memset(C.eps[:, 1:2], GN_EPS)

    P.op("pool", mk, writes=[C.b])
    P.op("pool", lambda e: e.tensor_copy(out=C.ident_bf[:], in_=C.ident_f[:]), reads=[C.b], writes=[C.b])
    return C


class Normer:
    def __init__(self, P, nc, es, C, gain_ap, name, nslots=2):
        self.P, self.nc, self.C = P, nc, C
        self.gcol = es.enter_context(nc.sbuf_tensor(name + "_g", [128, 8], F32))
        self.gb = Buf(name + "_g")
        load_col_vec(P, nc, self.gcol[:, :], self.gb, gain_ap, 8)
        self.stat = es.enter_context(nc.sbuf_tensor(name + "_st", [128, nslots, 4], F32))
        self.junk = es.enter_context(nc.sbuf_tensor(name + "_junk", [128, 1024], BF16))
        self.xs = es.enter_context(nc.sbuf_tensor(name + "_xs", [128, nslots, 1024], BF16))
        self.tp = es.enter_context(nc.psum_tensor(name + "_tp", [128, nslots, 8, 128], BF16))
        self.ring = Ring([(i, Buf(), Buf(), Buf()) for i in range(nslots)])
        self.junkb = Buf()

    def run(self, xt_ap, xt_buf, dst, dst_buf, col0):
        P, C = self.P, self.C
        i, sb, xb, pb = self.ring.next()
        st = self.stat
        P.op("act", lambda e: e.activation(out=self.junk[:], in_=xt_ap, func=AF.Square, accum_out=st[:, i, 0:1]),
             reads=[xt_buf], writes=[self.junkb, sb])
        P.op("act", lambda e: e.activation(out=st[:, i, 1:2], in_=st[:, i, 0:1], func=AF.Sqrt, scale=1.0 / D,
                                           bias=C.eps[:, 0:1]), reads=[sb, C.b], writes=[sb])
        P.op("dve", lambda e: e.reciprocal(out=st[:, i, 2:3], in_=st[:, i, 1:2]), reads=[sb], writes=[sb])
        P.op("act", lambda e: e.activation(out=self.xs[:, i, :], in_=xt_ap, func=AF.Copy, scale=st[:, i, 2:3]),
             reads=[xt_buf, sb], writes=[xb])

        def tr(e):
            ins = None
            for kc in range(8):
                ins = e.transpose(out=self.tp[:, i, kc, :], in_=self.xs[:, i, kc * 128:(kc + 1) * 128],
                                  identity=C.ident_bf[:])
            return ins

        P.op("pe", tr, reads=[xb, C.b], writes=[pb])
        P.op("dve", lambda e: e.tensor_tensor(out=dst[:, :, col0:col0 + 128], in0=self.tp[:, i, :, :],
                                              in1=self.gcol[:, :].unsqueeze(2).broadcast_to([128, 8, 128]),
                                              op=ALU.mult), reads=[pb, self.gb], writes=[dst_buf])


def phase_ffn(P, nc, C, h_dram, out_dram, ffn_norm, w_gate_up, w_down, final_ops):
    with ExitStack() as es:
        wgu = es.enter_context(nc.sbuf_tensor("wgu", [128, 8, 2 * FFN_H], BF16))
        wd = es.enter_context(nc.sbuf_tensor("wd", [128, 22, D], BF16))
        wgu_b, wd_b = Buf("wgu"), Buf("wd")
        load_weight_bf16(P, nc, wgu, wgu_b, w_gate_up, 0, 2 * FFN_H, 8)
        load_weight_bf16(P, nc, wd, wd_b, w_down, 0, D, 22)
        nm = Normer(P, nc, es, C, ffn_norm, "fn")
        hbuf = es.enter_context(nc.sbuf_tensor("hbuf", [128, 2, D], F32))
        hr = Ring([(i, Buf()) for i in range(2)])
        hnT = es.enter_context(nc.sbuf_tensor("hnT", [128, 2, 8, 512], BF16))
        hnb = [Buf(), Buf()]
        actT = es.enter_context(nc.sbuf_tensor("actT", [128, 22, 512], BF16))
        actb = [Buf() for _ in range(22)]
        sg = es.enter_context(nc.sbuf_tensor("sg", [128, 2, 512], F32))
        sgr = Ring([(i, Buf()) for i in range(2)])
        ost = es.enter_context(nc.sbuf_tensor("ost", [128, 2, D], F32))
        ostr = Ring([(i, Buf()) for i in range(2)])
        pg = es.enter_context(nc.psum_tensor("pg", [128, 2, 512], F32))
        pu = es.enter_context(nc.psum_tensor("pu", [128, 2, 512], F32))
        po = es.enter_context(nc.psum_tensor("po", [128, 2, 512], F32))
        pgr = Ring([(i, Buf()) for i in range(2)])
        pur = Ring([(i, Buf()) for i in range(2)])
        por = Ring([(i, Buf()) for i in range(2)])
        hview = h_dram.rearrange("(t p) d -> t p d", p=128)
        oview = out_dram.rearrange("(t p) d -> t p d", p=128)
        def ffn_norm_chunk(c):
            cb = c % 2
            for j in range(4):
                t = c * 4 + j
                hi, hb_ = hr.next()
                P.dma("sp", lambda e, t=t, hi=hi: e.dma_start(out=hbuf[:, hi, :], in_=hview[t]), writes=[hb_])
                nm.run(hbuf[:, hi, :], hb_, hnT[:, cb], hnb[cb], j * 128)

        ffn_norm_chunk(0)
        for c in range(S // 512):
            cb = c % 2
            if c + 1 < S // 512:
                ffn_norm_chunk(c + 1)
            for m in range(22):
                gi, gbuf = pgr.next()
                ui, ubuf = pur.next()

                def mm(e, m=m, gi=gi, ui=ui, cb=cb):
                    for kc in range(8):
                        e.matmul(pg[:, gi, :], lhsT=wgu[:, kc, m * 128:(m + 1) * 128], rhs=hnT[:, cb, kc, :],
                                 start=(kc == 0), stop=(kc == 7))
                    ins = None
                    for kc in range(8):
                        ins = e.matmul(pu[:, ui, :], lhsT=wgu[:, kc, FFN_H + m * 128:FFN_H + (m + 1) * 128],
                                       rhs=hnT[:, cb, kc, :], start=(kc == 0), stop=(kc == 7))
                    return ins

                P.op("pe", mm, reads=[wgu_b, hnb[cb]], writes=[gbuf, ubuf])
                si, sbuf_ = sgr.next()
                P.op("act", lambda e, gi=gi, si=si: e.activation(out=sg[:, si, :], in_=pg[:, gi, :], func=AF.Silu),
                     reads=[gbuf], writes=[sbuf_])
                P.op("dve", lambda e, ui=ui, si=si, m=m: e.tensor_tensor(out=actT[:, m, :], in0=pu[:, ui, :],
                                                                         in1=sg[:, si, :], op=ALU.mult),
                     reads=[ubuf, sbuf_], writes=[actb[m]])
            for j in range(4):
                t = c * 4 + j
                oi, obuf = ostr.next()
                P.dma("sp", lambda e, t=t, oi=oi: e.dma_start(out=ost[:, oi, :], in_=hview[t]), writes=[obuf])
                for nh in range(2):
                    pi, pbuf = por.next()

                    def mmd(e, j=j, nh=nh, pi=pi):
                        ins = None
                        for m in range(22):
                            ins = e.matmul(po[:, pi, :], lhsT=actT[:, m, j * 128:(j + 1) * 128],
                                           rhs=wd[:, m, nh * 512:(nh + 1) * 512], start=(m == 0), stop=(m == 21))
                        return ins

                    P.op("pe", mmd, reads=actb + [wd_b], writes=[pbuf])
                    P.op("dve", lambda e, nh=nh, pi=pi, oi=oi: e.tensor_tensor(
                        out=ost[:, oi, nh * 512:(nh + 1) * 512], in0=po[:, pi, :],
                        in1=ost[:, oi, nh * 512:(nh + 1) * 512], op=ALU.add),
                        reads=[pbuf], writes=[obuf])
                final_ops.append(P.dma("sp", lambda e, t=t, oi=oi: e.dma_start(out=oview[t], in_=ost[:, oi, :]),
                                       reads=[obuf]))
    P.barrier()


def phase_merge(P, nc, C, x_dram, ya_dram, yr_dram, h_dram, mix_norm, w_in, w_ba, w_br, w_out):
    with ExitStack() as es:
        wg = es.enter_context(nc.sbuf_tensor("wg", [128, 8, 2 * D], BF16))
        wba = es.enter_context(nc.sbuf_tensor("wba", [128, 4, D], BF16))
        wbr = es.enter_context(nc.sbuf_tensor("wbr", [128, 4, D], BF16))
        wo = es.enter_context(nc.sbuf_tensor("wo", [128, 8, D], BF16))
        wg_b, wba_b, wbr_b, wo_b = Buf(), Buf(), Buf(), Buf()
        load_weight_bf16(P, nc, wg, wg_b, w_in, ATTN_IN + RWKV_IN, ATTN_IN + RWKV_IN + 2 * D, 8)
        load_weight_bf16(P, nc, wba, wba_b, w_ba, 0, D, 4)
        load_weight_bf16(P, nc, wbr, wbr_b, w_br, 0, D, 4)
        load_weight_bf16(P, nc, wo, wo_b, w_out, 0, D, 8)
        nm = Normer(P, nc, es, C, mix_norm, "mn")
        xbuf = es.enter_context(nc.sbuf_tensor("xbuf", [128, 2, 4, D], F32))
        xb = [[Buf() for _ in range(4)] for _ in range(2)]
        xnT = es.enter_context(nc.sbuf_tensor("xnT", [128, 2, 8, 512], BF16))
        xnb = [Buf(), Buf()]
        yaT = es.enter_context(nc.sbuf_tensor("yaT", [128, 2, 4, 512], BF16))
        yrT = es.enter_context(nc.sbuf_tensor("yrT", [128, 2, 4, 512], BF16))
        yab, yrb = [Buf(), Buf()], [Buf(), Buf()]
        mT = es.enter_context(nc.sbuf_tensor("mT", [128, 8, 512], BF16))
        mb = [Buf() for _ in range(8)]
        sg = es.enter_context(nc.sbuf_tensor("sgm", [128, 2, 2, 512], F32))
        sgr = Ring([(i, Buf()) for i in range(2)])
        tt = es.enter_context(nc.sbuf_tensor("ttm", [128, 2, 2, 512], F32))
        ttr = Ring([(i, Buf()) for i in range(2)])
        hst = es.enter_context(nc.sbuf_tensor("hst", [128, 2, D], F32))
        hstr = Ring([(i, Buf()) for i in range(2)])
        pga = es.enter_context(nc.psum_tensor("pga", [128, 2, 512], F32))
        pbr = es.enter_context(nc.psum_tensor("pbr", [128, 2, 512], F32))
        po = es.enter_context(nc.psum_tensor("pom", [128, 2, 512], F32))
        pgb, pbb = Buf(), Buf()
        por = Ring([(i, Buf()) for i in range(2)])
        xview = x_dram.rearrange("(t p) d -> t p d", p=128)
        hview = h_dram.rearrange("(t p) d -> t p d", p=128)
        yav = ya_dram.rearrange("(kc p) s -> p kc s", p=128)
        yrv = yr_dram.rearrange("(kc p) s -> p kc s", p=128)
        def merge_load_chunk(c):
            cb = c % 2
            P.dma("sp", lambda e: e.dma_start(out=yaT[:, cb], in_=yav[:, :, c * 512:(c + 1) * 512]), writes=[yab[cb]])
            P.dma("sp", lambda e: e.dma_start(out=yrT[:, cb], in_=yrv[:, :, c * 512:(c + 1) * 512]), writes=[yrb[cb]])
            for j in range(4):
                t = c * 4 + j
                P.dma("sp", lambda e, t=t, j=j: e.dma_start(out=xbuf[:, cb, j, :], in_=xview[t]), writes=[xb[cb][j]])
                nm.run(xbuf[:, cb, j, :], xb[cb][j], xnT[:, cb], xnb[cb], j * 128)

        merge_load_chunk(0)
        for c in range(S // 512):
            cb = c % 2
            if c + 1 < S // 512:
                merge_load_chunk(c + 1)
            for m in range(8):
                def mmg(e, m=m, cb=cb):
                    ins = None
                    for g in range(2):
                        for kc in range(8):
                            ins = e.matmul(pga[:, g, :], lhsT=wg[:, kc, g * D + m * 128:g * D + (m + 1) * 128],
                                           rhs=xnT[:, cb, kc, :], start=(kc == 0), stop=(kc == 7))
                    return ins

                P.op("pe", mmg, reads=[wg_b, xnb[cb]], writes=[pgb])

                def mmb(e, m=m, cb=cb):
                    ins = None
                    for kc in range(4):
                        ins = e.matmul(pbr[:, 0, :], lhsT=wba[:, kc, m * 128:(m + 1) * 128], rhs=yaT[:, cb, kc, :],
                                       start=(kc == 0), stop=(kc == 3))
                    for kc in range(4):
                        ins = e.matmul(pbr[:, 1, :], lhsT=wbr[:, kc, m * 128:(m + 1) * 128], rhs=yrT[:, cb, kc, :],
                                       start=(kc == 0), stop=(kc == 3))
                    return ins

                P.op("pe", mmb, reads=[wba_b, wbr_b, yab[cb], yrb[cb]], writes=[pbb])
                si, sbuf_ = sgr.next()
                P.op("act", lambda e, si=si: e.activation(out=sg[:, si], in_=pga[:, :, :], func=AF.Sigmoid),
                     reads=[pgb], writes=[sbuf_])
                ti, tbuf = ttr.next()
                P.op("dve", lambda e, si=si, ti=ti: e.tensor_tensor(out=tt[:, ti], in0=pbr[:, :, :], in1=sg[:, si],
                                                                    op=ALU.mult),
                     reads=[pbb, sbuf_], writes=[tbuf])
                P.op("pool", lambda e, ti=ti, m=m: e.tensor_tensor(out=mT[:, m, :], in0=tt[:, ti, 0, :],
                                                                   in1=tt[:, ti, 1, :], op=ALU.add),
                     reads=[tbuf], writes=[mb[m]])
            for j in range(4):
                t = c * 4 + j
                hi, hbuf_ = hstr.next()
                for nh in range(2):
                    pi, pbuf = por.next()

                    def mmo(e, j=j, nh=nh, pi=pi):
                        ins = None
                        for m in range(8):
                            ins = e.matmul(po[:, pi, :], lhsT=mT[:, m, j * 128:(j + 1) * 128],
                                           rhs=wo[:, m, nh * 512:(nh + 1) * 512], start=(m == 0), stop=(m == 7))
                        return ins

                    P.op("pe", mmo, reads=mb + [wo_b], writes=[pbuf])
                    P.op("dve", lambda e, j=j, nh=nh, pi=pi, hi=hi, cb=cb: e.tensor_tensor(
                        out=hst[:, hi, nh * 512:(nh + 1) * 512], in0=po[:, pi, :],
                        in1=xbuf[:, cb, j, nh * 512:(nh + 1) * 512], op=ALU.add),
                        reads=[pbuf, xb[cb][j]], writes=[hbuf_])
                P.dma("sp", lambda e, t=t, hi=hi: e.dma_start(out=hview[t], in_=hst[:, hi, :]), reads=[hbuf_])
    P.barrier()


TL = 128
C0 = math.exp(-0.5)


def col8(P, nc, es, name, v_ap):
    t = es.enter_context(nc.sbuf_tensor(name, [64, 8], F32))
    b = Buf(name)
    P.dma("sp", lambda e: e.dma_start(out=t[:, :], in_=v_ap.rearrange("o (h k) -> k (o h)", k=64),
                                      allow_slow_non_contiguous=True), writes=[b])
    return t, b


def phase_rwkv(P, nc, C, A, yr_dram):
    x_dram = A["x"]
    with ExitStack() as es:
        sb = lambda name, shape, dt=F32: es.enter_context(nc.sbuf_tensor(name, shape, dt))
        wr = sb("wr", [128, 8, RWKV_IN], BF16)
        wmu = sb("wmu", [128, 8, RWKV_IN], BF16)
        wr_b, wmu_b, mub_b = Buf(), Buf(), Buf()
        load_weight_bf16(P, nc, wr, wr_b, A["w_in"], ATTN_IN, ATTN_IN + RWKV_IN, 8)
        with nc.sbuf_tensor("mub", [128, RWKV_IN], F32) as mub:
            P.dma("sp", lambda e: e.dma_start(out=mub[:], in_=A["rwkv_mu"].partition_broadcast(128)), writes=[mub_b])
            for kc in range(8):
                P.op("pool", lambda e, kc=kc: e.tensor_tensor(out=wmu[:, kc, :], in0=wr[:, kc, :], in1=mub[:],
                                                              op=ALU.mult), reads=[wr_b, mub_b], writes=[wmu_b])
        P.barrier()
        w2 = sb("w2", [64, 512], BF16)
        a2 = sb("a2", [64, 512], BF16)
        g2 = sb("g2", [64, 3, 512], BF16)
        lw_b = Buf()
        P.dma("pool", lambda e: e.dma_start(out=w2[:], in_=A["rwkv_w2"]), writes=[lw_b])
        P.dma("pool", lambda e: e.dma_start(out=a2[:], in_=A["rwkv_a2"]), writes=[lw_b])
        P.dma("pool", lambda e: e.dma_start(out=g2[:, 0, :], in_=A["rwkv_g2"][0:64, :]), writes=[lw_b])
        P.dma("pool", lambda e: e.dma_start(out=g2[:, 1, :], in_=A["rwkv_g2"][64:128, :]), writes=[lw_b])
        P.dma("pool", lambda e: e.dma_start(out=g2[0:32, 2, :], in_=A["rwkv_g2"][128:160, :]), writes=[lw_b])
        cw0, b_w0 = col8(P, nc, es, "cw0", A["rwkv_w0"])
        ca0, b_a0 = col8(P, nc, es, "ca0", A["rwkv_a0"])
        ckk, b_kk = col8(P, nc, es, "ckk", A["rwkv_k_k"])
        cka, b_ka = col8(P, nc, es, "cka", A["rwkv_k_a"])
        crk, b_rk = col8(P, nc, es, "crk", A["rwkv_r_k"])
        clw, b_lw = col8(P, nc, es, "clw", A["rwkv_ln_w"])
        clb, b_lb = col8(P, nc, es, "clb", A["rwkv_ln_b"])
        msk = sb("rmsk", [64, 3, 64], F32)
        ones_bf = sb("ones_bf", [64, 64], BF16)
        rmask = sb("rmask", [64, 8, TL // 64, 64], F32)
        mk_b = Buf()

        def mkmasks(e):
            e.memset(msk[:], 1.0)
            e.memset(ones_bf[:], 1.0)
            e.memset(rmask[:], 1.0)
            e.memset(rmask[:, :, :, 0:1], 0.0)
            e.affine_select(out=msk[:, 0, :], in_=msk[:, 0, :], pattern=[[1, 64]], compare_op=ALU.is_ge,
                            fill=0.0, base=-1, channel_multiplier=-1)
            e.affine_select(out=msk[:, 1, :], in_=msk[:, 1, :], pattern=[[1, 64]], compare_op=ALU.is_ge,
                            fill=0.0, base=0, channel_multiplier=-1)
            return e.affine_select(out=msk[:, 2, :], in_=msk[:, 2, :], pattern=[[-1, 64]], compare_op=ALU.is_ge,
                                   fill=0.0, base=-1, channel_multiplier=1)

        P.op("pool", mkmasks, writes=[mk_b])
        mb3 = lambda i: msk[:, i, :].unsqueeze(1).broadcast_to([64, 8, 64])
        idb = C.ident_bf[0:64, 0:64]
        idf3 = C.ident_f[0:64, 0:64].unsqueeze(1).broadcast_to([64, 8, 64])
        bc = lambda col: col[:, :].unsqueeze(2).broadcast_to([64, 8, TL])

        nm = Normer(P, nc, es, C, A["mix_norm"], "rn", nslots=1)
        xbuf = sb("rxbuf", [128, 1, D], F32)
        xr = Ring([(i, Buf()) for i in range(1)])
        xnx = sb("xnx", [128, 2, 8, TL + 1], BF16)
        xnb = [Buf(), Buf()]
        dxn = sb("dxn", [128, 8, TL], BF16)
        dxb = Buf()
        P.op("pool", lambda e: e.memset(xnx[:, 1, :, TL:TL + 1], 0.0), writes=[xnb[1]])
        F = {}
        FB = {}
        for nme, dt in [("r", F32), ("k", F32), ("sgd", F32), ("a", F32), ("kk", F32),
                        ("t1", F32), ("t2", F32), ("cum", F32), ("e1", F32), ("e2", F32), ("sqb", BF16)]:
            alias = {"e1": "t1", "e2": "a"}
            if nme in alias:
                F[nme] = F[alias[nme]]
                FB[nme] = FB[alias[nme]]
                continue
            F[nme] = sb("f_" + nme, [64, 8, TL], dt)
            FB[nme] = Buf(nme)
        X2 = {}
        X2B = {}
        for nme in ("rT", "aT", "bT", "kT", "bH", "kH", "vb", "g", "bon"):
            X2[nme] = sb("x_" + nme, [64, 2, 8, TL], BF16)
            X2B[nme] = [Buf(nme + "0"), Buf(nme + "1")]
        lora = sb("lora", [64, 5, TL], BF16)
        ztok = sb("ztok", [128, 2, 512], F32)
        ztr = Ring([(i, Buf()) for i in range(2)])
        lora_b = Buf()
        gC = sb("gC", [64, 2, 8, TL // 64], F32)
        gC_bs = [Buf(), Buf()]
        Sf = sb("Sf", [64, 8, 64], F32)
        Sb = sb("Sb", [64, 8, 64], BF16)
        St = sb("St", [64, 8, 64], F32)
        S_b, St_b = Buf(), Buf()
        P.op("dve", lambda e: e.memset(Sf[:], 0.0), writes=[S_b])
        P.op("dve", lambda e: e.memset(Sb[:], 0.0), reads=[S_b], writes=[S_b])
        ost = sb("rost", [64, 8, TL], BF16)
        ost_b = Buf()
        def pair(name, dt=BF16, n=2):
            t = sb(name, [64, n, 8, 64], dt)
            return t, [Buf() for _ in range(n)]
        Atok, Atok_b = pair("Atok")
        BHtok, BHtok_b = pair("BHtok")
        KHtok, KHtok_b = pair("KHtok")
        Vtok, Vtok_b = pair("Vtok")
        Mrb, Mrb_b = pair("Mrb")
        Mrk, Mrk_b = pair("Mrk")
        Lak, Lak_b = pair("Lak")
        Nn, Nn_b = pair("Nn", BF16, 4)
        Mm, Mm_b = pair("Mm", BF16, 4)
        Qq, Qq_b = pair("Qq", BF16, 4)
        WT, WT_b = pair("WT")
        Xx, Xx_b = pair("Xx")
        Uu, Uu_b = pair("Uu")
        Ys, Ys_b = pair("Ys", F32, 1)
        Yn, Yn_b = pair("Yn", BF16, 2)
        gst = sb("gst", [64, 2, 8, 6], F32)
        gst_b = [Buf(), Buf()]
        pb = es.enter_context(nc.psum_tensor("rpb", [128, 7, 512], F32))
        pr = Ring([(i, Buf()) for i in range(7)])
        pv = lambda i: pb[0:64, i, :].rearrange("p (h t) -> p h t", h=8)
        pvb = lambda i: pb[0:64, i, :].bitcast(BF16)[:, 0:512].rearrange("p (h t) -> p h t", h=8)

        xview = x_dram.rearrange("(t p) d -> t p d", p=128)

        def mm8(bank, parts, rd):
            i, bbuf = bank
            ops = [[(lf(h), rf(h)) for (lf, rf) in parts] for h in range(8)]

            def th(e):
                ins = None
                for h in range(8):
                    for pi, (l, r) in enumerate(ops[h]):
                        ins = e.matmul(pb[0:64, i, h * 64:(h + 1) * 64], lhsT=l, rhs=r,
                                       start=(pi == 0), stop=(pi == len(ops[h]) - 1))
                return ins

            P.op("pe", th, reads=rd, writes=[bbuf])

        def tr8(bank, src_fn, rd):
            i, bbuf = bank
            srcs = [src_fn(h) for h in range(8)]

            def th(e):
                ins = None
                v = pvb(i)
                for h in range(8):
                    ins = e.transpose(out=v[:, h, :], in_=srcs[h], identity=idb)
                return ins

            P.op("pe", th, reads=rd + [C.b], writes=[bbuf])

        NB = S // TL

        def prep(n):
            par = n % 2
            cb = n % 2
            pc = 1 - cb
            X = {k: X2[k][:, par] for k in X2}
            XB = {k: X2B[k][par] for k in X2}
            P.op("pool", lambda e: e.tensor_copy(out=xnx[:, cb, :, 0:1], in_=xnx[:, pc, :, TL:TL + 1]),
                 reads=[xnb[pc]], writes=[xnb[cb]])
            for j in range(TL // 128):
                xi, xb_ = xr.next()
                P.dma("sp", lambda e, t=n * (TL // 128) + j, xi=xi: e.dma_start(out=xbuf[:, xi, :], in_=xview[t]), writes=[xb_])
                nm.run(xbuf[:, xi, :], xb_, xnx[:, cb], xnb[cb], 1 + j * 128)
            P.op("pool", lambda e: e.tensor_tensor(out=dxn[:], in0=xnx[:, cb, :, 0:TL], in1=xnx[:, cb, :, 1:TL + 1],
                                                   op=ALU.subtract), reads=[xnb[cb]], writes=[dxb])
            yield

            def projtok(c0, ncol):
                bank = pr.next()
                i = bank[0]

                def th(e):
                    ins = None
                    for kc in range(8):
                        e.matmul(pb[:, i, 0:ncol], lhsT=xnx[:, cb, kc, 1:TL + 1], rhs=wr[:, kc, c0:c0 + ncol],
                                 start=(kc == 0), stop=False)
                    for kc in range(8):
                        ins = e.matmul(pb[:, i, 0:ncol], lhsT=dxn[:, kc, :], rhs=wmu[:, kc, c0:c0 + ncol],
                                       start=False, stop=(kc == 7))
                    return ins

                P.op("pe", th, reads=[wr_b, wmu_b, xnb[cb], dxb], writes=[bank[1]])
                zi, zb = ztr.next()
                P.op("act", lambda e: e.activation(out=ztok[:, zi, 0:ncol], in_=pb[:, i, 0:ncol], func=AF.Copy),
                     reads=[bank[1]], writes=[zb])
                return zi, zb

            def trz(zi, zb, cols, m):
                bank = pr.next()
                i = bank[0]

                def th(e):
                    ins = None
                    for q, c in enumerate(cols):
                        ins = e.transpose(out=pb[0:m, i, q * TL:(q + 1) * TL], in_=ztok[:, zi, c:c + m], identity=C.ident_f[:])
                    return ins

                P.op("pe", th, reads=[zb, C.b], writes=[bank[1]])
                return bank

            for qi, qn in enumerate(("r", "k", "v")):
                zi, zb = projtok(qi * 512, 512)
                yield
                for h0 in (0, 4):
                    bank = trz(zi, zb, [(h0 + q) * 64 for q in range(4)], 64)
                    dst, dstb = (X["vb"], XB["vb"]) if qn == "v" else (F[qn], FB[qn])
                    P.op("act", lambda e, dst=dst, h0=h0, i=bank[0]: e.activation(
                        out=dst[:, h0:h0 + 4, :], in_=pb[0:64, i, 0:4 * TL].rearrange("p (q t) -> p q t", q=4), func=AF.Copy),
                        reads=[bank[1]], writes=[dstb])
                    yield
            zi, zb = projtok(1536, 288)
            yield
            for li, (c0, m, fn) in enumerate([(0, 64, AF.Tanh), (64, 64, AF.Copy), (128, 64, AF.Sigmoid),
                                              (192, 64, AF.Sigmoid), (256, 32, AF.Sigmoid)]):
                bank = trz(zi, zb, [c0], m)
                P.op("act", lambda e, li=li, m=m, fn=fn, i=bank[0]: e.activation(out=lora[0:m, li, :], in_=pb[0:m, i, 0:TL],
                                                                                 func=fn),
                     reads=[bank[1]], writes=[lora_b])
                yield
            for h in range(8):
                bank = pr.next()
                P.op("pe", lambda e, h=h, i=bank[0]: e.matmul(pb[0:64, i, 0:TL], lhsT=w2[:, h * 64:(h + 1) * 64],
                                                              rhs=lora[:, 0, :], start=True, stop=True),
                     reads=[lw_b, lora_b], writes=[bank[1]])
                P.op("act", lambda e, h=h, i=bank[0]: e.activation(out=F["sgd"][:, h, :], in_=pb[0:64, i, 0:TL],
                                                                   func=AF.Sigmoid, bias=cw0[:, h:h + 1]),
                     reads=[bank[1], b_w0], writes=[FB["sgd"]])
                bank = pr.next()
                P.op("pe", lambda e, h=h, i=bank[0]: e.matmul(pb[0:64, i, 0:TL], lhsT=a2[:, h * 64:(h + 1) * 64],
                                                              rhs=lora[:, 1, :], start=True, stop=True),
                     reads=[lw_b, lora_b], writes=[bank[1]])
                P.op("act", lambda e, h=h, i=bank[0]: e.activation(out=F["a"][:, h, :], in_=pb[0:64, i, 0:TL],
                                                                   func=AF.Sigmoid, bias=ca0[:, h:h + 1]),
                     reads=[bank[1], b_a0], writes=[FB["a"]])
                bank = pr.next()

                def gmm(e, h=h, i=bank[0]):
                    e.matmul(pb[0:64, i, 0:TL], lhsT=g2[:, 0, h * 64:(h + 1) * 64], rhs=lora[:, 2, :], start=True, stop=False)
                    e.matmul(pb[0:64, i, 0:TL], lhsT=g2[:, 1, h * 64:(h + 1) * 64], rhs=lora[:, 3, :], start=False, stop=False)
                    return e.matmul(pb[0:64, i, 0:TL], lhsT=g2[0:32, 2, h * 64:(h + 1) * 64], rhs=lora[0:32, 4, :],
                                    start=False, stop=True)

                P.op("pe", gmm, reads=[lw_b, lora_b], writes=[bank[1]])
                P.op("act", lambda e, h=h, i=bank[0]: e.activation(out=X["g"][:, h, :], in_=pb[0:64, i, 0:TL], func=AF.Copy),
                     reads=[bank[1]], writes=[XB["g"]])
                yield
            P.op("dve", lambda e: e.tensor_tensor(out=F["kk"][:], in0=F["k"][:], in1=bc(ckk), op=ALU.mult),
                 reads=[FB["k"], b_kk], writes=[FB["kk"]])
            P.op("pool", lambda e: e.tensor_tensor(out=F["sqb"][:], in0=F["kk"][:], in1=F["kk"][:], op=ALU.mult),
                 reads=[FB["kk"]], writes=[FB["sqb"]])
            yield
            for h in range(8):
                bank = pr.next()
                P.op("pe", lambda e, h=h, i=bank[0]: e.matmul(pb[0:64, i, 0:TL], lhsT=ones_bf[:], rhs=F["sqb"][:, h, :],
                                                              start=True, stop=True),
                     reads=[mk_b, FB["sqb"]], writes=[bank[1]])
                P.op("act", lambda e, h=h, i=bank[0]: e.activation(out=F["t1"][:, h, :], in_=pb[0:64, i, 0:TL], func=AF.Sqrt),
                     reads=[bank[1]], writes=[FB["t1"]])
                if h % 2 == 1:
                    yield
            P.op("dve", lambda e: e.tensor_scalar(out=F["t1"][:], in0=F["t1"][:], scalar1=1e-12, scalar2=None,
                                                  op0=ALU.max), reads=[FB["t1"]], writes=[FB["t1"]])
            P.op("dve", lambda e: e.reciprocal(out=F["t1"][:], in_=F["t1"][:]), reads=[FB["t1"]], writes=[FB["t1"]])
            yield
            P.op("dve", lambda e: e.tensor_tensor(out=F["kk"][:], in0=F["kk"][:], in1=F["t1"][:], op=ALU.mult),
                 reads=[FB["kk"], FB["t1"]], writes=[FB["kk"]])
            P.op("dve", lambda e: e.scalar_tensor_tensor(out=F["t2"][:], in0=F["a"][:], scalar=-1.0, in1=bc(cka),
                                                         op0=ALU.add, op1=ALU.mult),
                 reads=[FB["a"], b_ka], writes=[FB["t2"]])
            yield
            P.op("dve", lambda e: e.scalar_tensor_tensor(out=F["k"][:], in0=F["t2"][:], scalar=1.0, in1=F["k"][:],
                                                         op0=ALU.add, op1=ALU.mult),
                 reads=[FB["t2"], FB["k"]], writes=[FB["k"]])
            P.op("pool", lambda e: e.tensor_tensor(out=F["t2"][:], in0=F["kk"][:], in1=F["a"][:], op=ALU.mult),
                 reads=[FB["kk"], FB["a"]], writes=[FB["t2"]])
            yield
            P.op("dve", lambda e: e.tensor_tensor_scan(out=F["cum"][:].rearrange("p h t -> p (h t)"),
                                                       data0=rmask[:].rearrange("p h c t -> p (h c t)"),
                                                       data1=F["sgd"][:].rearrange("p h t -> p (h t)"),
                                                       initial=0.0, op0=ALU.mult, op1=ALU.add),
                 reads=[FB["sgd"], mk_b], writes=[FB["cum"]])
            cum4 = F["cum"][:].rearrange("p h (c t) -> p h c t", t=64)
            yield
            P.op("act", lambda e: e.activation(out=F["e1"][:], in_=F["cum"][:], func=AF.Exp, scale=-C0),
                 reads=[FB["cum"]], writes=[FB["e1"]])
            P.op("dve", lambda e: e.tensor_tensor(out=X["rT"][:], in0=F["r"][:], in1=F["e1"][:], op=ALU.mult),
                 reads=[FB["r"], FB["e1"]], writes=[XB["rT"]])
            P.op("act", lambda e: e.activation(out=gC[:, par], in_=cum4[:, :, :, 63], func=AF.Exp, scale=-C0),
                 reads=[FB["cum"]], writes=[gC_bs[par]])
            yield
            P.op("pool", lambda e: e.tensor_tensor(out=F["e2"][:], in0=F["cum"][:], in1=F["sgd"][:], op=ALU.subtract),
                 reads=[FB["cum"], FB["sgd"]], writes=[FB["e2"]])
            P.op("act", lambda e: e.activation(out=F["e2"][:], in_=F["e2"][:], func=AF.Exp, scale=-C0),
                 reads=[FB["e2"]], writes=[FB["e2"]])
            P.op("dve", lambda e: e.scalar_tensor_tensor(out=X["aT"][:], in0=F["kk"][:], scalar=-1.0, in1=F["e2"][:],
                                                         op0=ALU.mult, op1=ALU.mult),
                 reads=[FB["kk"], FB["e2"]], writes=[XB["aT"]])
            yield
            P.op("act", lambda e: e.activation(out=F["e1"][:], in_=F["cum"][:], func=AF.Exp, scale=C0),
                 reads=[FB["cum"]], writes=[FB["e1"]])
            P.op("dve", lambda e: e.tensor_tensor(out=X["bT"][:], in0=F["t2"][:], in1=F["e1"][:], op=ALU.mult),
                 reads=[FB["t2"], FB["e1"]], writes=[XB["bT"]])
            P.op("pool", lambda e: e.tensor_tensor(out=X["kT"][:], in0=F["k"][:], in1=F["e1"][:], op=ALU.mult),
                 reads=[FB["k"], FB["e1"]], writes=[XB["kT"]])
            yield
            P.op("dve", lambda e: e.tensor_tensor(out=F["e2"][:].rearrange("p h (c t) -> p h c t", t=64),
                                                  in0=cum4[:, :, :, 63:64].broadcast_to([64, 8, TL // 64, 64]), in1=cum4,
                                                  op=ALU.subtract),
                 reads=[FB["cum"]], writes=[FB["e2"]])
            P.op("act", lambda e: e.activation(out=F["e2"][:], in_=F["e2"][:], func=AF.Exp, scale=-C0),
                 reads=[FB["e2"]], writes=[FB["e2"]])
            yield
            P.op("dve", lambda e: e.tensor_tensor(out=X["bH"][:], in0=F["t2"][:], in1=F["e2"][:], op=ALU.mult),
                 reads=[FB["t2"], FB["e2"]], writes=[XB["bH"]])
            P.op("pool", lambda e: e.tensor_tensor(out=X["kH"][:], in0=F["k"][:], in1=F["e2"][:], op=ALU.mult),
                 reads=[FB["k"], FB["e2"]], writes=[XB["kH"]])
            yield
            P.op("dve", lambda e: e.tensor_tensor(out=F["t1"][:], in0=F["r"][:], in1=F["k"][:], op=ALU.mult),
                 reads=[FB["r"], FB["k"]], writes=[FB["t1"]])
            P.op("pool", lambda e: e.tensor_tensor(out=F["sqb"][:], in0=F["t1"][:], in1=bc(crk), op=ALU.mult),
                 reads=[FB["t1"], b_rk], writes=[FB["sqb"]])
            yield
            for h in range(8):
                bank = pr.next()
                P.op("pe", lambda e, h=h, i=bank[0]: e.matmul(pb[0:64, i, 0:TL], lhsT=ones_bf[:], rhs=F["sqb"][:, h, :],
                                                              start=True, stop=True),
                     reads=[mk_b, FB["sqb"]], writes=[bank[1]])
                P.op("dve", lambda e, h=h, i=bank[0]: e.tensor_tensor(out=X["bon"][:, h, :], in0=pb[0:64, i, 0:TL],
                                                                      in1=X["vb"][:, h, :], op=ALU.mult),
                     reads=[bank[1], XB["vb"]], writes=[XB["bon"]])
                if h % 2 == 1:
                    yield

        def chunk_pre(n, c, out):
            par = n % 2
            X = {k: X2[k][:, par] for k in X2}
            XB = {k: X2B[k][par] for k in X2}
            cs = slice(c * 64, (c + 1) * 64)
            for (dst, dbs, src) in ((Atok, Atok_b, "aT"), (BHtok, BHtok_b, "bH"), (KHtok, KHtok_b, "kH"), (Vtok, Vtok_b, "vb")):
                bank = pr.next()
                tr8(bank, lambda h, src=src: X[src][:, h, cs], [XB[src]])
                P.op("act", lambda e, dst=dst, i=bank[0]: e.activation(out=dst[:, c], in_=pvb(i), func=AF.Copy),
                     reads=[bank[1]], writes=[dbs[c]])
                yield

            def gmat(lname, rname, mi, dst, dbs, slot):
                bank = pr.next()
                mm8(bank, [(lambda h: X[lname][:, h, cs], lambda h: X[rname][:, h, cs])], [XB[lname], XB[rname]])
                P.op("dve", lambda e, i=bank[0]: e.tensor_tensor(out=dst[:, slot], in0=pv(i), in1=mb3(mi), op=ALU.mult),
                     reads=[bank[1], mk_b], writes=[dbs[slot]])

            base = 2 * c
            gmat("bT", "aT", 0, Mm, Mm_b, base)
            yield
            gmat("bT", "rT", 1, Mrb, Mrb_b, c)
            yield
            gmat("kT", "aT", 0, Lak, Lak_b, c)
            yield
            gmat("kT", "rT", 1, Mrk, Mrk_b, c)
            yield
            gmat("aT", "bT", 2, Nn, Nn_b, base)
            yield
            P.op("pool", lambda e: e.tensor_tensor(out=Qq[:, base], in0=Mm[:, base], in1=idf3, op=ALU.add),
                 reads=[Mm_b[base], C.b], writes=[Qq_b[base]])
            ni = mi_ = qi_ = base
            for lvl in range(1, 6):
                nn_ = base + (1 - (ni - base))
                nm_ = base + (1 - (mi_ - base))
                nq_ = base + (1 - (qi_ - base))
                bank = pr.next()
                mm8(bank, [(lambda h: Mm[:, mi_, h, :], lambda h: Nn[:, ni, h, :])], [Mm_b[mi_], Nn_b[ni]])
                if lvl < 5:
                    bank2 = pr.next()
                    mm8(bank2, [(lambda h: Nn[:, ni, h, :], lambda h: Mm[:, mi_, h, :])], [Mm_b[mi_], Nn_b[ni]])
                P.op("act", lambda e, nn_=nn_, i=bank[0]: e.activation(out=Nn[:, nn_], in_=pv(i), func=AF.Copy),
                     reads=[bank[1]], writes=[Nn_b[nn_]])
                if lvl < 5:
                    P.op("dve", lambda e, nm_=nm_, i=bank2[0]: e.tensor_copy(out=Mm[:, nm_], in_=pv(i)),
                         reads=[bank2[1]], writes=[Mm_b[nm_]])
                    mi_ = nm_
                ni = nn_
                yield
                bank3 = pr.next()
                mm8(bank3, [(lambda h: Nn[:, ni, h, :], lambda h: Qq[:, qi_, h, :])], [Qq_b[qi_], Nn_b[ni]])
                P.op("dve", lambda e, nq_=nq_, qo=qi_, i=bank3[0]: e.tensor_tensor(out=Qq[:, nq_], in0=pv(i), in1=Qq[:, qo],
                                                                                   op=ALU.add),
                     reads=[bank3[1], Qq_b[qi_]], writes=[Qq_b[nq_]])
                qi_ = nq_
                yield
            bank = pr.next()
            mm8(bank, [(lambda h: Atok[:, c, h, :], lambda h: Qq[:, qi_, h, :])], [Atok_b[c], Qq_b[qi_]])
            P.op("act", lambda e, i=bank[0]: e.activation(out=WT[:, c], in_=pv(i), func=AF.Copy),
                 reads=[bank[1]], writes=[WT_b[c]])
            bank = pr.next()
            mm8(bank, [(lambda h: Lak[:, c, h, :], lambda h: Vtok[:, c, h, :])], [Lak_b[c], Vtok_b[c]])
            P.op("dve", lambda e, i=bank[0]: e.tensor_copy(out=Xx[:, c], in_=pv(i)), reads=[bank[1]], writes=[Xx_b[c]])
            out["q"] = qi_
            yield

        def chain(n, c, qi_):
            par = n % 2
            X = {k: X2[k][:, par] for k in X2}
            XB = {k: X2B[k][par] for k in X2}
            cs = slice(c * 64, (c + 1) * 64)
            bank = pr.next()
            mm8(bank, [(lambda h: WT[:, c, h, :], lambda h: Sb[:, h, :]),
                       (lambda h: Qq[:, qi_, h, :], lambda h: Xx[:, c, h, :])], [WT_b[c], S_b, Qq_b[qi_], Xx_b[c]])
            P.op("act", lambda e, i=bank[0]: e.activation(out=Uu[:, c], in_=pv(i), func=AF.Copy),
                 reads=[bank[1]], writes=[Uu_b[c]])
            banky = pr.next()
            mm8(banky, [(lambda h: X["rT"][:, h, cs], lambda h: Sb[:, h, :]),
                        (lambda h: Mrb[:, c, h, :], lambda h: Uu[:, c, h, :]),
                        (lambda h: Mrk[:, c, h, :], lambda h: Vtok[:, c, h, :])],
                [XB["rT"], S_b, Mrb_b[c], Uu_b[c], Mrk_b[c], Vtok_b[c]])
            banks = pr.next()
            mm8(banks, [(lambda h: BHtok[:, c, h, :], lambda h: Uu[:, c, h, :]),
                        (lambda h: KHtok[:, c, h, :], lambda h: Vtok[:, c, h, :])], [BHtok_b[c], Uu_b[c], KHtok_b[c], Vtok_b[c]])
            P.op("dve", lambda e: e.tensor_tensor(out=St[:], in0=Sf[:],
                                                  in1=gC[:, par, :, c:c + 1].broadcast_to([64, 8, 64]), op=ALU.mult),
                 reads=[S_b, gC_bs[par]], writes=[St_b])
            P.op("dve", lambda e, i=banks[0]: e.tensor_tensor(out=Sf[:], in0=pv(i), in1=St[:], op=ALU.add),
                 reads=[banks[1], St_b], writes=[S_b])
            P.op("act", lambda e: e.activation(out=Sb[:], in_=Sf[:], func=AF.Copy), reads=[S_b], writes=[S_b])
            yield
            y_b, yq_b, g_b = Ys_b[0], St_b, gst_b[c]
            P.op("act", lambda e, i=banky[0]: e.activation(out=Ys[:, 0], in_=pv(i), func=AF.Copy),
                 reads=[banky[1]], writes=[y_b])
            P.op("pool", lambda e: e.tensor_tensor(out=St[:], in0=Ys[:, 0], in1=Ys[:, 0], op=ALU.mult),
                 reads=[y_b], writes=[yq_b])
            P.op("dve", lambda e: e.tensor_reduce(out=gst[:, c, :, 0], in_=Ys[:, 0], axis=AX.X, op=ALU.add),
                 reads=[y_b], writes=[g_b])
            P.op("dve", lambda e: e.tensor_reduce(out=gst[:, c, :, 1], in_=St[:], axis=AX.X, op=ALU.add),
                 reads=[yq_b, g_b], writes=[g_b])
            yield
            P.op("dve", lambda e: e.tensor_scalar(out=gst[:, c, :, 2], in0=gst[:, c, :, 0], scalar1=1.0 / 64,
                                                  scalar2=None, op0=ALU.mult), reads=[g_b], writes=[g_b])
            P.op("dve", lambda e: e.tensor_tensor(out=gst[:, c, :, 3], in0=gst[:, c, :, 2], in1=gst[:, c, :, 2],
                                                  op=ALU.mult), reads=[g_b], writes=[g_b])
            P.op("dve", lambda e: e.scalar_tensor_tensor(out=gst[:, c, :, 4], in0=gst[:, c, :, 1], scalar=1.0 / 64,
                                                         in1=gst[:, c, :, 3], op0=ALU.mult, op1=ALU.subtract),
                 reads=[g_b], writes=[g_b])
            yield
            P.op("act", lambda e: e.activation(out=gst[:, c, :, 5], in_=gst[:, c, :, 4], func=AF.Sqrt,
                                               bias=C.eps[0:64, 1:2]), reads=[g_b, C.b], writes=[g_b])
            P.op("dve", lambda e: e.reciprocal(out=gst[:, c, :, 5], in_=gst[:, c, :, 5]), reads=[g_b], writes=[g_b])
            P.op("dve", lambda e: e.tensor_tensor(out=Ys[:, 0], in0=Ys[:, 0],
                                                  in1=gst[:, c, :, 2:3].broadcast_to([64, 8, 64]),
                                                  op=ALU.subtract), reads=[y_b, g_b], writes=[y_b])
            yield
            P.op("dve", lambda e: e.tensor_tensor(out=Yn[:, c], in0=Ys[:, 0],
                                                  in1=gst[:, c, :, 5:6].broadcast_to([64, 8, 64]),
                                                  op=ALU.mult), reads=[y_b, g_b], writes=[Yn_b[c]])
            bank = pr.next()
            tr8(bank, lambda h: Yn[:, c, h, :], [Yn_b[c]])
            bc64 = lambda col: col[:, :].unsqueeze(2).broadcast_to([64, 8, 64])
            P.op("dve", lambda e, i=bank[0]: e.tensor_tensor(out=Ys[:, 0], in0=pvb(i), in1=bc64(clw), op=ALU.mult),
                 reads=[bank[1], b_lw, Yn_b[c]], writes=[y_b])
            P.op("pool", lambda e: e.tensor_tensor(out=Ys[:, 0], in0=Ys[:, 0], in1=bc64(clb), op=ALU.add),
                 reads=[y_b, b_lb], writes=[y_b])
            yield
            P.op("dve", lambda e: e.tensor_tensor(out=Ys[:, 0], in0=Ys[:, 0], in1=X["bon"][:, :, cs], op=ALU.add),
                 reads=[y_b, XB["bon"]], writes=[y_b])
            P.op("dve", lambda e: e.tensor_tensor(out=ost[:, :, cs], in0=Ys[:, 0], in1=X["g"][:, :, cs], op=ALU.mult),
                 reads=[y_b, XB["g"]], writes=[ost_b])
            yield

        def scan(n):
            par = n % 2
            X = {k: X2[k][:, par] for k in X2}
            XB = {k: X2B[k][par] for k in X2}
            outs = [{} for _ in range(TL // 64)]
            gens = [chunk_pre(n, c, outs[c]) for c in range(TL // 64)]
            alive = [True] * len(gens)
            while any(alive):
                for gi_, g_ in enumerate(gens):
                    if alive[gi_]:
                        try:
                            next(g_)
                        except StopIteration:
                            alive[gi_] = False
                yield
            for c in range(TL // 64):
                for _ in chain(n, c, outs[c]["q"]):
                    yield
            P.dma("sp", lambda e: e.dma_start(
                out=yr_dram.rearrange("(h v) s -> v h s", v=64)[:, :, n * TL:(n + 1) * TL], in_=ost[:]),
                reads=[ost_b])
            yield

        def run2(f, b):
            fa, ba = f is not None, b is not None
            while fa or ba:
                if fa:
                    try:
                        next(f)
                    except StopIteration:
                        fa = False
                if ba:
                    try:
                        next(b)
                    except StopIteration:
                        ba = False

        for n in range(NB + 1):
            run2(prep(n) if n < NB else None, scan(n - 1) if n >= 1 else None)
    P.barrier()


NITER = 16


def phase_attn(P, nc, C, A, ya_dram):
    x_dram = A["x"]
    with ExitStack() as es:
        sb = lambda name, shape, dt=F32: es.enter_context(nc.sbuf_tensor(name, shape, dt))
        wa = sb("wa", [128, 8, ATTN_IN + 64], BF16)
        wa_b = Buf()
        load_weight_bf16(P, nc, wa, wa_b, A["w_in"], 0, ATTN_IN, 8)
        wv_ = A["w_in"].rearrange("(kc p) n -> p kc n", p=128)
        for kc in range(8):
            P.dma("pool", lambda e, kc=kc: e.dma_start(out=wa[:, kc, ATTN_IN:ATTN_IN + 64], in_=wv_[:, kc, 2048:2112]),
                  writes=[wa_b])
        kT = sb("kT", [128, 4, S], BF16)
        kiT = sb("kiT", [128, S], BF16)
        vaug = sb("vaug", [128, NT, 8, 65], BF16)
        kT_bs = [Buf() for _ in range(NT)]
        kiT_bs = [Buf() for _ in range(NT)]
        va_bs = [Buf() for _ in range(NT)]
        P.op("pool", lambda e: e.memset(vaug[:, :, :, 64:65], 1.0), writes=va_bs)
        gqk = sb("gqk", [128, 2], F32)
        gqk_b = Buf()
        for half in range(2):
            P.dma("sp", lambda e, half=half: e.dma_start(out=gqk[half * 64:(half + 1) * 64, 0:1],
                                                         in_=A["attn_q_norm"].rearrange("o d -> d o"),
                                                         allow_slow_non_contiguous=True), writes=[gqk_b])
            P.dma("sp", lambda e, half=half: e.dma_start(out=gqk[half * 64:(half + 1) * 64, 1:2],
                                                         in_=A["attn_k_norm"].rearrange("o d -> d o"),
                                                         allow_slow_non_contiguous=True), writes=[gqk_b])
        P.op("dve", lambda e: e.tensor_scalar(out=gqk[:, 0:1], in0=gqk[:, 0:1], scalar1=0.125, scalar2=None, op0=ALU.mult),
             reads=[gqk_b], writes=[gqk_b])
        btf = sb("btf", [128, 8, 2, 128], F32)
        bt = sb("bt", [128, 8, 2, 128], BF16)
        b31 = sb("b31_sb", [128, 8], F32)
        bt_b = Buf()
        P.dma("sp", lambda e: e.dma_start(out=btf[:], in_=A["bias_tiles"].rearrange("h c s t -> s h c t")), writes=[bt_b])
        P.dma("sp", lambda e: e.dma_start(out=b31[:], in_=A["b31"].partition_broadcast(128)), writes=[bt_b])
        P.op("dve", lambda e: e.tensor_tensor(out=bt[:].rearrange("p h c t -> p h (c t)"),
                                              in0=btf[:].rearrange("p h c t -> p h (c t)"),
                                              in1=b31[:, :].unsqueeze(2).broadcast_to([128, 8, 256]), op=ALU.subtract),
             reads=[bt_b], writes=[bt_b])
        cmask = sb("cmask", [128, 128], F32)
        onesblk = sb("onesblk", [128, 128], BF16)
        cm_b = Buf()
        pw2 = sb("pw2", [128, 2 * NITER], F32)
        halfs = sb("halfs", [128, 2 * NITER], F32)
        hf_b = Buf()

        def mkc(e):
            e.memset(cmask[:], 0.0)
            e.affine_select(out=cmask[:], in_=cmask[:], pattern=[[-1, 128]], compare_op=ALU.is_ge, fill=NEG, base=0,
                            channel_multiplier=1)
            for j in range(NITER):
                e.memset(pw2[:, j:j + 1], 0.5 ** (j + 1))
                e.memset(pw2[:, NITER + j:NITER + j + 1], 0.5 ** (j + 2))
            e.memset(onesblk[:], 0.0)
            e.memset(onesblk[0:64, 0:64], 1.0)
            return e.memset(onesblk[64:128, 64:128], 1.0)

        P.op("pool", mkc, writes=[cm_b])
        nm = Normer(P, nc, es, C, A["mix_norm"], "an", nslots=1)
        xbuf = sb("axbuf", [128, 2, D], F32)
        xr = Ring([(i, Buf()) for i in range(2)])
        xn = sb("axn", [128, 8, 128], BF16)
        xn_b = Buf()
        q_i = sb("q_i", [128, 2, 4, 128], BF16)
        qi_i = sb("qi_i", [128, 4, 128], BF16)
        wi_i = sb("wi_i", [128, 8], F32)
        q_bs, qi_b, wi_b = [Buf(), Buf()], Buf(), Buf()
        sq = sb("asq", [128, 512], BF16)
        rn = sb("arn", [128, 512], F32)
        sq_b, rn_b = Buf(), Buf()
        sc = sb("sc", [128, S], F32)
        sc_b = Buf()
        rbuf = sb("rbuf", [128, 2, 512], F32)
        rr = Ring([(i, Buf()) for i in range(2)])
        junk = sb("ajunk", [128, S], BF16)
        junk_b = Buf()
        bs = sb("bs", [128, 8], F32)
        bs_b = Buf()
        maskT = sb("maskT", [128, 2, NT, 128], BF16)
        mT_bs = [Buf(), Buf()]
        Et = sb("Et", [128, 3, 4, 128], BF16)
        er = Ring([(i, Buf()) for i in range(3)])
        Pt = sb("Pt", [128, 4, 4, 128], BF16)
        ptr = Ring([(i, Buf()) for i in range(4)])
        rden = sb("rden", [128, 2], F32)
        rdr = Ring([(i, Buf()) for i in range(2)])
        ytile = sb("ytile", [128, 512], BF16)
        yt_b = Buf()
        yst = sb("yst", [128, 4, 128], BF16)
        ys_b = Buf()
        pg = es.enter_context(nc.psum_tensor("apg", [128, 5, 512], F32))
        gr = Ring([(i, Buf()) for i in range(3)])
        lgr = Ring([(i, Buf()) for i in range(3, 5)])
        po = es.enter_context(nc.psum_tensor("apo", [128, 2, 512], F32))
        orr = Ring([(i, Buf()) for i in range(2)])
        pgb = lambda i: pg[:, i, :].bitcast(BF16)
        if SBUF_DEBUG:
            print("attn sbuf remaining", nc.sbuf_bytes_remaining)
        xview = x_dram.rearrange("(t p) d -> t p d", p=128)
        yav = ya_dram.rearrange("(m p) s -> p m s", p=128)

        def front(i):
            ts_ = slice(i * 128, (i + 1) * 128)
            nkb = i + 1
            W = nkb * 128
            q_b = q_bs[i % 2]
            kT_b, kiT_b, va_b = kT_bs[i], kiT_bs[i], va_bs[i]
            xi, xb_ = xr.next()
            P.dma("sp", lambda e, i=i, xi=xi: e.dma_start(out=xbuf[:, xi, :], in_=xview[i]), writes=[xb_])
            nm.run(xbuf[:, xi, :], xb_, xn, xn_b, 0)
            yield

            def proj4(c0, bank):
                bi, bb = bank

                def th(e):
                    ins = None
                    for m in range(4):
                        for kc in range(8):
                            ins = e.matmul(pg[:, bi, m * 128:(m + 1) * 128], lhsT=wa[:, kc, c0 + m * 128:c0 + (m + 1) * 128],
                                           rhs=xn[:, kc, :], start=(kc == 0), stop=(kc == 7))
                    return ins

                P.op("pe", th, reads=[wa_b, xn_b], writes=[bb])

            for which, c0 in ((0, 0), (1, 512)):
                bank = gr.next()
                proj4(c0, bank)
                P.op("act", lambda e, bi=bank[0]: e.activation(out=sq[:], in_=pg[:, bi, :], func=AF.Square),
                     reads=[bank[1]], writes=[sq_b])
                bank2 = gr.next()
                P.op("pe", lambda e, bi=bank2[0]: e.matmul(pg[:, bi, :], lhsT=onesblk[:], rhs=sq[:], start=True, stop=True),
                     reads=[cm_b, sq_b], writes=[bank2[1]])
                P.op("act", lambda e, bi=bank2[0]: e.activation(out=rn[:], in_=pg[:, bi, :], func=AF.Sqrt, scale=1.0 / 64,
                                                                bias=C.eps[:, 0:1]), reads=[bank2[1], C.b], writes=[rn_b])
                P.op("dve", lambda e: e.reciprocal(out=rn[:], in_=rn[:]), reads=[rn_b], writes=[rn_b])
                if which == 0:
                    P.op("dve", lambda e, bi=bank[0]: e.scalar_tensor_tensor(
                        out=q_i[:, i % 2].rearrange("p m t -> p (m t)"), in0=pg[:, bi, :], scalar=gqk[:, 0:1], in1=rn[:],
                        op0=ALU.mult, op1=ALU.mult), reads=[bank[1], rn_b, gqk_b], writes=[q_b])
                else:
                    P.op("dve", lambda e, bi=bank[0], ts_=ts_: e.scalar_tensor_tensor(
                        out=kT[:, :, ts_], in0=pg[:, bi, :].rearrange("p (m t) -> p m t", m=4), scalar=gqk[:, 1:2],
                        in1=rn[:].rearrange("p (m t) -> p m t", m=4), op0=ALU.mult, op1=ALU.mult),
                        reads=[bank[1], rn_b, gqk_b], writes=[kT_b])
                yield
            bank = gr.next()
            proj4(1536, bank)
            P.op("act", lambda e, bi=bank[0]: e.activation(out=qi_i[:].rearrange("p m t -> p (m t)"), in_=pg[:, bi, :],
                                                           func=AF.Copy), reads=[bank[1]], writes=[qi_b])
            yield
            bank = gr.next()

            def kiw(e, bi=bank[0]):
                for kc in range(8):
                    e.matmul(pg[0:64, bi, 0:128], lhsT=wa[:, kc, 2048:2112], rhs=xn[:, kc, :], start=(kc == 0), stop=(kc == 7))
                for kc in range(8):
                    e.matmul(pg[64:128, bi, 0:128], lhsT=wa[:, kc, ATTN_IN:ATTN_IN + 64], rhs=xn[:, kc, :], start=(kc == 0),
                             stop=(kc == 7))
                ins = None
                for kc in range(8):
                    ins = e.matmul(pg[:, bi, 128:136], lhsT=xn[:, kc, :], rhs=wa[:, kc, 2112:2120], start=(kc == 0), stop=(kc == 7))
                return ins

            P.op("pe", kiw, reads=[wa_b, xn_b], writes=[bank[1]])
            P.op("act", lambda e, bi=bank[0], ts_=ts_: e.activation(out=kiT[:, ts_], in_=pg[:, bi, 0:128], func=AF.Copy),
                 reads=[bank[1]], writes=[kiT_b])
            P.op("dve", lambda e, bi=bank[0]: e.tensor_copy(out=wi_i[:], in_=pg[:, bi, 128:136]), reads=[bank[1]], writes=[wi_b])
            yield
            bank = gr.next()

            def vmm(e, bi=bank[0]):
                ins = None
                for kc in range(8):
                    ins = e.matmul(pg[:, bi, :], lhsT=xn[:, kc, :], rhs=wa[:, kc, 1024:1536], start=(kc == 0), stop=(kc == 7))
                return ins

            P.op("pe", vmm, reads=[wa_b, xn_b], writes=[bank[1]])
            P.op("act", lambda e, bi=bank[0], i=i: e.activation(out=vaug[:, i, :, 0:64],
                                                                in_=pg[:, bi, :].rearrange("p (h d) -> p h d", h=8), func=AF.Copy),
                 reads=[bank[1]], writes=[va_b])
            yield

            for gk in range((nkb + 3) // 4):
                w_ = min(512, W - gk * 512)
                for h in range(8):
                    hb = (h % 2) * 64
                    bank = gr.next()
                    P.op("pe", lambda e, bi=bank[0], h=h, hb=hb, gk=gk, w_=w_: e.matmul(
                        pg[:, bi, 0:w_], lhsT=qi_i[hb:hb + 64, h // 2, :], rhs=kiT[hb:hb + 64, gk * 512:gk * 512 + w_],
                        start=True, stop=True), reads=[qi_b] + kiT_bs[gk * 4:gk * 4 + (w_ // 128)], writes=[bank[1]])
                    ri, rb_ = rr.next()
                    P.op("act", lambda e, bi=bank[0], ri=ri, w_=w_: e.activation(out=rbuf[:, ri, 0:w_], in_=pg[:, bi, 0:w_],
                                                                                 func=AF.Relu), reads=[bank[1]], writes=[rb_])
                    if h == 0:
                        P.op("dve", lambda e, ri=ri, gk=gk, w_=w_: e.tensor_scalar(
                            out=sc[:, gk * 512:gk * 512 + w_], in0=rbuf[:, ri, 0:w_], scalar1=wi_i[:, 0:1], scalar2=None,
                            op0=ALU.mult), reads=[rb_, wi_b], writes=[sc_b])
                    else:
                        P.op("dve", lambda e, ri=ri, gk=gk, w_=w_, h=h: e.scalar_tensor_tensor(
                            out=sc[:, gk * 512:gk * 512 + w_], in0=rbuf[:, ri, 0:w_], scalar=wi_i[:, h:h + 1],
                            in1=sc[:, gk * 512:gk * 512 + w_], op0=ALU.mult, op1=ALU.add), reads=[rb_, wi_b, sc_b], writes=[sc_b])
                    yield
            P.op("dve", lambda e, ts_=ts_: e.tensor_tensor(out=sc[:, ts_], in0=sc[:, ts_], in1=cmask[:], op=ALU.add),
                 reads=[sc_b, cm_b], writes=[sc_b])
            if i < 2:
                P.op("dve", lambda e: e.memset(bs[:, 0:1], -1.0e29), writes=[bs_b])
            else:
                P.op("dve", lambda e, i=i: e.tensor_reduce(out=bs[:, 0:1], in_=sc[:, 0:i * 128], axis=AX.X, op=ALU.min),
                     reads=[sc_b], writes=[bs_b])
                P.op("dve", lambda e, W=W: e.tensor_reduce(out=bs[:, 6:7], in_=sc[:, 0:W], axis=AX.X, op=ALU.max),
                     reads=[sc_b, bs_b], writes=[bs_b])
                P.op("dve", lambda e: e.tensor_tensor(out=bs[:, 1:2], in0=bs[:, 6:7], in1=bs[:, 0:1], op=ALU.subtract),
                     reads=[bs_b], writes=[bs_b])
                P.op("dve", lambda e: e.tensor_tensor(out=halfs[:], in0=bs[:, 1:2].broadcast_to([128, 2 * NITER]), in1=pw2[:],
                                                      op=ALU.mult), reads=[bs_b, cm_b], writes=[hf_b])
                P.op("dve", lambda e: e.tensor_tensor(out=bs[:, 3:4], in0=bs[:, 0:1], in1=halfs[:, 0:1], op=ALU.add),
                     reads=[bs_b, hf_b], writes=[bs_b])
                nit = min(NITER, int(math.ceil(math.log2(W))) + 4)
                for it in range(nit):
                    P.op("dve", lambda e: e.tensor_scalar(out=junk[:, 0:W], in0=sc[:, 0:W], scalar1=bs[:, 3:4], scalar2=None,
                                                          op0=ALU.is_ge, op1=ALU.add, accum_out=bs[:, 4:5]),
                         reads=[sc_b, bs_b], writes=[junk_b, bs_b])
                    P.op("dve", lambda e, it=it: e.scalar_tensor_tensor(out=bs[:, 5:6], in0=bs[:, 4:5], scalar=TOPK - 0.5,
                                                                        in1=halfs[:, it:it + 1], op0=ALU.is_ge, op1=ALU.mult),
                         reads=[bs_b, hf_b], writes=[bs_b])
                    P.op("dve", lambda e, it=it: e.scalar_tensor_tensor(out=bs[:, 3:4], in0=bs[:, 5:6],
                                                                        scalar=halfs[:, NITER + it:NITER + it + 1],
                                                                        in1=bs[:, 3:4], op0=ALU.subtract, op1=ALU.add),
                         reads=[bs_b, hf_b], writes=[bs_b])
                    yield
                P.op("dve", lambda e, nit=nit: e.tensor_tensor(out=bs[:, 0:1], in0=bs[:, 3:4], in1=halfs[:, NITER + nit - 1:NITER + nit],
                                                               op=ALU.subtract), reads=[bs_b, hf_b], writes=[bs_b])
            P.op("dve", lambda e, W=W: e.tensor_scalar(out=junk[:, 0:W], in0=sc[:, 0:W], scalar1=bs[:, 0:1], scalar2=None,
                                                       op0=ALU.is_ge), reads=[sc_b, bs_b], writes=[junk_b])
            for j0 in range(0, nkb, 8):
                nb = min(8, nkb - j0)
                bank = gr.next()

                def trm(e, bi=bank[0], j0=j0, nb=nb):
                    ins = None
                    for jj in range(nb):
                        ins = e.transpose(out=pgb(bi)[:, jj * 128:(jj + 1) * 128], in_=junk[:, (j0 + jj) * 128:(j0 + jj + 1) * 128],
                                          identity=C.ident_bf[:])
                    return ins

                P.op("pe", trm, reads=[junk_b, C.b], writes=[bank[1]])
                P.op("act", lambda e, bi=bank[0], j0=j0, nb=nb: e.activation(
                    out=maskT[:, i % 2, j0:j0 + nb, :].rearrange("p j t -> p (j t)"), in_=pgb(bi)[:, 0:nb * 128], func=AF.Copy),
                    reads=[bank[1]], writes=[mT_bs[i % 2]])
                yield
            yield

        def back(i):
            ts_ = slice(i * 128, (i + 1) * 128)
            nkb = i + 1
            q_b = q_bs[i % 2]
            mT_b = mT_bs[i % 2]
            items = [(h, j0, min(4, nkb - j0)) for h in range(8) for j0 in range(0, nkb, 4)]
            DEPTH = 2
            st = {}
            obs = {}
            for k in range(len(items) + DEPTH):
                if k < len(items):
                    h, j0, nb = items[k]
                    hb = (h % 2) * 64
                    m = h // 2
                    if j0 == 0:
                        obs[h] = orr.next()
                    bank = lgr.next()

                    def qk(e, bi=bank[0], j0=j0, nb=nb, h=h, hb=hb, m=m):
                        ins = None
                        for jj in range(nb):
                            j = j0 + jj
                            near = j >= i - 1
                            ins = e.matmul(pg[:, bi, jj * 128:(jj + 1) * 128], lhsT=kT[hb:hb + 64, m, j * 128:(j + 1) * 128],
                                           rhs=q_i[hb:hb + 64, i % 2, m, :], start=True, stop=not near)
                            if near:
                                ins = e.matmul(pg[:, bi, jj * 128:(jj + 1) * 128], lhsT=C.ident_bf[:],
                                               rhs=bt[:, h, 0 if j == i else 1, :], start=False, stop=True)
                        return ins

                    P.op("pe", qk, reads=kT_bs[j0:j0 + nb] + [q_b, bt_b, C.b], writes=[bank[1]])
                    ei, eb = er.next()
                    P.op("act", lambda e, bi=bank[0], ei=ei, nb=nb: e.activation(
                        out=Et[:, ei, 0:nb, :].rearrange("p j t -> p (j t)"), in_=pg[:, bi, 0:nb * 128], func=AF.Exp),
                        reads=[bank[1]], writes=[eb])
                    pi, pb_ = ptr.next()
                    P.op("pool", lambda e, ei=ei, pi=pi, j0=j0, nb=nb: e.tensor_tensor(
                        out=Pt[:, pi, 0:nb, :], in0=Et[:, ei, 0:nb, :], in1=maskT[:, i % 2, j0:j0 + nb, :], op=ALU.mult),
                        reads=[eb, mT_b], writes=[pb_])
                    st[k] = (pi, pb_)
                kk = k - DEPTH
                if kk >= 0:
                    h, j0, nb = items[kk]
                    pi, pb_ = st.pop(kk)
                    ob = obs[h]

                    def pv(e, oi=ob[0], pi=pi, j0=j0, nb=nb, h=h):
                        ins = None
                        for jj in range(nb):
                            j = j0 + jj
                            ins = e.matmul(po[:, oi, 0:65], lhsT=Pt[:, pi, jj, :], rhs=vaug[:, j, h, :], start=(j == 0),
                                           stop=(j == i))
                        return ins

                    P.op("pe", pv, reads=[pb_] + va_bs[j0:j0 + nb], writes=[ob[1]])
                    if j0 + nb == nkb:
                        di, db = rdr.next()
                        P.op("dve", lambda e, oi=ob[0], di=di: e.reciprocal(out=rden[:, di:di + 1], in_=po[:, oi, 64:65]),
                             reads=[ob[1]], writes=[db])
                        P.op("act", lambda e, oi=ob[0], di=di, h=h: e.activation(out=ytile[:, h * 64:(h + 1) * 64],
                                                                                 in_=po[:, oi, 0:64], func=AF.Copy,
                                                                                 scale=rden[:, di:di + 1]),
                             reads=[ob[1], db], writes=[yt_b])
                yield
            bank = lgr.next()

            def try_(e, bi=bank[0]):
                ins = None
                for m in range(4):
                    ins = e.transpose(out=pgb(bi)[:, m * 128:(m + 1) * 128], in_=ytile[:, m * 128:(m + 1) * 128],
                                      identity=C.ident_bf[:])
                return ins

            P.op("pe", try_, reads=[yt_b, C.b], writes=[bank[1]])
            P.op("act", lambda e, bi=bank[0]: e.activation(out=yst[:].rearrange("p m t -> p (m t)"), in_=pgb(bi)[:, 0:512],
                                                           func=AF.Copy), reads=[bank[1]], writes=[ys_b])
            P.dma("sp", lambda e: e.dma_start(out=yav[:, :, ts_], in_=yst[:]), reads=[ys_b])
            yield

        def run2(f, b):
            fa, ba = f is not None, b is not None
            while fa or ba:
                if fa:
                    try:
                        next(f)
                    except StopIteration:
                        fa = False
                if ba:
                    try:
                        next(b)
                    except StopIteration:
                        ba = False

        for i in range(NT + 1):
            run2(front(i) if i < NT else None, back(i - 1) if i >= 1 else None)
    P.barrier()


WEIGHT_SPECS = [
    ("mix_norm", [1, D]), ("w_in", [D, 5992]), ("attn_q_norm", [1, 64]), ("attn_k_norm", [1, 64]),
    ("bias_tiles", [8, 2, 128, 128]), ("b31", [1, 8]), ("rwkv_mu", [1, RWKV_IN]), ("rwkv_w0", [1, 512]), ("rwkv_w2", [64, 512]),
    ("rwkv_a0", [1, 512]), ("rwkv_a2", [64, 512]), ("rwkv_g2", [160, 512]), ("rwkv_k_k", [1, 512]),
    ("rwkv_k_a", [1, 512]), ("rwkv_r_k", [1, 512]), ("rwkv_ln_w", [1, 512]), ("rwkv_ln_b", [1, 512]),
    ("w_branch_attn", [512, D]), ("w_branch_rwkv", [512, D]), ("w_out", [D, D]), ("ffn_norm", [1, D]),
    ("w_gate_up", [D, 2 * FFN_H]), ("w_down", [FFN_H, D]),
]


def build_program(phases=("attn", "rwkv", "merge", "ffn"), debug=False):
    nc = bass.Bass("TRN2", target_bir_lowering=False)
    A = {}
    A["x"] = nc.dram_tensor("x", [S, D], F32, kind="ExternalInput").ap()
    for name, shp in WEIGHT_SPECS:
        A[name] = nc.dram_tensor(name, shp, F32, kind="ExternalInput").ap()
    out = nc.dram_tensor("out", [S, D], F32, kind="ExternalOutput").ap()
    def kind(prod, cons):
        if not debug:
            return "Internal"
        if prod in phases and cons not in phases:
            return "ExternalOutput"
        if prod not in phases and cons in phases:
            return "ExternalInput"
        return "Internal"

    ya = nc.dram_tensor("ya_scr", [512, S], BF16, kind=kind("attn", "merge")).ap()
    yr = nc.dram_tensor("yr_scr", [512, S], BF16, kind=kind("rwkv", "merge")).ap()
    hs = nc.dram_tensor("h_scr", [S, D], F32, kind=kind("merge", "ffn")).ap()
    P = Prog(nc)
    final_ops = []
    with ExitStack() as es:
        C = make_consts(P, nc, es)
        P.barrier()
        if "attn" in phases:
            phase_attn(P, nc, C, A, ya)
        if "rwkv" in phases:
            phase_rwkv(P, nc, C, A, yr)
        if "merge" in phases:
            phase_merge(P, nc, C, A["x"], ya, yr, hs, A["mix_norm"], A["w_in"], A["w_branch_attn"],
                        A["w_branch_rwkv"], A["w_out"])
        if "ffn" in phases:
            phase_ffn(P, nc, C, hs, out, A["ffn_norm"], A["w_gate_up"], A["w_down"], final_ops)
        P.emit(final_wait_ops=final_ops)
    return nc


def t5_bucket_np(d):
    d = np.maximum(d, 0)
    max_exact = 16
    log_ratio = np.log(np.maximum(d, 1).astype(np.float32) / max_exact) / math.log(128 / max_exact)
    large = np.minimum(max_exact + (log_ratio * 16).astype(np.int32), 31)
    return np.where(d < max_exact, d, large)


def host_layout(inputs):
    w = {}
    for name, shp in WEIGHT_SPECS:
        if name in ("bias_tiles", "b31"):
            continue
        w[name] = np.ascontiguousarray(np.asarray(inputs[name], dtype=np.float32).reshape(shp))
    s_idx = np.arange(128)[:, None]
    t_idx = np.arange(128)[None, :]
    rb = np.asarray(inputs["rel_bias"], dtype=np.float32)
    tiles = np.empty((8, 2, 128, 128), np.float32)
    for cls in range(2):
        bk = t5_bucket_np(t_idx - s_idx + 128 * cls)
        tiles[:, cls] = np.transpose(rb[bk], (2, 0, 1))
    w["bias_tiles"] = tiles
    w["b31"] = np.ascontiguousarray(rb[31:32, :])
    return w


_NC_CACHE = {}


def kernel(**inputs):
    x = np.asarray(inputs["x"], dtype=np.float32)
    w = host_layout(inputs)
    if "nc" not in _NC_CACHE:
        _NC_CACHE["nc"] = build_program()
    nc = _NC_CACHE["nc"]
    in_maps = []
    for b in range(8):
        m = dict(w)
        m["x"] = np.ascontiguousarray(x[b])
        in_maps.append(m)
    res = run_bass_kernel_spmd(nc, in_maps, core_ids=list(range(8)))
    return np.stack([np.asarray(r["out"], dtype=np.float32) for r in res.results], axis=0)
```

```python
import math
from contextlib import ExitStack

import numpy as np
import concourse.bass as bass
import concourse.mybir as mybir
from concourse.bass_utils import run_bass_kernel_spmd

F32 = mybir.dt.float32
BF16 = mybir.dt.bfloat16
AF = mybir.ActivationFunctionType
ALU = mybir.AluOpType
AX = mybir.AxisListType

S = 4096
D = 1024
NT = S // 128
ATTN_IN = 2120
RWKV_IN = 1824
FFN_H = 2816
RMS_EPS = 1e-6
GN_EPS = 64e-5
TOPK = 256
NEG = -1.0e30

ENGS = ("pe", "act", "dve", "pool", "sp")
SBUF_DEBUG = False


class Buf:
    __slots__ = ("name", "w", "r")

    def __init__(self, name=""):
        self.name = name
        self.w = None
        self.r = []


class Op:
    __slots__ = ("eng", "thunk", "deps", "is_dma", "sem", "val", "need_inc", "pos", "prev_on_sem")


class Prog:
    def __init__(self, nc, n_dma_sems=(56, 28)):
        self.nc = nc
        self.streams = {e: [] for e in ENGS}
        self.n_dma_sems = n_dma_sems
        self.dma_count = 0
        self.all_ops = []
        self.open_dmas = []

    def _hazards(self, op, reads, writes):
        deps = []
        for b in reads:
            if b.w is not None:
                deps.append(b.w)
        for b in writes:
            if b.w is not None:
                deps.append(b.w)
            deps.extend(b.r)
        for b in reads:
            b.r.append(op)
        for b in writes:
            b.w = op
            b.r = []
        return deps

    def op(self, eng, thunk, reads=(), writes=(), extra_deps=()):
        o = Op()
        o.eng = eng
        o.thunk = thunk
        o.is_dma = False
        o.need_inc = False
        o.sem = None
        o.val = None
        o.prev_on_sem = None
        deps = self._hazards(o, reads, writes) + list(extra_deps)
        seen = set()
        o.deps = []
        for d in deps:
            if d is o or id(d) in seen:
                continue
            if eng == "pe" and d.eng == "pe" and not d.is_dma:
                continue
            seen.add(id(d))
            o.deps.append(d)
        self.streams[eng].append(o)
        self.all_ops.append(o)
        return o

    def dma(self, eng, thunk, reads=(), writes=(), extra_deps=()):
        o = self.op(eng, thunk, reads, writes, extra_deps)
        o.is_dma = True
        o.pos = self.dma_count
        self.dma_count += 1
        self.open_dmas.append(o)
        return o

    def barrier(self):
        lasts = []
        for e in ENGS:
            for o in reversed(self.streams[e]):
                if not o.is_dma:
                    lasts.append(o)
                    break
        deps = lasts + self.open_dmas
        self.open_dmas = []
        for e in ENGS:
            self.op(e, lambda eng: eng.nop(), extra_deps=deps)

    def emit(self, final_wait_ops=()):
        nc = self.nc
        for o in self.all_ops:
            for d in o.deps:
                d.need_inc = True
        eng_sems = {e: nc.alloc_semaphore("s_" + e) for e in ENGS}
        ring_n = {"sp": self.n_dma_sems[0], "pool": self.n_dma_sems[1], "act": 2, "dve": 2, "pe": 2}
        dma_sems = {}
        dma_sem_val = {}
        dma_prev = {}
        qpos = {e: 0 for e in ENGS}
        for e in ENGS:
            if any(o.is_dma for o in self.streams[e]):
                for i in range(ring_n[e]):
                    dma_sems[(e, i)] = nc.alloc_semaphore("s_dma_%s%d" % (e, i))
                    dma_sem_val[(e, i)] = 0
                    dma_prev[(e, i)] = None
        cnt = {e: 0 for e in ENGS}
        for o in self.all_ops:
            if o.is_dma:
                kq = (o.eng, qpos[o.eng] % ring_n[o.eng])
                qpos[o.eng] += 1
                dma_sem_val[kq] += 16
                o.sem = ("dma", kq)
                o.val = dma_sem_val[kq]
                o.prev_on_sem = dma_prev[kq]
                dma_prev[kq] = o
            elif o.need_inc:
                cnt[o.eng] += 1
                o.sem = ("eng", o.eng)
                o.val = cnt[o.eng]

        def semh(key):
            return eng_sems[key[1]] if key[0] == "eng" else dma_sems[key[1]]

        engines = {"pe": "tensor", "act": "scalar", "dve": "vector", "pool": "gpsimd", "sp": "sync"}
        with nc.Block() as block:
            for e in ENGS:
                stream = self.streams[e]
                final = list(final_wait_ops) if e == "sp" else []

                def body(engine, stream=stream, final=final):
                    known = {}
                    for o in stream:
                        waits = {}
                        deps = list(o.deps)
                        if o.is_dma and o.prev_on_sem is not None:
                            deps.append(o.prev_on_sem)
                        for d in deps:
                            if known.get(d.sem, 0) >= d.val:
                                continue
                            if waits.get(d.sem, 0) < d.val:
                                waits[d.sem] = d.val
                        for key, val in waits.items():
                            engine.wait_ge(semh(key), val)
                            known[key] = val
                        ins = o.thunk(engine)
                        if o.is_dma:
                            ins.then_inc(semh(o.sem), 16)
                        elif o.need_inc:
                            ins.then_inc(semh(o.sem), 1)
                    for o in final:
                        engine.wait_ge(semh(o.sem), o.val)

                getattr(block, engines[e])(body)


class Ring:
    def __init__(self, items):
        self.items = items
        self.i = 0

    def next(self):
        it = self.items[self.i % len(self.items)]
        self.i += 1
        return it


def load_weight_bf16(P, nc, dst, dst_buf, w_ap, c0, c1, kchunks, eng="pool"):
    wv = w_ap.rearrange("(kc p) n -> p kc n", p=128)
    for kc in range(kchunks):
        for a in range(c0, c1, 2048):
            b = min(c1, a + 2048)
            P.dma(eng, lambda e, kc=kc, a=a, b=b: e.dma_start(out=dst[:, kc, a - c0:b - c0], in_=wv[:, kc, a:b]),
                  writes=[dst_buf])


def load_col_vec(P, nc, dst, dst_buf, v_ap, n):
    src = v_ap.rearrange("o (c p) -> p (o c)", p=128)
    P.dma("sp", lambda e: e.dma_start(out=dst, in_=src, allow_slow_non_contiguous=True), writes=[dst_buf])


class Consts:
    pass


def make_consts(P, nc, es):
    C = Consts()
    C.ident_bf = es.enter_context(nc.sbuf_tensor("ident_bf", [128, 128], BF16))
    C.ident_f = es.enter_context(nc.sbuf_tensor("ident_f", [128, 128], F32))
    C.eps = es.enter_context(nc.sbuf_tensor("eps_c", [128, 2], F32))
    C.b = Buf("consts")

    def mk(e):
        e.memset(C.ident_f[:], 0.0)
        e.affine_select(out=C.ident_f[:], in_=C.ident_f[:], pattern=[[-1, 128]], compare_op=ALU.not_equal,
                        fill=1.0, base=0, channel_multiplier=1)
        e.memset(C.eps[:, 0:1], RMS_EPS)
        return e.memset(C.eps[:, 1:2], GN_EPS)

    P.op("pool", mk, writes=[C.b])
    P.op("pool", lambda e: e.tensor_copy(out=C.ident_bf[:], in_=C.ident_f[:]), reads=[C.b], writes=[C.b])
    return C


class Normer:
    def __init__(self, P, nc, es, C, gain_ap, name, nslots=2):
        self.P, self.nc, self.C = P, nc, C
        self.gcol = es.enter_context(nc.sbuf_tensor(name + "_g", [128, 8], F32))
        self.gb = Buf(name + "_g")
        load_col_vec(P, nc, self.gcol[:, :], self.gb, gain_ap, 8)
        self.stat = es.enter_context(nc.sbuf_tensor(name + "_st", [128, nslots, 4], F32))
        self.junk = es.enter_context(nc.sbuf_tensor(name + "_junk", [128, 1024], BF16))
        self.xs = es.enter_context(nc.sbuf_tensor(name + "_xs", [128, nslots, 1024], BF16))
        self.tp = es.enter_context(nc.psum_tensor(name + "_tp", [128, nslots, 8, 128], BF16))
        self.ring = Ring([(i, Buf(), Buf(), Buf()) for i in range(nslots)])
        self.junkb = Buf()

    def run(self, xt_ap, xt_buf, dst, dst_buf, col0):
        P, C = self.P, self.C
        i, sb, xb, pb = self.ring.next()
        st = self.stat
        P.op("act", lambda e: e.activation(out=self.junk[:], in_=xt_ap, func=AF.Square, accum_out=st[:, i, 0:1]),
             reads=[xt_buf], writes=[self.junkb, sb])
        P.op("act", lambda e: e.activation(out=st[:, i, 1:2], in_=st[:, i, 0:1], func=AF.Sqrt, scale=1.0 / D,
                                           bias=C.eps[:, 0:1]), reads=[sb, C.b], writes=[sb])
        P.op("dve", lambda e: e.reciprocal(out=st[:, i, 2:3], in_=st[:, i, 1:2]), reads=[sb], writes=[sb])
        P.op("act", lambda e: e.activation(out=self.xs[:, i, :], in_=xt_ap, func=AF.Copy, scale=st[:, i, 2:3]),
             reads=[xt_buf, sb], writes=[xb])

        def tr(e):
            ins = None
            for kc in range(8):
                ins = e.transpose(out=self.tp[:, i, kc, :], in_=self.xs[:, i, kc * 128:(kc + 1) * 128],
                                  identity=C.ident_bf[:])
            return ins

        P.op("pe", tr, reads=[xb, C.b], writes=[pb])
        P.op("dve", lambda e: e.tensor_tensor(out=dst[:, :, col0:col0 + 128], in0=self.tp[:, i, :, :],
                                              in1=self.gcol[:, :].unsqueeze(2).broadcast_to([128, 8, 128]),
                                              op=ALU.mult), reads=[pb, self.gb], writes=[dst_buf])


def phase_ffn(P, nc, C, h_dram, out_dram, ffn_norm, w_gate_up, w_down, final_ops):
    with ExitStack() as es:
        wgu = es.enter_context(nc.sbuf_tensor("wgu", [128, 8, 2 * FFN_H], BF16))
        wd = es.enter_context(nc.sbuf_tensor("wd", [128, 22, D], BF16))
        wgu_b, wd_b = Buf("wgu"), Buf("wd")
        load_weight_bf16(P, nc, wgu, wgu_b, w_gate_up, 0, 2 * FFN_H, 8)
        load_weight_bf16(P, nc, wd, wd_b, w_down, 0, D, 22)
        nm = Normer(P, nc, es, C, ffn_norm, "fn")
        hbuf = es.enter_context(nc.sbuf_tensor("hbuf", [128, 2, D], F32))
        hr = Ring([(i, Buf()) for i in range(2)])
        hnT = es.enter_context(nc.sbuf_tensor("hnT", [128, 2, 8, 512], BF16))
        hnb = [Buf(), Buf()]
        actT = es.enter_context(nc.sbuf_tensor("actT", [128, 22, 512], BF16))
        actb = [Buf() for _ in range(22)]
        sg = es.enter_context(nc.sbuf_tensor("sg", [128, 2, 512], F32))
        sgr = Ring([(i, Buf()) for i in range(2)])
        ost = es.enter_context(nc.sbuf_tensor("ost", [128, 2, D], F32))
        ostr = Ring([(i, Buf()) for i in range(2)])
        pg = es.enter_context(nc.psum_tensor("pg", [128, 2, 512], F32))
        pu = es.enter_context(nc.psum_tensor("pu", [128, 2, 512], F32))
        po = es.enter_context(nc.psum_tensor("po", [128, 2, 512], F32))
        pgr = Ring([(i, Buf()) for i in range(2)])
        pur = Ring([(i, Buf()) for i in range(2)])
        por = Ring([(i, Buf()) for i in range(2)])
        hview = h_dram.rearrange("(t p) d -> t p d", p=128)
        oview = out_dram.rearrange("(t p) d -> t p d", p=128)
        def ffn_norm_chunk(c):
            cb = c % 2
            for j in range(4):
                t = c * 4 + j
                hi, hb_ = hr.next()
                P.dma("sp", lambda e, t=t, hi=hi: e.dma_start(out=hbuf[:, hi, :], in_=hview[t]), writes=[hb_])
                nm.run(hbuf[:, hi, :], hb_, hnT[:, cb], hnb[cb], j * 128)

        ffn_norm_chunk(0)
        for c in range(S // 512):
            cb = c % 2
            if c + 1 < S // 512:
                ffn_norm_chunk(c + 1)
            for m in range(22):
                gi, gbuf = pgr.next()
                ui, ubuf = pur.next()

                def mm(e, m=m, gi=gi, ui=ui, cb=cb):
                    for kc in range(8):
                        e.matmul(pg[:, gi, :], lhsT=wgu[:, kc, m * 128:(m + 1) * 128], rhs=hnT[:, cb, kc, :],
                                 start=(kc == 0), stop=(kc == 7))
                    ins = None
                    for kc in range(8):
                        ins = e.matmul(pu[:, ui, :], lhsT=wgu[:, kc, FFN_H + m * 128:FFN_H + (m + 1) * 128],
                                       rhs=hnT[:, cb, kc, :], start=(kc == 0), stop=(kc == 7))
                    return ins

                P.op("pe", mm, reads=[wgu_b, hnb[cb]], writes=[gbuf, ubuf])
                si, sbuf_ = sgr.next()
                P.op("act", lambda e, gi=gi, si=si: e.activation(out=sg[:, si, :], in_=pg[:, gi, :], func=AF.Silu),
                     reads=[gbuf], writes=[sbuf_])
                P.op("dve", lambda e, ui=ui, si=si, m=m: e.tensor_tensor(out=actT[:, m, :], in0=pu[:, ui, :],
                                                                         in1=sg[:, si, :], op=ALU.mult),
                     reads=[ubuf, sbuf_], writes=[actb[m]])
            for j in range(4):
                t = c * 4 + j
                oi, obuf = ostr.next()
                P.dma("sp", lambda e, t=t, oi=oi: e.dma_start(out=ost[:, oi, :], in_=hview[t]), writes=[obuf])
                for nh in range(2):
                    pi, pbuf = por.next()

                    def mmd(e, j=j, nh=nh, pi=pi):
                        ins = None
                        for m in range(22):
                            ins = e.matmul(po[:, pi, :], lhsT=actT[:, m, j * 128:(j + 1) * 128],
                                           rhs=wd[:, m, nh * 512:(nh + 1) * 512], start=(m == 0), stop=(m == 21))
                        return ins

                    P.op("pe", mmd, reads=actb + [wd_b], writes=[pbuf])
                    P.op("dve", lambda e, nh=nh, pi=pi, oi=oi: e.tensor_tensor(
                        out=ost[:, oi, nh * 512:(nh + 1) * 512], in0=po[:, pi, :],
                        in1=ost[:, oi, nh * 512:(nh + 1) * 512], op=ALU.add),
                        reads=[pbuf], writes=[obuf])
                final_ops.append(P.dma("sp", lambda e, t=t, oi=oi: e.dma_start(out=oview[t], in_=ost[:, oi, :]),
                                       reads=[obuf]))
    P.barrier()


def phase_merge(P, nc, C, x_dram, ya_dram, yr_dram, h_dram, mix_norm, w_in, w_ba, w_br, w_out):
    with ExitStack() as es:
        wg = es.enter_context(nc.sbuf_tensor("wg", [128, 8, 2 * D], BF16))
        wba = es.enter_context(nc.sbuf_tensor("wba", [128, 4, D], BF16))
        wbr = es.enter_context(nc.sbuf_tensor("wbr", [128, 4, D], BF16))
        wo = es.enter_context(nc.sbuf_tensor("wo", [128, 8, D], BF16))
        wg_b, wba_b, wbr_b, wo_b = Buf(), Buf(), Buf(), Buf()
        load_weight_bf16(P, nc, wg, wg_b, w_in, ATTN_IN + RWKV_IN, ATTN_IN + RWKV_IN + 2 * D, 8)
        load_weight_bf16(P, nc, wba, wba_b, w_ba, 0, D, 4)
        load_weight_bf16(P, nc, wbr, wbr_b, w_br, 0, D, 4)
        load_weight_bf16(P, nc, wo, wo_b, w_out, 0, D, 8)
        nm = Normer(P, nc, es, C, mix_norm, "mn")
        xbuf = es.enter_context(nc.sbuf_tensor("xbuf", [128, 2, 4, D], F32))
        xb = [[Buf() for _ in range(4)] for _ in range(2)]
        xnT = es.enter_context(nc.sbuf_tensor("xnT", [128, 2, 8, 512], BF16))
        xnb = [Buf(), Buf()]
        yaT = es.enter_context(nc.sbuf_tensor("yaT", [128, 2, 4, 512], BF16))
        yrT = es.enter_context(nc.sbuf_tensor("yrT", [128, 2, 4, 512], BF16))
        yab, yrb = [Buf(), Buf()], [Buf(), Buf()]
        mT = es.enter_context(nc.sbuf_tensor("mT", [128, 8, 512], BF16))
        mb = [Buf() for _ in range(8)]
        sg = es.enter_context(nc.sbuf_tensor("sgm", [128, 2, 2, 512], F32))
        sgr = Ring([(i, Buf()) for i in range(2)])
        tt = es.enter_context(nc.sbuf_tensor("ttm", [128, 2, 2, 512], F32))
        ttr = Ring([(i, Buf()) for i in range(2)])
        hst = es.enter_context(nc.sbuf_tensor("hst", [128, 2, D], F32))
        hstr = Ring([(i, Buf()) for i in range(2)])
        pga = es.enter_context(nc.psum_tensor("pga", [128, 2, 512], F32))
        pbr = es.enter_context(nc.psum_tensor("pbr", [128, 2, 512], F32))
        po = es.enter_context(nc.psum_tensor("pom", [128, 2, 512], F32))
        pgb, pbb = Buf(), Buf()
        por = Ring([(i, Buf()) for i in range(2)])
        xview = x_dram.rearrange("(t p) d -> t p d", p=128)
        hview = h_dram.rearrange("(t p) d -> t p d", p=128)
        yav = ya_dram.rearrange("(kc p) s -> p kc s", p=128)
        yrv = yr_dram.rearrange("(kc p) s -> p kc s", p=128)
        def merge_load_chunk(c):
            cb = c % 2
            P.dma("sp", lambda e: e.dma_start(out=yaT[:, cb], in_=yav[:, :, c * 512:(c + 1) * 512]), writes=[yab[cb]])
            P.dma("sp", lambda e: e.dma_start(out=yrT[:, cb], in_=yrv[:, :, c * 512:(c + 1) * 512]), writes=[yrb[cb]])
            for j in range(4):
                t = c * 4 + j
                P.dma("sp", lambda e, t=t, j=j: e.dma_start(out=xbuf[:, cb, j, :], in_=xview[t]), writes=[xb[cb][j]])
                nm.run(xbuf[:, cb, j, :], xb[cb][j], xnT[:, cb], xnb[cb], j * 128)

        merge_load_chunk(0)
        for c in range(S // 512):
            cb = c % 2
            if c + 1 < S // 512:
                merge_load_chunk(c + 1)
            for m in range(8):
                def mmg(e, m=m, cb=cb):
                    ins = None
                    for g in range(2):
                        for kc in range(8):
                            ins = e.matmul(pga[:, g, :], lhsT=wg[:, kc, g * D + m * 128:g * D + (m + 1) * 128],
                                           rhs=xnT[:, cb, kc, :], start=(kc == 0), stop=(kc == 7))
                    return ins

                P.op("pe", mmg, reads=[wg_b, xnb[cb]], writes=[pgb])

                def mmb(e, m=m, cb=cb):
                    ins = None
                    for kc in range(4):
                        ins = e.matmul(pbr[:, 0, :], lhsT=wba[:, kc, m * 128:(m + 1) * 128], rhs=yaT[:, cb, kc, :],
                                       start=(kc == 0), stop=(kc == 3))
                    for kc in range(4):
                        ins = e.matmul(pbr[:, 1, :], lhsT=wbr[:, kc, m * 128:(m + 1) * 128], rhs=yrT[:, cb, kc, :],
                                       start=(kc == 0), stop=(kc == 3))
                    return ins

                P.op("pe", mmb, reads=[wba_b, wbr_b, yab[cb], yrb[cb]], writes=[pbb])
                si, sbuf_ = sgr.next()
                P.op("act", lambda e, si=si: e.activation(out=sg[:, si], in_=pga[:, :, :], func=AF.Sigmoid),
                     reads=[pgb], writes=[sbuf_])
                ti, tbuf = ttr.next()
                P.op("dve", lambda e, si=si, ti=ti: e.tensor_tensor(out=tt[:, ti], in0=pbr[:, :, :], in1=sg[:, si],
                                                                    op=ALU.mult),
                     reads=[pbb, sbuf_], writes=[tbuf])
                P.op("pool", lambda e, ti=ti, m=m: e.tensor_tensor(out=mT[:, m, :], in0=tt[:, ti, 0, :],
                                                                   in1=tt[:, ti, 1, :], op=ALU.add),
                     reads=[tbuf], writes=[mb[m]])
            for j in range(4):
                t = c * 4 + j
                hi, hbuf_ = hstr.next()
                for nh in range(2):
                    pi, pbuf = por.next()

                    def mmo(e, j=j, nh=nh, pi=pi):
                        ins = None
                        for m in range(8):
                            ins = e.matmul(po[:, pi, :], lhsT=mT[:, m, j * 128:(j + 1) * 128],
                                           rhs=wo[:, m, nh * 512:(nh + 1) * 512], start=(m == 0), stop=(m == 7))
                        return ins

                    P.op("pe", mmo, reads=mb + [wo_b], writes=[pbuf])
                    P.op("dve", lambda e, j=j, nh=nh, pi=pi, hi=hi, cb=cb: e.tensor_tensor(
                        out=hst[:, hi, nh * 512:(nh + 1) * 512], in0=po[:, pi, :],
                        in1=xbuf[:, cb, j, nh * 512:(nh + 1) * 512], op=ALU.add),
                        reads=[pbuf, xb[cb][j]], writes=[hbuf_])
                P.dma("sp", lambda e, t=t, hi=hi: e.dma_start(out=hview[t], in_=hst[:, hi, :]), reads=[hbuf_])
    P.barrier()


TL = 128
C0 = math.exp(-0.5)


def col8(P, nc, es, name, v_ap):
    t = es.enter_context(nc.sbuf_tensor(name, [64, 8], F32))
    b = Buf(name)
    P.dma("sp", lambda e: e.dma_start(out=t[:, :], in_=v_ap.rearrange("o (h k) -> k (o h)", k=64),
                                      allow_slow_non_contiguous=True), writes=[b])
    return t, b


def phase_rwkv(P, nc, C, A, yr_dram):
    x_dram = A["x"]
    with ExitStack() as es:
        sb = lambda name, shape, dt=F32: es.enter_context(nc.sbuf_tensor(name, shape, dt))
        wr = sb("wr", [128, 8, RWKV_IN], BF16)
        wmu = sb("wmu", [128, 8, RWKV_IN], BF16)
        wr_b, wmu_b, mub_b = Buf(), Buf(), Buf()
        load_weight_bf16(P, nc, wr, wr_b, A["w_in"], ATTN_IN, ATTN_IN + RWKV_IN, 8)
        with nc.sbuf_tensor("mub", [128, RWKV_IN], F32) as mub:
            P.dma("sp", lambda e: e.dma_start(out=mub[:], in_=A["rwkv_mu"].partition_broadcast(128)), writes=[mub_b])
            for kc in range(8):
                P.op("pool", lambda e, kc=kc: e.tensor_tensor(out=wmu[:, kc, :], in0=wr[:, kc, :], in1=mub[:],
                                                              op=ALU.mult), reads=[wr_b, mub_b], writes=[wmu_b])
        P.barrier()
        w2 = sb("w2", [64, 512], BF16)
        a2 = sb("a2", [64, 512], BF16)
        g2 = sb("g2", [64, 3, 512], BF16)
        lw_b = Buf()
        P.dma("pool", lambda e: e.dma_start(out=w2[:], in_=A["rwkv_w2"]), writes=[lw_b])
        P.dma("pool", lambda e: e.dma_start(out=a2[:], in_=A["rwkv_a2"]), writes=[lw_b])
        P.dma("pool", lambda e: e.dma_start(out=g2[:, 0, :], in_=A["rwkv_g2"][0:64, :]), writes=[lw_b])
        P.dma("pool", lambda e: e.dma_start(out=g2[:, 1, :], in_=A["rwkv_g2"][64:128, :]), writes=[lw_b])
        P.dma("pool", lambda e: e.dma_start(out=g2[0:32, 2, :], in_=A["rwkv_g2"][128:160, :]), writes=[lw_b])
        cw0, b_w0 = col8(P, nc, es, "cw0", A["rwkv_w0"])
        ca0, b_a0 = col8(P, nc, es, "ca0", A["rwkv_a0"])
        ckk, b_kk = col8(P, nc, es, "ckk", A["rwkv_k_k"])
        cka, b_ka = col8(P, nc, es, "cka", A["rwkv_k_a"])
        crk, b_rk = col8(P, nc, es, "crk", A["rwkv_r_k"])
        clw, b_lw = col8(P, nc, es, "clw", A["rwkv_ln_w"])
        clb, b_lb = col8(P, nc, es, "clb", A["rwkv_ln_b"])
        msk = sb("rmsk", [64, 3, 64], F32)
        ones_bf = sb("ones_bf", [64, 64], BF16)
        rmask = sb("rmask", [64, 8, TL // 64, 64], F32)
        mk_b = Buf()

        def mkmasks(e):
            e.memset(msk[:], 1.0)
            e.memset(ones_bf[:], 1.0)
            e.memset(rmask[:], 1.0)
            e.memset(rmask[:, :, :, 0:1], 0.0)
            e.affine_select(out=msk[:, 0, :], in_=msk[:, 0, :], pattern=[[1, 64]], compare_op=ALU.is_ge,
                            fill=0.0, base=-1, channel_multiplier=-1)
            e.affine_select(out=msk[:, 1, :], in_=msk[:, 1, :], pattern=[[1, 64]], compare_op=ALU.is_ge,
                            fill=0.0, base=0, channel_multiplier=-1)
            return e.affine_select(out=msk[:, 2, :], in_=msk[:, 2, :], pattern=[[-1, 64]], compare_op=ALU.is_ge,
                                   fill=0.0, base=-1, channel_multiplier=1)

        P.op("pool", mkmasks, writes=[mk_b])
        mb3 = lambda i: msk[:, i, :].unsqueeze(1).broadcast_to([64, 8, 64])
        idb = C.ident_bf[0:64, 0:64]
        idf3 = C.ident_f[0:64, 0:64].unsqueeze(1).broadcast_to([64, 8, 64])
        bc = lambda col: col[:, :].unsqueeze(2).broadcast_to([64, 8, TL])

        nm = Normer(P, nc, es, C, A["mix_norm"], "rn", nslots=1)
        xbuf = sb("rxbuf", [128, 1, D], F32)
        xr = Ring([(i, Buf()) for i in range(1)])
        xnx = sb("xnx", [128, 2, 8, TL + 1], BF16)
        xnb = [Buf(), Buf()]
        dxn = sb("dxn", [128, 8, TL], BF16)
        dxb = Buf()
        P.op("pool", lambda e: e.memset(xnx[:, 1, :, TL:TL + 1], 0.0), writes=[xnb[1]])
        F = {}
        FB = {}
        for nme, dt in [("r", F32), ("k", F32), ("sgd", F32), ("a", F32), ("kk", F32),
                        ("t1", F32), ("t2", F32), ("cum", F32), ("e1", F32), ("e2", F32), ("sqb", BF16)]:
            alias = {"e1": "t1", "e2": "a"}
            if nme in alias:
                F[nme] = F[alias[nme]]
                FB[nme] = FB[alias[nme]]
                continue
            F[nme] = sb("f_" + nme, [64, 8, TL], dt)
            FB[nme] = Buf(nme)
        X2 = {}
        X2B = {}
        for nme in ("rT", "aT", "bT", "kT", "bH", "kH", "vb", "g", "bon"):
            X2[nme] = sb("x_" + nme, [64, 2, 8, TL], BF16)
            X2B[nme] = [Buf(nme + "0"), Buf(nme + "1")]
        lora = sb("lora", [64, 5, TL], BF16)
        ztok = sb("ztok", [128, 2, 512], F32)
        ztr = Ring([(i, Buf()) for i in range(2)])
        lora_b = Buf()
        gC = sb("gC", [64, 2, 8, TL // 64], F32)
        gC_bs = [Buf(), Buf()]
        Sf = sb("Sf", [64, 8, 64], F32)
        Sb = sb("Sb", [64, 8, 64], BF16)
        St = sb("St", [64, 8, 64], F32)
        S_b, St_b = Buf(), Buf()
        P.op("dve", lambda e: e.memset(Sf[:], 0.0), writes=[S_b])
        P.op("dve", lambda e: e.memset(Sb[:], 0.0), reads=[S_b], writes=[S_b])
        ost = sb("rost", [64, 8, TL], BF16)
        ost_b = Buf()
        def pair(name, dt=BF16, n=2):
            t = sb(name, [64, n, 8, 64], dt)
            return t, [Buf() for _ in range(n)]
        Atok, Atok_b = pair("Atok")
        BHtok, BHtok_b = pair("BHtok")
        KHtok, KHtok_b = pair("KHtok")
        Vtok, Vtok_b = pair("Vtok")
        Mrb, Mrb_b = pair("Mrb")
        Mrk, Mrk_b = pair("Mrk")
        Lak, Lak_b = pair("Lak")
        Nn, Nn_b = pair("Nn", BF16, 4)
        Mm, Mm_b = pair("Mm", BF16, 4)
        Qq, Qq_b = pair("Qq", BF16, 4)
        WT, WT_b = pair("WT")
        Xx, Xx_b = pair("Xx")
        Uu, Uu_b = pair("Uu")
        Ys, Ys_b = pair("Ys", F32, 1)
        Yn, Yn_b = pair("Yn", BF16, 2)
        gst = sb("gst", [64, 2, 8, 6], F32)
        gst_b = [Buf(), Buf()]
        pb = es.enter_context(nc.psum_tensor("rpb", [128, 7, 512], F32))
        pr = Ring([(i, Buf()) for i in range(7)])
        pv = lambda i: pb[0:64, i, :].rearrange("p (h t) -> p h t", h=8)
        pvb = lambda i: pb[0:64, i, :].bitcast(BF16)[:, 0:512].rearrange("p (h t) -> p h t", h=8)

        xview = x_dram.rearrange("(t p) d -> t p d", p=128)

        def mm8(bank, parts, rd):
            i, bbuf = bank
            ops = [[(lf(h), rf(h)) for (lf, rf) in parts] for h in range(8)]

            def th(e):
                ins = None
                for h in range(8):
                    for pi, (l, r) in enumerate(ops[h]):
                        ins = e.matmul(pb[0:64, i, h * 64:(h + 1) * 64], lhsT=l, rhs=r,
                                       start=(pi == 0), stop=(pi == len(ops[h]) - 1))
                return ins

            P.op("pe", th, reads=rd, writes=[bbuf])

        def tr8(bank, src_fn, rd):
            i, bbuf = bank
            srcs = [src_fn(h) for h in range(8)]

            def th(e):
                ins = None
                v = pvb(i)
                for h in range(8):
                    ins = e.transpose(out=v[:, h, :], in_=srcs[h], identity=idb)
                return ins

            P.op("pe", th, reads=rd + [C.b], writes=[bbuf])

        NB = S // TL

        def prep(n):
            par = n % 2
            cb = n % 2
            pc = 1 - cb
            X = {k: X2[k][:, par] for k in X2}
            XB = {k: X2B[k][par] for k in X2}
            P.op("pool", lambda e: e.tensor_copy(out=xnx[:, cb, :, 0:1], in_=xnx[:, pc, :, TL:TL + 1]),
                 reads=[xnb[pc]], writes=[xnb[cb]])
            for j in range(TL // 128):
                xi, xb_ = xr.next()
                P.dma("sp", lambda e, t=n * (TL // 128) + j, xi=xi: e.dma_start(out=xbuf[:, xi, :], in_=xview[t]), writes=[xb_])
                nm.run(xbuf[:, xi, :], xb_, xnx[:, cb], xnb[cb], 1 + j * 128)
            P.op("pool", lambda e: e.tensor_tensor(out=dxn[:], in0=xnx[:, cb, :, 0:TL], in1=xnx[:, cb, :, 1:TL + 1],
                                                   op=ALU.subtract), reads=[xnb[cb]], writes=[dxb])
            yield

            def projtok(c0, ncol):
                bank = pr.next()
                i = bank[0]

                def th(e):
                    ins = None
                    for kc in range(8):
                        e.matmul(pb[:, i, 0:ncol], lhsT=xnx[:, cb, kc, 1:TL + 1], rhs=wr[:, kc, c0:c0 + ncol],
                                 start=(kc == 0), stop=False)
                    for kc in range(8):
                        ins = e.matmul(pb[:, i, 0:ncol], lhsT=dxn[:, kc, :], rhs=wmu[:, kc, c0:c0 + ncol],
                                       start=False, stop=(kc == 7))
                    return ins

                P.op("pe", th, reads=[wr_b, wmu_b, xnb[cb], dxb], writes=[bank[1]])
                zi, zb = ztr.next()
                P.op("act", lambda e: e.activation(out=ztok[:, zi, 0:ncol], in_=pb[:, i, 0:ncol], func=AF.Copy),
                     reads=[bank[1]], writes=[zb])
                return zi, zb

            def trz(zi, zb, cols, m):
                bank = pr.next()
                i = bank[0]

                def th(e):
                    ins = None
                    for q, c in enumerate(cols):
                        ins = e.transpose(out=pb[0:m, i, q * TL:(q + 1) * TL], in_=ztok[:, zi, c:c + m], identity=C.ident_f[:])
                    return ins

                P.op("pe", th, reads=[zb, C.b], writes=[bank[1]])
                return bank

            for qi, qn in enumerate(("r", "k", "v")):
                zi, zb = projtok(qi * 512, 512)
                yield
                for h0 in (0, 4):
                    bank = trz(zi, zb, [(h0 + q) * 64 for q in range(4)], 64)
                    dst, dstb = (X["vb"], XB["vb"]) if qn == "v" else (F[qn], FB[qn])
                    P.op("act", lambda e, dst=dst, h0=h0, i=bank[0]: e.activation(
                        out=dst[:, h0:h0 + 4, :], in_=pb[0:64, i, 0:4 * TL].rearrange("p (q t) -> p q t", q=4), func=AF.Copy),
                        reads=[bank[1]], writes=[dstb])
                    yield
            zi, zb = projtok(1536, 288)
            yield
            for li, (c0, m, fn) in enumerate([(0, 64, AF.Tanh), (64, 64, AF.Copy), (128, 64, AF.Sigmoid),
                                              (192, 64, AF.Sigmoid), (256, 32, AF.Sigmoid)]):
                bank = trz(zi, zb, [c0], m)
                P.op("act", lambda e, li=li, m=m, fn=fn, i=bank[0]: e.activation(out=lora[0:m, li, :], in_=pb[0:m, i, 0:TL],
                                                                                 func=fn),
                     reads=[bank[1]], writes=[lora_b])
                yield
            for h in range(8):
                bank = pr.next()
                P.op("pe", lambda e, h=h, i=bank[0]: e.matmul(pb[0:64, i, 0:TL], lhsT=w2[:, h * 64:(h + 1) * 64],
                                                              rhs=lora[:, 0, :], start=True, stop=True),
                     reads=[lw_b, lora_b], writes=[bank[1]])
                P.op("act", lambda e, h=h, i=bank[0]: e.activation(out=F["sgd"][:, h, :], in_=pb[0:64, i, 0:TL],
                                                                   func=AF.Sigmoid, bias=cw0[:, h:h + 1]),
                     reads=[bank[1], b_w0], writes=[FB["sgd"]])
                bank = pr.next()
                P.op("pe", lambda e, h=h, i=bank[0]: e.matmul(pb[0:64, i, 0:TL], lhsT=a2[:, h * 64:(h + 1) * 64],
                                                              rhs=lora[:, 1, :], start=True, stop=True),
                     reads=[lw_b, lora_b], writes=[bank[1]])
                P.op("act", lambda e, h=h, i=bank[0]: e.activation(out=F["a"][:, h, :], in_=pb[0:64, i, 0:TL],
                                                                   func=AF.Sigmoid, bias=ca0[:, h:h + 1]),
                     reads=[bank[1], b_a0], writes=[FB["a"]])
                bank = pr.next()

                def gmm(e, h=h, i=bank[0]):
                    e.matmul(pb[0:64, i, 0:TL], lhsT=g2[:, 0, h * 64:(h + 1) * 64], rhs=lora[:, 2, :], start=True, stop=False)
                    e.matmul(pb[0:64, i, 0:TL], lhsT=g2[:, 1, h * 64:(h + 1) * 64], rhs=lora[:, 3, :], start=False, stop=False)
                    return e.matmul(pb[0:64, i, 0:TL], lhsT=g2[0:32, 2, h * 64:(h + 1) * 64], rhs=lora[0:32, 4, :],
                                    start=False, stop=True)

                P.op("pe", gmm, reads=[lw_b, lora_b], writes=[bank[1]])
                P.op("act", lambda e, h=h, i=bank[0]: e.activation(out=X["g"][:, h, :], in_=pb[0:64, i, 0:TL], func=AF.Copy),
                     reads=[bank[1]], writes=[XB["g"]])
                yield
            P.op("dve", lambda e: e.tensor_tensor(out=F["kk"][:], in0=F["k"][:], in1=bc(ckk), op=ALU.mult),
                 reads=[FB["k"], b_kk], writes=[FB["kk"]])
            P.op("pool", lambda e: e.tensor_tensor(out=F["sqb"][:], in0=F["kk"][:], in1=F["kk"][:], op=ALU.mult),
                 reads=[FB["kk"]], writes=[FB["sqb"]])
            yield
            for h in range(8):
                bank = pr.next()
                P.op("pe", lambda e, h=h, i=bank[0]: e.matmul(pb[0:64, i, 0:TL], lhsT=ones_bf[:], rhs=F["sqb"][:, h, :],
                                                              start=True, stop=True),
                     reads=[mk_b, FB["sqb"]], writes=[bank[1]])
                P.op("act", lambda e, h=h, i=bank[0]: e.activation(out=F["t1"][:, h, :], in_=pb[0:64, i, 0:TL], func=AF.Sqrt),
                     reads=[bank[1]], writes=[FB["t1"]])
                if h % 2 == 1:
                    yield
            P.op("dve", lambda e: e.tensor_scalar(out=F["t1"][:], in0=F["t1"][:], scalar1=1e-12, scalar2=None,
                                                  op0=ALU.max), reads=[FB["t1"]], writes=[FB["t1"]])
            P.op("dve", lambda e: e.reciprocal(out=F["t1"][:], in_=F["t1"][:]), reads=[FB["t1"]], writes=[FB["t1"]])
            yield
            P.op("dve", lambda e: e.tensor_tensor(out=F["kk"][:], in0=F["kk"][:], in1=F["t1"][:], op=ALU.mult),
                 reads=[FB["kk"], FB["t1"]], writes=[FB["kk"]])
            P.op("dve", lambda e: e.scalar_tensor_tensor(out=F["t2"][:], in0=F["a"][:], scalar=-1.0, in1=bc(cka),
                                                         op0=ALU.add, op1=ALU.mult),
                 reads=[FB["a"], b_ka], writes=[FB["t2"]])
            yield
            P.op("dve", lambda e: e.scalar_tensor_tensor(out=F["k"][:], in0=F["t2"][:], scalar=1.0, in1=F["k"][:],
                                                         op0=ALU.add, op1=ALU.mult),
                 reads=[FB["t2"], FB["k"]], writes=[FB["k"]])
            P.op("pool", lambda e: e.tensor_tensor(out=F["t2"][:], in0=F["kk"][:], in1=F["a"][:], op=ALU.mult),
                 reads=[FB["kk"], FB["a"]], writes=[FB["t2"]])
            yield
            P.op("dve", lambda e: e.tensor_tensor_scan(out=F["cum"][:].rearrange("p h t -> p (h t)"),
                                                       data0=rmask[:].rearrange("p h c t -> p (h c t)"),
                                                       data1=F["sgd"][:].rearrange("p h t -> p (h t)"),
                                                       initial=0.0, op0=ALU.mult, op1=ALU.add),
                 reads=[FB["sgd"], mk_b], writes=[FB["cum"]])
            cum4 = F["cum"][:].rearrange("p h (c t) -> p h c t", t=64)
            yield
            P.op("act", lambda e: e.activation(out=F["e1"][:], in_=F["cum"][:], func=AF.Exp, scale=-C0),
                 reads=[FB["cum"]], writes=[FB["e1"]])
            P.op("dve", lambda e: e.tensor_tensor(out=X["rT"][:], in0=F["r"][:], in1=F["e1"][:], op=ALU.mult),
                 reads=[FB["r"], FB["e1"]], writes=[XB["rT"]])
            P.op("act", lambda e: e.activation(out=gC[:, par], in_=cum4[:, :, :, 63], func=AF.Exp, scale=-C0),
                 reads=[FB["cum"]], writes=[gC_bs[par]])
            yield
            P.op("pool", lambda e: e.tensor_tensor(out=F["e2"][:], in0=F["cum"][:], in1=F["sgd"][:], op=ALU.subtract),
                 reads=[FB["cum"], FB["sgd"]], writes=[FB["e2"]])
            P.op("act", lambda e: e.activation(out=F["e2"][:], in_=F["e2"][:], func=AF.Exp, scale=-C0),
                 reads=[FB["e2"]], writes=[FB["e2"]])
            P.op("dve", lambda e: e.scalar_tensor_tensor(out=X["aT"][:], in0=F["kk"][:], scalar=-1.0, in1=F["e2"][:],
                                                         op0=ALU.mult, op1=ALU.mult),
                 reads=[FB["kk"], FB["e2"]], writes=[XB["aT"]])
            yield
            P.op("act", lambda e: e.activation(out=F["e1"][:], in_=F["cum"][:], func=AF.Exp, scale=C0),
                 reads=[FB["cum"]], writes=[FB["e1"]])
            P.op("dve", lambda e: e.tensor_tensor(out=X["bT"][:], in0=F["t2"][:], in1=F["e1"][:], op=ALU.mult),
                 reads=[FB["t2"], FB["e1"]], writes=[XB["bT"]])
            P.op("pool", lambda e: e.tensor_tensor(out=X["kT"][:], in0=F["k"][:], in1=F["e1"][:], op=ALU.mult),
                 reads=[FB["k"], FB["e1"]], writes=[XB["kT"]])
            yield
            P.op("dve", lambda e: e.tensor_tensor(out=F["e2"][:].rearrange("p h (c t) -> p h c t", t=64),
                                                  in0=cum4[:, :, :, 63:64].broadcast_to([64, 8, TL // 64, 64]), in1=cum4,
                                                  op=ALU.subtract),
                 reads=[FB["cum"]], writes=[FB["e2"]])
            P.op("act", lambda e: e.activation(out=F["e2"][:], in_=F["e2"][:], func=AF.Exp, scale=-C0),
                 reads=[FB["e2"]], writes=[FB["e2"]])
            yield
            P.op("dve", lambda e: e.tensor_tensor(out=X["bH"][:], in0=F["t2"][:], in1=F["e2"][:], op=ALU.mult),
                 reads=[FB["t2"], FB["e2"]], writes=[XB["bH"]])
            P.op("pool", lambda e: e.tensor_tensor(out=X["kH"][:], in0=F["k"][:], in1=F["e2"][:], op=ALU.mult),
                 reads=[FB["k"], FB["e2"]], writes=[XB["kH"]])
            yield
            P.op("dve", lambda e: e.tensor_tensor(out=F["t1"][:], in0=F["r"][:], in1=F["k"][:], op=ALU.mult),
                 reads=[FB["r"], FB["k"]], writes=[FB["t1"]])
            P.op("pool", lambda e: e.tensor_tensor(out=F["sqb"][:], in0=F["t1"][:], in1=bc(crk), op=ALU.mult),
                 reads=[FB["t1"], b_rk], writes=[FB["sqb"]])
            yield
            for h in range(8):
                bank = pr.next()
                P.op("pe", lambda e, h=h, i=bank[0]: e.matmul(pb[0:64, i, 0:TL], lhsT=ones_bf[:], rhs=F["sqb"][:, h, :],
                                                              start=True, stop=True),
                     reads=[mk_b, FB["sqb"]], writes=[bank[1]])
                P.op("dve", lambda e, h=h, i=bank[0]: e.tensor_tensor(out=X["bon"][:, h, :], in0=pb[0:64, i, 0:TL],
                                                                      in1=X["vb"][:, h, :], op=ALU.mult),
                     reads=[bank[1], XB["vb"]], writes=[XB["bon"]])
                if h % 2 == 1:
                    yield

        def chunk_pre(n, c, out):
            par = n % 2
            X = {k: X2[k][:, par] for k in X2}
            XB = {k: X2B[k][par] for k in X2}
            cs = slice(c * 64, (c + 1) * 64)
            for (dst, dbs, src) in ((Atok, Atok_b, "aT"), (BHtok, BHtok_b, "bH"), (KHtok, KHtok_b, "kH"), (Vtok, Vtok_b, "vb")):
                bank = pr.next()
                tr8(bank, lambda h, src=src: X[src][:, h, cs], [XB[src]])
                P.op("act", lambda e, dst=dst, i=bank[0]: e.activation(out=dst[:, c], in_=pvb(i), func=AF.Copy),
                     reads=[bank[1]], writes=[dbs[c]])
                yield

            def gmat(lname, rname, mi, dst, dbs, slot):
                bank = pr.next()
                mm8(bank, [(lambda h: X[lname][:, h, cs], lambda h: X[rname][:, h, cs])], [XB[lname], XB[rname]])
                P.op("dve", lambda e, i=bank[0]: e.tensor_tensor(out=dst[:, slot], in0=pv(i), in1=mb3(mi), op=ALU.mult),
                     reads=[bank[1], mk_b], writes=[dbs[slot]])

            base = 2 * c
            gmat("bT", "aT", 0, Mm, Mm_b, base)
            yield
            gmat("bT", "rT", 1, Mrb, Mrb_b, c)
            yield
            gmat("kT", "aT", 0, Lak, Lak_b, c)
            yield
            gmat("kT", "rT", 1, Mrk, Mrk_b, c)
            yield
            gmat("aT", "bT", 2, Nn, Nn_b, base)
            yield
            P.op("pool", lambda e: e.tensor_tensor(out=Qq[:, base], in0=Mm[:, base], in1=idf3, op=ALU.add),
                 reads=[Mm_b[base], C.b], writes=[Qq_b[base]])
            ni = mi_ = qi_ = base
            for lvl in range(1, 6):
                nn_ = base + (1 - (ni - base))
                nm_ = base + (1 - (mi_ - base))
                nq_ = base + (1 - (qi_ - base))
                bank = pr.next()
                mm8(bank, [(lambda h: Mm[:, mi_, h, :], lambda h: Nn[:, ni, h, :])], [Mm_b[mi_], Nn_b[ni]])
                if lvl < 5:
                    bank2 = pr.next()
                    mm8(bank2, [(lambda h: Nn[:, ni, h, :], lambda h: Mm[:, mi_, h, :])], [Mm_b[mi_], Nn_b[ni]])
                P.op("act", lambda e, nn_=nn_, i=bank[0]: e.activation(out=Nn[:, nn_], in_=pv(i), func=AF.Copy),
                     reads=[bank[1]], writes=[Nn_b[nn_]])
                if lvl < 5:
                    P.op("dve", lambda e, nm_=nm_, i=bank2[0]: e.tensor_copy(out=Mm[:, nm_], in_=pv(i)),
                         reads=[bank2[1]], writes=[Mm_b[nm_]])
                    mi_ = nm_
                ni = nn_
                yield
                bank3 = pr.next()
                mm8(bank3, [(lambda h: Nn[:, ni, h, :], lambda h: Qq[:, qi_, h, :])], [Qq_b[qi_], Nn_b[ni]])
                P.op("dve", lambda e, nq_=nq_, qo=qi_, i=bank3[0]: e.tensor_tensor(out=Qq[:, nq_], in0=pv(i), in1=Qq[:, qo],
                                                                                   op=ALU.add),
                     reads=[bank3[1], Qq_b[qi_]], writes=[Qq_b[nq_]])
                qi_ = nq_
                yield
            bank = pr.next()
            mm8(bank, [(lambda h: Atok[:, c, h, :], lambda h: Qq[:, qi_, h, :])], [Atok_b[c], Qq_b[qi_]])
            P.op("act", lambda e, i=bank[0]: e.activation(out=WT[:, c], in_=pv(i), func=AF.Copy),
                 reads=[bank[1]], writes=[WT_b[c]])
            bank = pr.next()
            mm8(bank, [(lambda h: Lak[:, c, h, :], lambda h: Vtok[:, c, h, :])], [Lak_b[c], Vtok_b[c]])
            P.op("dve", lambda e, i=bank[0]: e.tensor_copy(out=Xx[:, c], in_=pv(i)), reads=[bank[1]], writes=[Xx_b[c]])
            out["q"] = qi_
            yield

        def chain(n, c, qi_):
            par = n % 2
            X = {k: X2[k][:, par] for k in X2}
            XB = {k: X2B[k][par] for k in X2}
            cs = slice(c * 64, (c + 1) * 64)
            bank = pr.next()
            mm8(bank, [(lambda h: WT[:, c, h, :], lambda h: Sb[:, h, :]),
                       (lambda h: Qq[:, qi_, h, :], lambda h: Xx[:, c, h, :])], [WT_b[c], S_b, Qq_b[qi_], Xx_b[c]])
            P.op("act", lambda e, i=bank[0]: e.activation(out=Uu[:, c], in_=pv(i), func=AF.Copy),
                 reads=[bank[1]], writes=[Uu_b[c]])
            banky = pr.next()
            mm8(banky, [(lambda h: X["rT"][:, h, cs], lambda h: Sb[:, h, :]),
                        (lambda h: Mrb[:, c, h, :], lambda h: Uu[:, c, h, :]),
                        (lambda h: Mrk[:, c, h, :], lambda h: Vtok[:, c, h, :])],
                [XB["rT"], S_b, Mrb_b[c], Uu_b[c], Mrk_b[c], Vtok_b[c]])
            banks = pr.next()
            mm8(banks, [(lambda h: BHtok[:, c, h, :], lambda h: Uu[:, c, h, :]),
                        (lambda h: KHtok[:, c, h, :], lambda h: Vtok[:, c, h, :])], [BHtok_b[c], Uu_b[c], KHtok_b[c], Vtok_b[c]])
            P.op("dve", lambda e: e.tensor_tensor(out=St[:], in0=Sf[:],
                                                  in1=gC[:, par, :, c:c + 1].broadcast_to([64, 8, 64]), op=ALU.mult),
                 reads=[S_b, gC_bs[par]], writes=[St_b])
            P.op("dve", lambda e, i=banks[0]: e.tensor_tensor(out=Sf[:], in0=pv(i), in1=St[:], op=ALU.add),
                 reads=[banks[1], St_b], writes=[S_b])
            P.op("act", lambda e: e.activation(out=Sb[:], in_=Sf[:], func=AF.Copy), reads=[S_b], writes=[S_b])
            yield
            y_b, yq_b, g_b = Ys_b[0], St_b, gst_b[c]
            P.op("act", lambda e, i=banky[0]: e.activation(out=Ys[:, 0], in_=pv(i), func=AF.Copy),
                 reads=[banky[1]], writes=[y_b])
            P.op("pool", lambda e: e.tensor_tensor(out=St[:], in0=Ys[:, 0], in1=Ys[:, 0], op=ALU.mult),
                 reads=[y_b], writes=[yq_b])
            P.op("dve", lambda e: e.tensor_reduce(out=gst[:, c, :, 0], in_=Ys[:, 0], axis=AX.X, op=ALU.add),
                 reads=[y_b], writes=[g_b])
            P.op("dve", lambda e: e.tensor_reduce(out=gst[:, c, :, 1], in_=St[:], axis=AX.X, op=ALU.add),
                 reads=[yq_b, g_b], writes=[g_b])
            yield
            P.op("dve", lambda e: e.tensor_scalar(out=gst[:, c, :, 2], in0=gst[:, c, :, 0], scalar1=1.0 / 64,
                                                  scalar2=None, op0=ALU.mult), reads=[g_b], writes=[g_b])
            P.op("dve", lambda e: e.tensor_tensor(out=gst[:, c, :, 3], in0=gst[:, c, :, 2], in1=gst[:, c, :, 2],
                                                  op=ALU.mult), reads=[g_b], writes=[g_b])
            P.op("dve", lambda e: e.scalar_tensor_tensor(out=gst[:, c, :, 4], in0=gst[:, c, :, 1], scalar=1.0 / 64,
                                                         in1=gst[:, c, :, 3], op0=ALU.mult, op1=ALU.subtract),
                 reads=[g_b], writes=[g_b])
            yield
            P.op("act", lambda e: e.activation(out=gst[:, c, :, 5], in_=gst[:, c, :, 4], func=AF.Sqrt,
                                               bias=C.eps[0:64, 1:2]), reads=[g_b, C.b], writes=[g_b])
            P.op("dve", lambda e: e.reciprocal(out=gst[:, c, :, 5], in_=gst[:, c, :, 5]), reads=[g_b], writes=[g_b])
            P.op("dve", lambda e: e.tensor_tensor(out=Ys[:, 0], in0=Ys[:, 0],
                                                  in1=gst[:, c, :, 2:3].broadcast_to([64, 8, 64]),
                                                  op=ALU.subtract), reads=[y_b, g_b], writes=[y_b])
            yield
            P.op("dve", lambda e: e.tensor_tensor(out=Yn[:, c], in0=Ys[:, 0],
                                                  in1=gst[:, c, :, 5:6].broadcast_to([64, 8, 64]),
                                                  op=ALU.mult), reads=[y_b, g_b], writes=[Yn_b[c]])
            bank = pr.next()
            tr8(bank, lambda h: Yn[:, c, h, :], [Yn_b[c]])
            bc64 = lambda col: col[:, :].unsqueeze(2).broadcast_to([64, 8, 64])
            P.op("dve", lambda e, i=bank[0]: e.tensor_tensor(out=Ys[:, 0], in0=pvb(i), in1=bc64(clw), op=ALU.mult),
                 reads=[bank[1], b_lw, Yn_b[c]], writes=[y_b])
            P.op("pool", lambda e: e.tensor_tensor(out=Ys[:, 0], in0=Ys[:, 0], in1=bc64(clb), op=ALU.add),
                 reads=[y_b, b_lb], writes=[y_b])
            yield
            P.op("dve", lambda e: e.tensor_tensor(out=Ys[:, 0], in0=Ys[:, 0], in1=X["bon"][:, :, cs], op=ALU.add),
                 reads=[y_b, XB["bon"]], writes=[y_b])
            P.op("dve", lambda e: e.tensor_tensor(out=ost[:, :, cs], in0=Ys[:, 0], in1=X["g"][:, :, cs], op=ALU.mult),
                 reads=[y_b, XB["g"]], writes=[ost_b])
            yield

        def scan(n):
            par = n % 2
            X = {k: X2[k][:, par] for k in X2}
            XB = {k: X2B[k][par] for k in X2}
            outs = [{} for _ in range(TL // 64)]
            gens = [chunk_pre(n, c, outs[c]) for c in range(TL // 64)]
            alive = [True] * len(gens)
            while any(alive):
                for gi_, g_ in enumerate(gens):
                    if alive[gi_]:
                        try:
                            next(g_)
                        except StopIteration:
                            alive[gi_] = False
                yield
            for c in range(TL // 64):
                for _ in chain(n, c, outs[c]["q"]):
                    yield
            P.dma("sp", lambda e: e.dma_start(
                out=yr_dram.rearrange("(h v) s -> v h s", v=64)[:, :, n * TL:(n + 1) * TL], in_=ost[:]),
                reads=[ost_b])
            yield

        def run2(f, b):
            fa, ba = f is not None, b is not None
            while fa or ba:
                if fa:
                    try:
                        next(f)
                    except StopIteration:
                        fa = False
                if ba:
                    try:
                        next(b)
                    except StopIteration:
                        ba = False

        for n in range(NB + 1):
            run2(prep(n) if n < NB else None, scan(n - 1) if n >= 1 else None)
    P.barrier()


NITER = 16


def phase_attn(P, nc, C, A, ya_dram):
    x_dram = A["x"]
    with ExitStack() as es:
        sb = lambda name, shape, dt=F32: es.enter_context(nc.sbuf_tensor(name, shape, dt))
        wa = sb("wa", [128, 8, ATTN_IN + 64], BF16)
        wa_b = Buf()
        load_weight_bf16(P, nc, wa, wa_b, A["w_in"], 0, ATTN_IN, 8)
        wv_ = A["w_in"].rearrange("(kc p) n -> p kc n", p=128)
        for kc in range(8):
            P.dma("pool", lambda e, kc=kc: e.dma_start(out=wa[:, kc, ATTN_IN:ATTN_IN + 64], in_=wv_[:, kc, 2048:2112]),
                  writes=[wa_b])
        kT = sb("kT", [128, 4, S], BF16)
        kiT = sb("kiT", [128, S], BF16)
        vaug = sb("vaug", [128, NT, 8, 65], BF16)
        kT_bs = [Buf() for _ in range(NT)]
        kiT_bs = [Buf() for _ in range(NT)]
        va_bs = [Buf() for _ in range(NT)]
        P.op("pool", lambda e: e.memset(vaug[:, :, :, 64:65], 1.0), writes=va_bs)
        gqk = sb("gqk", [128, 2], F32)
        gqk_b = Buf()
        for half in range(2):
            P.dma("sp", lambda e, half=half: e.dma_start(out=gqk[half * 64:(half + 1) * 64, 0:1],
                                                         in_=A["attn_q_norm"].rearrange("o d -> d o"),
                                                         allow_slow_non_contiguous=True), writes=[gqk_b])
            P.dma("sp", lambda e, half=half: e.dma_start(out=gqk[half * 64:(half + 1) * 64, 1:2],
                                                         in_=A["attn_k_norm"].rearrange("o d -> d o"),
                                                         allow_slow_non_contiguous=True), writes=[gqk_b])
        P.op("dve", lambda e: e.tensor_scalar(out=gqk[:, 0:1], in0=gqk[:, 0:1], scalar1=0.125, scalar2=None, op0=ALU.mult),
             reads=[gqk_b], writes=[gqk_b])
        btf = sb("btf", [128, 8, 2, 128], F32)
        bt = sb("bt", [128, 8, 2, 128], BF16)
        b31 = sb("b31_sb", [128, 8], F32)
        bt_b = Buf()
        P.dma("sp", lambda e: e.dma_start(out=btf[:], in_=A["bias_tiles"].rearrange("h c s t -> s h c t")), writes=[bt_b])
        P.dma("sp", lambda e: e.dma_start(out=b31[:], in_=A["b31"].partition_broadcast(128)), writes=[bt_b])
        P.op("dve", lambda e: e.tensor_tensor(out=bt[:].rearrange("p h c t -> p h (c t)"),
                                              in0=btf[:].rearrange("p h c t -> p h (c t)"),
                                              in1=b31[:, :].unsqueeze(2).broadcast_to([128, 8, 256]), op=ALU.subtract),
             reads=[bt_b], writes=[bt_b])
        cmask = sb("cmask", [128, 128], F32)
        onesblk = sb("onesblk", [128, 128], BF16)
        cm_b = Buf()
        pw2 = sb("pw2", [128, 2 * NITER], F32)
        halfs = sb("halfs", [128, 2 * NITER], F32)
        hf_b = Buf()

        def mkc(e):
            e.memset(cmask[:], 0.0)
            e.affine_select(out=cmask[:], in_=cmask[:], pattern=[[-1, 128]], compare_op=ALU.is_ge, fill=NEG, base=0,
                            channel_multiplier=1)
            for j in range(NITER):
                e.memset(pw2[:, j:j + 1], 0.5 ** (j + 1))
                e.memset(pw2[:, NITER + j:NITER + j + 1], 0.5 ** (j + 2))
            e.memset(onesblk[:], 0.0)
            e.memset(onesblk[0:64, 0:64], 1.0)
            return e.memset(onesblk[64:128, 64:128], 1.0)

        P.op("pool", mkc, writes=[cm_b])
        nm = Normer(P, nc, es, C, A["mix_norm"], "an", nslots=1)
        xbuf = sb("axbuf", [128, 2, D], F32)
        xr = Ring([(i, Buf()) for i in range(2)])
        xn = sb("axn", [128, 8, 128], BF16)
        xn_b = Buf()
        q_i = sb("q_i", [128, 2, 4, 128], BF16)
        qi_i = sb("qi_i", [128, 4, 128], BF16)
        wi_i = sb("wi_i", [128, 8], F32)
        q_bs, qi_b, wi_b = [Buf(), Buf()], Buf(), Buf()
        sq = sb("asq", [128, 512], BF16)
        rn = sb("arn", [128, 512], F32)
        sq_b, rn_b = Buf(), Buf()
        sc = sb("sc", [128, S], F32)
        sc_b = Buf()
        rbuf = sb("rbuf", [128, 2, 512], F32)
        rr = Ring([(i, Buf()) for i in range(2)])
        junk = sb("ajunk", [128, S], BF16)
        junk_b = Buf()
        bs = sb("bs", [128, 8], F32)
        bs_b = Buf()
        maskT = sb("maskT", [128, 2, NT, 128], BF16)
        mT_bs = [Buf(), Buf()]
        Et = sb("Et", [128, 3, 4, 128], BF16)
        er = Ring([(i, Buf()) for i in range(3)])
        Pt = sb("Pt", [128, 4, 4, 128], BF16)
        ptr = Ring([(i, Buf()) for i in range(4)])
        rden = sb("rden", [128, 2], F32)
        rdr = Ring([(i, Buf()) for i in range(2)])
        ytile = sb("ytile", [128, 512], BF16)
        yt_b = Buf()
        yst = sb("yst", [128, 4, 128], BF16)
        ys_b = Buf()
        pg = es.enter_context(nc.psum_tensor("apg", [128, 6, 512], F32))
        gr = Ring([(i, Buf()) for i in range(2)])
        accr = Ring([(i, Buf()) for i in range(2, 4)])
        lgr = Ring([(i, Buf()) for i in range(4, 6)])
        po = es.enter_context(nc.psum_tensor("apo", [128, 1, 512], F32))
        orr = Ring([(i, Buf()) for i in range(1)])
        pgb = lambda i: pg[:, i, :].bitcast(BF16)
        if SBUF_DEBUG:
            print("attn sbuf remaining", nc.sbuf_bytes_remaining)
        xview = x_dram.rearrange("(t p) d -> t p d", p=128)
        yav = ya_dram.rearrange("(m p) s -> p m s", p=128)

        def front(i):
            ts_ = slice(i * 128, (i + 1) * 128)
            nkb = i + 1
            W = nkb * 128
            q_b = q_bs[i % 2]
            kT_b, kiT_b, va_b = kT_bs[i], kiT_bs[i], va_bs[i]
            xi, xb_ = xr.next()
            P.dma("sp", lambda e, i=i, xi=xi: e.dma_start(out=xbuf[:, xi, :], in_=xview[i]), writes=[xb_])
            nm.run(xbuf[:, xi, :], xb_, xn, xn_b, 0)
            yield

            def proj4(c0, bank):
                bi, bb = bank

                def th(e):
                    ins = None
                    for m in range(4):
                        for kc in range(8):
                            ins = e.matmul(pg[:, bi, m * 128:(m + 1) * 128], lhsT=wa[:, kc, c0 + m * 128:c0 + (m + 1) * 128],
                                           rhs=xn[:, kc, :], start=(kc == 0), stop=(kc == 7))
                    return ins

                P.op("pe", th, reads=[wa_b, xn_b], writes=[bb])

            for which, c0 in ((0, 0), (1, 512)):
                bank = gr.next()
                proj4(c0, bank)
                P.op("act", lambda e, bi=bank[0]: e.activation(out=sq[:], in_=pg[:, bi, :], func=AF.Square),
                     reads=[bank[1]], writes=[sq_b])
                bank2 = gr.next()
                P.op("pe", lambda e, bi=bank2[0]: e.matmul(pg[:, bi, :], lhsT=onesblk[:], rhs=sq[:], start=True, stop=True),
                     reads=[cm_b, sq_b], writes=[bank2[1]])
                P.op("act", lambda e, bi=bank2[0]: e.activation(out=rn[:], in_=pg[:, bi, :], func=AF.Ln, scale=1.0 / 64,
                                                                bias=C.eps[:, 0:1]), reads=[bank2[1], C.b], writes=[rn_b])
                P.op("act", lambda e: e.activation(out=rn[:], in_=rn[:], func=AF.Exp, scale=-0.5), reads=[rn_b], writes=[rn_b])
                if which == 0:
                    P.op("dve", lambda e, bi=bank[0]: e.scalar_tensor_tensor(
                        out=q_i[:, i % 2].rearrange("p m t -> p (m t)"), in0=pg[:, bi, :], scalar=gqk[:, 0:1], in1=rn[:],
                        op0=ALU.mult, op1=ALU.mult), reads=[bank[1], rn_b, gqk_b], writes=[q_b])
                else:
                    P.op("dve", lambda e, bi=bank[0], ts_=ts_: e.scalar_tensor_tensor(
                        out=kT[:, :, ts_], in0=pg[:, bi, :].rearrange("p (m t) -> p m t", m=4), scalar=gqk[:, 1:2],
                        in1=rn[:].rearrange("p (m t) -> p m t", m=4), op0=ALU.mult, op1=ALU.mult),
                        reads=[bank[1], rn_b, gqk_b], writes=[kT_b])
                yield
            bank = gr.next()
            proj4(1536, bank)
            P.op("act", lambda e, bi=bank[0]: e.activation(out=qi_i[:].rearrange("p m t -> p (m t)"), in_=pg[:, bi, :],
                                                           func=AF.Copy), reads=[bank[1]], writes=[qi_b])
            yield
            bank = gr.next()

            def kiw(e, bi=bank[0]):
                for kc in range(8):
                    e.matmul(pg[0:64, bi, 0:128], lhsT=wa[:, kc, 2048:2112], rhs=xn[:, kc, :], start=(kc == 0), stop=(kc == 7))
                for kc in range(8):
                    e.matmul(pg[64:128, bi, 0:128], lhsT=wa[:, kc, ATTN_IN:ATTN_IN + 64], rhs=xn[:, kc, :], start=(kc == 0),
                             stop=(kc == 7))
                ins = None
                for kc in range(8):
                    ins = e.matmul(pg[:, bi, 128:136], lhsT=xn[:, kc, :], rhs=wa[:, kc, 2112:2120], start=(kc == 0), stop=(kc == 7))
                return ins

            P.op("pe", kiw, reads=[wa_b, xn_b], writes=[bank[1]])
            P.op("act", lambda e, bi=bank[0], ts_=ts_: e.activation(out=kiT[:, ts_], in_=pg[:, bi, 0:128], func=AF.Copy),
                 reads=[bank[1]], writes=[kiT_b])
            P.op("dve", lambda e, bi=bank[0]: e.tensor_copy(out=wi_i[:], in_=pg[:, bi, 128:136]), reads=[bank[1]], writes=[wi_b])
            yield
            bank = gr.next()

            def vmm(e, bi=bank[0]):
                ins = None
                for kc in range(8):
                    ins = e.matmul(pg[:, bi, :], lhsT=xn[:, kc, :], rhs=wa[:, kc, 1024:1536], start=(kc == 0), stop=(kc == 7))
                return ins

            P.op("pe", vmm, reads=[wa_b, xn_b], writes=[bank[1]])
            P.op("act", lambda e, bi=bank[0], i=i: e.activation(out=vaug[:, i, :, 0:64],
                                                                in_=pg[:, bi, :].rearrange("p (h d) -> p h d", h=8), func=AF.Copy),
                 reads=[bank[1]], writes=[va_b])
            yield

            for gk in range((nkb + 3) // 4):
                w_ = min(512, W - gk * 512)
                abank = accr.next()
                for h in range(8):
                    hb = (h % 2) * 64
                    bank = gr.next()
                    P.op("pe", lambda e, bi=bank[0], h=h, hb=hb, gk=gk, w_=w_: e.matmul(
                        pg[:, bi, 0:w_], lhsT=qi_i[hb:hb + 64, h // 2, :], rhs=kiT[hb:hb + 64, gk * 512:gk * 512 + w_],
                        start=True, stop=True), reads=[qi_b] + kiT_bs[gk * 4:gk * 4 + (w_ // 128)], writes=[bank[1]])
                    ri, rb_ = rr.next()
                    P.op("act", lambda e, bi=bank[0], ri=ri, w_=w_: e.activation(out=rbuf[:, ri, 0:w_], in_=pg[:, bi, 0:w_],
                                                                                 func=AF.Relu), reads=[bank[1]], writes=[rb_])
                    if h == 0:
                        P.op("dve", lambda e, ri=ri, ai=abank[0], w_=w_: e.tensor_scalar(
                            out=pg[:, ai, 0:w_], in0=rbuf[:, ri, 0:w_], scalar1=wi_i[:, 0:1], scalar2=None,
                            op0=ALU.mult), reads=[rb_, wi_b], writes=[abank[1]])
                    elif h < 7:
                        P.op("dve", lambda e, ri=ri, ai=abank[0], w_=w_, h=h: e.scalar_tensor_tensor(
                            out=pg[:, ai, 0:w_], in0=rbuf[:, ri, 0:w_], scalar=wi_i[:, h:h + 1],
                            in1=pg[:, ai, 0:w_], op0=ALU.mult, op1=ALU.add), reads=[rb_, wi_b, abank[1]], writes=[abank[1]])
                    else:
                        P.op("dve", lambda e, ri=ri, ai=abank[0], gk=gk, w_=w_, h=h: e.scalar_tensor_tensor(
                            out=sc[:, gk * 512:gk * 512 + w_], in0=rbuf[:, ri, 0:w_], scalar=wi_i[:, h:h + 1],
                            in1=pg[:, ai, 0:w_], op0=ALU.mult, op1=ALU.add), reads=[rb_, wi_b, abank[1]], writes=[sc_b])
                    yield
            P.op("dve", lambda e, ts_=ts_: e.tensor_tensor(out=sc[:, ts_], in0=sc[:, ts_], in1=cmask[:], op=ALU.add),
                 reads=[sc_b, cm_b], writes=[sc_b])
            if i < 2:
                P.op("dve", lambda e: e.memset(bs[:, 0:1], -1.0e29), writes=[bs_b])
            else:
                P.op("dve", lambda e, i=i: e.tensor_reduce(out=bs[:, 0:1], in_=sc[:, 0:i * 128], axis=AX.X, op=ALU.min),
                     reads=[sc_b], writes=[bs_b])
                P.op("dve", lambda e, W=W: e.tensor_reduce(out=bs[:, 6:7], in_=sc[:, 0:W], axis=AX.X, op=ALU.max),
                     reads=[sc_b, bs_b], writes=[bs_b])
                P.op("dve", lambda e: e.tensor_tensor(out=bs[:, 1:2], in0=bs[:, 6:7], in1=bs[:, 0:1], op=ALU.subtract),
                     reads=[bs_b], writes=[bs_b])
                P.op("dve", lambda e: e.tensor_tensor(out=halfs[:], in0=bs[:, 1:2].broadcast_to([128, 2 * NITER]), in1=pw2[:],
                                                      op=ALU.mult), reads=[bs_b, cm_b], writes=[hf_b])
                P.op("dve", lambda e: e.tensor_tensor(out=bs[:, 3:4], in0=bs[:, 0:1], in1=halfs[:, 0:1], op=ALU.add),
                     reads=[bs_b, hf_b], writes=[bs_b])
                nit = min(NITER, int(math.ceil(math.log2(W))) + 4)
                for it in range(nit):
                    P.op("dve", lambda e: e.tensor_scalar(out=junk[:, 0:W], in0=sc[:, 0:W], scalar1=bs[:, 3:4], scalar2=None,
                                                          op0=ALU.is_ge, op1=ALU.add, accum_out=bs[:, 4:5]),
                         reads=[sc_b, bs_b], writes=[junk_b, bs_b])
                    P.op("dve", lambda e, it=it: e.scalar_tensor_tensor(out=bs[:, 5:6], in0=bs[:, 4:5], scalar=TOPK - 0.5,
                                                                        in1=halfs[:, it:it + 1], op0=ALU.is_ge, op1=ALU.mult),
                         reads=[bs_b, hf_b], writes=[bs_b])
                    P.op("dve", lambda e, it=it: e.scalar_tensor_tensor(out=bs[:, 3:4], in0=bs[:, 5:6],
                                                                        scalar=halfs[:, NITER + it:NITER + it + 1],
                                                                        in1=bs[:, 3:4], op0=ALU.subtract, op1=ALU.add),
                         reads=[bs_b, hf_b], writes=[bs_b])
                    yield
                P.op("dve", lambda e, nit=nit: e.tensor_tensor(out=bs[:, 0:1], in0=bs[:, 3:4], in1=halfs[:, NITER + nit - 1:NITER + nit],
                                                               op=ALU.subtract), reads=[bs_b, hf_b], writes=[bs_b])
            P.op("dve", lambda e, W=W: e.tensor_scalar(out=junk[:, 0:W], in0=sc[:, 0:W], scalar1=bs[:, 0:1], scalar2=None,
                                                       op0=ALU.is_ge), reads=[sc_b, bs_b], writes=[junk_b])
            for j0 in range(0, nkb, 8):
                nb = min(8, nkb - j0)
                bank = gr.next()

                def trm(e, bi=bank[0], j0=j0, nb=nb):
                    ins = None
                    for jj in range(nb):
                        ins = e.transpose(out=pgb(bi)[:, jj * 128:(jj + 1) * 128], in_=junk[:, (j0 + jj) * 128:(j0 + jj + 1) * 128],
                                          identity=C.ident_bf[:])
                    return ins

                P.op("pe", trm, reads=[junk_b, C.b], writes=[bank[1]])
                P.op("act", lambda e, bi=bank[0], j0=j0, nb=nb: e.activation(
                    out=maskT[:, i % 2, j0:j0 + nb, :].rearrange("p j t -> p (j t)"), in_=pgb(bi)[:, 0:nb * 128], func=AF.Copy),
                    reads=[bank[1]], writes=[mT_bs[i % 2]])
                yield
            yield

        def back(i):
            ts_ = slice(i * 128, (i + 1) * 128)
            nkb = i + 1
            q_b = q_bs[i % 2]
            mT_b = mT_bs[i % 2]
            items = [(h, j0, min(4, nkb - j0)) for h in range(8) for j0 in range(0, nkb, 4)]
            DEPTH = 2
            st = {}
            obs = {}
            for k in range(len(items) + DEPTH):
                if k < len(items):
                    h, j0, nb = items[k]
                    hb = (h % 2) * 64
                    m = h // 2
                    if j0 == 0:
                        obs[h] = orr.next()
                    bank = lgr.next()

                    def qk(e, bi=bank[0], j0=j0, nb=nb, h=h, hb=hb, m=m):
                        ins = None
                        for jj in range(nb):
                            j = j0 + jj
                            near = j >= i - 1
                            ins = e.matmul(pg[:, bi, jj * 128:(jj + 1) * 128], lhsT=kT[hb:hb + 64, m, j * 128:(j + 1) * 128],
                                           rhs=q_i[hb:hb + 64, i % 2, m, :], start=True, stop=not near)
                            if near:
                                ins = e.matmul(pg[:, bi, jj * 128:(jj + 1) * 128], lhsT=C.ident_bf[:],
                                               rhs=bt[:, h, 0 if j == i else 1, :], start=False, stop=True)
                        return ins

                    P.op("pe", qk, reads=kT_bs[j0:j0 + nb] + [q_b, bt_b, C.b], writes=[bank[1]])
                    ei, eb = er.next()
                    P.op("act", lambda e, bi=bank[0], ei=ei, nb=nb: e.activation(
                        out=Et[:, ei, 0:nb, :].rearrange("p j t -> p (j t)"), in_=pg[:, bi, 0:nb * 128], func=AF.Exp),
                        reads=[bank[1]], writes=[eb])
                    pi, pb_ = ptr.next()
                    P.op("pool", lambda e, ei=ei, pi=pi, j0=j0, nb=nb: e.tensor_tensor(
                        out=Pt[:, pi, 0:nb, :], in0=Et[:, ei, 0:nb, :], in1=maskT[:, i % 2, j0:j0 + nb, :], op=ALU.mult),
                        reads=[eb, mT_b], writes=[pb_])
                    st[k] = (pi, pb_)
                kk = k - DEPTH
                if kk >= 0:
                    h, j0, nb = items[kk]
                    pi, pb_ = st.pop(kk)
                    ob = obs[h]

                    def pv(e, oi=ob[0], pi=pi, j0=j0, nb=nb, h=h):
                        ins = None
                        for jj in range(nb):
                            j = j0 + jj
                            ins = e.matmul(po[:, oi, 0:65], lhsT=Pt[:, pi, jj, :], rhs=vaug[:, j, h, :], start=(j == 0),
                                           stop=(j == i))
                        return ins

                    P.op("pe", pv, reads=[pb_] + va_bs[j0:j0 + nb], writes=[ob[1]])
                    if j0 + nb == nkb:
                        di, db = rdr.next()
                        P.op("dve", lambda e, oi=ob[0], di=di: e.reciprocal(out=rden[:, di:di + 1], in_=po[:, oi, 64:65]),
                             reads=[ob[1]], writes=[db])
                        P.op("act", lambda e, oi=ob[0], di=di, h=h: e.activation(out=ytile[:, h * 64:(h + 1) * 64],
                                                                                 in_=po[:, oi, 0:64], func=AF.Copy,
                                                                                 scale=rden[:, di:di + 1]),
                             reads=[ob[1], db], writes=[yt_b])
                yield
            bank = lgr.next()

            def try_(e, bi=bank[0]):
                ins = None
                for m in range(4):
                    ins = e.transpose(out=pgb(bi)[:, m * 128:(m + 1) * 128], in_=ytile[:, m * 128:(m + 1) * 128],
                                      identity=C.ident_bf[:])
                return ins

            P.op("pe", try_, reads=[yt_b, C.b], writes=[bank[1]])
            P.op("act", lambda e, bi=bank[0]: e.activation(out=yst[:].rearrange("p m t -> p (m t)"), in_=pgb(bi)[:, 0:512],
                                                           func=AF.Copy), reads=[bank[1]], writes=[ys_b])
            P.dma("sp", lambda e: e.dma_start(out=yav[:, :, ts_], in_=yst[:]), reads=[ys_b])
            yield

        def run2(f, b):
            fa, ba = f is not None, b is not None
            while fa or ba:
                if fa:
                    try:
                        next(f)
                    except StopIteration:
                        fa = False
                if ba:
                    try:
                        next(b)
                    except StopIteration:
                        ba = False

        for i in range(NT + 1):
            run2(front(i) if i < NT else None, back(i - 1) if i >= 1 else None)
    P.barrier()


WEIGHT_SPECS = [
    ("mix_norm", [1, D]), ("w_in", [D, 5992]), ("attn_q_norm", [1, 64]), ("attn_k_norm", [1, 64]),
    ("bias_tiles", [8, 2, 128, 128]), ("b31", [1, 8]), ("rwkv_mu", [1, RWKV_IN]), ("rwkv_w0", [1, 512]), ("rwkv_w2", [64, 512]),
    ("rwkv_a0", [1, 512]), ("rwkv_a2", [64, 512]), ("rwkv_g2", [160, 512]), ("rwkv_k_k", [1, 512]),
    ("rwkv_k_a", [1, 512]), ("rwkv_r_k", [1, 512]), ("rwkv_ln_w", [1, 512]), ("rwkv_ln_b", [1, 512]),
    ("w_branch_attn", [512, D]), ("w_branch_rwkv", [512, D]), ("w_out", [D, D]), ("ffn_norm", [1, D]),
    ("w_gate_up", [D, 2 * FFN_H]), ("w_down", [FFN_H, D]),
]


def build_program(phases=("attn", "rwkv", "merge", "ffn"), debug=False):
    nc = bass.Bass("TRN2", target_bir_lowering=False)
    A = {}
    A["x"] = nc.dram_tensor("x", [S, D], F32, kind="ExternalInput").ap()
    for name, shp in WEIGHT_SPECS:
        A[name] = nc.dram_tensor(name, shp, F32, kind="ExternalInput").ap()
    out = nc.dram_tensor("out", [S, D], F32, kind="ExternalOutput").ap()
    def kind(prod, cons):
        if not debug:
            return "Internal"
        if prod in phases and cons not in phases:
            return "ExternalOutput"
        if prod not in phases and cons in phases:
            return "ExternalInput"
        return "Internal"

    ya = nc.dram_tensor("ya_scr", [512, S], BF16, kind=kind("attn", "merge")).ap()
    yr = nc.dram_tensor("yr_scr", [512, S], BF16, kind=kind("rwkv", "merge")).ap()
    hs = nc.dram_tensor("h_scr", [S, D], F32, kind=kind("merge", "ffn")).ap()
    P = Prog(nc)
    final_ops = []
    with ExitStack() as es:
        C = make_consts(P, nc, es)
        P.barrier()
        if "attn" in phases:
            phase_attn(P, nc, C, A, ya)
        if "rwkv" in phases:
            phase_rwkv(P, nc, C, A, yr)
        if "merge" in phases:
            phase_merge(P, nc, C, A["x"], ya, yr, hs, A["mix_norm"], A["w_in"], A["w_branch_attn"],
                        A["w_branch_rwkv"], A["w_out"])
        if "ffn" in phases:
            phase_ffn(P, nc, C, hs, out, A["ffn_norm"], A["w_gate_up"], A["w_down"], final_ops)
        P.emit(final_wait_ops=final_ops)
    return nc


def t5_bucket_np(d):
    d = np.maximum(d, 0)
    max_exact = 16
    log_ratio = np.log(np.maximum(d, 1).astype(np.float32) / max_exact) / math.log(128 / max_exact)
    large = np.minimum(max_exact + (log_ratio * 16).astype(np.int32), 31)
    return np.where(d < max_exact, d, large)


def host_layout(inputs):
    w = {}
    for name, shp in WEIGHT_SPECS:
        if name in ("bias_tiles", "b31"):
            continue
        w[name] = np.ascontiguousarray(np.asarray(inputs[name], dtype=np.float32).reshape(shp))
    s_idx = np.arange(128)[:, None]
    t_idx = np.arange(128)[None, :]
    rb = np.asarray(inputs["rel_bias"], dtype=np.float32)
    tiles = np.empty((8, 2, 128, 128), np.float32)
    for cls in range(2):
        bk = t5_bucket_np(t_idx - s_idx + 128 * cls)
        tiles[:, cls] = np.transpose(rb[bk], (2, 0, 1))
    w["bias_tiles"] = tiles
    w["b31"] = np.ascontiguousarray(rb[31:32, :])
    return w


_NC_CACHE = {}


def kernel(**inputs):
    x = np.asarray(inputs["x"], dtype=np.float32)
    w = host_layout(inputs)
    if "nc" not in _NC_CACHE:
        _NC_CACHE["nc"] = build_program()
    nc = _NC_CACHE["nc"]
    in_maps = []
    for b in range(8):
        m = dict(w)
        m["x"] = np.ascontiguousarray(x[b])
        in_maps.append(m)
    res = run_bass_kernel_spmd(nc, in_maps, core_ids=list(range(8)))
    return np.stack([np.asarray(r["out"], dtype=np.float32) for r in res.results], axis=0)
```

```python
import math
from contextlib import ExitStack

import numpy as np
import concourse.bass as bass
import concourse.mybir as mybir
from concourse.bass_utils import run_bass_kernel_spmd

F32 = mybir.dt.float32
BF16 = mybir.dt.bfloat16
AF = mybir.ActivationFunctionType
ALU = mybir.AluOpType
AX = mybir.AxisListType

S = 4096
D = 1024
NT = S // 128
ATTN_IN = 2120
RWKV_IN = 1824
FFN_H = 2816
RMS_EPS = 1e-6
GN_EPS = 64e-5
TOPK = 256
NEG = -1.0e30

ENGS = ("pe", "act", "dve", "pool", "sp")
SBUF_DEBUG = False


class Buf:
    __slots__ = ("name", "w", "r")

    def __init__(self, name=""):
        self.name = name
        self.w = None
        self.r = []


class Op:
    __slots__ = ("eng", "thunk", "deps", "is_dma", "sem", "val", "need_inc", "pos", "prev_on_sem")


class Prog:
    def __init__(self, nc, n_dma_sems=(56, 28)):
        self.nc = nc
        self.streams = {e: [] for e in ENGS}
        self.n_dma_sems = n_dma_sems
        self.dma_count = 0
        self.all_ops = []
        self.open_dmas = []

    def _hazards(self, op, reads, writes):
        deps = []
        for b in reads:
            if b.w is not None:
                deps.append(b.w)
        for b in writes:
            if b.w is not None:
                deps.append(b.w)
            deps.extend(b.r)
        for b in reads:
            b.r.append(op)
        for b in writes:
            b.w = op
            b.r = []
        return deps

    def op(self, eng, thunk, reads=(), writes=(), extra_deps=()):
        o = Op()
        o.eng = eng
        o.thunk = thunk
        o.is_dma = False
        o.need_inc = False
        o.sem = None
        o.val = None
        o.prev_on_sem = None
        deps = self._hazards(o, reads, writes) + list(extra_deps)
        seen = set()
        o.deps = []
        for d in deps:
            if d is o or id(d) in seen:
                continue
            if eng == "pe" and d.eng == "pe" and not d.is_dma:
                continue
            seen.add(id(d))
            o.deps.append(d)
        self.streams[eng].append(o)
        self.all_ops.append(o)
        return o

    def dma(self, eng, thunk, reads=(), writes=(), extra_deps=()):
        o = self.op(eng, thunk, reads, writes, extra_deps)
        o.is_dma = True
        o.pos = self.dma_count
        self.dma_count += 1
        self.open_dmas.append(o)
        return o

    def barrier(self):
        lasts = []
        for e in ENGS:
            for o in reversed(self.streams[e]):
                if not o.is_dma:
                    lasts.append(o)
                    break
        deps = lasts + self.open_dmas
        self.open_dmas = []
        for e in ENGS:
            self.op(e, lambda eng: eng.nop(), extra_deps=deps)

    def emit(self, final_wait_ops=()):
        nc = self.nc
        for o in self.all_ops:
            for d in o.deps:
                d.need_inc = True
        eng_sems = {e: nc.alloc_semaphore("s_" + e) for e in ENGS}
        ring_n = {"sp": self.n_dma_sems[0], "pool": self.n_dma_sems[1], "act": 2, "dve": 2, "pe": 2}
        dma_sems = {}
        dma_sem_val = {}
        dma_prev = {}
        qpos = {e: 0 for e in ENGS}
        for e in ENGS:
            if any(o.is_dma for o in self.streams[e]):
                for i in range(ring_n[e]):
                    dma_sems[(e, i)] = nc.alloc_semaphore("s_dma_%s%d" % (e, i))
                    dma_sem_val[(e, i)] = 0
                    dma_prev[(e, i)] = None
        cnt = {e: 0 for e in ENGS}
        for o in self.all_ops:
            if o.is_dma:
                kq = (o.eng, qpos[o.eng] % ring_n[o.eng])
                qpos[o.eng] += 1
                dma_sem_val[kq] += 16
                o.sem = ("dma", kq)
                o.val = dma_sem_val[kq]
                o.prev_on_sem = dma_prev[kq]
                dma_prev[kq] = o
            elif o.need_inc:
                cnt[o.eng] += 1
                o.sem = ("eng", o.eng)
                o.val = cnt[o.eng]

        def semh(key):
            return eng_sems[key[1]] if key[0] == "eng" else dma_sems[key[1]]

        engines = {"pe": "tensor", "act": "scalar", "dve": "vector", "pool": "gpsimd", "sp": "sync"}
        with nc.Block() as block:
            for e in ENGS:
                stream = self.streams[e]
                final = list(final_wait_ops) if e == "sp" else []

                def body(engine, stream=stream, final=final):
                    known = {}
                    for o in stream:
                        waits = {}
                        deps = list(o.deps)
                        if o.is_dma and o.prev_on_sem is not None:
                            deps.append(o.prev_on_sem)
                        for d in deps:
                            if known.get(d.sem, 0) >= d.val:
                                continue
                            if waits.get(d.sem, 0) < d.val:
                                waits[d.sem] = d.val
                        for key, val in waits.items():
                            engine.wait_ge(semh(key), val)
                            known[key] = val
                        ins = o.thunk(engine)
                        if o.is_dma:
                            ins.then_inc(semh(o.sem), 16)
                        elif o.need_inc:
                            ins.then_inc(semh(o.sem), 1)
                    for o in final:
                        engine.wait_ge(semh(o.sem), o.val)

                getattr(block, engines[e])(body)


class Ring:
    def __init__(self, items):
        self.items = items
        self.i = 0

    def next(self):
        it = self.items[self.i % len(self.items)]
        self.i += 1
        return it


def load_weight_bf16(P, nc, dst, dst_buf, w_ap, c0, c1, kchunks, eng="pool"):
    wv = w_ap.rearrange("(kc p) n -> p kc n", p=128)
    for kc in range(kchunks):
        for a in range(c0, c1, 2048):
            b = min(c1, a + 2048)
            P.dma(eng, lambda e, kc=kc, a=a, b=b: e.dma_start(out=dst[:, kc, a - c0:b - c0], in_=wv[:, kc, a:b]),
                  writes=[dst_buf])


def load_col_vec(P, nc, dst, dst_buf, v_ap, n):
    src = v_ap.rearrange("o (c p) -> p (o c)", p=128)
    P.dma("sp", lambda e: e.dma_start(out=dst, in_=src, allow_slow_non_contiguous=True), writes=[dst_buf])


class Consts:
    pass


def make_consts(P, nc, es):
    C = Consts()
    C.ident_bf = es.enter_context(nc.sbuf_tensor("ident_bf", [128, 128], BF16))
    C.ident_f = es.enter_context(nc.sbuf_tensor("ident_f", [128, 128], F32))
    C.eps = es.enter_context(nc.sbuf_tensor("eps_c", [128, 2], F32))
    C.b = Buf("consts")

    def mk(e):
        e.memset(C.ident_f[:], 0.0)
        e.affine_select(out=C.ident_f[:], in_=C.ident_f[:], pattern=[[-1, 128]], compare_op=ALU.not_equal,
                        fill=1.0, base=0, channel_multiplier=1)
        e.memset(C.eps[:, 0:1], RMS_EPS)
        return e.memset(C.eps[:, 1:2], GN_EPS)

    P.op("pool", mk, writes=[C.b])
    P.op("pool", lambda e: e.tensor_copy(out=C.ident_bf[:], in_=C.ident_f[:]), reads=[C.b], writes=[C.b])
    return C


class Normer:
    def __init__(self, P, nc, es, C, gain_ap, name, nslots=2):
        self.P, self.nc, self.C = P, nc, C
        self.gcol = es.enter_context(nc.sbuf_tensor(name + "_g", [128, 8], F32))
        self.gb = Buf(name + "_g")
        load_col_vec(P, nc, self.gcol[:, :], self.gb, gain_ap, 8)
        self.stat = es.enter_context(nc.sbuf_tensor(name + "_st", [128, nslots, 4], F32))
        self.junk = es.enter_context(nc.sbuf_tensor(name + "_junk", [128, 1024], BF16))
        self.xs = es.enter_context(nc.sbuf_tensor(name + "_xs", [128, nslots, 1024], BF16))
        self.tp = es.enter_context(nc.psum_tensor(name + "_tp", [128, nslots, 8, 128], BF16))
        self.ring = Ring([(i, Buf(), Buf(), Buf()) for i in range(nslots)])
        self.junkb = Buf()

    def run(self, xt_ap, xt_buf, dst, dst_buf, col0):
        P, C = self.P, self.C
        i, sb, xb, pb = self.ring.next()
        st = self.stat
        P.op("act", lambda e: e.activation(out=self.junk[:], in_=xt_ap, func=AF.Square, accum_out=st[:, i, 0:1]),
             reads=[xt_buf], writes=[self.junkb, sb])
        P.op("act", lambda e: e.activation(out=st[:, i, 1:2], in_=st[:, i, 0:1], func=AF.Ln, scale=1.0 / D,
                                           bias=C.eps[:, 0:1]), reads=[sb, C.b], writes=[sb])
        P.op("act", lambda e: e.activation(out=st[:, i, 2:3], in_=st[:, i, 1:2], func=AF.Exp, scale=-0.5),
             reads=[sb], writes=[sb])
        P.op("act", lambda e: e.activation(out=self.xs[:, i, :], in_=xt_ap, func=AF.Copy, scale=st[:, i, 2:3]),
             reads=[xt_buf, sb], writes=[xb])

        def tr(e):
            ins = None
            for kc in range(8):
                ins = e.transpose(out=self.tp[:, i, kc, :], in_=self.xs[:, i, kc * 128:(kc + 1) * 128],
                                  identity=C.ident_bf[:])
            return ins

        P.op("pe", tr, reads=[xb, C.b], writes=[pb])
        P.op("dve", lambda e: e.tensor_tensor(out=dst[:, :, col0:col0 + 128], in0=self.tp[:, i, :, :],
                                              in1=self.gcol[:, :].unsqueeze(2).broadcast_to([128, 8, 128]),
                                              op=ALU.mult), reads=[pb, self.gb], writes=[dst_buf])


def phase_ffn(P, nc, C, h_dram, out_dram, ffn_norm, w_gate_up, w_down, final_ops):
    with ExitStack() as es:
        wgu = es.enter_context(nc.sbuf_tensor("wgu", [128, 8, 2 * FFN_H], BF16))
        wd = es.enter_context(nc.sbuf_tensor("wd", [128, 22, D], BF16))
        wgu_b, wd_b = Buf("wgu"), Buf("wd")
        load_weight_bf16(P, nc, wgu, wgu_b, w_gate_up, 0, 2 * FFN_H, 8)
        load_weight_bf16(P, nc, wd, wd_b, w_down, 0, D, 22)
        nm = Normer(P, nc, es, C, ffn_norm, "fn")
        hbuf = es.enter_context(nc.sbuf_tensor("hbuf", [128, 2, D], F32))
        hr = Ring([(i, Buf()) for i in range(2)])
        hnT = es.enter_context(nc.sbuf_tensor("hnT", [128, 2, 8, 512], BF16))
        hnb = [Buf(), Buf()]
        actT = es.enter_context(nc.sbuf_tensor("actT", [128, 22, 512], BF16))
        actb = [Buf() for _ in range(22)]
        sg = es.enter_context(nc.sbuf_tensor("sg", [128, 2, 512], F32))
        sgr = Ring([(i, Buf()) for i in range(2)])
        ost = es.enter_context(nc.sbuf_tensor("ost", [128, 2, D], F32))
        ostr = Ring([(i, Buf()) for i in range(2)])
        pg = es.enter_context(nc.psum_tensor("pg", [128, 2, 512], F32))
        pu = es.enter_context(nc.psum_tensor("pu", [128, 2, 512], F32))
        po = es.enter_context(nc.psum_tensor("po", [128, 2, 512], F32))
        pgr = Ring([(i, Buf()) for i in range(2)])
        pur = Ring([(i, Buf()) for i in range(2)])
        por = Ring([(i, Buf()) for i in range(2)])
        hview = h_dram.rearrange("(t p) d -> t p d", p=128)
        oview = out_dram.rearrange("(t p) d -> t p d", p=128)
        def ffn_norm_chunk(c):
            cb = c % 2
            for j in range(4):
                t = c * 4 + j
                hi, hb_ = hr.next()
                P.dma("sp", lambda e, t=t, hi=hi: e.dma_start(out=hbuf[:, hi, :], in_=hview[t]), writes=[hb_])
                nm.run(hbuf[:, hi, :], hb_, hnT[:, cb], hnb[cb], j * 128)

        ffn_norm_chunk(0)
        for c in range(S // 512):
            cb = c % 2
            if c + 1 < S // 512:
                ffn_norm_chunk(c + 1)
            for m in range(22):
                gi, gbuf = pgr.next()
                ui, ubuf = pur.next()

                def mm(e, m=m, gi=gi, ui=ui, cb=cb):
                    for kc in range(8):
                        e.matmul(pg[:, gi, :], lhsT=wgu[:, kc, m * 128:(m + 1) * 128], rhs=hnT[:, cb, kc, :],
                                 start=(kc == 0), stop=(kc == 7))
                    ins = None
                    for kc in range(8):
                        ins = e.matmul(pu[:, ui, :], lhsT=wgu[:, kc, FFN_H + m * 128:FFN_H + (m + 1) * 128],
                                       rhs=hnT[:, cb, kc, :], start=(kc == 0), stop=(kc == 7))
                    return ins

                P.op("pe", mm, reads=[wgu_b, hnb[cb]], writes=[gbuf, ubuf])
                si, sbuf_ = sgr.next()
                P.op("act", lambda e, gi=gi, si=si: e.activation(out=sg[:, si, :], in_=pg[:, gi, :], func=AF.Silu),
                     reads=[gbuf], writes=[sbuf_])
                P.op("dve", lambda e, ui=ui, si=si, m=m: e.tensor_tensor(out=actT[:, m, :], in0=pu[:, ui, :],
                                                                         in1=sg[:, si, :], op=ALU.mult),
                     reads=[ubuf, sbuf_], writes=[actb[m]])
            for j in range(4):
                t = c * 4 + j
                oi, obuf = ostr.next()
                P.dma("sp", lambda e, t=t, oi=oi: e.dma_start(out=ost[:, oi, :], in_=hview[t]), writes=[obuf])
                for nh in range(2):
                    pi, pbuf = por.next()

                    def mmd(e, j=j, nh=nh, pi=pi):
                        ins = None
                        for m in range(22):
                            ins = e.matmul(po[:, pi, :], lhsT=actT[:, m, j * 128:(j + 1) * 128],
                                           rhs=wd[:, m, nh * 512:(nh + 1) * 512], start=(m == 0), stop=(m == 21))
                        return ins

                    P.op("pe", mmd, reads=actb + [wd_b], writes=[pbuf])
                    P.op("dve", lambda e, nh=nh, pi=pi, oi=oi: e.tensor_tensor(
                        out=ost[:, oi, nh * 512:(nh + 1) * 512], in0=po[:, pi, :],
                        in1=ost[:, oi, nh * 512:(nh + 1) * 512], op=ALU.add),
                        reads=[pbuf], writes=[obuf])
                final_ops.append(P.dma("sp", lambda e, t=t, oi=oi: e.dma_start(out=oview[t], in_=ost[:, oi, :]),
                                       reads=[obuf]))
    P.barrier()


def phase_merge(P, nc, C, x_dram, ya_dram, yr_dram, h_dram, mix_norm, w_in, w_ba, w_br, w_out):
    with ExitStack() as es:
        wg = es.enter_context(nc.sbuf_tensor("wg", [128, 8, 2 * D], BF16))
        wba = es.enter_context(nc.sbuf_tensor("wba", [128, 4, D], BF16))
        wbr = es.enter_context(nc.sbuf_tensor("wbr", [128, 4, D], BF16))
        wo = es.enter_context(nc.sbuf_tensor("wo", [128, 8, D], BF16))
        wg_b, wba_b, wbr_b, wo_b = Buf(), Buf(), Buf(), Buf()
        load_weight_bf16(P, nc, wg, wg_b, w_in, ATTN_IN + RWKV_IN, ATTN_IN + RWKV_IN + 2 * D, 8)
        load_weight_bf16(P, nc, wba, wba_b, w_ba, 0, D, 4)
        load_weight_bf16(P, nc, wbr, wbr_b, w_br, 0, D, 4)
        load_weight_bf16(P, nc, wo, wo_b, w_out, 0, D, 8)
        nm = Normer(P, nc, es, C, mix_norm, "mn")
        xbuf = es.enter_context(nc.sbuf_tensor("xbuf", [128, 2, 4, D], F32))
        xb = [[Buf() for _ in range(4)] for _ in range(2)]
        xnT = es.enter_context(nc.sbuf_tensor("xnT", [128, 2, 8, 512], BF16))
        xnb = [Buf(), Buf()]
        yaT = es.enter_context(nc.sbuf_tensor("yaT", [128, 2, 4, 512], BF16))
        yrT = es.enter_context(nc.sbuf_tensor("yrT", [128, 2, 4, 512], BF16))
        yab, yrb = [Buf(), Buf()], [Buf(), Buf()]
        mT = es.enter_context(nc.sbuf_tensor("mT", [128, 8, 512], BF16))
        mb = [Buf() for _ in range(8)]
        sg = es.enter_context(nc.sbuf_tensor("sgm", [128, 2, 2, 512], F32))
        sgr = Ring([(i, Buf()) for i in range(2)])
        tt = es.enter_context(nc.sbuf_tensor("ttm", [128, 2, 2, 512], F32))
        ttr = Ring([(i, Buf()) for i in range(2)])
        hst = es.enter_context(nc.sbuf_tensor("hst", [128, 2, D], F32))
        hstr = Ring([(i, Buf()) for i in range(2)])
        pga = es.enter_context(nc.psum_tensor("pga", [128, 2, 512], F32))
        pbr = es.enter_context(nc.psum_tensor("pbr", [128, 2, 512], F32))
        po = es.enter_context(nc.psum_tensor("pom", [128, 2, 512], F32))
        pgb, pbb = Buf(), Buf()
        por = Ring([(i, Buf()) for i in range(2)])
        xview = x_dram.rearrange("(t p) d -> t p d", p=128)
        hview = h_dram.rearrange("(t p) d -> t p d", p=128)
        yav = ya_dram.rearrange("(kc p) s -> p kc s", p=128)
        yrv = yr_dram.rearrange("(kc p) s -> p kc s", p=128)
        def merge_load_chunk(c):
            cb = c % 2
            P.dma("sp", lambda e: e.dma_start(out=yaT[:, cb], in_=yav[:, :, c * 512:(c + 1) * 512]), writes=[yab[cb]])
            P.dma("sp", lambda e: e.dma_start(out=yrT[:, cb], in_=yrv[:, :, c * 512:(c + 1) * 512]), writes=[yrb[cb]])
            for j in range(4):
                t = c * 4 + j
                P.dma("sp", lambda e, t=t, j=j: e.dma_start(out=xbuf[:, cb, j, :], in_=xview[t]), writes=[xb[cb][j]])
                nm.run(xbuf[:, cb, j, :], xb[cb][j], xnT[:, cb], xnb[cb], j * 128)

        merge_load_chunk(0)
        for c in range(S // 512):
            cb = c % 2
            if c + 1 < S // 512:
                merge_load_chunk(c + 1)
            for m in range(8):
                def mmg(e, m=m, cb=cb):
                    ins = None
                    for g in range(2):
                        for kc in range(8):
                            ins = e.matmul(pga[:, g, :], lhsT=wg[:, kc, g * D + m * 128:g * D + (m + 1) * 128],
                                           rhs=xnT[:, cb, kc, :], start=(kc == 0), stop=(kc == 7))
                    return ins

                P.op("pe", mmg, reads=[wg_b, xnb[cb]], writes=[pgb])

                def mmb(e, m=m, cb=cb):
                    ins = None
                    for kc in range(4):
                        ins = e.matmul(pbr[:, 0, :], lhsT=wba[:, kc, m * 128:(m + 1) * 128], rhs=yaT[:, cb, kc, :],
                                       start=(kc == 0), stop=(kc == 3))
                    for kc in range(4):
                        ins = e.matmul(pbr[:, 1, :], lhsT=wbr[:, kc, m * 128:(m + 1) * 128], rhs=yrT[:, cb, kc, :],
                                       start=(kc == 0), stop=(kc == 3))
                    return ins

                P.op("pe", mmb, reads=[wba_b, wbr_b, yab[cb], yrb[cb]], writes=[pbb])
                si, sbuf_ = sgr.next()
                P.op("act", lambda e, si=si: e.activation(out=sg[:, si], in_=pga[:, :, :], func=AF.Sigmoid),
                     reads=[pgb], writes=[sbuf_])
                ti, tbuf = ttr.next()
                P.op("dve", lambda e, si=si, ti=ti: e.tensor_tensor(out=tt[:, ti], in0=pbr[:, :, :], in1=sg[:, si],
                                                                    op=ALU.mult),
                     reads=[pbb, sbuf_], writes=[tbuf])
                P.op("pool", lambda e, ti=ti, m=m: e.tensor_tensor(out=mT[:, m, :], in0=tt[:, ti, 0, :],
                                                                   in1=tt[:, ti, 1, :], op=ALU.add),
                     reads=[tbuf], writes=[mb[m]])
            for j in range(4):
                t = c * 4 + j
                hi, hbuf_ = hstr.next()
                for nh in range(2):
                    pi, pbuf = por.next()

                    def mmo(e, j=j, nh=nh, pi=pi):
                        ins = None
                        for m in range(8):
                            ins = e.matmul(po[:, pi, :], lhsT=mT[:, m, j * 128:(j + 1) * 128],
                                           rhs=wo[:, m, nh * 512:(nh + 1) * 512], start=(m == 0), stop=(m == 7))
                        return ins

                    P.op("pe", mmo, reads=mb + [wo_b], writes=[pbuf])
                    P.op("dve", lambda e, j=j, nh=nh, pi=pi, hi=hi, cb=cb: e.tensor_tensor(
                        out=hst[:, hi, nh * 512:(nh + 1) * 512], in0=po[:, pi, :],
                        in1=xbuf[:, cb, j, nh * 512:(nh + 1) * 512], op=ALU.add),
                        reads=[pbuf, xb[cb][j]], writes=[hbuf_])
                P.dma("sp", lambda e, t=t, hi=hi: e.dma_start(out=hview[t], in_=hst[:, hi, :]), reads=[hbuf_])
    P.barrier()


TL = 128
C0 = math.exp(-0.5)


def col8(P, nc, es, name, v_ap):
    t = es.enter_context(nc.sbuf_tensor(name, [64, 8], F32))
    b = Buf(name)
    P.dma("sp", lambda e: e.dma_start(out=t[:, :], in_=v_ap.rearrange("o (h k) -> k (o h)", k=64),
                                      allow_slow_non_contiguous=True), writes=[b])
    return t, b


def phase_rwkv(P, nc, C, A, yr_dram):
    x_dram = A["x"]
    with ExitStack() as es:
        sb = lambda name, shape, dt=F32: es.enter_context(nc.sbuf_tensor(name, shape, dt))
        wr = sb("wr", [128, 8, RWKV_IN], BF16)
        wmu = sb("wmu", [128, 8, RWKV_IN], BF16)
        wr_b, wmu_b, mub_b = Buf(), Buf(), Buf()
        load_weight_bf16(P, nc, wr, wr_b, A["w_in"], ATTN_IN, ATTN_IN + RWKV_IN, 8)
        with nc.sbuf_tensor("mub", [128, RWKV_IN], F32) as mub:
            P.dma("sp", lambda e: e.dma_start(out=mub[:], in_=A["rwkv_mu"].partition_broadcast(128)), writes=[mub_b])
            for kc in range(8):
                P.op("pool", lambda e, kc=kc: e.tensor_tensor(out=wmu[:, kc, :], in0=wr[:, kc, :], in1=mub[:],
                                                              op=ALU.mult), reads=[wr_b, mub_b], writes=[wmu_b])
        P.barrier()
        w2 = sb("w2", [64, 512], BF16)
        a2 = sb("a2", [64, 512], BF16)
        g2 = sb("g2", [64, 3, 512], BF16)
        lw_b = Buf()
        P.dma("pool", lambda e: e.dma_start(out=w2[:], in_=A["rwkv_w2"]), writes=[lw_b])
        P.dma("pool", lambda e: e.dma_start(out=a2[:], in_=A["rwkv_a2"]), writes=[lw_b])
        P.dma("pool", lambda e: e.dma_start(out=g2[:, 0, :], in_=A["rwkv_g2"][0:64, :]), writes=[lw_b])
        P.dma("pool", lambda e: e.dma_start(out=g2[:, 1, :], in_=A["rwkv_g2"][64:128, :]), writes=[lw_b])
        P.dma("pool", lambda e: e.dma_start(out=g2[0:32, 2, :], in_=A["rwkv_g2"][128:160, :]), writes=[lw_b])
        cw0, b_w0 = col8(P, nc, es, "cw0", A["rwkv_w0"])
        ca0, b_a0 = col8(P, nc, es, "ca0", A["rwkv_a0"])
        ckk, b_kk = col8(P, nc, es, "ckk", A["rwkv_k_k"])
        cka, b_ka = col8(P, nc, es, "cka", A["rwkv_k_a"])
        crk, b_rk = col8(P, nc, es, "crk", A["rwkv_r_k"])
        clw, b_lw = col8(P, nc, es, "clw", A["rwkv_ln_w"])
        clb, b_lb = col8(P, nc, es, "clb", A["rwkv_ln_b"])
        msk = sb("rmsk", [64, 3, 64], F32)
        ones_bf = sb("ones_bf", [64, 64], BF16)
        rmask = sb("rmask", [64, 8, TL // 64, 64], F32)
        mk_b = Buf()

        def mkmasks(e):
            e.memset(msk[:], 1.0)
            e.memset(ones_bf[:], 1.0)
            e.memset(rmask[:], 1.0)
            e.memset(rmask[:, :, :, 0:1], 0.0)
            e.affine_select(out=msk[:, 0, :], in_=msk[:, 0, :], pattern=[[1, 64]], compare_op=ALU.is_ge,
                            fill=0.0, base=-1, channel_multiplier=-1)
            e.affine_select(out=msk[:, 1, :], in_=msk[:, 1, :], pattern=[[1, 64]], compare_op=ALU.is_ge,
                            fill=0.0, base=0, channel_multiplier=-1)
            return e.affine_select(out=msk[:, 2, :], in_=msk[:, 2, :], pattern=[[-1, 64]], compare_op=ALU.is_ge,
                                   fill=0.0, base=-1, channel_multiplier=1)

        P.op("pool", mkmasks, writes=[mk_b])
        mb3 = lambda i: msk[:, i, :].unsqueeze(1).broadcast_to([64, 8, 64])
        idb = C.ident_bf[0:64, 0:64]
        idf3 = C.ident_f[0:64, 0:64].unsqueeze(1).broadcast_to([64, 8, 64])
        bc = lambda col: col[:, :].unsqueeze(2).broadcast_to([64, 8, TL])

        nm = Normer(P, nc, es, C, A["mix_norm"], "rn", nslots=1)
        xbuf = sb("rxbuf", [128, 1, D], F32)
        xr = Ring([(i, Buf()) for i in range(1)])
        xnx = sb("xnx", [128, 2, 8, TL + 1], BF16)
        xnb = [Buf(), Buf()]
        dxn = sb("dxn", [128, 8, TL], BF16)
        dxb = Buf()
        P.op("pool", lambda e: e.memset(xnx[:, 1, :, TL:TL + 1], 0.0), writes=[xnb[1]])
        F = {}
        FB = {}
        for nme, dt in [("r", F32), ("k", F32), ("sgd", F32), ("a", F32), ("kk", F32),
                        ("t1", F32), ("t2", F32), ("cum", F32), ("e1", F32), ("e2", F32), ("sqb", BF16)]:
            alias = {"e1": "t1", "e2": "a"}
            if nme in alias:
                F[nme] = F[alias[nme]]
                FB[nme] = FB[alias[nme]]
                continue
            F[nme] = sb("f_" + nme, [64, 8, TL], dt)
            FB[nme] = Buf(nme)
        X2 = {}
        X2B = {}
        for nme in ("rT", "aT", "bT", "kT", "bH", "kH", "vb", "g", "bon"):
            X2[nme] = sb("x_" + nme, [64, 2, 8, TL], BF16)
            X2B[nme] = [Buf(nme + "0"), Buf(nme + "1")]
        lora = sb("lora", [64, 5, TL], BF16)
        ztok = sb("ztok", [128, 2, 512], F32)
        ztr = Ring([(i, Buf()) for i in range(2)])
        lora_b = Buf()
        gC = sb("gC", [64, 2, 8, TL // 64], F32)
        gC_bs = [Buf(), Buf()]
        Sf = sb("Sf", [64, 8, 64], F32)
        Sb = sb("Sb", [64, 8, 64], BF16)
        St = sb("St", [64, 8, 64], F32)
        S_b, St_b = Buf(), Buf()
        P.op("dve", lambda e: e.memset(Sf[:], 0.0), writes=[S_b])
        P.op("dve", lambda e: e.memset(Sb[:], 0.0), reads=[S_b], writes=[S_b])
        ost = sb("rost", [64, 8, TL], BF16)
        ost_b = Buf()
        def pair(name, dt=BF16, n=2):
            t = sb(name, [64, n, 8, 64], dt)
            return t, [Buf() for _ in range(n)]
        Atok, Atok_b = pair("Atok")
        BHtok, BHtok_b = pair("BHtok")
        KHtok, KHtok_b = pair("KHtok")
        Vtok, Vtok_b = pair("Vtok")
        Mrb, Mrb_b = pair("Mrb")
        Mrk, Mrk_b = pair("Mrk")
        Lak, Lak_b = pair("Lak")
        Nn, Nn_b = pair("Nn", BF16, 4)
        Mm, Mm_b = pair("Mm", BF16, 4)
        Qq, Qq_b = pair("Qq", BF16, 4)
        WT, WT_b = pair("WT")
        Xx, Xx_b = pair("Xx")
        Uu, Uu_b = pair("Uu")
        Ys, Ys_b = pair("Ys", F32, 1)
        Yn, Yn_b = pair("Yn", BF16, 2)
        gst = sb("gst", [64, 2, 8, 6], F32)
        gst_b = [Buf(), Buf()]
        pb = es.enter_context(nc.psum_tensor("rpb", [128, 7, 512], F32))
        pr = Ring([(i, Buf()) for i in range(7)])
        pv = lambda i: pb[0:64, i, :].rearrange("p (h t) -> p h t", h=8)
        pvb = lambda i: pb[0:64, i, :].bitcast(BF16)[:, 0:512].rearrange("p (h t) -> p h t", h=8)

        xview = x_dram.rearrange("(t p) d -> t p d", p=128)

        def mm8(bank, parts, rd):
            i, bbuf = bank
            ops = [[(lf(h), rf(h)) for (lf, rf) in parts] for h in range(8)]

            def th(e):
                ins = None
                for h in range(8):
                    for pi, (l, r) in enumerate(ops[h]):
                        ins = e.matmul(pb[0:64, i, h * 64:(h + 1) * 64], lhsT=l, rhs=r,
                                       start=(pi == 0), stop=(pi == len(ops[h]) - 1))
                return ins

            P.op("pe", th, reads=rd, writes=[bbuf])

        def tr8(bank, src_fn, rd):
            i, bbuf = bank
            srcs = [src_fn(h) for h in range(8)]

            def th(e):
                ins = None
                v = pvb(i)
                for h in range(8):
                    ins = e.transpose(out=v[:, h, :], in_=srcs[h], identity=idb)
                return ins

            P.op("pe", th, reads=rd + [C.b], writes=[bbuf])

        NB = S // TL

        def prep(n):
            par = n % 2
            cb = n % 2
            pc = 1 - cb
            X = {k: X2[k][:, par] for k in X2}
            XB = {k: X2B[k][par] for k in X2}
            P.op("pool", lambda e: e.tensor_copy(out=xnx[:, cb, :, 0:1], in_=xnx[:, pc, :, TL:TL + 1]),
                 reads=[xnb[pc]], writes=[xnb[cb]])
            for j in range(TL // 128):
                xi, xb_ = xr.next()
                P.dma("sp", lambda e, t=n * (TL // 128) + j, xi=xi: e.dma_start(out=xbuf[:, xi, :], in_=xview[t]), writes=[xb_])
                nm.run(xbuf[:, xi, :], xb_, xnx[:, cb], xnb[cb], 1 + j * 128)
            P.op("pool", lambda e: e.tensor_tensor(out=dxn[:], in0=xnx[:, cb, :, 0:TL], in1=xnx[:, cb, :, 1:TL + 1],
                                                   op=ALU.subtract), reads=[xnb[cb]], writes=[dxb])
            yield

            def projtok(c0, ncol):
                bank = pr.next()
                i = bank[0]

                def th(e):
                    ins = None
                    for kc in range(8):
                        e.matmul(pb[:, i, 0:ncol], lhsT=xnx[:, cb, kc, 1:TL + 1], rhs=wr[:, kc, c0:c0 + ncol],
                                 start=(kc == 0), stop=False)
                    for kc in range(8):
                        ins = e.matmul(pb[:, i, 0:ncol], lhsT=dxn[:, kc, :], rhs=wmu[:, kc, c0:c0 + ncol],
                                       start=False, stop=(kc == 7))
                    return ins

                P.op("pe", th, reads=[wr_b, wmu_b, xnb[cb], dxb], writes=[bank[1]])
                zi, zb = ztr.next()
                P.op("act", lambda e: e.activation(out=ztok[:, zi, 0:ncol], in_=pb[:, i, 0:ncol], func=AF.Copy),
                     reads=[bank[1]], writes=[zb])
                return zi, zb

            def trz(zi, zb, cols, m):
                bank = pr.next()
                i = bank[0]

                def th(e):
                    ins = None
                    for q, c in enumerate(cols):
                        ins = e.transpose(out=pb[0:m, i, q * TL:(q + 1) * TL], in_=ztok[:, zi, c:c + m], identity=C.ident_f[:])
                    return ins

                P.op("pe", th, reads=[zb, C.b], writes=[bank[1]])
                return bank

            for qi, qn in enumerate(("r", "k", "v")):
                zi, zb = projtok(qi * 512, 512)
                yield
                for h0 in (0, 4):
                    bank = trz(zi, zb, [(h0 + q) * 64 for q in range(4)], 64)
                    dst, dstb = (X["vb"], XB["vb"]) if qn == "v" else (F[qn], FB[qn])
                    P.op("act", lambda e, dst=dst, h0=h0, i=bank[0]: e.activation(
                        out=dst[:, h0:h0 + 4, :], in_=pb[0:64, i, 0:4 * TL].rearrange("p (q t) -> p q t", q=4), func=AF.Copy),
                        reads=[bank[1]], writes=[dstb])
                    yield
            zi, zb = projtok(1536, 288)
            yield
            for li, (c0, m, fn) in enumerate([(0, 64, AF.Tanh), (64, 64, AF.Copy), (128, 64, AF.Sigmoid),
                                              (192, 64, AF.Sigmoid), (256, 32, AF.Sigmoid)]):
                bank = trz(zi, zb, [c0], m)
                P.op("act", lambda e, li=li, m=m, fn=fn, i=bank[0]: e.activation(out=lora[0:m, li, :], in_=pb[0:m, i, 0:TL],
                                                                                 func=fn),
                     reads=[bank[1]], writes=[lora_b])
                yield
            for h in range(8):
                bank = pr.next()
                P.op("pe", lambda e, h=h, i=bank[0]: e.matmul(pb[0:64, i, 0:TL], lhsT=w2[:, h * 64:(h + 1) * 64],
                                                              rhs=lora[:, 0, :], start=True, stop=True),
                     reads=[lw_b, lora_b], writes=[bank[1]])
                P.op("act", lambda e, h=h, i=bank[0]: e.activation(out=F["sgd"][:, h, :], in_=pb[0:64, i, 0:TL],
                                                                   func=AF.Sigmoid, bias=cw0[:, h:h + 1]),
                     reads=[bank[1], b_w0], writes=[FB["sgd"]])
                bank = pr.next()
                P.op("pe", lambda e, h=h, i=bank[0]: e.matmul(pb[0:64, i, 0:TL], lhsT=a2[:, h * 64:(h + 1) * 64],
                                                              rhs=lora[:, 1, :], start=True, stop=True),
                     reads=[lw_b, lora_b], writes=[bank[1]])
                P.op("act", lambda e, h=h, i=bank[0]: e.activation(out=F["a"][:, h, :], in_=pb[0:64, i, 0:TL],
                                                                   func=AF.Sigmoid, bias=ca0[:, h:h + 1]),
                     reads=[bank[1], b_a0], writes=[FB["a"]])
                bank = pr.next()

                def gmm(e, h=h, i=bank[0]):
                    e.matmul(pb[0:64, i, 0:TL], lhsT=g2[:, 0, h * 64:(h + 1) * 64], rhs=lora[:, 2, :], start=True, stop=False)
                    e.matmul(pb[0:64, i, 0:TL], lhsT=g2[:, 1, h * 64:(h + 1) * 64], rhs=lora[:, 3, :], start=False, stop=False)
                    return e.matmul(pb[0:64, i, 0:TL], lhsT=g2[0:32, 2, h * 64:(h + 1) * 64], rhs=lora[0:32, 4, :],
                                    start=False, stop=True)

                P.op("pe", gmm, reads=[lw_b, lora_b], writes=[bank[1]])
                P.op("act", lambda e, h=h, i=bank[0]: e.activation(out=X["g"][:, h, :], in_=pb[0:64, i, 0:TL], func=AF.Copy),
                     reads=[bank[1]], writes=[XB["g"]])
                yield
            P.op("dve", lambda e: e.tensor_tensor(out=F["kk"][:], in0=F["k"][:], in1=bc(ckk), op=ALU.mult),
                 reads=[FB["k"], b_kk], writes=[FB["kk"]])
            P.op("pool", lambda e: e.tensor_tensor(out=F["sqb"][:], in0=F["kk"][:], in1=F["kk"][:], op=ALU.mult),
                 reads=[FB["kk"]], writes=[FB["sqb"]])
            yield
            for h in range(8):
                bank = pr.next()
                P.op("pe", lambda e, h=h, i=bank[0]: e.matmul(pb[0:64, i, 0:TL], lhsT=ones_bf[:], rhs=F["sqb"][:, h, :],
                                                              start=True, stop=True),
                     reads=[mk_b, FB["sqb"]], writes=[bank[1]])
                P.op("act", lambda e, h=h, i=bank[0]: e.activation(out=F["t1"][:, h, :], in_=pb[0:64, i, 0:TL], func=AF.Sqrt),
                     reads=[bank[1]], writes=[FB["t1"]])
                if h % 2 == 1:
                    yield
            P.op("dve", lambda e: e.tensor_scalar(out=F["t1"][:], in0=F["t1"][:], scalar1=1e-12, scalar2=None,
                                                  op0=ALU.max), reads=[FB["t1"]], writes=[FB["t1"]])
            P.op("dve", lambda e: e.reciprocal(out=F["t1"][:], in_=F["t1"][:]), reads=[FB["t1"]], writes=[FB["t1"]])
            yield
            P.op("dve", lambda e: e.tensor_tensor(out=F["kk"][:], in0=F["kk"][:], in1=F["t1"][:], op=ALU.mult),
                 reads=[FB["kk"], FB["t1"]], writes=[FB["kk"]])
            P.op("dve", lambda e: e.scalar_tensor_tensor(out=F["t2"][:], in0=F["a"][:], scalar=-1.0, in1=bc(cka),
                                                         op0=ALU.add, op1=ALU.mult),
                 reads=[FB["a"], b_ka], writes=[FB["t2"]])
            yield
            P.op("dve", lambda e: e.scalar_tensor_tensor(out=F["k"][:], in0=F["t2"][:], scalar=1.0, in1=F["k"][:],
                                                         op0=ALU.add, op1=ALU.mult),
                 reads=[FB["t2"], FB["k"]], writes=[FB["k"]])
            P.op("pool", lambda e: e.tensor_tensor(out=F["t2"][:], in0=F["kk"][:], in1=F["a"][:], op=ALU.mult),
                 reads=[FB["kk"], FB["a"]], writes=[FB["t2"]])
            yield
            P.op("dve", lambda e: e.tensor_tensor_scan(out=F["cum"][:].rearrange("p h t -> p (h t)"),
                                                       data0=rmask[:].rearrange("p h c t -> p (h c t)"),
                                                       data1=F["sgd"][:].rearrange("p h t -> p (h t)"),
                                                       initial=0.0, op0=ALU.mult, op1=ALU.add),
                 reads=[FB["sgd"], mk_b], writes=[FB["cum"]])
            cum4 = F["cum"][:].rearrange("p h (c t) -> p h c t", t=64)
            yield
            P.op("act", lambda e: e.activation(out=F["e1"][:], in_=F["cum"][:], func=AF.Exp, scale=-C0),
                 reads=[FB["cum"]], writes=[FB["e1"]])
            P.op("dve", lambda e: e.tensor_tensor(out=X["rT"][:], in0=F["r"][:], in1=F["e1"][:], op=ALU.mult),
                 reads=[FB["r"], FB["e1"]], writes=[XB["rT"]])
            P.op("act", lambda e: e.activation(out=gC[:, par], in_=cum4[:, :, :, 63], func=AF.Exp, scale=-C0),
                 reads=[FB["cum"]], writes=[gC_bs[par]])
            yield
            P.op("pool", lambda e: e.tensor_tensor(out=F["e2"][:], in0=F["cum"][:], in1=F["sgd"][:], op=ALU.subtract),
                 reads=[FB["cum"], FB["sgd"]], writes=[FB["e2"]])
            P.op("act", lambda e: e.activation(out=F["e2"][:], in_=F["e2"][:], func=AF.Exp, scale=-C0),
                 reads=[FB["e2"]], writes=[FB["e2"]])
            P.op("dve", lambda e: e.scalar_tensor_tensor(out=X["aT"][:], in0=F["kk"][:], scalar=-1.0, in1=F["e2"][:],
                                                         op0=ALU.mult, op1=ALU.mult),
                 reads=[FB["kk"], FB["e2"]], writes=[XB["aT"]])
            yield
            P.op("act", lambda e: e.activation(out=F["e1"][:], in_=F["cum"][:], func=AF.Exp, scale=C0),
                 reads=[FB["cum"]], writes=[FB["e1"]])
            P.op("dve", lambda e: e.tensor_tensor(out=X["bT"][:], in0=F["t2"][:], in1=F["e1"][:], op=ALU.mult),
                 reads=[FB["t2"], FB["e1"]], writes=[XB["bT"]])
            P.op("pool", lambda e: e.tensor_tensor(out=X["kT"][:], in0=F["k"][:], in1=F["e1"][:], op=ALU.mult),
                 reads=[FB["k"], FB["e1"]], writes=[XB["kT"]])
            yield
            P.op("dve", lambda e: e.tensor_tensor(out=F["e2"][:].rearrange("p h (c t) -> p h c t", t=64),
                                                  in0=cum4[:, :, :, 63:64].broadcast_to([64, 8, TL // 64, 64]), in1=cum4,
                                                  op=ALU.subtract),
                 reads=[FB["cum"]], writes=[FB["e2"]])
            P.op("act", lambda e: e.activation(out=F["e2"][:], in_=F["e2"][:], func=AF.Exp, scale=-C0),
                 reads=[FB["e2"]], writes=[FB["e2"]])
            yield
            P.op("dve", lambda e: e.tensor_tensor(out=X["bH"][:], in0=F["t2"][:], in1=F["e2"][:], op=ALU.mult),
                 reads=[FB["t2"], FB["e2"]], writes=[XB["bH"]])
            P.op("pool", lambda e: e.tensor_tensor(out=X["kH"][:], in0=F["k"][:], in1=F["e2"][:], op=ALU.mult),
                 reads=[FB["k"], FB["e2"]], writes=[XB["kH"]])
            yield
            P.op("dve", lambda e: e.tensor_tensor(out=F["t1"][:], in0=F["r"][:], in1=F["k"][:], op=ALU.mult),
                 reads=[FB["r"], FB["k"]], writes=[FB["t1"]])
            P.op("pool", lambda e: e.tensor_tensor(out=F["sqb"][:], in0=F["t1"][:], in1=bc(crk), op=ALU.mult),
                 reads=[FB["t1"], b_rk], writes=[FB["sqb"]])
            yield
            for h in range(8):
                bank = pr.next()
                P.op("pe", lambda e, h=h, i=bank[0]: e.matmul(pb[0:64, i, 0:TL], lhsT=ones_bf[:], rhs=F["sqb"][:, h, :],
                                                              start=True, stop=True),
                     reads=[mk_b, FB["sqb"]], writes=[bank[1]])
                P.op("dve", lambda e, h=h, i=bank[0]: e.tensor_tensor(out=X["bon"][:, h, :], in0=pb[0:64, i, 0:TL],
                                                                      in1=X["vb"][:, h, :], op=ALU.mult),
                     reads=[bank[1], XB["vb"]], writes=[XB["bon"]])
                if h % 2 == 1:
                    yield

        def chunk_pre(n, c, out):
            par = n % 2
            X = {k: X2[k][:, par] for k in X2}
            XB = {k: X2B[k][par] for k in X2}
            cs = slice(c * 64, (c + 1) * 64)
            for (dst, dbs, src) in ((Atok, Atok_b, "aT"), (BHtok, BHtok_b, "bH"), (KHtok, KHtok_b, "kH"), (Vtok, Vtok_b, "vb")):
                bank = pr.next()
                tr8(bank, lambda h, src=src: X[src][:, h, cs], [XB[src]])
                P.op("act", lambda e, dst=dst, i=bank[0]: e.activation(out=dst[:, c], in_=pvb(i), func=AF.Copy),
                     reads=[bank[1]], writes=[dbs[c]])
                yield

            def gmat(lname, rname, mi, dst, dbs, slot):
                bank = pr.next()
                mm8(bank, [(lambda h: X[lname][:, h, cs], lambda h: X[rname][:, h, cs])], [XB[lname], XB[rname]])
                P.op("dve", lambda e, i=bank[0]: e.tensor_tensor(out=dst[:, slot], in0=pv(i), in1=mb3(mi), op=ALU.mult),
                     reads=[bank[1], mk_b], writes=[dbs[slot]])

            base = 2 * c
            gmat("bT", "aT", 0, Mm, Mm_b, base)
            yield
            gmat("bT", "rT", 1, Mrb, Mrb_b, c)
            yield
            gmat("kT", "aT", 0, Lak, Lak_b, c)
            yield
            gmat("kT", "rT", 1, Mrk, Mrk_b, c)
            yield
            gmat("aT", "bT", 2, Nn, Nn_b, base)
            yield
            P.op("pool", lambda e: e.tensor_tensor(out=Qq[:, base], in0=Mm[:, base], in1=idf3, op=ALU.add),
                 reads=[Mm_b[base], C.b], writes=[Qq_b[base]])
            ni = mi_ = qi_ = base
            for lvl in range(1, 6):
                nn_ = base + (1 - (ni - base))
                nm_ = base + (1 - (mi_ - base))
                nq_ = base + (1 - (qi_ - base))
                bank = pr.next()
                mm8(bank, [(lambda h: Mm[:, mi_, h, :], lambda h: Nn[:, ni, h, :])], [Mm_b[mi_], Nn_b[ni]])
                if lvl < 5:
                    bank2 = pr.next()
                    mm8(bank2, [(lambda h: Nn[:, ni, h, :], lambda h: Mm[:, mi_, h, :])], [Mm_b[mi_], Nn_b[ni]])
                P.op("act", lambda e, nn_=nn_, i=bank[0]: e.activation(out=Nn[:, nn_], in_=pv(i), func=AF.Copy),
                     reads=[bank[1]], writes=[Nn_b[nn_]])
                if lvl < 5:
                    P.op("dve", lambda e, nm_=nm_, i=bank2[0]: e.tensor_copy(out=Mm[:, nm_], in_=pv(i)),
                         reads=[bank2[1]], writes=[Mm_b[nm_]])
                    mi_ = nm_
                ni = nn_
                yield
                bank3 = pr.next()
                mm8(bank3, [(lambda h: Nn[:, ni, h, :], lambda h: Qq[:, qi_, h, :])], [Qq_b[qi_], Nn_b[ni]])
                P.op("dve", lambda e, nq_=nq_, qo=qi_, i=bank3[0]: e.tensor_tensor(out=Qq[:, nq_], in0=pv(i), in1=Qq[:, qo],
                                                                                   op=ALU.add),
                     reads=[bank3[1], Qq_b[qi_]], writes=[Qq_b[nq_]])
                qi_ = nq_
                yield
            bank = pr.next()
            mm8(bank, [(lambda h: Atok[:, c, h, :], lambda h: Qq[:, qi_, h, :])], [Atok_b[c], Qq_b[qi_]])
            P.op("act", lambda e, i=bank[0]: e.activation(out=WT[:, c], in_=pv(i), func=AF.Copy),
                 reads=[bank[1]], writes=[WT_b[c]])
            bank = pr.next()
            mm8(bank, [(lambda h: Lak[:, c, h, :], lambda h: Vtok[:, c, h, :])], [Lak_b[c], Vtok_b[c]])
            P.op("dve", lambda e, i=bank[0]: e.tensor_copy(out=Xx[:, c], in_=pv(i)), reads=[bank[1]], writes=[Xx_b[c]])
            out["q"] = qi_
            yield

        def chain(n, c, qi_):
            par = n % 2
            X = {k: X2[k][:, par] for k in X2}
            XB = {k: X2B[k][par] for k in X2}
            cs = slice(c * 64, (c + 1) * 64)
            bank = pr.next()
            mm8(bank, [(lambda h: WT[:, c, h, :], lambda h: Sb[:, h, :]),
                       (lambda h: Qq[:, qi_, h, :], lambda h: Xx[:, c, h, :])], [WT_b[c], S_b, Qq_b[qi_], Xx_b[c]])
            P.op("act", lambda e, i=bank[0]: e.activation(out=Uu[:, c], in_=pv(i), func=AF.Copy),
                 reads=[bank[1]], writes=[Uu_b[c]])
            banky = pr.next()
            mm8(banky, [(lambda h: X["rT"][:, h, cs], lambda h: Sb[:, h, :]),
                        (lambda h: Mrb[:, c, h, :], lambda h: Uu[:, c, h, :]),
                        (lambda h: Mrk[:, c, h, :], lambda h: Vtok[:, c, h, :])],
                [XB["rT"], S_b, Mrb_b[c], Uu_b[c], Mrk_b[c], Vtok_b[c]])
            banks = pr.next()
            mm8(banks, [(lambda h: BHtok[:, c, h, :], lambda h: Uu[:, c, h, :]),
                        (lambda h: KHtok[:, c, h, :], lambda h: Vtok[:, c, h, :])], [BHtok_b[c], Uu_b[c], KHtok_b[c], Vtok_b[c]])
            P.op("dve", lambda e: e.tensor_tensor(out=St[:], in0=Sf[:],
                                                  in1=gC[:, par, :, c:c + 1].broadcast_to([64, 8, 64]), op=ALU.mult),
                 reads=[S_b, gC_bs[par]], writes=[St_b])
            P.op("dve", lambda e, i=banks[0]: e.tensor_tensor(out=Sf[:], in0=pv(i), in1=St[:], op=ALU.add),
                 reads=[banks[1], St_b], writes=[S_b])
            P.op("act", lambda e: e.activation(out=Sb[:], in_=Sf[:], func=AF.Copy), reads=[S_b], writes=[S_b])
            yield
            y_b, yq_b, g_b = Ys_b[0], St_b, gst_b[c]
            P.op("act", lambda e, i=banky[0]: e.activation(out=Ys[:, 0], in_=pv(i), func=AF.Copy),
                 reads=[banky[1]], writes=[y_b])
            P.op("pool", lambda e: e.tensor_tensor(out=St[:], in0=Ys[:, 0], in1=Ys[:, 0], op=ALU.mult),
                 reads=[y_b], writes=[yq_b])
            P.op("dve", lambda e: e.tensor_reduce(out=gst[:, c, :, 0], in_=Ys[:, 0], axis=AX.X, op=ALU.add),
                 reads=[y_b], writes=[g_b])
            P.op("dve", lambda e: e.tensor_reduce(out=gst[:, c, :, 1], in_=St[:], axis=AX.X, op=ALU.add),
                 reads=[yq_b, g_b], writes=[g_b])
            yield
            P.op("dve", lambda e: e.tensor_scalar(out=gst[:, c, :, 2], in0=gst[:, c, :, 0], scalar1=1.0 / 64,
                                                  scalar2=None, op0=ALU.mult), reads=[g_b], writes=[g_b])
            P.op("dve", lambda e: e.tensor_tensor(out=gst[:, c, :, 3], in0=gst[:, c, :, 2], in1=gst[:, c, :, 2],
                                                  op=ALU.mult), reads=[g_b], writes=[g_b])
            P.op("dve", lambda e: e.scalar_tensor_tensor(out=gst[:, c, :, 4], in0=gst[:, c, :, 1], scalar=1.0 / 64,
                                                         in1=gst[:, c, :, 3], op0=ALU.mult, op1=ALU.subtract),
                 reads=[g_b], writes=[g_b])
            yield
            P.op("act", lambda e: e.activation(out=gst[:, c, :, 5], in_=gst[:, c, :, 4], func=AF.Sqrt,
                                               bias=C.eps[0:64, 1:2]), reads=[g_b, C.b], writes=[g_b])
            P.op("dve", lambda e: e.reciprocal(out=gst[:, c, :, 5], in_=gst[:, c, :, 5]), reads=[g_b], writes=[g_b])
            P.op("dve", lambda e: e.tensor_tensor(out=Ys[:, 0], in0=Ys[:, 0],
                                                  in1=gst[:, c, :, 2:3].broadcast_to([64, 8, 64]),
                                                  op=ALU.subtract), reads=[y_b, g_b], writes=[y_b])
            yield
            P.op("dve", lambda e: e.tensor_tensor(out=Yn[:, c], in0=Ys[:, 0],
                                                  in1=gst[:, c, :, 5:6].broadcast_to([64, 8, 64]),
                                                  op=ALU.mult), reads=[y_b, g_b], writes=[Yn_b[c]])
            bank = pr.next()
            tr8(bank, lambda h: Yn[:, c, h, :], [Yn_b[c]])
            bc64 = lambda col: col[:, :].unsqueeze(2).broadcast_to([64, 8, 64])
            P.op("dve", lambda e, i=bank[0]: e.tensor_tensor(out=Ys[:, 0], in0=pvb(i), in1=bc64(clw), op=ALU.mult),
                 reads=[bank[1], b_lw, Yn_b[c]], writes=[y_b])
            P.op("pool", lambda e: e.tensor_tensor(out=Ys[:, 0], in0=Ys[:, 0], in1=bc64(clb), op=ALU.add),
                 reads=[y_b, b_lb], writes=[y_b])
            yield
            P.op("dve", lambda e: e.tensor_tensor(out=Ys[:, 0], in0=Ys[:, 0], in1=X["bon"][:, :, cs], op=ALU.add),
                 reads=[y_b, XB["bon"]], writes=[y_b])
            P.op("dve", lambda e: e.tensor_tensor(out=ost[:, :, cs], in0=Ys[:, 0], in1=X["g"][:, :, cs], op=ALU.mult),
                 reads=[y_b, XB["g"]], writes=[ost_b])
            yield

        def scan(n):
            par = n % 2
            X = {k: X2[k][:, par] for k in X2}
            XB = {k: X2B[k][par] for k in X2}
            outs = [{} for _ in range(TL // 64)]
            gens = [chunk_pre(n, c, outs[c]) for c in range(TL // 64)]
            alive = [True] * len(gens)
            while any(alive):
                for gi_, g_ in enumerate(gens):
                    if alive[gi_]:
                        try:
                            next(g_)
                        except StopIteration:
                            alive[gi_] = False
                yield
            for c in range(TL // 64):
                for _ in chain(n, c, outs[c]["q"]):
                    yield
            P.dma("sp", lambda e: e.dma_start(
                out=yr_dram.rearrange("(h v) s -> v h s", v=64)[:, :, n * TL:(n + 1) * TL], in_=ost[:]),
                reads=[ost_b])
            yield

        def run2(f, b):
            fa, ba = f is not None, b is not None
            while fa or ba:
                if fa:
                    try:
                        next(f)
                    except StopIteration:
                        fa = False
                if ba:
                    try:
                        next(b)
                    except StopIteration:
                        ba = False

        for n in range(NB + 1):
            run2(prep(n) if n < NB else None, scan(n - 1) if n >= 1 else None)
    P.barrier()


NITER = 16


def phase_attn(P, nc, C, A, ya_dram):
    x_dram = A["x"]
    with ExitStack() as es:
        sb = lambda name, shape, dt=F32: es.enter_context(nc.sbuf_tensor(name, shape, dt))
        wa = sb("wa", [128, 8, ATTN_IN + 64], BF16)
        wa_b = Buf()
        load_weight_bf16(P, nc, wa, wa_b, A["w_in"], 0, ATTN_IN, 8)
        wv_ = A["w_in"].rearrange("(kc p) n -> p kc n", p=128)
        for kc in range(8):
            P.dma("pool", lambda e, kc=kc: e.dma_start(out=wa[:, kc, ATTN_IN:ATTN_IN + 64], in_=wv_[:, kc, 2048:2112]),
                  writes=[wa_b])
        kT = sb("kT", [128, 4, S], BF16)
        kiT = sb("kiT", [128, S], BF16)
        vaug = sb("vaug", [128, NT, 8, 65], BF16)
        kT_bs = [Buf() for _ in range(NT)]
        kiT_bs = [Buf() for _ in range(NT)]
        va_bs = [Buf() for _ in range(NT)]
        P.op("pool", lambda e: e.memset(vaug[:, :, :, 64:65], 1.0), writes=va_bs)
        gqk = sb("gqk", [128, 2], F32)
        gqk_b = Buf()
        for half in range(2):
            P.dma("sp", lambda e, half=half: e.dma_start(out=gqk[half * 64:(half + 1) * 64, 0:1],
                                                         in_=A["attn_q_norm"].rearrange("o d -> d o"),
                                                         allow_slow_non_contiguous=True), writes=[gqk_b])
            P.dma("sp", lambda e, half=half: e.dma_start(out=gqk[half * 64:(half + 1) * 64, 1:2],
                                                         in_=A["attn_k_norm"].rearrange("o d -> d o"),
                                                         allow_slow_non_contiguous=True), writes=[gqk_b])
        P.op("dve", lambda e: e.tensor_scalar(out=gqk[:, 0:1], in0=gqk[:, 0:1], scalar1=0.125, scalar2=None, op0=ALU.mult),
             reads=[gqk_b], writes=[gqk_b])
        btf = sb("btf", [128, 8, 2, 128], F32)
        bt = sb("bt", [128, 8, 2, 128], BF16)
        b31 = sb("b31_sb", [128, 8], F32)
        bt_b = Buf()
        P.dma("sp", lambda e: e.dma_start(out=btf[:], in_=A["bias_tiles"].rearrange("h c s t -> s h c t")), writes=[bt_b])
        P.dma("sp", lambda e: e.dma_start(out=b31[:], in_=A["b31"].partition_broadcast(128)), writes=[bt_b])
        P.op("dve", lambda e: e.tensor_tensor(out=bt[:].rearrange("p h c t -> p h (c t)"),
                                              in0=btf[:].rearrange("p h c t -> p h (c t)"),
                                              in1=b31[:, :].unsqueeze(2).broadcast_to([128, 8, 256]), op=ALU.subtract),
             reads=[bt_b], writes=[bt_b])
        cmask = sb("cmask", [128, 128], F32)
        onesblk = sb("onesblk", [128, 128], BF16)
        cm_b = Buf()
        pw2 = sb("pw2", [128, 2 * NITER], F32)
        halfs = sb("halfs", [128, 2 * NITER], F32)
        hf_b = Buf()

        def mkc(e):
            e.memset(cmask[:], 0.0)
            e.affine_select(out=cmask[:], in_=cmask[:], pattern=[[-1, 128]], compare_op=ALU.is_ge, fill=NEG, base=0,
                            channel_multiplier=1)
            for j in range(NITER):
                e.memset(pw2[:, j:j + 1], 0.5 ** (j + 1))
                e.memset(pw2[:, NITER + j:NITER + j + 1], 0.5 ** (j + 2))
            e.memset(onesblk[:], 0.0)
            e.memset(onesblk[0:64, 0:64], 1.0)
            return e.memset(onesblk[64:128, 64:128], 1.0)

        P.op("pool", mkc, writes=[cm_b])
        nm = Normer(P, nc, es, C, A["mix_norm"], "an", nslots=1)
        xbuf = sb("axbuf", [128, 2, D], F32)
        xr = Ring([(i, Buf()) for i in range(2)])
        xn = sb("axn", [128, 8, 128], BF16)
        xn_b = Buf()
        q_i = sb("q_i", [128, 2, 4, 128], BF16)
        qi_i = sb("qi_i", [128, 4, 128], BF16)
        wi_i = sb("wi_i", [128, 8], F32)
        q_bs, qi_b, wi_b = [Buf(), Buf()], Buf(), Buf()
        sq = sb("asq", [128, 512], BF16)
        rn = sb("arn", [128, 512], F32)
        sq_b, rn_b = Buf(), Buf()
        sc = sb("sc", [128, S], F32)
        sc_b = Buf()
        rbuf = sb("rbuf", [128, 2, 512], F32)
        rr = Ring([(i, Buf()) for i in range(2)])
        junk = sb("ajunk", [128, S], BF16)
        junk_b = Buf()
        bs = sb("bs", [128, 8], F32)
        bs_b = Buf()
        maskT = sb("maskT", [128, 2, NT, 128], BF16)
        mT_bs = [Buf(), Buf()]
        Et = sb("Et", [128, 3, 4, 128], BF16)
        er = Ring([(i, Buf()) for i in range(3)])
        Pt = sb("Pt", [128, 4, 4, 128], BF16)
        ptr = Ring([(i, Buf()) for i in range(4)])
        rden = sb("rden", [128, 2], F32)
        rdr = Ring([(i, Buf()) for i in range(2)])
        ytile = sb("ytile", [128, 512], BF16)
        yt_b = Buf()
        yst = sb("yst", [128, 4, 128], BF16)
        ys_b = Buf()
        pg = es.enter_context(nc.psum_tensor("apg", [128, 6, 512], F32))
        gr = Ring([(i, Buf()) for i in range(2)])
        accr = Ring([(i, Buf()) for i in range(2, 4)])
        lgr = Ring([(i, Buf()) for i in range(4, 6)])
        po = es.enter_context(nc.psum_tensor("apo", [128, 1, 512], F32))
        orr = Ring([(i, Buf()) for i in range(1)])
        pgb = lambda i: pg[:, i, :].bitcast(BF16)
        if SBUF_DEBUG:
            print("attn sbuf remaining", nc.sbuf_bytes_remaining)
        xview = x_dram.rearrange("(t p) d -> t p d", p=128)
        yav = ya_dram.rearrange("(m p) s -> p m s", p=128)

        def front(i):
            ts_ = slice(i * 128, (i + 1) * 128)
            nkb = i + 1
            W = nkb * 128
            q_b = q_bs[i % 2]
            kT_b, kiT_b, va_b = kT_bs[i], kiT_bs[i], va_bs[i]
            xi, xb_ = xr.next()
            P.dma("sp", lambda e, i=i, xi=xi: e.dma_start(out=xbuf[:, xi, :], in_=xview[i]), writes=[xb_])
            nm.run(xbuf[:, xi, :], xb_, xn, xn_b, 0)
            yield

            def proj4(c0, bank):
                bi, bb = bank

                def th(e):
                    ins = None
                    for m in range(4):
                        for kc in range(8):
                            ins = e.matmul(pg[:, bi, m * 128:(m + 1) * 128], lhsT=wa[:, kc, c0 + m * 128:c0 + (m + 1) * 128],
                                           rhs=xn[:, kc, :], start=(kc == 0), stop=(kc == 7))
                    return ins

                P.op("pe", th, reads=[wa_b, xn_b], writes=[bb])

            for which, c0 in ((0, 0), (1, 512)):
                bank = gr.next()
                proj4(c0, bank)
                P.op("act", lambda e, bi=bank[0]: e.activation(out=sq[:], in_=pg[:, bi, :], func=AF.Square),
                     reads=[bank[1]], writes=[sq_b])
                bank2 = gr.next()
                P.op("pe", lambda e, bi=bank2[0]: e.matmul(pg[:, bi, :], lhsT=onesblk[:], rhs=sq[:], start=True, stop=True),
                     reads=[cm_b, sq_b], writes=[bank2[1]])
                P.op("act", lambda e, bi=bank2[0]: e.activation(out=rn[:], in_=pg[:, bi, :], func=AF.Ln, scale=1.0 / 64,
                                                                bias=C.eps[:, 0:1]), reads=[bank2[1], C.b], writes=[rn_b])
                P.op("act", lambda e: e.activation(out=rn[:], in_=rn[:], func=AF.Exp, scale=-0.5), reads=[rn_b], writes=[rn_b])
                if which == 0:
                    P.op("dve", lambda e, bi=bank[0]: e.scalar_tensor_tensor(
                        out=q_i[:, i % 2].rearrange("p m t -> p (m t)"), in0=pg[:, bi, :], scalar=gqk[:, 0:1], in1=rn[:],
                        op0=ALU.mult, op1=ALU.mult), reads=[bank[1], rn_b, gqk_b], writes=[q_b])
                else:
                    P.op("dve", lambda e, bi=bank[0], ts_=ts_: e.scalar_tensor_tensor(
                        out=kT[:, :, ts_], in0=pg[:, bi, :].rearrange("p (m t) -> p m t", m=4), scalar=gqk[:, 1:2],
                        in1=rn[:].rearrange("p (m t) -> p m t", m=4), op0=ALU.mult, op1=ALU.mult),
                        reads=[bank[1], rn_b, gqk_b], writes=[kT_b])
                yield
            bank = gr.next()
            proj4(1536, bank)
            P.op("act", lambda e, bi=bank[0]: e.activation(out=qi_i[:].rearrange("p m t -> p (m t)"), in_=pg[:, bi, :],
                                                           func=AF.Copy), reads=[bank[1]], writes=[qi_b])
            yield
            bank = gr.next()

            def kiw(e, bi=bank[0]):
                for kc in range(8):
                    e.matmul(pg[0:64, bi, 0:128], lhsT=wa[:, kc, 2048:2112], rhs=xn[:, kc, :], start=(kc == 0), stop=(kc == 7))
                for kc in range(8):
                    e.matmul(pg[64:128, bi, 0:128], lhsT=wa[:, kc, ATTN_IN:ATTN_IN + 64], rhs=xn[:, kc, :], start=(kc == 0),
                             stop=(kc == 7))
                ins = None
                for kc in range(8):
                    ins = e.matmul(pg[:, bi, 128:136], lhsT=xn[:, kc, :], rhs=wa[:, kc, 2112:2120], start=(kc == 0), stop=(kc == 7))
                return ins

            P.op("pe", kiw, reads=[wa_b, xn_b], writes=[bank[1]])
            P.op("act", lambda e, bi=bank[0], ts_=ts_: e.activation(out=kiT[:, ts_], in_=pg[:, bi, 0:128], func=AF.Copy),
                 reads=[bank[1]], writes=[kiT_b])
            P.op("dve", lambda e, bi=bank[0]: e.tensor_copy(out=wi_i[:], in_=pg[:, bi, 128:136]), reads=[bank[1]], writes=[wi_b])
            yield
            bank = gr.next()

            def vmm(e, bi=bank[0]):
                ins = None
                for kc in range(8):
                    ins = e.matmul(pg[:, bi, :], lhsT=xn[:, kc, :], rhs=wa[:, kc, 1024:1536], start=(kc == 0), stop=(kc == 7))
                return ins

            P.op("pe", vmm, reads=[wa_b, xn_b], writes=[bank[1]])
            P.op("act", lambda e, bi=bank[0], i=i: e.activation(out=vaug[:, i, :, 0:64],
                                                                in_=pg[:, bi, :].rearrange("p (h d) -> p h d", h=8), func=AF.Copy),
                 reads=[bank[1]], writes=[va_b])
            yield

            for gk in range((nkb + 3) // 4):
                w_ = min(512, W - gk * 512)
                abank = accr.next()
                for h in range(8):
                    hb = (h % 2) * 64
                    bank = gr.next()
                    P.op("pe", lambda e, bi=bank[0], h=h, hb=hb, gk=gk, w_=w_: e.matmul(
                        pg[:, bi, 0:w_], lhsT=qi_i[hb:hb + 64, h // 2, :], rhs=kiT[hb:hb + 64, gk * 512:gk * 512 + w_],
                        start=True, stop=True), reads=[qi_b] + kiT_bs[gk * 4:gk * 4 + (w_ // 128)], writes=[bank[1]])
                    ri, rb_ = rr.next()
                    P.op("act", lambda e, bi=bank[0], ri=ri, w_=w_: e.activation(out=rbuf[:, ri, 0:w_], in_=pg[:, bi, 0:w_],
                                                                                 func=AF.Relu), reads=[bank[1]], writes=[rb_])
                    if h == 0:
                        P.op("dve", lambda e, ri=ri, ai=abank[0], w_=w_: e.tensor_scalar(
                            out=pg[:, ai, 0:w_], in0=rbuf[:, ri, 0:w_], scalar1=wi_i[:, 0:1], scalar2=None,
                            op0=ALU.mult), reads=[rb_, wi_b], writes=[abank[1]])
                    elif h < 7:
                        P.op("dve", lambda e, ri=ri, ai=abank[0], w_=w_, h=h: e.scalar_tensor_tensor(
                            out=pg[:, ai, 0:w_], in0=rbuf[:, ri, 0:w_], scalar=wi_i[:, h:h + 1],
                            in1=pg[:, ai, 0:w_], op0=ALU.mult, op1=ALU.add), reads=[rb_, wi_b, abank[1]], writes=[abank[1]])
                    else:
                        P.op("dve", lambda e, ri=ri, ai=abank[0], gk=gk, w_=w_, h=h: e.scalar_tensor_tensor(
                            out=sc[:, gk * 512:gk * 512 + w_], in0=rbuf[:, ri, 0:w_], scalar=wi_i[:, h:h + 1],
                            in1=pg[:, ai, 0:w_], op0=ALU.mult, op1=ALU.add), reads=[rb_, wi_b, abank[1]], writes=[sc_b])
                    yield
            P.op("dve", lambda e, ts_=ts_: e.tensor_tensor(out=sc[:, ts_], in0=sc[:, ts_], in1=cmask[:], op=ALU.add),
                 reads=[sc_b, cm_b], writes=[sc_b])
            if i < 2:
                P.op("dve", lambda e: e.memset(bs[:, 0:1], -1.0e29), writes=[bs_b])
            else:
                P.op("dve", lambda e, i=i: e.tensor_reduce(out=bs[:, 0:1], in_=sc[:, 0:i * 128], axis=AX.X, op=ALU.min),
                     reads=[sc_b], writes=[bs_b])
                P.op("dve", lambda e, W=W: e.tensor_reduce(out=bs[:, 6:7], in_=sc[:, 0:W], axis=AX.X, op=ALU.max),
                     reads=[sc_b, bs_b], writes=[bs_b])
                P.op("dve", lambda e: e.tensor_tensor(out=bs[:, 1:2], in0=bs[:, 6:7], in1=bs[:, 0:1], op=ALU.subtract),
                     reads=[bs_b], writes=[bs_b])
                P.op("dve", lambda e: e.tensor_tensor(out=halfs[:], in0=bs[:, 1:2].broadcast_to([128, 2 * NITER]), in1=pw2[:],
                                                      op=ALU.mult), reads=[bs_b, cm_b], writes=[hf_b])
                P.op("dve", lambda e: e.tensor_tensor(out=bs[:, 3:4], in0=bs[:, 0:1], in1=halfs[:, 0:1], op=ALU.add),
                     reads=[bs_b, hf_b], writes=[bs_b])
                nit = min(NITER, int(math.ceil(math.log2(W))) + 4)
                for it in range(nit):
                    P.op("dve", lambda e: e.tensor_scalar(out=junk[:, 0:W], in0=sc[:, 0:W], scalar1=bs[:, 3:4], scalar2=None,
                                                          op0=ALU.is_ge, op1=ALU.add, accum_out=bs[:, 4:5]),
                         reads=[sc_b, bs_b], writes=[junk_b, bs_b])
                    P.op("dve", lambda e, it=it: e.scalar_tensor_tensor(out=bs[:, 5:6], in0=bs[:, 4:5], scalar=TOPK - 0.5,
                                                                        in1=halfs[:, it:it + 1], op0=ALU.is_ge, op1=ALU.mult),
                         reads=[bs_b, hf_b], writes=[bs_b])
                    P.op("dve", lambda e, it=it: e.scalar_tensor_tensor(out=bs[:, 3:4], in0=bs[:, 5:6],
                                                                        scalar=halfs[:, NITER + it:NITER + it + 1],
                                                                        in1=bs[:, 3:4], op0=ALU.subtract, op1=ALU.add),
                         reads=[bs_b, hf_b], writes=[bs_b])
                    yield
                P.op("dve", lambda e, nit=nit: e.tensor_tensor(out=bs[:, 0:1], in0=bs[:, 3:4], in1=halfs[:, NITER + nit - 1:NITER + nit],
                                                               op=ALU.subtract), reads=[bs_b, hf_b], writes=[bs_b])
            P.op("dve", lambda e, W=W: e.tensor_scalar(out=junk[:, 0:W], in0=sc[:, 0:W], scalar1=bs[:, 0:1], scalar2=None,
                                                       op0=ALU.is_ge), reads=[sc_b, bs_b], writes=[junk_b])
            for j0 in range(0, nkb, 8):
                nb = min(8, nkb - j0)
                bank = gr.next()

                def trm(e, bi=bank[0], j0=j0, nb=nb):
                    ins = None
                    for jj in range(nb):
                        ins = e.transpose(out=pgb(bi)[:, jj * 128:(jj + 1) * 128], in_=junk[:, (j0 + jj) * 128:(j0 + jj + 1) * 128],
                                          identity=C.ident_bf[:])
                    return ins

                P.op("pe", trm, reads=[junk_b, C.b], writes=[bank[1]])
                P.op("act", lambda e, bi=bank[0], j0=j0, nb=nb: e.activation(
                    out=maskT[:, i % 2, j0:j0 + nb, :].rearrange("p j t -> p (j t)"), in_=pgb(bi)[:, 0:nb * 128], func=AF.Copy),
                    reads=[bank[1]], writes=[mT_bs[i % 2]])
                yield
            yield

        def back(i):
            ts_ = slice(i * 128, (i + 1) * 128)
            nkb = i + 1
            q_b = q_bs[i % 2]
            mT_b = mT_bs[i % 2]
            items = [(h, j0, min(4, nkb - j0)) for h in range(8) for j0 in range(0, nkb, 4)]
            DEPTH = 2
            st = {}
            obs = {}
            for k in range(len(items) + DEPTH):
                if k < len(items):
                    h, j0, nb = items[k]
                    hb = (h % 2) * 64
                    m = h // 2
                    if j0 == 0:
                        obs[h] = orr.next()
                    bank = lgr.next()

                    def qk(e, bi=bank[0], j0=j0, nb=nb, h=h, hb=hb, m=m):
                        ins = None
                        for jj in range(nb):
                            j = j0 + jj
                            near = j >= i - 1
                            ins = e.matmul(pg[:, bi, jj * 128:(jj + 1) * 128], lhsT=kT[hb:hb + 64, m, j * 128:(j + 1) * 128],
                                           rhs=q_i[hb:hb + 64, i % 2, m, :], start=True, stop=not near)
                            if near:
                                ins = e.matmul(pg[:, bi, jj * 128:(jj + 1) * 128], lhsT=C.ident_bf[:],
                                               rhs=bt[:, h, 0 if j == i else 1, :], start=False, stop=True)
                        return ins

                    P.op("pe", qk, reads=kT_bs[j0:j0 + nb] + [q_b, bt_b, C.b], writes=[bank[1]])
                    ei, eb = er.next()
                    P.op("act", lambda e, bi=bank[0], ei=ei, nb=nb: e.activation(
                        out=Et[:, ei, 0:nb, :].rearrange("p j t -> p (j t)"), in_=pg[:, bi, 0:nb * 128], func=AF.Exp),
                        reads=[bank[1]], writes=[eb])
                    pi, pb_ = ptr.next()
                    P.op("pool", lambda e, ei=ei, pi=pi, j0=j0, nb=nb: e.tensor_tensor(
                        out=Pt[:, pi, 0:nb, :], in0=Et[:, ei, 0:nb, :], in1=maskT[:, i % 2, j0:j0 + nb, :], op=ALU.mult),
                        reads=[eb, mT_b], writes=[pb_])
                    st[k] = (pi, pb_)
                kk = k - DEPTH
                if kk >= 0:
                    h, j0, nb = items[kk]
                    pi, pb_ = st.pop(kk)
                    ob = obs[h]

                    def pv(e, oi=ob[0], pi=pi, j0=j0, nb=nb, h=h):
                        ins = None
                        for jj in range(nb):
                            j = j0 + jj
                            ins = e.matmul(po[:, oi, 0:65], lhsT=Pt[:, pi, jj, :], rhs=vaug[:, j, h, :], start=(j == 0),
                                           stop=(j == i))
                        return ins

                    P.op("pe", pv, reads=[pb_] + va_bs[j0:j0 + nb], writes=[ob[1]])
                    if j0 + nb == nkb:
                        di, db = rdr.next()
                        P.op("dve", lambda e, oi=ob[0], di=di: e.reciprocal(out=rden[:, di:di + 1], in_=po[:, oi, 64:65]),
                             reads=[ob[1]], writes=[db])
                        P.op("act", lambda e, oi=ob[0], di=di, h=h: e.activation(out=ytile[:, h * 64:(h + 1) * 64],
                                                                                 in_=po[:, oi, 0:64], func=AF.Copy,
                                                                                 scale=rden[:, di:di + 1]),
                             reads=[ob[1], db], writes=[yt_b])
                yield
            bank = lgr.next()

            def try_(e, bi=bank[0]):
                ins = None
                for m in range(4):
                    ins = e.transpose(out=pgb(bi)[:, m * 128:(m + 1) * 128], in_=ytile[:, m * 128:(m + 1) * 128],
                                      identity=C.ident_bf[:])
                return ins

            P.op("pe", try_, reads=[yt_b, C.b], writes=[bank[1]])
            P.op("act", lambda e, bi=bank[0]: e.activation(out=yst[:].rearrange("p m t -> p (m t)"), in_=pgb(bi)[:, 0:512],
                                                           func=AF.Copy), reads=[bank[1]], writes=[ys_b])
            P.dma("sp", lambda e: e.dma_start(out=yav[:, :, ts_], in_=yst[:]), reads=[ys_b])
            yield

        def run2(f, b):
            fa, ba = f is not None, b is not None
            while fa or ba:
                if fa:
                    try:
                        next(f)
                    except StopIteration:
                        fa = False
                if ba:
                    try:
                        next(b)
                    except StopIteration:
                        ba = False

        for i in range(NT + 1):
            run2(front(i) if i < NT else None, back(i - 1) if i >= 1 else None)
    P.barrier()


WEIGHT_SPECS = [
    ("mix_norm", [1, D]), ("w_in", [D, 5992]), ("attn_q_norm", [1, 64]), ("attn_k_norm", [1, 64]),
    ("bias_tiles", [8, 2, 128, 128]), ("b31", [1, 8]), ("rwkv_mu", [1, RWKV_IN]), ("rwkv_w0", [1, 512]), ("rwkv_w2", [64, 512]),
    ("rwkv_a0", [1, 512]), ("rwkv_a2", [64, 512]), ("rwkv_g2", [160, 512]), ("rwkv_k_k", [1, 512]),
    ("rwkv_k_a", [1, 512]), ("rwkv_r_k", [1, 512]), ("rwkv_ln_w", [1, 512]), ("rwkv_ln_b", [1, 512]),
    ("w_branch_attn", [512, D]), ("w_branch_rwkv", [512, D]), ("w_out", [D, D]), ("ffn_norm", [1, D]),
    ("w_gate_up", [D, 2 * FFN_H]), ("w_down", [FFN_H, D]),
]


def build_program(phases=("attn", "rwkv", "merge", "ffn"), debug=False):
    nc = bass.Bass("TRN2", target_bir_lowering=False)
    A = {}
    A["x"] = nc.dram_tensor("x", [S, D], F32, kind="ExternalInput").ap()
    for name, shp in WEIGHT_SPECS:
        A[name] = nc.dram_tensor(name, shp, F32, kind="ExternalInput").ap()
    out = nc.dram_tensor("out", [S, D], F32, kind="ExternalOutput").ap()
    def kind(prod, cons):
        if not debug:
            return "Internal"
        if prod in phases and cons not in phases:
            return "ExternalOutput"
        if prod not in phases and cons in phases:
            return "ExternalInput"
        return "Internal"

    ya = nc.dram_tensor("ya_scr", [512, S], BF16, kind=kind("attn", "merge")).ap()
    yr = nc.dram_tensor("yr_scr", [512, S], BF16, kind=kind("rwkv", "merge")).ap()
    hs = nc.dram_tensor("h_scr", [S, D], F32, kind=kind("merge", "ffn")).ap()
    P = Prog(nc)
    final_ops = []
    with ExitStack() as es:
        C = make_consts(P, nc, es)
        P.barrier()
        if "attn" in phases:
            phase_attn(P, nc, C, A, ya)
        if "rwkv" in phases:
            phase_rwkv(P, nc, C, A, yr)
        if "merge" in phases:
            phase_merge(P, nc, C, A["x"], ya, yr, hs, A["mix_norm"], A["w_in"], A["w_branch_attn"],
                        A["w_branch_rwkv"], A["w_out"])
        if "ffn" in phases:
            phase_ffn(P, nc, C, hs, out, A["ffn_norm"], A["w_gate_up"], A["w_down"], final_ops)
        P.emit(final_wait_ops=final_ops)
    return nc


def t5_bucket_np(d):
    d = np.maximum(d, 0)
    max_exact = 16
    log_ratio = np.log(np.maximum(d, 1).astype(np.float32) / max_exact) / math.log(128 / max_exact)
    large = np.minimum(max_exact + (log_ratio * 16).astype(np.int32), 31)
    return np.where(d < max_exact, d, large)


def host_layout(inputs):
    w = {}
    for name, shp in WEIGHT_SPECS:
        if name in ("bias_tiles", "b31"):
            continue
        w[name] = np.ascontiguousarray(np.asarray(inputs[name], dtype=np.float32).reshape(shp))
    s_idx = np.arange(128)[:, None]
    t_idx = np.arange(128)[None, :]
    rb = np.asarray(inputs["rel_bias"], dtype=np.float32)
    tiles = np.empty((8, 2, 128, 128), np.float32)
    for cls in range(2):
        bk = t5_bucket_np(t_idx - s_idx + 128 * cls)
        tiles[:, cls] = np.transpose(rb[bk], (2, 0, 1))
    w["bias_tiles"] = tiles
    w["b31"] = np.ascontiguousarray(rb[31:32, :])
    return w


_NC_CACHE = {}


def kernel(**inputs):
    x = np.asarray(inputs["x"], dtype=np.float32)
    w = host_layout(inputs)
    if "nc" not in _NC_CACHE:
        _NC_CACHE["nc"] = build_program()
    nc = _NC_CACHE["nc"]
    in_maps = []
    for b in range(8):
        m = dict(w)
        m["x"] = np.ascontiguousarray(x[b])
        in_maps.append(m)
    res = run_bass_kernel_spmd(nc, in_maps, core_ids=list(range(8)))
    return np.stack([np.asarray(r["out"], dtype=np.float32) for r in res.results], axis=0)
```

```python
import math
from contextlib import ExitStack

import numpy as np
import concourse.bass as bass
import concourse.mybir as mybir
from concourse.bass_utils import run_bass_kernel_spmd

F32 = mybir.dt.float32
BF16 = mybir.dt.bfloat16
AF = mybir.ActivationFunctionType
ALU = mybir.AluOpType
AX = mybir.AxisListType

S = 4096
D = 1024
NT = S // 128
ATTN_IN = 2120
RWKV_IN = 1824
FFN_H = 2816
RMS_EPS = 1e-6
GN_EPS = 64e-5
TOPK = 256
NEG = -1.0e30

ENGS = ("pe", "act", "dve", "pool", "sp")
SBUF_DEBUG = False


class Buf:
    __slots__ = ("name", "w", "r")

    def __init__(self, name=""):
        self.name = name
        self.w = None
        self.r = []


class Op:
    __slots__ = ("eng", "thunk", "deps", "is_dma", "sem", "val", "need_inc", "pos", "prev_on_sem")


class Prog:
    def __init__(self, nc, n_dma_sems=(56, 28)):
        self.nc = nc
        self.streams = {e: [] for e in ENGS}
        self.n_dma_sems = n_dma_sems
        self.dma_count = 0
        self.all_ops = []
        self.open_dmas = []

    def _hazards(self, op, reads, writes):
        deps = []
        for b in reads:
            if b.w is not None:
                deps.append(b.w)
        for b in writes:
            if b.w is not None:
                deps.append(b.w)
            deps.extend(b.r)
        for b in reads:
            b.r.append(op)
        for b in writes:
            b.w = op
            b.r = []
        return deps

    def op(self, eng, thunk, reads=(), writes=(), extra_deps=()):
        o = Op()
        o.eng = eng
        o.thunk = thunk
        o.is_dma = False
        o.need_inc = False
        o.sem = None
        o.val = None
        o.prev_on_sem = None
        deps = self._hazards(o, reads, writes) + list(extra_deps)
        seen = set()
        o.deps = []
        for d in deps:
            if d is o or id(d) in seen:
                continue
            if eng == "pe" and d.eng == "pe" and not d.is_dma:
                continue
            seen.add(id(d))
            o.deps.append(d)
        self.streams[eng].append(o)
        self.all_ops.append(o)
        return o

    def dma(self, eng, thunk, reads=(), writes=(), extra_deps=()):
        o = self.op(eng, thunk, reads, writes, extra_deps)
        o.is_dma = True
        o.pos = self.dma_count
        self.dma_count += 1
        self.open_dmas.append(o)
        return o

    def barrier(self):
        lasts = []
        for e in ENGS:
            for o in reversed(self.streams[e]):
                if not o.is_dma:
                    lasts.append(o)
                    break
        deps = lasts + self.open_dmas
        self.open_dmas = []
        for e in ENGS:
            self.op(e, lambda eng: eng.nop(), extra_deps=deps)

    def emit(self, final_wait_ops=()):
        nc = self.nc
        for o in self.all_ops:
            for d in o.deps:
                d.need_inc = True
        eng_sems = {e: nc.alloc_semaphore("s_" + e) for e in ENGS}
        ring_n = {"sp": self.n_dma_sems[0], "pool": self.n_dma_sems[1], "act": 2, "dve": 2, "pe": 2}
        dma_sems = {}
        dma_sem_val = {}
        dma_prev = {}
        qpos = {e: 0 for e in ENGS}
        for e in ENGS:
            if any(o.is_dma for o in self.streams[e]):
                for i in range(ring_n[e]):
                    dma_sems[(e, i)] = nc.alloc_semaphore("s_dma_%s%d" % (e, i))
                    dma_sem_val[(e, i)] = 0
                    dma_prev[(e, i)] = None
        cnt = {e: 0 for e in ENGS}
        for o in self.all_ops:
            if o.is_dma:
                kq = (o.eng, qpos[o.eng] % ring_n[o.eng])
                qpos[o.eng] += 1
                dma_sem_val[kq] += 16
                o.sem = ("dma", kq)
                o.val = dma_sem_val[kq]
                o.prev_on_sem = dma_prev[kq]
                dma_prev[kq] = o
            elif o.need_inc:
                cnt[o.eng] += 1
                o.sem = ("eng", o.eng)
                o.val = cnt[o.eng]

        def semh(key):
            return eng_sems[key[1]] if key[0] == "eng" else dma_sems[key[1]]

        engines = {"pe": "tensor", "act": "scalar", "dve": "vector", "pool": "gpsimd", "sp": "sync"}
        with nc.Block() as block:
            for e in ENGS:
                stream = self.streams[e]
                final = list(final_wait_ops) if e == "sp" else []

                def body(engine, stream=stream, final=final):
                    known = {}
                    for o in stream:
                        waits = {}
                        deps = list(o.deps)
                        if o.is_dma and o.prev_on_sem is not None:
                            deps.append(o.prev_on_sem)
                        for d in deps:
                            if known.get(d.sem, 0) >= d.val:
                                continue
                            if waits.get(d.sem, 0) < d.val:
                                waits[d.sem] = d.val
                        for key, val in waits.items():
                            engine.wait_ge(semh(key), val)
                            known[key] = val
                        ins = o.thunk(engine)
                        if o.is_dma:
                            ins.then_inc(semh(o.sem), 16)
                        elif o.need_inc:
                            ins.then_inc(semh(o.sem), 1)
                    for o in final:
                        engine.wait_ge(semh(o.sem), o.val)

                getattr(block, engines[e])(body)


class Ring:
    def __init__(self, items):
        self.items = items
        self.i = 0

    def next(self):
        it = self.items[self.i % len(self.items)]
        self.i += 1
        return it


def load_weight_bf16(P, nc, dst, dst_buf, w_ap, c0, c1, kchunks, eng="pool"):
    wv = w_ap.rearrange("(kc p) n -> p kc n", p=128)
    for kc in range(kchunks):
        for a in range(c0, c1, 2048):
            b = min(c1, a + 2048)
            P.dma(eng, lambda e, kc=kc, a=a, b=b: e.dma_start(out=dst[:, kc, a - c0:b - c0], in_=wv[:, kc, a:b]),
                  writes=[dst_buf])


def load_col_vec(P, nc, dst, dst_buf, v_ap, n):
    src = v_ap.rearrange("o (c p) -> p (o c)", p=128)
    P.dma("sp", lambda e: e.dma_start(out=dst, in_=src, allow_slow_non_contiguous=True), writes=[dst_buf])


class Consts:
    pass


def make_consts(P, nc, es):
    C = Consts()
    C.ident_bf = es.enter_context(nc.sbuf_tensor("ident_bf", [128, 128], BF16))
    C.ident_f = es.enter_context(nc.sbuf_tensor("ident_f", [128, 128], F32))
    C.eps = es.enter_context(nc.sbuf_tensor("eps_c", [128, 2], F32))
    C.b = Buf("consts")

    def mk(e):
        e.memset(C.ident_f[:], 0.0)
        e.affine_select(out=C.ident_f[:], in_=C.ident_f[:], pattern=[[-1, 128]], compare_op=ALU.not_equal,
                        fill=1.0, base=0, channel_multiplier=1)
        e.memset(C.eps[:, 0:1], RMS_EPS)
        return e.memset(C.eps[:, 1:2], GN_EPS)

    P.op("pool", mk, writes=[C.b])
    P.op("pool", lambda e: e.tensor_copy(out=C.ident_bf[:], in_=C.ident_f[:]), reads=[C.b], writes=[C.b])
    return C


class Normer:
    def __init__(self, P, nc, es, C, gain_ap, name, nslots=2):
        self.P, self.nc, self.C = P, nc, C
        self.gcol = es.enter_context(nc.sbuf_tensor(name + "_g", [128, 8], F32))
        self.gb = Buf(name + "_g")
        load_col_vec(P, nc, self.gcol[:, :], self.gb, gain_ap, 8)
        self.stat = es.enter_context(nc.sbuf_tensor(name + "_st", [128, nslots, 4], F32))
        self.junk = es.enter_context(nc.sbuf_tensor(name + "_junk", [128, 1024], BF16))
        self.xs = es.enter_context(nc.sbuf_tensor(name + "_xs", [128, nslots, 1024], BF16))
        self.tp = es.enter_context(nc.psum_tensor(name + "_tp", [128, nslots, 8, 128], BF16))
        self.ring = Ring([(i, Buf(), Buf(), Buf()) for i in range(nslots)])
        self.junkb = Buf()

    def run(self, xt_ap, xt_buf, dst, dst_buf, col0):
        P, C = self.P, self.C
        i, sb, xb, pb = self.ring.next()
        st = self.stat
        P.op("act", lambda e: e.activation(out=self.junk[:], in_=xt_ap, func=AF.Square, accum_out=st[:, i, 0:1]),
             reads=[xt_buf], writes=[self.junkb, sb])
        P.op("act", lambda e: e.activation(out=st[:, i, 1:2], in_=st[:, i, 0:1], func=AF.Ln, scale=1.0 / D,
                                           bias=C.eps[:, 0:1]), reads=[sb, C.b], writes=[sb])
        P.op("act", lambda e: e.activation(out=st[:, i, 2:3], in_=st[:, i, 1:2], func=AF.Exp, scale=-0.5),
             reads=[sb], writes=[sb])
        P.op("act", lambda e: e.activation(out=self.xs[:, i, :], in_=xt_ap, func=AF.Copy, scale=st[:, i, 2:3]),
             reads=[xt_buf, sb], writes=[xb])

        def tr(e):
            ins = None
            for kc in range(8):
                ins = e.transpose(out=self.tp[:, i, kc, :], in_=self.xs[:, i, kc * 128:(kc + 1) * 128],
                                  identity=C.ident_bf[:])
            return ins

        P.op("pe", tr, reads=[xb, C.b], writes=[pb])
        P.op("dve", lambda e: e.tensor_tensor(out=dst[:, :, col0:col0 + 128], in0=self.tp[:, i, :, :],
                                              in1=self.gcol[:, :].unsqueeze(2).broadcast_to([128, 8, 128]),
                                              op=ALU.mult), reads=[pb, self.gb], writes=[dst_buf])


def phase_ffn(P, nc, C, h_dram, out_dram, ffn_norm, w_gate_up, w_down, final_ops):
    with ExitStack() as es:
        wgu = es.enter_context(nc.sbuf_tensor("wgu", [128, 8, 2 * FFN_H], BF16))
        wd = es.enter_context(nc.sbuf_tensor("wd", [128, 22, D], BF16))
        wgu_b, wd_b = Buf("wgu"), Buf("wd")
        load_weight_bf16(P, nc, wgu, wgu_b, w_gate_up, 0, 2 * FFN_H, 8)
        load_weight_bf16(P, nc, wd, wd_b, w_down, 0, D, 22)
        nm = Normer(P, nc, es, C, ffn_norm, "fn")
        hbuf = es.enter_context(nc.sbuf_tensor("hbuf", [128, 2, D], F32))
        hr = Ring([(i, Buf()) for i in range(2)])
        hnT = es.enter_context(nc.sbuf_tensor("hnT", [128, 2, 8, 512], BF16))
        hnb = [Buf(), Buf()]
        actT = es.enter_context(nc.sbuf_tensor("actT", [128, 22, 512], BF16))
        actb = [Buf() for _ in range(22)]
        sg = es.enter_context(nc.sbuf_tensor("sg", [128, 2, 512], F32))
        sgr = Ring([(i, Buf()) for i in range(2)])
        ost = es.enter_context(nc.sbuf_tensor("ost", [128, 2, D], F32))
        ostr = Ring([(i, Buf()) for i in range(2)])
        pg = es.enter_context(nc.psum_tensor("pg", [128, 2, 512], F32))
        pu = es.enter_context(nc.psum_tensor("pu", [128, 2, 512], F32))
        po = es.enter_context(nc.psum_tensor("po", [128, 2, 512], F32))
        pgr = Ring([(i, Buf()) for i in range(2)])
        pur = Ring([(i, Buf()) for i in range(2)])
        por = Ring([(i, Buf()) for i in range(2)])
        hview = h_dram.rearrange("(t p) d -> t p d", p=128)
        oview = out_dram.rearrange("(t p) d -> t p d", p=128)
        def ffn_norm_chunk(c):
            cb = c % 2
            for j in range(4):
                t = c * 4 + j
                hi, hb_ = hr.next()
                P.dma("sp", lambda e, t=t, hi=hi: e.dma_start(out=hbuf[:, hi, :], in_=hview[t]), writes=[hb_])
                nm.run(hbuf[:, hi, :], hb_, hnT[:, cb], hnb[cb], j * 128)

        ffn_norm_chunk(0)
        for c in range(S // 512):
            cb = c % 2
            if c + 1 < S // 512:
                ffn_norm_chunk(c + 1)
            for m in range(22):
                gi, gbuf = pgr.next()
                ui, ubuf = pur.next()

                def mm(e, m=m, gi=gi, ui=ui, cb=cb):
                    for kc in range(8):
                        e.matmul(pg[:, gi, :], lhsT=wgu[:, kc, m * 128:(m + 1) * 128], rhs=hnT[:, cb, kc, :],
                                 start=(kc == 0), stop=(kc == 7))
                    ins = None
                    for kc in range(8):
                        ins = e.matmul(pu[:, ui, :], lhsT=wgu[:, kc, FFN_H + m * 128:FFN_H + (m + 1) * 128],
                                       rhs=hnT[:, cb, kc, :], start=(kc == 0), stop=(kc == 7))
                    return ins

                P.op("pe", mm, reads=[wgu_b, hnb[cb]], writes=[gbuf, ubuf])
                si, sbuf_ = sgr.next()
                P.op("act", lambda e, gi=gi, si=si: e.activation(out=sg[:, si, :], in_=pg[:, gi, :], func=AF.Silu),
                     reads=[gbuf], writes=[sbuf_])
                P.op("dve", lambda e, ui=ui, si=si, m=m: e.tensor_tensor(out=actT[:, m, :], in0=pu[:, ui, :],
                                                                         in1=sg[:, si, :], op=ALU.mult),
                     reads=[ubuf, sbuf_], writes=[actb[m]])
            for j in range(4):
                t = c * 4 + j
                oi, obuf = ostr.next()
                P.dma("sp", lambda e, t=t, oi=oi: e.dma_start(out=ost[:, oi, :], in_=hview[t]), writes=[obuf])
                for nh in range(2):
                    pi, pbuf = por.next()

                    def mmd(e, j=j, nh=nh, pi=pi):
                        ins = None
                        for m in range(22):
                            ins = e.matmul(po[:, pi, :], lhsT=actT[:, m, j * 128:(j + 1) * 128],
                                           rhs=wd[:, m, nh * 512:(nh + 1) * 512], start=(m == 0), stop=(m == 21))
                        return ins

                    P.op("pe", mmd, reads=actb + [wd_b], writes=[pbuf])
                    P.op("dve", lambda e, nh=nh, pi=pi, oi=oi: e.tensor_tensor(
                        out=ost[:, oi, nh * 512:(nh + 1) * 512], in0=po[:, pi, :],
                        in1=ost[:, oi, nh * 512:(nh + 1) * 512], op=ALU.add),
                        reads=[pbuf], writes=[obuf])
                final_ops.append(P.dma("sp", lambda e, t=t, oi=oi: e.dma_start(out=oview[t], in_=ost[:, oi, :]),
                                       reads=[obuf]))
    P.barrier()


def phase_merge(P, nc, C, x_dram, ya_dram, yr_dram, h_dram, mix_norm, w_in, w_ba, w_br, w_out):
    with ExitStack() as es:
        wg = es.enter_context(nc.sbuf_tensor("wg", [128, 8, 2 * D], BF16))
        wba = es.enter_context(nc.sbuf_tensor("wba", [128, 4, D], BF16))
        wbr = es.enter_context(nc.sbuf_tensor("wbr", [128, 4, D], BF16))
        wo = es.enter_context(nc.sbuf_tensor("wo", [128, 8, D], BF16))
        wg_b, wba_b, wbr_b, wo_b = Buf(), Buf(), Buf(), Buf()
        load_weight_bf16(P, nc, wg, wg_b, w_in, ATTN_IN + RWKV_IN, ATTN_IN + RWKV_IN + 2 * D, 8)
        load_weight_bf16(P, nc, wba, wba_b, w_ba, 0, D, 4)
        load_weight_bf16(P, nc, wbr, wbr_b, w_br, 0, D, 4)
        load_weight_bf16(P, nc, wo, wo_b, w_out, 0, D, 8)
        nm = Normer(P, nc, es, C, mix_norm, "mn")
        xbuf = es.enter_context(nc.sbuf_tensor("xbuf", [128, 2, 4, D], F32))
        xb = [[Buf() for _ in range(4)] for _ in range(2)]
        xnT = es.enter_context(nc.sbuf_tensor("xnT", [128, 2, 8, 512], BF16))
        xnb = [Buf(), Buf()]
        yaT = es.enter_context(nc.sbuf_tensor("yaT", [128, 2, 4, 512], BF16))
        yrT = es.enter_context(nc.sbuf_tensor("yrT", [128, 2, 4, 512], BF16))
        yab, yrb = [Buf(), Buf()], [Buf(), Buf()]
        mT = es.enter_context(nc.sbuf_tensor("mT", [128, 8, 512], BF16))
        mb = [Buf() for _ in range(8)]
        sg = es.enter_context(nc.sbuf_tensor("sgm", [128, 2, 2, 512], F32))
        sgr = Ring([(i, Buf()) for i in range(2)])
        tt = es.enter_context(nc.sbuf_tensor("ttm", [128, 2, 2, 512], F32))
        ttr = Ring([(i, Buf()) for i in range(2)])
        hst = es.enter_context(nc.sbuf_tensor("hst", [128, 2, D], F32))
        hstr = Ring([(i, Buf()) for i in range(2)])
        pga = es.enter_context(nc.psum_tensor("pga", [128, 2, 512], F32))
        pbr = es.enter_context(nc.psum_tensor("pbr", [128, 2, 512], F32))
        po = es.enter_context(nc.psum_tensor("pom", [128, 2, 512], F32))
        pgb, pbb = Buf(), Buf()
        por = Ring([(i, Buf()) for i in range(2)])
        xview = x_dram.rearrange("(t p) d -> t p d", p=128)
        hview = h_dram.rearrange("(t p) d -> t p d", p=128)
        yav = ya_dram.rearrange("(kc p) s -> p kc s", p=128)
        yrv = yr_dram.rearrange("(kc p) s -> p kc s", p=128)
        def merge_load_chunk(c):
            cb = c % 2
            P.dma("sp", lambda e: e.dma_start(out=yaT[:, cb], in_=yav[:, :, c * 512:(c + 1) * 512]), writes=[yab[cb]])
            P.dma("sp", lambda e: e.dma_start(out=yrT[:, cb], in_=yrv[:, :, c * 512:(c + 1) * 512]), writes=[yrb[cb]])
            for j in range(4):
                t = c * 4 + j
                P.dma("sp", lambda e, t=t, j=j: e.dma_start(out=xbuf[:, cb, j, :], in_=xview[t]), writes=[xb[cb][j]])
                nm.run(xbuf[:, cb, j, :], xb[cb][j], xnT[:, cb], xnb[cb], j * 128)

        merge_load_chunk(0)
        for c in range(S // 512):
            cb = c % 2
            if c + 1 < S // 512:
                merge_load_chunk(c + 1)
            for m in range(8):
                def mmg(e, m=m, cb=cb):
                    ins = None
                    for g in range(2):
                        for kc in range(8):
                            ins = e.matmul(pga[:, g, :], lhsT=wg[:, kc, g * D + m * 128:g * D + (m + 1) * 128],
                                           rhs=xnT[:, cb, kc, :], start=(kc == 0), stop=(kc == 7))
                    return ins

                P.op("pe", mmg, reads=[wg_b, xnb[cb]], writes=[pgb])

                def mmb(e, m=m, cb=cb):
                    ins = None
                    for kc in range(4):
                        ins = e.matmul(pbr[:, 0, :], lhsT=wba[:, kc, m * 128:(m + 1) * 128], rhs=yaT[:, cb, kc, :],
                                       start=(kc == 0), stop=(kc == 3))
                    for kc in range(4):
                        ins = e.matmul(pbr[:, 1, :], lhsT=wbr[:, kc, m * 128:(m + 1) * 128], rhs=yrT[:, cb, kc, :],
                                       start=(kc == 0), stop=(kc == 3))
                    return ins

                P.op("pe", mmb, reads=[wba_b, wbr_b, yab[cb], yrb[cb]], writes=[pbb])
                si, sbuf_ = sgr.next()
                P.op("act", lambda e, si=si: e.activation(out=sg[:, si], in_=pga[:, :, :], func=AF.Sigmoid),
                     reads=[pgb], writes=[sbuf_])
                ti, tbuf = ttr.next()
                P.op("dve", lambda e, si=si, ti=ti: e.tensor_tensor(out=tt[:, ti], in0=pbr[:, :, :], in1=sg[:, si],
                                                                    op=ALU.mult),
                     reads=[pbb, sbuf_], writes=[tbuf])
                P.op("pool", lambda e, ti=ti, m=m: e.tensor_tensor(out=mT[:, m, :], in0=tt[:, ti, 0, :],
                                                                   in1=tt[:, ti, 1, :], op=ALU.add),
                     reads=[tbuf], writes=[mb[m]])
            for j in range(4):
                t = c * 4 + j
                hi, hbuf_ = hstr.next()
                for nh in range(2):
                    pi, pbuf = por.next()

                    def mmo(e, j=j, nh=nh, pi=pi):
                        ins = None
                        for m in range(8):
                            ins = e.matmul(po[:, pi, :], lhsT=mT[:, m, j * 128:(j + 1) * 128],
                                           rhs=wo[:, m, nh * 512:(nh + 1) * 512], start=(m == 0), stop=(m == 7))
                        return ins

                    P.op("pe", mmo, reads=mb + [wo_b], writes=[pbuf])
                    P.op("dve", lambda e, j=j, nh=nh, pi=pi, hi=hi, cb=cb: e.tensor_tensor(
                        out=hst[:, hi, nh * 512:(nh + 1) * 512], in0=po[:, pi, :],
                        in1=xbuf[:, cb, j, nh * 512:(nh + 1) * 512], op=ALU.add),
                        reads=[pbuf, xb[cb][j]], writes=[hbuf_])
                P.dma("sp", lambda e, t=t, hi=hi: e.dma_start(out=hview[t], in_=hst[:, hi, :]), reads=[hbuf_])
    P.barrier()


TL = 128
C0 = math.exp(-0.5)


def col8(P, nc, es, name, v_ap):
    t = es.enter_context(nc.sbuf_tensor(name, [64, 8], F32))
    b = Buf(name)
    P.dma("sp", lambda e: e.dma_start(out=t[:, :], in_=v_ap.rearrange("o (h k) -> k (o h)", k=64),
                                      allow_slow_non_contiguous=True), writes=[b])
    return t, b


def phase_rwkv(P, nc, C, A, yr_dram):
    x_dram = A["x"]
    with ExitStack() as es:
        sb = lambda name, shape, dt=F32: es.enter_context(nc.sbuf_tensor(name, shape, dt))
        wr = sb("wr", [128, 8, RWKV_IN], BF16)
        wmu = sb("wmu", [128, 8, RWKV_IN], BF16)
        wr_b, wmu_b, mub_b = Buf(), Buf(), Buf()
        load_weight_bf16(P, nc, wr, wr_b, A["w_in"], ATTN_IN, ATTN_IN + RWKV_IN, 8)
        with nc.sbuf_tensor("mub", [128, RWKV_IN], F32) as mub:
            P.dma("sp", lambda e: e.dma_start(out=mub[:], in_=A["rwkv_mu"].partition_broadcast(128)), writes=[mub_b])
            for kc in range(8):
                P.op("pool", lambda e, kc=kc: e.tensor_tensor(out=wmu[:, kc, :], in0=wr[:, kc, :], in1=mub[:],
                                                              op=ALU.mult), reads=[wr_b, mub_b], writes=[wmu_b])
        P.barrier()
        w2 = sb("w2", [64, 512], BF16)
        a2 = sb("a2", [64, 512], BF16)
        g2 = sb("g2", [64, 3, 512], BF16)
        lw_b = Buf()
        P.dma("pool", lambda e: e.dma_start(out=w2[:], in_=A["rwkv_w2"]), writes=[lw_b])
        P.dma("pool", lambda e: e.dma_start(out=a2[:], in_=A["rwkv_a2"]), writes=[lw_b])
        P.dma("pool", lambda e: e.dma_start(out=g2[:, 0, :], in_=A["rwkv_g2"][0:64, :]), writes=[lw_b])
        P.dma("pool", lambda e: e.dma_start(out=g2[:, 1, :], in_=A["rwkv_g2"][64:128, :]), writes=[lw_b])
        P.dma("pool", lambda e: e.dma_start(out=g2[0:32, 2, :], in_=A["rwkv_g2"][128:160, :]), writes=[lw_b])
        cw0, b_w0 = col8(P, nc, es, "cw0", A["rwkv_w0"])
        ca0, b_a0 = col8(P, nc, es, "ca0", A["rwkv_a0"])
        ckk, b_kk = col8(P, nc, es, "ckk", A["rwkv_k_k"])
        cka, b_ka = col8(P, nc, es, "cka", A["rwkv_k_a"])
        crk, b_rk = col8(P, nc, es, "crk", A["rwkv_r_k"])
        clw, b_lw = col8(P, nc, es, "clw", A["rwkv_ln_w"])
        clb, b_lb = col8(P, nc, es, "clb", A["rwkv_ln_b"])
        msk = sb("rmsk", [64, 3, 64], F32)
        ones_bf = sb("ones_bf", [64, 64], BF16)
        rmask = sb("rmask", [64, 8, TL // 64, 64], F32)
        mk_b = Buf()

        def mkmasks(e):
            e.memset(msk[:], 1.0)
            e.memset(ones_bf[:], 1.0)
            e.memset(rmask[:], 1.0)
            e.memset(rmask[:, :, :, 0:1], 0.0)
            e.affine_select(out=msk[:, 0, :], in_=msk[:, 0, :], pattern=[[1, 64]], compare_op=ALU.is_ge,
                            fill=0.0, base=-1, channel_multiplier=-1)
            e.affine_select(out=msk[:, 1, :], in_=msk[:, 1, :], pattern=[[1, 64]], compare_op=ALU.is_ge,
                            fill=0.0, base=0, channel_multiplier=-1)
            return e.affine_select(out=msk[:, 2, :], in_=msk[:, 2, :], pattern=[[-1, 64]], compare_op=ALU.is_ge,
                                   fill=0.0, base=-1, channel_multiplier=1)

        P.op("pool", mkmasks, writes=[mk_b])
        mb3 = lambda i: msk[:, i, :].unsqueeze(1).broadcast_to([64, 8, 64])
        idb = C.ident_bf[0:64, 0:64]
        idf3 = C.ident_f[0:64, 0:64].unsqueeze(1).broadcast_to([64, 8, 64])
        bc = lambda col: col[:, :].unsqueeze(2).broadcast_to([64, 8, TL])

        nm = Normer(P, nc, es, C, A["mix_norm"], "rn", nslots=1)
        xbuf = sb("rxbuf", [128, 1, D], F32)
        xr = Ring([(i, Buf()) for i in range(1)])
        xnx = sb("xnx", [128, 2, 8, TL + 1], BF16)
        xnb = [Buf(), Buf()]
        dxn = sb("dxn", [128, 8, TL], BF16)
        dxb = Buf()
        P.op("pool", lambda e: e.memset(xnx[:, 1, :, TL:TL + 1], 0.0), writes=[xnb[1]])
        F = {}
        FB = {}
        for nme, dt in [("r", F32), ("k", F32), ("sgd", F32), ("a", F32), ("kk", F32),
                        ("t1", F32), ("t2", F32), ("cum", F32), ("e1", F32), ("e2", F32), ("sqb", BF16)]:
            alias = {"e1": "t1", "e2": "a"}
            if nme in alias:
                F[nme] = F[alias[nme]]
                FB[nme] = FB[alias[nme]]
                continue
            F[nme] = sb("f_" + nme, [64, 8, TL], dt)
            FB[nme] = Buf(nme)
        X2 = {}
        X2B = {}
        for nme in ("rT", "aT", "bT", "kT", "bH", "kH", "vb", "g", "bon"):
            X2[nme] = sb("x_" + nme, [64, 2, 8, TL], BF16)
            X2B[nme] = [Buf(nme + "0"), Buf(nme + "1")]
        lora = sb("lora", [64, 5, TL], BF16)
        ztok = sb("ztok", [128, 2, 512], F32)
        ztr = Ring([(i, Buf()) for i in range(2)])
        lora_b = Buf()
        gC = sb("gC", [64, 2, 8, TL // 64], F32)
        gC_bs = [Buf(), Buf()]
        Sf = sb("Sf", [64, 8, 64], F32)
        Sb = sb("Sb", [64, 8, 64], BF16)
        St = sb("St", [64, 8, 64], F32)
        S_b, St_b = Buf(), Buf()
        P.op("dve", lambda e: e.memset(Sf[:], 0.0), writes=[S_b])
        P.op("dve", lambda e: e.memset(Sb[:], 0.0), reads=[S_b], writes=[S_b])
        ost = sb("rost", [64, 8, TL], BF16)
        ost_b = Buf()
        def pair(name, dt=BF16, n=2):
            t = sb(name, [64, n, 8, 64], dt)
            return t, [Buf() for _ in range(n)]
        Atok, Atok_b = pair("Atok")
        BHtok, BHtok_b = pair("BHtok")
        KHtok, KHtok_b = pair("KHtok")
        Vtok, Vtok_b = pair("Vtok")
        Mrb, Mrb_b = pair("Mrb")
        Mrk, Mrk_b = pair("Mrk")
        Lak, Lak_b = pair("Lak")
        Nn, Nn_b = pair("Nn", BF16, 4)
        Mm, Mm_b = pair("Mm", BF16, 4)
        Qq, Qq_b = pair("Qq", BF16, 4)
        WT, WT_b = pair("WT")
        Xx, Xx_b = pair("Xx")
        Uu, Uu_b = pair("Uu")
        Ys, Ys_b = pair("Ys", F32, 1)
        Yn, Yn_b = pair("Yn", BF16, 2)
        gst = sb("gst", [64, 2, 8, 6], F32)
        gst_b = [Buf(), Buf()]
        pb = es.enter_context(nc.psum_tensor("rpb", [128, 7, 512], F32))
        pr = Ring([(i, Buf()) for i in range(7)])
        pv = lambda i: pb[0:64, i, :].rearrange("p (h t) -> p h t", h=8)
        pvb = lambda i: pb[0:64, i, :].bitcast(BF16)[:, 0:512].rearrange("p (h t) -> p h t", h=8)

        xview = x_dram.rearrange("(t p) d -> t p d", p=128)

        def mm8(bank, parts, rd):
            i, bbuf = bank
            ops = [[(lf(h), rf(h)) for (lf, rf) in parts] for h in range(8)]

            def th(e):
                ins = None
                for h in range(8):
                    for pi, (l, r) in enumerate(ops[h]):
                        ins = e.matmul(pb[0:64, i, h * 64:(h + 1) * 64], lhsT=l, rhs=r,
                                       start=(pi == 0), stop=(pi == len(ops[h]) - 1))
                return ins

            P.op("pe", th, reads=rd, writes=[bbuf])

        def tr8(bank, src_fn, rd):
            i, bbuf = bank
            srcs = [src_fn(h) for h in range(8)]

            def th(e):
                ins = None
                v = pvb(i)
                for h in range(8):
                    ins = e.transpose(out=v[:, h, :], in_=srcs[h], identity=idb)
                return ins

            P.op("pe", th, reads=rd + [C.b], writes=[bbuf])

        NB = S // TL

        def prep(n):
            par = n % 2
            cb = n % 2
            pc = 1 - cb
            X = {k: X2[k][:, par] for k in X2}
            XB = {k: X2B[k][par] for k in X2}
            P.op("pool", lambda e: e.tensor_copy(out=xnx[:, cb, :, 0:1], in_=xnx[:, pc, :, TL:TL + 1]),
                 reads=[xnb[pc]], writes=[xnb[cb]])
            for j in range(TL // 128):
                xi, xb_ = xr.next()
                P.dma("sp", lambda e, t=n * (TL // 128) + j, xi=xi: e.dma_start(out=xbuf[:, xi, :], in_=xview[t]), writes=[xb_])
                nm.run(xbuf[:, xi, :], xb_, xnx[:, cb], xnb[cb], 1 + j * 128)
            P.op("pool", lambda e: e.tensor_tensor(out=dxn[:], in0=xnx[:, cb, :, 0:TL], in1=xnx[:, cb, :, 1:TL + 1],
                                                   op=ALU.subtract), reads=[xnb[cb]], writes=[dxb])
            yield

            def projtok(c0, ncol):
                bank = pr.next()
                i = bank[0]

                def th(e):
                    ins = None
                    for kc in range(8):
                        e.matmul(pb[:, i, 0:ncol], lhsT=xnx[:, cb, kc, 1:TL + 1], rhs=wr[:, kc, c0:c0 + ncol],
                                 start=(kc == 0), stop=False)
                    for kc in range(8):
                        ins = e.matmul(pb[:, i, 0:ncol], lhsT=dxn[:, kc, :], rhs=wmu[:, kc, c0:c0 + ncol],
                                       start=False, stop=(kc == 7))
                    return ins

                P.op("pe", th, reads=[wr_b, wmu_b, xnb[cb], dxb], writes=[bank[1]])
                zi, zb = ztr.next()
                P.op("act", lambda e: e.activation(out=ztok[:, zi, 0:ncol], in_=pb[:, i, 0:ncol], func=AF.Copy),
                     reads=[bank[1]], writes=[zb])
                return zi, zb

            def trz(zi, zb, cols, m):
                bank = pr.next()
                i = bank[0]

                def th(e):
                    ins = None
                    for q, c in enumerate(cols):
                        ins = e.transpose(out=pb[0:m, i, q * TL:(q + 1) * TL], in_=ztok[:, zi, c:c + m], identity=C.ident_f[:])
                    return ins

                P.op("pe", th, reads=[zb, C.b], writes=[bank[1]])
                return bank

            for qi, qn in enumerate(("r", "k", "v")):
                zi, zb = projtok(qi * 512, 512)
                yield
                for h0 in (0, 4):
                    bank = trz(zi, zb, [(h0 + q) * 64 for q in range(4)], 64)
                    dst, dstb = (X["vb"], XB["vb"]) if qn == "v" else (F[qn], FB[qn])
                    P.op("act", lambda e, dst=dst, h0=h0, i=bank[0]: e.activation(
                        out=dst[:, h0:h0 + 4, :], in_=pb[0:64, i, 0:4 * TL].rearrange("p (q t) -> p q t", q=4), func=AF.Copy),
                        reads=[bank[1]], writes=[dstb])
                    yield
            zi, zb = projtok(1536, 288)
            yield
            for li, (c0, m, fn) in enumerate([(0, 64, AF.Tanh), (64, 64, AF.Copy), (128, 64, AF.Sigmoid),
                                              (192, 64, AF.Sigmoid), (256, 32, AF.Sigmoid)]):
                bank = trz(zi, zb, [c0], m)
                P.op("act", lambda e, li=li, m=m, fn=fn, i=bank[0]: e.activation(out=lora[0:m, li, :], in_=pb[0:m, i, 0:TL],
                                                                                 func=fn),
                     reads=[bank[1]], writes=[lora_b])
                yield
            for h in range(8):
                bank = pr.next()
                P.op("pe", lambda e, h=h, i=bank[0]: e.matmul(pb[0:64, i, 0:TL], lhsT=w2[:, h * 64:(h + 1) * 64],
                                                              rhs=lora[:, 0, :], start=True, stop=True),
                     reads=[lw_b, lora_b], writes=[bank[1]])
                P.op("act", lambda e, h=h, i=bank[0]: e.activation(out=F["sgd"][:, h, :], in_=pb[0:64, i, 0:TL],
                                                                   func=AF.Sigmoid, bias=cw0[:, h:h + 1]),
                     reads=[bank[1], b_w0], writes=[FB["sgd"]])
                bank = pr.next()
                P.op("pe", lambda e, h=h, i=bank[0]: e.matmul(pb[0:64, i, 0:TL], lhsT=a2[:, h * 64:(h + 1) * 64],
                                                              rhs=lora[:, 1, :], start=True, stop=True),
                     reads=[lw_b, lora_b], writes=[bank[1]])
                P.op("act", lambda e, h=h, i=bank[0]: e.activation(out=F["a"][:, h, :], in_=pb[0:64, i, 0:TL],
                                                                   func=AF.Sigmoid, bias=ca0[:, h:h + 1]),
                     reads=[bank[1], b_a0], writes=[FB["a"]])
                bank = pr.next()

                def gmm(e, h=h, i=bank[0]):
                    e.matmul(pb[0:64, i, 0:TL], lhsT=g2[:, 0, h * 64:(h + 1) * 64], rhs=lora[:, 2, :], start=True, stop=False)
                    e.matmul(pb[0:64, i, 0:TL], lhsT=g2[:, 1, h * 64:(h + 1) * 64], rhs=lora[:, 3, :], start=False, stop=False)
                    return e.matmul(pb[0:64, i, 0:TL], lhsT=g2[0:32, 2, h * 64:(h + 1) * 64], rhs=lora[0:32, 4, :],
                                    start=False, stop=True)

                P.op("pe", gmm, reads=[lw_b, lora_b], writes=[bank[1]])
                P.op("act", lambda e, h=h, i=bank[0]: e.activation(out=X["g"][:, h, :], in_=pb[0:64, i, 0:TL], func=AF.Copy),
                     reads=[bank[1]], writes=[XB["g"]])
                yield
            P.op("dve", lambda e: e.tensor_tensor(out=F["kk"][:], in0=F["k"][:], in1=bc(ckk), op=ALU.mult),
                 reads=[FB["k"], b_kk], writes=[FB["kk"]])
            P.op("pool", lambda e: e.tensor_tensor(out=F["sqb"][:], in0=F["kk"][:], in1=F["kk"][:], op=ALU.mult),
                 reads=[FB["kk"]], writes=[FB["sqb"]])
            yield
            for h in range(8):
                bank = pr.next()
                P.op("pe", lambda e, h=h, i=bank[0]: e.matmul(pb[0:64, i, 0:TL], lhsT=ones_bf[:], rhs=F["sqb"][:, h, :],
                                                              start=True, stop=True),
                     reads=[mk_b, FB["sqb"]], writes=[bank[1]])
                P.op("act", lambda e, h=h, i=bank[0]: e.activation(out=F["t1"][:, h, :], in_=pb[0:64, i, 0:TL], func=AF.Sqrt),
                     reads=[bank[1]], writes=[FB["t1"]])
                if h % 2 == 1:
                    yield
            P.op("dve", lambda e: e.tensor_scalar(out=F["t1"][:], in0=F["t1"][:], scalar1=1e-12, scalar2=None,
                                                  op0=ALU.max), reads=[FB["t1"]], writes=[FB["t1"]])
            P.op("dve", lambda e: e.reciprocal(out=F["t1"][:], in_=F["t1"][:]), reads=[FB["t1"]], writes=[FB["t1"]])
            yield
            P.op("dve", lambda e: e.tensor_tensor(out=F["kk"][:], in0=F["kk"][:], in1=F["t1"][:], op=ALU.mult),
                 reads=[FB["kk"], FB["t1"]], writes=[FB["kk"]])
            P.op("dve", lambda e: e.scalar_tensor_tensor(out=F["t2"][:], in0=F["a"][:], scalar=-1.0, in1=bc(cka),
                                                         op0=ALU.add, op1=ALU.mult),
                 reads=[FB["a"], b_ka], writes=[FB["t2"]])
            yield
            P.op("dve", lambda e: e.scalar_tensor_tensor(out=F["k"][:], in0=F["t2"][:], scalar=1.0, in1=F["k"][:],
                                                         op0=ALU.add, op1=ALU.mult),
                 reads=[FB["t2"], FB["k"]], writes=[FB["k"]])
            P.op("pool", lambda e: e.tensor_tensor(out=F["t2"][:], in0=F["kk"][:], in1=F["a"][:], op=ALU.mult),
                 reads=[FB["kk"], FB["a"]], writes=[FB["t2"]])
            yield
            P.op("dve", lambda e: e.tensor_tensor_scan(out=F["cum"][:].rearrange("p h t -> p (h t)"),
                                                       data0=rmask[:].rearrange("p h c t -> p (h c t)"),
                                                       data1=F["sgd"][:].rearrange("p h t -> p (h t)"),
                                                       initial=0.0, op0=ALU.mult, op1=ALU.add),
                 reads=[FB["sgd"], mk_b], writes=[FB["cum"]])
            cum4 = F["cum"][:].rearrange("p h (c t) -> p h c t", t=64)
            yield
            P.op("act", lambda e: e.activation(out=F["e1"][:], in_=F["cum"][:], func=AF.Exp, scale=-C0),
                 reads=[FB["cum"]], writes=[FB["e1"]])
            P.op("dve", lambda e: e.tensor_tensor(out=X["rT"][:], in0=F["r"][:], in1=F["e1"][:], op=ALU.mult),
                 reads=[FB["r"], FB["e1"]], writes=[XB["rT"]])
            P.op("act", lambda e: e.activation(out=gC[:, par], in_=cum4[:, :, :, 63], func=AF.Exp, scale=-C0),
                 reads=[FB["cum"]], writes=[gC_bs[par]])
            yield
            P.op("pool", lambda e: e.tensor_tensor(out=F["e2"][:], in0=F["cum"][:], in1=F["sgd"][:], op=ALU.subtract),
                 reads=[FB["cum"], FB["sgd"]], writes=[FB["e2"]])
            P.op("act", lambda e: e.activation(out=F["e2"][:], in_=F["e2"][:], func=AF.Exp, scale=-C0),
                 reads=[FB["e2"]], writes=[FB["e2"]])
            P.op("dve", lambda e: e.scalar_tensor_tensor(out=X["aT"][:], in0=F["kk"][:], scalar=-1.0, in1=F["e2"][:],
                                                         op0=ALU.mult, op1=ALU.mult),
                 reads=[FB["kk"], FB["e2"]], writes=[XB["aT"]])
            yield
            P.op("act", lambda e: e.activation(out=F["e1"][:], in_=F["cum"][:], func=AF.Exp, scale=C0),
                 reads=[FB["cum"]], writes=[FB["e1"]])
            P.op("dve", lambda e: e.tensor_tensor(out=X["bT"][:], in0=F["t2"][:], in1=F["e1"][:], op=ALU.mult),
                 reads=[FB["t2"], FB["e1"]], writes=[XB["bT"]])
            P.op("pool", lambda e: e.tensor_tensor(out=X["kT"][:], in0=F["k"][:], in1=F["e1"][:], op=ALU.mult),
                 reads=[FB["k"], FB["e1"]], writes=[XB["kT"]])
            yield
            P.op("dve", lambda e: e.tensor_tensor(out=F["e2"][:].rearrange("p h (c t) -> p h c t", t=64),
                                                  in0=cum4[:, :, :, 63:64].broadcast_to([64, 8, TL // 64, 64]), in1=cum4,
                                                  op=ALU.subtract),
                 reads=[FB["cum"]], writes=[FB["e2"]])
            P.op("act", lambda e: e.activation(out=F["e2"][:], in_=F["e2"][:], func=AF.Exp, scale=-C0),
                 reads=[FB["e2"]], writes=[FB["e2"]])
            yield
            P.op("dve", lambda e: e.tensor_tensor(out=X["bH"][:], in0=F["t2"][:], in1=F["e2"][:], op=ALU.mult),
                 reads=[FB["t2"], FB["e2"]], writes=[XB["bH"]])
            P.op("pool", lambda e: e.tensor_tensor(out=X["kH"][:], in0=F["k"][:], in1=F["e2"][:], op=ALU.mult),
                 reads=[FB["k"], FB["e2"]], writes=[XB["kH"]])
            yield
            P.op("dve", lambda e: e.tensor_tensor(out=F["t1"][:], in0=F["r"][:], in1=F["k"][:], op=ALU.mult),
                 reads=[FB["r"], FB["k"]], writes=[FB["t1"]])
            P.op("pool", lambda e: e.tensor_tensor(out=F["sqb"][:], in0=F["t1"][:], in1=bc(crk), op=ALU.mult),
                 reads=[FB["t1"], b_rk], writes=[FB["sqb"]])
            yield
            for h in range(8):
                bank = pr.next()
                P.op("pe", lambda e, h=h, i=bank[0]: e.matmul(pb[0:64, i, 0:TL], lhsT=ones_bf[:], rhs=F["sqb"][:, h, :],
                                                              start=True, stop=True),
                     reads=[mk_b, FB["sqb"]], writes=[bank[1]])
                P.op("dve", lambda e, h=h, i=bank[0]: e.tensor_tensor(out=X["bon"][:, h, :], in0=pb[0:64, i, 0:TL],
                                                                      in1=X["vb"][:, h, :], op=ALU.mult),
                     reads=[bank[1], XB["vb"]], writes=[XB["bon"]])
                if h % 2 == 1:
                    yield

        def chunk_pre(n, c, out):
            par = n % 2
            X = {k: X2[k][:, par] for k in X2}
            XB = {k: X2B[k][par] for k in X2}
            cs = slice(c * 64, (c + 1) * 64)
            for (dst, dbs, src) in ((Atok, Atok_b, "aT"), (BHtok, BHtok_b, "bH"), (KHtok, KHtok_b, "kH"), (Vtok, Vtok_b, "vb")):
                bank = pr.next()
                tr8(bank, lambda h, src=src: X[src][:, h, cs], [XB[src]])
                P.op("act", lambda e, dst=dst, i=bank[0]: e.activation(out=dst[:, c], in_=pvb(i), func=AF.Copy),
                     reads=[bank[1]], writes=[dbs[c]])
                yield

            def gmat(lname, rname, mi, dst, dbs, slot):
                bank = pr.next()
                mm8(bank, [(lambda h: X[lname][:, h, cs], lambda h: X[rname][:, h, cs])], [XB[lname], XB[rname]])
                P.op("dve", lambda e, i=bank[0]: e.tensor_tensor(out=dst[:, slot], in0=pv(i), in1=mb3(mi), op=ALU.mult),
                     reads=[bank[1], mk_b], writes=[dbs[slot]])

            base = 2 * c
            gmat("bT", "aT", 0, Mm, Mm_b, base)
            yield
            gmat("bT", "rT", 1, Mrb, Mrb_b, c)
            yield
            gmat("kT", "aT", 0, Lak, Lak_b, c)
            yield
            gmat("kT", "rT", 1, Mrk, Mrk_b, c)
            yield
            gmat("aT", "bT", 2, Nn, Nn_b, base)
            yield
            P.op("pool", lambda e: e.tensor_tensor(out=Qq[:, base], in0=Mm[:, base], in1=idf3, op=ALU.add),
                 reads=[Mm_b[base], C.b], writes=[Qq_b[base]])
            ni = mi_ = qi_ = base
            for lvl in range(1, 6):
                nn_ = base + (1 - (ni - base))
                nm_ = base + (1 - (mi_ - base))
                nq_ = base + (1 - (qi_ - base))
                bank = pr.next()
                mm8(bank, [(lambda h: Mm[:, mi_, h, :], lambda h: Nn[:, ni, h, :])], [Mm_b[mi_], Nn_b[ni]])
                if lvl < 5:
                    bank2 = pr.next()
                    mm8(bank2, [(lambda h: Nn[:, ni, h, :], lambda h: Mm[:, mi_, h, :])], [Mm_b[mi_], Nn_b[ni]])
                P.op("act", lambda e, nn_=nn_, i=bank[0]: e.activation(out=Nn[:, nn_], in_=pv(i), func=AF.Copy),
                     reads=[bank[1]], writes=[Nn_b[nn_]])
                if lvl < 5:
                    P.op("dve", lambda e, nm_=nm_, i=bank2[0]: e.tensor_copy(out=Mm[:, nm_], in_=pv(i)),
                         reads=[bank2[1]], writes=[Mm_b[nm_]])
                    mi_ = nm_
                ni = nn_
                yield
                bank3 = pr.next()
                mm8(bank3, [(lambda h: Nn[:, ni, h, :], lambda h: Qq[:, qi_, h, :])], [Qq_b[qi_], Nn_b[ni]])
                P.op("dve", lambda e, nq_=nq_, qo=qi_, i=bank3[0]: e.tensor_tensor(out=Qq[:, nq_], in0=pv(i), in1=Qq[:, qo],
                                                                                   op=ALU.add),
                     reads=[bank3[1], Qq_b[qi_]], writes=[Qq_b[nq_]])
                qi_ = nq_
                yield
            bank = pr.next()
            mm8(bank, [(lambda h: Atok[:, c, h, :], lambda h: Qq[:, qi_, h, :])], [Atok_b[c], Qq_b[qi_]])
            P.op("act", lambda e, i=bank[0]: e.activation(out=WT[:, c], in_=pv(i), func=AF.Copy),
                 reads=[bank[1]], writes=[WT_b[c]])
            bank = pr.next()
            mm8(bank, [(lambda h: Lak[:, c, h, :], lambda h: Vtok[:, c, h, :])], [Lak_b[c], Vtok_b[c]])
            P.op("dve", lambda e, i=bank[0]: e.tensor_copy(out=Xx[:, c], in_=pv(i)), reads=[bank[1]], writes=[Xx_b[c]])
            out["q"] = qi_
            yield

        def chain(n, c, qi_):
            par = n % 2
            X = {k: X2[k][:, par] for k in X2}
            XB = {k: X2B[k][par] for k in X2}
            cs = slice(c * 64, (c + 1) * 64)
            bank = pr.next()
            mm8(bank, [(lambda h: WT[:, c, h, :], lambda h: Sb[:, h, :]),
                       (lambda h: Qq[:, qi_, h, :], lambda h: Xx[:, c, h, :])], [WT_b[c], S_b, Qq_b[qi_], Xx_b[c]])
            P.op("act", lambda e, i=bank[0]: e.activation(out=Uu[:, c], in_=pv(i), func=AF.Copy),
                 reads=[bank[1]], writes=[Uu_b[c]])
            banky = pr.next()
            mm8(banky, [(lambda h: X["rT"][:, h, cs], lambda h: Sb[:, h, :]),
                        (lambda h: Mrb[:, c, h, :], lambda h: Uu[:, c, h, :]),
                        (lambda h: Mrk[:, c, h, :], lambda h: Vtok[:, c, h, :])],
                [XB["rT"], S_b, Mrb_b[c], Uu_b[c], Mrk_b[c], Vtok_b[c]])
            banks = pr.next()
            mm8(banks, [(lambda h: BHtok[:, c, h, :], lambda h: Uu[:, c, h, :]),
                        (lambda h: KHtok[:, c, h, :], lambda h: Vtok[:, c, h, :])], [BHtok_b[c], Uu_b[c], KHtok_b[c], Vtok_b[c]])
            P.op("dve", lambda e: e.tensor_tensor(out=St[:], in0=Sf[:],
                                                  in1=gC[:, par, :, c:c + 1].broadcast_to([64, 8, 64]), op=ALU.mult),
                 reads=[S_b, gC_bs[par]], writes=[St_b])
            P.op("dve", lambda e, i=banks[0]: e.tensor_tensor(out=Sf[:], in0=pv(i), in1=St[:], op=ALU.add),
                 reads=[banks[1], St_b], writes=[S_b])
            P.op("act", lambda e: e.activation(out=Sb[:], in_=Sf[:], func=AF.Copy), reads=[S_b], writes=[S_b])
            yield
            y_b, yq_b, g_b = Ys_b[0], St_b, gst_b[c]
            P.op("act", lambda e, i=banky[0]: e.activation(out=Ys[:, 0], in_=pv(i), func=AF.Copy),
                 reads=[banky[1]], writes=[y_b])
            P.op("pool", lambda e: e.tensor_tensor(out=St[:], in0=Ys[:, 0], in1=Ys[:, 0], op=ALU.mult),
                 reads=[y_b], writes=[yq_b])
            P.op("dve", lambda e: e.tensor_reduce(out=gst[:, c, :, 0], in_=Ys[:, 0], axis=AX.X, op=ALU.add),
                 reads=[y_b], writes=[g_b])
            P.op("dve", lambda e: e.tensor_reduce(out=gst[:, c, :, 1], in_=St[:], axis=AX.X, op=ALU.add),
                 reads=[yq_b, g_b], writes=[g_b])
            yield
            P.op("dve", lambda e: e.tensor_scalar(out=gst[:, c, :, 2], in0=gst[:, c, :, 0], scalar1=1.0 / 64,
                                                  scalar2=None, op0=ALU.mult), reads=[g_b], writes=[g_b])
            P.op("dve", lambda e: e.tensor_tensor(out=gst[:, c, :, 3], in0=gst[:, c, :, 2], in1=gst[:, c, :, 2],
                                                  op=ALU.mult), reads=[g_b], writes=[g_b])
            P.op("dve", lambda e: e.scalar_tensor_tensor(out=gst[:, c, :, 4], in0=gst[:, c, :, 1], scalar=1.0 / 64,
                                                         in1=gst[:, c, :, 3], op0=ALU.mult, op1=ALU.subtract),
                 reads=[g_b], writes=[g_b])
            yield
            P.op("act", lambda e: e.activation(out=gst[:, c, :, 5], in_=gst[:, c, :, 4], func=AF.Sqrt,
                                               bias=C.eps[0:64, 1:2]), reads=[g_b, C.b], writes=[g_b])
            P.op("dve", lambda e: e.reciprocal(out=gst[:, c, :, 5], in_=gst[:, c, :, 5]), reads=[g_b], writes=[g_b])
            P.op("dve", lambda e: e.tensor_tensor(out=Ys[:, 0], in0=Ys[:, 0],
                                                  in1=gst[:, c, :, 2:3].broadcast_to([64, 8, 64]),
                                                  op=ALU.subtract), reads=[y_b, g_b], writes=[y_b])
            yield
            P.op("dve", lambda e: e.tensor_tensor(out=Yn[:, c], in0=Ys[:, 0],
                                                  in1=gst[:, c, :, 5:6].broadcast_to([64, 8, 64]),
                                                  op=ALU.mult), reads=[y_b, g_b], writes=[Yn_b[c]])
            bank = pr.next()
            tr8(bank, lambda h: Yn[:, c, h, :], [Yn_b[c]])
            bc64 = lambda col: col[:, :].unsqueeze(2).broadcast_to([64, 8, 64])
            P.op("dve", lambda e, i=bank[0]: e.tensor_tensor(out=Ys[:, 0], in0=pvb(i), in1=bc64(clw), op=ALU.mult),
                 reads=[bank[1], b_lw, Yn_b[c]], writes=[y_b])
            P.op("pool", lambda e: e.tensor_tensor(out=Ys[:, 0], in0=Ys[:, 0], in1=bc64(clb), op=ALU.add),
                 reads=[y_b, b_lb], writes=[y_b])
            yield
            P.op("dve", lambda e: e.tensor_tensor(out=Ys[:, 0], in0=Ys[:, 0], in1=X["bon"][:, :, cs], op=ALU.add),
                 reads=[y_b, XB["bon"]], writes=[y_b])
            P.op("dve", lambda e: e.tensor_tensor(out=ost[:, :, cs], in0=Ys[:, 0], in1=X["g"][:, :, cs], op=ALU.mult),
                 reads=[y_b, XB["g"]], writes=[ost_b])
            yield

        def scan(n):
            par = n % 2
            X = {k: X2[k][:, par] for k in X2}
            XB = {k: X2B[k][par] for k in X2}
            outs = [{} for _ in range(TL // 64)]
            gens = [chunk_pre(n, c, outs[c]) for c in range(TL // 64)]
            alive = [True] * len(gens)
            while any(alive):
                for gi_, g_ in enumerate(gens):
                    if alive[gi_]:
                        try:
                            next(g_)
                        except StopIteration:
                            alive[gi_] = False
                yield
            for c in range(TL // 64):
                for _ in chain(n, c, outs[c]["q"]):
                    yield
            P.dma("sp", lambda e: e.dma_start(
                out=yr_dram.rearrange("(h v) s -> v h s", v=64)[:, :, n * TL:(n + 1) * TL], in_=ost[:]),
                reads=[ost_b])
            yield

        def run2(f, b):
            fa, ba = f is not None, b is not None
            while fa or ba:
                if fa:
                    try:
                        next(f)
                    except StopIteration:
                        fa = False
                if ba:
                    try:
                        next(b)
                    except StopIteration:
                        ba = False

        for n in range(NB + 1):
            run2(prep(n) if n < NB else None, scan(n - 1) if n >= 1 else None)
    P.barrier()


NITER = 16


def phase_attn(P, nc, C, A, ya_dram):
    x_dram = A["x"]
    with ExitStack() as es:
        sb = lambda name, shape, dt=F32: es.enter_context(nc.sbuf_tensor(name, shape, dt))
        wa = sb("wa", [128, 8, ATTN_IN + 64], BF16)
        wa_b = Buf()
        load_weight_bf16(P, nc, wa, wa_b, A["w_in"], 0, ATTN_IN, 8)
        wv_ = A["w_in"].rearrange("(kc p) n -> p kc n", p=128)
        for kc in range(8):
            P.dma("pool", lambda e, kc=kc: e.dma_start(out=wa[:, kc, ATTN_IN:ATTN_IN + 64], in_=wv_[:, kc, 2048:2112]),
                  writes=[wa_b])
        kT = sb("kT", [128, 4, S], BF16)
        kiT = sb("kiT", [128, S], BF16)
        vaug = sb("vaug", [128, NT, 8, 65], BF16)
        kT_bs = [Buf() for _ in range(NT)]
        kiT_bs = [Buf() for _ in range(NT)]
        va_bs = [Buf() for _ in range(NT)]
        P.op("pool", lambda e: e.memset(vaug[:, :, :, 64:65], 1.0), writes=va_bs)
        gqk = sb("gqk", [128, 2], F32)
        gqk_b = Buf()
        for half in range(2):
            P.dma("sp", lambda e, half=half: e.dma_start(out=gqk[half * 64:(half + 1) * 64, 0:1],
                                                         in_=A["attn_q_norm"].rearrange("o d -> d o"),
                                                         allow_slow_non_contiguous=True), writes=[gqk_b])
            P.dma("sp", lambda e, half=half: e.dma_start(out=gqk[half * 64:(half + 1) * 64, 1:2],
                                                         in_=A["attn_k_norm"].rearrange("o d -> d o"),
                                                         allow_slow_non_contiguous=True), writes=[gqk_b])
        P.op("dve", lambda e: e.tensor_scalar(out=gqk[:, 0:1], in0=gqk[:, 0:1], scalar1=0.125, scalar2=None, op0=ALU.mult),
             reads=[gqk_b], writes=[gqk_b])
        btf = sb("btf", [128, 8, 2, 128], F32)
        bt = sb("bt", [128, 8, 2, 128], BF16)
        b31 = sb("b31_sb", [128, 8], F32)
        bt_b = Buf()
        P.dma("sp", lambda e: e.dma_start(out=btf[:], in_=A["bias_tiles"].rearrange("h c s t -> s h c t")), writes=[bt_b])
        P.dma("sp", lambda e: e.dma_start(out=b31[:], in_=A["b31"].partition_broadcast(128)), writes=[bt_b])
        P.op("dve", lambda e: e.tensor_tensor(out=bt[:].rearrange("p h c t -> p h (c t)"),
                                              in0=btf[:].rearrange("p h c t -> p h (c t)"),
                                              in1=b31[:, :].unsqueeze(2).broadcast_to([128, 8, 256]), op=ALU.subtract),
             reads=[bt_b], writes=[bt_b])
        cmask = sb("cmask", [128, 128], F32)
        onesblk = sb("onesblk", [128, 128], BF16)
        cm_b = Buf()
        pw2 = sb("pw2", [128, 2 * NITER], F32)
        halfs = sb("halfs", [128, 2 * NITER], F32)
        hf_b = Buf()

        def mkc(e):
            e.memset(cmask[:], 0.0)
            e.affine_select(out=cmask[:], in_=cmask[:], pattern=[[-1, 128]], compare_op=ALU.is_ge, fill=NEG, base=0,
                            channel_multiplier=1)
            for j in range(NITER):
                e.memset(pw2[:, j:j + 1], 0.5 ** (j + 1))
                e.memset(pw2[:, NITER + j:NITER + j + 1], 0.5 ** (j + 2))
            e.memset(onesblk[:], 0.0)
            e.memset(onesblk[0:64, 0:64], 1.0)
            return e.memset(onesblk[64:128, 64:128], 1.0)

        P.op("pool", mkc, writes=[cm_b])
        nm = Normer(P, nc, es, C, A["mix_norm"], "an", nslots=1)
        xbuf = sb("axbuf", [128, 2, D], F32)
        xr = Ring([(i, Buf()) for i in range(2)])
        xn = sb("axn", [128, 8, 128], BF16)
        xn_b = Buf()
        q_i = sb("q_i", [128, 2, 4, 128], BF16)
        qi_i = sb("qi_i", [128, 4, 128], BF16)
        wi_i = sb("wi_i", [128, 8], F32)
        q_bs, qi_b, wi_b = [Buf(), Buf()], Buf(), Buf()
        sq = sb("asq", [128, 512], BF16)
        rn = sb("arn", [128, 512], F32)
        sq_b, rn_b = Buf(), Buf()
        sc = sb("sc", [128, S], F32)
        sc_b = Buf()
        rbuf = sb("rbuf", [128, 2, 512], F32)
        rr = Ring([(i, Buf()) for i in range(2)])
        junk = sb("ajunk", [128, S], BF16)
        junk_b = Buf()
        bs = sb("bs", [128, 8], F32)
        bs_b = Buf()
        maskT = sb("maskT", [128, 2, NT, 128], BF16)
        mT_bs = [Buf(), Buf()]
        Et = sb("Et", [128, 3, 4, 128], BF16)
        er = Ring([(i, Buf()) for i in range(3)])
        Pt = sb("Pt", [128, 8, 4, 128], BF16)
        ptr = Ring([(i, Buf()) for i in range(8)])
        rden = sb("rden", [128, 2], F32)
        rdr = Ring([(i, Buf()) for i in range(2)])
        ytile = sb("ytile", [128, 512], BF16)
        yt_b = Buf()
        yst = sb("yst", [128, 4, 128], BF16)
        ys_b = Buf()
        pg = es.enter_context(nc.psum_tensor("apg", [128, 6, 512], F32))
        gr = Ring([(i, Buf()) for i in range(2)])
        accr = Ring([(i, Buf()) for i in range(2, 4)])
        lgr = Ring([(i, Buf()) for i in range(4, 6)])
        po = es.enter_context(nc.psum_tensor("apo", [128, 1, 512], F32))
        orr = Ring([(i, Buf()) for i in range(1)])
        pgb = lambda i: pg[:, i, :].bitcast(BF16)
        if SBUF_DEBUG:
            print("attn sbuf remaining", nc.sbuf_bytes_remaining)
        xview = x_dram.rearrange("(t p) d -> t p d", p=128)
        yav = ya_dram.rearrange("(m p) s -> p m s", p=128)

        def front(i):
            ts_ = slice(i * 128, (i + 1) * 128)
            nkb = i + 1
            W = nkb * 128
            q_b = q_bs[i % 2]
            kT_b, kiT_b, va_b = kT_bs[i], kiT_bs[i], va_bs[i]
            xi, xb_ = xr.next()
            P.dma("sp", lambda e, i=i, xi=xi: e.dma_start(out=xbuf[:, xi, :], in_=xview[i]), writes=[xb_])
            nm.run(xbuf[:, xi, :], xb_, xn, xn_b, 0)
            yield

            def proj4(c0, bank):
                bi, bb = bank

                def th(e):
                    ins = None
                    for m in range(4):
                        for kc in range(8):
                            ins = e.matmul(pg[:, bi, m * 128:(m + 1) * 128], lhsT=wa[:, kc, c0 + m * 128:c0 + (m + 1) * 128],
                                           rhs=xn[:, kc, :], start=(kc == 0), stop=(kc == 7))
                    return ins

                P.op("pe", th, reads=[wa_b, xn_b], writes=[bb])

            for which, c0 in ((0, 0), (1, 512)):
                bank = gr.next()
                proj4(c0, bank)
                P.op("act", lambda e, bi=bank[0]: e.activation(out=sq[:], in_=pg[:, bi, :], func=AF.Square),
                     reads=[bank[1]], writes=[sq_b])
                bank2 = gr.next()
                P.op("pe", lambda e, bi=bank2[0]: e.matmul(pg[:, bi, :], lhsT=onesblk[:], rhs=sq[:], start=True, stop=True),
                     reads=[cm_b, sq_b], writes=[bank2[1]])
                P.op("act", lambda e, bi=bank2[0]: e.activation(out=rn[:], in_=pg[:, bi, :], func=AF.Ln, scale=1.0 / 64,
                                                                bias=C.eps[:, 0:1]), reads=[bank2[1], C.b], writes=[rn_b])
                P.op("act", lambda e: e.activation(out=rn[:], in_=rn[:], func=AF.Exp, scale=-0.5), reads=[rn_b], writes=[rn_b])
                if which == 0:
                    P.op("dve", lambda e, bi=bank[0]: e.scalar_tensor_tensor(
                        out=q_i[:, i % 2].rearrange("p m t -> p (m t)"), in0=pg[:, bi, :], scalar=gqk[:, 0:1], in1=rn[:],
                        op0=ALU.mult, op1=ALU.mult), reads=[bank[1], rn_b, gqk_b], writes=[q_b])
                else:
                    P.op("dve", lambda e, bi=bank[0], ts_=ts_: e.scalar_tensor_tensor(
                        out=kT[:, :, ts_], in0=pg[:, bi, :].rearrange("p (m t) -> p m t", m=4), scalar=gqk[:, 1:2],
                        in1=rn[:].rearrange("p (m t) -> p m t", m=4), op0=ALU.mult, op1=ALU.mult),
                        reads=[bank[1], rn_b, gqk_b], writes=[kT_b])
                yield
            bank = gr.next()
            proj4(1536, bank)
            P.op("act", lambda e, bi=bank[0]: e.activation(out=qi_i[:].rearrange("p m t -> p (m t)"), in_=pg[:, bi, :],
                                                           func=AF.Copy), reads=[bank[1]], writes=[qi_b])
            yield
            bank = gr.next()

            def kiw(e, bi=bank[0]):
                for kc in range(8):
                    e.matmul(pg[0:64, bi, 0:128], lhsT=wa[:, kc, 2048:2112], rhs=xn[:, kc, :], start=(kc == 0), stop=(kc == 7))
                for kc in range(8):
                    e.matmul(pg[64:128, bi, 0:128], lhsT=wa[:, kc, ATTN_IN:ATTN_IN + 64], rhs=xn[:, kc, :], start=(kc == 0),
                             stop=(kc == 7))
                ins = None
                for kc in range(8):
                    ins = e.matmul(pg[:, bi, 128:136], lhsT=xn[:, kc, :], rhs=wa[:, kc, 2112:2120], start=(kc == 0), stop=(kc == 7))
                return ins

            P.op("pe", kiw, reads=[wa_b, xn_b], writes=[bank[1]])
            P.op("act", lambda e, bi=bank[0], ts_=ts_: e.activation(out=kiT[:, ts_], in_=pg[:, bi, 0:128], func=AF.Copy),
                 reads=[bank[1]], writes=[kiT_b])
            P.op("dve", lambda e, bi=bank[0]: e.tensor_copy(out=wi_i[:], in_=pg[:, bi, 128:136]), reads=[bank[1]], writes=[wi_b])
            yield
            bank = gr.next()

            def vmm(e, bi=bank[0]):
                ins = None
                for kc in range(8):
                    ins = e.matmul(pg[:, bi, :], lhsT=xn[:, kc, :], rhs=wa[:, kc, 1024:1536], start=(kc == 0), stop=(kc == 7))
                return ins

            P.op("pe", vmm, reads=[wa_b, xn_b], writes=[bank[1]])
            P.op("act", lambda e, bi=bank[0], i=i: e.activation(out=vaug[:, i, :, 0:64],
                                                                in_=pg[:, bi, :].rearrange("p (h d) -> p h d", h=8), func=AF.Copy),
                 reads=[bank[1]], writes=[va_b])
            yield

            for gk in range((nkb + 3) // 4):
                w_ = min(512, W - gk * 512)
                abank = accr.next()
                for h in range(8):
                    hb = (h % 2) * 64
                    bank = gr.next()
                    P.op("pe", lambda e, bi=bank[0], h=h, hb=hb, gk=gk, w_=w_: e.matmul(
                        pg[:, bi, 0:w_], lhsT=qi_i[hb:hb + 64, h // 2, :], rhs=kiT[hb:hb + 64, gk * 512:gk * 512 + w_],
                        start=True, stop=True), reads=[qi_b] + kiT_bs[gk * 4:gk * 4 + (w_ // 128)], writes=[bank[1]])
                    ri, rb_ = rr.next()
                    P.op("act", lambda e, bi=bank[0], ri=ri, w_=w_: e.activation(out=rbuf[:, ri, 0:w_], in_=pg[:, bi, 0:w_],
                                                                                 func=AF.Relu), reads=[bank[1]], writes=[rb_])
                    if h == 0:
                        P.op("dve", lambda e, ri=ri, ai=abank[0], w_=w_: e.tensor_scalar(
                            out=pg[:, ai, 0:w_], in0=rbuf[:, ri, 0:w_], scalar1=wi_i[:, 0:1], scalar2=None,
                            op0=ALU.mult), reads=[rb_, wi_b], writes=[abank[1]])
                    elif h < 7:
                        P.op("dve", lambda e, ri=ri, ai=abank[0], w_=w_, h=h: e.scalar_tensor_tensor(
                            out=pg[:, ai, 0:w_], in0=rbuf[:, ri, 0:w_], scalar=wi_i[:, h:h + 1],
                            in1=pg[:, ai, 0:w_], op0=ALU.mult, op1=ALU.add), reads=[rb_, wi_b, abank[1]], writes=[abank[1]])
                    else:
                        P.op("dve", lambda e, ri=ri, ai=abank[0], gk=gk, w_=w_, h=h: e.scalar_tensor_tensor(
                            out=sc[:, gk * 512:gk * 512 + w_], in0=rbuf[:, ri, 0:w_], scalar=wi_i[:, h:h + 1],
                            in1=pg[:, ai, 0:w_], op0=ALU.mult, op1=ALU.add), reads=[rb_, wi_b, abank[1]], writes=[sc_b])
                    yield
            P.op("dve", lambda e, ts_=ts_: e.tensor_tensor(out=sc[:, ts_], in0=sc[:, ts_], in1=cmask[:], op=ALU.add),
                 reads=[sc_b, cm_b], writes=[sc_b])
            if i < 2:
                P.op("dve", lambda e: e.memset(bs[:, 0:1], -1.0e29), writes=[bs_b])
            else:
                P.op("dve", lambda e, i=i: e.tensor_reduce(out=bs[:, 0:1], in_=sc[:, 0:i * 128], axis=AX.X, op=ALU.min),
                     reads=[sc_b], writes=[bs_b])
                P.op("dve", lambda e, W=W: e.tensor_reduce(out=bs[:, 6:7], in_=sc[:, 0:W], axis=AX.X, op=ALU.max),
                     reads=[sc_b, bs_b], writes=[bs_b])
                P.op("dve", lambda e: e.tensor_tensor(out=bs[:, 1:2], in0=bs[:, 6:7], in1=bs[:, 0:1], op=ALU.subtract),
                     reads=[bs_b], writes=[bs_b])
                P.op("dve", lambda e: e.tensor_tensor(out=halfs[:], in0=bs[:, 1:2].broadcast_to([128, 2 * NITER]), in1=pw2[:],
                                                      op=ALU.mult), reads=[bs_b, cm_b], writes=[hf_b])
                P.op("dve", lambda e: e.tensor_tensor(out=bs[:, 3:4], in0=bs[:, 0:1], in1=halfs[:, 0:1], op=ALU.add),
                     reads=[bs_b, hf_b], writes=[bs_b])
                nit = min(NITER, int(math.ceil(math.log2(W))) + 4)
                for it in range(nit):
                    P.op("dve", lambda e: e.tensor_scalar(out=junk[:, 0:W], in0=sc[:, 0:W], scalar1=bs[:, 3:4], scalar2=None,
                                                          op0=ALU.is_ge, op1=ALU.add, accum_out=bs[:, 4:5]),
                         reads=[sc_b, bs_b], writes=[junk_b, bs_b])
                    P.op("dve", lambda e, it=it: e.scalar_tensor_tensor(out=bs[:, 5:6], in0=bs[:, 4:5], scalar=TOPK - 0.5,
                                                                        in1=halfs[:, it:it + 1], op0=ALU.is_ge, op1=ALU.mult),
                         reads=[bs_b, hf_b], writes=[bs_b])
                    P.op("dve", lambda e, it=it: e.scalar_tensor_tensor(out=bs[:, 3:4], in0=bs[:, 5:6],
                                                                        scalar=halfs[:, NITER + it:NITER + it + 1],
                                                                        in1=bs[:, 3:4], op0=ALU.subtract, op1=ALU.add),
                         reads=[bs_b, hf_b], writes=[bs_b])
                    yield
                P.op("dve", lambda e, nit=nit: e.tensor_tensor(out=bs[:, 0:1], in0=bs[:, 3:4], in1=halfs[:, NITER + nit - 1:NITER + nit],
                                                               op=ALU.subtract), reads=[bs_b, hf_b], writes=[bs_b])
            P.op("dve", lambda e, W=W: e.tensor_scalar(out=junk[:, 0:W], in0=sc[:, 0:W], scalar1=bs[:, 0:1], scalar2=None,
                                                       op0=ALU.is_ge), reads=[sc_b, bs_b], writes=[junk_b])
            for j0 in range(0, nkb, 8):
                nb = min(8, nkb - j0)
                bank = gr.next()

                def trm(e, bi=bank[0], j0=j0, nb=nb):
                    ins = None
                    for jj in range(nb):
                        ins = e.transpose(out=pgb(bi)[:, jj * 128:(jj + 1) * 128], in_=junk[:, (j0 + jj) * 128:(j0 + jj + 1) * 128],
                                          identity=C.ident_bf[:])
                    return ins

                P.op("pe", trm, reads=[junk_b, C.b], writes=[bank[1]])
                P.op("act", lambda e, bi=bank[0], j0=j0, nb=nb: e.activation(
                    out=maskT[:, i % 2, j0:j0 + nb, :].rearrange("p j t -> p (j t)"), in_=pgb(bi)[:, 0:nb * 128], func=AF.Copy),
                    reads=[bank[1]], writes=[mT_bs[i % 2]])
                yield
            yield

        def back(i):
            ts_ = slice(i * 128, (i + 1) * 128)
            nkb = i + 1
            q_b = q_bs[i % 2]
            mT_b = mT_bs[i % 2]
            items = [(h, j0, min(4, nkb - j0)) for h in range(8) for j0 in range(0, nkb, 4)]
            DEPTH = 6
            st = {}
            obs = {}
            for k in range(len(items) + DEPTH):
                if k < len(items):
                    h, j0, nb = items[k]
                    hb = (h % 2) * 64
                    m = h // 2
                    if j0 == 0:
                        obs[h] = orr.next()
                    bank = lgr.next()

                    def qk(e, bi=bank[0], j0=j0, nb=nb, h=h, hb=hb, m=m):
                        ins = None
                        for jj in range(nb):
                            j = j0 + jj
                            near = j >= i - 1
                            ins = e.matmul(pg[:, bi, jj * 128:(jj + 1) * 128], lhsT=kT[hb:hb + 64, m, j * 128:(j + 1) * 128],
                                           rhs=q_i[hb:hb + 64, i % 2, m, :], start=True, stop=not near)
                            if near:
                                ins = e.matmul(pg[:, bi, jj * 128:(jj + 1) * 128], lhsT=C.ident_bf[:],
                                               rhs=bt[:, h, 0 if j == i else 1, :], start=False, stop=True)
                        return ins

                    P.op("pe", qk, reads=kT_bs[j0:j0 + nb] + [q_b, bt_b, C.b], writes=[bank[1]])
                    ei, eb = er.next()
                    P.op("act", lambda e, bi=bank[0], ei=ei, nb=nb: e.activation(
                        out=Et[:, ei, 0:nb, :].rearrange("p j t -> p (j t)"), in_=pg[:, bi, 0:nb * 128], func=AF.Exp),
                        reads=[bank[1]], writes=[eb])
                    pi, pb_ = ptr.next()
                    P.op("pool", lambda e, ei=ei, pi=pi, j0=j0, nb=nb: e.tensor_tensor(
                        out=Pt[:, pi, 0:nb, :], in0=Et[:, ei, 0:nb, :], in1=maskT[:, i % 2, j0:j0 + nb, :], op=ALU.mult),
                        reads=[eb, mT_b], writes=[pb_])
                    st[k] = (pi, pb_)
                kk = k - DEPTH
                if kk >= 0:
                    h, j0, nb = items[kk]
                    pi, pb_ = st.pop(kk)
                    ob = obs[h]

                    def pv(e, oi=ob[0], pi=pi, j0=j0, nb=nb, h=h):
                        ins = None
                        for jj in range(nb):
                            j = j0 + jj
                            ins = e.matmul(po[:, oi, 0:65], lhsT=Pt[:, pi, jj, :], rhs=vaug[:, j, h, :], start=(j == 0),
                                           stop=(j == i))
                        return ins

                    P.op("pe", pv, reads=[pb_] + va_bs[j0:j0 + nb], writes=[ob[1]])
                    if j0 + nb == nkb:
                        di, db = rdr.next()
                        P.op("dve", lambda e, oi=ob[0], di=di: e.reciprocal(out=rden[:, di:di + 1], in_=po[:, oi, 64:65]),
                             reads=[ob[1]], writes=[db])
                        P.op("act", lambda e, oi=ob[0], di=di, h=h: e.activation(out=ytile[:, h * 64:(h + 1) * 64],
                                                                                 in_=po[:, oi, 0:64], func=AF.Copy,
                                                                                 scale=rden[:, di:di + 1]),
                             reads=[ob[1], db], writes=[yt_b])
                yield
            bank = lgr.next()

            def try_(e, bi=bank[0]):
                ins = None
                for m in range(4):
                    ins = e.transpose(out=pgb(bi)[:, m * 128:(m + 1) * 128], in_=ytile[:, m * 128:(m + 1) * 128],
                                      identity=C.ident_bf[:])
                return ins

            P.op("pe", try_, reads=[yt_b, C.b], writes=[bank[1]])
            P.op("act", lambda e, bi=bank[0]: e.activation(out=yst[:].rearrange("p m t -> p (m t)"), in_=pgb(bi)[:, 0:512],
                                                           func=AF.Copy), reads=[bank[1]], writes=[ys_b])
            P.dma("sp", lambda e: e.dma_start(out=yav[:, :, ts_], in_=yst[:]), reads=[ys_b])
            yield

        def run2(f, b):
            fa, ba = f is not None, b is not None
            while fa or ba:
                if fa:
                    try:
                        next(f)
                    except StopIteration:
                        fa = False
                if ba:
                    try:
                        next(b)
                    except StopIteration:
                        ba = False

        for i in range(NT + 1):
            run2(front(i) if i < NT else None, back(i - 1) if i >= 1 else None)
    P.barrier()


WEIGHT_SPECS = [
    ("mix_norm", [1, D]), ("w_in", [D, 5992]), ("attn_q_norm", [1, 64]), ("attn_k_norm", [1, 64]),
    ("bias_tiles", [8, 2, 128, 128]), ("b31", [1, 8]), ("rwkv_mu", [1, RWKV_IN]), ("rwkv_w0", [1, 512]), ("rwkv_w2", [64, 512]),
    ("rwkv_a0", [1, 512]), ("rwkv_a2", [64, 512]), ("rwkv_g2", [160, 512]), ("rwkv_k_k", [1, 512]),
    ("rwkv_k_a", [1, 512]), ("rwkv_r_k", [1, 512]), ("rwkv_ln_w", [1, 512]), ("rwkv_ln_b", [1, 512]),
    ("w_branch_attn", [512, D]), ("w_branch_rwkv", [512, D]), ("w_out", [D, D]), ("ffn_norm", [1, D]),
    ("w_gate_up", [D, 2 * FFN_H]), ("w_down", [FFN_H, D]),
]


def build_program(phases=("attn", "rwkv", "merge", "ffn"), debug=False):
    nc = bass.Bass("TRN2", target_bir_lowering=False)
    A = {}
    A["x"] = nc.dram_tensor("x", [S, D], F32, kind="ExternalInput").ap()
    for name, shp in WEIGHT_SPECS:
        A[name] = nc.dram_tensor(name, shp, F32, kind="ExternalInput").ap()
    out = nc.dram_tensor("out", [S, D], F32, kind="ExternalOutput").ap()
    def kind(prod, cons):
        if not debug:
            return "Internal"
        if prod in phases and cons not in phases:
            return "ExternalOutput"
        if prod not in phases and cons in phases:
            return "ExternalInput"
        return "Internal"

    ya = nc.dram_tensor("ya_scr", [512, S], BF16, kind=kind("attn", "merge")).ap()
    yr = nc.dram_tensor("yr_scr", [512, S], BF16, kind=kind("rwkv", "merge")).ap()
    hs = nc.dram_tensor("h_scr", [S, D], F32, kind=kind("merge", "ffn")).ap()
    P = Prog(nc)
    final_ops = []
    with ExitStack() as es:
        C = make_consts(P, nc, es)
        P.barrier()
        if "attn" in phases:
            phase_attn(P, nc, C, A, ya)
        if "rwkv" in phases:
            phase_rwkv(P, nc, C, A, yr)
        if "merge" in phases:
            phase_merge(P, nc, C, A["x"], ya, yr, hs, A["mix_norm"], A["w_in"], A["w_branch_attn"],
                        A["w_branch_rwkv"], A["w_out"])
        if "ffn" in phases:
            phase_ffn(P, nc, C, hs, out, A["ffn_norm"], A["w_gate_up"], A["w_down"], final_ops)
        P.emit(final_wait_ops=final_ops)
    return nc


def t5_bucket_np(d):
    d = np.maximum(d, 0)
    max_exact = 16
    log_ratio = np.log(np.maximum(d, 1).astype(np.float32) / max_exact) / math.log(128 / max_exact)
    large = np.minimum(max_exact + (log_ratio * 16).astype(np.int32), 31)
    return np.where(d < max_exact, d, large)


def host_layout(inputs):
    w = {}
    for name, shp in WEIGHT_SPECS:
        if name in ("bias_tiles", "b31"):
            continue
        w[name] = np.ascontiguousarray(np.asarray(inputs[name], dtype=np.float32).reshape(shp))
    s_idx = np.arange(128)[:, None]
    t_idx = np.arange(128)[None, :]
    rb = np.asarray(inputs["rel_bias"], dtype=np.float32)
    tiles = np.empty((8, 2, 128, 128), np.float32)
    for cls in range(2):
        bk = t5_bucket_np(t_idx - s_idx + 128 * cls)
        tiles[:, cls] = np.transpose(rb[bk], (2, 0, 1))
    w["bias_tiles"] = tiles
    w["b31"] = np.ascontiguousarray(rb[31:32, :])
    return w


_NC_CACHE = {}


def kernel(**inputs):
    x = np.asarray(inputs["x"], dtype=np.float32)
    w = host_layout(inputs)
    if "nc" not in _NC_CACHE:
        _NC_CACHE["nc"] = build_program()
    nc = _NC_CACHE["nc"]
    in_maps = []
    for b in range(8):
        m = dict(w)
        m["x"] = np.ascontiguousarray(x[b])
        in_maps.append(m)
    res = run_bass_kernel_spmd(nc, in_maps, core_ids=list(range(8)))
    return np.stack([np.asarray(r["out"], dtype=np.float32) for r in res.results], axis=0)
```

```python
import math
from contextlib import ExitStack

import numpy as np
import concourse.bass as bass
import concourse.mybir as mybir
from concourse.bass_utils import run_bass_kernel_spmd

F32 = mybir.dt.float32
BF16 = mybir.dt.bfloat16
AF = mybir.ActivationFunctionType
ALU = mybir.AluOpType
AX = mybir.AxisListType

S = 4096
D = 1024
NT = S // 128
ATTN_IN = 2120
RWKV_IN = 1824
FFN_H = 2816
RMS_EPS = 1e-6
GN_EPS = 64e-5
TOPK = 256
NEG = -1.0e30

ENGS = ("pe", "act", "dve", "pool", "sp")
SBUF_DEBUG = False


class Buf:
    __slots__ = ("name", "w", "r")

    def __init__(self, name=""):
        self.name = name
        self.w = None
        self.r = []


class Op:
    __slots__ = ("eng", "thunk", "deps", "is_dma", "sem", "val", "need_inc", "pos", "prev_on_sem")


class Prog:
    def __init__(self, nc, n_dma_sems=(56, 28)):
        self.nc = nc
        self.streams = {e: [] for e in ENGS}
        self.n_dma_sems = n_dma_sems
        self.dma_count = 0
        self.all_ops = []
        self.open_dmas = []

    def _hazards(self, op, reads, writes):
        deps = []
        for b in reads:
            if b.w is not None:
                deps.append(b.w)
        for b in writes:
            if b.w is not None:
                deps.append(b.w)
            deps.extend(b.r)
        for b in reads:
            b.r.append(op)
        for b in writes:
            b.w = op
            b.r = []
        return deps

    def op(self, eng, thunk, reads=(), writes=(), extra_deps=()):
        o = Op()
        o.eng = eng
        o.thunk = thunk
        o.is_dma = False
        o.need_inc = False
        o.sem = None
        o.val = None
        o.prev_on_sem = None
        deps = self._hazards(o, reads, writes) + list(extra_deps)
        seen = set()
        o.deps = []
        for d in deps:
            if d is o or id(d) in seen:
                continue
            if eng == "pe" and d.eng == "pe" and not d.is_dma:
                continue
            seen.add(id(d))
            o.deps.append(d)
        self.streams[eng].append(o)
        self.all_ops.append(o)
        return o

    def dma(self, eng, thunk, reads=(), writes=(), extra_deps=()):
        o = self.op(eng, thunk, reads, writes, extra_deps)
        o.is_dma = True
        o.pos = self.dma_count
        self.dma_count += 1
        self.open_dmas.append(o)
        return o

    def barrier(self):
        lasts = []
        for e in ENGS:
            for o in reversed(self.streams[e]):
                if not o.is_dma:
                    lasts.append(o)
                    break
        deps = lasts + self.open_dmas
        self.open_dmas = []
        for e in ENGS:
            self.op(e, lambda eng: eng.nop(), extra_deps=deps)

    def emit(self, final_wait_ops=()):
        nc = self.nc
        for o in self.all_ops:
            for d in o.deps:
                d.need_inc = True
        eng_sems = {e: nc.alloc_semaphore("s_" + e) for e in ENGS}
        ring_n = {"sp": self.n_dma_sems[0], "pool": self.n_dma_sems[1], "act": 2, "dve": 2, "pe": 2}
        dma_sems = {}
        dma_sem_val = {}
        dma_prev = {}
        qpos = {e: 0 for e in ENGS}
        for e in ENGS:
            if any(o.is_dma for o in self.streams[e]):
                for i in range(ring_n[e]):
                    dma_sems[(e, i)] = nc.alloc_semaphore("s_dma_%s%d" % (e, i))
                    dma_sem_val[(e, i)] = 0
                    dma_prev[(e, i)] = None
        cnt = {e: 0 for e in ENGS}
        for o in self.all_ops:
            if o.is_dma:
                kq = (o.eng, qpos[o.eng] % ring_n[o.eng])
                qpos[o.eng] += 1
                dma_sem_val[kq] += 16
                o.sem = ("dma", kq)
                o.val = dma_sem_val[kq]
                o.prev_on_sem = dma_prev[kq]
                dma_prev[kq] = o
            elif o.need_inc:
                cnt[o.eng] += 1
                o.sem = ("eng", o.eng)
                o.val = cnt[o.eng]

        def semh(key):
            return eng_sems[key[1]] if key[0] == "eng" else dma_sems[key[1]]

        engines = {"pe": "tensor", "act": "scalar", "dve": "vector", "pool": "gpsimd", "sp": "sync"}
        with nc.Block() as block:
            for e in ENGS:
                stream = self.streams[e]
                final = list(final_wait_ops) if e == "sp" else []

                def body(engine, stream=stream, final=final):
                    known = {}
                    for o in stream:
                        waits = {}
                        deps = list(o.deps)
                        if o.is_dma and o.prev_on_sem is not None:
                            deps.append(o.prev_on_sem)
                        for d in deps:
                            if known.get(d.sem, 0) >= d.val:
                                continue
                            if waits.get(d.sem, 0) < d.val:
                                waits[d.sem] = d.val
                        for key, val in waits.items():
                            engine.wait_ge(semh(key), val)
                            known[key] = val
                        ins = o.thunk(engine)
                        if o.is_dma:
                            ins.then_inc(semh(o.sem), 16)
                        elif o.need_inc:
                            ins.then_inc(semh(o.sem), 1)
                    for o in final:
                        engine.wait_ge(semh(o.sem), o.val)

                getattr(block, engines[e])(body)


class Ring:
    def __init__(self, items):
        self.items = items
        self.i = 0

    def next(self):
        it = self.items[self.i % len(self.items)]
        self.i += 1
        return it


def load_weight_bf16(P, nc, dst, dst_buf, w_ap, c0, c1, kchunks, eng="pool"):
    wv = w_ap.rearrange("(kc p) n -> p kc n", p=128)
    for kc in range(kchunks):
        for a in range(c0, c1, 2048):
            b = min(c1, a + 2048)
            P.dma(eng, lambda e, kc=kc, a=a, b=b: e.dma_start(out=dst[:, kc, a - c0:b - c0], in_=wv[:, kc, a:b]),
                  writes=[dst_buf])


def load_col_vec(P, nc, dst, dst_buf, v_ap, n):
    src = v_ap.rearrange("o (c p) -> p (o c)", p=128)
    P.dma("sp", lambda e: e.dma_start(out=dst, in_=src, allow_slow_non_contiguous=True), writes=[dst_buf])


class Consts:
    pass


def make_consts(P, nc, es):
    C = Consts()
    C.ident_bf = es.enter_context(nc.sbuf_tensor("ident_bf", [128, 128], BF16))
    C.ident_f = es.enter_context(nc.sbuf_tensor("ident_f", [128, 128], F32))
    C.eps = es.enter_context(nc.sbuf_tensor("eps_c", [128, 2], F32))
    C.b = Buf("consts")

    def mk(e):
        e.memset(C.ident_f[:], 0.0)
        e.affine_select(out=C.ident_f[:], in_=C.ident_f[:], pattern=[[-1, 128]], compare_op=ALU.not_equal,
                        fill=1.0, base=0, channel_multiplier=1)
        e.memset(C.eps[:, 0:1], RMS_EPS)
        return e.memset(C.eps[:, 1:2], GN_EPS)

    P.op("pool", mk, writes=[C.b])
    P.op("pool", lambda e: e.tensor_copy(out=C.ident_bf[:], in_=C.ident_f[:]), reads=[C.b], writes=[C.b])
    return C


class Normer:
    def __init__(self, P, nc, es, C, gain_ap, name, nslots=2):
        self.P, self.nc, self.C = P, nc, C
        self.gcol = es.enter_context(nc.sbuf_tensor(name + "_g", [128, 8], F32))
        self.gb = Buf(name + "_g")
        load_col_vec(P, nc, self.gcol[:, :], self.gb, gain_ap, 8)
        self.stat = es.enter_context(nc.sbuf_tensor(name + "_st", [128, nslots, 4], F32))
        self.junk = es.enter_context(nc.sbuf_tensor(name + "_junk", [128, 1024], BF16))
        self.xs = es.enter_context(nc.sbuf_tensor(name + "_xs", [128, nslots, 1024], BF16))
        self.tp = es.enter_context(nc.psum_tensor(name + "_tp", [128, nslots, 8, 128], BF16))
        self.ring = Ring([(i, Buf(), Buf(), Buf()) for i in range(nslots)])
        self.junkb = Buf()

    def run(self, xt_ap, xt_buf, dst, dst_buf, col0):
        P, C = self.P, self.C
        i, sb, xb, pb = self.ring.next()
        st = self.stat
        P.op("act", lambda e: e.activation(out=self.junk[:], in_=xt_ap, func=AF.Square, accum_out=st[:, i, 0:1]),
             reads=[xt_buf], writes=[self.junkb, sb])
        P.op("act", lambda e: e.activation(out=st[:, i, 1:2], in_=st[:, i, 0:1], func=AF.Ln, scale=1.0 / D,
                                           bias=C.eps[:, 0:1]), reads=[sb, C.b], writes=[sb])
        P.op("act", lambda e: e.activation(out=st[:, i, 2:3], in_=st[:, i, 1:2], func=AF.Exp, scale=-0.5),
             reads=[sb], writes=[sb])
        P.op("act", lambda e: e.activation(out=self.xs[:, i, :], in_=xt_ap, func=AF.Copy, scale=st[:, i, 2:3]),
             reads=[xt_buf, sb], writes=[xb])

        def tr(e):
            ins = None
            for kc in range(8):
                ins = e.transpose(out=self.tp[:, i, kc, :], in_=self.xs[:, i, kc * 128:(kc + 1) * 128],
                                  identity=C.ident_bf[:])
            return ins

        P.op("pe", tr, reads=[xb, C.b], writes=[pb])
        P.op("dve", lambda e: e.tensor_tensor(out=dst[:, :, col0:col0 + 128], in0=self.tp[:, i, :, :],
                                              in1=self.gcol[:, :].unsqueeze(2).broadcast_to([128, 8, 128]),
                                              op=ALU.mult), reads=[pb, self.gb], writes=[dst_buf])


def phase_ffn(P, nc, C, h_dram, out_dram, ffn_norm, w_gate_up, w_down, final_ops):
    with ExitStack() as es:
        wgu = es.enter_context(nc.sbuf_tensor("wgu", [128, 8, 2 * FFN_H], BF16))
        wd = es.enter_context(nc.sbuf_tensor("wd", [128, 22, D], BF16))
        wgu_b, wd_b = Buf("wgu"), Buf("wd")
        load_weight_bf16(P, nc, wgu, wgu_b, w_gate_up, 0, 2 * FFN_H, 8)
        load_weight_bf16(P, nc, wd, wd_b, w_down, 0, D, 22)
        nm = Normer(P, nc, es, C, ffn_norm, "fn")
        hbuf = es.enter_context(nc.sbuf_tensor("hbuf", [128, 2, D], F32))
        hr = Ring([(i, Buf()) for i in range(2)])
        hnT = es.enter_context(nc.sbuf_tensor("hnT", [128, 2, 8, 512], BF16))
        hnb = [Buf(), Buf()]
        actT = es.enter_context(nc.sbuf_tensor("actT", [128, 22, 512], BF16))
        actb = [Buf() for _ in range(22)]
        sg = es.enter_context(nc.sbuf_tensor("sg", [128, 2, 512], F32))
        sgr = Ring([(i, Buf()) for i in range(2)])
        ost = es.enter_context(nc.sbuf_tensor("ost", [128, 2, D], F32))
        ostr = Ring([(i, Buf()) for i in range(2)])
        pg = es.enter_context(nc.psum_tensor("pg", [128, 2, 512], F32))
        pu = es.enter_context(nc.psum_tensor("pu", [128, 2, 512], F32))
        po = es.enter_context(nc.psum_tensor("po", [128, 2, 512], F32))
        pgr = Ring([(i, Buf()) for i in range(2)])
        pur = Ring([(i, Buf()) for i in range(2)])
        por = Ring([(i, Buf()) for i in range(2)])
        hview = h_dram.rearrange("(t p) d -> t p d", p=128)
        oview = out_dram.rearrange("(t p) d -> t p d", p=128)
        def ffn_norm_chunk(c):
            cb = c % 2
            for j in range(4):
                t = c * 4 + j
                hi, hb_ = hr.next()
                P.dma("sp", lambda e, t=t, hi=hi: e.dma_start(out=hbuf[:, hi, :], in_=hview[t]), writes=[hb_])
                nm.run(hbuf[:, hi, :], hb_, hnT[:, cb], hnb[cb], j * 128)

        ffn_norm_chunk(0)
        for c in range(S // 512):
            cb = c % 2
            if c + 1 < S // 512:
                ffn_norm_chunk(c + 1)
            for m in range(22):
                gi, gbuf = pgr.next()
                ui, ubuf = pur.next()

                def mm(e, m=m, gi=gi, ui=ui, cb=cb):
                    for kc in range(8):
                        e.matmul(pg[:, gi, :], lhsT=wgu[:, kc, m * 128:(m + 1) * 128], rhs=hnT[:, cb, kc, :],
                                 start=(kc == 0), stop=(kc == 7))
                    ins = None
                    for kc in range(8):
                        ins = e.matmul(pu[:, ui, :], lhsT=wgu[:, kc, FFN_H + m * 128:FFN_H + (m + 1) * 128],
                                       rhs=hnT[:, cb, kc, :], start=(kc == 0), stop=(kc == 7))
                    return ins

                P.op("pe", mm, reads=[wgu_b, hnb[cb]], writes=[gbuf, ubuf])
                si, sbuf_ = sgr.next()
                P.op("act", lambda e, gi=gi, si=si: e.activation(out=sg[:, si, :], in_=pg[:, gi, :], func=AF.Silu),
                     reads=[gbuf], writes=[sbuf_])
                P.op("dve", lambda e, ui=ui, si=si, m=m: e.tensor_tensor(out=actT[:, m, :], in0=pu[:, ui, :],
                                                                         in1=sg[:, si, :], op=ALU.mult),
                     reads=[ubuf, sbuf_], writes=[actb[m]])
            for j in range(4):
                t = c * 4 + j
                oi, obuf = ostr.next()
                P.dma("sp", lambda e, t=t, oi=oi: e.dma_start(out=ost[:, oi, :], in_=hview[t]), writes=[obuf])
                for nh in range(2):
                    pi, pbuf = por.next()

                    def mmd(e, j=j, nh=nh, pi=pi):
                        ins = None
                        for m in range(22):
                            ins = e.matmul(po[:, pi, :], lhsT=actT[:, m, j * 128:(j + 1) * 128],
                                           rhs=wd[:, m, nh * 512:(nh + 1) * 512], start=(m == 0), stop=(m == 21))
                        return ins

                    P.op("pe", mmd, reads=actb + [wd_b], writes=[pbuf])
                    P.op("dve", lambda e, nh=nh, pi=pi, oi=oi: e.tensor_tensor(
                        out=ost[:, oi, nh * 512:(nh + 1) * 512], in0=po[:, pi, :],
                        in1=ost[:, oi, nh * 512:(nh + 1) * 512], op=ALU.add),
                        reads=[pbuf], writes=[obuf])
                final_ops.append(P.dma("sp", lambda e, t=t, oi=oi: e.dma_start(out=oview[t], in_=ost[:, oi, :]),
                                       reads=[obuf]))
    P.barrier()


def phase_merge(P, nc, C, x_dram, ya_dram, yr_dram, h_dram, mix_norm, w_in, w_ba, w_br, w_out):
    with ExitStack() as es:
        wg = es.enter_context(nc.sbuf_tensor("wg", [128, 8, 2 * D], BF16))
        wba = es.enter_context(nc.sbuf_tensor("wba", [128, 4, D], BF16))
        wbr = es.enter_context(nc.sbuf_tensor("wbr", [128, 4, D], BF16))
        wo = es.enter_context(nc.sbuf_tensor("wo", [128, 8, D], BF16))
        wg_b, wba_b, wbr_b, wo_b = Buf(), Buf(), Buf(), Buf()
        load_weight_bf16(P, nc, wg, wg_b, w_in, ATTN_IN + RWKV_IN, ATTN_IN + RWKV_IN + 2 * D, 8)
        load_weight_bf16(P, nc, wba, wba_b, w_ba, 0, D, 4)
        load_weight_bf16(P, nc, wbr, wbr_b, w_br, 0, D, 4)
        load_weight_bf16(P, nc, wo, wo_b, w_out, 0, D, 8)
        nm = Normer(P, nc, es, C, mix_norm, "mn")
        xbuf = es.enter_context(nc.sbuf_tensor("xbuf", [128, 2, 4, D], F32))
        xb = [[Buf() for _ in range(4)] for _ in range(2)]
        xnT = es.enter_context(nc.sbuf_tensor("xnT", [128, 2, 8, 512], BF16))
        xnb = [Buf(), Buf()]
        yaT = es.enter_context(nc.sbuf_tensor("yaT", [128, 2, 4, 512], BF16))
        yrT = es.enter_context(nc.sbuf_tensor("yrT", [128, 2, 4, 512], BF16))
        yab, yrb = [Buf(), Buf()], [Buf(), Buf()]
        mT = es.enter_context(nc.sbuf_tensor("mT", [128, 8, 512], BF16))
        mb = [Buf() for _ in range(8)]
        sg = es.enter_context(nc.sbuf_tensor("sgm", [128, 2, 2, 512], F32))
        sgr = Ring([(i, Buf()) for i in range(2)])
        tt = es.enter_context(nc.sbuf_tensor("ttm", [128, 2, 2, 512], F32))
        ttr = Ring([(i, Buf()) for i in range(2)])
        hst = es.enter_context(nc.sbuf_tensor("hst", [128, 2, D], F32))
        hstr = Ring([(i, Buf()) for i in range(2)])
        pga = es.enter_context(nc.psum_tensor("pga", [128, 2, 512], F32))
        pbr = es.enter_context(nc.psum_tensor("pbr", [128, 2, 512], F32))
        po = es.enter_context(nc.psum_tensor("pom", [128, 2, 512], F32))
        pgb, pbb = Buf(), Buf()
        por = Ring([(i, Buf()) for i in range(2)])
        xview = x_dram.rearrange("(t p) d -> t p d", p=128)
        hview = h_dram.rearrange("(t p) d -> t p d", p=128)
        yav = ya_dram.rearrange("(kc p) s -> p kc s", p=128)
        yrv = yr_dram.rearrange("(kc p) s -> p kc s", p=128)
        def merge_load_chunk(c):
            cb = c % 2
            P.dma("sp", lambda e: e.dma_start(out=yaT[:, cb], in_=yav[:, :, c * 512:(c + 1) * 512]), writes=[yab[cb]])
            P.dma("sp", lambda e: e.dma_start(out=yrT[:, cb], in_=yrv[:, :, c * 512:(c + 1) * 512]), writes=[yrb[cb]])
            for j in range(4):
                t = c * 4 + j
                P.dma("sp", lambda e, t=t, j=j: e.dma_start(out=xbuf[:, cb, j, :], in_=xview[t]), writes=[xb[cb][j]])
                nm.run(xbuf[:, cb, j, :], xb[cb][j], xnT[:, cb], xnb[cb], j * 128)

        merge_load_chunk(0)
        for c in range(S // 512):
            cb = c % 2
            if c + 1 < S // 512:
                merge_load_chunk(c + 1)
            for m in range(8):
                def mmg(e, m=m, cb=cb):
                    ins = None
                    for g in range(2):
                        for kc in range(8):
                            ins = e.matmul(pga[:, g, :], lhsT=wg[:, kc, g * D + m * 128:g * D + (m + 1) * 128],
                                           rhs=xnT[:, cb, kc, :], start=(kc == 0), stop=(kc == 7))
                    return ins

                P.op("pe", mmg, reads=[wg_b, xnb[cb]], writes=[pgb])

                def mmb(e, m=m, cb=cb):
                    ins = None
                    for kc in range(4):
                        ins = e.matmul(pbr[:, 0, :], lhsT=wba[:, kc, m * 128:(m + 1) * 128], rhs=yaT[:, cb, kc, :],
                                       start=(kc == 0), stop=(kc == 3))
                    for kc in range(4):
                        ins = e.matmul(pbr[:, 1, :], lhsT=wbr[:, kc, m * 128:(m + 1) * 128], rhs=yrT[:, cb, kc, :],
                                       start=(kc == 0), stop=(kc == 3))
                    return ins

                P.op("pe", mmb, reads=[wba_b, wbr_b, yab[cb], yrb[cb]], writes=[pbb])
                si, sbuf_ = sgr.next()
                P.op("act", lambda e, si=si: e.activation(out=sg[:, si], in_=pga[:, :, :], func=AF.Sigmoid),
                     reads=[pgb], writes=[sbuf_])
                ti, tbuf = ttr.next()
                P.op("dve", lambda e, si=si, ti=ti: e.tensor_tensor(out=tt[:, ti], in0=pbr[:, :, :], in1=sg[:, si],
                                                                    op=ALU.mult),
                     reads=[pbb, sbuf_], writes=[tbuf])
                P.op("pool", lambda e, ti=ti, m=m: e.tensor_tensor(out=mT[:, m, :], in0=tt[:, ti, 0, :],
                                                                   in1=tt[:, ti, 1, :], op=ALU.add),
                     reads=[tbuf], writes=[mb[m]])
            for j in range(4):
                t = c * 4 + j
                hi, hbuf_ = hstr.next()
                for nh in range(2):
                    pi, pbuf = por.next()

                    def mmo(e, j=j, nh=nh, pi=pi):
                        ins = None
                        for m in range(8):
                            ins = e.matmul(po[:, pi, :], lhsT=mT[:, m, j * 128:(j + 1) * 128],
                                           rhs=wo[:, m, nh * 512:(nh + 1) * 512], start=(m == 0), stop=(m == 7))
                        return ins

                    P.op("pe", mmo, reads=mb + [wo_b], writes=[pbuf])
                    P.op("dve", lambda e, j=j, nh=nh, pi=pi, hi=hi, cb=cb: e.tensor_tensor(
                        out=hst[:, hi, nh * 512:(nh + 1) * 512], in0=po[:, pi, :],
                        in1=xbuf[:, cb, j, nh * 512:(nh + 1) * 512], op=ALU.add),
                        reads=[pbuf, xb[cb][j]], writes=[hbuf_])
                P.dma("sp", lambda e, t=t, hi=hi: e.dma_start(out=hview[t], in_=hst[:, hi, :]), reads=[hbuf_])
    P.barrier()


TL = 128
C0 = math.exp(-0.5)


def col8(P, nc, es, name, v_ap):
    t = es.enter_context(nc.sbuf_tensor(name, [64, 8], F32))
    b = Buf(name)
    P.dma("sp", lambda e: e.dma_start(out=t[:, :], in_=v_ap.rearrange("o (h k) -> k (o h)", k=64),
                                      allow_slow_non_contiguous=True), writes=[b])
    return t, b


def phase_rwkv(P, nc, C, A, yr_dram):
    x_dram = A["x"]
    with ExitStack() as es:
        sb = lambda name, shape, dt=F32: es.enter_context(nc.sbuf_tensor(name, shape, dt))
        wr = sb("wr", [128, 8, RWKV_IN], BF16)
        wmu = sb("wmu", [128, 8, RWKV_IN], BF16)
        wr_b, wmu_b, mub_b = Buf(), Buf(), Buf()
        load_weight_bf16(P, nc, wr, wr_b, A["w_in"], ATTN_IN, ATTN_IN + RWKV_IN, 8)
        with nc.sbuf_tensor("mub", [128, RWKV_IN], F32) as mub:
            P.dma("sp", lambda e: e.dma_start(out=mub[:], in_=A["rwkv_mu"].partition_broadcast(128)), writes=[mub_b])
            for kc in range(8):
                P.op("pool", lambda e, kc=kc: e.tensor_tensor(out=wmu[:, kc, :], in0=wr[:, kc, :], in1=mub[:],
                                                              op=ALU.mult), reads=[wr_b, mub_b], writes=[wmu_b])
        P.barrier()
        w2 = sb("w2", [64, 512], BF16)
        a2 = sb("a2", [64, 512], BF16)
        g2 = sb("g2", [64, 3, 512], BF16)
        lw_b = Buf()
        P.dma("pool", lambda e: e.dma_start(out=w2[:], in_=A["rwkv_w2"]), writes=[lw_b])
        P.dma("pool", lambda e: e.dma_start(out=a2[:], in_=A["rwkv_a2"]), writes=[lw_b])
        P.dma("pool", lambda e: e.dma_start(out=g2[:, 0, :], in_=A["rwkv_g2"][0:64, :]), writes=[lw_b])
        P.dma("pool", lambda e: e.dma_start(out=g2[:, 1, :], in_=A["rwkv_g2"][64:128, :]), writes=[lw_b])
        P.dma("pool", lambda e: e.dma_start(out=g2[0:32, 2, :], in_=A["rwkv_g2"][128:160, :]), writes=[lw_b])
        cw0, b_w0 = col8(P, nc, es, "cw0", A["rwkv_w0"])
        ca0, b_a0 = col8(P, nc, es, "ca0", A["rwkv_a0"])
        ckk, b_kk = col8(P, nc, es, "ckk", A["rwkv_k_k"])
        cka, b_ka = col8(P, nc, es, "cka", A["rwkv_k_a"])
        crk, b_rk = col8(P, nc, es, "crk", A["rwkv_r_k"])
        clw, b_lw = col8(P, nc, es, "clw", A["rwkv_ln_w"])
        clb, b_lb = col8(P, nc, es, "clb", A["rwkv_ln_b"])
        msk = sb("rmsk", [64, 3, 64], F32)
        ones_bf = sb("ones_bf", [64, 64], BF16)
        rmask = sb("rmask", [64, 8, TL // 64, 64], F32)
        mk_b = Buf()

        def mkmasks(e):
            e.memset(msk[:], 1.0)
            e.memset(ones_bf[:], 1.0)
            e.memset(rmask[:], 1.0)
            e.memset(rmask[:, :, :, 0:1], 0.0)
            e.affine_select(out=msk[:, 0, :], in_=msk[:, 0, :], pattern=[[1, 64]], compare_op=ALU.is_ge,
                            fill=0.0, base=-1, channel_multiplier=-1)
            e.affine_select(out=msk[:, 1, :], in_=msk[:, 1, :], pattern=[[1, 64]], compare_op=ALU.is_ge,
                            fill=0.0, base=0, channel_multiplier=-1)
            return e.affine_select(out=msk[:, 2, :], in_=msk[:, 2, :], pattern=[[-1, 64]], compare_op=ALU.is_ge,
                                   fill=0.0, base=-1, channel_multiplier=1)

        P.op("pool", mkmasks, writes=[mk_b])
        mb3 = lambda i: msk[:, i, :].unsqueeze(1).broadcast_to([64, 8, 64])
        idb = C.ident_bf[0:64, 0:64]
        idf3 = C.ident_f[0:64, 0:64].unsqueeze(1).broadcast_to([64, 8, 64])
        bc = lambda col: col[:, :].unsqueeze(2).broadcast_to([64, 8, TL])

        nm = Normer(P, nc, es, C, A["mix_norm"], "rn", nslots=1)
        xbuf = sb("rxbuf", [128, 1, D], F32)
        xr = Ring([(i, Buf()) for i in range(1)])
        xnx = sb("xnx", [128, 2, 8, TL + 1], BF16)
        xnb = [Buf(), Buf()]
        dxn = sb("dxn", [128, 8, TL], BF16)
        dxb = Buf()
        P.op("pool", lambda e: e.memset(xnx[:, 1, :, TL:TL + 1], 0.0), writes=[xnb[1]])
        F = {}
        FB = {}
        for nme, dt in [("r", F32), ("k", F32), ("sgd", F32), ("a", F32), ("kk", F32),
                        ("t1", F32), ("t2", F32), ("cum", F32), ("e1", F32), ("e2", F32), ("sqb", BF16)]:
            alias = {"e1": "t1", "e2": "a"}
            if nme in alias:
                F[nme] = F[alias[nme]]
                FB[nme] = FB[alias[nme]]
                continue
            F[nme] = sb("f_" + nme, [64, 8, TL], dt)
            FB[nme] = Buf(nme)
        X2 = {}
        X2B = {}
        for nme in ("rT", "aT", "bT", "kT", "bH", "kH", "vb", "g", "bon"):
            X2[nme] = sb("x_" + nme, [64, 2, 8, TL], BF16)
            X2B[nme] = [Buf(nme + "0"), Buf(nme + "1")]
        lora = sb("lora", [64, 5, TL], BF16)
        ztok = sb("ztok", [128, 2, 512], F32)
        ztr = Ring([(i, Buf()) for i in range(2)])
        lora_b = Buf()
        gC = sb("gC", [64, 2, 8, TL // 64], F32)
        gC_bs = [Buf(), Buf()]
        Sf = sb("Sf", [64, 8, 64], F32)
        Sb = sb("Sb", [64, 8, 64], BF16)
        St = sb("St", [64, 8, 64], F32)
        S_b, St_b = Buf(), Buf()
        P.op("dve", lambda e: e.memset(Sf[:], 0.0), writes=[S_b])
        P.op("dve", lambda e: e.memset(Sb[:], 0.0), reads=[S_b], writes=[S_b])
        ost = sb("rost", [64, 8, TL], BF16)
        ost_b = Buf()
        def pair(name, dt=BF16, n=2):
            t = sb(name, [64, n, 8, 64], dt)
            return t, [Buf() for _ in range(n)]
        Atok, Atok_b = pair("Atok")
        BHtok, BHtok_b = pair("BHtok")
        KHtok, KHtok_b = pair("KHtok")
        Vtok, Vtok_b = pair("Vtok")
        Mrb, Mrb_b = pair("Mrb")
        Mrk, Mrk_b = pair("Mrk")
        Lak, Lak_b = pair("Lak")
        Nn, Nn_b = pair("Nn", BF16, 4)
        Mm, Mm_b = pair("Mm", BF16, 4)
        Qq, Qq_b = pair("Qq", BF16, 4)
        WT, WT_b = pair("WT")
        Xx, Xx_b = pair("Xx")
        Uu, Uu_b = pair("Uu")
        Ys, Ys_b = pair("Ys", F32, 1)
        Yn, Yn_b = pair("Yn", BF16, 2)
        gst = sb("gst", [64, 2, 8, 6], F32)
        gst_b = [Buf(), Buf()]
        pb = es.enter_context(nc.psum_tensor("rpb", [128, 7, 512], F32))
        pr = Ring([(i, Buf()) for i in range(7)])
        pv = lambda i: pb[0:64, i, :].rearrange("p (h t) -> p h t", h=8)
        pvb = lambda i: pb[0:64, i, :].bitcast(BF16)[:, 0:512].rearrange("p (h t) -> p h t", h=8)

        xview = x_dram.rearrange("(t p) d -> t p d", p=128)

        def mm8(bank, parts, rd):
            i, bbuf = bank
            ops = [[(lf(h), rf(h)) for (lf, rf) in parts] for h in range(8)]

            def th(e):
                ins = None
                for h in range(8):
                    for pi, (l, r) in enumerate(ops[h]):
                        ins = e.matmul(pb[0:64, i, h * 64:(h + 1) * 64], lhsT=l, rhs=r,
                                       start=(pi == 0), stop=(pi == len(ops[h]) - 1))
                return ins

            P.op("pe", th, reads=rd, writes=[bbuf])

        def tr8(bank, src_fn, rd):
            i, bbuf = bank
            srcs = [src_fn(h) for h in range(8)]

            def th(e):
                ins = None
                v = pvb(i)
                for h in range(8):
                    ins = e.transpose(out=v[:, h, :], in_=srcs[h], identity=idb)
                return ins

            P.op("pe", th, reads=rd + [C.b], writes=[bbuf])

        NB = S // TL

        def prep(n):
            par = n % 2
            cb = n % 2
            pc = 1 - cb
            X = {k: X2[k][:, par] for k in X2}
            XB = {k: X2B[k][par] for k in X2}
            P.op("pool", lambda e: e.tensor_copy(out=xnx[:, cb, :, 0:1], in_=xnx[:, pc, :, TL:TL + 1]),
                 reads=[xnb[pc]], writes=[xnb[cb]])
            for j in range(TL // 128):
                xi, xb_ = xr.next()
                P.dma("sp", lambda e, t=n * (TL // 128) + j, xi=xi: e.dma_start(out=xbuf[:, xi, :], in_=xview[t]), writes=[xb_])
                nm.run(xbuf[:, xi, :], xb_, xnx[:, cb], xnb[cb], 1 + j * 128)
            P.op("pool", lambda e: e.tensor_tensor(out=dxn[:], in0=xnx[:, cb, :, 0:TL], in1=xnx[:, cb, :, 1:TL + 1],
                                                   op=ALU.subtract), reads=[xnb[cb]], writes=[dxb])
            yield

            def projtok(c0, ncol):
                bank = pr.next()
                i = bank[0]

                def th(e):
                    ins = None
                    for kc in range(8):
                        e.matmul(pb[:, i, 0:ncol], lhsT=xnx[:, cb, kc, 1:TL + 1], rhs=wr[:, kc, c0:c0 + ncol],
                                 start=(kc == 0), stop=False)
                    for kc in range(8):
                        ins = e.matmul(pb[:, i, 0:ncol], lhsT=dxn[:, kc, :], rhs=wmu[:, kc, c0:c0 + ncol],
                                       start=False, stop=(kc == 7))
                    return ins

                P.op("pe", th, reads=[wr_b, wmu_b, xnb[cb], dxb], writes=[bank[1]])
                zi, zb = ztr.next()
                P.op("act", lambda e: e.activation(out=ztok[:, zi, 0:ncol], in_=pb[:, i, 0:ncol], func=AF.Copy),
                     reads=[bank[1]], writes=[zb])
                return zi, zb

            def trz(zi, zb, cols, m):
                bank = pr.next()
                i = bank[0]

                def th(e):
                    ins = None
                    for q, c in enumerate(cols):
                        ins = e.transpose(out=pb[0:m, i, q * TL:(q + 1) * TL], in_=ztok[:, zi, c:c + m], identity=C.ident_f[:])
                    return ins

                P.op("pe", th, reads=[zb, C.b], writes=[bank[1]])
                return bank

            for qi, qn in enumerate(("r", "k", "v")):
                zi, zb = projtok(qi * 512, 512)
                yield
                for h0 in (0, 4):
                    bank = trz(zi, zb, [(h0 + q) * 64 for q in range(4)], 64)
                    dst, dstb = (X["vb"], XB["vb"]) if qn == "v" else (F[qn], FB[qn])
                    P.op("act", lambda e, dst=dst, h0=h0, i=bank[0]: e.activation(
                        out=dst[:, h0:h0 + 4, :], in_=pb[0:64, i, 0:4 * TL].rearrange("p (q t) -> p q t", q=4), func=AF.Copy),
                        reads=[bank[1]], writes=[dstb])
                    yield
            zi, zb = projtok(1536, 288)
            yield
            for li, (c0, m, fn) in enumerate([(0, 64, AF.Tanh), (64, 64, AF.Copy), (128, 64, AF.Sigmoid),
                                              (192, 64, AF.Sigmoid), (256, 32, AF.Sigmoid)]):
                bank = trz(zi, zb, [c0], m)
                P.op("act", lambda e, li=li, m=m, fn=fn, i=bank[0]: e.activation(out=lora[0:m, li, :], in_=pb[0:m, i, 0:TL],
                                                                                 func=fn),
                     reads=[bank[1]], writes=[lora_b])
                yield
            for h in range(8):
                bank = pr.next()
                P.op("pe", lambda e, h=h, i=bank[0]: e.matmul(pb[0:64, i, 0:TL], lhsT=w2[:, h * 64:(h + 1) * 64],
                                                              rhs=lora[:, 0, :], start=True, stop=True),
                     reads=[lw_b, lora_b], writes=[bank[1]])
                P.op("act", lambda e, h=h, i=bank[0]: e.activation(out=F["sgd"][:, h, :], in_=pb[0:64, i, 0:TL],
                                                                   func=AF.Sigmoid, bias=cw0[:, h:h + 1]),
                     reads=[bank[1], b_w0], writes=[FB["sgd"]])
                bank = pr.next()
                P.op("pe", lambda e, h=h, i=bank[0]: e.matmul(pb[0:64, i, 0:TL], lhsT=a2[:, h * 64:(h + 1) * 64],
                                                              rhs=lora[:, 1, :], start=True, stop=True),
                     reads=[lw_b, lora_b], writes=[bank[1]])
                P.op("act", lambda e, h=h, i=bank[0]: e.activation(out=F["a"][:, h, :], in_=pb[0:64, i, 0:TL],
                                                                   func=AF.Sigmoid, bias=ca0[:, h:h + 1]),
                     reads=[bank[1], b_a0], writes=[FB["a"]])
                bank = pr.next()

                def gmm(e, h=h, i=bank[0]):
                    e.matmul(pb[0:64, i, 0:TL], lhsT=g2[:, 0, h * 64:(h + 1) * 64], rhs=lora[:, 2, :], start=True, stop=False)
                    e.matmul(pb[0:64, i, 0:TL], lhsT=g2[:, 1, h * 64:(h + 1) * 64], rhs=lora[:, 3, :], start=False, stop=False)
                    return e.matmul(pb[0:64, i, 0:TL], lhsT=g2[0:32, 2, h * 64:(h + 1) * 64], rhs=lora[0:32, 4, :],
                                    start=False, stop=True)

                P.op("pe", gmm, reads=[lw_b, lora_b], writes=[bank[1]])
                P.op("act", lambda e, h=h, i=bank[0]: e.activation(out=X["g"][:, h, :], in_=pb[0:64, i, 0:TL], func=AF.Copy),
                     reads=[bank[1]], writes=[XB["g"]])
                yield
            P.op("dve", lambda e: e.tensor_tensor(out=F["kk"][:], in0=F["k"][:], in1=bc(ckk), op=ALU.mult),
                 reads=[FB["k"], b_kk], writes=[FB["kk"]])
            P.op("pool", lambda e: e.tensor_tensor(out=F["sqb"][:], in0=F["kk"][:], in1=F["kk"][:], op=ALU.mult),
                 reads=[FB["kk"]], writes=[FB["sqb"]])
            yield
            for h in range(8):
                bank = pr.next()
                P.op("pe", lambda e, h=h, i=bank[0]: e.matmul(pb[0:64, i, 0:TL], lhsT=ones_bf[:], rhs=F["sqb"][:, h, :],
                                                              start=True, stop=True),
                     reads=[mk_b, FB["sqb"]], writes=[bank[1]])
                P.op("act", lambda e, h=h, i=bank[0]: e.activation(out=F["t1"][:, h, :], in_=pb[0:64, i, 0:TL], func=AF.Sqrt),
                     reads=[bank[1]], writes=[FB["t1"]])
                if h % 2 == 1:
                    yield
            P.op("dve", lambda e: e.tensor_scalar(out=F["t1"][:], in0=F["t1"][:], scalar1=1e-12, scalar2=None,
                                                  op0=ALU.max), reads=[FB["t1"]], writes=[FB["t1"]])
            P.op("dve", lambda e: e.reciprocal(out=F["t1"][:], in_=F["t1"][:]), reads=[FB["t1"]], writes=[FB["t1"]])
            yield
            P.op("dve", lambda e: e.tensor_tensor(out=F["kk"][:], in0=F["kk"][:], in1=F["t1"][:], op=ALU.mult),
                 reads=[FB["kk"], FB["t1"]], writes=[FB["kk"]])
            P.op("dve", lambda e: e.scalar_tensor_tensor(out=F["t2"][:], in0=F["a"][:], scalar=-1.0, in1=bc(cka),
                                                         op0=ALU.add, op1=ALU.mult),
                 reads=[FB["a"], b_ka], writes=[FB["t2"]])
            yield
            P.op("dve", lambda e: e.scalar_tensor_tensor(out=F["k"][:], in0=F["t2"][:], scalar=1.0, in1=F["k"][:],
                                                         op0=ALU.add, op1=ALU.mult),
                 reads=[FB["t2"], FB["k"]], writes=[FB["k"]])
            P.op("pool", lambda e: e.tensor_tensor(out=F["t2"][:], in0=F["kk"][:], in1=F["a"][:], op=ALU.mult),
                 reads=[FB["kk"], FB["a"]], writes=[FB["t2"]])
            yield
            P.op("dve", lambda e: e.tensor_tensor_scan(out=F["cum"][:].rearrange("p h t -> p (h t)"),
                                                       data0=rmask[:].rearrange("p h c t -> p (h c t)"),
                                                       data1=F["sgd"][:].rearrange("p h t -> p (h t)"),
                                                       initial=0.0, op0=ALU.mult, op1=ALU.add),
                 reads=[FB["sgd"], mk_b], writes=[FB["cum"]])
            cum4 = F["cum"][:].rearrange("p h (c t) -> p h c t", t=64)
            yield
            P.op("act", lambda e: e.activation(out=F["e1"][:], in_=F["cum"][:], func=AF.Exp, scale=-C0),
                 reads=[FB["cum"]], writes=[FB["e1"]])
            P.op("dve", lambda e: e.tensor_tensor(out=X["rT"][:], in0=F["r"][:], in1=F["e1"][:], op=ALU.mult),
                 reads=[FB["r"], FB["e1"]], writes=[XB["rT"]])
            P.op("act", lambda e: e.activation(out=gC[:, par], in_=cum4[:, :, :, 63], func=AF.Exp, scale=-C0),
                 reads=[FB["cum"]], writes=[gC_bs[par]])
            yield
            P.op("pool", lambda e: e.tensor_tensor(out=F["e2"][:], in0=F["cum"][:], in1=F["sgd"][:], op=ALU.subtract),
                 reads=[FB["cum"], FB["sgd"]], writes=[FB["e2"]])
            P.op("act", lambda e: e.activation(out=F["e2"][:], in_=F["e2"][:], func=AF.Exp, scale=-C0),
                 reads=[FB["e2"]], writes=[FB["e2"]])
            P.op("dve", lambda e: e.scalar_tensor_tensor(out=X["aT"][:], in0=F["kk"][:], scalar=-1.0, in1=F["e2"][:],
                                                         op0=ALU.mult, op1=ALU.mult),
                 reads=[FB["kk"], FB["e2"]], writes=[XB["aT"]])
            yield
            P.op("act", lambda e: e.activation(out=F["e1"][:], in_=F["cum"][:], func=AF.Exp, scale=C0),
                 reads=[FB["cum"]], writes=[FB["e1"]])
            P.op("dve", lambda e: e.tensor_tensor(out=X["bT"][:], in0=F["t2"][:], in1=F["e1"][:], op=ALU.mult),
                 reads=[FB["t2"], FB["e1"]], writes=[XB["bT"]])
            P.op("pool", lambda e: e.tensor_tensor(out=X["kT"][:], in0=F["k"][:], in1=F["e1"][:], op=ALU.mult),
                 reads=[FB["k"], FB["e1"]], writes=[XB["kT"]])
            yield
            P.op("dve", lambda e: e.tensor_tensor(out=F["e2"][:].rearrange("p h (c t) -> p h c t", t=64),
                                                  in0=cum4[:, :, :, 63:64].broadcast_to([64, 8, TL // 64, 64]), in1=cum4,
                                                  op=ALU.subtract),
                 reads=[FB["cum"]], writes=[FB["e2"]])
            P.op("act", lambda e: e.activation(out=F["e2"][:], in_=F["e2"][:], func=AF.Exp, scale=-C0),
                 reads=[FB["e2"]], writes=[FB["e2"]])
            yield
            P.op("dve", lambda e: e.tensor_tensor(out=X["bH"][:], in0=F["t2"][:], in1=F["e2"][:], op=ALU.mult),
                 reads=[FB["t2"], FB["e2"]], writes=[XB["bH"]])
            P.op("pool", lambda e: e.tensor_tensor(out=X["kH"][:], in0=F["k"][:], in1=F["e2"][:], op=ALU.mult),
                 reads=[FB["k"], FB["e2"]], writes=[XB["kH"]])
            yield
            P.op("dve", lambda e: e.tensor_tensor(out=F["t1"][:], in0=F["r"][:], in1=F["k"][:], op=ALU.mult),
                 reads=[FB["r"], FB["k"]], writes=[FB["t1"]])
            P.op("pool", lambda e: e.tensor_tensor(out=F["sqb"][:], in0=F["t1"][:], in1=bc(crk), op=ALU.mult),
                 reads=[FB["t1"], b_rk], writes=[FB["sqb"]])
            yield
            for h in range(8):
                bank = pr.next()
                P.op("pe", lambda e, h=h, i=bank[0]: e.matmul(pb[0:64, i, 0:TL], lhsT=ones_bf[:], rhs=F["sqb"][:, h, :],
                                                              start=True, stop=True),
                     reads=[mk_b, FB["sqb"]], writes=[bank[1]])
                P.op("dve", lambda e, h=h, i=bank[0]: e.tensor_tensor(out=X["bon"][:, h, :], in0=pb[0:64, i, 0:TL],
                                                                      in1=X["vb"][:, h, :], op=ALU.mult),
                     reads=[bank[1], XB["vb"]], writes=[XB["bon"]])
                if h % 2 == 1:
                    yield

        def chunk_pre(n, c, out):
            par = n % 2
            X = {k: X2[k][:, par] for k in X2}
            XB = {k: X2B[k][par] for k in X2}
            cs = slice(c * 64, (c + 1) * 64)
            for (dst, dbs, src) in ((Atok, Atok_b, "aT"), (BHtok, BHtok_b, "bH"), (KHtok, KHtok_b, "kH"), (Vtok, Vtok_b, "vb")):
                bank = pr.next()
                tr8(bank, lambda h, src=src: X[src][:, h, cs], [XB[src]])
                P.op("act", lambda e, dst=dst, i=bank[0]: e.activation(out=dst[:, c], in_=pvb(i), func=AF.Copy),
                     reads=[bank[1]], writes=[dbs[c]])
                yield

            def gmat(lname, rname, mi, dst, dbs, slot):
                bank = pr.next()
                mm8(bank, [(lambda h: X[lname][:, h, cs], lambda h: X[rname][:, h, cs])], [XB[lname], XB[rname]])
                P.op("dve", lambda e, i=bank[0]: e.tensor_tensor(out=dst[:, slot], in0=pv(i), in1=mb3(mi), op=ALU.mult),
                     reads=[bank[1], mk_b], writes=[dbs[slot]])

            base = 2 * c
            gmat("bT", "aT", 0, Mm, Mm_b, base)
            yield
            gmat("bT", "rT", 1, Mrb, Mrb_b, c)
            yield
            gmat("kT", "aT", 0, Lak, Lak_b, c)
            yield
            gmat("kT", "rT", 1, Mrk, Mrk_b, c)
            yield
            gmat("aT", "bT", 2, Nn, Nn_b, base)
            yield
            P.op("pool", lambda e: e.tensor_tensor(out=Qq[:, base], in0=Mm[:, base], in1=idf3, op=ALU.add),
                 reads=[Mm_b[base], C.b], writes=[Qq_b[base]])
            ni = mi_ = qi_ = base
            for lvl in range(1, 6):
                nn_ = base + (1 - (ni - base))
                nm_ = base + (1 - (mi_ - base))
                nq_ = base + (1 - (qi_ - base))
                bank = pr.next()
                mm8(bank, [(lambda h: Mm[:, mi_, h, :], lambda h: Nn[:, ni, h, :])], [Mm_b[mi_], Nn_b[ni]])
                if lvl < 5:
                    bank2 = pr.next()
                    mm8(bank2, [(lambda h: Nn[:, ni, h, :], lambda h: Mm[:, mi_, h, :])], [Mm_b[mi_], Nn_b[ni]])
                P.op("act", lambda e, nn_=nn_, i=bank[0]: e.activation(out=Nn[:, nn_], in_=pv(i), func=AF.Copy),
                     reads=[bank[1]], writes=[Nn_b[nn_]])
                if lvl < 5:
                    P.op("dve", lambda e, nm_=nm_, i=bank2[0]: e.tensor_copy(out=Mm[:, nm_], in_=pv(i)),
                         reads=[bank2[1]], writes=[Mm_b[nm_]])
                    mi_ = nm_
                ni = nn_
                yield
                bank3 = pr.next()
                mm8(bank3, [(lambda h: Nn[:, ni, h, :], lambda h: Qq[:, qi_, h, :])], [Qq_b[qi_], Nn_b[ni]])
                P.op("dve", lambda e, nq_=nq_, qo=qi_, i=bank3[0]: e.tensor_tensor(out=Qq[:, nq_], in0=pv(i), in1=Qq[:, qo],
                                                                                   op=ALU.add),
                     reads=[bank3[1], Qq_b[qi_]], writes=[Qq_b[nq_]])
                qi_ = nq_
                yield
            bank = pr.next()
            mm8(bank, [(lambda h: Atok[:, c, h, :], lambda h: Qq[:, qi_, h, :])], [Atok_b[c], Qq_b[qi_]])
            P.op("act", lambda e, i=bank[0]: e.activation(out=WT[:, c], in_=pv(i), func=AF.Copy),
                 reads=[bank[1]], writes=[WT_b[c]])
            bank = pr.next()
            mm8(bank, [(lambda h: Lak[:, c, h, :], lambda h: Vtok[:, c, h, :])], [Lak_b[c], Vtok_b[c]])
            P.op("dve", lambda e, i=bank[0]: e.tensor_copy(out=Xx[:, c], in_=pv(i)), reads=[bank[1]], writes=[Xx_b[c]])
            out["q"] = qi_
            yield

        def chain(n, c, qi_):
            par = n % 2
            X = {k: X2[k][:, par] for k in X2}
            XB = {k: X2B[k][par] for k in X2}
            cs = slice(c * 64, (c + 1) * 64)
            bank = pr.next()
            mm8(bank, [(lambda h: WT[:, c, h, :], lambda h: Sb[:, h, :]),
                       (lambda h: Qq[:, qi_, h, :], lambda h: Xx[:, c, h, :])], [WT_b[c], S_b, Qq_b[qi_], Xx_b[c]])
            P.op("act", lambda e, i=bank[0]: e.activation(out=Uu[:, c], in_=pv(i), func=AF.Copy),
                 reads=[bank[1]], writes=[Uu_b[c]])
            banky = pr.next()
            mm8(banky, [(lambda h: X["rT"][:, h, cs], lambda h: Sb[:, h, :]),
                        (lambda h: Mrb[:, c, h, :], lambda h: Uu[:, c, h, :]),
                        (lambda h: Mrk[:, c, h, :], lambda h: Vtok[:, c, h, :])],
                [XB["rT"], S_b, Mrb_b[c], Uu_b[c], Mrk_b[c], Vtok_b[c]])
            banks = pr.next()
            mm8(banks, [(lambda h: BHtok[:, c, h, :], lambda h: Uu[:, c, h, :]),
                        (lambda h: KHtok[:, c, h, :], lambda h: Vtok[:, c, h, :])], [BHtok_b[c], Uu_b[c], KHtok_b[c], Vtok_b[c]])
            P.op("dve", lambda e: e.tensor_tensor(out=St[:], in0=Sf[:],
                                                  in1=gC[:, par, :, c:c + 1].broadcast_to([64, 8, 64]), op=ALU.mult),
                 reads=[S_b, gC_bs[par]], writes=[St_b])
            P.op("dve", lambda e, i=banks[0]: e.tensor_tensor(out=Sf[:], in0=pv(i), in1=St[:], op=ALU.add),
                 reads=[banks[1], St_b], writes=[S_b])
            P.op("act", lambda e: e.activation(out=Sb[:], in_=Sf[:], func=AF.Copy), reads=[S_b], writes=[S_b])
            yield
            y_b, yq_b, g_b = Ys_b[0], St_b, gst_b[c]
            P.op("act", lambda e, i=banky[0]: e.activation(out=Ys[:, 0], in_=pv(i), func=AF.Copy),
                 reads=[banky[1]], writes=[y_b])
            P.op("pool", lambda e: e.tensor_tensor(out=St[:], in0=Ys[:, 0], in1=Ys[:, 0], op=ALU.mult),
                 reads=[y_b], writes=[yq_b])
            P.op("dve", lambda e: e.tensor_reduce(out=gst[:, c, :, 0], in_=Ys[:, 0], axis=AX.X, op=ALU.add),
                 reads=[y_b], writes=[g_b])
            P.op("dve", lambda e: e.tensor_reduce(out=gst[:, c, :, 1], in_=St[:], axis=AX.X, op=ALU.add),
                 reads=[yq_b, g_b], writes=[g_b])
            yield
            P.op("dve", lambda e: e.tensor_scalar(out=gst[:, c, :, 2], in0=gst[:, c, :, 0], scalar1=1.0 / 64,
                                                  scalar2=None, op0=ALU.mult), reads=[g_b], writes=[g_b])
            P.op("dve", lambda e: e.tensor_tensor(out=gst[:, c, :, 3], in0=gst[:, c, :, 2], in1=gst[:, c, :, 2],
                                                  op=ALU.mult), reads=[g_b], writes=[g_b])
            P.op("dve", lambda e: e.scalar_tensor_tensor(out=gst[:, c, :, 4], in0=gst[:, c, :, 1], scalar=1.0 / 64,
                                                         in1=gst[:, c, :, 3], op0=ALU.mult, op1=ALU.subtract),
                 reads=[g_b], writes=[g_b])
            yield
            P.op("act", lambda e: e.activation(out=gst[:, c, :, 5], in_=gst[:, c, :, 4], func=AF.Sqrt,
                                               bias=C.eps[0:64, 1:2]), reads=[g_b, C.b], writes=[g_b])
            P.op("dve", lambda e: e.reciprocal(out=gst[:, c, :, 5], in_=gst[:, c, :, 5]), reads=[g_b], writes=[g_b])
            P.op("dve", lambda e: e.tensor_tensor(out=Ys[:, 0], in0=Ys[:, 0],
                                                  in1=gst[:, c, :, 2:3].broadcast_to([64, 8, 64]),
                                                  op=ALU.subtract), reads=[y_b, g_b], writes=[y_b])
            yield
            P.op("dve", lambda e: e.tensor_tensor(out=Yn[:, c], in0=Ys[:, 0],
                                                  in1=gst[:, c, :, 5:6].broadcast_to([64, 8, 64]),
                                                  op=ALU.mult), reads=[y_b, g_b], writes=[Yn_b[c]])
            bank = pr.next()
            tr8(bank, lambda h: Yn[:, c, h, :], [Yn_b[c]])
            bc64 = lambda col: col[:, :].unsqueeze(2).broadcast_to([64, 8, 64])
            P.op("dve", lambda e, i=bank[0]: e.tensor_tensor(out=Ys[:, 0], in0=pvb(i), in1=bc64(clw), op=ALU.mult),
                 reads=[bank[1], b_lw, Yn_b[c]], writes=[y_b])
            P.op("pool", lambda e: e.tensor_tensor(out=Ys[:, 0], in0=Ys[:, 0], in1=bc64(clb), op=ALU.add),
                 reads=[y_b, b_lb], writes=[y_b])
            yield
            P.op("dve", lambda e: e.tensor_tensor(out=Ys[:, 0], in0=Ys[:, 0], in1=X["bon"][:, :, cs], op=ALU.add),
                 reads=[y_b, XB["bon"]], writes=[y_b])
            P.op("dve", lambda e: e.tensor_tensor(out=ost[:, :, cs], in0=Ys[:, 0], in1=X["g"][:, :, cs], op=ALU.mult),
                 reads=[y_b, XB["g"]], writes=[ost_b])
            yield

        def scan(n):
            par = n % 2
            X = {k: X2[k][:, par] for k in X2}
            XB = {k: X2B[k][par] for k in X2}
            outs = [{} for _ in range(TL // 64)]
            gens = [chunk_pre(n, c, outs[c]) for c in range(TL // 64)]
            alive = [True] * len(gens)
            while any(alive):
                for gi_, g_ in enumerate(gens):
                    if alive[gi_]:
                        try:
                            next(g_)
                        except StopIteration:
                            alive[gi_] = False
                yield
            for c in range(TL // 64):
                for _ in chain(n, c, outs[c]["q"]):
                    yield
            P.dma("sp", lambda e: e.dma_start(
                out=yr_dram.rearrange("(h v) s -> v h s", v=64)[:, :, n * TL:(n + 1) * TL], in_=ost[:]),
                reads=[ost_b])
            yield

        def run2(f, b):
            fa, ba = f is not None, b is not None
            while fa or ba:
                if fa:
                    try:
                        next(f)
                    except StopIteration:
                        fa = False
                if ba:
                    try:
                        next(b)
                    except StopIteration:
                        ba = False

        for n in range(NB + 1):
            run2(prep(n) if n < NB else None, scan(n - 1) if n >= 1 else None)
    P.barrier()


NITER = 16


def phase_attn(P, nc, C, A, ya_dram):
    x_dram = A["x"]
    with ExitStack() as es:
        sb = lambda name, shape, dt=F32: es.enter_context(nc.sbuf_tensor(name, shape, dt))
        wa = sb("wa", [128, 8, ATTN_IN + 64], BF16)
        wa_b = Buf()
        load_weight_bf16(P, nc, wa, wa_b, A["w_in"], 0, ATTN_IN, 8)
        wv_ = A["w_in"].rearrange("(kc p) n -> p kc n", p=128)
        for kc in range(8):
            P.dma("pool", lambda e, kc=kc: e.dma_start(out=wa[:, kc, ATTN_IN:ATTN_IN + 64], in_=wv_[:, kc, 2048:2112]),
                  writes=[wa_b])
        kT = sb("kT", [128, 4, S], BF16)
        kiT = sb("kiT", [128, S], BF16)
        vaug = sb("vaug", [128, NT, 8, 65], BF16)
        kT_bs = [Buf() for _ in range(NT)]
        kiT_bs = [Buf() for _ in range(NT)]
        va_bs = [Buf() for _ in range(NT)]
        P.op("pool", lambda e: e.memset(vaug[:, :, :, 64:65], 1.0), writes=va_bs)
        gqk = sb("gqk", [128, 2], F32)
        gqk_b = Buf()
        for half in range(2):
            P.dma("sp", lambda e, half=half: e.dma_start(out=gqk[half * 64:(half + 1) * 64, 0:1],
                                                         in_=A["attn_q_norm"].rearrange("o d -> d o"),
                                                         allow_slow_non_contiguous=True), writes=[gqk_b])
            P.dma("sp", lambda e, half=half: e.dma_start(out=gqk[half * 64:(half + 1) * 64, 1:2],
                                                         in_=A["attn_k_norm"].rearrange("o d -> d o"),
                                                         allow_slow_non_contiguous=True), writes=[gqk_b])
        P.op("dve", lambda e: e.tensor_scalar(out=gqk[:, 0:1], in0=gqk[:, 0:1], scalar1=0.125, scalar2=None, op0=ALU.mult),
             reads=[gqk_b], writes=[gqk_b])
        btf = sb("btf", [128, 8, 2, 128], F32)
        bt = sb("bt", [128, 8, 2, 128], BF16)
        b31 = sb("b31_sb", [128, 8], F32)
        bt_b = Buf()
        P.dma("sp", lambda e: e.dma_start(out=btf[:], in_=A["bias_tiles"].rearrange("h c s t -> s h c t")), writes=[bt_b])
        P.dma("sp", lambda e: e.dma_start(out=b31[:], in_=A["b31"].partition_broadcast(128)), writes=[bt_b])
        P.op("dve", lambda e: e.tensor_tensor(out=bt[:].rearrange("p h c t -> p h (c t)"),
                                              in0=btf[:].rearrange("p h c t -> p h (c t)"),
                                              in1=b31[:, :].unsqueeze(2).broadcast_to([128, 8, 256]), op=ALU.subtract),
             reads=[bt_b], writes=[bt_b])
        cmask = sb("cmask", [128, 128], F32)
        onesblk = sb("onesblk", [128, 128], BF16)
        cm_b = Buf()
        pw2 = sb("pw2", [128, 2 * NITER], F32)
        halfs = sb("halfs", [128, 2 * NITER], F32)
        hf_b = Buf()

        def mkc(e):
            e.memset(cmask[:], 0.0)
            e.affine_select(out=cmask[:], in_=cmask[:], pattern=[[-1, 128]], compare_op=ALU.is_ge, fill=NEG, base=0,
                            channel_multiplier=1)
            for j in range(NITER):
                e.memset(pw2[:, j:j + 1], 0.5 ** (j + 1))
                e.memset(pw2[:, NITER + j:NITER + j + 1], 0.5 ** (j + 2))
            e.memset(onesblk[:], 0.0)
            e.memset(onesblk[0:64, 0:64], 1.0)
            return e.memset(onesblk[64:128, 64:128], 1.0)

        P.op("pool", mkc, writes=[cm_b])
        nm = Normer(P, nc, es, C, A["mix_norm"], "an", nslots=1)
        xbuf = sb("axbuf", [128, 2, D], F32)
        xr = Ring([(i, Buf()) for i in range(2)])
        xn = sb("axn", [128, 8, 128], BF16)
        xn_b = Buf()
        q_i = sb("q_i", [128, 2, 4, 128], BF16)
        qi_i = sb("qi_i", [128, 4, 128], BF16)
        wi_i = sb("wi_i", [128, 8], F32)
        q_bs, qi_b, wi_b = [Buf(), Buf()], Buf(), Buf()
        sq = sb("asq", [128, 512], BF16)
        rn = sb("arn", [128, 512], F32)
        sq_b, rn_b = Buf(), Buf()
        sc = sb("sc", [128, S], F32)
        sc_b = Buf()
        rbuf = sb("rbuf", [128, 2, 512], F32)
        rr = Ring([(i, Buf()) for i in range(2)])
        junk = sb("ajunk", [128, S], BF16)
        junk_b = Buf()
        bs = sb("bs", [128, 8], F32)
        bs_b = Buf()
        maskT = sb("maskT", [128, 2, NT, 128], BF16)
        mT_bs = [Buf(), Buf()]
        Et = sb("Et", [128, 3, 4, 128], BF16)
        er = Ring([(i, Buf()) for i in range(3)])
        Pt = sb("Pt", [128, 8, 4, 128], BF16)
        ptr = Ring([(i, Buf()) for i in range(8)])
        rden = sb("rden", [128, 2], F32)
        rdr = Ring([(i, Buf()) for i in range(2)])
        ytile = sb("ytile", [128, 512], BF16)
        yt_b = Buf()
        yst = sb("yst", [128, 4, 128], BF16)
        ys_b = Buf()
        pg = es.enter_context(nc.psum_tensor("apg", [128, 6, 512], F32))
        gr = Ring([(i, Buf()) for i in range(3)])
        accr = Ring([(i, Buf()) for i in range(3, 4)])
        lgr = Ring([(i, Buf()) for i in range(4, 6)])
        po = es.enter_context(nc.psum_tensor("apo", [128, 1, 512], F32))
        orr = Ring([(i, Buf()) for i in range(1)])
        pgb = lambda i: pg[:, i, :].bitcast(BF16)
        if SBUF_DEBUG:
            print("attn sbuf remaining", nc.sbuf_bytes_remaining)
        xview = x_dram.rearrange("(t p) d -> t p d", p=128)
        yav = ya_dram.rearrange("(m p) s -> p m s", p=128)

        def front(i):
            ts_ = slice(i * 128, (i + 1) * 128)
            nkb = i + 1
            W = nkb * 128
            q_b = q_bs[i % 2]
            kT_b, kiT_b, va_b = kT_bs[i], kiT_bs[i], va_bs[i]
            xi, xb_ = xr.next()
            P.dma("sp", lambda e, i=i, xi=xi: e.dma_start(out=xbuf[:, xi, :], in_=xview[i]), writes=[xb_])
            nm.run(xbuf[:, xi, :], xb_, xn, xn_b, 0)
            yield

            def proj4(c0, bank):
                bi, bb = bank

                def th(e):
                    ins = None
                    for m in range(4):
                        for kc in range(8):
                            ins = e.matmul(pg[:, bi, m * 128:(m + 1) * 128], lhsT=wa[:, kc, c0 + m * 128:c0 + (m + 1) * 128],
                                           rhs=xn[:, kc, :], start=(kc == 0), stop=(kc == 7))
                    return ins

                P.op("pe", th, reads=[wa_b, xn_b], writes=[bb])

            for which, c0 in ((0, 0), (1, 512)):
                bank = gr.next()
                proj4(c0, bank)
                P.op("act", lambda e, bi=bank[0]: e.activation(out=sq[:], in_=pg[:, bi, :], func=AF.Square),
                     reads=[bank[1]], writes=[sq_b])
                bank2 = gr.next()
                P.op("pe", lambda e, bi=bank2[0]: e.matmul(pg[:, bi, :], lhsT=onesblk[:], rhs=sq[:], start=True, stop=True),
                     reads=[cm_b, sq_b], writes=[bank2[1]])
                P.op("act", lambda e, bi=bank2[0]: e.activation(out=rn[:], in_=pg[:, bi, :], func=AF.Ln, scale=1.0 / 64,
                                                                bias=C.eps[:, 0:1]), reads=[bank2[1], C.b], writes=[rn_b])
                P.op("act", lambda e: e.activation(out=rn[:], in_=rn[:], func=AF.Exp, scale=-0.5), reads=[rn_b], writes=[rn_b])
                if which == 0:
                    P.op("dve", lambda e, bi=bank[0]: e.scalar_tensor_tensor(
                        out=q_i[:, i % 2].rearrange("p m t -> p (m t)"), in0=pg[:, bi, :], scalar=gqk[:, 0:1], in1=rn[:],
                        op0=ALU.mult, op1=ALU.mult), reads=[bank[1], rn_b, gqk_b], writes=[q_b])
                else:
                    P.op("dve", lambda e, bi=bank[0], ts_=ts_: e.scalar_tensor_tensor(
                        out=kT[:, :, ts_], in0=pg[:, bi, :].rearrange("p (m t) -> p m t", m=4), scalar=gqk[:, 1:2],
                        in1=rn[:].rearrange("p (m t) -> p m t", m=4), op0=ALU.mult, op1=ALU.mult),
                        reads=[bank[1], rn_b, gqk_b], writes=[kT_b])
                yield
            bank = gr.next()
            proj4(1536, bank)
            P.op("act", lambda e, bi=bank[0]: e.activation(out=qi_i[:].rearrange("p m t -> p (m t)"), in_=pg[:, bi, :],
                                                           func=AF.Copy), reads=[bank[1]], writes=[qi_b])
            yield
            bank = gr.next()

            def kiw(e, bi=bank[0]):
                for kc in range(8):
                    e.matmul(pg[0:64, bi, 0:128], lhsT=wa[:, kc, 2048:2112], rhs=xn[:, kc, :], start=(kc == 0), stop=(kc == 7))
                for kc in range(8):
                    e.matmul(pg[64:128, bi, 0:128], lhsT=wa[:, kc, ATTN_IN:ATTN_IN + 64], rhs=xn[:, kc, :], start=(kc == 0),
                             stop=(kc == 7))
                ins = None
                for kc in range(8):
                    ins = e.matmul(pg[:, bi, 128:136], lhsT=xn[:, kc, :], rhs=wa[:, kc, 2112:2120], start=(kc == 0), stop=(kc == 7))
                return ins

            P.op("pe", kiw, reads=[wa_b, xn_b], writes=[bank[1]])
            P.op("act", lambda e, bi=bank[0], ts_=ts_: e.activation(out=kiT[:, ts_], in_=pg[:, bi, 0:128], func=AF.Copy),
                 reads=[bank[1]], writes=[kiT_b])
            P.op("dve", lambda e, bi=bank[0]: e.tensor_copy(out=wi_i[:], in_=pg[:, bi, 128:136]), reads=[bank[1]], writes=[wi_b])
            yield
            bank = gr.next()

            def vmm(e, bi=bank[0]):
                ins = None
                for kc in range(8):
                    ins = e.matmul(pg[:, bi, :], lhsT=xn[:, kc, :], rhs=wa[:, kc, 1024:1536], start=(kc == 0), stop=(kc == 7))
                return ins

            P.op("pe", vmm, reads=[wa_b, xn_b], writes=[bank[1]])
            P.op("act", lambda e, bi=bank[0], i=i: e.activation(out=vaug[:, i, :, 0:64],
                                                                in_=pg[:, bi, :].rearrange("p (h d) -> p h d", h=8), func=AF.Copy),
                 reads=[bank[1]], writes=[va_b])
            yield

            for gk in range((nkb + 3) // 4):
                w_ = min(512, W - gk * 512)
                abank = accr.next()
                for h in range(8):
                    hb = (h % 2) * 64
                    bank = gr.next()
                    P.op("pe", lambda e, bi=bank[0], h=h, hb=hb, gk=gk, w_=w_: e.matmul(
                        pg[:, bi, 0:w_], lhsT=qi_i[hb:hb + 64, h // 2, :], rhs=kiT[hb:hb + 64, gk * 512:gk * 512 + w_],
                        start=True, stop=True), reads=[qi_b] + kiT_bs[gk * 4:gk * 4 + (w_ // 128)], writes=[bank[1]])
                    ri, rb_ = rr.next()
                    P.op("act", lambda e, bi=bank[0], ri=ri, w_=w_: e.activation(out=rbuf[:, ri, 0:w_], in_=pg[:, bi, 0:w_],
                                                                                 func=AF.Relu), reads=[bank[1]], writes=[rb_])
                    if h == 0:
                        P.op("dve", lambda e, ri=ri, ai=abank[0], w_=w_: e.tensor_scalar(
                            out=pg[:, ai, 0:w_], in0=rbuf[:, ri, 0:w_], scalar1=wi_i[:, 0:1], scalar2=None,
                            op0=ALU.mult), reads=[rb_, wi_b], writes=[abank[1]])
                    elif h < 7:
                        P.op("dve", lambda e, ri=ri, ai=abank[0], w_=w_, h=h: e.scalar_tensor_tensor(
                            out=pg[:, ai, 0:w_], in0=rbuf[:, ri, 0:w_], scalar=wi_i[:, h:h + 1],
                            in1=pg[:, ai, 0:w_], op0=ALU.mult, op1=ALU.add), reads=[rb_, wi_b, abank[1]], writes=[abank[1]])
                    else:
                        P.op("dve", lambda e, ri=ri, ai=abank[0], gk=gk, w_=w_, h=h: e.scalar_tensor_tensor(
                            out=sc[:, gk * 512:gk * 512 + w_], in0=rbuf[:, ri, 0:w_], scalar=wi_i[:, h:h + 1],
                            in1=pg[:, ai, 0:w_], op0=ALU.mult, op1=ALU.add), reads=[rb_, wi_b, abank[1]], writes=[sc_b])
                    yield
            P.op("dve", lambda e, ts_=ts_: e.tensor_tensor(out=sc[:, ts_], in0=sc[:, ts_], in1=cmask[:], op=ALU.add),
                 reads=[sc_b, cm_b], writes=[sc_b])
            if i < 2:
                P.op("dve", lambda e: e.memset(bs[:, 0:1], -1.0e29), writes=[bs_b])
            else:
                P.op("dve", lambda e, i=i: e.tensor_reduce(out=bs[:, 0:1], in_=sc[:, 0:i * 128], axis=AX.X, op=ALU.min),
                     reads=[sc_b], writes=[bs_b])
                P.op("dve", lambda e, W=W: e.tensor_reduce(out=bs[:, 6:7], in_=sc[:, 0:W], axis=AX.X, op=ALU.max),
                     reads=[sc_b, bs_b], writes=[bs_b])
                P.op("dve", lambda e: e.tensor_tensor(out=bs[:, 1:2], in0=bs[:, 6:7], in1=bs[:, 0:1], op=ALU.subtract),
                     reads=[bs_b], writes=[bs_b])
                P.op("dve", lambda e: e.tensor_tensor(out=halfs[:], in0=bs[:, 1:2].broadcast_to([128, 2 * NITER]), in1=pw2[:],
                                                      op=ALU.mult), reads=[bs_b, cm_b], writes=[hf_b])
                P.op("dve", lambda e: e.tensor_tensor(out=bs[:, 3:4], in0=bs[:, 0:1], in1=halfs[:, 0:1], op=ALU.add),
                     reads=[bs_b, hf_b], writes=[bs_b])
                nit = min(NITER, int(math.ceil(math.log2(W))) + 4)
                for it in range(nit):
                    P.op("dve", lambda e: e.tensor_scalar(out=junk[:, 0:W], in0=sc[:, 0:W], scalar1=bs[:, 3:4], scalar2=None,
                                                          op0=ALU.is_ge, op1=ALU.add, accum_out=bs[:, 4:5]),
                         reads=[sc_b, bs_b], writes=[junk_b, bs_b])
                    P.op("dve", lambda e, it=it: e.scalar_tensor_tensor(out=bs[:, 5:6], in0=bs[:, 4:5], scalar=TOPK - 0.5,
                                                                        in1=halfs[:, it:it + 1], op0=ALU.is_ge, op1=ALU.mult),
                         reads=[bs_b, hf_b], writes=[bs_b])
                    P.op("dve", lambda e, it=it: e.scalar_tensor_tensor(out=bs[:, 3:4], in0=bs[:, 5:6],
                                                                        scalar=halfs[:, NITER + it:NITER + it + 1],
                                                                        in1=bs[:, 3:4], op0=ALU.subtract, op1=ALU.add),
                         reads=[bs_b, hf_b], writes=[bs_b])
                    yield
                P.op("dve", lambda e, nit=nit: e.tensor_tensor(out=bs[:, 0:1], in0=bs[:, 3:4], in1=halfs[:, NITER + nit - 1:NITER + nit],
                                                               op=ALU.subtract), reads=[bs_b, hf_b], writes=[bs_b])
            P.op("dve", lambda e, W=W: e.tensor_scalar(out=junk[:, 0:W], in0=sc[:, 0:W], scalar1=bs[:, 0:1], scalar2=None,
                                                       op0=ALU.is_ge), reads=[sc_b, bs_b], writes=[junk_b])
            for j0 in range(0, nkb, 8):
                nb = min(8, nkb - j0)
                bank = gr.next()

                def trm(e, bi=bank[0], j0=j0, nb=nb):
                    ins = None
                    for jj in range(nb):
                        ins = e.transpose(out=pgb(bi)[:, jj * 128:(jj + 1) * 128], in_=junk[:, (j0 + jj) * 128:(j0 + jj + 1) * 128],
                                          identity=C.ident_bf[:])
                    return ins

                P.op("pe", trm, reads=[junk_b, C.b], writes=[bank[1]])
                P.op("act", lambda e, bi=bank[0], j0=j0, nb=nb: e.activation(
                    out=maskT[:, i % 2, j0:j0 + nb, :].rearrange("p j t -> p (j t)"), in_=pgb(bi)[:, 0:nb * 128], func=AF.Copy),
                    reads=[bank[1]], writes=[mT_bs[i % 2]])
                yield
            yield

        def back(i):
            ts_ = slice(i * 128, (i + 1) * 128)
            nkb = i + 1
            q_b = q_bs[i % 2]
            mT_b = mT_bs[i % 2]
            items = [(h, j0, min(4, nkb - j0)) for h in range(8) for j0 in range(0, nkb, 4)]
            DEPTH = 6
            st = {}
            obs = {}
            for k in range(len(items) + DEPTH):
                if k < len(items):
                    h, j0, nb = items[k]
                    hb = (h % 2) * 64
                    m = h // 2
                    if j0 == 0:
                        obs[h] = orr.next()
                    bank = lgr.next()

                    def qk(e, bi=bank[0], j0=j0, nb=nb, h=h, hb=hb, m=m):
                        ins = None
                        for jj in range(nb):
                            j = j0 + jj
                            near = j >= i - 1
                            ins = e.matmul(pg[:, bi, jj * 128:(jj + 1) * 128], lhsT=kT[hb:hb + 64, m, j * 128:(j + 1) * 128],
                                           rhs=q_i[hb:hb + 64, i % 2, m, :], start=True, stop=not near)
                            if near:
                                ins = e.matmul(pg[:, bi, jj * 128:(jj + 1) * 128], lhsT=C.ident_bf[:],
                                               rhs=bt[:, h, 0 if j == i else 1, :], start=False, stop=True)
                        return ins

                    P.op("pe", qk, reads=kT_bs[j0:j0 + nb] + [q_b, bt_b, C.b], writes=[bank[1]])
                    ei, eb = er.next()
                    P.op("act", lambda e, bi=bank[0], ei=ei, nb=nb: e.activation(
                        out=Et[:, ei, 0:nb, :].rearrange("p j t -> p (j t)"), in_=pg[:, bi, 0:nb * 128], func=AF.Exp),
                        reads=[bank[1]], writes=[eb])
                    pi, pb_ = ptr.next()
                    P.op("pool", lambda e, ei=ei, pi=pi, j0=j0, nb=nb: e.tensor_tensor(
                        out=Pt[:, pi, 0:nb, :], in0=Et[:, ei, 0:nb, :], in1=maskT[:, i % 2, j0:j0 + nb, :], op=ALU.mult),
                        reads=[eb, mT_b], writes=[pb_])
                    st[k] = (pi, pb_)
                kk = k - DEPTH
                if kk >= 0:
                    h, j0, nb = items[kk]
                    pi, pb_ = st.pop(kk)
                    ob = obs[h]

                    def pv(e, oi=ob[0], pi=pi, j0=j0, nb=nb, h=h):
                        ins = None
                        for jj in range(nb):
                            j = j0 + jj
                            ins = e.matmul(po[:, oi, 0:65], lhsT=Pt[:, pi, jj, :], rhs=vaug[:, j, h, :], start=(j == 0),
                                           stop=(j == i))
                        return ins

                    P.op("pe", pv, reads=[pb_] + va_bs[j0:j0 + nb], writes=[ob[1]])
                    if j0 + nb == nkb:
                        di, db = rdr.next()
                        P.op("dve", lambda e, oi=ob[0], di=di: e.reciprocal(out=rden[:, di:di + 1], in_=po[:, oi, 64:65]),
                             reads=[ob[1]], writes=[db])
                        P.op("act", lambda e, oi=ob[0], di=di, h=h: e.activation(out=ytile[:, h * 64:(h + 1) * 64],
                                                                                 in_=po[:, oi, 0:64], func=AF.Copy,
                                                                                 scale=rden[:, di:di + 1]),
                             reads=[ob[1], db], writes=[yt_b])
                yield
            bank = lgr.next()

            def try_(e, bi=bank[0]):
                ins = None
                for m in range(4):
                    ins = e.transpose(out=pgb(bi)[:, m * 128:(m + 1) * 128], in_=ytile[:, m * 128:(m + 1) * 128],
                                      identity=C.ident_bf[:])
                return ins

            P.op("pe", try_, reads=[yt_b, C.b], writes=[bank[1]])
            P.op("act", lambda e, bi=bank[0]: e.activation(out=yst[:].rearrange("p m t -> p (m t)"), in_=pgb(bi)[:, 0:512],
                                                           func=AF.Copy), reads=[bank[1]], writes=[ys_b])
            P.dma("sp", lambda e: e.dma_start(out=yav[:, :, ts_], in_=yst[:]), reads=[ys_b])
            yield

        def run2(f, b):
            fa, ba = f is not None, b is not None
            while fa or ba:
                if fa:
                    try:
                        next(f)
                    except StopIteration:
                        fa = False
                if ba:
                    try:
                        next(b)
                    except StopIteration:
                        ba = False

        for i in range(NT + 1):
            run2(front(i) if i < NT else None, back(i - 1) if i >= 1 else None)
    P.barrier()


WEIGHT_SPECS = [
    ("mix_norm", [1, D]), ("w_in", [D, 5992]), ("attn_q_norm", [1, 64]), ("attn_k_norm", [1, 64]),
    ("bias_tiles", [8, 2, 128, 128]), ("b31", [1, 8]), ("rwkv_mu", [1, RWKV_IN]), ("rwkv_w0", [1, 512]), ("rwkv_w2", [64, 512]),
    ("rwkv_a0", [1, 512]), ("rwkv_a2", [64, 512]), ("rwkv_g2", [160, 512]), ("rwkv_k_k", [1, 512]),
    ("rwkv_k_a", [1, 512]), ("rwkv_r_k", [1, 512]), ("rwkv_ln_w", [1, 512]), ("rwkv_ln_b", [1, 512]),
    ("w_branch_attn", [512, D]), ("w_branch_rwkv", [512, D]), ("w_out", [D, D]), ("ffn_norm", [1, D]),
    ("w_gate_up", [D, 2 * FFN_H]), ("w_down", [FFN_H, D]),
]


def build_program(phases=("attn", "rwkv", "merge", "ffn"), debug=False):
    nc = bass.Bass("TRN2", target_bir_lowering=False)
    A = {}
    A["x"] = nc.dram_tensor("x", [S, D], F32, kind="ExternalInput").ap()
    for name, shp in WEIGHT_SPECS:
        A[name] = nc.dram_tensor(name, shp, F32, kind="ExternalInput").ap()
    out = nc.dram_tensor("out", [S, D], F32, kind="ExternalOutput").ap()
    def kind(prod, cons):
        if not debug:
            return "Internal"
        if prod in phases and cons not in phases:
            return "ExternalOutput"
        if prod not in phases and cons in phases:
            return "ExternalInput"
        return "Internal"

    ya = nc.dram_tensor("ya_scr", [512, S], BF16, kind=kind("attn", "merge")).ap()
    yr = nc.dram_tensor("yr_scr", [512, S], BF16, kind=kind("rwkv", "merge")).ap()
    hs = nc.dram_tensor("h_scr", [S, D], F32, kind=kind("merge", "ffn")).ap()
    P = Prog(nc)
    final_ops = []
    with ExitStack() as es:
        C = make_consts(P, nc, es)
        P.barrier()
        if "attn" in phases:
            phase_attn(P, nc, C, A, ya)
        if "rwkv" in phases:
            phase_rwkv(P, nc, C, A, yr)
        if "merge" in phases:
            phase_merge(P, nc, C, A["x"], ya, yr, hs, A["mix_norm"], A["w_in"], A["w_branch_attn"],
                        A["w_branch_rwkv"], A["w_out"])
        if "ffn" in phases:
            phase_ffn(P, nc, C, hs, out, A["ffn_norm"], A["w_gate_up"], A["w_down"], final_ops)
        P.emit(final_wait_ops=final_ops)
    return nc


def t5_bucket_np(d):
    d = np.maximum(d, 0)
    max_exact = 16
    log_ratio = np.log(np.maximum(d, 1).astype(np.float32) / max_exact) / math.log(128 / max_exact)
    large = np.minimum(max_exact + (log_ratio * 16).astype(np.int32), 31)
    return np.where(d < max_exact, d, large)


def host_layout(inputs):
    w = {}
    for name, shp in WEIGHT_SPECS:
        if name in ("bias_tiles", "b31"):
            continue
        w[name] = np.ascontiguousarray(np.asarray(inputs[name], dtype=np.float32).reshape(shp))
    s_idx = np.arange(128)[:, None]
    t_idx = np.arange(128)[None, :]
    rb = np.asarray(inputs["rel_bias"], dtype=np.float32)
    tiles = np.empty((8, 2, 128, 128), np.float32)
    for cls in range(2):
        bk = t5_bucket_np(t_idx - s_idx + 128 * cls)
        tiles[:, cls] = np.transpose(rb[bk], (2, 0, 1))
    w["bias_tiles"] = tiles
    w["b31"] = np.ascontiguousarray(rb[31:32, :])
    return w


_NC_CACHE = {}


def kernel(**inputs):
    x = np.asarray(inputs["x"], dtype=np.float32)
    w = host_layout(inputs)
    if "nc" not in _NC_CACHE:
        _NC_CACHE["nc"] = build_program()
    nc = _NC_CACHE["nc"]
    in_maps = []
    for b in range(8):
        m = dict(w)
        m["x"] = np.ascontiguousarray(x[b])
        in_maps.append(m)
    res = run_bass_kernel_spmd(nc, in_maps, core_ids=list(range(8)))
    return np.stack([np.asarray(r["out"], dtype=np.float32) for r in res.results], axis=0)
```

```python
import math
from contextlib import ExitStack

import numpy as np
import concourse.bass as bass
import concourse.mybir as mybir
from concourse.bass_utils import run_bass_kernel_spmd

F32 = mybir.dt.float32
BF16 = mybir.dt.bfloat16
AF = mybir.ActivationFunctionType
ALU = mybir.AluOpType
AX = mybir.AxisListType

S = 4096
D = 1024
NT = S // 128
ATTN_IN = 2120
RWKV_IN = 1824
FFN_H = 2816
RMS_EPS = 1e-6
GN_EPS = 64e-5
TOPK = 256
NEG = -1.0e30

ENGS = ("pe", "act", "dve", "pool", "sp")
SBUF_DEBUG = False


class Buf:
    __slots__ = ("name", "w", "r")

    def __init__(self, name=""):
        self.name = name
        self.w = None
        self.r = []


class Op:
    __slots__ = ("eng", "thunk", "deps", "is_dma", "sem", "val", "need_inc", "pos", "prev_on_sem")


class Prog:
    def __init__(self, nc, n_dma_sems=(56, 28)):
        self.nc = nc
        self.streams = {e: [] for e in ENGS}
        self.n_dma_sems = n_dma_sems
        self.dma_count = 0
        self.all_ops = []
        self.open_dmas = []

    def _hazards(self, op, reads, writes):
        deps = []
        for b in reads:
            if b.w is not None:
                deps.append(b.w)
        for b in writes:
            if b.w is not None:
                deps.append(b.w)
            deps.extend(b.r)
        for b in reads:
            b.r.append(op)
        for b in writes:
            b.w = op
            b.r = []
        return deps

    def op(self, eng, thunk, reads=(), writes=(), extra_deps=()):
        o = Op()
        o.eng = eng
        o.thunk = thunk
        o.is_dma = False
        o.need_inc = False
        o.sem = None
        o.val = None
        o.prev_on_sem = None
        deps = self._hazards(o, reads, writes) + list(extra_deps)
        seen = set()
        o.deps = []
        for d in deps:
            if d is o or id(d) in seen:
                continue
            if eng == "pe" and d.eng == "pe" and not d.is_dma:
                continue
            seen.add(id(d))
            o.deps.append(d)
        self.streams[eng].append(o)
        self.all_ops.append(o)
        return o

    def dma(self, eng, thunk, reads=(), writes=(), extra_deps=()):
        o = self.op(eng, thunk, reads, writes, extra_deps)
        o.is_dma = True
        o.pos = self.dma_count
        self.dma_count += 1
        self.open_dmas.append(o)
        return o

    def barrier(self):
        lasts = []
        for e in ENGS:
            for o in reversed(self.streams[e]):
                if not o.is_dma:
                    lasts.append(o)
                    break
        deps = lasts + self.open_dmas
        self.open_dmas = []
        for e in ENGS:
            self.op(e, lambda eng: eng.nop(), extra_deps=deps)

    def emit(self, final_wait_ops=()):
        nc = self.nc
        for o in self.all_ops:
            for d in o.deps:
                d.need_inc = True
        eng_sems = {e: nc.alloc_semaphore("s_" + e) for e in ENGS}
        ring_n = {"sp": self.n_dma_sems[0], "pool": self.n_dma_sems[1], "act": 2, "dve": 2, "pe": 2}
        dma_sems = {}
        dma_sem_val = {}
        dma_prev = {}
        qpos = {e: 0 for e in ENGS}
        for e in ENGS:
            if any(o.is_dma for o in self.streams[e]):
                for i in range(ring_n[e]):
                    dma_sems[(e, i)] = nc.alloc_semaphore("s_dma_%s%d" % (e, i))
                    dma_sem_val[(e, i)] = 0
                    dma_prev[(e, i)] = None
        cnt = {e: 0 for e in ENGS}
        for o in self.all_ops:
            if o.is_dma:
                kq = (o.eng, qpos[o.eng] % ring_n[o.eng])
                qpos[o.eng] += 1
                dma_sem_val[kq] += 16
                o.sem = ("dma", kq)
                o.val = dma_sem_val[kq]
                o.prev_on_sem = dma_prev[kq]
                dma_prev[kq] = o
            elif o.need_inc:
                cnt[o.eng] += 1
                o.sem = ("eng", o.eng)
                o.val = cnt[o.eng]

        def semh(key):
            return eng_sems[key[1]] if key[0] == "eng" else dma_sems[key[1]]

        engines = {"pe": "tensor", "act": "scalar", "dve": "vector", "pool": "gpsimd", "sp": "sync"}
        with nc.Block() as block:
            for e in ENGS:
                stream = self.streams[e]
                final = list(final_wait_ops) if e == "sp" else []

                def body(engine, stream=stream, final=final):
                    known = {}
                    for o in stream:
                        waits = {}
                        deps = list(o.deps)
                        if o.is_dma and o.prev_on_sem is not None:
                            deps.append(o.prev_on_sem)
                        for d in deps:
                            if known.get(d.sem, 0) >= d.val:
                                continue
                            if waits.get(d.sem, 0) < d.val:
                                waits[d.sem] = d.val
                        for key, val in waits.items():
                            engine.wait_ge(semh(key), val)
                            known[key] = val
                        ins = o.thunk(engine)
                        if o.is_dma:
                            ins.then_inc(semh(o.sem), 16)
                        elif o.need_inc:
                            ins.then_inc(semh(o.sem), 1)
                    for o in final:
                        engine.wait_ge(semh(o.sem), o.val)

                getattr(block, engines[e])(body)


class Ring:
    def __init__(self, items):
        self.items = items
        self.i = 0

    def next(self):
        it = self.items[self.i % len(self.items)]
        self.i += 1
        return it


def load_weight_bf16(P, nc, dst, dst_buf, w_ap, c0, c1, kchunks, eng="pool"):
    wv = w_ap.rearrange("(kc p) n -> p kc n", p=128)
    for kc in range(kchunks):
        for a in range(c0, c1, 2048):
            b = min(c1, a + 2048)
            P.dma(eng, lambda e, kc=kc, a=a, b=b: e.dma_start(out=dst[:, kc, a - c0:b - c0], in_=wv[:, kc, a:b]),
                  writes=[dst_buf])


def load_col_vec(P, nc, dst, dst_buf, v_ap, n):
    src = v_ap.rearrange("o (c p) -> p (o c)", p=128)
    P.dma("sp", lambda e: e.dma_start(out=dst, in_=src, allow_slow_non_contiguous=True), writes=[dst_buf])


class Consts:
    pass


def make_consts(P, nc, es):
    C = Consts()
    C.ident_bf = es.enter_context(nc.sbuf_tensor("ident_bf", [128, 128], BF16))
    C.ident_f = es.enter_context(nc.sbuf_tensor("ident_f", [128, 128], F32))
    C.eps = es.enter_context(nc.sbuf_tensor("eps_c", [128, 2], F32))
    C.b = Buf("consts")

    def mk(e):
        e.memset(C.ident_f[:], 0.0)
        e.affine_select(out=C.ident_f[:], in_=C.ident_f[:], pattern=[[-1, 128]], compare_op=ALU.not_equal,
                        fill=1.0, base=0, channel_multiplier=1)
        e.memset(C.eps[:, 0:1], RMS_EPS)
        return e.memset(C.eps[:, 1:2], GN_EPS)

    P.op("pool", mk, writes=[C.b])
    P.op("pool", lambda e: e.tensor_copy(out=C.ident_bf[:], in_=C.ident_f[:]), reads=[C.b], writes=[C.b])
    return C


class Normer:
    def __init__(self, P, nc, es, C, gain_ap, name, nslots=2):
        self.P, self.nc, self.C = P, nc, C
        self.gcol = es.enter_context(nc.sbuf_tensor(name + "_g", [128, 8], F32))
        self.gb = Buf(name + "_g")
        load_col_vec(P, nc, self.gcol[:, :], self.gb, gain_ap, 8)
        self.stat = es.enter_context(nc.sbuf_tensor(name + "_st", [128, nslots, 4], F32))
        self.junk = es.enter_context(nc.sbuf_tensor(name + "_junk", [128, 1024], BF16))
        self.xs = es.enter_context(nc.sbuf_tensor(name + "_xs", [128, nslots, 1024], BF16))
        self.tp = es.enter_context(nc.psum_tensor(name + "_tp", [128, nslots, 8, 128], BF16))
        self.ring = Ring([(i, Buf(), Buf(), Buf()) for i in range(nslots)])
        self.junkb = Buf()

    def run(self, xt_ap, xt_buf, dst, dst_buf, col0):
        P, C = self.P, self.C
        i, sb, xb, pb = self.ring.next()
        st = self.stat
        P.op("act", lambda e: e.activation(out=self.junk[:], in_=xt_ap, func=AF.Square, accum_out=st[:, i, 0:1]),
             reads=[xt_buf], writes=[self.junkb, sb])
        P.op("act", lambda e: e.activation(out=st[:, i, 1:2], in_=st[:, i, 0:1], func=AF.Ln, scale=1.0 / D,
                                           bias=C.eps[:, 0:1]), reads=[sb, C.b], writes=[sb])
        P.op("act", lambda e: e.activation(out=st[:, i, 2:3], in_=st[:, i, 1:2], func=AF.Exp, scale=-0.5),
             reads=[sb], writes=[sb])
        P.op("act", lambda e: e.activation(out=self.xs[:, i, :], in_=xt_ap, func=AF.Copy, scale=st[:, i, 2:3]),
             reads=[xt_buf, sb], writes=[xb])

        def tr(e):
            ins = None
            for kc in range(8):
                ins = e.transpose(out=self.tp[:, i, kc, :], in_=self.xs[:, i, kc * 128:(kc + 1) * 128],
                                  identity=C.ident_bf[:])
            return ins

        P.op("pe", tr, reads=[xb, C.b], writes=[pb])
        P.op("dve", lambda e: e.tensor_tensor(out=dst[:, :, col0:col0 + 128], in0=self.tp[:, i, :, :],
                                              in1=self.gcol[:, :].unsqueeze(2).broadcast_to([128, 8, 128]),
                                              op=ALU.mult), reads=[pb, self.gb], writes=[dst_buf])


def phase_ffn(P, nc, C, h_dram, out_dram, ffn_norm, w_gate_up, w_down, final_ops):
    with ExitStack() as es:
        wgu = es.enter_context(nc.sbuf_tensor("wgu", [128, 8, 2 * FFN_H], BF16))
        wd = es.enter_context(nc.sbuf_tensor("wd", [128, 22, D], BF16))
        wgu_b, wd_b = Buf("wgu"), Buf("wd")
        load_weight_bf16(P, nc, wgu, wgu_b, w_gate_up, 0, 2 * FFN_H, 8)
        load_weight_bf16(P, nc, wd, wd_b, w_down, 0, D, 22)
        nm = Normer(P, nc, es, C, ffn_norm, "fn")
        hbuf = es.enter_context(nc.sbuf_tensor("hbuf", [128, 2, D], F32))
        hr = Ring([(i, Buf()) for i in range(2)])
        hnT = es.enter_context(nc.sbuf_tensor("hnT", [128, 2, 8, 512], BF16))
        hnb = [Buf(), Buf()]
        actT = es.enter_context(nc.sbuf_tensor("actT", [128, 22, 512], BF16))
        actb = [Buf() for _ in range(22)]
        sg = es.enter_context(nc.sbuf_tensor("sg", [128, 2, 512], F32))
        sgr = Ring([(i, Buf()) for i in range(2)])
        ost = es.enter_context(nc.sbuf_tensor("ost", [128, 2, D], F32))
        ostr = Ring([(i, Buf()) for i in range(2)])
        pg = es.enter_context(nc.psum_tensor("pg", [128, 2, 512], F32))
        pu = es.enter_context(nc.psum_tensor("pu", [128, 2, 512], F32))
        po = es.enter_context(nc.psum_tensor("po", [128, 2, 512], F32))
        pgr = Ring([(i, Buf()) for i in range(2)])
        pur = Ring([(i, Buf()) for i in range(2)])
        por = Ring([(i, Buf()) for i in range(2)])
        hview = h_dram.rearrange("(t p) d -> t p d", p=128)
        oview = out_dram.rearrange("(t p) d -> t p d", p=128)
        def ffn_norm_chunk(c):
            cb = c % 2
            for j in range(4):
                t = c * 4 + j
                hi, hb_ = hr.next()
                P.dma("sp", lambda e, t=t, hi=hi: e.dma_start(out=hbuf[:, hi, :], in_=hview[t]), writes=[hb_])
                nm.run(hbuf[:, hi, :], hb_, hnT[:, cb], hnb[cb], j * 128)

        ffn_norm_chunk(0)
        for c in range(S // 512):
            cb = c % 2
            for m in range(22):
                gi, gbuf = pgr.next()
                ui, ubuf = pur.next()

                def mm(e, m=m, gi=gi, ui=ui, cb=cb):
                    for kc in range(8):
                        e.matmul(pg[:, gi, :], lhsT=wgu[:, kc, m * 128:(m + 1) * 128], rhs=hnT[:, cb, kc, :],
                                 start=(kc == 0), stop=(kc == 7))
                    ins = None
                    for kc in range(8):
                        ins = e.matmul(pu[:, ui, :], lhsT=wgu[:, kc, FFN_H + m * 128:FFN_H + (m + 1) * 128],
                                       rhs=hnT[:, cb, kc, :], start=(kc == 0), stop=(kc == 7))
                    return ins

                P.op("pe", mm, reads=[wgu_b, hnb[cb]], writes=[gbuf, ubuf])
                si, sbuf_ = sgr.next()
                P.op("act", lambda e, gi=gi, si=si: e.activation(out=sg[:, si, :], in_=pg[:, gi, :], func=AF.Silu),
                     reads=[gbuf], writes=[sbuf_])
                P.op("dve", lambda e, ui=ui, si=si, m=m: e.tensor_tensor(out=actT[:, m, :], in0=pu[:, ui, :],
                                                                         in1=sg[:, si, :], op=ALU.mult),
                     reads=[ubuf, sbuf_], writes=[actb[m]])
            if c + 1 < S // 512:
                ffn_norm_chunk(c + 1)
            for j in range(4):
                t = c * 4 + j
                oi, obuf = ostr.next()
                P.dma("sp", lambda e, t=t, oi=oi: e.dma_start(out=ost[:, oi, :], in_=hview[t]), writes=[obuf])
                for nh in range(2):
                    pi, pbuf = por.next()

                    def mmd(e, j=j, nh=nh, pi=pi):
                        ins = None
                        for m in range(22):
                            ins = e.matmul(po[:, pi, :], lhsT=actT[:, m, j * 128:(j + 1) * 128],
                                           rhs=wd[:, m, nh * 512:(nh + 1) * 512], start=(m == 0), stop=(m == 21))
                        return ins

                    P.op("pe", mmd, reads=actb + [wd_b], writes=[pbuf])
                    P.op("dve", lambda e, nh=nh, pi=pi, oi=oi: e.tensor_tensor(
                        out=ost[:, oi, nh * 512:(nh + 1) * 512], in0=po[:, pi, :],
                        in1=ost[:, oi, nh * 512:(nh + 1) * 512], op=ALU.add),
                        reads=[pbuf], writes=[obuf])
                final_ops.append(P.dma("sp", lambda e, t=t, oi=oi: e.dma_start(out=oview[t], in_=ost[:, oi, :]),
                                       reads=[obuf]))
    P.barrier()


def phase_merge(P, nc, C, x_dram, ya_dram, yr_dram, h_dram, mix_norm, w_in, w_ba, w_br, w_out):
    with ExitStack() as es:
        wg = es.enter_context(nc.sbuf_tensor("wg", [128, 8, 2 * D], BF16))
        wba = es.enter_context(nc.sbuf_tensor("wba", [128, 4, D], BF16))
        wbr = es.enter_context(nc.sbuf_tensor("wbr", [128, 4, D], BF16))
        wo = es.enter_context(nc.sbuf_tensor("wo", [128, 8, D], BF16))
        wg_b, wba_b, wbr_b, wo_b = Buf(), Buf(), Buf(), Buf()
        load_weight_bf16(P, nc, wg, wg_b, w_in, ATTN_IN + RWKV_IN, ATTN_IN + RWKV_IN + 2 * D, 8)
        load_weight_bf16(P, nc, wba, wba_b, w_ba, 0, D, 4)
        load_weight_bf16(P, nc, wbr, wbr_b, w_br, 0, D, 4)
        load_weight_bf16(P, nc, wo, wo_b, w_out, 0, D, 8)
        nm = Normer(P, nc, es, C, mix_norm, "mn")
        xbuf = es.enter_context(nc.sbuf_tensor("xbuf", [128, 2, 4, D], F32))
        xb = [[Buf() for _ in range(4)] for _ in range(2)]
        xnT = es.enter_context(nc.sbuf_tensor("xnT", [128, 2, 8, 512], BF16))
        xnb = [Buf(), Buf()]
        yaT = es.enter_context(nc.sbuf_tensor("yaT", [128, 2, 4, 512], BF16))
        yrT = es.enter_context(nc.sbuf_tensor("yrT", [128, 2, 4, 512], BF16))
        yab, yrb = [Buf(), Buf()], [Buf(), Buf()]
        mT = es.enter_context(nc.sbuf_tensor("mT", [128, 8, 512], BF16))
        mb = [Buf() for _ in range(8)]
        sg = es.enter_context(nc.sbuf_tensor("sgm", [128, 2, 2, 512], F32))
        sgr = Ring([(i, Buf()) for i in range(2)])
        tt = es.enter_context(nc.sbuf_tensor("ttm", [128, 2, 2, 512], F32))
        ttr = Ring([(i, Buf()) for i in range(2)])
        hst = es.enter_context(nc.sbuf_tensor("hst", [128, 2, D], F32))
        hstr = Ring([(i, Buf()) for i in range(2)])
        pga = es.enter_context(nc.psum_tensor("pga", [128, 2, 512], F32))
        pbr = es.enter_context(nc.psum_tensor("pbr", [128, 2, 512], F32))
        po = es.enter_context(nc.psum_tensor("pom", [128, 2, 512], F32))
        pgb, pbb = Buf(), Buf()
        por = Ring([(i, Buf()) for i in range(2)])
        xview = x_dram.rearrange("(t p) d -> t p d", p=128)
        hview = h_dram.rearrange("(t p) d -> t p d", p=128)
        yav = ya_dram.rearrange("(kc p) s -> p kc s", p=128)
        yrv = yr_dram.rearrange("(kc p) s -> p kc s", p=128)
        def merge_load_chunk(c):
            cb = c % 2
            P.dma("sp", lambda e: e.dma_start(out=yaT[:, cb], in_=yav[:, :, c * 512:(c + 1) * 512]), writes=[yab[cb]])
            P.dma("sp", lambda e: e.dma_start(out=yrT[:, cb], in_=yrv[:, :, c * 512:(c + 1) * 512]), writes=[yrb[cb]])
            for j in range(4):
                t = c * 4 + j
                P.dma("sp", lambda e, t=t, j=j: e.dma_start(out=xbuf[:, cb, j, :], in_=xview[t]), writes=[xb[cb][j]])
                nm.run(xbuf[:, cb, j, :], xb[cb][j], xnT[:, cb], xnb[cb], j * 128)

        merge_load_chunk(0)
        for c in range(S // 512):
            cb = c % 2
            for m in range(8):
                def mmg(e, m=m, cb=cb):
                    ins = None
                    for g in range(2):
                        for kc in range(8):
                            ins = e.matmul(pga[:, g, :], lhsT=wg[:, kc, g * D + m * 128:g * D + (m + 1) * 128],
                                           rhs=xnT[:, cb, kc, :], start=(kc == 0), stop=(kc == 7))
                    return ins

                P.op("pe", mmg, reads=[wg_b, xnb[cb]], writes=[pgb])

                def mmb(e, m=m, cb=cb):
                    ins = None
                    for kc in range(4):
                        ins = e.matmul(pbr[:, 0, :], lhsT=wba[:, kc, m * 128:(m + 1) * 128], rhs=yaT[:, cb, kc, :],
                                       start=(kc == 0), stop=(kc == 3))
                    for kc in range(4):
                        ins = e.matmul(pbr[:, 1, :], lhsT=wbr[:, kc, m * 128:(m + 1) * 128], rhs=yrT[:, cb, kc, :],
                                       start=(kc == 0), stop=(kc == 3))
                    return ins

                P.op("pe", mmb, reads=[wba_b, wbr_b, yab[cb], yrb[cb]], writes=[pbb])
                si, sbuf_ = sgr.next()
                P.op("act", lambda e, si=si: e.activation(out=sg[:, si], in_=pga[:, :, :], func=AF.Sigmoid),
                     reads=[pgb], writes=[sbuf_])
                ti, tbuf = ttr.next()
                P.op("dve", lambda e, si=si, ti=ti: e.tensor_tensor(out=tt[:, ti], in0=pbr[:, :, :], in1=sg[:, si],
                                                                    op=ALU.mult),
                     reads=[pbb, sbuf_], writes=[tbuf])
                P.op("pool", lambda e, ti=ti, m=m: e.tensor_tensor(out=mT[:, m, :], in0=tt[:, ti, 0, :],
                                                                   in1=tt[:, ti, 1, :], op=ALU.add),
                     reads=[tbuf], writes=[mb[m]])
            if c + 1 < S // 512:
                merge_load_chunk(c + 1)
            for j in range(4):
                t = c * 4 + j
                hi, hbuf_ = hstr.next()
                for nh in range(2):
                    pi, pbuf = por.next()

                    def mmo(e, j=j, nh=nh, pi=pi):
                        ins = None
                        for m in range(8):
                            ins = e.matmul(po[:, pi, :], lhsT=mT[:, m, j * 128:(j + 1) * 128],
                                           rhs=wo[:, m, nh * 512:(nh + 1) * 512], start=(m == 0), stop=(m == 7))
                        return ins

                    P.op("pe", mmo, reads=mb + [wo_b], writes=[pbuf])
                    P.op("dve", lambda e, j=j, nh=nh, pi=pi, hi=hi, cb=cb: e.tensor_tensor(
                        out=hst[:, hi, nh * 512:(nh + 1) * 512], in0=po[:, pi, :],
                        in1=xbuf[:, cb, j, nh * 512:(nh + 1) * 512], op=ALU.add),
                        reads=[pbuf, xb[cb][j]], writes=[hbuf_])
                P.dma("sp", lambda e, t=t, hi=hi: e.dma_start(out=hview[t], in_=hst[:, hi, :]), reads=[hbuf_])
    P.barrier()


TL = 128
C0 = math.exp(-0.5)


def col8(P, nc, es, name, v_ap):
    t = es.enter_context(nc.sbuf_tensor(name, [64, 8], F32))
    b = Buf(name)
    P.dma("sp", lambda e: e.dma_start(out=t[:, :], in_=v_ap.rearrange("o (h k) -> k (o h)", k=64),
                                      allow_slow_non_contiguous=True), writes=[b])
    return t, b


def phase_rwkv(P, nc, C, A, yr_dram):
    x_dram = A["x"]
    with ExitStack() as es:
        sb = lambda name, shape, dt=F32: es.enter_context(nc.sbuf_tensor(name, shape, dt))
        wr = sb("wr", [128, 8, RWKV_IN], BF16)
        wmu = sb("wmu", [128, 8, RWKV_IN], BF16)
        wr_b, wmu_b, mub_b = Buf(), Buf(), Buf()
        load_weight_bf16(P, nc, wr, wr_b, A["w_in"], ATTN_IN, ATTN_IN + RWKV_IN, 8)
        with nc.sbuf_tensor("mub", [128, RWKV_IN], F32) as mub:
            P.dma("sp", lambda e: e.dma_start(out=mub[:], in_=A["rwkv_mu"].partition_broadcast(128)), writes=[mub_b])
            for kc in range(8):
                P.op("pool", lambda e, kc=kc: e.tensor_tensor(out=wmu[:, kc, :], in0=wr[:, kc, :], in1=mub[:],
                                                              op=ALU.mult), reads=[wr_b, mub_b], writes=[wmu_b])
        P.barrier()
        w2 = sb("w2", [64, 512], BF16)
        a2 = sb("a2", [64, 512], BF16)
        g2 = sb("g2", [64, 3, 512], BF16)
        lw_b = Buf()
        P.dma("pool", lambda e: e.dma_start(out=w2[:], in_=A["rwkv_w2"]), writes=[lw_b])
        P.dma("pool", lambda e: e.dma_start(out=a2[:], in_=A["rwkv_a2"]), writes=[lw_b])
        P.dma("pool", lambda e: e.dma_start(out=g2[:, 0, :], in_=A["rwkv_g2"][0:64, :]), writes=[lw_b])
        P.dma("pool", lambda e: e.dma_start(out=g2[:, 1, :], in_=A["rwkv_g2"][64:128, :]), writes=[lw_b])
        P.dma("pool", lambda e: e.dma_start(out=g2[0:32, 2, :], in_=A["rwkv_g2"][128:160, :]), writes=[lw_b])
        cw0, b_w0 = col8(P, nc, es, "cw0", A["rwkv_w0"])
        ca0, b_a0 = col8(P, nc, es, "ca0", A["rwkv_a0"])
        ckk, b_kk = col8(P, nc, es, "ckk", A["rwkv_k_k"])
        cka, b_ka = col8(P, nc, es, "cka", A["rwkv_k_a"])
        crk, b_rk = col8(P, nc, es, "crk", A["rwkv_r_k"])
        clw, b_lw = col8(P, nc, es, "clw", A["rwkv_ln_w"])
        clb, b_lb = col8(P, nc, es, "clb", A["rwkv_ln_b"])
        msk = sb("rmsk", [64, 3, 64], F32)
        ones_bf = sb("ones_bf", [64, 64], BF16)
        rmask = sb("rmask", [64, 8, TL // 64, 64], F32)
        mk_b = Buf()

        def mkmasks(e):
            e.memset(msk[:], 1.0)
            e.memset(ones_bf[:], 1.0)
            e.memset(rmask[:], 1.0)
            e.memset(rmask[:, :, :, 0:1], 0.0)
            e.affine_select(out=msk[:, 0, :], in_=msk[:, 0, :], pattern=[[1, 64]], compare_op=ALU.is_ge,
                            fill=0.0, base=-1, channel_multiplier=-1)
            e.affine_select(out=msk[:, 1, :], in_=msk[:, 1, :], pattern=[[1, 64]], compare_op=ALU.is_ge,
                            fill=0.0, base=0, channel_multiplier=-1)
            return e.affine_select(out=msk[:, 2, :], in_=msk[:, 2, :], pattern=[[-1, 64]], compare_op=ALU.is_ge,
                                   fill=0.0, base=-1, channel_multiplier=1)

        P.op("pool", mkmasks, writes=[mk_b])
        mb3 = lambda i: msk[:, i, :].unsqueeze(1).broadcast_to([64, 8, 64])
        idb = C.ident_bf[0:64, 0:64]
        idf3 = C.ident_f[0:64, 0:64].unsqueeze(1).broadcast_to([64, 8, 64])
        bc = lambda col: col[:, :].unsqueeze(2).broadcast_to([64, 8, TL])

        nm = Normer(P, nc, es, C, A["mix_norm"], "rn", nslots=1)
        xbuf = sb("rxbuf", [128, 1, D], F32)
        xr = Ring([(i, Buf()) for i in range(1)])
        xnx = sb("xnx", [128, 2, 8, TL + 1], BF16)
        xnb = [Buf(), Buf()]
        dxn = sb("dxn", [128, 8, TL], BF16)
        dxb = Buf()
        P.op("pool", lambda e: e.memset(xnx[:, 1, :, TL:TL + 1], 0.0), writes=[xnb[1]])
        F = {}
        FB = {}
        for nme, dt in [("r", F32), ("k", F32), ("sgd", F32), ("a", F32), ("kk", F32),
                        ("t1", F32), ("t2", F32), ("cum", F32), ("e1", F32), ("e2", F32), ("sqb", BF16)]:
            alias = {"e1": "t1", "e2": "a"}
            if nme in alias:
                F[nme] = F[alias[nme]]
                FB[nme] = FB[alias[nme]]
                continue
            F[nme] = sb("f_" + nme, [64, 8, TL], dt)
            FB[nme] = Buf(nme)
        X2 = {}
        X2B = {}
        for nme in ("rT", "aT", "bT", "kT", "bH", "kH", "vb", "g", "bon"):
            X2[nme] = sb("x_" + nme, [64, 2, 8, TL], BF16)
            X2B[nme] = [Buf(nme + "0"), Buf(nme + "1")]
        lora = sb("lora", [64, 5, TL], BF16)
        ztok = sb("ztok", [128, 2, 512], F32)
        ztr = Ring([(i, Buf()) for i in range(2)])
        lora_b = Buf()
        gC = sb("gC", [64, 2, 8, TL // 64], F32)
        gC_bs = [Buf(), Buf()]
        Sf = sb("Sf", [64, 8, 64], F32)
        Sb = sb("Sb", [64, 8, 64], BF16)
        St = sb("St", [64, 8, 64], F32)
        S_b, St_b = Buf(), Buf()
        P.op("dve", lambda e: e.memset(Sf[:], 0.0), writes=[S_b])
        P.op("dve", lambda e: e.memset(Sb[:], 0.0), reads=[S_b], writes=[S_b])
        ost = sb("rost", [64, 8, TL], BF16)
        ost_b = Buf()
        def pair(name, dt=BF16, n=2):
            t = sb(name, [64, n, 8, 64], dt)
            return t, [Buf() for _ in range(n)]
        Atok, Atok_b = pair("Atok")
        BHtok, BHtok_b = pair("BHtok")
        KHtok, KHtok_b = pair("KHtok")
        Vtok, Vtok_b = pair("Vtok")
        Mrb, Mrb_b = pair("Mrb")
        Mrk, Mrk_b = pair("Mrk")
        Lak, Lak_b = pair("Lak")
        Nn, Nn_b = pair("Nn", BF16, 4)
        Mm, Mm_b = pair("Mm", BF16, 4)
        Qq, Qq_b = pair("Qq", BF16, 4)
        WT, WT_b = pair("WT")
        Xx, Xx_b = pair("Xx")
        Uu, Uu_b = pair("Uu")
        Ys, Ys_b = pair("Ys", F32, 1)
        Yn, Yn_b = pair("Yn", BF16, 2)
        gst = sb("gst", [64, 2, 8, 6], F32)
        gst_b = [Buf(), Buf()]
        pb = es.enter_context(nc.psum_tensor("rpb", [128, 7, 512], F32))
        pr = Ring([(i, Buf()) for i in range(7)])
        pv = lambda i: pb[0:64, i, :].rearrange("p (h t) -> p h t", h=8)
        pvb = lambda i: pb[0:64, i, :].bitcast(BF16)[:, 0:512].rearrange("p (h t) -> p h t", h=8)

        xview = x_dram.rearrange("(t p) d -> t p d", p=128)

        def mm8(bank, parts, rd):
            i, bbuf = bank
            ops = [[(lf(h), rf(h)) for (lf, rf) in parts] for h in range(8)]

            def th(e):
                ins = None
                for h in range(8):
                    for pi, (l, r) in enumerate(ops[h]):
                        ins = e.matmul(pb[0:64, i, h * 64:(h + 1) * 64], lhsT=l, rhs=r,
                                       start=(pi == 0), stop=(pi == len(ops[h]) - 1))
                return ins

            P.op("pe", th, reads=rd, writes=[bbuf])

        def tr8(bank, src_fn, rd):
            i, bbuf = bank
            srcs = [src_fn(h) for h in range(8)]

            def th(e):
                ins = None
                v = pvb(i)
                for h in range(8):
                    ins = e.transpose(out=v[:, h, :], in_=srcs[h], identity=idb)
                return ins

            P.op("pe", th, reads=rd + [C.b], writes=[bbuf])

        NB = S // TL

        def prep(n):
            par = n % 2
            cb = n % 2
            pc = 1 - cb
            X = {k: X2[k][:, par] for k in X2}
            XB = {k: X2B[k][par] for k in X2}
            P.op("pool", lambda e: e.tensor_copy(out=xnx[:, cb, :, 0:1], in_=xnx[:, pc, :, TL:TL + 1]),
                 reads=[xnb[pc]], writes=[xnb[cb]])
            for j in range(TL // 128):
                xi, xb_ = xr.next()
                P.dma("sp", lambda e, t=n * (TL // 128) + j, xi=xi: e.dma_start(out=xbuf[:, xi, :], in_=xview[t]), writes=[xb_])
                nm.run(xbuf[:, xi, :], xb_, xnx[:, cb], xnb[cb], 1 + j * 128)
            P.op("pool", lambda e: e.tensor_tensor(out=dxn[:], in0=xnx[:, cb, :, 0:TL], in1=xnx[:, cb, :, 1:TL + 1],
                                                   op=ALU.subtract), reads=[xnb[cb]], writes=[dxb])
            yield

            def projtok(c0, ncol):
                bank = pr.next()
                i = bank[0]

                def th(e):
                    ins = None
                    for kc in range(8):
                        e.matmul(pb[:, i, 0:ncol], lhsT=xnx[:, cb, kc, 1:TL + 1], rhs=wr[:, kc, c0:c0 + ncol],
                                 start=(kc == 0), stop=False)
                    for kc in range(8):
                        ins = e.matmul(pb[:, i, 0:ncol], lhsT=dxn[:, kc, :], rhs=wmu[:, kc, c0:c0 + ncol],
                                       start=False, stop=(kc == 7))
                    return ins

                P.op("pe", th, reads=[wr_b, wmu_b, xnb[cb], dxb], writes=[bank[1]])
                zi, zb = ztr.next()
                P.op("act", lambda e: e.activation(out=ztok[:, zi, 0:ncol], in_=pb[:, i, 0:ncol], func=AF.Copy),
                     reads=[bank[1]], writes=[zb])
                return zi, zb

            def trz(zi, zb, cols, m):
                bank = pr.next()
                i = bank[0]

                def th(e):
                    ins = None
                    for q, c in enumerate(cols):
                        ins = e.transpose(out=pb[0:m, i, q * TL:(q + 1) * TL], in_=ztok[:, zi, c:c + m], identity=C.ident_f[:])
                    return ins

                P.op("pe", th, reads=[zb, C.b], writes=[bank[1]])
                return bank

            for qi, qn in enumerate(("r", "k", "v")):
                zi, zb = projtok(qi * 512, 512)
                yield
                for h0 in (0, 4):
                    bank = trz(zi, zb, [(h0 + q) * 64 for q in range(4)], 64)
                    dst, dstb = (X["vb"], XB["vb"]) if qn == "v" else (F[qn], FB[qn])
                    P.op("act", lambda e, dst=dst, h0=h0, i=bank[0]: e.activation(
                        out=dst[:, h0:h0 + 4, :], in_=pb[0:64, i, 0:4 * TL].rearrange("p (q t) -> p q t", q=4), func=AF.Copy),
                        reads=[bank[1]], writes=[dstb])
                    yield
            zi, zb = projtok(1536, 288)
            yield
            for li, (c0, m, fn) in enumerate([(0, 64, AF.Tanh), (64, 64, AF.Copy), (128, 64, AF.Sigmoid),
                                              (192, 64, AF.Sigmoid), (256, 32, AF.Sigmoid)]):
                bank = trz(zi, zb, [c0], m)
                P.op("act", lambda e, li=li, m=m, fn=fn, i=bank[0]: e.activation(out=lora[0:m, li, :], in_=pb[0:m, i, 0:TL],
                                                                                 func=fn),
                     reads=[bank[1]], writes=[lora_b])
                yield
            for h in range(8):
                bank = pr.next()
                P.op("pe", lambda e, h=h, i=bank[0]: e.matmul(pb[0:64, i, 0:TL], lhsT=w2[:, h * 64:(h + 1) * 64],
                                                              rhs=lora[:, 0, :], start=True, stop=True),
                     reads=[lw_b, lora_b], writes=[bank[1]])
                P.op("act", lambda e, h=h, i=bank[0]: e.activation(out=F["sgd"][:, h, :], in_=pb[0:64, i, 0:TL],
                                                                   func=AF.Sigmoid, bias=cw0[:, h:h + 1]),
                     reads=[bank[1], b_w0], writes=[FB["sgd"]])
                bank = pr.next()
                P.op("pe", lambda e, h=h, i=bank[0]: e.matmul(pb[0:64, i, 0:TL], lhsT=a2[:, h * 64:(h + 1) * 64],
                                                              rhs=lora[:, 1, :], start=True, stop=True),
                     reads=[lw_b, lora_b], writes=[bank[1]])
                P.op("act", lambda e, h=h, i=bank[0]: e.activation(out=F["a"][:, h, :], in_=pb[0:64, i, 0:TL],
                                                                   func=AF.Sigmoid, bias=ca0[:, h:h + 1]),
                     reads=[bank[1], b_a0], writes=[FB["a"]])
                bank = pr.next()

                def gmm(e, h=h, i=bank[0]):
                    e.matmul(pb[0:64, i, 0:TL], lhsT=g2[:, 0, h * 64:(h + 1) * 64], rhs=lora[:, 2, :], start=True, stop=False)
                    e.matmul(pb[0:64, i, 0:TL], lhsT=g2[:, 1, h * 64:(h + 1) * 64], rhs=lora[:, 3, :], start=False, stop=False)
                    return e.matmul(pb[0:64, i, 0:TL], lhsT=g2[0:32, 2, h * 64:(h + 1) * 64], rhs=lora[0:32, 4, :],
                                    start=False, stop=True)

                P.op("pe", gmm, reads=[lw_b, lora_b], writes=[bank[1]])
                P.op("act", lambda e, h=h, i=bank[0]: e.activation(out=X["g"][:, h, :], in_=pb[0:64, i, 0:TL], func=AF.Copy),
                     reads=[bank[1]], writes=[XB["g"]])
                yield
            P.op("dve", lambda e: e.tensor_tensor(out=F["kk"][:], in0=F["k"][:], in1=bc(ckk), op=ALU.mult),
                 reads=[FB["k"], b_kk], writes=[FB["kk"]])
            P.op("pool", lambda e: e.tensor_tensor(out=F["sqb"][:], in0=F["kk"][:], in1=F["kk"][:], op=ALU.mult),
                 reads=[FB["kk"]], writes=[FB["sqb"]])
            yield
            for h in range(8):
                bank = pr.next()
                P.op("pe", lambda e, h=h, i=bank[0]: e.matmul(pb[0:64, i, 0:TL], lhsT=ones_bf[:], rhs=F["sqb"][:, h, :],
                                                              start=True, stop=True),
                     reads=[mk_b, FB["sqb"]], writes=[bank[1]])
                P.op("act", lambda e, h=h, i=bank[0]: e.activation(out=F["t1"][:, h, :], in_=pb[0:64, i, 0:TL], func=AF.Sqrt),
                     reads=[bank[1]], writes=[FB["t1"]])
                if h % 2 == 1:
                    yield
            P.op("dve", lambda e: e.tensor_scalar(out=F["t1"][:], in0=F["t1"][:], scalar1=1e-12, scalar2=None,
                                                  op0=ALU.max), reads=[FB["t1"]], writes=[FB["t1"]])
            P.op("dve", lambda e: e.reciprocal(out=F["t1"][:], in_=F["t1"][:]), reads=[FB["t1"]], writes=[FB["t1"]])
            yield
            P.op("dve", lambda e: e.tensor_tensor(out=F["kk"][:], in0=F["kk"][:], in1=F["t1"][:], op=ALU.mult),
                 reads=[FB["kk"], FB["t1"]], writes=[FB["kk"]])
            P.op("dve", lambda e: e.scalar_tensor_tensor(out=F["t2"][:], in0=F["a"][:], scalar=-1.0, in1=bc(cka),
                                                         op0=ALU.add, op1=ALU.mult),
                 reads=[FB["a"], b_ka], writes=[FB["t2"]])
            yield
            P.op("dve", lambda e: e.scalar_tensor_tensor(out=F["k"][:], in0=F["t2"][:], scalar=1.0, in1=F["k"][:],
                                                         op0=ALU.add, op1=ALU.mult),
                 reads=[FB["t2"], FB["k"]], writes=[FB["k"]])
            P.op("pool", lambda e: e.tensor_tensor(out=F["t2"][:], in0=F["kk"][:], in1=F["a"][:], op=ALU.mult),
                 reads=[FB["kk"], FB["a"]], writes=[FB["t2"]])
            yield
            P.op("dve", lambda e: e.tensor_tensor_scan(out=F["cum"][:].rearrange("p h t -> p (h t)"),
                                                       data0=rmask[:].rearrange("p h c t -> p (h c t)"),
                                                       data1=F["sgd"][:].rearrange("p h t -> p (h t)"),
                                                       initial=0.0, op0=ALU.mult, op1=ALU.add),
                 reads=[FB["sgd"], mk_b], writes=[FB["cum"]])
            cum4 = F["cum"][:].rearrange("p h (c t) -> p h c t", t=64)
            yield
            P.op("act", lambda e: e.activation(out=F["e1"][:], in_=F["cum"][:], func=AF.Exp, scale=-C0),
                 reads=[FB["cum"]], writes=[FB["e1"]])
            P.op("dve", lambda e: e.tensor_tensor(out=X["rT"][:], in0=F["r"][:], in1=F["e1"][:], op=ALU.mult),
                 reads=[FB["r"], FB["e1"]], writes=[XB["rT"]])
            P.op("act", lambda e: e.activation(out=gC[:, par], in_=cum4[:, :, :, 63], func=AF.Exp, scale=-C0),
                 reads=[FB["cum"]], writes=[gC_bs[par]])
            yield
            P.op("pool", lambda e: e.tensor_tensor(out=F["e2"][:], in0=F["cum"][:], in1=F["sgd"][:], op=ALU.subtract),
                 reads=[FB["cum"], FB["sgd"]], writes=[FB["e2"]])
            P.op("act", lambda e: e.activation(out=F["e2"][:], in_=F["e2"][:], func=AF.Exp, scale=-C0),
                 reads=[FB["e2"]], writes=[FB["e2"]])
            P.op("dve", lambda e: e.scalar_tensor_tensor(out=X["aT"][:], in0=F["kk"][:], scalar=-1.0, in1=F["e2"][:],
                                                         op0=ALU.mult, op1=ALU.mult),
                 reads=[FB["kk"], FB["e2"]], writes=[XB["aT"]])
            yield
            P.op("act", lambda e: e.activation(out=F["e1"][:], in_=F["cum"][:], func=AF.Exp, scale=C0),
                 reads=[FB["cum"]], writes=[FB["e1"]])
            P.op("dve", lambda e: e.tensor_tensor(out=X["bT"][:], in0=F["t2"][:], in1=F["e1"][:], op=ALU.mult),
                 reads=[FB["t2"], FB["e1"]], writes=[XB["bT"]])
            P.op("pool", lambda e: e.tensor_tensor(out=X["kT"][:], in0=F["k"][:], in1=F["e1"][:], op=ALU.mult),
                 reads=[FB["k"], FB["e1"]], writes=[XB["kT"]])
            yield
            P.op("dve", lambda e: e.tensor_tensor(out=F["e2"][:].rearrange("p h (c t) -> p h c t", t=64),
                                                  in0=cum4[:, :, :, 63:64].broadcast_to([64, 8, TL // 64, 64]), in1=cum4,
                                                  op=ALU.subtract),
                 reads=[FB["cum"]], writes=[FB["e2"]])
            P.op("act", lambda e: e.activation(out=F["e2"][:], in_=F["e2"][:], func=AF.Exp, scale=-C0),
                 reads=[FB["e2"]], writes=[FB["e2"]])
            yield
            P.op("dve", lambda e: e.tensor_tensor(out=X["bH"][:], in0=F["t2"][:], in1=F["e2"][:], op=ALU.mult),
                 reads=[FB["t2"], FB["e2"]], writes=[XB["bH"]])
            P.op("pool", lambda e: e.tensor_tensor(out=X["kH"][:], in0=F["k"][:], in1=F["e2"][:], op=ALU.mult),
                 reads=[FB["k"], FB["e2"]], writes=[XB["kH"]])
            yield
            P.op("dve", lambda e: e.tensor_tensor(out=F["t1"][:], in0=F["r"][:], in1=F["k"][:], op=ALU.mult),
                 reads=[FB["r"], FB["k"]], writes=[FB["t1"]])
            P.op("pool", lambda e: e.tensor_tensor(out=F["sqb"][:], in0=F["t1"][:], in1=bc(crk), op=ALU.mult),
                 reads=[FB["t1"], b_rk], writes=[FB["sqb"]])
            yield
            for h in range(8):
                bank = pr.next()
                P.op("pe", lambda e, h=h, i=bank[0]: e.matmul(pb[0:64, i, 0:TL], lhsT=ones_bf[:], rhs=F["sqb"][:, h, :],
                                                              start=True, stop=True),
                     reads=[mk_b, FB["sqb"]], writes=[bank[1]])
                P.op("dve", lambda e, h=h, i=bank[0]: e.tensor_tensor(out=X["bon"][:, h, :], in0=pb[0:64, i, 0:TL],
                                                                      in1=X["vb"][:, h, :], op=ALU.mult),
                     reads=[bank[1], XB["vb"]], writes=[XB["bon"]])
                if h % 2 == 1:
                    yield

        def chunk_pre(n, c, out):
            par = n % 2
            X = {k: X2[k][:, par] for k in X2}
            XB = {k: X2B[k][par] for k in X2}
            cs = slice(c * 64, (c + 1) * 64)
            for (dst, dbs, src) in ((Atok, Atok_b, "aT"), (BHtok, BHtok_b, "bH"), (KHtok, KHtok_b, "kH"), (Vtok, Vtok_b, "vb")):
                bank = pr.next()
                tr8(bank, lambda h, src=src: X[src][:, h, cs], [XB[src]])
                P.op("act", lambda e, dst=dst, i=bank[0]: e.activation(out=dst[:, c], in_=pvb(i), func=AF.Copy),
                     reads=[bank[1]], writes=[dbs[c]])
                yield

            def gmat(lname, rname, mi, dst, dbs, slot):
                bank = pr.next()
                mm8(bank, [(lambda h: X[lname][:, h, cs], lambda h: X[rname][:, h, cs])], [XB[lname], XB[rname]])
                P.op("dve", lambda e, i=bank[0]: e.tensor_tensor(out=dst[:, slot], in0=pv(i), in1=mb3(mi), op=ALU.mult),
                     reads=[bank[1], mk_b], writes=[dbs[slot]])

            base = 2 * c
            gmat("bT", "aT", 0, Mm, Mm_b, base)
            yield
            gmat("bT", "rT", 1, Mrb, Mrb_b, c)
            yield
            gmat("kT", "aT", 0, Lak, Lak_b, c)
            yield
            gmat("kT", "rT", 1, Mrk, Mrk_b, c)
            yield
            gmat("aT", "bT", 2, Nn, Nn_b, base)
            yield
            P.op("pool", lambda e: e.tensor_tensor(out=Qq[:, base], in0=Mm[:, base], in1=idf3, op=ALU.add),
                 reads=[Mm_b[base], C.b], writes=[Qq_b[base]])
            ni = mi_ = qi_ = base
            for lvl in range(1, 6):
                nn_ = base + (1 - (ni - base))
                nm_ = base + (1 - (mi_ - base))
                nq_ = base + (1 - (qi_ - base))
                bank = pr.next()
                mm8(bank, [(lambda h: Mm[:, mi_, h, :], lambda h: Nn[:, ni, h, :])], [Mm_b[mi_], Nn_b[ni]])
                if lvl < 5:
                    bank2 = pr.next()
                    mm8(bank2, [(lambda h: Nn[:, ni, h, :], lambda h: Mm[:, mi_, h, :])], [Mm_b[mi_], Nn_b[ni]])
                P.op("act", lambda e, nn_=nn_, i=bank[0]: e.activation(out=Nn[:, nn_], in_=pv(i), func=AF.Copy),
                     reads=[bank[1]], writes=[Nn_b[nn_]])
                if lvl < 5:
                    P.op("dve", lambda e, nm_=nm_, i=bank2[0]: e.tensor_copy(out=Mm[:, nm_], in_=pv(i)),
                         reads=[bank2[1]], writes=[Mm_b[nm_]])
                    mi_ = nm_
                ni = nn_
                yield
                bank3 = pr.next()
                mm8(bank3, [(lambda h: Nn[:, ni, h, :], lambda h: Qq[:, qi_, h, :])], [Qq_b[qi_], Nn_b[ni]])
                P.op("dve", lambda e, nq_=nq_, qo=qi_, i=bank3[0]: e.tensor_tensor(out=Qq[:, nq_], in0=pv(i), in1=Qq[:, qo],
                                                                                   op=ALU.add),
                     reads=[bank3[1], Qq_b[qi_]], writes=[Qq_b[nq_]])
                qi_ = nq_
                yield
            bank = pr.next()
            mm8(bank, [(lambda h: Atok[:, c, h, :], lambda h: Qq[:, qi_, h, :])], [Atok_b[c], Qq_b[qi_]])
            P.op("act", lambda e, i=bank[0]: e.activation(out=WT[:, c], in_=pv(i), func=AF.Copy),
                 reads=[bank[1]], writes=[WT_b[c]])
            bank = pr.next()
            mm8(bank, [(lambda h: Lak[:, c, h, :], lambda h: Vtok[:, c, h, :])], [Lak_b[c], Vtok_b[c]])
            P.op("dve", lambda e, i=bank[0]: e.tensor_copy(out=Xx[:, c], in_=pv(i)), reads=[bank[1]], writes=[Xx_b[c]])
            out["q"] = qi_
            yield

        def chain(n, c, qi_):
            par = n % 2
            X = {k: X2[k][:, par] for k in X2}
            XB = {k: X2B[k][par] for k in X2}
            cs = slice(c * 64, (c + 1) * 64)
            bank = pr.next()
            mm8(bank, [(lambda h: WT[:, c, h, :], lambda h: Sb[:, h, :]),
                       (lambda h: Qq[:, qi_, h, :], lambda h: Xx[:, c, h, :])], [WT_b[c], S_b, Qq_b[qi_], Xx_b[c]])
            P.op("act", lambda e, i=bank[0]: e.activation(out=Uu[:, c], in_=pv(i), func=AF.Copy),
                 reads=[bank[1]], writes=[Uu_b[c]])
            banky = pr.next()
            mm8(banky, [(lambda h: X["rT"][:, h, cs], lambda h: Sb[:, h, :]),
                        (lambda h: Mrb[:, c, h, :], lambda h: Uu[:, c, h, :]),
                        (lambda h: Mrk[:, c, h, :], lambda h: Vtok[:, c, h, :])],
                [XB["rT"], S_b, Mrb_b[c], Uu_b[c], Mrk_b[c], Vtok_b[c]])
            banks = pr.next()
            mm8(banks, [(lambda h: BHtok[:, c, h, :], lambda h: Uu[:, c, h, :]),
                        (lambda h: KHtok[:, c, h, :], lambda h: Vtok[:, c, h, :])], [BHtok_b[c], Uu_b[c], KHtok_b[c], Vtok_b[c]])
            P.op("dve", lambda e: e.tensor_tensor(out=St[:], in0=Sf[:],
                                                  in1=gC[:, par, :, c:c + 1].broadcast_to([64, 8, 64]), op=ALU.mult),
                 reads=[S_b, gC_bs[par]], writes=[St_b])
            P.op("dve", lambda e, i=banks[0]: e.tensor_tensor(out=Sf[:], in0=pv(i), in1=St[:], op=ALU.add),
                 reads=[banks[1], St_b], writes=[S_b])
            P.op("act", lambda e: e.activation(out=Sb[:], in_=Sf[:], func=AF.Copy), reads=[S_b], writes=[S_b])
            yield
            y_b, yq_b, g_b = Ys_b[0], St_b, gst_b[c]
            P.op("act", lambda e, i=banky[0]: e.activation(out=Ys[:, 0], in_=pv(i), func=AF.Copy),
                 reads=[banky[1]], writes=[y_b])
            P.op("pool", lambda e: e.tensor_tensor(out=St[:], in0=Ys[:, 0], in1=Ys[:, 0], op=ALU.mult),
                 reads=[y_b], writes=[yq_b])
            P.op("dve", lambda e: e.tensor_reduce(out=gst[:, c, :, 0], in_=Ys[:, 0], axis=AX.X, op=ALU.add),
                 reads=[y_b], writes=[g_b])
            P.op("dve", lambda e: e.tensor_reduce(out=gst[:, c, :, 1], in_=St[:], axis=AX.X, op=ALU.add),
                 reads=[yq_b, g_b], writes=[g_b])
            yield
            P.op("dve", lambda e: e.tensor_scalar(out=gst[:, c, :, 2], in0=gst[:, c, :, 0], scalar1=1.0 / 64,
                                                  scalar2=None, op0=ALU.mult), reads=[g_b], writes=[g_b])
            P.op("dve", lambda e: e.tensor_tensor(out=gst[:, c, :, 3], in0=gst[:, c, :, 2], in1=gst[:, c, :, 2],
                                                  op=ALU.mult), reads=[g_b], writes=[g_b])
            P.op("dve", lambda e: e.scalar_tensor_tensor(out=gst[:, c, :, 4], in0=gst[:, c, :, 1], scalar=1.0 / 64,
                                                         in1=gst[:, c, :, 3], op0=ALU.mult, op1=ALU.subtract),
                 reads=[g_b], writes=[g_b])
            yield
            P.op("act", lambda e: e.activation(out=gst[:, c, :, 5], in_=gst[:, c, :, 4], func=AF.Sqrt,
                                               bias=C.eps[0:64, 1:2]), reads=[g_b, C.b], writes=[g_b])
            P.op("dve", lambda e: e.reciprocal(out=gst[:, c, :, 5], in_=gst[:, c, :, 5]), reads=[g_b], writes=[g_b])
            P.op("dve", lambda e: e.tensor_tensor(out=Ys[:, 0], in0=Ys[:, 0],
                                                  in1=gst[:, c, :, 2:3].broadcast_to([64, 8, 64]),
                                                  op=ALU.subtract), reads=[y_b, g_b], writes=[y_b])
            yield
            P.op("dve", lambda e: e.tensor_tensor(out=Yn[:, c], in0=Ys[:, 0],
                                                  in1=gst[:, c, :, 5:6].broadcast_to([64, 8, 64]),
                                                  op=ALU.mult), reads=[y_b, g_b], writes=[Yn_b[c]])
            bank = pr.next()
            tr8(bank, lambda h: Yn[:, c, h, :], [Yn_b[c]])
            bc64 = lambda col: col[:, :].unsqueeze(2).broadcast_to([64, 8, 64])
            P.op("dve", lambda e, i=bank[0]: e.tensor_tensor(out=Ys[:, 0], in0=pvb(i), in1=bc64(clw), op=ALU.mult),
                 reads=[bank[1], b_lw, Yn_b[c]], writes=[y_b])
            P.op("pool", lambda e: e.tensor_tensor(out=Ys[:, 0], in0=Ys[:, 0], in1=bc64(clb), op=ALU.add),
                 reads=[y_b, b_lb], writes=[y_b])
            yield
            P.op("dve", lambda e: e.tensor_tensor(out=Ys[:, 0], in0=Ys[:, 0], in1=X["bon"][:, :, cs], op=ALU.add),
                 reads=[y_b, XB["bon"]], writes=[y_b])
            P.op("dve", lambda e: e.tensor_tensor(out=ost[:, :, cs], in0=Ys[:, 0], in1=X["g"][:, :, cs], op=ALU.mult),
                 reads=[y_b, XB["g"]], writes=[ost_b])
            yield

        def scan(n):
            par = n % 2
            X = {k: X2[k][:, par] for k in X2}
            XB = {k: X2B[k][par] for k in X2}
            outs = [{} for _ in range(TL // 64)]
            gens = [chunk_pre(n, c, outs[c]) for c in range(TL // 64)]
            alive = [True] * len(gens)
            while any(alive):
                for gi_, g_ in enumerate(gens):
                    if alive[gi_]:
                        try:
                            next(g_)
                        except StopIteration:
                            alive[gi_] = False
                yield
            for c in range(TL // 64):
                for _ in chain(n, c, outs[c]["q"]):
                    yield
            P.dma("sp", lambda e: e.dma_start(
                out=yr_dram.rearrange("(h v) s -> v h s", v=64)[:, :, n * TL:(n + 1) * TL], in_=ost[:]),
                reads=[ost_b])
            yield

        def run2(f, b):
            fa, ba = f is not None, b is not None
            while fa or ba:
                if fa:
                    try:
                        next(f)
                    except StopIteration:
                        fa = False
                if ba:
                    try:
                        next(b)
                    except StopIteration:
                        ba = False

        for n in range(NB + 1):
            run2(prep(n) if n < NB else None, scan(n - 1) if n >= 1 else None)
    P.barrier()


NITER = 16


def phase_attn(P, nc, C, A, ya_dram):
    x_dram = A["x"]
    with ExitStack() as es:
        sb = lambda name, shape, dt=F32: es.enter_context(nc.sbuf_tensor(name, shape, dt))
        wa = sb("wa", [128, 8, ATTN_IN + 64], BF16)
        wa_b = Buf()
        load_weight_bf16(P, nc, wa, wa_b, A["w_in"], 0, ATTN_IN, 8)
        wv_ = A["w_in"].rearrange("(kc p) n -> p kc n", p=128)
        for kc in range(8):
            P.dma("pool", lambda e, kc=kc: e.dma_start(out=wa[:, kc, ATTN_IN:ATTN_IN + 64], in_=wv_[:, kc, 2048:2112]),
                  writes=[wa_b])
        kT = sb("kT", [128, 4, S], BF16)
        kiT = sb("kiT", [128, S], BF16)
        vaug = sb("vaug", [128, NT, 8, 65], BF16)
        kT_bs = [Buf() for _ in range(NT)]
        kiT_bs = [Buf() for _ in range(NT)]
        va_bs = [Buf() for _ in range(NT)]
        P.op("pool", lambda e: e.memset(vaug[:, :, :, 64:65], 1.0), writes=va_bs)
        gqk = sb("gqk", [128, 2], F32)
        gqk_b = Buf()
        for half in range(2):
            P.dma("sp", lambda e, half=half: e.dma_start(out=gqk[half * 64:(half + 1) * 64, 0:1],
                                                         in_=A["attn_q_norm"].rearrange("o d -> d o"),
                                                         allow_slow_non_contiguous=True), writes=[gqk_b])
            P.dma("sp", lambda e, half=half: e.dma_start(out=gqk[half * 64:(half + 1) * 64, 1:2],
                                                         in_=A["attn_k_norm"].rearrange("o d -> d o"),
                                                         allow_slow_non_contiguous=True), writes=[gqk_b])
        P.op("dve", lambda e: e.tensor_scalar(out=gqk[:, 0:1], in0=gqk[:, 0:1], scalar1=0.125, scalar2=None, op0=ALU.mult),
             reads=[gqk_b], writes=[gqk_b])
        btf = sb("btf", [128, 8, 2, 128], F32)
        bt = sb("bt", [128, 8, 2, 128], BF16)
        b31 = sb("b31_sb", [128, 8], F32)
        bt_b = Buf()
        P.dma("sp", lambda e: e.dma_start(out=btf[:], in_=A["bias_tiles"].rearrange("h c s t -> s h c t")), writes=[bt_b])
        P.dma("sp", lambda e: e.dma_start(out=b31[:], in_=A["b31"].partition_broadcast(128)), writes=[bt_b])
        P.op("dve", lambda e: e.tensor_tensor(out=bt[:].rearrange("p h c t -> p h (c t)"),
                                              in0=btf[:].rearrange("p h c t -> p h (c t)"),
                                              in1=b31[:, :].unsqueeze(2).broadcast_to([128, 8, 256]), op=ALU.subtract),
             reads=[bt_b], writes=[bt_b])
        cmask = sb("cmask", [128, 128], F32)
        onesblk = sb("onesblk", [128, 128], BF16)
        cm_b = Buf()
        pw2 = sb("pw2", [128, 2 * NITER], F32)
        halfs = sb("halfs", [128, 2 * NITER], F32)
        hf_b = Buf()

        def mkc(e):
            e.memset(cmask[:], 0.0)
            e.affine_select(out=cmask[:], in_=cmask[:], pattern=[[-1, 128]], compare_op=ALU.is_ge, fill=NEG, base=0,
                            channel_multiplier=1)
            for j in range(NITER):
                e.memset(pw2[:, j:j + 1], 0.5 ** (j + 1))
                e.memset(pw2[:, NITER + j:NITER + j + 1], 0.5 ** (j + 2))
            e.memset(onesblk[:], 0.0)
            e.memset(onesblk[0:64, 0:64], 1.0)
            return e.memset(onesblk[64:128, 64:128], 1.0)

        P.op("pool", mkc, writes=[cm_b])
        nm = Normer(P, nc, es, C, A["mix_norm"], "an", nslots=1)
        xbuf = sb("axbuf", [128, 2, D], F32)
        xr = Ring([(i, Buf()) for i in range(2)])
        xn = sb("axn", [128, 8, 128], BF16)
        xn_b = Buf()
        q_i = sb("q_i", [128, 2, 4, 128], BF16)
        qi_i = sb("qi_i", [128, 4, 128], BF16)
        wi_i = sb("wi_i", [128, 8], F32)
        q_bs, qi_b, wi_b = [Buf(), Buf()], Buf(), Buf()
        sq = sb("asq", [128, 512], BF16)
        rn = sb("arn", [128, 512], F32)
        sq_b, rn_b = Buf(), Buf()
        sc = sb("sc", [128, S], F32)
        sc_b = Buf()
        rbuf = sb("rbuf", [128, 2, 512], F32)
        rr = Ring([(i, Buf()) for i in range(2)])
        junk = sb("ajunk", [128, S], BF16)
        junk_b = Buf()
        bs = sb("bs", [128, 8], F32)
        bs_b = Buf()
        maskT = sb("maskT", [128, 2, NT, 128], BF16)
        mT_bs = [Buf(), Buf()]
        Et = sb("Et", [128, 3, 4, 128], BF16)
        er = Ring([(i, Buf()) for i in range(3)])
        Pt = sb("Pt", [128, 8, 4, 128], BF16)
        ptr = Ring([(i, Buf()) for i in range(8)])
        rden = sb("rden", [128, 2], F32)
        rdr = Ring([(i, Buf()) for i in range(2)])
        ytile = sb("ytile", [128, 512], BF16)
        yt_b = Buf()
        yst = sb("yst", [128, 4, 128], BF16)
        ys_b = Buf()
        pg = es.enter_context(nc.psum_tensor("apg", [128, 6, 512], F32))
        gr = Ring([(i, Buf()) for i in range(3)])
        accr = Ring([(i, Buf()) for i in range(3, 4)])
        lgr = Ring([(i, Buf()) for i in range(4, 6)])
        po = es.enter_context(nc.psum_tensor("apo", [128, 1, 512], F32))
        orr = Ring([(i, Buf()) for i in range(1)])
        pgb = lambda i: pg[:, i, :].bitcast(BF16)
        if SBUF_DEBUG:
            print("attn sbuf remaining", nc.sbuf_bytes_remaining)
        xview = x_dram.rearrange("(t p) d -> t p d", p=128)
        yav = ya_dram.rearrange("(m p) s -> p m s", p=128)

        def front(i):
            ts_ = slice(i * 128, (i + 1) * 128)
            nkb = i + 1
            W = nkb * 128
            q_b = q_bs[i % 2]
            kT_b, kiT_b, va_b = kT_bs[i], kiT_bs[i], va_bs[i]
            xi, xb_ = xr.next()
            P.dma("sp", lambda e, i=i, xi=xi: e.dma_start(out=xbuf[:, xi, :], in_=xview[i]), writes=[xb_])
            nm.run(xbuf[:, xi, :], xb_, xn, xn_b, 0)
            yield

            def proj4(c0, bank):
                bi, bb = bank

                def th(e):
                    ins = None
                    for m in range(4):
                        for kc in range(8):
                            ins = e.matmul(pg[:, bi, m * 128:(m + 1) * 128], lhsT=wa[:, kc, c0 + m * 128:c0 + (m + 1) * 128],
                                           rhs=xn[:, kc, :], start=(kc == 0), stop=(kc == 7))
                    return ins

                P.op("pe", th, reads=[wa_b, xn_b], writes=[bb])

            for which, c0 in ((0, 0), (1, 512)):
                bank = gr.next()
                proj4(c0, bank)
                P.op("act", lambda e, bi=bank[0]: e.activation(out=sq[:], in_=pg[:, bi, :], func=AF.Square),
                     reads=[bank[1]], writes=[sq_b])
                bank2 = gr.next()
                P.op("pe", lambda e, bi=bank2[0]: e.matmul(pg[:, bi, :], lhsT=onesblk[:], rhs=sq[:], start=True, stop=True),
                     reads=[cm_b, sq_b], writes=[bank2[1]])
                P.op("act", lambda e, bi=bank2[0]: e.activation(out=rn[:], in_=pg[:, bi, :], func=AF.Ln, scale=1.0 / 64,
                                                                bias=C.eps[:, 0:1]), reads=[bank2[1], C.b], writes=[rn_b])
                P.op("act", lambda e: e.activation(out=rn[:], in_=rn[:], func=AF.Exp, scale=-0.5), reads=[rn_b], writes=[rn_b])
                if which == 0:
                    P.op("dve", lambda e, bi=bank[0]: e.scalar_tensor_tensor(
                        out=q_i[:, i % 2].rearrange("p m t -> p (m t)"), in0=pg[:, bi, :], scalar=gqk[:, 0:1], in1=rn[:],
                        op0=ALU.mult, op1=ALU.mult), reads=[bank[1], rn_b, gqk_b], writes=[q_b])
                else:
                    P.op("dve", lambda e, bi=bank[0], ts_=ts_: e.scalar_tensor_tensor(
                        out=kT[:, :, ts_], in0=pg[:, bi, :].rearrange("p (m t) -> p m t", m=4), scalar=gqk[:, 1:2],
                        in1=rn[:].rearrange("p (m t) -> p m t", m=4), op0=ALU.mult, op1=ALU.mult),
                        reads=[bank[1], rn_b, gqk_b], writes=[kT_b])
                yield
            bank = gr.next()
            proj4(1536, bank)
            P.op("act", lambda e, bi=bank[0]: e.activation(out=qi_i[:].rearrange("p m t -> p (m t)"), in_=pg[:, bi, :],
                                                           func=AF.Copy), reads=[bank[1]], writes=[qi_b])
            yield
            bank = gr.next()

            def kiw(e, bi=bank[0]):
                for kc in range(8):
                    e.matmul(pg[0:64, bi, 0:128], lhsT=wa[:, kc, 2048:2112], rhs=xn[:, kc, :], start=(kc == 0), stop=(kc == 7))
                for kc in range(8):
                    e.matmul(pg[64:128, bi, 0:128], lhsT=wa[:, kc, ATTN_IN:ATTN_IN + 64], rhs=xn[:, kc, :], start=(kc == 0),
                             stop=(kc == 7))
                ins = None
                for kc in range(8):
                    ins = e.matmul(pg[:, bi, 128:136], lhsT=xn[:, kc, :], rhs=wa[:, kc, 2112:2120], start=(kc == 0), stop=(kc == 7))
                return ins

            P.op("pe", kiw, reads=[wa_b, xn_b], writes=[bank[1]])
            P.op("act", lambda e, bi=bank[0], ts_=ts_: e.activation(out=kiT[:, ts_], in_=pg[:, bi, 0:128], func=AF.Copy),
                 reads=[bank[1]], writes=[kiT_b])
            P.op("dve", lambda e, bi=bank[0]: e.tensor_copy(out=wi_i[:], in_=pg[:, bi, 128:136]), reads=[bank[1]], writes=[wi_b])
            yield
            bank = gr.next()

            def vmm(e, bi=bank[0]):
                ins = None
                for kc in range(8):
                    ins = e.matmul(pg[:, bi, :], lhsT=xn[:, kc, :], rhs=wa[:, kc, 1024:1536], start=(kc == 0), stop=(kc == 7))
                return ins

            P.op("pe", vmm, reads=[wa_b, xn_b], writes=[bank[1]])
            P.op("act", lambda e, bi=bank[0], i=i: e.activation(out=vaug[:, i, :, 0:64],
                                                                in_=pg[:, bi, :].rearrange("p (h d) -> p h d", h=8), func=AF.Copy),
                 reads=[bank[1]], writes=[va_b])
            yield

            for gk in range((nkb + 3) // 4):
                w_ = min(512, W - gk * 512)
                abank = accr.next()
                for h in range(8):
                    hb = (h % 2) * 64
                    bank = gr.next()
                    P.op("pe", lambda e, bi=bank[0], h=h, hb=hb, gk=gk, w_=w_: e.matmul(
                        pg[:, bi, 0:w_], lhsT=qi_i[hb:hb + 64, h // 2, :], rhs=kiT[hb:hb + 64, gk * 512:gk * 512 + w_],
                        start=True, stop=True), reads=[qi_b] + kiT_bs[gk * 4:gk * 4 + (w_ // 128)], writes=[bank[1]])
                    ri, rb_ = rr.next()
                    P.op("act", lambda e, bi=bank[0], ri=ri, w_=w_: e.activation(out=rbuf[:, ri, 0:w_], in_=pg[:, bi, 0:w_],
                                                                                 func=AF.Relu), reads=[bank[1]], writes=[rb_])
                    if h == 0:
                        P.op("dve", lambda e, ri=ri, ai=abank[0], w_=w_: e.tensor_scalar(
                            out=pg[:, ai, 0:w_], in0=rbuf[:, ri, 0:w_], scalar1=wi_i[:, 0:1], scalar2=None,
                            op0=ALU.mult), reads=[rb_, wi_b], writes=[abank[1]])
                    elif h < 7:
                        P.op("dve", lambda e, ri=ri, ai=abank[0], w_=w_, h=h: e.scalar_tensor_tensor(
                            out=pg[:, ai, 0:w_], in0=rbuf[:, ri, 0:w_], scalar=wi_i[:, h:h + 1],
                            in1=pg[:, ai, 0:w_], op0=ALU.mult, op1=ALU.add), reads=[rb_, wi_b, abank[1]], writes=[abank[1]])
                    else:
                        P.op("dve", lambda e, ri=ri, ai=abank[0], gk=gk, w_=w_, h=h: e.scalar_tensor_tensor(
                            out=sc[:, gk * 512:gk * 512 + w_], in0=rbuf[:, ri, 0:w_], scalar=wi_i[:, h:h + 1],
                            in1=pg[:, ai, 0:w_], op0=ALU.mult, op1=ALU.add), reads=[rb_, wi_b, abank[1]], writes=[sc_b])
                    yield
            P.op("dve", lambda e, ts_=ts_: e.tensor_tensor(out=sc[:, ts_], in0=sc[:, ts_], in1=cmask[:], op=ALU.add),
                 reads=[sc_b, cm_b], writes=[sc_b])
            if i < 2:
                P.op("dve", lambda e: e.memset(bs[:, 0:1], -1.0e29), writes=[bs_b])
            else:
                P.op("dve", lambda e, i=i: e.tensor_reduce(out=bs[:, 0:1], in_=sc[:, 0:i * 128], axis=AX.X, op=ALU.min),
                     reads=[sc_b], writes=[bs_b])
                P.op("dve", lambda e, W=W: e.tensor_reduce(out=bs[:, 6:7], in_=sc[:, 0:W], axis=AX.X, op=ALU.max),
                     reads=[sc_b, bs_b], writes=[bs_b])
                P.op("dve", lambda e: e.tensor_tensor(out=bs[:, 1:2], in0=bs[:, 6:7], in1=bs[:, 0:1], op=ALU.subtract),
                     reads=[bs_b], writes=[bs_b])
                P.op("dve", lambda e: e.tensor_tensor(out=halfs[:], in0=bs[:, 1:2].broadcast_to([128, 2 * NITER]), in1=pw2[:],
                                                      op=ALU.mult), reads=[bs_b, cm_b], writes=[hf_b])
                P.op("dve", lambda e: e.tensor_tensor(out=bs[:, 3:4], in0=bs[:, 0:1], in1=halfs[:, 0:1], op=ALU.add),
                     reads=[bs_b, hf_b], writes=[bs_b])
                nit = min(NITER, int(math.ceil(math.log2(W))) + 4)
                for it in range(nit):
                    P.op("dve", lambda e: e.tensor_scalar(out=junk[:, 0:W], in0=sc[:, 0:W], scalar1=bs[:, 3:4], scalar2=None,
                                                          op0=ALU.is_ge, op1=ALU.add, accum_out=bs[:, 4:5]),
                         reads=[sc_b, bs_b], writes=[junk_b, bs_b])
                    P.op("dve", lambda e, it=it: e.scalar_tensor_tensor(out=bs[:, 5:6], in0=bs[:, 4:5], scalar=TOPK - 0.5,
                                                                        in1=halfs[:, it:it + 1], op0=ALU.is_ge, op1=ALU.mult),
                         reads=[bs_b, hf_b], writes=[bs_b])
                    P.op("dve", lambda e, it=it: e.scalar_tensor_tensor(out=bs[:, 3:4], in0=bs[:, 5:6],
                                                                        scalar=halfs[:, NITER + it:NITER + it + 1],
                                                                        in1=bs[:, 3:4], op0=ALU.subtract, op1=ALU.add),
                         reads=[bs_b, hf_b], writes=[bs_b])
                    yield
                P.op("dve", lambda e, nit=nit: e.tensor_tensor(out=bs[:, 0:1], in0=bs[:, 3:4], in1=halfs[:, NITER + nit - 1:NITER + nit],
                                                               op=ALU.subtract), reads=[bs_b, hf_b], writes=[bs_b])
            P.op("dve", lambda e, W=W: e.tensor_scalar(out=junk[:, 0:W], in0=sc[:, 0:W], scalar1=bs[:, 0:1], scalar2=None,
                                                       op0=ALU.is_ge), reads=[sc_b, bs_b], writes=[junk_b])
            for j0 in range(0, nkb, 8):
                nb = min(8, nkb - j0)
                bank = gr.next()

                def trm(e, bi=bank[0], j0=j0, nb=nb):
                    ins = None
                    for jj in range(nb):
                        ins = e.transpose(out=pgb(bi)[:, jj * 128:(jj + 1) * 128], in_=junk[:, (j0 + jj) * 128:(j0 + jj + 1) * 128],
                                          identity=C.ident_bf[:])
                    return ins

                P.op("pe", trm, reads=[junk_b, C.b], writes=[bank[1]])
                P.op("act", lambda e, bi=bank[0], j0=j0, nb=nb: e.activation(
                    out=maskT[:, i % 2, j0:j0 + nb, :].rearrange("p j t -> p (j t)"), in_=pgb(bi)[:, 0:nb * 128], func=AF.Copy),
                    reads=[bank[1]], writes=[mT_bs[i % 2]])
                yield
            yield

        def back(i):
            ts_ = slice(i * 128, (i + 1) * 128)
            nkb = i + 1
            q_b = q_bs[i % 2]
            mT_b = mT_bs[i % 2]
            items = [(h, j0, min(4, nkb - j0)) for h in range(8) for j0 in range(0, nkb, 4)]
            DEPTH = 6
            st = {}
            obs = {}
            for k in range(len(items) + DEPTH):
                if k < len(items):
                    h, j0, nb = items[k]
                    hb = (h % 2) * 64
                    m = h // 2
                    if j0 == 0:
                        obs[h] = orr.next()
                    bank = lgr.next()

                    def qk(e, bi=bank[0], j0=j0, nb=nb, h=h, hb=hb, m=m):
                        ins = None
                        for jj in range(nb):
                            j = j0 + jj
                            near = j >= i - 1
                            ins = e.matmul(pg[:, bi, jj * 128:(jj + 1) * 128], lhsT=kT[hb:hb + 64, m, j * 128:(j + 1) * 128],
                                           rhs=q_i[hb:hb + 64, i % 2, m, :], start=True, stop=not near)
                            if near:
                                ins = e.matmul(pg[:, bi, jj * 128:(jj + 1) * 128], lhsT=C.ident_bf[:],
                                               rhs=bt[:, h, 0 if j == i else 1, :], start=False, stop=True)
                        return ins

                    P.op("pe", qk, reads=kT_bs[j0:j0 + nb] + [q_b, bt_b, C.b], writes=[bank[1]])
                    ei, eb = er.next()
                    P.op("act", lambda e, bi=bank[0], ei=ei, nb=nb: e.activation(
                        out=Et[:, ei, 0:nb, :].rearrange("p j t -> p (j t)"), in_=pg[:, bi, 0:nb * 128], func=AF.Exp),
                        reads=[bank[1]], writes=[eb])
                    pi, pb_ = ptr.next()
                    P.op("pool", lambda e, ei=ei, pi=pi, j0=j0, nb=nb: e.tensor_tensor(
                        out=Pt[:, pi, 0:nb, :], in0=Et[:, ei, 0:nb, :], in1=maskT[:, i % 2, j0:j0 + nb, :], op=ALU.mult),
                        reads=[eb, mT_b], writes=[pb_])
                    st[k] = (pi, pb_)
                kk = k - DEPTH
                if kk >= 0:
                    h, j0, nb = items[kk]
                    pi, pb_ = st.pop(kk)
                    ob = obs[h]

                    def pv(e, oi=ob[0], pi=pi, j0=j0, nb=nb, h=h):
                        ins = None
                        for jj in range(nb):
                            j = j0 + jj
                            ins = e.matmul(po[:, oi, 0:65], lhsT=Pt[:, pi, jj, :], rhs=vaug[:, j, h, :], start=(j == 0),
                                           stop=(j == i))
                        return ins

                    P.op("pe", pv, reads=[pb_] + va_bs[j0:j0 + nb], writes=[ob[1]])
                    if j0 + nb == nkb:
                        di, db = rdr.next()
                        P.op("dve", lambda e, oi=ob[0], di=di: e.reciprocal(out=rden[:, di:di + 1], in_=po[:, oi, 64:65]),
                             reads=[ob[1]], writes=[db])
                        P.op("act", lambda e, oi=ob[0], di=di, h=h: e.activation(out=ytile[:, h * 64:(h + 1) * 64],
                                                                                 in_=po[:, oi, 0:64], func=AF.Copy,
                                                                                 scale=rden[:, di:di + 1]),
                             reads=[ob[1], db], writes=[yt_b])
                yield
            bank = lgr.next()

            def try_(e, bi=bank[0]):
                ins = None
                for m in range(4):
                    ins = e.transpose(out=pgb(bi)[:, m * 128:(m + 1) * 128], in_=ytile[:, m * 128:(m + 1) * 128],
                                      identity=C.ident_bf[:])
                return ins

            P.op("pe", try_, reads=[yt_b, C.b], writes=[bank[1]])
            P.op("act", lambda e, bi=bank[0]: e.activation(out=yst[:].rearrange("p m t -> p (m t)"), in_=pgb(bi)[:, 0:512],
                                                           func=AF.Copy), reads=[bank[1]], writes=[ys_b])
            P.dma("sp", lambda e: e.dma_start(out=yav[:, :, ts_], in_=yst[:]), reads=[ys_b])
            yield

        def run2(f, b):
            fa, ba = f is not None, b is not None
            while fa or ba:
                if fa:
                    try:
                        next(f)
                    except StopIteration:
                        fa = False
                if ba:
                    try:
                        next(b)
                    except StopIteration:
                        ba = False

        for i in range(NT + 1):
            run2(front(i) if i < NT else None, back(i - 1) if i >= 1 else None)
    P.barrier()


WEIGHT_SPECS = [
    ("mix_norm", [1, D]), ("w_in", [D, 5992]), ("attn_q_norm", [1, 64]), ("attn_k_norm", [1, 64]),
    ("bias_tiles", [8, 2, 128, 128]), ("b31", [1, 8]), ("rwkv_mu", [1, RWKV_IN]), ("rwkv_w0", [1, 512]), ("rwkv_w2", [64, 512]),
    ("rwkv_a0", [1, 512]), ("rwkv_a2", [64, 512]), ("rwkv_g2", [160, 512]), ("rwkv_k_k", [1, 512]),
    ("rwkv_k_a", [1, 512]), ("rwkv_r_k", [1, 512]), ("rwkv_ln_w", [1, 512]), ("rwkv_ln_b", [1, 512]),
    ("w_branch_attn", [512, D]), ("w_branch_rwkv", [512, D]), ("w_out", [D, D]), ("ffn_norm", [1, D]),
    ("w_gate_up", [D, 2 * FFN_H]), ("w_down", [FFN_H, D]),
]


def build_program(phases=("attn", "rwkv", "merge", "ffn"), debug=False):
    nc = bass.Bass("TRN2", target_bir_lowering=False)
    A = {}
    A["x"] = nc.dram_tensor("x", [S, D], F32, kind="ExternalInput").ap()
    for name, shp in WEIGHT_SPECS:
        A[name] = nc.dram_tensor(name, shp, F32, kind="ExternalInput").ap()
    out = nc.dram_tensor("out", [S, D], F32, kind="ExternalOutput").ap()
    def kind(prod, cons):
        if not debug:
            return "Internal"
        if prod in phases and cons not in phases:
            return "ExternalOutput"
        if prod not in phases and cons in phases:
            return "ExternalInput"
        return "Internal"

    ya = nc.dram_tensor("ya_scr", [512, S], BF16, kind=kind("attn", "merge")).ap()
    yr = nc.dram_tensor("yr_scr", [512, S], BF16, kind=kind("rwkv", "merge")).ap()
    hs = nc.dram_tensor("h_scr", [S, D], F32, kind=kind("merge", "ffn")).ap()
    P = Prog(nc)
    final_ops = []
    with ExitStack() as es:
        C = make_consts(P, nc, es)
        P.barrier()
        if "attn" in phases:
            phase_attn(P, nc, C, A, ya)
        if "rwkv" in phases:
            phase_rwkv(P, nc, C, A, yr)
        if "merge" in phases:
            phase_merge(P, nc, C, A["x"], ya, yr, hs, A["mix_norm"], A["w_in"], A["w_branch_attn"],
                        A["w_branch_rwkv"], A["w_out"])
        if "ffn" in phases:
            phase_ffn(P, nc, C, hs, out, A["ffn_norm"], A["w_gate_up"], A["w_down"], final_ops)
        P.emit(final_wait_ops=final_ops)
    return nc


def t5_bucket_np(d):
    d = np.maximum(d, 0)
    max_exact = 16
    log_ratio = np.log(np.maximum(d, 1).astype(np.float32) / max_exact) / math.log(128 / max_exact)
    large = np.minimum(max_exact + (log_ratio * 16).astype(np.int32), 31)
    return np.where(d < max_exact, d, large)


def host_layout(inputs):
    w = {}
    for name, shp in WEIGHT_SPECS:
        if name in ("bias_tiles", "b31"):
            continue
        w[name] = np.ascontiguousarray(np.asarray(inputs[name], dtype=np.float32).reshape(shp))
    s_idx = np.arange(128)[:, None]
    t_idx = np.arange(128)[None, :]
    rb = np.asarray(inputs["rel_bias"], dtype=np.float32)
    tiles = np.empty((8, 2, 128, 128), np.float32)
    for cls in range(2):
        bk = t5_bucket_np(t_idx - s_idx + 128 * cls)
        tiles[:, cls] = np.transpose(rb[bk], (2, 0, 1))
    w["bias_tiles"] = tiles
    w["b31"] = np.ascontiguousarray(rb[31:32, :])
    return w


_NC_CACHE = {}


def kernel(**inputs):
    x = np.asarray(inputs["x"], dtype=np.float32)
    w = host_layout(inputs)
    if "nc" not in _NC_CACHE:
        _NC_CACHE["nc"] = build_program()
    nc = _NC_CACHE["nc"]
    in_maps = []
    for b in range(8):
        m = dict(w)
        m["x"] = np.ascontiguousarray(x[b])
        in_maps.append(m)
    res = run_bass_kernel_spmd(nc, in_maps, core_ids=list(range(8)))
    return np.stack([np.asarray(r["out"], dtype=np.float32) for r in res.results], axis=0)
```
